# Optimizing a Trainium2 kernel written in Bass

```python
import math
import jax, jax.numpy as jnp
from jax import lax
import numpy as np

D_MODEL = 1024
BATCH = 32
SEQ = 256
DEPTH = 4
DEC_BATCH = 8
DEC_SEQ = 1024
PAST_LEN = 512

GRID_W = 64
N_MIXERS = 3
N_ATTN = (DEPTH + 2) // 3
N_HGRN = (DEPTH + 1) // 3
N_GDN = DEPTH // 3
EPS = 1e-6

DA_HEADS = 8
DA_HD = D_MODEL // (2 * DA_HEADS)
ROPE_BASE = 10000.0
Q_BLOCK = 128

HG_HEADS = 8
HG_K = D_MODEL // HG_HEADS
HG_V = D_MODEL // HG_HEADS
HG_CHUNK = 16

GD_HEADS = 8
GD_K = D_MODEL // GD_HEADS
GD_V = D_MODEL // GD_HEADS
GD_CONV = 3
GD_CHUNK = 64

D_FF = 2816
FFN_CONV = 3

kernel_name = 'hybrid_diffusion_prefix_trunk_step'


def rmsnorm(x, g, eps=EPS):
    xf = x.astype(jnp.float32)
    y = xf * lax.rsqrt(jnp.mean(xf * xf, axis=-1, keepdims=True) + eps)
    return (y * g.astype(jnp.float32)).astype(x.dtype)


def dwconv(x, w):
    pad = w.shape[0] // 2
    return lax.conv_general_dilated(x, w[:, None, :], window_strides=(1,), padding=[(pad, pad)],
                                    dimension_numbers=('NWC', 'WIO', 'NWC'),
                                    feature_group_count=x.shape[-1])


def modulation(cond, w_mod, b_mod):
    m = jax.nn.silu(cond) @ w_mod + b_mod
    return [t[..., None, :] for t in jnp.split(m, 6, axis=-1)]


def axial_rope_tables(n, dtype):
    rows = n // GRID_W
    row = jnp.repeat(jnp.arange(rows, dtype=jnp.float32), GRID_W)
    col = jnp.tile(jnp.arange(GRID_W, dtype=jnp.float32), rows)
    nf = DA_HD // 4
    inv = ROPE_BASE ** (-jnp.arange(nf, dtype=jnp.float32) / nf)
    ar = row[:, None] * inv
    ac = col[:, None] * inv
    return tuple(t.reshape(n, 1, 1, nf).astype(dtype)
                 for t in (jnp.cos(ar), jnp.sin(ar), jnp.cos(ac), jnp.sin(ac)))


def _rotate(u, cos, sin):
    u1, u2 = jnp.split(u, 2, axis=-1)
    return jnp.concatenate([u1 * cos - u2 * sin, u2 * cos + u1 * sin], axis=-1)


def rope_2d(x, tables):
    cr, sr, cc, sc = tables
    xr, xc = jnp.split(x, 2, axis=-1)
    return jnp.concatenate([_rotate(xr, cr, sr), _rotate(xc, cc, sc)], axis=-1)


def diff_attend(q, k, v, lam):
    b, n = q.shape[0], q.shape[1]
    nb = n // Q_BLOCK
    qb = jnp.moveaxis(q.reshape((b, nb, Q_BLOCK) + q.shape[2:]), 1, 0)
    scale = DA_HD ** -0.5

    def block(qi):
        s = jnp.einsum('bqhcd,bkhcd->bhcqk', qi, k, preferred_element_type=jnp.float32) * scale
        p = jax.nn.softmax(s, axis=-1)
        a = (p[:, :, 0] - lam * p[:, :, 1]).astype(v.dtype)
        return jnp.einsum('bhqk,bkhe->bqhe', a, v)

    o = lax.map(block, qb)
    return jnp.moveaxis(o, 0, 1).reshape(b, n, DA_HEADS, 2 * DA_HD)


def diff_attn_mixer(h, w_in, lam_p, subln, w_out, layer, rope=None, ctx_k=None, ctx_v=None):
    b, n, _ = h.shape
    q, k, v = jnp.split(h @ w_in, 3, axis=-1)
    q = q.reshape(b, n, DA_HEADS, 2, DA_HD)
    k = k.reshape(b, n, DA_HEADS, 2, DA_HD)
    v = v.reshape(b, n, DA_HEADS, 2 * DA_HD)
    lam_init = 0.8 - 0.6 * math.exp(-0.3 * layer)
    lp = lam_p.astype(jnp.float32)
    lam = jnp.exp(jnp.sum(lp[0] * lp[1])) - jnp.exp(jnp.sum(lp[2] * lp[3])) + lam_init
    if ctx_k is None:
        keys, vals = k, v
        kv = (k.reshape(b, n, DA_HEADS, 2 * DA_HD), v)
    else:
        q = rope_2d(q, rope)
        keys = jnp.concatenate([ctx_k.reshape(b, -1, DA_HEADS, 2, DA_HD), rope_2d(k, rope)], axis=1)
        vals = jnp.concatenate([ctx_v, v], axis=1)
        kv = None
    o = rmsnorm(diff_attend(q, keys, vals, lam), subln, 1e-5) * (1.0 - lam_init)
    return o.reshape(b, n, D_MODEL) @ w_out, kv


def _heads(x, n_heads):
    b, n, _ = x.shape
    return x.reshape(b, n, n_heads, -1).transpose(0, 2, 1, 3).astype(jnp.float32)


def _chunks(x, chunk):
    b, hh, n = x.shape[:3]
    return jnp.moveaxis(x.reshape((b, hh, n // chunk, chunk) + x.shape[3:]), 2, 0)


def _unchunk(o):
    nc, b, hh, c, e = o.shape
    return jnp.moveaxis(o, 0, 2).reshape(b, hh, nc * c, e)


def gla_chunk_scan(q, k, v, logf, s0, chunk):
    causal = jnp.tril(jnp.ones((chunk, chunk), dtype=bool))[:, :, None]

    def step(S, inp):
        qc, kc, vc, fc = inp
        g = jnp.cumsum(fc, axis=2)
        decay = jnp.exp(jnp.where(causal, g[:, :, :, None, :] - g[:, :, None, :, :], -jnp.inf))
        scores = jnp.einsum('bhtk,bhtsk,bhsk->bhts', qc, decay, kc)
        o = jnp.einsum('bhck,bhkv->bhcv', qc * jnp.exp(g), S) + jnp.einsum('bhts,bhsv->bhtv', scores, vc)
        g_last = g[:, :, -1:, :]
        S = jnp.exp(g_last[:, :, 0, :, None]) * S + jnp.einsum('bhsk,bhsv->bhkv', kc * jnp.exp(g_last - g), vc)
        return S, o

    s_last, o = lax.scan(step, s0, tuple(_chunks(t, chunk) for t in (q, k, v, logf)))
    return _unchunk(o), s_last


def gated_delta_chunk_scan(q, k, v, log_a, beta, s0, chunk):
    dv = v.shape[-1]
    causal = jnp.tril(jnp.ones((chunk, chunk), dtype=bool))
    strict = jnp.tril(jnp.ones((chunk, chunk), dtype=bool), -1)
    eye = jnp.eye(chunk, dtype=jnp.float32)

    def step(S, inp):
        qc, kc, vc, ac, bc = inp
        g = jnp.cumsum(ac, axis=-1)
        decay = jnp.exp(jnp.where(causal, g[..., :, None] - g[..., None, :], -jnp.inf))
        kb = kc * bc[..., None]
        lhs = eye + jnp.where(strict, jnp.einsum('bhtk,bhsk->bhts', kb, kc) * decay, 0.0)
        rhs = jnp.concatenate([vc * bc[..., None], kb * jnp.exp(g)[..., None]], axis=-1)
        sol = lax.linalg.triangular_solve(lhs, rhs, left_side=True, lower=True)
        u, w = sol[..., :dv], sol[..., dv:]
        v_new = u - jnp.einsum('bhck,bhkv->bhcv', w, S)
        scores = jnp.einsum('bhtk,bhsk->bhts', qc, kc) * decay
        o = (jnp.einsum('bhck,bhkv->bhcv', qc * jnp.exp(g)[..., None], S)
             + jnp.einsum('bhts,bhsv->bhtv', scores, v_new))
        g_last = g[..., -1:]
        S = jnp.exp(g_last)[..., None] * S + jnp.einsum('bhsk,bhsv->bhkv', kc * jnp.exp(g_last - g)[..., None], v_new)
        return S, o

    s_last, o = lax.scan(step, s0, tuple(_chunks(t, chunk) for t in (q, k, v, log_a, beta)))
    return _unchunk(o), s_last


def hgrn2_mixer(h, w_in, lb, norm_g, w_out, s_fwd, s_bwd):
    b, n, _ = h.shape
    q, zf, zb, i, g = jnp.split(h @ w_in, 5, axis=-1)
    q = _heads(q, HG_HEADS) * HG_K ** -0.5
    i = _heads(i, HG_HEADS)
    lbh = lb.reshape(HG_HEADS, 1, HG_K)
    log_lb = jnp.log(lbh)
    log_ub = jnp.log1p(-lbh)

    def direction(z, s0, rev):
        logf = jnp.logaddexp(log_lb, log_ub + jax.nn.log_sigmoid(_heads(z, HG_HEADS)))
        key = -jnp.expm1(logf)
        args = (q, key, i, logf)
        if rev:
            args = tuple(jnp.flip(t, axis=2) for t in args)
        o, s_last = gla_chunk_scan(*args, s0.astype(jnp.float32), HG_CHUNK)
        return (jnp.flip(o, axis=2) if rev else o), s_last

    o_f, sf = direction(zf, s_fwd, False)
    o_b, sb = direction(zb, s_bwd, True)
    o = rmsnorm((o_f + o_b).transpose(0, 2, 1, 3), norm_g)
    o = o * jax.nn.silu(g.reshape(b, n, HG_HEADS, HG_V).astype(jnp.float32))
    return o.reshape(b, n, D_MODEL).astype(h.dtype) @ w_out, jnp.stack([sf, sb], axis=1)


def l2norm(x):
    return x * lax.rsqrt(jnp.sum(x * x, axis=-1, keepdims=True) + 1e-6)


def gdn_mixer(h, w_in, conv_w, a_log, dt_bias, norm_g, w_out, s_fwd, s_bwd):
    b, n, _ = h.shape
    proj = h @ w_in
    qkv = jax.nn.silu(dwconv(proj[..., :3 * D_MODEL], conv_w))
    gate = proj[..., 3 * D_MODEL:4 * D_MODEL]
    ab = proj[..., 4 * D_MODEL:].astype(jnp.float32).reshape(b, n, 4, GD_HEADS).transpose(2, 0, 3, 1)
    q, k, v = jnp.split(qkv, 3, axis=-1)
    q = l2norm(_heads(q, GD_HEADS)) * GD_K ** -0.5
    k = l2norm(_heads(k, GD_HEADS))
    v = _heads(v, GD_HEADS)

    def direction(d, s0, rev):
        a_rate = jnp.exp(a_log[d].astype(jnp.float32))[:, None]
        log_a = -a_rate * jax.nn.softplus(ab[d] + dt_bias[d].astype(jnp.float32)[:, None])
        beta = jax.nn.sigmoid(ab[2 + d])
        args = (q, k, v, log_a, beta)
        if rev:
            args = tuple(jnp.flip(t, axis=2) for t in args)
        o, s_last = gated_delta_chunk_scan(*args, s0.astype(jnp.float32), GD_CHUNK)
        return (jnp.flip(o, axis=2) if rev else o), s_last

    o_f, sf = direction(0, s_fwd, False)
    o_b, sb = direction(1, s_bwd, True)
    o = rmsnorm((o_f + o_b).transpose(0, 2, 1, 3), norm_g)
    o = o * jax.nn.silu(gate.reshape(b, n, GD_HEADS, GD_V).astype(jnp.float32))
    return o.reshape(b, n, D_MODEL).astype(h.dtype) @ w_out, jnp.stack([sf, sb], axis=1)


def conv_ffn(h, w_up, conv_w, conv_b, w_down):
    u = dwconv(h @ w_up, conv_w) + conv_b
    val, gte = jnp.split(u, 2, axis=-1)
    return (val * jax.nn.silu(gte)) @ w_down


def trunk(x, cond, P, rope=None, cache=None):
    ctx_mode = cache is None
    b = x.shape[0]
    lb_all = jnp.cumsum(jax.nn.softmax(P['hgrn_lb'].astype(jnp.float32), axis=0), axis=0)
    lb_all = lb_all - lb_all[0]
    attn_k, attn_v, hgrn_s, gdn_s = [], [], [], []
    for i in range(DEPTH):
        sh1, sc1, g1, sh2, sc2, g2 = modulation(cond, P['w_mod'][i], P['b_mod'][i])
        h = rmsnorm(x, P['norm_g'][i, 0]) * (1 + sc1) + sh1
        kind, j = i % N_MIXERS, i // N_MIXERS
        if kind == 0:
            ck = None if ctx_mode else cache['attn_k'][:, j]
            cv = None if ctx_mode else cache['attn_v'][:, j]
            out, kv = diff_attn_mixer(h, P['attn_w_in'][j], P['attn_lambda'][j], P['attn_subln'][j],
                                      P['attn_w_out'][j], i, rope, ck, cv)
            if ctx_mode:
                attn_k.append(kv[0])
                attn_v.append(kv[1])
        elif kind == 1:
            s0 = jnp.zeros((b, 2, HG_HEADS, HG_K, HG_V), jnp.float32) if ctx_mode else cache['hgrn'][:, j]
            out, st = hgrn2_mixer(h, P['hgrn_w_in'][j], lb_all[i], P['hgrn_norm'][j], P['hgrn_w_out'][j],
                                  s0[:, 0], s0[:, 1])
            if ctx_mode:
                hgrn_s.append(st)
        else:
            s0 = jnp.zeros((b, 2, GD_HEADS, GD_K, GD_V), jnp.float32) if ctx_mode else cache['gdn'][:, j]
            out, st = gdn_mixer(h, P['gdn_w_in'][j], P['gdn_conv'][j], P['gdn_a_log'][j], P['gdn_dt_bias'][j],
                                P['gdn_norm'][j], P['gdn_w_out'][j], s0[:, 0], s0[:, 1])
            if ctx_mode:
                gdn_s.append(st)
        x = x + g1 * out
        h = rmsnorm(x, P['norm_g'][i, 1]) * (1 + sc2) + sh2
        x = x + g2 * conv_ffn(h, P['ffn_w_up'][i], P['ffn_conv'][i], P['ffn_conv_b'][i], P['ffn_w_down'][i])
    y = rmsnorm(x, P['final_g'])
    if not ctx_mode:
        return y, None
    dt = x.dtype
    return y, (jnp.stack(attn_k, axis=1), jnp.stack(attn_v, axis=1),
               jnp.stack(hgrn_s, axis=1).astype(dt), jnp.stack(gdn_s, axis=1).astype(dt))


def setup_inputs(seed: int = 0) -> dict:
    key = jax.random.key(seed)
    keys = iter(jax.random.split(key, 40))

    def nrm(shape, scale):
        return jax.random.normal(next(keys), shape, jnp.float32) * scale

    def gain(shape):
        return 1.0 + nrm(shape, 0.02)

    D = D_MODEL
    dt = jnp.exp(jax.random.uniform(next(keys), (N_GDN, 2, GD_HEADS), jnp.float32,
                                    math.log(1e-3), math.log(1e-1)))
    a_log = jnp.log(jax.random.uniform(next(keys), (N_GDN, 2, GD_HEADS), jnp.float32, 1.0, 16.0))
    return {
        'x_prompt': nrm((BATCH, SEQ, D), 1.0),
        'x_sample': nrm((DEC_BATCH, DEC_SEQ, D), 1.0),
        'cache_attn_k': nrm((DEC_BATCH, N_ATTN, PAST_LEN, DA_HEADS, 2 * DA_HD), 1.0),
        'cache_attn_v': nrm((DEC_BATCH, N_ATTN, PAST_LEN, DA_HEADS, 2 * DA_HD), 1.0),
        'state_hgrn': nrm((DEC_BATCH, N_HGRN, 2, HG_HEADS, HG_K, HG_V), 0.3),
        'state_gdn': nrm((DEC_BATCH, N_GDN, 2, GD_HEADS, GD_K, GD_V), 0.1),
        'c': nrm((DEC_BATCH, D), 1.0),
        'c_ctx': nrm((D,), 1.0),
        'norm_g': gain((DEPTH, 2, D)),
        'w_mod': nrm((DEPTH, D, 6 * D), 0.5 * D ** -0.5),
        'b_mod': nrm((DEPTH, 6 * D), 0.02),
        'final_g': gain((D,)),
        'attn_w_in': nrm((N_ATTN, D, 3 * D), D ** -0.5),
        'attn_lambda': nrm((N_ATTN, 4, DA_HD), 0.1),
        'attn_subln': gain((N_ATTN, 2 * DA_HD)),
        'attn_w_out': nrm((N_ATTN, D, D), D ** -0.5),
        'hgrn_w_in': nrm((N_HGRN, D, 5 * D), D ** -0.5),
        'hgrn_lb': nrm((DEPTH, D), 0.5),
        'hgrn_norm': gain((N_HGRN, HG_V)),
        'hgrn_w_out': nrm((N_HGRN, D, D), D ** -0.5),
        'gdn_w_in': nrm((N_GDN, D, 4 * D + 4 * GD_HEADS), D ** -0.5),
        'gdn_conv': nrm((N_GDN, GD_CONV, 3 * D), GD_CONV ** -0.5),
        'gdn_a_log': a_log,
        'gdn_dt_bias': dt + jnp.log(-jnp.expm1(-dt)),
        'gdn_norm': gain((N_GDN, GD_V)),
        'gdn_w_out': nrm((N_GDN, D, D), D ** -0.5),
        'ffn_w_up': nrm((DEPTH, D, 2 * D_FF), D ** -0.5),
        'ffn_conv': nrm((DEPTH, FFN_CONV, 2 * D_FF), FFN_CONV ** -0.5),
        'ffn_conv_b': nrm((DEPTH, 2 * D_FF), 0.02),
        'ffn_w_down': nrm((DEPTH, D_FF, D), D_FF ** -0.5),
    }


def reference(x_prompt, x_sample, cache_attn_k, cache_attn_v, state_hgrn, state_gdn, c, c_ctx,
              norm_g, w_mod, b_mod, final_g, attn_w_in, attn_lambda, attn_subln, attn_w_out,
              hgrn_w_in, hgrn_lb, hgrn_norm, hgrn_w_out, gdn_w_in, gdn_conv, gdn_a_log, gdn_dt_bias,
              gdn_norm, gdn_w_out, ffn_w_up, ffn_conv, ffn_conv_b, ffn_w_down):
    P = dict(norm_g=norm_g, w_mod=w_mod, b_mod=b_mod, final_g=final_g,
             attn_w_in=attn_w_in, attn_lambda=attn_lambda, attn_subln=attn_subln, attn_w_out=attn_w_out,
             hgrn_w_in=hgrn_w_in, hgrn_lb=hgrn_lb, hgrn_norm=hgrn_norm, hgrn_w_out=hgrn_w_out,
             gdn_w_in=gdn_w_in, gdn_conv=gdn_conv, gdn_a_log=gdn_a_log, gdn_dt_bias=gdn_dt_bias,
             gdn_norm=gdn_norm, gdn_w_out=gdn_w_out,
             ffn_w_up=ffn_w_up, ffn_conv=ffn_conv, ffn_conv_b=ffn_conv_b, ffn_w_down=ffn_w_down)
    y_prompt, ctx_state = trunk(x_prompt, c_ctx, P)
    new_k, new_v, new_hgrn, new_gdn = ctx_state
    rope = axial_rope_tables(x_sample.shape[1], x_sample.dtype)
    cache = dict(attn_k=cache_attn_k, attn_v=cache_attn_v, hgrn=state_hgrn, gdn=state_gdn)
    y_sample, _ = trunk(x_sample, c, P, rope, cache)
    return (y_prompt, y_sample, new_k, new_v, new_hgrn, new_gdn)
```

```python
import math
from contextlib import ExitStack

import numpy as np
import concourse.bass as bass
import concourse.mybir as mybir
from concourse.bass_utils import run_bass_kernel_spmd

F32 = mybir.dt.float32
F32R = mybir.dt.float32r
AF = mybir.ActivationFunctionType
ALU = mybir.AluOpType
AX = mybir.AxisListType

D = 1024
NCH = 8
TOK = 1024
DEPTH = 4
D_FF = 2816
NFF = 22
EPS = 1e-6
N_CORES = 8
WCOLS = (4 * 1024 * 6144 + 4 * 1024 * 5632 + 4 * 2816 * 1024 + 2 * 1024 * 3072 + 2 * 1024 * 1024 + 1024 * 5120 + 1024 * 1024
         + 1024 * 4128 + 1024 * 1024) // 128
CCOLS = (2 * 8 * 128 * 512 + 2 * 512 * 1024) // 128


class Cols:
    def __init__(self):
        self.off = {}
        self.n = 0

    def add(self, name, w):
        self.off[name] = (self.n, w)
        self.n += w

    def __getitem__(self, name):
        return self.off[name]


def _param_cols():
    c = Cols()
    c.add('cond', 16)
    c.add('norm_g', 64)
    c.add('b_mod', 192)
    c.add('final_g', 8)
    c.add('subln', 2)
    c.add('hgrn_lb', 32)
    c.add('hgrn_norm', 1)
    c.add('gdn_norm', 1)
    c.add('gdn_conv', 72)
    c.add('ffn_conv', 528)
    c.add('ffn_conv_b', 176)
    c.add('gdn_alog_col', 1)
    c.add('gdn_dt_col', 1)
    c.add('gdn_alog_row', 16)
    c.add('gdn_dt_row', 16)
    return c


PC = _param_cols()


def _fm(v):
    v = np.asarray(v, np.float32)
    r = v.reshape(-1, 128)
    return np.ascontiguousarray(r.T)


def _build_params(core, inp):
    P = np.zeros((128, PC.n), np.float32)

    def put(name, arr):
        o, w = PC[name]
        assert arr.shape == (128, w), (name, arr.shape, w)
        P[:, o:o + w] = arr

    cond = np.stack([inp['c_ctx'], inp['c'][core]], axis=0)
    put('cond', np.ascontiguousarray(cond.reshape(2, 8, 128).transpose(2, 1, 0)).reshape(128, 16))
    put('norm_g', _fm(inp['norm_g']))
    put('b_mod', _fm(inp['b_mod']))
    put('final_g', _fm(inp['final_g']))
    put('subln', _fm(inp['attn_subln']))
    put('hgrn_lb', _fm(inp['hgrn_lb']))
    put('hgrn_norm', _fm(inp['hgrn_norm']))
    put('gdn_norm', _fm(inp['gdn_norm']))
    put('gdn_conv', _fm(inp['gdn_conv']))
    put('ffn_conv', _fm(inp['ffn_conv']))
    put('ffn_conv_b', _fm(inp['ffn_conv_b']))
    al = np.zeros((128, 1), np.float32)
    al[:16, 0] = np.asarray(inp['gdn_a_log'], np.float32).reshape(16)
    put('gdn_alog_col', al)
    dtb = np.zeros((128, 1), np.float32)
    dtb[:16, 0] = np.asarray(inp['gdn_dt_bias'], np.float32).reshape(16)
    put('gdn_dt_col', dtb)
    put('gdn_alog_row', np.broadcast_to(np.asarray(inp['gdn_a_log'], np.float32).reshape(1, 16), (128, 16)))
    put('gdn_dt_row', np.broadcast_to(np.asarray(inp['gdn_dt_bias'], np.float32).reshape(1, 16), (128, 16)))
    return P


def _const_cols():
    c = Cols()
    c.add('ident', 128)
    c.add('ones', 128)
    c.add('perm', 128)
    c.add('mask_f', 256)
    c.add('mask_b', 256)
    c.add('negu_f', 256)
    c.add('negl_f', 256)
    c.add('negu_b', 256)
    c.add('negl_b', 256)
    c.add('sneg_f', 256)
    c.add('sneg_b', 256)
    c.add('idrep', 256)
    c.add('sel', 512)
    c.add('nsel', 128)
    return c


CC = _const_cols()


def _build_consts():
    C = np.zeros((128, CC.n), np.float32)
    o, w = CC['ident']
    C[:, o:o + w] = np.eye(128, dtype=np.float32)
    o, w = CC['ones']
    C[:, o:o + w] = 1.0
    o, w = CC['perm']
    tok = np.arange(1024)
    row = (tok // 64).astype(np.float32)
    col = (tok % 64).astype(np.float32)
    inv = (np.float32(10000.0) ** (-np.arange(16, dtype=np.float32) / np.float32(16))).astype(np.float32)
    ROPE = np.zeros((128, 2048), np.float32)
    oc_, os_ = 0, 1024
    for p in range(128):
        dd = p % 64
        i = dd % 32
        partner = p + 16 if i < 16 else p - 16
        C[partner, o + p] = 1.0
        f = i % 16
        pos = row if dd < 32 else col
        ang = (pos * inv[f]).astype(np.float32)
        ROPE[p, oc_:oc_ + 1024] = np.cos(ang)
        ROPE[p, os_:os_ + 1024] = np.sin(ang) * (-1.0 if i < 16 else 1.0)
    sidx = np.arange(64)[:, None]
    tidx = np.arange(64)[None, :]
    om, _ = CC['mask_f']
    C[:64, om:om + 256] = np.tile((sidx <= tidx).astype(np.float32), (1, 4))
    om, _ = CC['mask_b']
    C[:64, om:om + 256] = np.tile((sidx >= tidx).astype(np.float32), (1, 4))
    BIG = 30000.0
    p_, j_ = sidx, tidx

    def putm(name, m):
        o_, _ = CC[name]
        C[:64, o_:o_ + 256] = np.tile(m.astype(np.float32), (1, 4))

    putm('negu_f', np.where(j_ >= p_, 0.0, -BIG))
    putm('negl_f', np.where(j_ < p_, 0.0, -BIG))
    putm('negu_b', np.where(j_ <= p_, 0.0, -BIG))
    putm('negl_b', np.where(j_ > p_, 0.0, -BIG))
    putm('sneg_f', np.where(j_ > p_, -1.0, 0.0))
    putm('sneg_b', np.where(j_ < p_, -1.0, 0.0))
    putm('idrep', (j_ == p_))
    o_, _ = CC['sel']
    for kk in range(4):
        C[kk, o_ + kk * 128:o_ + (kk + 1) * 128] = 1.0
    o_, _ = CC['nsel']
    for kk in range(2):
        C[kk, o_ + kk * 64:o_ + (kk + 1) * 64] = -1.0
    return C, ROPE


class _GT:
    def __init__(self, name):
        self.name = name


class Geo:
    def __init__(self, name, shape, offset=0, pat=None):
        self.tensor = _GT(name)
        if pat is None:
            pat = []
            st = 1
            for n in reversed(shape):
                pat.insert(0, (st, n))
                st *= n
        self.ap = tuple(pat)
        self.offset = offset

    @property
    def shape(self):
        return tuple(n for _, n in self.ap)

    def __getitem__(self, key):
        if not isinstance(key, tuple):
            key = (key,)
        key = key + (slice(None),) * (len(self.ap) - len(key))
        off = self.offset
        pat = []
        for (st, n), kk in zip(self.ap, key):
            if isinstance(kk, int):
                off += st * kk
            else:
                a, b, _ = kk.indices(n)
                off += st * a
                pat.append((st, b - a))
        return Geo(self.tensor.name, None, off, pat)


class Trk:
    __slots__ = ('name', 'w', 'rs', 'sem', 'cnt', 'psum')

    def __init__(self, name):
        self.name = name
        self.psum = False
        self.w = None
        self.rs = {}
        self.sem = None
        self.cnt = 0


def trks(name, *dims):
    if len(dims) == 1:
        return [Trk(f'{name}{i}') for i in range(dims[0])]
    return [trks(f'{name}{i}_', *dims[1:]) for i in range(dims[0])]


def flat(x):
    if isinstance(x, Trk):
        return [x]
    out = []
    for e in x:
        out.extend(flat(e))
    return out


class KB:
    def __init__(self, nc, es):
        self.nc = nc
        self.es = es
        self.E = {'pe': nc.tensor, 'act': nc.scalar, 'dve': nc.vector, 'pool': nc.gpsimd, 'sp': nc.sync}
        self.sem = {e: es.enter_context(nc.semaphore(f's_{e}')) for e in self.E}
        self.cnt = {e: 0 for e in self.E}
        self.waited = {e: {} for e in self.E}
        self.dma_sems = []
        self.n_ins = 0
        self.n_wait = 0

    def _wait(self, eng, deps):
        need = {}
        for key, val in deps:
            if key == 'pe' and eng == 'pe':
                continue
            if need.get(key, 0) < val:
                need[key] = val
        wt = self.waited[eng]
        for key, val in need.items():
            if wt.get(key, 0) >= val:
                continue
            sem = self.sem[key] if isinstance(key, str) else key
            self.E[eng].wait_ge(sem, val)
            self.n_wait += 1
            wt[key] = val

    def _deps(self, reads, writes):
        deps = []
        for t in reads:
            if t.w is not None:
                deps.append(t.w)
            if t.psum:
                deps.extend(t.rs.items())
        for t in writes:
            if t.w is not None:
                deps.append(t.w)
            deps.extend(t.rs.items())
        return deps

    def _mark(self, tok, reads, writes):
        k, v = tok
        for t in reads:
            if t.rs.get(k, 0) < v:
                t.rs[k] = v
        for t in writes:
            t.w = tok
            t.rs = {}

    def emit(self, eng, fn, reads=(), writes=()):
        reads = flat(reads)
        writes = flat(writes)
        self._wait(eng, self._deps(reads, writes))
        ins = fn(self.E[eng])
        self.cnt[eng] += 1
        ins.then_inc(self.sem[eng], 1)
        self.n_ins += 1
        tok = (eng, self.cnt[eng])
        self._mark(tok, reads, writes)
        return tok

    def dma(self, out, in_, reads=(), writes=(), q='sp'):
        reads = flat(reads)
        writes = flat(writes)
        self._wait(q, self._deps(reads, writes))
        owner = (writes + reads)[0]
        if owner.sem is None:
            owner.sem = self.es.enter_context(self.nc.semaphore(f'd_{owner.name}'))
            self.dma_sems.append(owner)
        self.E[q].dma_start(out=out, in_=in_).then_inc(owner.sem, 16)
        owner.cnt += 16
        self.n_ins += 1
        tok = (owner.sem, owner.cnt)
        self._mark(tok, reads, writes)
        return tok

    def dma_group(self, pairs, writes, q='sp'):
        writes = flat(writes)
        self._wait(q, self._deps([], writes))
        owner = writes[0]
        if owner.sem is None:
            owner.sem = self.es.enter_context(self.nc.semaphore(f'd_{owner.name}'))
            self.dma_sems.append(owner)
        for out, in_ in pairs:
            self.E[q].dma_start(out=out, in_=in_).then_inc(owner.sem, 16)
            owner.cnt += 16
            self.n_ins += 1
        tok = (owner.sem, owner.cnt)
        self._mark(tok, [], writes)
        return tok

    def barrier(self):
        for e in self.E:
            deps = [(o, self.cnt[o]) for o in self.E if o != e and self.cnt[o] > 0]
            deps += [(t.sem, t.cnt) for t in self.dma_sems]
            wt = self.waited[e]
            for key, val in deps:
                if wt.get(key, 0) >= val:
                    continue
                sem = self.sem[key] if isinstance(key, str) else key
                self.E[e].wait_ge(sem, val)
                self.n_wait += 1
                wt[key] = val

    def finish(self):
        deps = [(t.sem, t.cnt) for t in self.dma_sems]
        deps += [(o, self.cnt[o]) for o in self.E if o != 'sp' and self.cnt[o] > 0]
        self._wait('sp', deps)

    def sb(self, name, shape, dt=F32):
        return self.es.enter_context(self.nc.sbuf_tensor(name, list(shape), dt))


class PF:
    def __init__(self, prog, specs):
        self.p, self.specs, self.h = prog, specs, {}

    def get(self, i):
        for t in (i, i + 1):
            if t < len(self.specs) and t not in self.h:
                self.h[t] = self.p.wpiece(self.specs[t])
        return self.h.pop(i)


class Prog:
    def __init__(self, cfg):
        self.cfg = cfg
        nc = bass.Bass("TRN2", target_bir_lowering=False)
        self.nc = nc
        self.es = ExitStack()
        self.k = KB(nc, self.es)
        self.rr = 0
        self.wplan = {'wpk': [], 'cpk': []}
        self.wcols = {'wpk': 0, 'cpk': 0}
        self.wkeys = {}

    def declare(self):
        nc = self.nc

        def din(name, shape):
            return nc.dram_tensor(name, list(shape), F32, kind="ExternalInput").ap()

        def dout(name, shape):
            return nc.dram_tensor(name, list(shape), F32, kind="ExternalOutput").ap()

        d = {}
        d['xin'] = din('xin', [2, 8, 128, TOK])
        d['params'] = din('params', [128, PC.n])
        d['consts'] = din('consts', [128, CC.n])
        d['rope'] = din('rope', [128, 2048])
        d['wpk'] = din('wpk', [128, WCOLS])
        d['cpk'] = din('cpk', [128, CCOLS])
        d['lamtab'] = din('lamtab', [128, 512])
        d['gscr'] = nc.dram_tensor('gscr', [48, TOK], F32, kind="Internal").ap()
        d['w_mod'] = Geo('w_mod', [4, D, 6 * D])
        d['ffn_w_up'] = Geo('ffn_w_up', [4, D, 2 * D_FF])
        d['ffn_w_down'] = Geo('ffn_w_down', [4, D_FF, D])
        d['attn_w_in'] = Geo('attn_w_in', [2, D, 3 * D])
        d['attn_w_out'] = Geo('attn_w_out', [2, D, D])
        d['hgrn_w_in'] = Geo('hgrn_w_in', [1, D, 5 * D])
        d['hgrn_w_out'] = Geo('hgrn_w_out', [1, D, D])
        d['gdn_w_in'] = Geo('gdn_w_in', [1, D, 4 * D + 32])
        d['gdn_w_out'] = Geo('gdn_w_out', [1, D, D])
        d['ck'] = Geo('ck', [2, 8, 128, 512])
        d['cv'] = Geo('cv', [2, 512, D])
        d['st_hgrn'] = din('st_hgrn', [2, 8, 128, 128])
        d['st_gdn'] = din('st_gdn', [2, 8, 128, 128])
        d['yout'] = dout('yout', [2, 8, 128, TOK])
        d['kout'] = dout('kout', [2, 8, 128, TOK])
        d['vout'] = dout('vout', [2, TOK, D])
        d['hgout'] = dout('hgout', [4, 2, 8, 128, 128])
        d['gdout'] = dout('gdout', [4, 2, 8, 128, 128])
        self.d = d

    def ptrk(self, name, n=None):
        if not hasattr(self, '_pt'):
            self._pt = {}
        if name not in self._pt:
            self._pt[name] = Trk(name) if n is None else trks(name, n)
        return self._pt[name]

    def tmp(self, es, name, shape, dt=F32):
        self._uid = getattr(self, '_uid', 0) + 1
        return es.enter_context(self.nc.sbuf_tensor(f'{name}_{self._uid}', list(shape), dt))

    def evac_eng(self):
        self.rr += 1
        return 'act' if self.rr % 2 else 'dve'

    def copy(self, eng, out, in_, reads, writes):
        if eng == 'act':
            return self.k.emit('act', lambda e: e.activation(out=out, in_=in_, func=AF.Copy), reads, writes)
        return self.k.emit(eng, lambda e: e.tensor_copy(out=out, in_=in_), reads, writes)

    def init_wpool(self):
        k = self.k
        self.ws = [k.sb(f'ws{i}', [128, 2048], F32) for i in range(2)]
        self.ws_t = trks('ws', 2)
        self.wr = [k.sb(f'wr{i}', [128, 2048], F32R) for i in range(2)]
        self.wr_t = trks('wr', 2)
        self.ws_i = 0
        self.wr_i = 0

    def wpiece(self, segs, rounded=True, dest=None):
        k = self.k
        si = self.ws_i
        self.ws_i = (si + 1) % 2
        st, stt = self.ws[si], self.ws_t[si]
        off = 0
        views = []
        key = []
        percore = False
        for ap in segs:
            rows, ncols = ap.shape
            kc = rows // 128
            pat = tuple((int(a), int(b)) for a, b in ap.ap)
            assert len(pat) == 2 and pat[1][0] == 1, pat
            name = ap.tensor.name
            percore = percore or name in ('ck', 'cv')
            key.append((name, int(ap.offset), pat[0][0], rows, ncols))
            views.append((off, kc, ncols))
            off += kc * ncols
        key = tuple(key)
        pk = 'cpk' if percore else 'wpk'
        if key not in self.wkeys:
            self.wkeys[key] = (pk, self.wcols[pk])
            self.wplan[pk].append((self.wcols[pk], key))
            self.wcols[pk] += off
        pk, c0 = self.wkeys[key]
        k.dma(st[:, 0:off], self.d[pk][:, c0:c0 + off], writes=[stt])
        if not rounded:
            outs = [st[:, o:o + kc * n].rearrange("p (c n) -> p c n", c=kc) for (o, kc, n) in views]
            return outs, stt
        if dest is not None:
            rt, rtt = dest
        else:
            ri = self.wr_i
            self.wr_i = (ri + 1) % 2
            rt, rtt = self.wr[ri], self.wr_t[ri]
        k.emit('act', lambda e: e.activation(out=rt[:, 0:off], in_=st[:, 0:off], func=AF.Copy), [stt], [rtt])
        outs = [rt[:, o:o + kc * n].rearrange("p (c n) -> p c n", c=kc) for (o, kc, n) in views]
        return outs, rtt

    def build(self):
        nc, k, d, cfg = self.nc, self.k, self.d, self.cfg
        self.ps = [self.es.enter_context(nc.psum_tensor(f'ps{i}', [128, 512], F32)) for i in range(8)]
        self.ps_t = trks('ps', 8)
        for t in self.ps_t:
            t.psum = True
        self.PT = k.sb('PT', [128, PC.n])
        self.PT_t = Trk('PT')
        self.CT = k.sb('CT', [128, CC.n])
        self.CT_t = Trk('CT')
        self.onesR = k.sb('onesR', [128, 128], F32R)
        self.onesR_t = Trk('onesR')
        self.identR = k.sb('identR', [128, 128], F32R)
        self.identR_t = Trk('identR')
        self.xT = [k.sb(f'xT{h}', [128, NCH, TOK]) for h in range(2)]
        self.xT_t = trks('xT', 2, NCH, 2)
        self.hT = k.sb('hT', [128, NCH, TOK], F32R)
        self.hT_t = trks('hT', NCH, 2)
        self.permR = k.sb('permR', [128, 128], F32R)
        self.permR_t = Trk('permR')
        self.AV = k.sb('AV', [128, 8])
        self.AV_t = Trk('AV')
        self.MODs = [k.sb(f'MOD{i}', [128, 48, 2]) for i in range(2)]
        self.MODs_t = trks('MOD', 2)
        self.ABs = [k.sb(f'AB{i}', [128, 2, 2, 8, 2]) for i in range(2)]
        self.ABs_t = trks('AB', 2)
        self.modgen = None
        self.sc = k.sb('sc', [128, 16])
        self.sc_t = Trk('sc')
        self.init_wpool()

        k.dma(self.PT[:], d['params'][:, :], writes=[self.PT_t])
        k.dma(self.CT[:], d['consts'][:, :], writes=[self.CT_t])
        for h in range(2):
            k.dma_group([(self.xT[h][:, c, :], d['xin'][h, c, :, :]) for c in range(NCH)], self.xT_t[h])
        o, w = CC['ones']
        k.emit('dve', lambda e: e.tensor_copy(out=self.onesR[:], in_=self.CT[:, o:o + w]), [self.CT_t], [self.onesR_t])
        o2, w2 = CC['ident']
        k.emit('dve', lambda e: e.tensor_copy(out=self.identR[:], in_=self.CT[:, o2:o2 + w2]), [self.CT_t], [self.identR_t])
        o3, w3 = CC['perm']
        k.emit('dve', lambda e: e.tensor_copy(out=self.permR[:], in_=self.CT[:, o3:o3 + w3]), [self.CT_t], [self.permR_t])
        oc, wc = PC['cond']
        k.emit('act', lambda e: e.activation(out=self.sc[:], in_=self.PT[:, oc:oc + wc], func=AF.Silu), [self.PT_t], [self.sc_t])

        nl = cfg.get('layers', DEPTH)
        for _ in self.modulation(0):
            pass
        for layer in range(nl):
            p = layer % 2
            self.MOD, self.MOD_t, self.AB, self.AB_t = self.MODs[p], self.MODs_t[p], self.ABs[p], self.ABs_t[p]
            for half in range(2):
                if cfg.get('mixers', True):
                    self.rmsnorm_mod(layer, 0, half)
                    self.mixer(layer, half)
                if half == 1 and layer + 1 < nl:
                    self.modgen = self.modulation(layer + 1)
                if cfg.get('ffn', True):
                    self.rmsnorm_mod(layer, 1, half)
                    self.ffn(layer, half)
                if self.modgen is not None:
                    for _ in self.modgen:
                        pass
                    self.modgen = None
        self.final(cfg)
        k.finish()

    def modulation(self, layer):
        k, d = self.k, self.d
        p = layer % 2
        MOD, MOD_t, AB, AB_t = self.MODs[p], self.MODs_t[p], self.ABs[p], self.ABs_t[p]
        ps, pst = self.ps[7], self.ps_t[7]
        scv = self.sc[:].rearrange("p (c j) -> p c j", j=2)
        for piece in range(24):
            (w,), wt = self.wpiece([d['w_mod'][layer, :, piece * 256:(piece + 1) * 256]], rounded=False)
            for q2 in range(2):
                q = piece * 2 + q2
                for c in range(NCH):
                    k.emit('pe', lambda e, c=c, q2=q2, q=q: e.matmul(
                        ps[:, q * 2:q * 2 + 2], w[:, c, q2 * 128:(q2 + 1) * 128], scv[:, c, :],
                        start=(c == 0), stop=(c == NCH - 1)), [wt, self.sc_t], [pst])
            yield piece
        ob, wb = PC['b_mod']
        bm = self.PT[:, ob + layer * 48: ob + layer * 48 + 48]
        k.emit('dve', lambda e: e.tensor_tensor(
            out=MOD[:], in0=ps[:, 0:96].rearrange("p (q j) -> p q j", j=2),
            in1=bm.unsqueeze(2).broadcast_to([128, 48, 2]), op=ALU.add), [pst, self.PT_t], [MOD_t])
        og, wg = PC['norm_g']
        for s in range(2):
            g = self.PT[:, og + (layer * 2 + s) * 8: og + (layer * 2 + s) * 8 + 8]
            sh = MOD[:, s * 24 + 0: s * 24 + 8, :]
            scl = MOD[:, s * 24 + 8: s * 24 + 16, :]
            A = AB[:, s, 0, :, :]
            B = AB[:, s, 1, :, :]
            k.emit('dve', lambda e, scl=scl, A=A: e.tensor_scalar(
                out=A, in0=scl, scalar1=1.0, scalar2=32.0, op0=ALU.add, op1=ALU.mult), [MOD_t], [AB_t])
            k.emit('dve', lambda e, A=A, g=g: e.tensor_tensor(
                out=A, in0=A, in1=g.unsqueeze(2).broadcast_to([128, 8, 2]), op=ALU.mult), [AB_t, self.PT_t], [AB_t])
            k.emit('dve', lambda e, B=B, sh=sh: e.tensor_copy(out=B, in_=sh), [MOD_t], [AB_t])

    def gate(self, s, c, half):
        return self.MOD[:, s * 24 + 16 + c, half:half + 1]

    def rmsnorm_mod(self, layer, s, half):
        k = self.k
        with ExitStack() as es:
            sq = self.tmp(es, 'nsq', [128, NCH, 512], F32R)
            sq_t = Trk('nsq')
            tmp = self.tmp(es, 'ntmp', [128, NCH, 512], F32)
            tmp_t = Trk('ntmp')
            rstd = self.tmp(es, 'nrstd', [128, 512], F32)
            rstd_t = Trk('nrstd')
            for tt in range(2):
                xs = self.xT[half][:, :, tt * 512:(tt + 1) * 512]
                xs_t = [self.xT_t[half][c][tt] for c in range(NCH)]
                k.emit('act', lambda e: e.activation(out=sq[:], in_=xs, func=AF.Square), xs_t, [sq_t])
                ps, pst = self.ps[6], self.ps_t[6]
                for c in range(NCH):
                    k.emit('pe', lambda e, c=c: e.matmul(ps[:], self.onesR[:], sq[:, c, :], start=(c == 0), stop=(c == NCH - 1)),
                           [self.onesR_t, sq_t], [pst])
                k.emit('act', lambda e: e.activation(out=rstd[:], in_=ps[:], func=AF.Sqrt, bias=float(D * EPS), scale=1.0),
                       [pst], [rstd_t])
                k.emit('dve', lambda e: e.reciprocal(out=rstd[:], in_=rstd[:]), [rstd_t], [rstd_t])
                k.emit('dve', lambda e: e.tensor_tensor(out=tmp[:], in0=xs, in1=rstd[:].unsqueeze(1).broadcast_to([128, NCH, 512]),
                                                        op=ALU.mult), xs_t + [rstd_t], [tmp_t])
                for c in range(NCH):
                    A = self.AB[:, s, 0, c, half:half + 1]
                    B = self.AB[:, s, 1, c, half:half + 1]
                    out = self.hT[:, c, tt * 512:(tt + 1) * 512]
                    if c % 2 == 0:
                        k.emit('act', lambda e, c=c, A=A, B=B, out=out: e.activation(
                            out=out, in_=tmp[:, c, :], func=AF.Identity, bias=B, scale=A),
                            [tmp_t, self.AB_t], [self.hT_t[c][tt]])
                    else:
                        k.emit('dve', lambda e, c=c, A=A, B=B, out=out: e.tensor_scalar(
                            out=out, in0=tmp[:, c, :], scalar1=A, scalar2=B, op0=ALU.mult, op1=ALU.add),
                            [tmp_t, self.AB_t], [self.hT_t[c][tt]])
            k.barrier()

    def mixer(self, layer, half):
        kind = layer % 3
        ml = self.cfg.get('mixlist', (0, 1, 2))
        if kind not in ml:
            return
        if kind == 0:
            if half == 0:
                self.attn_prep(layer)
            self.attn(layer, half)
        elif kind == 1:
            self.hgrn(layer, half)
        else:
            self.gdn(layer, half)

    def out_proj(self, w_out_rows, src, src_t, nh, half, banks):
        k = self.k
        bi = 0
        pf = PF(self, [[w_out_rows[:, dp * 256:(dp + 1) * 256]] for dp in range(4)])
        for dp in range(4):
            (wo,), wot = pf.get(dp)
            for dmi in range(2):
                dm = dp * 2 + dmi
                for tt in range(2):
                    b = banks[bi % len(banks)]
                    bi += 1
                    ps, pst = self.ps[b], self.ps_t[b]
                    for hl in range(nh):
                        k.emit('pe', lambda e, ps=ps, hl=hl, dmi=dmi, tt=tt: e.matmul(
                            ps[:], wo[:, hl, dmi * 128:(dmi + 1) * 128], src[:, hl, tt * 512:(tt + 1) * 512],
                            start=(hl == 0), stop=(hl == nh - 1)), [wot, src_t[hl]], [pst])
                    xs = self.xT[half][:, dm, tt * 512:(tt + 1) * 512]
                    k.emit('dve', lambda e, ps=ps, xs=xs, dm=dm: e.scalar_tensor_tensor(
                        out=xs, in0=ps[:], scalar=self.gate(0, dm, half), in1=xs, op0=ALU.mult, op1=ALU.add),
                        [pst, self.MOD_t, self.xT_t[half][dm][tt]], [self.xT_t[half][dm][tt]])

    def head_norm(self, tmps, src, src_t, dst, dst_t, gcol, eps_total, bank, extra_mul=None, extra_t=None):
        k = self.k
        sq, sq_t, rs, rs_t = tmps
        ps, pst = self.ps[bank], self.ps_t[bank]
        for tt in range(2):
            sl = slice(tt * 512, (tt + 1) * 512)
            k.emit('act', lambda e, sl=sl: e.activation(out=sq[:], in_=src[:, sl], func=AF.Square), [src_t], [sq_t])
            k.emit('pe', lambda e: e.matmul(ps[:], self.onesR[:], sq[:], start=True, stop=True), [self.onesR_t, sq_t], [pst])
            k.emit('act', lambda e: e.activation(out=rs[:], in_=ps[:], func=AF.Sqrt, bias=float(eps_total), scale=1.0), [pst], [rs_t])
            k.emit('dve', lambda e: e.reciprocal(out=rs[:], in_=rs[:]), [rs_t], [rs_t])
            if extra_mul is not None:
                k.emit('dve', lambda e, sl=sl: e.tensor_tensor(out=rs[:], in0=rs[:], in1=extra_mul[:, sl], op=ALU.mult), [rs_t, extra_t], [rs_t])
            k.emit('dve', lambda e, sl=sl: e.scalar_tensor_tensor(out=dst[:, sl], in0=src[:, sl], scalar=gcol, in1=rs[:], op0=ALU.mult, op1=ALU.mult),
                   [src_t, rs_t, self.PT_t, self.AV_t] + ([self.HV_t] if hasattr(self, 'HV_t') else []), [dst_t])

    def norm_tmps(self, es):
        return (self.tmp(es, 'hsq', [128, 512], F32R), Trk('hsq'), self.tmp(es, 'hrs', [128, 512], F32), Trk('hrs'))

    def attn_prep(self, layer):
        k = self.k
        j = layer // 3
        lam_init = 0.8 - 0.6 * math.exp(-0.3 * layer)
        with ExitStack() as es:
            lt = self.tmp(es, 'ltab', [128, 512], F32)
            lt_t = self.ptrk('ltab')
            k.dma(lt[:], self.d['lamtab'][:, :], writes=[lt_t])
            pr = self.tmp(es, 'lpr', [128, 2, 64], F32)
            pr_t = Trk('lpr')
            sm = self.tmp(es, 'lsm', [128, 2], F32)
            sm_t = Trk('lsm')
            base = j * 256
            lq = lt[:, base:base + 256].rearrange("p (a r n) -> p a r n", a=2, r=2)
            k.emit('dve', lambda e: e.tensor_tensor(out=pr[:], in0=lq[:, :, 0, :], in1=lq[:, :, 1, :], op=ALU.mult), [lt_t], [pr_t])
            k.emit('dve', lambda e: e.reduce_sum(out=sm[:], in_=pr[:], axis=AX.X), [pr_t], [sm_t])
            k.emit('act', lambda e: e.activation(out=sm[:], in_=sm[:], func=AF.Exp), [sm_t], [sm_t])
            k.emit('dve', lambda e: e.tensor_tensor(out=self.AV[:, 0:1], in0=sm[:, 0:1], in1=sm[:, 1:2], op=ALU.subtract), [sm_t], [self.AV_t])
            k.emit('dve', lambda e: e.tensor_scalar(out=self.AV[:, 0:1], in0=self.AV[:, 0:1], scalar1=float(lam_init), scalar2=None, op0=ALU.add),
                   [self.AV_t], [self.AV_t])
            k.emit('dve', lambda e: e.tensor_scalar(out=self.AV[:, 1:2], in0=self.AV[:, 0:1], scalar1=-1.0, scalar2=None, op0=ALU.mult),
                   [self.AV_t], [self.AV_t])
            osl, _ = PC['subln']
            k.emit('dve', lambda e: e.tensor_scalar(out=self.AV[:, 2:3], in0=self.PT[:, osl + j:osl + j + 1],
                                                    scalar1=float((1.0 - lam_init) * math.sqrt(128.0)), scalar2=None, op0=ALU.mult),
                   [self.PT_t, self.AV_t], [self.AV_t])
            k.barrier()

    def attn(self, layer, half):
        k, d, nc = self.k, self.d, self.nc
        j = layer // 3
        w_in = d['attn_w_in']
        scale = 0.125
        nkc = 2 if half == 0 else 12
        with ExitStack() as es:
            GH = 2
            V = self.tmp(es, 'aV', [128, 8, GH * 128], F32R)
            V_t = self.ptrk('aV', 8)
            ntm = self.norm_tmps(es)
            agrp = self.tmp(es, 'agrp', [128, GH, TOK], F32R)
            agrp_t = trks('agrp', GH)
            QT = [self.tmp(es, f'aQ{i}', [128, TOK], F32R) for i in range(1)]
            QT_t = trks('aQ', 1)
            KT = [self.tmp(es, f'aK{i}', [128, TOK], F32R) for i in range(1)]
            KT_t = self.ptrk('aK', 1)
            Pt = [self.tmp(es, f'aP{i}', [128, 512], F32R) for i in range(2)]
            Pt_t = trks('aP', 2)
            att = self.tmp(es, 'att', [128, TOK], F32)
            att_t = Trk('att')
            Rr = self.tmp(es, 'aR', [128, 2, 512], F32)
            Rr_t = Trk('aR')
            Tt, Tt_t = Rr, Rr_t
            if half == 1:
                ropet = self.tmp(es, 'arope', [128, 2048], F32)
                ropet_t = self.ptrk('arope')
                k.dma(ropet[:], d['rope'][:, :], writes=[ropet_t])
                COS = ropet[:, 0:1024]
                SIN = ropet[:, 1024:2048]
                raw = [self.tmp(es, f'araw{i}', [128, 512], F32R) for i in range(1)]
                raw_t = trks('araw', 1)
                ri_ = 0
                t1 = self.tmp(es, 'at1', [128, 512], F32)
                t1_t = Trk('at1')
                kcr = self.tmp(es, 'akcr', [128, 512], F32R)
                kcr_t = Trk('akcr')
                vcr = self.tmp(es, 'avcr', [128, 4 * GH * 128], F32R)
                vcr_t = Trk('avcr')
            pi = 0
            pb = 0
            for grp in range(8 // GH):
                for piece in range(1):
                    c0 = 2 * D + grp * 256
                    (wv,), wvt = self.wpiece([w_in[j, :, c0:c0 + 256]])
                    for tile in range(8):
                        b = 6 + (pb % 2)
                        pb += 1
                        ps, pst = self.ps[b], self.ps_t[b]
                        for c in range(NCH):
                            k.emit('pe', lambda e, ps=ps, c=c, tile=tile: e.matmul(
                                ps[:, 0:256], self.hT[:, c, tile * 128:(tile + 1) * 128], wv[:, c, :],
                                start=(c == 0), stop=(c == NCH - 1)), [wvt, self.hT_t[c][tile // 4]], [pst])
                        self.copy(self.evac_eng(), V[:, tile, piece * 256:(piece + 1) * 256], ps[:, 0:256], [pst], [V_t[tile]])
                if half == 0:
                    for tile in range(8):
                        k.dma(d['vout'][j, tile * 128:(tile + 1) * 128, grp * 256:(grp + 1) * 256], V[:, tile, :].bitcast(F32), reads=[V_t[tile]])
                else:
                    (vcv,), _ = self.wpiece([d['cv'][j, :, grp * 256:(grp + 1) * 256]], dest=(vcr, vcr_t))
                for hl in range(GH):
                    hh = grp * GH + hl
                    qi = 0
                    Q, Q_t, Kk, K_t = QT[qi], QT_t[qi], KT[qi], KT_t[qi]
                    (wq, wk), wt = self.wpiece([w_in[j, :, hh * 128:(hh + 1) * 128],
                                                w_in[j, :, D + hh * 128:D + (hh + 1) * 128]])
                    for wi, (w, dst, dst_t) in enumerate(((wq, Q, Q_t), (wk, Kk, K_t))):
                        for tt in range(2):
                            sl = slice(tt * 512, (tt + 1) * 512)
                            b = 6 + (pb % 2)
                            pb += 1
                            ps, pst = self.ps[b], self.ps_t[b]
                            for c in range(NCH):
                                k.emit('pe', lambda e, ps=ps, w=w, c=c, sl=sl: e.matmul(
                                    ps[:], w[:, c, :], self.hT[:, c, sl],
                                    start=(c == 0), stop=(c == NCH - 1)), [wt, self.hT_t[c][tt]], [pst])
                            if half == 0:
                                self.copy(self.evac_eng(), dst[:, sl], ps[:], [pst], [dst_t])
                            else:
                                rw, rw_t = raw[0], raw_t[0]
                                ri_ += 1
                                self.copy('act', rw[:], ps[:], [pst], [rw_t])
                                b2 = 6 + (pb % 2)
                                pb += 1
                                ps2, ps2t = self.ps[b2], self.ps_t[b2]
                                k.emit('pe', lambda e, ps2=ps2, rw=rw: e.matmul(ps2[:], self.permR[:], rw[:], start=True, stop=True),
                                       [self.permR_t, rw_t], [ps2t])
                                k.emit('pool', lambda e, rw=rw, sl=sl: e.tensor_tensor(out=t1[:], in0=rw[:].bitcast(F32), in1=COS[:, sl], op=ALU.mult),
                                       [rw_t, ropet_t], [t1_t])
                                k.emit('dve', lambda e, ps2=ps2, sl=sl, dst=dst: e.tensor_tensor(out=dst[:, sl], in0=ps2[:], in1=SIN[:, sl], op=ALU.mult),
                                       [ps2t, ropet_t], [dst_t])
                                k.emit('dve', lambda e, dst=dst, sl=sl: e.tensor_tensor(out=dst[:, sl], in0=dst[:, sl].bitcast(F32), in1=t1[:], op=ALU.add),
                                       [t1_t, dst_t], [dst_t])
                    if half == 0:
                        k.dma(d['kout'][j, hh, :, :], Kk[:].bitcast(F32), reads=[K_t])
                    else:
                        self.wpiece([d['ck'][j, hh, :, :]], dest=(kcr, kcr_t))

                    def keyT(comp, kc, s=0):
                        r = slice(comp * 64, (comp + 1) * 64)
                        if half == 0:
                            return Kk[r, s * 256 + kc * 128: s * 256 + (kc + 1) * 128], K_t
                        if kc < 4:
                            return kcr[r, kc * 128:(kc + 1) * 128], kcr_t
                        return Kk[r, (kc - 4) * 128:(kc - 3) * 128], K_t

                    def valT(kc, s=0):
                        cs = slice(hl * 128, (hl + 1) * 128)
                        if half == 0:
                            return V[:, s * 2 + kc, cs], V_t[s * 2 + kc]
                        if kc < 4:
                            return vcv[:, kc, cs], vcr_t
                        return V[:, kc - 4, cs], V_t[kc - 4]

                    if half == 0:
                        qw = 256
                        units = [(s, 0) for s in range(4)]
                    else:
                        qw = 512
                        units = [(0, qt) for qt in range(2)]
                    for (s, qt) in units:
                        q0 = s * 256 if half == 0 else qt * 512
                        for comp in range(2):
                            r = slice(comp * 64, (comp + 1) * 64)
                            if half == 0:
                                ob, zb = 2, 3
                                osl = slice(comp * 256, (comp + 1) * 256)
                            else:
                                ob, zb = 2 + comp * 2, 3 + comp * 2
                                osl = slice(0, 512)
                            pO, pO_t = self.ps[ob], self.ps_t[ob]
                            pZ, pZ_t = self.ps[zb], self.ps_t[zb]
                            if half == 0:
                                sb_ = pi % 2
                                pS, pS_t = self.ps[sb_], self.ps_t[sb_]
                                P_, P_t = Pt[pi % 2], Pt_t[pi % 2]
                                pi += 1
                                for kc in range(2):
                                    kl, kl_t = keyT(comp, kc, s)
                                    k.emit('pe', lambda e, pS=pS, kl=kl, kc=kc, r=r, q0=q0: e.matmul(
                                        pS[:, kc * 256:(kc + 1) * 256], kl, Q[r, q0:q0 + 256], start=True, stop=True),
                                        [kl_t, Q_t], [pS_t])
                                k.emit('act', lambda e, pS=pS, P_=P_: e.activation(out=P_[:], in_=pS[:], func=AF.Exp, scale=scale), [pS_t], [P_t])
                                for kc in range(2):
                                    vl, vl_t = valT(kc, s)
                                    k.emit('pe', lambda e, pO=pO, vl=vl, P_=P_, kc=kc, osl=osl: e.matmul(
                                        pO[:, osl], vl, P_[:, kc * 256:(kc + 1) * 256], start=(kc == 0), stop=(kc == 1)),
                                        [vl_t, P_t], [pO_t])
                                for kc in range(2):
                                    k.emit('pe', lambda e, pZ=pZ, P_=P_, kc=kc, osl=osl: e.matmul(
                                        pZ[:, osl], self.onesR[:], P_[:, kc * 256:(kc + 1) * 256], start=(kc == 0), stop=(kc == 1)),
                                        [self.onesR_t, P_t], [pZ_t])
                            else:
                                for kc in range(nkc):
                                    sb_ = pi % 2
                                    pS, pS_t = self.ps[sb_], self.ps_t[sb_]
                                    P_, P_t = Pt[pi % 2], Pt_t[pi % 2]
                                    pi += 1
                                    kl, kl_t = keyT(comp, kc)
                                    k.emit('pe', lambda e, pS=pS, kl=kl, r=r, q0=q0: e.matmul(
                                        pS[:], kl, Q[r, q0:q0 + 512], start=True, stop=True), [kl_t, Q_t], [pS_t])
                                    k.emit('act', lambda e, pS=pS, P_=P_: e.activation(out=P_[:], in_=pS[:], func=AF.Exp, scale=scale), [pS_t], [P_t])
                                    vl, vl_t = valT(kc)
                                    k.emit('pe', lambda e, pO=pO, vl=vl, P_=P_, kc=kc: e.matmul(
                                        pO[:], vl, P_[:], start=(kc == 0), stop=(kc == nkc - 1)), [vl_t, P_t], [pO_t])
                                    k.emit('pe', lambda e, pZ=pZ, P_=P_, kc=kc: e.matmul(
                                        pZ[:], self.onesR[:], P_[:], start=(kc == 0), stop=(kc == nkc - 1)), [self.onesR_t, P_t], [pZ_t])
                        if half == 0:
                            pO, pO_t, pZ, pZ_t = self.ps[2], self.ps_t[2], self.ps[3], self.ps_t[3]
                            k.emit('dve', lambda e, pZ=pZ: e.reciprocal(out=Rr[:, 0, :], in_=pZ[:]), [pZ_t], [Rr_t])
                            k.emit('dve', lambda e, pO=pO: e.tensor_tensor(out=Tt[:, 0, :], in0=pO[:], in1=Rr[:, 0, :], op=ALU.mult), [pO_t, Rr_t], [Tt_t])
                            k.emit('dve', lambda e, q0=q0: e.scalar_tensor_tensor(
                                out=att[:, q0:q0 + 256], in0=Tt[:, 0, 256:512], scalar=self.AV[:, 1:2], in1=Tt[:, 0, 0:256],
                                op0=ALU.mult, op1=ALU.add), [Tt_t, self.AV_t], [att_t])
                        else:
                            for comp in range(2):
                                pO, pO_t = self.ps[2 + comp * 2], self.ps_t[2 + comp * 2]
                                pZ, pZ_t = self.ps[3 + comp * 2], self.ps_t[3 + comp * 2]
                                k.emit('dve', lambda e, pZ=pZ, comp=comp: e.reciprocal(out=Rr[:, comp, :], in_=pZ[:]), [pZ_t], [Rr_t])
                                k.emit('dve', lambda e, pO=pO, comp=comp: e.tensor_tensor(out=Tt[:, comp, :], in0=pO[:], in1=Rr[:, comp, :], op=ALU.mult),
                                       [pO_t, Rr_t], [Tt_t])
                            k.emit('dve', lambda e, q0=q0: e.scalar_tensor_tensor(
                                out=att[:, q0:q0 + 512], in0=Tt[:, 1, :], scalar=self.AV[:, 1:2], in1=Tt[:, 0, :],
                                op0=ALU.mult, op1=ALU.add), [Tt_t, self.AV_t], [att_t])
                    self.head_norm(ntm, att[:], att_t, agrp[:, hl, :], agrp_t[hl], self.AV[:, 2:3], 128.0 * 1e-5, 6 + (pb % 2))
                    pb += 1
                self.out_proj(d['attn_w_out'][j, grp * GH * 128:(grp + 1) * GH * 128, :], agrp, agrp_t, GH, half, [6, 7, 0, 1])
            k.barrier()

    def hgrn_prep(self, layer):
        k = self.k
        ol, _ = PC['hgrn_lb']
        self.HV = self.k.sb('HV', [128, 3, 8])
        self.HV_t = Trk('HV')
        with ExitStack() as es:
            ex = self.tmp(es, 'hex', [128, 4, 8], F32)
            ex_t = Trk('hex')
            tot = self.tmp(es, 'htot', [128, 8], F32)
            tot_t = Trk('htot')
            k.emit('act', lambda e: e.activation(out=ex[:], in_=self.PT[:, ol:ol + 32].rearrange("p (l c) -> p l c", l=4), func=AF.Exp),
                   [self.PT_t], [ex_t])
            k.emit('dve', lambda e: e.tensor_tensor(out=tot[:], in0=ex[:, 0, :], in1=ex[:, 1, :], op=ALU.add), [ex_t], [tot_t])
            for l in (2, 3):
                k.emit('dve', lambda e, l=l: e.tensor_tensor(out=tot[:], in0=tot[:], in1=ex[:, l, :], op=ALU.add), [ex_t, tot_t], [tot_t])
            k.emit('dve', lambda e: e.reciprocal(out=tot[:], in_=tot[:]), [tot_t], [tot_t])
            k.emit('dve', lambda e: e.tensor_copy(out=self.HV[:, 0, :], in_=ex[:, 1, :]), [ex_t], [self.HV_t])
            for l in range(2, layer + 1):
                k.emit('dve', lambda e, l=l: e.tensor_tensor(out=self.HV[:, 0, :], in0=self.HV[:, 0, :], in1=ex[:, l, :], op=ALU.add),
                       [ex_t, self.HV_t], [self.HV_t])
            k.emit('dve', lambda e: e.tensor_tensor(out=self.HV[:, 0, :], in0=self.HV[:, 0, :], in1=tot[:], op=ALU.mult), [tot_t, self.HV_t], [self.HV_t])
            k.emit('dve', lambda e: e.tensor_scalar(out=self.HV[:, 1, :], in0=self.HV[:, 0, :], scalar1=-1.0, scalar2=1.0, op0=ALU.mult, op1=ALU.add),
                   [self.HV_t], [self.HV_t])
            on, _ = PC['hgrn_norm']
            k.emit('dve', lambda e: e.tensor_scalar(out=self.HV[:, 2, 0:1], in0=self.PT[:, on:on + 1], scalar1=float(math.sqrt(128.0)), scalar2=None, op0=ALU.mult),
                   [self.PT_t, self.HV_t], [self.HV_t])
            k.barrier()

    def hgrn(self, layer, half):
        k, d, nc = self.k, self.d, self.nc
        j = layer // 3
        if half == 0:
            self.hgrn_prep(layer)
        w_in = d['hgrn_w_in']
        nseq = 4 if half == 0 else 1
        cps = 16 // nseq
        oo, _ = CC['ones']
        ONES = self.CT[:, oo:oo + 1].broadcast_to([128, TOK])
        oi, _ = CC['ident']
        IDENT = self.CT[:, oi:oi + 128]
        masks = []
        for nm in ('mask_f', 'mask_b'):
            om, _ = CC[nm]
            masks.append(self.CT[0:64, om:om + 256])
        with ExitStack() as es:
            V64 = self.tmp(es, 'hV', [64, 16, 128], F32)
            V64_t = trks('hV', 16)
            mix = self.tmp(es, 'hmix', [128, 1, TOK], F32R)
            mix_t = trks('hmix', 1)
            ntm = self.norm_tmps(es)
            qT = self.tmp(es, 'hq', [128, TOK], F32)
            qT_t = Trk('hq')
            gs, gs_t = qT, qT_t
            oT = self.tmp(es, 'ho', [128, TOK], F32)
            oT_t = Trk('ho')
            Fb = [self.tmp(es, f'hF{i}', [128, TOK], F32) for i in range(2)]
            Fb_t = trks('hF', 2)
            L = self.tmp(es, 'hL', [128, TOK], F32)
            L_t = Trk('hL')
            Gp = self.tmp(es, 'hGp', [128, 64 + TOK + 64], F32)
            Gp_t = Trk('hGp')
            E1 = self.tmp(es, 'hE1', [128, TOK], F32)
            E1_t = Trk('hE1')
            E2 = self.tmp(es, 'hE2', [128, TOK], F32)
            E2_t = Trk('hE2')
            Am = self.tmp(es, 'hAm', [64, 4, 64], F32)
            Am_t = Trk('hAm')
            Ktok = self.tmp(es, 'hKt', [64, 4, 128], F32)
            Ktok_t = Trk('hKt')
            Sb = [self.tmp(es, f'hS{i}', [128, 128], F32) for i in range(2)]
            Sb_t = trks('hS', 2)
            DK = self.tmp(es, 'hDK', [128, 3, 16], F32)
            DK_t = Trk('hDK')
            G3 = self.tmp(es, 'hG3', [128, 3, 16], F32)
            G3_t = Trk('hG3')
            Sp = [self.tmp(es, f'hSp{i}', [128, 128], F32) for i in range(2)]
            Sp_t = trks('hSp', 2)
            tS = self.tmp(es, 'htS', [128, 128], F32)
            tS_t = Trk('htS')
            spi = 0
            sout_t = self.ptrk('hso')
            if half == 0:
                sout = self.tmp(es, 'hso', [128, 4, 2, 128], F32)
            k.emit('dve', lambda e: e.memset(Gp[:], 0.0), [], [Gp_t])
            pb = 0
            for pair in range(4):
                for hl in range(2):
                    hh = pair * 2 + hl
                    (wi,), wit = self.wpiece([w_in[j, :, 3 * D + hh * 128:3 * D + (hh + 1) * 128]])
                    for tt in range(2):
                        sl = slice(tt * 512, (tt + 1) * 512)
                        b = 6 + (pb % 2)
                        pb += 1
                        ps, pst = self.ps[b], self.ps_t[b]
                        for c in range(NCH):
                            k.emit('pe', lambda e, ps=ps, c=c, sl=sl: e.matmul(ps[:], wi[:, c, :], self.hT[:, c, sl], start=(c == 0), stop=(c == NCH - 1)),
                                   [wit, self.hT_t[c][tt]], [pst])
                        self.copy(self.evac_eng(), L[:, sl], ps[:], [pst], [L_t])
                    for g4 in range(4):
                        b = 6 + (pb % 2)
                        pb += 1
                        ps, pst = self.ps[b], self.ps_t[b]
                        for q_ in range(4):
                            ch = g4 * 4 + q_
                            k.emit('pe', lambda e, ps=ps, q_=q_, ch=ch: e.transpose(ps[0:64, q_ * 128:(q_ + 1) * 128], L[:, ch * 64:(ch + 1) * 64], IDENT),
                                   [L_t, self.CT_t], [pst])
                        self.copy(self.evac_eng(), V64[:, g4 * 4:(g4 + 1) * 4, :].rearrange("p a n -> p (a n)"), ps[0:64, :], [pst], V64_t[g4 * 4:(g4 + 1) * 4])
                    lbc = self.HV[:, 0, hh:hh + 1]
                    omc = self.HV[:, 1, hh:hh + 1]
                    (wq, wzf), wt1 = self.wpiece([w_in[j, :, hh * 128:(hh + 1) * 128], w_in[j, :, D + hh * 128:D + (hh + 1) * 128]])
                    (wzb,), wt2 = self.wpiece([w_in[j, :, 2 * D + hh * 128:2 * D + (hh + 1) * 128]])
                    for (w, wt, kindp) in ((wq, wt1, 'q'), (wzf, wt1, 'zf'), (wzb, wt2, 'zb')):
                        for tt in range(2):
                            sl = slice(tt * 512, (tt + 1) * 512)
                            b = 6 + (pb % 2)
                            pb += 1
                            ps, pst = self.ps[b], self.ps_t[b]
                            for c in range(NCH):
                                k.emit('pe', lambda e, ps=ps, w=w, c=c, sl=sl: e.matmul(
                                    ps[:], w[:, c, :], self.hT[:, c, sl], start=(c == 0), stop=(c == NCH - 1)),
                                    [wt, self.hT_t[c][tt]], [pst])
                            if kindp == 'q':
                                k.emit('act', lambda e, ps=ps, sl=sl: e.activation(out=qT[:, sl], in_=ps[:], func=AF.Copy, scale=float(128.0 ** -0.5)),
                                       [pst], [qT_t])
                            elif kindp == 'g':
                                k.emit('act', lambda e, ps=ps, sl=sl: e.activation(out=gs[:, sl], in_=ps[:], func=AF.Silu), [pst], [gs_t])
                            else:
                                di = 0 if kindp == 'zf' else 1
                                k.emit('act', lambda e, ps=ps, sl=sl, di=di: e.activation(out=Fb[di][:, sl], in_=ps[:], func=AF.Sigmoid), [pst], [Fb_t[di]])
                    for di in range(2):
                        F_, F_t = Fb[di], Fb_t[di]
                        k.emit('dve', lambda e, F_=F_: e.tensor_scalar(out=F_[:], in0=F_[:], scalar1=omc, scalar2=lbc, op0=ALU.mult, op1=ALU.add),
                               [F_t, self.HV_t], [F_t])
                        k.emit('act', lambda e, F_=F_: e.activation(out=L[:], in_=F_[:], func=AF.Ln), [F_t], [L_t])
                        k.emit('pool', lambda e, F_=F_: e.tensor_scalar(out=F_[:], in0=F_[:], scalar1=-1.0, scalar2=1.0, op0=ALU.mult, op1=ALU.add),
                               [F_t], [F_t])
                        k.emit('dve', lambda e: e.tensor_tensor_scan(out=Gp[:, 64:64 + TOK], data0=ONES, data1=L[:], initial=0.0,
                                                                     op0=ALU.mult, op1=ALU.add), [L_t, self.CT_t], [Gp_t])
                        Lv = L[:].rearrange("p (j n) -> p j n", n=64)
                        if di == 0:
                            gprev = Gp[:, 63:63 + TOK].rearrange("p (j n) -> p j n", n=64)[:, :, 0:1].broadcast_to([128, 16, 64])
                            gcur = Gp[:, 64:64 + TOK].rearrange("p (j n) -> p j n", n=64)
                            k.emit('dve', lambda e: e.tensor_tensor(out=Lv, in0=gcur, in1=gprev, op=ALU.subtract), [Gp_t], [L_t])
                        else:
                            gend = Gp[:, 127:127 + TOK].rearrange("p (j n) -> p j n", n=64)[:, :, 0:1].broadcast_to([128, 16, 64])
                            gsh = Gp[:, 63:63 + TOK].rearrange("p (j n) -> p j n", n=64)
                            k.emit('dve', lambda e: e.tensor_tensor(out=Lv, in0=gend, in1=gsh, op=ALU.subtract), [Gp_t], [L_t])
                        pos = 63 if di == 0 else 0
                        mid = 31 if di == 0 else 32
                        k.emit('pool', lambda e, pos=pos: e.tensor_copy(out=G3[:, 0, :], in_=Lv[:, :, pos]), [L_t], [G3_t])
                        k.emit('pool', lambda e, mid=mid: e.tensor_copy(out=G3[:, 1, :], in_=Lv[:, :, mid]), [L_t], [G3_t])
                        k.emit('dve', lambda e: e.tensor_tensor(out=G3[:, 2, :], in0=G3[:, 0, :], in1=G3[:, 1, :], op=ALU.subtract), [G3_t], [G3_t])
                        k.emit('act', lambda e: e.activation(out=DK[:], in_=G3[:], func=AF.Exp), [G3_t], [DK_t])
                        k.emit('dve', lambda e: e.tensor_tensor(out=Lv, in0=Lv, in1=G3[:, 1, :].unsqueeze(2).broadcast_to([128, 16, 64]), op=ALU.subtract),
                               [L_t, G3_t], [L_t])
                        k.emit('act', lambda e: e.activation(out=E1[:], in_=L[:], func=AF.Exp), [L_t], [E1_t])
                        k.emit('act', lambda e: e.activation(out=E2[:], in_=L[:], func=AF.Exp, scale=-1.0), [L_t], [E2_t])
                        k.emit('dve', lambda e: e.tensor_tensor(out=E1[:], in0=E1[:], in1=qT[:], op=ALU.mult), [E1_t, qT_t], [E1_t])
                        k.emit('pool', lambda e, F_=F_: e.tensor_tensor(out=E2[:], in0=E2[:], in1=F_[:], op=ALU.mult), [E2_t, F_t], [E2_t])
                        order = list(range(16)) if di == 0 else list(range(15, -1, -1))
                        if half == 1:
                            k.dma(Sb[0][:], d['st_hgrn'][di, hh, :, :], writes=[Sb_t[0]])
                        si = 0
                        for gi in range(4):
                            chs = order[gi * 4:(gi + 1) * 4]
                            lo = min(chs)
                            psA, psA_t = self.ps[0], self.ps_t[0]
                            psT, psT_t = self.ps[1], self.ps_t[1]
                            for ch in chs:
                                cs = slice(ch * 64, (ch + 1) * 64)
                                q_ = ch - lo
                                k.emit('pe', lambda e, cs=cs, q_=q_: e.matmul(psA[0:64, q_ * 64:(q_ + 1) * 64], E2[:, cs], E1[:, cs], start=True, stop=True),
                                       [E1_t, E2_t], [psA_t])
                                k.emit('pe', lambda e, cs=cs, q_=q_: e.transpose(psT[0:64, q_ * 128:(q_ + 1) * 128], E2[:, cs], IDENT),
                                       [E2_t, self.CT_t], [psT_t])
                            k.emit('dve', lambda e, di=di: e.tensor_tensor(out=Am[:].rearrange("p a n -> p (a n)"), in0=psA[0:64, 0:256], in1=masks[di], op=ALU.mult),
                                   [psA_t, self.CT_t], [Am_t])
                            k.emit('act', lambda e: e.activation(out=Ktok[:].rearrange("p a n -> p (a n)"), in_=psT[0:64, :], func=AF.Copy), [psT_t], [Ktok_t])
                            psO, psO_t = self.ps[2], self.ps_t[2]
                            for ch in chs:
                                cs = slice(ch * 64, (ch + 1) * 64)
                                q_ = ch - lo
                                loc = ch % cps
                                first = (loc == 0) if di == 0 else (loc == cps - 1)
                                last = (loc == cps - 1) if di == 0 else (loc == 0)
                                seq = ch // cps
                                zero_init = first and half == 0
                                vv = V64[:, ch, :]
                                S_prev, S_prev_t = Sb[si], Sb_t[si]
                                k.emit('pe', lambda e, vv=vv, q_=q_, zero_init=zero_init: e.matmul(
                                    psO[:, q_ * 64:(q_ + 1) * 64], vv, Am[:, q_, :], start=True, stop=zero_init), [V64_t[ch], Am_t], [psO_t])
                                if not zero_init:
                                    sp_, sp_t = Sp[spi % 2], Sp_t[spi % 2]
                                    spi += 1
                                    k.emit('pool', lambda e, sp_=sp_, S_prev=S_prev, ch=ch: e.tensor_scalar(
                                        out=sp_[:], in0=S_prev[:], scalar1=DK[:, 1, ch:ch + 1], scalar2=None, op0=ALU.mult), [S_prev_t, DK_t], [sp_t])
                                    k.emit('pe', lambda e, cs=cs, q_=q_, sp_=sp_: e.matmul(
                                        psO[:, q_ * 64:(q_ + 1) * 64], sp_[:], E1[:, cs], start=False, stop=True), [sp_t, E1_t], [psO_t])
                                bS = 3 + (pb % 2)
                                pb += 1
                                psS, psS_t = self.ps[bS], self.ps_t[bS]
                                k.emit('pe', lambda e, psS=psS, vv=vv, q_=q_: e.matmul(
                                    psS[:, 0:128], Ktok[:, q_, :], vv, start=True, stop=True), [Ktok_t, V64_t[ch]], [psS_t])
                                if last and half == 0:
                                    dstS, dstS_t = sout[:, seq, di, :], sout_t
                                else:
                                    si = 1 - si
                                    dstS, dstS_t = Sb[si][:], Sb_t[si]
                                if zero_init:
                                    k.emit('act', lambda e, psS=psS, ch=ch, dstS=dstS: e.activation(
                                        out=dstS, in_=psS[:, 0:128], func=AF.Identity, scale=DK[:, 2, ch:ch + 1]), [psS_t, DK_t], [dstS_t])
                                else:
                                    k.emit('act', lambda e, psS=psS, ch=ch: e.activation(
                                        out=tS[:], in_=psS[:, 0:128], func=AF.Identity, scale=DK[:, 2, ch:ch + 1]), [psS_t, DK_t], [tS_t])
                                    k.emit('dve', lambda e, dstS=dstS, S_prev=S_prev, ch=ch: e.scalar_tensor_tensor(
                                        out=dstS, in0=S_prev[:], scalar=DK[:, 0, ch:ch + 1], in1=tS[:], op0=ALU.mult, op1=ALU.add),
                                        [S_prev_t, DK_t, tS_t], [dstS_t])
                            osl = slice(lo * 64, lo * 64 + 256)
                            if di == 0:
                                k.emit('dve', lambda e, osl=osl: e.tensor_copy(out=oT[:, osl], in_=psO[:, 0:256]), [psO_t], [oT_t])
                            else:
                                k.emit('dve', lambda e, osl=osl: e.tensor_tensor(out=oT[:, osl], in0=oT[:, osl], in1=psO[:, 0:256], op=ALU.add), [psO_t, oT_t], [oT_t])
                    if half == 0:
                        k.dma(d['hgout'][:, :, hh, :, :].rearrange("s d k e -> k s d e"), sout[:], reads=[sout_t])
                    (wg,), wt3 = self.wpiece([w_in[j, :, 4 * D + hh * 128:4 * D + (hh + 1) * 128]])
                    for tt in range(2):
                        sl = slice(tt * 512, (tt + 1) * 512)
                        ps, pst = self.ps[6 + tt], self.ps_t[6 + tt]
                        for c in range(NCH):
                            k.emit('pe', lambda e, ps=ps, c=c, sl=sl: e.matmul(ps[:], wg[:, c, :], self.hT[:, c, sl], start=(c == 0), stop=(c == NCH - 1)),
                                   [wt3, self.hT_t[c][tt]], [pst])
                        k.emit('act', lambda e, ps=ps, sl=sl: e.activation(out=gs[:, sl], in_=ps[:], func=AF.Silu), [pst], [gs_t])
                    self.head_norm(ntm, oT[:], oT_t, mix[:, 0, :], mix_t[0], self.HV[:, 2, 0:1], 128.0 * 1e-6, 5, extra_mul=gs[:], extra_t=gs_t)
                    self.out_proj(d['hgrn_w_out'][j, hh * 128:(hh + 1) * 128, :], mix, mix_t, 1, half, [6, 7])
            k.barrier()

    def gdn(self, layer, half):
        k, d, nc = self.k, self.d, self.nc
        w_in = d['gdn_w_in']
        nseq = 4 if half == 0 else 1
        cps = 16 // nseq
        seqlen = TOK // nseq
        CT = self.CT

        def cc(name, rows=64, w=None):
            o_, w_ = CC[name]
            return CT[0:rows, o_:o_ + (w or w_)]

        ONES_ROW = cc('ones', 128, 1).broadcast_to([128, TOK])
        IDENT = cc('ident', 128)
        ID64 = cc('ident', 64, 64)
        TRIF = cc('mask_f', 64, 64)
        TRIB = cc('mask_b', 64, 64)
        ONES64 = cc('ones', 64, 64)
        NEGU = [cc('negu_f'), cc('negu_b')]
        NEGL = [cc('negl_f'), cc('negl_b')]
        SNEG = [cc('sneg_f'), cc('sneg_b')]
        IDREP = cc('idrep')
        osel, _ = CC['sel']
        onsel, _ = CC['nsel']

        def SEL(kk, m):
            return CT[0:4, osel + kk * 128: osel + kk * 128 + m]

        def NSEL(kk):
            return CT[0:4, onsel + kk * 64: onsel + (kk + 1) * 64]

        oa, _ = PC['gdn_alog_col']
        odt, _ = PC['gdn_dt_col']
        oar, _ = PC['gdn_alog_row']
        odr, _ = PC['gdn_dt_row']
        ocv, _ = PC['gdn_conv']
        ogn, _ = PC['gdn_norm']
        scr_t = self.ptrk('gscr')
        with ExitStack() as es0:
            GC = self.tmp(es0, 'gGC', [64, 16, 16]); BE = self.tmp(es0, 'gBE', [64, 16, 16])
            NBE = self.tmp(es0, 'gNBE', [64, 16, 16]); C1 = self.tmp(es0, 'gC1', [64, 16, 16])
            WW = self.tmp(es0, 'gWW', [64, 16, 16])
            TB_t = Trk('gTB')
            GV = self.tmp(es0, 'gGV', [128, 20])
            GV_t = Trk('gGV')
            k.emit('act', lambda e: e.activation(out=GV[:, 0:1], in_=self.PT[:, oa:oa + 1], func=AF.Exp), [self.PT_t], [GV_t])
            k.emit('dve', lambda e: e.tensor_scalar(out=GV[:, 0:1], in0=GV[:, 0:1], scalar1=-1.0, scalar2=None, op0=ALU.mult), [GV_t], [GV_t])
            k.emit('act', lambda e: e.activation(out=GV[:, 4:20], in_=self.PT[:, oar:oar + 16], func=AF.Exp), [self.PT_t], [GV_t])
            k.emit('dve', lambda e: e.tensor_scalar(out=GV[:, 4:20], in0=GV[:, 4:20], scalar1=-1.0, scalar2=None, op0=ALU.mult), [GV_t], [GV_t])
            k.emit('dve', lambda e: e.tensor_scalar(out=GV[:, 1:2], in0=self.PT[:, ogn:ogn + 1], scalar1=float(math.sqrt(128.0)), scalar2=None, op0=ALU.mult),
                   [self.PT_t, GV_t], [GV_t])
            (wab,), wab_t = self.wpiece([w_in[0, :, 4 * D:4 * D + 32]], rounded=False)
            with ExitStack() as es:
                LA = self.tmp(es, 'gLA', [16, TOK]); LA_t = Trk('gLA')
                Gp = self.tmp(es, 'gGp', [16, 64 + TOK + 64]); Gp_t = Trk('gGp')
                GF = self.tmp(es, 'gGF', [16, TOK]); GF_t = self.ptrk('gGF')
                GB = self.tmp(es, 'gGB', [16, TOK]); GB_t = self.ptrk('gGB')
                BT = self.tmp(es, 'gBT', [16, TOK]); BT_t = self.ptrk('gBT')
                LAt = self.tmp(es, 'gLAt', [64, 16, 16]); LAt_t = Trk('gLAt')
                k.emit('dve', lambda e: e.memset(Gp[:], 0.0), [], [Gp_t])
                hTf = self.hT[:].bitcast(F32)
                for part in range(2):
                    for tt in range(2):
                        sl = slice(tt * 512, (tt + 1) * 512)
                        ps, pst = self.ps[6 + tt], self.ps_t[6 + tt]
                        for c in range(NCH):
                            k.emit('pe', lambda e, ps=ps, c=c, sl=sl, part=part: e.matmul(
                                ps[0:16, :], wab[:, c, part * 16:(part + 1) * 16], hTf[:, c, sl], start=(c == 0), stop=(c == NCH - 1)),
                                [wab_t, self.hT_t[c][tt]], [pst])
                        if part == 0:
                            k.emit('act', lambda e, ps=ps, sl=sl: e.activation(out=LA[:, sl], in_=ps[0:16, :], func=AF.Exp, bias=self.PT[0:16, odt:odt + 1]),
                                   [pst, self.PT_t], [LA_t])
                        else:
                            k.emit('act', lambda e, ps=ps, sl=sl: e.activation(out=BT[:, sl], in_=ps[0:16, :], func=AF.Sigmoid), [pst], [BT_t])
                k.emit('act', lambda e: e.activation(out=LA[:], in_=LA[:], func=AF.Ln, bias=1.0), [LA_t], [LA_t])
                k.emit('dve', lambda e: e.tensor_scalar(out=LA[:], in0=LA[:], scalar1=GV[0:16, 0:1], scalar2=None, op0=ALU.mult), [LA_t, GV_t], [LA_t])
                k.emit('dve', lambda e: e.tensor_tensor_scan(out=Gp[:, 64:64 + TOK], data0=ONES_ROW[0:16, :], data1=LA[:], initial=0.0,
                                                             op0=ALU.mult, op1=ALU.add), [LA_t, self.CT_t], [Gp_t])
                gprev = Gp[:, 63:63 + TOK].rearrange("p (j n) -> p j n", n=64)[:, :, 0:1].broadcast_to([16, 16, 64])
                gcur = Gp[:, 64:64 + TOK].rearrange("p (j n) -> p j n", n=64)
                k.emit('dve', lambda e: e.tensor_tensor(out=GF[:].rearrange("p (j n) -> p j n", n=64), in0=gcur, in1=gprev, op=ALU.subtract), [Gp_t], [GF_t])
                gend = Gp[:, 127:127 + TOK].rearrange("p (j n) -> p j n", n=64)[:, :, 0:1].broadcast_to([16, 16, 64])
                gsh = Gp[:, 63:63 + TOK].rearrange("p (j n) -> p j n", n=64)
                k.emit('dve', lambda e: e.tensor_tensor(out=GB[:].rearrange("p (j n) -> p j n", n=64), in0=gend, in1=gsh, op=ALU.subtract), [Gp_t], [GB_t])
                k.dma(d['gscr'][0:16, :], GF[:], reads=[GF_t], writes=[scr_t])
                k.dma(d['gscr'][16:32, :], GB[:], reads=[GB_t], writes=[scr_t])
                k.dma(d['gscr'][32:48, :], BT[:], reads=[BT_t], writes=[scr_t])
                ps, pst = self.ps[5], self.ps_t[5]
                for ch in range(16):
                    for c in range(NCH):
                        k.emit('pe', lambda e, c=c, ch=ch: e.matmul(
                            ps[0:64, ch * 32:(ch + 1) * 32], hTf[:, c, ch * 64:(ch + 1) * 64], wab[:, c, :], start=(c == 0), stop=(c == NCH - 1)),
                            [wab_t, self.hT_t[c][ch // 8]], [pst])
                pv = ps[0:64, :].rearrange("p (c n) -> p c n", n=32)
                k.emit('dve', lambda e: e.tensor_tensor(out=LAt[:], in0=pv[:, :, 0:16],
                                                        in1=self.PT[0:64, odr:odr + 16].unsqueeze(1).broadcast_to([64, 16, 16]), op=ALU.add),
                       [pst, self.PT_t], [LAt_t])
                k.emit('act', lambda e: e.activation(out=BE[:], in_=pv[:, :, 16:32], func=AF.Sigmoid), [pst], [TB_t])
                k.emit('act', lambda e: e.activation(out=LAt[:], in_=LAt[:], func=AF.Exp), [LAt_t], [LAt_t])
                k.emit('act', lambda e: e.activation(out=LAt[:], in_=LAt[:], func=AF.Ln, bias=1.0), [LAt_t], [LAt_t])
                k.emit('dve', lambda e: e.tensor_tensor(out=LAt[:], in0=LAt[:], in1=GV[0:64, 4:20].unsqueeze(1).broadcast_to([64, 16, 16]), op=ALU.mult),
                       [LAt_t, GV_t], [LAt_t])
                LAf = LAt[:].rearrange("p c n -> p (c n)")
                pF, pF_t = self.ps[0], self.ps_t[0]
                pB, pB_t = self.ps[1], self.ps_t[1]
                pT, pT_t = self.ps[2], self.ps_t[2]
                k.emit('pe', lambda e: e.matmul(pF[0:64, 0:256], TRIF, LAf, start=True, stop=True), [LAt_t, self.CT_t], [pF_t])
                k.emit('pe', lambda e: e.matmul(pB[0:64, 0:256], TRIB, LAf, start=True, stop=True), [LAt_t, self.CT_t], [pB_t])
                k.emit('pe', lambda e: e.matmul(pT[0:64, 0:256], ONES64, LAf, start=True, stop=True), [LAt_t, self.CT_t], [pT_t])
                pFv = pF[0:64, 0:256].rearrange("p (c n) -> p c n", n=16)
                pBv = pB[0:64, 0:256].rearrange("p (c n) -> p c n", n=16)
                pTv = pT[0:64, 0:256].rearrange("p (c n) -> p c n", n=16)
                k.emit('dve', lambda e: e.tensor_copy(out=GC[:, :, 0:8], in_=pFv[:, :, 0:8]), [pF_t], [TB_t])
                k.emit('dve', lambda e: e.tensor_copy(out=GC[:, :, 8:16], in_=pBv[:, :, 8:16]), [pB_t], [TB_t])
                k.emit('dve', lambda e: e.tensor_tensor(out=WW[:], in0=pTv, in1=GC[:], op=ALU.subtract), [pT_t, TB_t], [TB_t])
                k.emit('act', lambda e: e.activation(out=WW[:], in_=WW[:], func=AF.Exp), [TB_t], [TB_t])
                k.emit('act', lambda e: e.activation(out=C1[:], in_=GC[:], func=AF.Exp), [TB_t], [TB_t])
                k.emit('dve', lambda e: e.scalar_tensor_tensor(out=C1[:], in0=C1[:], scalar=-1.0, in1=BE[:], op0=ALU.mult, op1=ALU.mult), [TB_t], [TB_t])
                k.emit('dve', lambda e: e.tensor_scalar(out=NBE[:], in0=BE[:], scalar1=-1.0, scalar2=None, op0=ALU.mult), [TB_t], [TB_t])
                k.barrier()
            stop = self.cfg.get('gdn_stop', 9)
            if stop <= 1:
                return
            with ExitStack() as es:
                qn = self.tmp(es, 'gq', [128, TOK]); qn_t = Trk('gq')
                kn = self.tmp(es, 'gk', [128, TOK]); kn_t = Trk('gk')
                vT = self.tmp(es, 'gv', [128, TOK]); vT_t = Trk('gv')
                oT = self.tmp(es, 'go', [128, TOK]); oT_t = Trk('go')
                Qt = self.tmp(es, 'gQt', [128, TOK]); Qt_t = Trk('gQt')
                mix = self.tmp(es, 'gmix', [128, 1, TOK], F32R); mix_t = trks('gmix', 1)
                ntm = self.norm_tmps(es)
                HR4 = self.tmp(es, 'gHR', [4, TOK]); HR4_t = self.ptrk('gHR')
                bt = [self.tmp(es, f'gb{i}', [64, 4, 64]) for i in range(10)]
                bt_t = trks('gb', 10)
                ktok = self.tmp(es, 'gkt', [64, 4, 128]); ktok_t = Trk('gkt')
                vtok = self.tmp(es, 'gvt', [64, 4, 128]); vtok_t = Trk('gvt')
                sm = [self.tmp(es, f'gs{i}', [64, 128]) for i in range(4)]
                sm_t = trks('gs', 4)
                Sb = [self.tmp(es, f'gS{i}', [128, 128]) for i in range(2)]
                Sb_t = trks('gS', 2)
                DKg = self.tmp(es, 'gDK', [128, 16]); DKg_t = Trk('gDK')
                sout_t = self.ptrk('gso')
                if half == 0:
                    sout = self.tmp(es, 'gso', [128, 4, 2, 128])
                pb = 0
                for hh in range(8):
                    (wq, wk), wt1 = self.wpiece([w_in[0, :, hh * 128:(hh + 1) * 128], w_in[0, :, D + hh * 128:D + (hh + 1) * 128]])
                    (wv,), wt2 = self.wpiece([w_in[0, :, 2 * D + hh * 128:2 * D + (hh + 1) * 128]])
                    k.dma_group([(HR4[0:1, :], d['gscr'][hh:hh + 1, :]), (HR4[1:2, :], d['gscr'][24 + hh:25 + hh, :]),
                                 (HR4[2:3, :], d['gscr'][32 + hh:33 + hh, :]), (HR4[3:4, :], d['gscr'][40 + hh:41 + hh, :])], [HR4_t])
                    HR4_t.rs[scr_t.w[0]] = 0
                    for ti, (w, wt, dst, dst_t) in enumerate(((wq, wt1, qn, qn_t), (wk, wt1, kn, kn_t), (wv, wt2, vT, vT_t))):
                        fch = ti * 8 + hh
                        w0 = self.PT[:, ocv + 0 * 24 + fch: ocv + 0 * 24 + fch + 1]
                        w1 = self.PT[:, ocv + 1 * 24 + fch: ocv + 1 * 24 + fch + 1]
                        w2 = self.PT[:, ocv + 2 * 24 + fch: ocv + 2 * 24 + fch + 1]
                        pss = []
                        for tt in range(2):
                            sl = slice(tt * 512, (tt + 1) * 512)
                            b = 6 + tt
                            ps, pst = self.ps[b], self.ps_t[b]
                            pss.append((ps, pst))
                            for c in range(NCH):
                                k.emit('pe', lambda e, ps=ps, w=w, c=c, sl=sl: e.matmul(
                                    ps[:], w[:, c, :], self.hT[:, c, sl], start=(c == 0), stop=(c == NCH - 1)), [wt, self.hT_t[c][tt]], [pst])
                            k.emit('act', lambda e, ps=ps, sl=sl, dst=dst, w1=w1: e.activation(out=dst[:, sl], in_=ps[:], func=AF.Copy, scale=w1),
                                   [pst, self.PT_t], [dst_t])
                        for tt in range(2):
                            ps, pst = pss[tt]
                            sl_ = min(seqlen, 512)
                            ns = 512 // sl_
                            pv = ps[:].rearrange("p (s n) -> p s n", s=ns)
                            av = dst[:, tt * 512:(tt + 1) * 512].rearrange("p (s n) -> p s n", s=ns)
                            k.emit('dve', lambda e, pv=pv, av=av, w0=w0, sl_=sl_: e.scalar_tensor_tensor(
                                out=av[:, :, 1:sl_], in0=pv[:, :, 0:sl_ - 1], scalar=w0, in1=av[:, :, 1:sl_], op0=ALU.mult, op1=ALU.add),
                                [pst, self.PT_t, dst_t], [dst_t])
                            k.emit('dve', lambda e, pv=pv, av=av, w2=w2, sl_=sl_: e.scalar_tensor_tensor(
                                out=av[:, :, 0:sl_ - 1], in0=pv[:, :, 1:sl_], scalar=w2, in1=av[:, :, 0:sl_ - 1], op0=ALU.mult, op1=ALU.add),
                                [pst, self.PT_t, dst_t], [dst_t])
                        if seqlen > 512:
                            p0, p0t = pss[0]
                            p1, p1t = pss[1]
                            k.emit('dve', lambda e, p0=p0, dst=dst, w0=w0: e.scalar_tensor_tensor(
                                out=dst[:, 512:513], in0=p0[:, 511:512], scalar=w0, in1=dst[:, 512:513], op0=ALU.mult, op1=ALU.add),
                                [p0t, self.PT_t, dst_t], [dst_t])
                            k.emit('dve', lambda e, p1=p1, dst=dst, w2=w2: e.scalar_tensor_tensor(
                                out=dst[:, 511:512], in0=p1[:, 0:1], scalar=w2, in1=dst[:, 511:512], op0=ALU.mult, op1=ALU.add),
                                [p1t, self.PT_t, dst_t], [dst_t])
                        k.emit('act', lambda e, dst=dst: e.activation(out=dst[:], in_=dst[:], func=AF.Silu), [dst_t], [dst_t])
                        if ti < 2:
                            sq, sq_t, rs, rs_t = ntm
                            for tt in range(2):
                                sl = slice(tt * 512, (tt + 1) * 512)
                                ps, pst = self.ps[5], self.ps_t[5]
                                k.emit('act', lambda e, dst=dst, sl=sl: e.activation(out=sq[:], in_=dst[:, sl], func=AF.Square), [dst_t], [sq_t])
                                k.emit('pe', lambda e, ps=ps: e.matmul(ps[:], self.onesR[:], sq[:], start=True, stop=True), [self.onesR_t, sq_t], [pst])
                                k.emit('act', lambda e, ps=ps: e.activation(out=rs[:], in_=ps[:], func=AF.Sqrt, bias=1e-6, scale=1.0), [pst], [rs_t])
                                k.emit('dve', lambda e: e.reciprocal(out=rs[:], in_=rs[:]), [rs_t], [rs_t])
                                sc_ = float(128.0 ** -0.5) if ti == 0 else 1.0
                                k.emit('dve', lambda e, dst=dst, sl=sl, sc_=sc_: e.scalar_tensor_tensor(
                                    out=dst[:, sl], in0=dst[:, sl], scalar=sc_, in1=rs[:], op0=ALU.mult, op1=ALU.mult), [dst_t, rs_t], [dst_t])
                    if stop <= 2:
                        continue
                    for di in range(2):
                        col = di * 8 + hh
                        for tt in range(2):
                            sl = slice(tt * 512, (tt + 1) * 512)
                            ps, pst = self.ps[6 + tt], self.ps_t[6 + tt]
                            k.emit('pe', lambda e, ps=ps, sl=sl, di=di: e.matmul(ps[:], SEL(di, 128), HR4[0:4, sl], start=True, stop=True),
                                   [HR4_t, self.CT_t], [pst])
                            k.emit('act', lambda e, ps=ps, sl=sl: e.activation(out=Qt[:, sl], in_=ps[:], func=AF.Exp), [pst], [Qt_t])
                        pos = 63 if di == 0 else 0
                        k.emit('pool', lambda e, pos=pos: e.tensor_copy(out=DKg[:], in_=Qt[:].rearrange("p (j n) -> p j n", n=64)[:, :, pos]), [Qt_t], [DKg_t])
                        k.emit('dve', lambda e: e.tensor_tensor(out=Qt[:], in0=Qt[:], in1=qn[:], op=ALU.mult), [Qt_t, qn_t], [Qt_t])
                        border = list(range(4)) if di == 0 else list(range(3, -1, -1))
                        if half == 1:
                            k.dma(Sb[0][:], d['st_gdn'][di, hh, :, :], writes=[Sb_t[0]])
                        si = 0
                        for bi in border:
                            chs = [bi * 4 + q for q in range(4)]
                            if di == 1:
                                chs = chs[::-1]
                            T0 = bi * 256
                            bsl = slice(T0, T0 + 256)
                            gcol = GC[:, bi * 4:bi * 4 + 4, col:col + 1].broadcast_to([64, 4, 64])
                            nbcol = NBE[:, bi * 4:bi * 4 + 4, col:col + 1].broadcast_to([64, 4, 64])
                            b0, b0t = self.ps[0], self.ps_t[0]
                            b1, b1t = self.ps[1], self.ps_t[1]
                            b2, b2t = self.ps[2], self.ps_t[2]
                            b3, b3t = self.ps[3], self.ps_t[3]
                            R = lambda ps: ps[0:64, 0:256]
                            R3 = lambda ps: ps[0:64, 0:256].rearrange("p (a n) -> p a n", a=4)
                            F2 = lambda t: t[:].rearrange("p a n -> p (a n)")
                            deps_c = [HR4_t, self.CT_t]
                            k.emit('pe', lambda e, di=di: e.matmul(R(b0), SEL(di, 64), HR4[0:4, bsl], start=True, stop=True), deps_c, [b0t])
                            k.emit('pe', lambda e, di=di: e.matmul(R(b2), SEL(2 + di, 64), HR4[0:4, bsl], start=True, stop=True), deps_c, [b2t])
                            DT, DT_t = bt[0], bt_t[0]
                            Dl, Dl_t = bt[1], bt_t[1]
                            DBT, DBT_t = bt[2], bt_t[2]
                            k.emit('dve', lambda e: e.tensor_tensor(out=DT[:], in0=R3(b0), in1=gcol, op=ALU.subtract), [b0t, TB_t], [DT_t])
                            k.emit('pool', lambda e, di=di: e.tensor_tensor(out=F2(Dl), in0=NEGL[di], in1=F2(DT), op=ALU.subtract), [DT_t, self.CT_t], [Dl_t])
                            k.emit('pool', lambda e, di=di: e.tensor_tensor(out=F2(DT), in0=F2(DT), in1=NEGU[di], op=ALU.add), [DT_t, self.CT_t], [DT_t])
                            k.emit('act', lambda e: e.activation(out=DT[:], in_=DT[:], func=AF.Exp), [DT_t], [DT_t])
                            k.emit('act', lambda e: e.activation(out=Dl[:], in_=Dl[:], func=AF.Exp), [Dl_t], [Dl_t])
                            k.emit('dve', lambda e, di=di: e.tensor_tensor(out=F2(DBT), in0=R(b2), in1=SNEG[di], op=ALU.mult), [b2t, self.CT_t], [DBT_t])
                            k.emit('pool', lambda e: e.tensor_tensor(out=DBT[:], in0=DBT[:], in1=DT[:], op=ALU.mult), [DBT_t, DT_t], [DBT_t])
                            k.emit('pool', lambda e: e.tensor_tensor(out=Dl[:], in0=Dl[:], in1=nbcol, op=ALU.mult), [Dl_t, TB_t], [Dl_t])
                            for q in range(4):
                                cs = slice(T0 + q * 64, T0 + (q + 1) * 64)
                                k.emit('pe', lambda e, q=q, cs=cs: e.matmul(b0[0:64, q * 64:(q + 1) * 64], kn[:, cs], kn[:, cs], start=True, stop=True), [kn_t], [b0t])
                                k.emit('pe', lambda e, q=q, cs=cs: e.matmul(b1[0:64, q * 64:(q + 1) * 64], kn[:, cs], qn[:, cs], start=True, stop=True), [kn_t, qn_t], [b1t])
                                k.emit('pe', lambda e, q=q, cs=cs: e.transpose(b2[0:64, q * 128:(q + 1) * 128], kn[:, cs], IDENT), [kn_t, self.CT_t], [b2t])
                                k.emit('pe', lambda e, q=q, cs=cs: e.transpose(b3[0:64, q * 128:(q + 1) * 128], vT[:, cs], IDENT), [vT_t, self.CT_t], [b3t])
                            NT, NT_t = bt[3], bt_t[3]
                            Nm, Nm_t = bt[4], bt_t[4]
                            QKT, QKT_t = bt[5], bt_t[5]
                            XT, XT_t = bt[6], bt_t[6]
                            k.emit('dve', lambda e: e.tensor_tensor(out=F2(NT), in0=R(b0), in1=F2(DBT), op=ALU.mult), [b0t, DBT_t], [NT_t])
                            k.emit('dve', lambda e: e.tensor_tensor(out=F2(Nm), in0=R(b0), in1=F2(Dl), op=ALU.mult), [b0t, Dl_t], [Nm_t])
                            k.emit('dve', lambda e: e.tensor_tensor(out=F2(QKT), in0=R(b1), in1=F2(DT), op=ALU.mult), [b1t, DT_t], [QKT_t])
                            k.emit('pool', lambda e: e.tensor_tensor(out=F2(XT), in0=F2(NT), in1=IDREP, op=ALU.add), [NT_t, self.CT_t], [XT_t])
                            k.emit('act', lambda e: e.activation(out=F2(ktok), in_=b2[0:64, :], func=AF.Copy), [b2t], [ktok_t])
                            k.emit('act', lambda e: e.activation(out=F2(vtok), in_=b3[0:64, :], func=AF.Copy), [b3t], [vtok_t])
                            P, P_t, PT_, PT_t = Nm, Nm_t, NT, NT_t
                            pp = [(bt[7], bt_t[7], bt[8], bt_t[8]), (bt[9], bt_t[9], bt[1], bt_t[1])]
                            XTs = [(bt[6], bt_t[6]), (bt[0], bt_t[0])]
                            xi = 0
                            for m in range(1, 6):
                                nP, nP_t, nPT, nPT_t = pp[(m - 1) % 2] if m > 1 else pp[0]
                                if m >= 3:
                                    nP, nP_t, nPT, nPT_t = pp[(m - 1) % 2]
                                if m == 2:
                                    nP, nP_t, nPT, nPT_t = pp[1]
                                for q in range(4):
                                    k.emit('pe', lambda e, q=q, P=P, PT_=PT_: e.matmul(b0[0:64, q * 64:(q + 1) * 64], PT_[:, q, :], P[:, q, :], start=True, stop=True),
                                           [P_t, PT_t], [b0t])
                                    if m < 5:
                                        k.emit('pe', lambda e, q=q, P=P, PT_=PT_: e.matmul(b1[0:64, q * 64:(q + 1) * 64], P[:, q, :], PT_[:, q, :], start=True, stop=True),
                                               [P_t, PT_t], [b1t])
                                k.emit('act', lambda e, nP=nP: e.activation(out=F2(nP), in_=R(b0), func=AF.Copy), [b0t], [nP_t])
                                if m < 5:
                                    k.emit('dve', lambda e, nPT=nPT: e.tensor_copy(out=F2(nPT), in_=R(b1)), [b1t], [nPT_t])
                                cX, cX_t = XTs[xi]
                                nX, nX_t = XTs[1 - xi]
                                for q in range(4):
                                    k.emit('pe', lambda e, q=q, nP=nP, cX=cX: e.matmul(b2[0:64, q * 64:(q + 1) * 64], nP[:, q, :], cX[:, q, :], start=True, stop=True),
                                           [nP_t, cX_t], [b2t])
                                k.emit('dve', lambda e, nX=nX, cX=cX: e.tensor_tensor(out=F2(nX), in0=R(b2), in1=F2(cX), op=ALU.add), [b2t, cX_t], [nX_t])
                                xi = 1 - xi
                                P, P_t, PT_, PT_t = nP, nP_t, nPT, nPT_t
                            XTf, XTf_t = XTs[xi]
                            if stop <= 3:
                                continue
                            psO, psO_t = self.ps[6], self.ps_t[6]
                            for ch in chs:
                                q = ch - bi * 4
                                cs = slice(ch * 64, (ch + 1) * 64)
                                loc = ch % cps
                                first = (loc == 0) if di == 0 else (loc == cps - 1)
                                last = (loc == cps - 1) if di == 0 else (loc == 0)
                                seq = ch // cps
                                zero_init = first and half == 0
                                S_prev, S_prev_t = Sb[si], Sb_t[si]
                                tmpv, tmpv_t = sm[0], sm_t[0]
                                r_, r_t = sm[1], sm_t[1]
                                vn, vn_t = sm[2], sm_t[2]
                                vs, vs_t = sm[3], sm_t[3]
                                k.emit('act', lambda e, q=q, ch=ch: e.activation(out=tmpv[:], in_=vtok[:, q, :], func=AF.Copy, scale=BE[:, ch, col:col + 1]),
                                       [vtok_t, TB_t], [tmpv_t])
                                if zero_init:
                                    rr, rr_t = tmpv, tmpv_t
                                else:
                                    p4, p4t = self.ps[4], self.ps_t[4]
                                    k.emit('pe', lambda e, cs=cs, S_prev=S_prev: e.matmul(p4[0:64, 0:128], kn[:, cs], S_prev[:], start=True, stop=True),
                                           [kn_t, S_prev_t], [p4t])
                                    k.emit('dve', lambda e, ch=ch: e.scalar_tensor_tensor(out=r_[:], in0=p4[0:64, 0:128], scalar=C1[:, ch, col:col + 1], in1=tmpv[:],
                                                                                         op0=ALU.mult, op1=ALU.add), [p4t, TB_t, tmpv_t], [r_t])
                                    rr, rr_t = r_, r_t
                                lvl = self.cfg.get('seq_lvl', 9)
                                if lvl <= 1:
                                    continue
                                p5, p5t = self.ps[5], self.ps_t[5]
                                k.emit('pe', lambda e, q=q, rr=rr: e.matmul(p5[0:64, 0:128], XTf[:, q, :], rr[:], start=True, stop=True), [XTf_t, rr_t], [p5t])
                                if lvl <= 1.5:
                                    continue
                                k.emit('act', lambda e: e.activation(out=vn[:], in_=p5[0:64, 0:128], func=AF.Copy), [p5t], [vn_t])
                                if lvl <= 1.7:
                                    continue
                                k.emit('dve', lambda e, ch=ch: e.tensor_scalar(out=vs[:], in0=p5[0:64, 0:128], scalar1=WW[:, ch, col:col + 1], scalar2=None, op0=ALU.mult),
                                       [p5t, TB_t], [vs_t])
                                if lvl <= 2:
                                    continue
                                k.emit('pe', lambda e, q=q, zero_init=zero_init: e.matmul(psO[:, q * 64:(q + 1) * 64], vn[:], QKT[:, q, :], start=True, stop=zero_init),
                                       [vn_t, QKT_t], [psO_t])
                                if not zero_init:
                                    k.emit('pe', lambda e, q=q, cs=cs, S_prev=S_prev: e.matmul(psO[:, q * 64:(q + 1) * 64], S_prev[:], Qt[:, cs], start=False, stop=True),
                                           [S_prev_t, Qt_t], [psO_t])
                                if lvl <= 3:
                                    continue
                                p7, p7t = self.ps[7], self.ps_t[7]
                                k.emit('pe', lambda e, q=q: e.matmul(p7[:, 0:128], ktok[:, q, :], vs[:], start=True, stop=True), [ktok_t, vs_t], [p7t])
                                if last and half == 0:
                                    dstS, dstS_t = sout[:, seq, di, :], sout_t
                                else:
                                    si = 1 - si
                                    dstS, dstS_t = Sb[si][:], Sb_t[si]
                                if zero_init:
                                    k.emit('dve', lambda e, dstS=dstS: e.tensor_copy(out=dstS, in_=p7[:, 0:128]), [p7t], [dstS_t])
                                else:
                                    k.emit('dve', lambda e, dstS=dstS, S_prev=S_prev, ch=ch: e.scalar_tensor_tensor(
                                        out=dstS, in0=S_prev[:], scalar=DKg[:, ch:ch + 1], in1=p7[:, 0:128], op0=ALU.mult, op1=ALU.add),
                                        [p7t, S_prev_t, DKg_t], [dstS_t])
                            if di == 0:
                                k.emit('act', lambda e, bsl=bsl: e.activation(out=oT[:, bsl], in_=psO[:, 0:256], func=AF.Copy), [psO_t], [oT_t])
                            else:
                                k.emit('dve', lambda e, bsl=bsl: e.tensor_tensor(out=oT[:, bsl], in0=oT[:, bsl], in1=psO[:, 0:256], op=ALU.add), [psO_t, oT_t], [oT_t])
                    if stop <= 4:
                        continue
                    if half == 0:
                        k.dma(d['gdout'][:, :, hh, :, :].rearrange("s d k e -> k s d e"), sout[:], reads=[sout_t])
                    (wg,), wt3 = self.wpiece([w_in[0, :, 3 * D + hh * 128:3 * D + (hh + 1) * 128]])
                    for tt in range(2):
                        sl = slice(tt * 512, (tt + 1) * 512)
                        ps, pst = self.ps[6 + tt], self.ps_t[6 + tt]
                        for c in range(NCH):
                            k.emit('pe', lambda e, ps=ps, c=c, sl=sl: e.matmul(ps[:], wg[:, c, :], self.hT[:, c, sl], start=(c == 0), stop=(c == NCH - 1)),
                                   [wt3, self.hT_t[c][tt]], [pst])
                        k.emit('act', lambda e, ps=ps, sl=sl: e.activation(out=vT[:, sl], in_=ps[:], func=AF.Silu), [pst], [vT_t])
                    self.HV_t = GV_t
                    self.head_norm(ntm, oT[:], oT_t, mix[:, 0, :], mix_t[0], GV[:, 1:2], 128.0 * 1e-6, 5, extra_mul=vT[:], extra_t=vT_t)
                    self.out_proj(d['gdn_w_out'][0, hh * 128:(hh + 1) * 128, :], mix, mix_t, 1, half, [6, 7])
                k.barrier()

    def ffn(self, layer, half):
        k, d, nc = self.k, self.d, self.nc
        seqlen = 256 if half == 0 else 1024
        oc, _ = PC['ffn_conv']
        ob, _ = PC['ffn_conv_b']

        def cw(tap, fchunk):
            col = oc + (layer * 3 + tap) * 44 + fchunk
            return self.PT[:, col:col + 1]

        def cb(fchunk):
            col = ob + layer * 44 + fchunk
            return self.PT[:, col:col + 1]

        groups = [(0, 8), (8, 16), (16, 22)]
        specs = []
        pidx = {}
        for (g0, g1) in groups:
            for j in range(g0, g1):
                pidx[('u', j)] = len(specs)
                specs.append([d['ffn_w_up'][layer, :, j * 128:(j + 1) * 128], d['ffn_w_up'][layer, :, D_FF + j * 128:D_FF + (j + 1) * 128]])
            for dp in range(4):
                pidx[('d', g0, dp)] = len(specs)
                specs.append([d['ffn_w_down'][layer, g0 * 128:g1 * 128, dp * 256:(dp + 1) * 256]])
        pf = PF(self, specs)
        PAIRS = [(0, 1), (2, 3), (4, 5)]
        u = 0
        v = 0
        with ExitStack() as es:
            aT = self.tmp(es, 'aT', [128, 8, TOK], F32R)
            aT_t = trks('aT', 8)
            acc = [self.tmp(es, f'facc{i}', [128, 2, TOK], F32) for i in range(2)]
            acc_t = trks('facc', 2, 2, 2)
            for (g0, g1) in groups:
                for j in range(g0, g1):
                    slot = j - g0
                    (wv, wg), wt = pf.get(pidx[('u', j)])
                    ai = j % 2
                    ac = acc[ai]
                    banks = {}
                    for tt in range(2):
                        pair = PAIRS[u % 3]
                        u += 1
                        sl = slice(tt * 512, (tt + 1) * 512)
                        for vi, (w, fch) in enumerate(((wv, j), (wg, NFF + j))):
                            ps, pst = self.ps[pair[vi]], self.ps_t[pair[vi]]
                            banks[(vi, tt)] = (ps, pst)
                            for c in range(NCH):
                                k.emit('pe', lambda e, ps=ps, w=w, c=c, sl=sl: e.matmul(
                                    ps[:], w[:, c, :], self.hT[:, c, sl],
                                    start=(c == 0), stop=(c == NCH - 1)), [wt, self.hT_t[c][tt]], [pst])
                        for vi, fch in ((0, j), (1, NFF + j)):
                            ps, pst = banks[(vi, tt)]
                            at_ = acc_t[ai][vi][tt]
                            k.emit('act', lambda e, ps=ps, sl=sl, fch=fch, vi=vi: e.activation(
                                out=ac[:, vi, sl], in_=ps[:], func=AF.Identity, bias=cb(fch), scale=cw(1, fch)), [pst, self.PT_t], [at_])
                            sl_ = min(seqlen, 512)
                            ns = 512 // sl_
                            pv = ps[:].rearrange("p (s n) -> p s n", s=ns)
                            av = ac[:, vi, sl].rearrange("p (s n) -> p s n", s=ns)
                            k.emit('dve', lambda e, pv=pv, av=av, fch=fch, sl_=sl_: e.scalar_tensor_tensor(
                                out=av[:, :, 1:sl_], in0=pv[:, :, 0:sl_ - 1], scalar=cw(0, fch), in1=av[:, :, 1:sl_],
                                op0=ALU.mult, op1=ALU.add), [pst, self.PT_t, at_], [at_])
                            k.emit('dve', lambda e, pv=pv, av=av, fch=fch, sl_=sl_: e.scalar_tensor_tensor(
                                out=av[:, :, 0:sl_ - 1], in0=pv[:, :, 1:sl_], scalar=cw(2, fch), in1=av[:, :, 0:sl_ - 1],
                                op0=ALU.mult, op1=ALU.add), [pst, self.PT_t, at_], [at_])
                    if seqlen > 512:
                        for vi, fch in ((0, j), (1, NFF + j)):
                            p0, p0t = banks[(vi, 0)]
                            p1, p1t = banks[(vi, 1)]
                            k.emit('dve', lambda e, p0=p0, fch=fch, vi=vi: e.scalar_tensor_tensor(
                                out=ac[:, vi, 512:513], in0=p0[:, 511:512], scalar=cw(0, fch), in1=ac[:, vi, 512:513],
                                op0=ALU.mult, op1=ALU.add), [p0t, self.PT_t, acc_t[ai][vi][1]], [acc_t[ai][vi][1]])
                            k.emit('dve', lambda e, p1=p1, fch=fch, vi=vi: e.scalar_tensor_tensor(
                                out=ac[:, vi, 511:512], in0=p1[:, 0:1], scalar=cw(2, fch), in1=ac[:, vi, 511:512],
                                op0=ALU.mult, op1=ALU.add), [p1t, self.PT_t, acc_t[ai][vi][0]], [acc_t[ai][vi][0]])
                    k.emit('act', lambda e, ac=ac: e.activation(out=ac[:, 1, :], in_=ac[:, 1, :], func=AF.Silu), acc_t[ai][1], acc_t[ai][1])
                    k.emit('pool', lambda e, slot=slot, ac=ac: e.tensor_tensor(out=aT[:, slot, :], in0=ac[:, 0, :], in1=ac[:, 1, :], op=ALU.mult),
                           acc_t[ai], [aT_t[slot]])
                    if self.modgen is not None:
                        next(self.modgen, None)
                ng = g1 - g0
                for dp in range(4):
                    (wd,), wdt = pf.get(pidx[('d', g0, dp)])
                    for dmi in range(2):
                        dm = dp * 2 + dmi
                        for tt in range(2):
                            b = v % 6
                            v += 1
                            ps, pst = self.ps[b], self.ps_t[b]
                            for jj in range(ng):
                                k.emit('pe', lambda e, ps=ps, jj=jj, dmi=dmi, tt=tt: e.matmul(
                                    ps[:], wd[:, jj, dmi * 128:(dmi + 1) * 128], aT[:, jj, tt * 512:(tt + 1) * 512],
                                    start=(jj == 0), stop=(jj == ng - 1)), [wdt, aT_t[jj]], [pst])
                            xs = self.xT[half][:, dm, tt * 512:(tt + 1) * 512]
                            k.emit('dve', lambda e, ps=ps, xs=xs, dm=dm: e.scalar_tensor_tensor(
                                out=xs, in0=ps[:], scalar=self.gate(1, dm, half), in1=xs, op0=ALU.mult, op1=ALU.add),
                                [pst, self.MOD_t, self.xT_t[half][dm][tt]], [self.xT_t[half][dm][tt]])
            k.barrier()

    def final(self, cfg):
        k, d, nc = self.k, self.d, self.nc
        og, _ = PC['final_g']
        with ExitStack() as es:
            sq = self.tmp(es, 'fsq', [128, NCH, 512], F32R)
            sq_t = Trk('fsq')
            rstd = self.tmp(es, 'frstd', [128, 512], F32)
            rstd_t = Trk('frstd')
            yo = self.tmp(es, 'fyo', [128, NCH, 512], F32)
            yo_t = self.ptrk('fyo', NCH)
            for half in range(2):
                for tt in range(2):
                    xs = self.xT[half][:, :, tt * 512:(tt + 1) * 512]
                    xs_t = [self.xT_t[half][c][tt] for c in range(NCH)]
                    k.emit('act', lambda e: e.activation(out=sq[:], in_=xs, func=AF.Square), xs_t, [sq_t])
                    ps, pst = self.ps[6], self.ps_t[6]
                    for c in range(NCH):
                        k.emit('pe', lambda e, c=c: e.matmul(ps[:], self.onesR[:], sq[:, c, :], start=(c == 0), stop=(c == NCH - 1)),
                               [self.onesR_t, sq_t], [pst])
                    k.emit('act', lambda e: e.activation(out=rstd[:], in_=ps[:], func=AF.Sqrt, bias=float(D * EPS), scale=1.0),
                           [pst], [rstd_t])
                    k.emit('dve', lambda e: e.reciprocal(out=rstd[:], in_=rstd[:]), [rstd_t], [rstd_t])
                    for c in range(NCH):
                        g = self.PT[:, og + c:og + c + 1]
                        k.emit('dve', lambda e, c=c, g=g: e.scalar_tensor_tensor(
                            out=yo[:, c, :], in0=self.xT[half][:, c, tt * 512:(tt + 1) * 512], scalar=g, in1=rstd[:],
                            op0=ALU.mult, op1=ALU.mult), [self.xT_t[half][c][tt], self.PT_t, rstd_t], [yo_t[c]])
                        k.emit('act', lambda e, c=c: e.activation(out=yo[:, c, :], in_=yo[:, c, :], func=AF.Copy, scale=32.0),
                               [yo_t[c]], [yo_t[c]])
                        k.dma(d['yout'][half, c, :, tt * 512:(tt + 1) * 512], yo[:, c, :], reads=[yo_t[c]])


_CACHE = {}


def _get_prog(cfg):
    key = repr(sorted(cfg.items()))
    if key not in _CACHE:
        p = Prog(dict(cfg))
        with p.es:
            p.declare()
            p.build()
        _CACHE[key] = p
    return _CACHE[key]


def _pack(plan, arrays, n):
    out = np.zeros((128, n), np.float32)
    for c0, key in plan:
        off = c0
        for (name, offset, rstride, rows, ncols) in key:
            flat = arrays[name].reshape(-1)
            a2 = np.lib.stride_tricks.as_strided(flat[offset:], shape=(rows, ncols), strides=(rstride * 4, 4))
            kc = rows // 128
            assert off + kc * ncols <= n
            out[:, off:off + kc * ncols] = a2.reshape(kc, 128, ncols).transpose(1, 0, 2).reshape(128, kc * ncols)
            off += kc * ncols
    return out


def _run(inp, cfg):
    p = _get_prog(cfg)
    consts, rope_tab = _build_consts()
    f32 = lambda a: np.ascontiguousarray(np.asarray(a, np.float32))
    warr = {n: f32(inp[n]) for n in ('w_mod', 'ffn_w_up', 'ffn_w_down', 'attn_w_in', 'attn_w_out',
                                     'hgrn_w_in', 'hgrn_w_out', 'gdn_w_in', 'gdn_w_out')}
    shared = {'wpk': _pack(p.wplan['wpk'], warr, WCOLS)}
    xp = f32(inp['x_prompt'])
    xs = f32(inp['x_sample'])
    ck = f32(inp['cache_attn_k'])
    cv = f32(inp['cache_attn_v'])
    in_maps = []
    for core in range(N_CORES):
        m = dict(shared)
        a = xp[4 * core:4 * core + 4].reshape(TOK, D).T.reshape(8, 128, TOK)
        b = xs[core].T.reshape(8, 128, TOK)
        m['xin'] = np.ascontiguousarray(np.stack([a, b], axis=0))
        m['params'] = _build_params(core, inp)
        m['consts'] = consts
        m['rope'] = rope_tab
        m['lamtab'] = np.ascontiguousarray(np.broadcast_to(np.asarray(inp['attn_lambda'], np.float32).reshape(1, 512), (128, 512)))
        carr = {'ck': np.ascontiguousarray(ck[core].transpose(0, 2, 3, 1)),
                'cv': np.ascontiguousarray(cv[core].reshape(2, 512, D))}
        m['cpk'] = _pack(p.wplan['cpk'], carr, CCOLS)
        m['st_hgrn'] = f32(inp['state_hgrn'][core, 0])
        m['st_gdn'] = f32(inp['state_gdn'][core, 0])
        in_maps.append(m)
    ncores = cfg.get('ncores', N_CORES)
    res = run_bass_kernel_spmd(p.nc, in_maps[:ncores], core_ids=list(range(ncores)))
    R = list(res.results) + [res.results[0]] * (N_CORES - ncores)
    y_prompt = np.empty((32, 256, D), np.float32)
    y_sample = np.empty((8, 1024, D), np.float32)
    new_k = np.empty((32, 2, 256, 8, 128), np.float32)
    new_v = np.empty((32, 2, 256, 8, 128), np.float32)
    new_h = np.empty((32, 1, 2, 8, 128, 128), np.float32)
    new_g = np.empty((32, 1, 2, 8, 128, 128), np.float32)
    for core in range(N_CORES):
        r = R[core]
        yo = r['yout']
        y_prompt[4 * core:4 * core + 4] = yo[0].reshape(D, TOK).T.reshape(4, 256, D)
        y_sample[core] = yo[1].reshape(D, TOK).T
        ko = r['kout']
        new_k[4 * core:4 * core + 4] = ko.reshape(2, 8, 128, 4, 256).transpose(3, 0, 4, 1, 2)
        vo = r['vout']
        new_v[4 * core:4 * core + 4] = vo.reshape(2, 4, 256, 8, 128).transpose(1, 0, 2, 3, 4)
        new_h[4 * core:4 * core + 4, 0] = r['hgout']
        new_g[4 * core:4 * core + 4, 0] = r['gdout']
    return (y_prompt, y_sample, new_k, new_v, new_h, new_g)


def kernel(**inputs):
    return _run(inputs, {})
```

```python
import math
from contextlib import ExitStack

import numpy as np
import concourse.bass as bass
import concourse.mybir as mybir
from concourse.bass_utils import run_bass_kernel_spmd

F32 = mybir.dt.float32
F32R = mybir.dt.float32r
AF = mybir.ActivationFunctionType
ALU = mybir.AluOpType
AX = mybir.AxisListType

D = 1024
NCH = 8
TOK = 1024
DEPTH = 4
D_FF = 2816
NFF = 22
EPS = 1e-6
N_CORES = 8
WCOLS = (4 * 1024 * 6144 + 4 * 1024 * 5632 + 4 * 2816 * 1024 + 2 * 1024 * 3072 + 2 * 1024 * 1024 + 1024 * 5120 + 1024 * 1024
         + 1024 * 4128 + 1024 * 1024) // 128
CCOLS = (2 * 8 * 128 * 512 + 2 * 512 * 1024) // 128


class Cols:
    def __init__(self):
        self.off = {}
        self.n = 0

    def add(self, name, w):
        self.off[name] = (self.n, w)
        self.n += w

    def __getitem__(self, name):
        return self.off[name]


def _param_cols():
    c = Cols()
    c.add('cond', 16)
    c.add('norm_g', 64)
    c.add('b_mod', 192)
    c.add('final_g', 8)
    c.add('subln', 2)
    c.add('hgrn_lb', 32)
    c.add('hgrn_norm', 1)
    c.add('gdn_norm', 1)
    c.add('gdn_conv', 72)
    c.add('ffn_conv', 528)
    c.add('ffn_conv_b', 176)
    c.add('gdn_alog_col', 1)
    c.add('gdn_dt_col', 1)
    c.add('gdn_alog_row', 16)
    c.add('gdn_dt_row', 16)
    return c


PC = _param_cols()


def _fm(v):
    v = np.asarray(v, np.float32)
    r = v.reshape(-1, 128)
    return np.ascontiguousarray(r.T)


def _build_params(core, inp):
    P = np.zeros((128, PC.n), np.float32)

    def put(name, arr):
        o, w = PC[name]
        assert arr.shape == (128, w), (name, arr.shape, w)
        P[:, o:o + w] = arr

    cond = np.stack([inp['c_ctx'], inp['c'][core]], axis=0)
    put('cond', np.ascontiguousarray(cond.reshape(2, 8, 128).transpose(2, 1, 0)).reshape(128, 16))
    put('norm_g', _fm(inp['norm_g']))
    put('b_mod', _fm(inp['b_mod']))
    put('final_g', _fm(inp['final_g']))
    put('subln', _fm(inp['attn_subln']))
    put('hgrn_lb', _fm(inp['hgrn_lb']))
    put('hgrn_norm', _fm(inp['hgrn_norm']))
    put('gdn_norm', _fm(inp['gdn_norm']))
    put('gdn_conv', _fm(inp['gdn_conv']))
    put('ffn_conv', _fm(inp['ffn_conv']))
    put('ffn_conv_b', _fm(inp['ffn_conv_b']))
    al = np.zeros((128, 1), np.float32)
    al[:16, 0] = np.asarray(inp['gdn_a_log'], np.float32).reshape(16)
    put('gdn_alog_col', al)
    dtb = np.zeros((128, 1), np.float32)
    dtb[:16, 0] = np.asarray(inp['gdn_dt_bias'], np.float32).reshape(16)
    put('gdn_dt_col', dtb)
    put('gdn_alog_row', np.broadcast_to(np.asarray(inp['gdn_a_log'], np.float32).reshape(1, 16), (128, 16)))
    put('gdn_dt_row', np.broadcast_to(np.asarray(inp['gdn_dt_bias'], np.float32).reshape(1, 16), (128, 16)))
    return P


def _const_cols():
    c = Cols()
    c.add('ident', 128)
    c.add('ones', 128)
    c.add('perm', 128)
    c.add('mask_f', 256)
    c.add('mask_b', 256)
    c.add('negu_f', 256)
    c.add('negl_f', 256)
    c.add('negu_b', 256)
    c.add('negl_b', 256)
    c.add('sneg_f', 256)
    c.add('sneg_b', 256)
    c.add('idrep', 256)
    c.add('sel', 512)
    c.add('nsel', 128)
    return c


CC = _const_cols()


def _build_consts():
    C = np.zeros((128, CC.n), np.float32)
    o, w = CC['ident']
    C[:, o:o + w] = np.eye(128, dtype=np.float32)
    o, w = CC['ones']
    C[:, o:o + w] = 1.0
    o, w = CC['perm']
    tok = np.arange(1024)
    row = (tok // 64).astype(np.float32)
    col = (tok % 64).astype(np.float32)
    inv = (np.float32(10000.0) ** (-np.arange(16, dtype=np.float32) / np.float32(16))).astype(np.float32)
    ROPE = np.zeros((128, 2048), np.float32)
    oc_, os_ = 0, 1024
    for p in range(128):
        dd = p % 64
        i = dd % 32
        partner = p + 16 if i < 16 else p - 16
        C[partner, o + p] = 1.0
        f = i % 16
        pos = row if dd < 32 else col
        ang = (pos * inv[f]).astype(np.float32)
        ROPE[p, oc_:oc_ + 1024] = np.cos(ang)
        ROPE[p, os_:os_ + 1024] = np.sin(ang) * (-1.0 if i < 16 else 1.0)
    sidx = np.arange(64)[:, None]
    tidx = np.arange(64)[None, :]
    om, _ = CC['mask_f']
    C[:64, om:om + 256] = np.tile((sidx <= tidx).astype(np.float32), (1, 4))
    om, _ = CC['mask_b']
    C[:64, om:om + 256] = np.tile((sidx >= tidx).astype(np.float32), (1, 4))
    BIG = 30000.0
    p_, j_ = sidx, tidx

    def putm(name, m):
        o_, _ = CC[name]
        C[:64, o_:o_ + 256] = np.tile(m.astype(np.float32), (1, 4))

    putm('negu_f', np.where(j_ >= p_, 0.0, -BIG))
    putm('negl_f', np.where(j_ < p_, 0.0, -BIG))
    putm('negu_b', np.where(j_ <= p_, 0.0, -BIG))
    putm('negl_b', np.where(j_ > p_, 0.0, -BIG))
    putm('sneg_f', np.where(j_ > p_, -1.0, 0.0))
    putm('sneg_b', np.where(j_ < p_, -1.0, 0.0))
    putm('idrep', (j_ == p_))
    o_, _ = CC['sel']
    for kk in range(4):
        C[kk, o_ + kk * 128:o_ + (kk + 1) * 128] = 1.0
    o_, _ = CC['nsel']
    for kk in range(2):
        C[kk, o_ + kk * 64:o_ + (kk + 1) * 64] = -1.0
    return C, ROPE


class _GT:
    def __init__(self, name):
        self.name = name


class Geo:
    def __init__(self, name, shape, offset=0, pat=None):
        self.tensor = _GT(name)
        if pat is None:
            pat = []
            st = 1
            for n in reversed(shape):
                pat.insert(0, (st, n))
                st *= n
        self.ap = tuple(pat)
        self.offset = offset

    @property
    def shape(self):
        return tuple(n for _, n in self.ap)

    def __getitem__(self, key):
        if not isinstance(key, tuple):
            key = (key,)
        key = key + (slice(None),) * (len(self.ap) - len(key))
        off = self.offset
        pat = []
        for (st, n), kk in zip(self.ap, key):
            if isinstance(kk, int):
                off += st * kk
            else:
                a, b, _ = kk.indices(n)
                off += st * a
                pat.append((st, b - a))
        return Geo(self.tensor.name, None, off, pat)


class Trk:
    __slots__ = ('name', 'w', 'rs', 'sem', 'cnt', 'psum')

    def __init__(self, name):
        self.name = name
        self.psum = False
        self.w = None
        self.rs = {}
        self.sem = None
        self.cnt = 0


def trks(name, *dims):
    if len(dims) == 1:
        return [Trk(f'{name}{i}') for i in range(dims[0])]
    return [trks(f'{name}{i}_', *dims[1:]) for i in range(dims[0])]


def flat(x):
    if isinstance(x, Trk):
        return [x]
    out = []
    for e in x:
        out.extend(flat(e))
    return out


class KB:
    def __init__(self, nc, es):
        self.nc = nc
        self.es = es
        self.E = {'pe': nc.tensor, 'act': nc.scalar, 'dve': nc.vector, 'pool': nc.gpsimd, 'sp': nc.sync}
        self.sem = {e: es.enter_context(nc.semaphore(f's_{e}')) for e in self.E}
        self.cnt = {e: 0 for e in self.E}
        self.waited = {e: {} for e in self.E}
        self.dma_sems = []
        self.n_ins = 0
        self.n_wait = 0

    def _wait(self, eng, deps):
        need = {}
        for key, val in deps:
            if key == 'pe' and eng == 'pe':
                continue
            if need.get(key, 0) < val:
                need[key] = val
        wt = self.waited[eng]
        for key, val in need.items():
            if wt.get(key, 0) >= val:
                continue
            sem = self.sem[key] if isinstance(key, str) else key
            self.E[eng].wait_ge(sem, val)
            self.n_wait += 1
            wt[key] = val

    def _deps(self, reads, writes):
        deps = []
        for t in reads:
            if t.w is not None:
                deps.append(t.w)
            if t.psum:
                deps.extend(t.rs.items())
        for t in writes:
            if t.w is not None:
                deps.append(t.w)
            deps.extend(t.rs.items())
        return deps

    def _mark(self, tok, reads, writes):
        k, v = tok
        for t in reads:
            if t.rs.get(k, 0) < v:
                t.rs[k] = v
        for t in writes:
            t.w = tok
            t.rs = {}

    def emit(self, eng, fn, reads=(), writes=()):
        reads = flat(reads)
        writes = flat(writes)
        self._wait(eng, self._deps(reads, writes))
        ins = fn(self.E[eng])
        self.cnt[eng] += 1
        ins.then_inc(self.sem[eng], 1)
        self.n_ins += 1
        tok = (eng, self.cnt[eng])
        self._mark(tok, reads, writes)
        return tok

    def dma(self, out, in_, reads=(), writes=(), q='sp'):
        reads = flat(reads)
        writes = flat(writes)
        self._wait(q, self._deps(reads, writes))
        owner = (writes + reads)[0]
        if owner.sem is None:
            owner.sem = self.es.enter_context(self.nc.semaphore(f'd_{owner.name}'))
            self.dma_sems.append(owner)
        self.E[q].dma_start(out=out, in_=in_).then_inc(owner.sem, 16)
        owner.cnt += 16
        self.n_ins += 1
        tok = (owner.sem, owner.cnt)
        self._mark(tok, reads, writes)
        return tok

    def dma_group(self, pairs, writes, q='sp'):
        writes = flat(writes)
        self._wait(q, self._deps([], writes))
        owner = writes[0]
        if owner.sem is None:
            owner.sem = self.es.enter_context(self.nc.semaphore(f'd_{owner.name}'))
            self.dma_sems.append(owner)
        for out, in_ in pairs:
            self.E[q].dma_start(out=out, in_=in_).then_inc(owner.sem, 16)
            owner.cnt += 16
            self.n_ins += 1
        tok = (owner.sem, owner.cnt)
        self._mark(tok, [], writes)
        return tok

    def barrier(self):
        for e in self.E:
            deps = [(o, self.cnt[o]) for o in self.E if o != e and self.cnt[o] > 0]
            deps += [(t.sem, t.cnt) for t in self.dma_sems]
            wt = self.waited[e]
            for key, val in deps:
                if wt.get(key, 0) >= val:
                    continue
                sem = self.sem[key] if isinstance(key, str) else key
                self.E[e].wait_ge(sem, val)
                self.n_wait += 1
                wt[key] = val

    def finish(self):
        deps = [(t.sem, t.cnt) for t in self.dma_sems]
        deps += [(o, self.cnt[o]) for o in self.E if o != 'sp' and self.cnt[o] > 0]
        self._wait('sp', deps)

    def sb(self, name, shape, dt=F32):
        return self.es.enter_context(self.nc.sbuf_tensor(name, list(shape), dt))


class PF:
    def __init__(self, prog, specs):
        self.p, self.specs, self.h = prog, specs, {}

    def get(self, i):
        for t in (i, i + 1):
            if t < len(self.specs) and t not in self.h:
                self.h[t] = self.p.wpiece(self.specs[t])
        return self.h.pop(i)


class Prog:
    def __init__(self, cfg):
        self.cfg = cfg
        nc = bass.Bass("TRN2", target_bir_lowering=False)
        self.nc = nc
        self.es = ExitStack()
        self.k = KB(nc, self.es)
        self.rr = 0
        self.wplan = {'wpk': [], 'cpk': []}
        self.wcols = {'wpk': 0, 'cpk': 0}
        self.wkeys = {}

    def declare(self):
        nc = self.nc

        def din(name, shape):
            return nc.dram_tensor(name, list(shape), F32, kind="ExternalInput").ap()

        def dout(name, shape):
            return nc.dram_tensor(name, list(shape), F32, kind="ExternalOutput").ap()

        d = {}
        d['xin'] = din('xin', [2, 8, 128, TOK])
        d['params'] = din('params', [128, PC.n])
        d['consts'] = din('consts', [128, CC.n])
        d['rope'] = din('rope', [128, 2048])
        d['wpk'] = din('wpk', [128, WCOLS])
        d['cpk'] = din('cpk', [128, CCOLS])
        d['lamtab'] = din('lamtab', [128, 512])
        d['gscr'] = nc.dram_tensor('gscr', [48, TOK], F32, kind="Internal").ap()
        d['w_mod'] = Geo('w_mod', [4, D, 6 * D])
        d['ffn_w_up'] = Geo('ffn_w_up', [4, D, 2 * D_FF])
        d['ffn_w_down'] = Geo('ffn_w_down', [4, D_FF, D])
        d['attn_w_in'] = Geo('attn_w_in', [2, D, 3 * D])
        d['attn_w_out'] = Geo('attn_w_out', [2, D, D])
        d['hgrn_w_in'] = Geo('hgrn_w_in', [1, D, 5 * D])
        d['hgrn_w_out'] = Geo('hgrn_w_out', [1, D, D])
        d['gdn_w_in'] = Geo('gdn_w_in', [1, D, 4 * D + 32])
        d['gdn_w_out'] = Geo('gdn_w_out', [1, D, D])
        d['ck'] = Geo('ck', [2, 8, 128, 512])
        d['cv'] = Geo('cv', [2, 512, D])
        d['st_hgrn'] = din('st_hgrn', [2, 8, 128, 128])
        d['st_gdn'] = din('st_gdn', [2, 8, 128, 128])
        d['yout'] = dout('yout', [2, 8, 128, TOK])
        d['kout'] = dout('kout', [2, 8, 128, TOK])
        d['vout'] = dout('vout', [2, TOK, D])
        d['hgout'] = dout('hgout', [4, 2, 8, 128, 128])
        d['gdout'] = dout('gdout', [4, 2, 8, 128, 128])
        self.d = d

    def ptrk(self, name, n=None):
        if not hasattr(self, '_pt'):
            self._pt = {}
        if name not in self._pt:
            self._pt[name] = Trk(name) if n is None else trks(name, n)
        return self._pt[name]

    def tmp(self, es, name, shape, dt=F32):
        self._uid = getattr(self, '_uid', 0) + 1
        return es.enter_context(self.nc.sbuf_tensor(f'{name}_{self._uid}', list(shape), dt))

    def evac_eng(self):
        self.rr += 1
        return 'act' if self.rr % 2 else 'dve'

    def copy(self, eng, out, in_, reads, writes):
        if eng == 'act':
            return self.k.emit('act', lambda e: e.activation(out=out, in_=in_, func=AF.Copy), reads, writes)
        return self.k.emit(eng, lambda e: e.tensor_copy(out=out, in_=in_), reads, writes)

    def init_wpool(self):
        k = self.k
        self.ws = [k.sb(f'ws{i}', [128, 2048], F32) for i in range(2)]
        self.ws_t = trks('ws', 2)
        self.wr = [k.sb(f'wr{i}', [128, 2048], F32R) for i in range(2)]
        self.wr_t = trks('wr', 2)
        self.ws_i = 0
        self.wr_i = 0

    def wpiece(self, segs, rounded=True, dest=None):
        k = self.k
        si = self.ws_i
        self.ws_i = (si + 1) % 2
        st, stt = self.ws[si], self.ws_t[si]
        off = 0
        views = []
        key = []
        percore = False
        for ap in segs:
            rows, ncols = ap.shape
            kc = rows // 128
            pat = tuple((int(a), int(b)) for a, b in ap.ap)
            assert len(pat) == 2 and pat[1][0] == 1, pat
            name = ap.tensor.name
            percore = percore or name in ('ck', 'cv')
            key.append((name, int(ap.offset), pat[0][0], rows, ncols))
            views.append((off, kc, ncols))
            off += kc * ncols
        key = tuple(key)
        pk = 'cpk' if percore else 'wpk'
        if key not in self.wkeys:
            self.wkeys[key] = (pk, self.wcols[pk])
            self.wplan[pk].append((self.wcols[pk], key))
            self.wcols[pk] += off
        pk, c0 = self.wkeys[key]
        k.dma(st[:, 0:off], self.d[pk][:, c0:c0 + off], writes=[stt])
        if not rounded:
            outs = [st[:, o:o + kc * n].rearrange("p (c n) -> p c n", c=kc) for (o, kc, n) in views]
            return outs, stt
        if dest is not None:
            rt, rtt = dest
        else:
            ri = self.wr_i
            self.wr_i = (ri + 1) % 2
            rt, rtt = self.wr[ri], self.wr_t[ri]
        k.emit('act', lambda e: e.activation(out=rt[:, 0:off], in_=st[:, 0:off], func=AF.Copy), [stt], [rtt])
        outs = [rt[:, o:o + kc * n].rearrange("p (c n) -> p c n", c=kc) for (o, kc, n) in views]
        return outs, rtt

    def build(self):
        nc, k, d, cfg = self.nc, self.k, self.d, self.cfg
        self.ps = [self.es.enter_context(nc.psum_tensor(f'ps{i}', [128, 512], F32)) for i in range(8)]
        self.ps_t = trks('ps', 8)
        for t in self.ps_t:
            t.psum = True
        self.PT = k.sb('PT', [128, PC.n])
        self.PT_t = Trk('PT')
        self.CT = k.sb('CT', [128, CC.n])
        self.CT_t = Trk('CT')
        self.onesR = k.sb('onesR', [128, 128], F32R)
        self.onesR_t = Trk('onesR')
        self.identR = k.sb('identR', [128, 128], F32R)
        self.identR_t = Trk('identR')
        self.xT = [k.sb(f'xT{h}', [128, NCH, TOK]) for h in range(2)]
        self.xT_t = trks('xT', 2, NCH, 2)
        self.hT = k.sb('hT', [128, NCH, TOK], F32R)
        self.hT_t = trks('hT', NCH, 2)
        self.permR = k.sb('permR', [128, 128], F32R)
        self.permR_t = Trk('permR')
        self.AV = k.sb('AV', [128, 8])
        self.AV_t = Trk('AV')
        self.MODs = [k.sb(f'MOD{i}', [128, 48, 2]) for i in range(2)]
        self.MODs_t = trks('MOD', 2)
        self.ABs = [k.sb(f'AB{i}', [128, 2, 2, 8, 2]) for i in range(2)]
        self.ABs_t = trks('AB', 2)
        self.modgen = None
        self.sc = k.sb('sc', [128, 16])
        self.sc_t = Trk('sc')
        self.init_wpool()

        k.dma(self.PT[:], d['params'][:, :], writes=[self.PT_t])
        k.dma(self.CT[:], d['consts'][:, :], writes=[self.CT_t])
        for h in range(2):
            k.dma_group([(self.xT[h][:, c, :], d['xin'][h, c, :, :]) for c in range(NCH)], self.xT_t[h])
        o, w = CC['ones']
        k.emit('dve', lambda e: e.tensor_copy(out=self.onesR[:], in_=self.CT[:, o:o + w]), [self.CT_t], [self.onesR_t])
        o2, w2 = CC['ident']
        k.emit('dve', lambda e: e.tensor_copy(out=self.identR[:], in_=self.CT[:, o2:o2 + w2]), [self.CT_t], [self.identR_t])
        o3, w3 = CC['perm']
        k.emit('dve', lambda e: e.tensor_copy(out=self.permR[:], in_=self.CT[:, o3:o3 + w3]), [self.CT_t], [self.permR_t])
        oc, wc = PC['cond']
        k.emit('act', lambda e: e.activation(out=self.sc[:], in_=self.PT[:, oc:oc + wc], func=AF.Silu), [self.PT_t], [self.sc_t])

        nl = cfg.get('layers', DEPTH)
        for _ in self.modulation(0):
            pass
        for layer in range(nl):
            p = layer % 2
            self.MOD, self.MOD_t, self.AB, self.AB_t = self.MODs[p], self.MODs_t[p], self.ABs[p], self.ABs_t[p]
            for half in range(2):
                if cfg.get('mixers', True):
                    self.rmsnorm_mod(layer, 0, half)
                    self.mixer(layer, half)
                if half == 1 and layer + 1 < nl:
                    self.modgen = self.modulation(layer + 1)
                if cfg.get('ffn', True):
                    self.rmsnorm_mod(layer, 1, half)
                    self.ffn(layer, half)
                if self.modgen is not None:
                    for _ in self.modgen:
                        pass
                    self.modgen = None
        self.final(cfg)
        k.finish()

    def modulation(self, layer):
        k, d = self.k, self.d
        p = layer % 2
        MOD, MOD_t, AB, AB_t = self.MODs[p], self.MODs_t[p], self.ABs[p], self.ABs_t[p]
        ps, pst = self.ps[7], self.ps_t[7]
        scv = self.sc[:].rearrange("p (c j) -> p c j", j=2)
        for piece in range(24):
            (w,), wt = self.wpiece([d['w_mod'][layer, :, piece * 256:(piece + 1) * 256]], rounded=False)
            for q2 in range(2):
                q = piece * 2 + q2
                for c in range(NCH):
                    k.emit('pe', lambda e, c=c, q2=q2, q=q: e.matmul(
                        ps[:, q * 2:q * 2 + 2], w[:, c, q2 * 128:(q2 + 1) * 128], scv[:, c, :],
                        start=(c == 0), stop=(c == NCH - 1)), [wt, self.sc_t], [pst])
            yield piece
        ob, wb = PC['b_mod']
        bm = self.PT[:, ob + layer * 48: ob + layer * 48 + 48]
        k.emit('dve', lambda e: e.tensor_tensor(
            out=MOD[:], in0=ps[:, 0:96].rearrange("p (q j) -> p q j", j=2),
            in1=bm.unsqueeze(2).broadcast_to([128, 48, 2]), op=ALU.add), [pst, self.PT_t], [MOD_t])
        og, wg = PC['norm_g']
        for s in range(2):
            g = self.PT[:, og + (layer * 2 + s) * 8: og + (layer * 2 + s) * 8 + 8]
            sh = MOD[:, s * 24 + 0: s * 24 + 8, :]
            scl = MOD[:, s * 24 + 8: s * 24 + 16, :]
            A = AB[:, s, 0, :, :]
            B = AB[:, s, 1, :, :]
            k.emit('dve', lambda e, scl=scl, A=A: e.tensor_scalar(
                out=A, in0=scl, scalar1=1.0, scalar2=32.0, op0=ALU.add, op1=ALU.mult), [MOD_t], [AB_t])
            k.emit('dve', lambda e, A=A, g=g: e.tensor_tensor(
                out=A, in0=A, in1=g.unsqueeze(2).broadcast_to([128, 8, 2]), op=ALU.mult), [AB_t, self.PT_t], [AB_t])
            k.emit('dve', lambda e, B=B, sh=sh: e.tensor_copy(out=B, in_=sh), [MOD_t], [AB_t])

    def gate(self, s, c, half):
        return self.MOD[:, s * 24 + 16 + c, half:half + 1]

    def rmsnorm_mod(self, layer, s, half):
        k = self.k
        with ExitStack() as es:
            sq = self.tmp(es, 'nsq', [128, NCH, 512], F32R)
            sq_t = Trk('nsq')
            tmp = self.tmp(es, 'ntmp', [128, NCH, 512], F32)
            tmp_t = Trk('ntmp')
            rstd = self.tmp(es, 'nrstd', [128, 512], F32)
            rstd_t = Trk('nrstd')
            for tt in range(2):
                xs = self.xT[half][:, :, tt * 512:(tt + 1) * 512]
                xs_t = [self.xT_t[half][c][tt] for c in range(NCH)]
                k.emit('act', lambda e: e.activation(out=sq[:], in_=xs, func=AF.Square), xs_t, [sq_t])
                ps, pst = self.ps[6], self.ps_t[6]
                for c in range(NCH):
                    k.emit('pe', lambda e, c=c: e.matmul(ps[:], self.onesR[:], sq[:, c, :], start=(c == 0), stop=(c == NCH - 1)),
                           [self.onesR_t, sq_t], [pst])
                k.emit('act', lambda e: e.activation(out=rstd[:], in_=ps[:], func=AF.Sqrt, bias=float(D * EPS), scale=1.0),
                       [pst], [rstd_t])
                k.emit('dve', lambda e: e.reciprocal(out=rstd[:], in_=rstd[:]), [rstd_t], [rstd_t])
                k.emit('dve', lambda e: e.tensor_tensor(out=tmp[:], in0=xs, in1=rstd[:].unsqueeze(1).broadcast_to([128, NCH, 512]),
                                                        op=ALU.mult), xs_t + [rstd_t], [tmp_t])
                for c in range(NCH):
                    A = self.AB[:, s, 0, c, half:half + 1]
                    B = self.AB[:, s, 1, c, half:half + 1]
                    out = self.hT[:, c, tt * 512:(tt + 1) * 512]
                    if c % 2 == 0:
                        k.emit('act', lambda e, c=c, A=A, B=B, out=out: e.activation(
                            out=out, in_=tmp[:, c, :], func=AF.Identity, bias=B, scale=A),
                            [tmp_t, self.AB_t], [self.hT_t[c][tt]])
                    else:
                        k.emit('dve', lambda e, c=c, A=A, B=B, out=out: e.tensor_scalar(
                            out=out, in0=tmp[:, c, :], scalar1=A, scalar2=B, op0=ALU.mult, op1=ALU.add),
                            [tmp_t, self.AB_t], [self.hT_t[c][tt]])
            k.barrier()

    def mixer(self, layer, half):
        kind = layer % 3
        ml = self.cfg.get('mixlist', (0, 1, 2))
        if kind not in ml:
            return
        if kind == 0:
            if half == 0:
                self.attn_prep(layer)
            self.attn(layer, half)
        elif kind == 1:
            self.hgrn(layer, half)
        else:
            self.gdn(layer, half)

    def out_proj(self, w_out_rows, src, src_t, nh, half, banks):
        k = self.k
        bi = 0
        pf = PF(self, [[w_out_rows[:, dp * 256:(dp + 1) * 256]] for dp in range(4)])
        for dp in range(4):
            (wo,), wot = pf.get(dp)
            for dmi in range(2):
                dm = dp * 2 + dmi
                for tt in range(2):
                    b = banks[bi % len(banks)]
                    bi += 1
                    ps, pst = self.ps[b], self.ps_t[b]
                    for hl in range(nh):
                        k.emit('pe', lambda e, ps=ps, hl=hl, dmi=dmi, tt=tt: e.matmul(
                            ps[:], wo[:, hl, dmi * 128:(dmi + 1) * 128], src[:, hl, tt * 512:(tt + 1) * 512],
                            start=(hl == 0), stop=(hl == nh - 1)), [wot, src_t[hl]], [pst])
                    xs = self.xT[half][:, dm, tt * 512:(tt + 1) * 512]
                    k.emit('dve', lambda e, ps=ps, xs=xs, dm=dm: e.scalar_tensor_tensor(
                        out=xs, in0=ps[:], scalar=self.gate(0, dm, half), in1=xs, op0=ALU.mult, op1=ALU.add),
                        [pst, self.MOD_t, self.xT_t[half][dm][tt]], [self.xT_t[half][dm][tt]])

    def head_norm(self, tmps, src, src_t, dst, dst_t, gcol, eps_total, bank, extra_mul=None, extra_t=None):
        k = self.k
        sq, sq_t, rs, rs_t = tmps
        ps, pst = self.ps[bank], self.ps_t[bank]
        for tt in range(2):
            sl = slice(tt * 512, (tt + 1) * 512)
            k.emit('act', lambda e, sl=sl: e.activation(out=sq[:], in_=src[:, sl], func=AF.Square), [src_t], [sq_t])
            k.emit('pe', lambda e: e.matmul(ps[:], self.onesR[:], sq[:], start=True, stop=True), [self.onesR_t, sq_t], [pst])
            k.emit('act', lambda e: e.activation(out=rs[:], in_=ps[:], func=AF.Sqrt, bias=float(eps_total), scale=1.0), [pst], [rs_t])
            k.emit('dve', lambda e: e.reciprocal(out=rs[:], in_=rs[:]), [rs_t], [rs_t])
            if extra_mul is not None:
                k.emit('dve', lambda e, sl=sl: e.tensor_tensor(out=rs[:], in0=rs[:], in1=extra_mul[:, sl], op=ALU.mult), [rs_t, extra_t], [rs_t])
            k.emit('dve', lambda e, sl=sl: e.scalar_tensor_tensor(out=dst[:, sl], in0=src[:, sl], scalar=gcol, in1=rs[:], op0=ALU.mult, op1=ALU.mult),
                   [src_t, rs_t, self.PT_t, self.AV_t] + ([self.HV_t] if hasattr(self, 'HV_t') else []), [dst_t])

    def norm_tmps(self, es):
        return (self.tmp(es, 'hsq', [128, 512], F32R), Trk('hsq'), self.tmp(es, 'hrs', [128, 512], F32), Trk('hrs'))

    def attn_prep(self, layer):
        k = self.k
        j = layer // 3
        lam_init = 0.8 - 0.6 * math.exp(-0.3 * layer)
        with ExitStack() as es:
            lt = self.tmp(es, 'ltab', [128, 512], F32)
            lt_t = self.ptrk('ltab')
            k.dma(lt[:], self.d['lamtab'][:, :], writes=[lt_t])
            pr = self.tmp(es, 'lpr', [128, 2, 64], F32)
            pr_t = Trk('lpr')
            sm = self.tmp(es, 'lsm', [128, 2], F32)
            sm_t = Trk('lsm')
            base = j * 256
            lq = lt[:, base:base + 256].rearrange("p (a r n) -> p a r n", a=2, r=2)
            k.emit('dve', lambda e: e.tensor_tensor(out=pr[:], in0=lq[:, :, 0, :], in1=lq[:, :, 1, :], op=ALU.mult), [lt_t], [pr_t])
            k.emit('dve', lambda e: e.reduce_sum(out=sm[:], in_=pr[:], axis=AX.X), [pr_t], [sm_t])
            k.emit('act', lambda e: e.activation(out=sm[:], in_=sm[:], func=AF.Exp), [sm_t], [sm_t])
            k.emit('dve', lambda e: e.tensor_tensor(out=self.AV[:, 0:1], in0=sm[:, 0:1], in1=sm[:, 1:2], op=ALU.subtract), [sm_t], [self.AV_t])
            k.emit('dve', lambda e: e.tensor_scalar(out=self.AV[:, 0:1], in0=self.AV[:, 0:1], scalar1=float(lam_init), scalar2=None, op0=ALU.add),
                   [self.AV_t], [self.AV_t])
            k.emit('dve', lambda e: e.tensor_scalar(out=self.AV[:, 1:2], in0=self.AV[:, 0:1], scalar1=-1.0, scalar2=None, op0=ALU.mult),
                   [self.AV_t], [self.AV_t])
            osl, _ = PC['subln']
            k.emit('dve', lambda e: e.tensor_scalar(out=self.AV[:, 2:3], in0=self.PT[:, osl + j:osl + j + 1],
                                                    scalar1=float((1.0 - lam_init) * math.sqrt(128.0)), scalar2=None, op0=ALU.mult),
                   [self.PT_t, self.AV_t], [self.AV_t])
            k.barrier()

    def attn(self, layer, half):
        k, d, nc = self.k, self.d, self.nc
        j = layer // 3
        w_in = d['attn_w_in']
        scale = 0.125
        nkc = 2 if half == 0 else 12
        with ExitStack() as es:
            GH = 2
            V = self.tmp(es, 'aV', [128, 8, GH * 128], F32R)
            V_t = self.ptrk('aV', 8)
            ntm = self.norm_tmps(es)
            agrp = self.tmp(es, 'agrp', [128, GH, TOK], F32R)
            agrp_t = trks('agrp', GH)
            QT = [self.tmp(es, f'aQ{i}', [128, TOK], F32R) for i in range(1)]
            QT_t = trks('aQ', 1)
            KT = [self.tmp(es, f'aK{i}', [128, TOK], F32R) for i in range(1)]
            KT_t = self.ptrk('aK', 1)
            Pt = [self.tmp(es, f'aP{i}', [128, 512], F32R) for i in range(2)]
            Pt_t = trks('aP', 2)
            att = self.tmp(es, 'att', [128, TOK], F32)
            att_t = Trk('att')
            Rr = self.tmp(es, 'aR', [128, 2, 512], F32)
            Rr_t = Trk('aR')
            Tt, Tt_t = Rr, Rr_t
            if half == 1:
                ropet = self.tmp(es, 'arope', [128, 2048], F32)
                ropet_t = self.ptrk('arope')
                k.dma(ropet[:], d['rope'][:, :], writes=[ropet_t])
                COS = ropet[:, 0:1024]
                SIN = ropet[:, 1024:2048]
                raw = [self.tmp(es, f'araw{i}', [128, 512], F32R) for i in range(1)]
                raw_t = trks('araw', 1)
                ri_ = 0
                t1 = self.tmp(es, 'at1', [128, 512], F32)
                t1_t = Trk('at1')
                kcr = self.tmp(es, 'akcr', [128, 512], F32R)
                kcr_t = Trk('akcr')
                vcr = self.tmp(es, 'avcr', [128, 4 * GH * 128], F32R)
                vcr_t = Trk('avcr')
            pi = 0
            pb = 0
            for grp in range(8 // GH):
                for piece in range(1):
                    c0 = 2 * D + grp * 256
                    (wv,), wvt = self.wpiece([w_in[j, :, c0:c0 + 256]])
                    for tile in range(8):
                        b = 6 + (pb % 2)
                        pb += 1
                        ps, pst = self.ps[b], self.ps_t[b]
                        for c in range(NCH):
                            k.emit('pe', lambda e, ps=ps, c=c, tile=tile: e.matmul(
                                ps[:, 0:256], self.hT[:, c, tile * 128:(tile + 1) * 128], wv[:, c, :],
                                start=(c == 0), stop=(c == NCH - 1)), [wvt, self.hT_t[c][tile // 4]], [pst])
                        self.copy(self.evac_eng(), V[:, tile, piece * 256:(piece + 1) * 256], ps[:, 0:256], [pst], [V_t[tile]])
                if half == 0:
                    for tile in range(8):
                        k.dma(d['vout'][j, tile * 128:(tile + 1) * 128, grp * 256:(grp + 1) * 256], V[:, tile, :].bitcast(F32), reads=[V_t[tile]])
                else:
                    (vcv,), _ = self.wpiece([d['cv'][j, :, grp * 256:(grp + 1) * 256]], dest=(vcr, vcr_t))
                for hl in range(GH):
                    hh = grp * GH + hl
                    qi = 0
                    Q, Q_t, Kk, K_t = QT[qi], QT_t[qi], KT[qi], KT_t[qi]
                    (wq, wk), wt = self.wpiece([w_in[j, :, hh * 128:(hh + 1) * 128],
                                                w_in[j, :, D + hh * 128:D + (hh + 1) * 128]])
                    for wi, (w, dst, dst_t) in enumerate(((wq, Q, Q_t), (wk, Kk, K_t))):
                        for tt in range(2):
                            sl = slice(tt * 512, (tt + 1) * 512)
                            b = 6 + (pb % 2)
                            pb += 1
                            ps, pst = self.ps[b], self.ps_t[b]
                            for c in range(NCH):
                                k.emit('pe', lambda e, ps=ps, w=w, c=c, sl=sl: e.matmul(
                                    ps[:], w[:, c, :], self.hT[:, c, sl],
                                    start=(c == 0), stop=(c == NCH - 1)), [wt, self.hT_t[c][tt]], [pst])
                            if half == 0:
                                self.copy(self.evac_eng(), dst[:, sl], ps[:], [pst], [dst_t])
                            else:
                                rw, rw_t = raw[0], raw_t[0]
                                ri_ += 1
                                self.copy('act', rw[:], ps[:], [pst], [rw_t])
                                b2 = 6 + (pb % 2)
                                pb += 1
                                ps2, ps2t = self.ps[b2], self.ps_t[b2]
                                k.emit('pe', lambda e, ps2=ps2, rw=rw: e.matmul(ps2[:], self.permR[:], rw[:], start=True, stop=True),
                                       [self.permR_t, rw_t], [ps2t])
                                k.emit('pool', lambda e, rw=rw, sl=sl: e.tensor_tensor(out=t1[:], in0=rw[:].bitcast(F32), in1=COS[:, sl], op=ALU.mult),
                                       [rw_t, ropet_t], [t1_t])
                                k.emit('dve', lambda e, ps2=ps2, sl=sl, dst=dst: e.tensor_tensor(out=dst[:, sl], in0=ps2[:], in1=SIN[:, sl], op=ALU.mult),
                                       [ps2t, ropet_t], [dst_t])
                                k.emit('dve', lambda e, dst=dst, sl=sl: e.tensor_tensor(out=dst[:, sl], in0=dst[:, sl].bitcast(F32), in1=t1[:], op=ALU.add),
                                       [t1_t, dst_t], [dst_t])
                    if half == 0:
                        k.dma(d['kout'][j, hh, :, :], Kk[:].bitcast(F32), reads=[K_t])
                    else:
                        self.wpiece([d['ck'][j, hh, :, :]], dest=(kcr, kcr_t))

                    def keyT(comp, kc, s=0):
                        r = slice(comp * 64, (comp + 1) * 64)
                        if half == 0:
                            return Kk[r, s * 256 + kc * 128: s * 256 + (kc + 1) * 128], K_t
                        if kc < 4:
                            return kcr[r, kc * 128:(kc + 1) * 128], kcr_t
                        return Kk[r, (kc - 4) * 128:(kc - 3) * 128], K_t

                    def valT(kc, s=0):
                        cs = slice(hl * 128, (hl + 1) * 128)
                        if half == 0:
                            return V[:, s * 2 + kc, cs], V_t[s * 2 + kc]
                        if kc < 4:
                            return vcv[:, kc, cs], vcr_t
                        return V[:, kc - 4, cs], V_t[kc - 4]

                    if half == 0:
                        qw = 256
                        units = [(s, 0) for s in range(4)]
                    else:
                        qw = 512
                        units = [(0, qt) for qt in range(2)]
                    for (s, qt) in units:
                        q0 = s * 256 if half == 0 else qt * 512
                        for comp in range(2):
                            r = slice(comp * 64, (comp + 1) * 64)
                            if half == 0:
                                ob, zb = 2, 3
                                osl = slice(comp * 256, (comp + 1) * 256)
                            else:
                                ob, zb = 2 + comp * 2, 3 + comp * 2
                                osl = slice(0, 512)
                            pO, pO_t = self.ps[ob], self.ps_t[ob]
                            pZ, pZ_t = self.ps[zb], self.ps_t[zb]
                            if half == 0:
                                sb_ = pi % 2
                                pS, pS_t = self.ps[sb_], self.ps_t[sb_]
                                P_, P_t = Pt[pi % 2], Pt_t[pi % 2]
                                pi += 1
                                for kc in range(2):
                                    kl, kl_t = keyT(comp, kc, s)
                                    k.emit('pe', lambda e, pS=pS, kl=kl, kc=kc, r=r, q0=q0: e.matmul(
                                        pS[:, kc * 256:(kc + 1) * 256], kl, Q[r, q0:q0 + 256], start=True, stop=True),
                                        [kl_t, Q_t], [pS_t])
                                k.emit('act', lambda e, pS=pS, P_=P_: e.activation(out=P_[:], in_=pS[:], func=AF.Exp, scale=scale), [pS_t], [P_t])
                                for kc in range(2):
                                    vl, vl_t = valT(kc, s)
                                    k.emit('pe', lambda e, pO=pO, vl=vl, P_=P_, kc=kc, osl=osl: e.matmul(
                                        pO[:, osl], vl, P_[:, kc * 256:(kc + 1) * 256], start=(kc == 0), stop=(kc == 1)),
                                        [vl_t, P_t], [pO_t])
                                for kc in range(2):
                                    k.emit('pe', lambda e, pZ=pZ, P_=P_, kc=kc, osl=osl: e.matmul(
                                        pZ[:, osl], self.onesR[:], P_[:, kc * 256:(kc + 1) * 256], start=(kc == 0), stop=(kc == 1)),
                                        [self.onesR_t, P_t], [pZ_t])
                            else:
                                for kc in range(nkc):
                                    sb_ = pi % 2
                                    pS, pS_t = self.ps[sb_], self.ps_t[sb_]
                                    P_, P_t = Pt[pi % 2], Pt_t[pi % 2]
                                    pi += 1
                                    kl, kl_t = keyT(comp, kc)
                                    k.emit('pe', lambda e, pS=pS, kl=kl, r=r, q0=q0: e.matmul(
                                        pS[:], kl, Q[r, q0:q0 + 512], start=True, stop=True), [kl_t, Q_t], [pS_t])
                                    k.emit('act', lambda e, pS=pS, P_=P_: e.activation(out=P_[:], in_=pS[:], func=AF.Exp, scale=scale), [pS_t], [P_t])
                                    vl, vl_t = valT(kc)
                                    k.emit('pe', lambda e, pO=pO, vl=vl, P_=P_, kc=kc: e.matmul(
                                        pO[:], vl, P_[:], start=(kc == 0), stop=(kc == nkc - 1)), [vl_t, P_t], [pO_t])
                                    k.emit('pe', lambda e, pZ=pZ, P_=P_, kc=kc: e.matmul(
                                        pZ[:], self.onesR[:], P_[:], start=(kc == 0), stop=(kc == nkc - 1)), [self.onesR_t, P_t], [pZ_t])
                        if half == 0:
                            pO, pO_t, pZ, pZ_t = self.ps[2], self.ps_t[2], self.ps[3], self.ps_t[3]
                            k.emit('dve', lambda e, pZ=pZ: e.reciprocal(out=Rr[:, 0, :], in_=pZ[:]), [pZ_t], [Rr_t])
                            k.emit('dve', lambda e, pO=pO: e.tensor_tensor(out=Tt[:, 0, :], in0=pO[:], in1=Rr[:, 0, :], op=ALU.mult), [pO_t, Rr_t], [Tt_t])
                            k.emit('dve', lambda e, q0=q0: e.scalar_tensor_tensor(
                                out=att[:, q0:q0 + 256], in0=Tt[:, 0, 256:512], scalar=self.AV[:, 1:2], in1=Tt[:, 0, 0:256],
                                op0=ALU.mult, op1=ALU.add), [Tt_t, self.AV_t], [att_t])
                        else:
                            for comp in range(2):
                                pO, pO_t = self.ps[2 + comp * 2], self.ps_t[2 + comp * 2]
                                pZ, pZ_t = self.ps[3 + comp * 2], self.ps_t[3 + comp * 2]
                                k.emit('dve', lambda e, pZ=pZ, comp=comp: e.reciprocal(out=Rr[:, comp, :], in_=pZ[:]), [pZ_t], [Rr_t])
                                k.emit('dve', lambda e, pO=pO, comp=comp: e.tensor_tensor(out=Tt[:, comp, :], in0=pO[:], in1=Rr[:, comp, :], op=ALU.mult),
                                       [pO_t, Rr_t], [Tt_t])
                            k.emit('dve', lambda e, q0=q0: e.scalar_tensor_tensor(
                                out=att[:, q0:q0 + 512], in0=Tt[:, 1, :], scalar=self.AV[:, 1:2], in1=Tt[:, 0, :],
                                op0=ALU.mult, op1=ALU.add), [Tt_t, self.AV_t], [att_t])
                    self.head_norm(ntm, att[:], att_t, agrp[:, hl, :], agrp_t[hl], self.AV[:, 2:3], 128.0 * 1e-5, 6 + (pb % 2))
                    pb += 1
                self.out_proj(d['attn_w_out'][j, grp * GH * 128:(grp + 1) * GH * 128, :], agrp, agrp_t, GH, half, [6, 7, 0, 1])
            k.barrier()

    def hgrn_prep(self, layer):
        k = self.k
        ol, _ = PC['hgrn_lb']
        self.HV = self.k.sb('HV', [128, 3, 8])
        self.HV_t = Trk('HV')
        with ExitStack() as es:
            ex = self.tmp(es, 'hex', [128, 4, 8], F32)
            ex_t = Trk('hex')
            tot = self.tmp(es, 'htot', [128, 8], F32)
            tot_t = Trk('htot')
            k.emit('act', lambda e: e.activation(out=ex[:], in_=self.PT[:, ol:ol + 32].rearrange("p (l c) -> p l c", l=4), func=AF.Exp),
                   [self.PT_t], [ex_t])
            k.emit('dve', lambda e: e.tensor_tensor(out=tot[:], in0=ex[:, 0, :], in1=ex[:, 1, :], op=ALU.add), [ex_t], [tot_t])
            for l in (2, 3):
                k.emit('dve', lambda e, l=l: e.tensor_tensor(out=tot[:], in0=tot[:], in1=ex[:, l, :], op=ALU.add), [ex_t, tot_t], [tot_t])
            k.emit('dve', lambda e: e.reciprocal(out=tot[:], in_=tot[:]), [tot_t], [tot_t])
            k.emit('dve', lambda e: e.tensor_copy(out=self.HV[:, 0, :], in_=ex[:, 1, :]), [ex_t], [self.HV_t])
            for l in range(2, layer + 1):
                k.emit('dve', lambda e, l=l: e.tensor_tensor(out=self.HV[:, 0, :], in0=self.HV[:, 0, :], in1=ex[:, l, :], op=ALU.add),
                       [ex_t, self.HV_t], [self.HV_t])
            k.emit('dve', lambda e: e.tensor_tensor(out=self.HV[:, 0, :], in0=self.HV[:, 0, :], in1=tot[:], op=ALU.mult), [tot_t, self.HV_t], [self.HV_t])
            k.emit('dve', lambda e: e.tensor_scalar(out=self.HV[:, 1, :], in0=self.HV[:, 0, :], scalar1=-1.0, scalar2=1.0, op0=ALU.mult, op1=ALU.add),
                   [self.HV_t], [self.HV_t])
            on, _ = PC['hgrn_norm']
            k.emit('dve', lambda e: e.tensor_scalar(out=self.HV[:, 2, 0:1], in0=self.PT[:, on:on + 1], scalar1=float(math.sqrt(128.0)), scalar2=None, op0=ALU.mult),
                   [self.PT_t, self.HV_t], [self.HV_t])
            k.barrier()

    def hgrn(self, layer, half):
        k, d, nc = self.k, self.d, self.nc
        j = layer // 3
        if half == 0:
            self.hgrn_prep(layer)
        w_in = d['hgrn_w_in']
        nseq = 4 if half == 0 else 1
        cps = 16 // nseq
        oo, _ = CC['ones']
        ONES = self.CT[:, oo:oo + 1].broadcast_to([128, TOK])
        oi, _ = CC['ident']
        IDENT = self.CT[:, oi:oi + 128]
        masks = []
        for nm in ('mask_f', 'mask_b'):
            om, _ = CC[nm]
            masks.append(self.CT[0:64, om:om + 256])
        with ExitStack() as es:
            V64 = self.tmp(es, 'hV', [64, 16, 128], F32)
            V64_t = trks('hV', 16)
            mix = self.tmp(es, 'hmix', [128, 1, TOK], F32R)
            mix_t = trks('hmix', 1)
            ntm = self.norm_tmps(es)
            qT = self.tmp(es, 'hq', [128, TOK], F32)
            qT_t = Trk('hq')
            gs, gs_t = qT, qT_t
            oT = self.tmp(es, 'ho', [128, TOK], F32)
            oT_t = Trk('ho')
            Fb = [self.tmp(es, f'hF{i}', [128, TOK], F32) for i in range(2)]
            Fb_t = trks('hF', 2)
            L = self.tmp(es, 'hL', [128, TOK], F32)
            L_t = Trk('hL')
            Gp = self.tmp(es, 'hGp', [128, 64 + TOK + 64], F32)
            Gp_t = Trk('hGp')
            E1 = self.tmp(es, 'hE1', [128, TOK], F32)
            E1_t = Trk('hE1')
            E2 = self.tmp(es, 'hE2', [128, TOK], F32)
            E2_t = Trk('hE2')
            Am = self.tmp(es, 'hAm', [64, 4, 64], F32)
            Am_t = Trk('hAm')
            Ktok = self.tmp(es, 'hKt', [64, 4, 128], F32)
            Ktok_t = Trk('hKt')
            Sb = [self.tmp(es, f'hS{i}', [128, 128], F32) for i in range(2)]
            Sb_t = trks('hS', 2)
            DK = self.tmp(es, 'hDK', [128, 3, 16], F32)
            DK_t = Trk('hDK')
            G3 = self.tmp(es, 'hG3', [128, 3, 16], F32)
            G3_t = Trk('hG3')
            Sp = [self.tmp(es, f'hSp{i}', [128, 128], F32) for i in range(2)]
            Sp_t = trks('hSp', 2)
            tS = self.tmp(es, 'htS', [128, 128], F32)
            tS_t = Trk('htS')
            spi = 0
            sout_t = self.ptrk('hso')
            if half == 0:
                sout = self.tmp(es, 'hso', [128, 4, 2, 128], F32)
            k.emit('dve', lambda e: e.memset(Gp[:], 0.0), [], [Gp_t])
            pb = 0
            for pair in range(4):
                for hl in range(2):
                    hh = pair * 2 + hl
                    (wi,), wit = self.wpiece([w_in[j, :, 3 * D + hh * 128:3 * D + (hh + 1) * 128]])
                    for tt in range(2):
                        sl = slice(tt * 512, (tt + 1) * 512)
                        b = 6 + (pb % 2)
                        pb += 1
                        ps, pst = self.ps[b], self.ps_t[b]
                        for c in range(NCH):
                            k.emit('pe', lambda e, ps=ps, c=c, sl=sl: e.matmul(ps[:], wi[:, c, :], self.hT[:, c, sl], start=(c == 0), stop=(c == NCH - 1)),
                                   [wit, self.hT_t[c][tt]], [pst])
                        self.copy(self.evac_eng(), L[:, sl], ps[:], [pst], [L_t])
                    for g4 in range(4):
                        b = 6 + (pb % 2)
                        pb += 1
                        ps, pst = self.ps[b], self.ps_t[b]
                        for q_ in range(4):
                            ch = g4 * 4 + q_
                            k.emit('pe', lambda e, ps=ps, q_=q_, ch=ch: e.transpose(ps[0:64, q_ * 128:(q_ + 1) * 128], L[:, ch * 64:(ch + 1) * 64], IDENT),
                                   [L_t, self.CT_t], [pst])
                        self.copy(self.evac_eng(), V64[:, g4 * 4:(g4 + 1) * 4, :].rearrange("p a n -> p (a n)"), ps[0:64, :], [pst], V64_t[g4 * 4:(g4 + 1) * 4])
                    lbc = self.HV[:, 0, hh:hh + 1]
                    omc = self.HV[:, 1, hh:hh + 1]
                    (wq, wzf), wt1 = self.wpiece([w_in[j, :, hh * 128:(hh + 1) * 128], w_in[j, :, D + hh * 128:D + (hh + 1) * 128]])
                    (wzb,), wt2 = self.wpiece([w_in[j, :, 2 * D + hh * 128:2 * D + (hh + 1) * 128]])
                    for (w, wt, kindp) in ((wq, wt1, 'q'), (wzf, wt1, 'zf'), (wzb, wt2, 'zb')):
                        for tt in range(2):
                            sl = slice(tt * 512, (tt + 1) * 512)
                            b = 6 + (pb % 2)
                            pb += 1
                            ps, pst = self.ps[b], self.ps_t[b]
                            for c in range(NCH):
                                k.emit('pe', lambda e, ps=ps, w=w, c=c, sl=sl: e.matmul(
                                    ps[:], w[:, c, :], self.hT[:, c, sl], start=(c == 0), stop=(c == NCH - 1)),
                                    [wt, self.hT_t[c][tt]], [pst])
                            if kindp == 'q':
                                k.emit('act', lambda e, ps=ps, sl=sl: e.activation(out=qT[:, sl], in_=ps[:], func=AF.Copy, scale=float(128.0 ** -0.5)),
                                       [pst], [qT_t])
                            elif kindp == 'g':
                                k.emit('act', lambda e, ps=ps, sl=sl: e.activation(out=gs[:, sl], in_=ps[:], func=AF.Silu), [pst], [gs_t])
                            else:
                                di = 0 if kindp == 'zf' else 1
                                k.emit('act', lambda e, ps=ps, sl=sl, di=di: e.activation(out=Fb[di][:, sl], in_=ps[:], func=AF.Sigmoid), [pst], [Fb_t[di]])
                    for di in range(2):
                        F_, F_t = Fb[di], Fb_t[di]
                        k.emit('dve', lambda e, F_=F_: e.tensor_scalar(out=F_[:], in0=F_[:], scalar1=omc, scalar2=lbc, op0=ALU.mult, op1=ALU.add),
                               [F_t, self.HV_t], [F_t])
                        k.emit('act', lambda e, F_=F_: e.activation(out=L[:], in_=F_[:], func=AF.Ln), [F_t], [L_t])
                        k.emit('pool', lambda e, F_=F_: e.tensor_scalar(out=F_[:], in0=F_[:], scalar1=-1.0, scalar2=1.0, op0=ALU.mult, op1=ALU.add),
                               [F_t], [F_t])
                        k.emit('dve', lambda e: e.tensor_tensor_scan(out=Gp[:, 64:64 + TOK], data0=ONES, data1=L[:], initial=0.0,
                                                                     op0=ALU.mult, op1=ALU.add), [L_t, self.CT_t], [Gp_t])
                        Lv = L[:].rearrange("p (j n) -> p j n", n=64)
                        if di == 0:
                            gprev = Gp[:, 63:63 + TOK].rearrange("p (j n) -> p j n", n=64)[:, :, 0:1].broadcast_to([128, 16, 64])
                            gcur = Gp[:, 64:64 + TOK].rearrange("p (j n) -> p j n", n=64)
                            k.emit('dve', lambda e: e.tensor_tensor(out=Lv, in0=gcur, in1=gprev, op=ALU.subtract), [Gp_t], [L_t])
                        else:
                            gend = Gp[:, 127:127 + TOK].rearrange("p (j n) -> p j n", n=64)[:, :, 0:1].broadcast_to([128, 16, 64])
                            gsh = Gp[:, 63:63 + TOK].rearrange("p (j n) -> p j n", n=64)
                            k.emit('dve', lambda e: e.tensor_tensor(out=Lv, in0=gend, in1=gsh, op=ALU.subtract), [Gp_t], [L_t])
                        pos = 63 if di == 0 else 0
                        mid = 31 if di == 0 else 32
                        k.emit('pool', lambda e, pos=pos: e.tensor_copy(out=G3[:, 0, :], in_=Lv[:, :, pos]), [L_t], [G3_t])
                        k.emit('pool', lambda e, mid=mid: e.tensor_copy(out=G3[:, 1, :], in_=Lv[:, :, mid]), [L_t], [G3_t])
                        k.emit('dve', lambda e: e.tensor_tensor(out=G3[:, 2, :], in0=G3[:, 0, :], in1=G3[:, 1, :], op=ALU.subtract), [G3_t], [G3_t])
                        k.emit('act', lambda e: e.activation(out=DK[:], in_=G3[:], func=AF.Exp), [G3_t], [DK_t])
                        k.emit('dve', lambda e: e.tensor_tensor(out=Lv, in0=Lv, in1=G3[:, 1, :].unsqueeze(2).broadcast_to([128, 16, 64]), op=ALU.subtract),
                               [L_t, G3_t], [L_t])
                        k.emit('act', lambda e: e.activation(out=E1[:], in_=L[:], func=AF.Exp), [L_t], [E1_t])
                        k.emit('act', lambda e: e.activation(out=E2[:], in_=L[:], func=AF.Exp, scale=-1.0), [L_t], [E2_t])
                        k.emit('dve', lambda e: e.tensor_tensor(out=E1[:], in0=E1[:], in1=qT[:], op=ALU.mult), [E1_t, qT_t], [E1_t])
                        k.emit('pool', lambda e, F_=F_: e.tensor_tensor(out=E2[:], in0=E2[:], in1=F_[:], op=ALU.mult), [E2_t, F_t], [E2_t])
                        order = list(range(16)) if di == 0 else list(range(15, -1, -1))
                        if half == 1:
                            k.dma(Sb[0][:], d['st_hgrn'][di, hh, :, :], writes=[Sb_t[0]])
                        si = 0
                        for gi in range(4):
                            chs = order[gi * 4:(gi + 1) * 4]
                            lo = min(chs)
                            psA, psA_t = self.ps[0], self.ps_t[0]
                            psT, psT_t = self.ps[1], self.ps_t[1]
                            for ch in chs:
                                cs = slice(ch * 64, (ch + 1) * 64)
                                q_ = ch - lo
                                k.emit('pe', lambda e, cs=cs, q_=q_: e.matmul(psA[0:64, q_ * 64:(q_ + 1) * 64], E2[:, cs], E1[:, cs], start=True, stop=True),
                                       [E1_t, E2_t], [psA_t])
                                k.emit('pe', lambda e, cs=cs, q_=q_: e.transpose(psT[0:64, q_ * 128:(q_ + 1) * 128], E2[:, cs], IDENT),
                                       [E2_t, self.CT_t], [psT_t])
                            k.emit('dve', lambda e, di=di: e.tensor_tensor(out=Am[:].rearrange("p a n -> p (a n)"), in0=psA[0:64, 0:256], in1=masks[di], op=ALU.mult),
                                   [psA_t, self.CT_t], [Am_t])
                            k.emit('act', lambda e: e.activation(out=Ktok[:].rearrange("p a n -> p (a n)"), in_=psT[0:64, :], func=AF.Copy), [psT_t], [Ktok_t])
                            psO, psO_t = self.ps[2], self.ps_t[2]
                            for ch in chs:
                                cs = slice(ch * 64, (ch + 1) * 64)
                                q_ = ch - lo
                                loc = ch % cps
                                first = (loc == 0) if di == 0 else (loc == cps - 1)
                                last = (loc == cps - 1) if di == 0 else (loc == 0)
                                seq = ch // cps
                                zero_init = first and half == 0
                                vv = V64[:, ch, :]
                                S_prev, S_prev_t = Sb[si], Sb_t[si]
                                k.emit('pe', lambda e, vv=vv, q_=q_, zero_init=zero_init: e.matmul(
                                    psO[:, q_ * 64:(q_ + 1) * 64], vv, Am[:, q_, :], start=True, stop=zero_init), [V64_t[ch], Am_t], [psO_t])
                                if not zero_init:
                                    sp_, sp_t = Sp[spi % 2], Sp_t[spi % 2]
                                    spi += 1
                                    k.emit('act', lambda e, sp_=sp_, S_prev=S_prev, ch=ch: e.activation(
                                        out=sp_[:], in_=S_prev[:], func=AF.Identity, scale=DK[:, 1, ch:ch + 1]), [S_prev_t, DK_t], [sp_t])
                                    k.emit('pe', lambda e, cs=cs, q_=q_, sp_=sp_: e.matmul(
                                        psO[:, q_ * 64:(q_ + 1) * 64], sp_[:], E1[:, cs], start=False, stop=True), [sp_t, E1_t], [psO_t])
                                bS = 3 + (pb % 2)
                                pb += 1
                                psS, psS_t = self.ps[bS], self.ps_t[bS]
                                k.emit('pe', lambda e, psS=psS, vv=vv, q_=q_: e.matmul(
                                    psS[:, 0:128], Ktok[:, q_, :], vv, start=True, stop=True), [Ktok_t, V64_t[ch]], [psS_t])
                                if last and half == 0:
                                    dstS, dstS_t = sout[:, seq, di, :], sout_t
                                else:
                                    si = 1 - si
                                    dstS, dstS_t = Sb[si][:], Sb_t[si]
                                if zero_init:
                                    k.emit('act', lambda e, psS=psS, ch=ch, dstS=dstS: e.activation(
                                        out=dstS, in_=psS[:, 0:128], func=AF.Identity, scale=DK[:, 2, ch:ch + 1]), [psS_t, DK_t], [dstS_t])
                                else:
                                    k.emit('act', lambda e, psS=psS, ch=ch: e.activation(
                                        out=tS[:], in_=psS[:, 0:128], func=AF.Identity, scale=DK[:, 2, ch:ch + 1]), [psS_t, DK_t], [tS_t])
                                    k.emit('dve', lambda e, dstS=dstS, S_prev=S_prev, ch=ch: e.scalar_tensor_tensor(
                                        out=dstS, in0=S_prev[:], scalar=DK[:, 0, ch:ch + 1], in1=tS[:], op0=ALU.mult, op1=ALU.add),
                                        [S_prev_t, DK_t, tS_t], [dstS_t])
                            osl = slice(lo * 64, lo * 64 + 256)
                            if di == 0:
                                k.emit('dve', lambda e, osl=osl: e.tensor_copy(out=oT[:, osl], in_=psO[:, 0:256]), [psO_t], [oT_t])
                            else:
                                k.emit('dve', lambda e, osl=osl: e.tensor_tensor(out=oT[:, osl], in0=oT[:, osl], in1=psO[:, 0:256], op=ALU.add), [psO_t, oT_t], [oT_t])
                    if half == 0:
                        k.dma(d['hgout'][:, :, hh, :, :].rearrange("s d k e -> k s d e"), sout[:], reads=[sout_t])
                    (wg,), wt3 = self.wpiece([w_in[j, :, 4 * D + hh * 128:4 * D + (hh + 1) * 128]])
                    for tt in range(2):
                        sl = slice(tt * 512, (tt + 1) * 512)
                        ps, pst = self.ps[6 + tt], self.ps_t[6 + tt]
                        for c in range(NCH):
                            k.emit('pe', lambda e, ps=ps, c=c, sl=sl: e.matmul(ps[:], wg[:, c, :], self.hT[:, c, sl], start=(c == 0), stop=(c == NCH - 1)),
                                   [wt3, self.hT_t[c][tt]], [pst])
                        k.emit('act', lambda e, ps=ps, sl=sl: e.activation(out=gs[:, sl], in_=ps[:], func=AF.Silu), [pst], [gs_t])
                    self.head_norm(ntm, oT[:], oT_t, mix[:, 0, :], mix_t[0], self.HV[:, 2, 0:1], 128.0 * 1e-6, 5, extra_mul=gs[:], extra_t=gs_t)
                    self.out_proj(d['hgrn_w_out'][j, hh * 128:(hh + 1) * 128, :], mix, mix_t, 1, half, [6, 7])
            k.barrier()

    def gdn(self, layer, half):
        k, d, nc = self.k, self.d, self.nc
        w_in = d['gdn_w_in']
        nseq = 4 if half == 0 else 1
        cps = 16 // nseq
        seqlen = TOK // nseq
        CT = self.CT

        def cc(name, rows=64, w=None):
            o_, w_ = CC[name]
            return CT[0:rows, o_:o_ + (w or w_)]

        ONES_ROW = cc('ones', 128, 1).broadcast_to([128, TOK])
        IDENT = cc('ident', 128)
        ID64 = cc('ident', 64, 64)
        TRIF = cc('mask_f', 64, 64)
        TRIB = cc('mask_b', 64, 64)
        ONES64 = cc('ones', 64, 64)
        NEGU = [cc('negu_f'), cc('negu_b')]
        NEGL = [cc('negl_f'), cc('negl_b')]
        SNEG = [cc('sneg_f'), cc('sneg_b')]
        IDREP = cc('idrep')
        osel, _ = CC['sel']
        onsel, _ = CC['nsel']

        def SEL(kk, m):
            return CT[0:4, osel + kk * 128: osel + kk * 128 + m]

        def NSEL(kk):
            return CT[0:4, onsel + kk * 64: onsel + (kk + 1) * 64]

        oa, _ = PC['gdn_alog_col']
        odt, _ = PC['gdn_dt_col']
        oar, _ = PC['gdn_alog_row']
        odr, _ = PC['gdn_dt_row']
        ocv, _ = PC['gdn_conv']
        ogn, _ = PC['gdn_norm']
        scr_t = self.ptrk('gscr')
        with ExitStack() as es0:
            GC = self.tmp(es0, 'gGC', [64, 16, 16]); BE = self.tmp(es0, 'gBE', [64, 16, 16])
            NBE = self.tmp(es0, 'gNBE', [64, 16, 16]); C1 = self.tmp(es0, 'gC1', [64, 16, 16])
            WW = self.tmp(es0, 'gWW', [64, 16, 16])
            TB_t = Trk('gTB')
            GV = self.tmp(es0, 'gGV', [128, 20])
            GV_t = Trk('gGV')
            k.emit('act', lambda e: e.activation(out=GV[:, 0:1], in_=self.PT[:, oa:oa + 1], func=AF.Exp), [self.PT_t], [GV_t])
            k.emit('dve', lambda e: e.tensor_scalar(out=GV[:, 0:1], in0=GV[:, 0:1], scalar1=-1.0, scalar2=None, op0=ALU.mult), [GV_t], [GV_t])
            k.emit('act', lambda e: e.activation(out=GV[:, 4:20], in_=self.PT[:, oar:oar + 16], func=AF.Exp), [self.PT_t], [GV_t])
            k.emit('dve', lambda e: e.tensor_scalar(out=GV[:, 4:20], in0=GV[:, 4:20], scalar1=-1.0, scalar2=None, op0=ALU.mult), [GV_t], [GV_t])
            k.emit('dve', lambda e: e.tensor_scalar(out=GV[:, 1:2], in0=self.PT[:, ogn:ogn + 1], scalar1=float(math.sqrt(128.0)), scalar2=None, op0=ALU.mult),
                   [self.PT_t, GV_t], [GV_t])
            (wab,), wab_t = self.wpiece([w_in[0, :, 4 * D:4 * D + 32]], rounded=False)
            with ExitStack() as es:
                LA = self.tmp(es, 'gLA', [16, TOK]); LA_t = Trk('gLA')
                Gp = self.tmp(es, 'gGp', [16, 64 + TOK + 64]); Gp_t = Trk('gGp')
                GF = self.tmp(es, 'gGF', [16, TOK]); GF_t = self.ptrk('gGF')
                GB = self.tmp(es, 'gGB', [16, TOK]); GB_t = self.ptrk('gGB')
                BT = self.tmp(es, 'gBT', [16, TOK]); BT_t = self.ptrk('gBT')
                LAt = self.tmp(es, 'gLAt', [64, 16, 16]); LAt_t = Trk('gLAt')
                k.emit('dve', lambda e: e.memset(Gp[:], 0.0), [], [Gp_t])
                hTf = self.hT[:].bitcast(F32)
                for part in range(2):
                    for tt in range(2):
                        sl = slice(tt * 512, (tt + 1) * 512)
                        ps, pst = self.ps[6 + tt], self.ps_t[6 + tt]
                        for c in range(NCH):
                            k.emit('pe', lambda e, ps=ps, c=c, sl=sl, part=part: e.matmul(
                                ps[0:16, :], wab[:, c, part * 16:(part + 1) * 16], hTf[:, c, sl], start=(c == 0), stop=(c == NCH - 1)),
                                [wab_t, self.hT_t[c][tt]], [pst])
                        if part == 0:
                            k.emit('act', lambda e, ps=ps, sl=sl: e.activation(out=LA[:, sl], in_=ps[0:16, :], func=AF.Exp, bias=self.PT[0:16, odt:odt + 1]),
                                   [pst, self.PT_t], [LA_t])
                        else:
                            k.emit('act', lambda e, ps=ps, sl=sl: e.activation(out=BT[:, sl], in_=ps[0:16, :], func=AF.Sigmoid), [pst], [BT_t])
                k.emit('act', lambda e: e.activation(out=LA[:], in_=LA[:], func=AF.Ln, bias=1.0), [LA_t], [LA_t])
                k.emit('dve', lambda e: e.tensor_scalar(out=LA[:], in0=LA[:], scalar1=GV[0:16, 0:1], scalar2=None, op0=ALU.mult), [LA_t, GV_t], [LA_t])
                k.emit('dve', lambda e: e.tensor_tensor_scan(out=Gp[:, 64:64 + TOK], data0=ONES_ROW[0:16, :], data1=LA[:], initial=0.0,
                                                             op0=ALU.mult, op1=ALU.add), [LA_t, self.CT_t], [Gp_t])
                gprev = Gp[:, 63:63 + TOK].rearrange("p (j n) -> p j n", n=64)[:, :, 0:1].broadcast_to([16, 16, 64])
                gcur = Gp[:, 64:64 + TOK].rearrange("p (j n) -> p j n", n=64)
                k.emit('dve', lambda e: e.tensor_tensor(out=GF[:].rearrange("p (j n) -> p j n", n=64), in0=gcur, in1=gprev, op=ALU.subtract), [Gp_t], [GF_t])
                gend = Gp[:, 127:127 + TOK].rearrange("p (j n) -> p j n", n=64)[:, :, 0:1].broadcast_to([16, 16, 64])
                gsh = Gp[:, 63:63 + TOK].rearrange("p (j n) -> p j n", n=64)
                k.emit('dve', lambda e: e.tensor_tensor(out=GB[:].rearrange("p (j n) -> p j n", n=64), in0=gend, in1=gsh, op=ALU.subtract), [Gp_t], [GB_t])
                k.dma(d['gscr'][0:16, :], GF[:], reads=[GF_t], writes=[scr_t])
                k.dma(d['gscr'][16:32, :], GB[:], reads=[GB_t], writes=[scr_t])
                k.dma(d['gscr'][32:48, :], BT[:], reads=[BT_t], writes=[scr_t])
                ps, pst = self.ps[5], self.ps_t[5]
                for ch in range(16):
                    for c in range(NCH):
                        k.emit('pe', lambda e, c=c, ch=ch: e.matmul(
                            ps[0:64, ch * 32:(ch + 1) * 32], hTf[:, c, ch * 64:(ch + 1) * 64], wab[:, c, :], start=(c == 0), stop=(c == NCH - 1)),
                            [wab_t, self.hT_t[c][ch // 8]], [pst])
                pv = ps[0:64, :].rearrange("p (c n) -> p c n", n=32)
                k.emit('dve', lambda e: e.tensor_tensor(out=LAt[:], in0=pv[:, :, 0:16],
                                                        in1=self.PT[0:64, odr:odr + 16].unsqueeze(1).broadcast_to([64, 16, 16]), op=ALU.add),
                       [pst, self.PT_t], [LAt_t])
                k.emit('act', lambda e: e.activation(out=BE[:], in_=pv[:, :, 16:32], func=AF.Sigmoid), [pst], [TB_t])
                k.emit('act', lambda e: e.activation(out=LAt[:], in_=LAt[:], func=AF.Exp), [LAt_t], [LAt_t])
                k.emit('act', lambda e: e.activation(out=LAt[:], in_=LAt[:], func=AF.Ln, bias=1.0), [LAt_t], [LAt_t])
                k.emit('dve', lambda e: e.tensor_tensor(out=LAt[:], in0=LAt[:], in1=GV[0:64, 4:20].unsqueeze(1).broadcast_to([64, 16, 16]), op=ALU.mult),
                       [LAt_t, GV_t], [LAt_t])
                LAf = LAt[:].rearrange("p c n -> p (c n)")
                pF, pF_t = self.ps[0], self.ps_t[0]
                pB, pB_t = self.ps[1], self.ps_t[1]
                pT, pT_t = self.ps[2], self.ps_t[2]
                k.emit('pe', lambda e: e.matmul(pF[0:64, 0:256], TRIF, LAf, start=True, stop=True), [LAt_t, self.CT_t], [pF_t])
                k.emit('pe', lambda e: e.matmul(pB[0:64, 0:256], TRIB, LAf, start=True, stop=True), [LAt_t, self.CT_t], [pB_t])
                k.emit('pe', lambda e: e.matmul(pT[0:64, 0:256], ONES64, LAf, start=True, stop=True), [LAt_t, self.CT_t], [pT_t])
                pFv = pF[0:64, 0:256].rearrange("p (c n) -> p c n", n=16)
                pBv = pB[0:64, 0:256].rearrange("p (c n) -> p c n", n=16)
                pTv = pT[0:64, 0:256].rearrange("p (c n) -> p c n", n=16)
                k.emit('dve', lambda e: e.tensor_copy(out=GC[:, :, 0:8], in_=pFv[:, :, 0:8]), [pF_t], [TB_t])
                k.emit('dve', lambda e: e.tensor_copy(out=GC[:, :, 8:16], in_=pBv[:, :, 8:16]), [pB_t], [TB_t])
                k.emit('dve', lambda e: e.tensor_tensor(out=WW[:], in0=pTv, in1=GC[:], op=ALU.subtract), [pT_t, TB_t], [TB_t])
                k.emit('act', lambda e: e.activation(out=WW[:], in_=WW[:], func=AF.Exp), [TB_t], [TB_t])
                k.emit('act', lambda e: e.activation(out=C1[:], in_=GC[:], func=AF.Exp), [TB_t], [TB_t])
                k.emit('dve', lambda e: e.scalar_tensor_tensor(out=C1[:], in0=C1[:], scalar=-1.0, in1=BE[:], op0=ALU.mult, op1=ALU.mult), [TB_t], [TB_t])
                k.emit('dve', lambda e: e.tensor_scalar(out=NBE[:], in0=BE[:], scalar1=-1.0, scalar2=None, op0=ALU.mult), [TB_t], [TB_t])
                k.barrier()
            stop = self.cfg.get('gdn_stop', 9)
            if stop <= 1:
                return
            with ExitStack() as es:
                qn = self.tmp(es, 'gq', [128, TOK]); qn_t = Trk('gq')
                kn = self.tmp(es, 'gk', [128, TOK]); kn_t = Trk('gk')
                vT = self.tmp(es, 'gv', [128, TOK]); vT_t = Trk('gv')
                oT = self.tmp(es, 'go', [128, TOK]); oT_t = Trk('go')
                Qt = self.tmp(es, 'gQt', [128, TOK]); Qt_t = Trk('gQt')
                mix = self.tmp(es, 'gmix', [128, 1, TOK], F32R); mix_t = trks('gmix', 1)
                ntm = self.norm_tmps(es)
                HR4 = self.tmp(es, 'gHR', [4, TOK]); HR4_t = self.ptrk('gHR')
                bt = [self.tmp(es, f'gb{i}', [64, 4, 64]) for i in range(10)]
                bt_t = trks('gb', 10)
                ktok = self.tmp(es, 'gkt', [64, 4, 128]); ktok_t = Trk('gkt')
                vtok = self.tmp(es, 'gvt', [64, 4, 128]); vtok_t = Trk('gvt')
                sm = [self.tmp(es, f'gs{i}', [64, 128]) for i in range(4)]
                sm_t = trks('gs', 4)
                Sb = [self.tmp(es, f'gS{i}', [128, 128]) for i in range(2)]
                Sb_t = trks('gS', 2)
                DKg = self.tmp(es, 'gDK', [128, 16]); DKg_t = Trk('gDK')
                sout_t = self.ptrk('gso')
                if half == 0:
                    sout = self.tmp(es, 'gso', [128, 4, 2, 128])
                pb = 0
                for hh in range(8):
                    (wq, wk), wt1 = self.wpiece([w_in[0, :, hh * 128:(hh + 1) * 128], w_in[0, :, D + hh * 128:D + (hh + 1) * 128]])
                    (wv,), wt2 = self.wpiece([w_in[0, :, 2 * D + hh * 128:2 * D + (hh + 1) * 128]])
                    k.dma_group([(HR4[0:1, :], d['gscr'][hh:hh + 1, :]), (HR4[1:2, :], d['gscr'][24 + hh:25 + hh, :]),
                                 (HR4[2:3, :], d['gscr'][32 + hh:33 + hh, :]), (HR4[3:4, :], d['gscr'][40 + hh:41 + hh, :])], [HR4_t])
                    HR4_t.rs[scr_t.w[0]] = 0
                    for ti, (w, wt, dst, dst_t) in enumerate(((wq, wt1, qn, qn_t), (wk, wt1, kn, kn_t), (wv, wt2, vT, vT_t))):
                        fch = ti * 8 + hh
                        w0 = self.PT[:, ocv + 0 * 24 + fch: ocv + 0 * 24 + fch + 1]
                        w1 = self.PT[:, ocv + 1 * 24 + fch: ocv + 1 * 24 + fch + 1]
                        w2 = self.PT[:, ocv + 2 * 24 + fch: ocv + 2 * 24 + fch + 1]
                        pss = []
                        for tt in range(2):
                            sl = slice(tt * 512, (tt + 1) * 512)
                            b = 6 + tt
                            ps, pst = self.ps[b], self.ps_t[b]
                            pss.append((ps, pst))
                            for c in range(NCH):
                                k.emit('pe', lambda e, ps=ps, w=w, c=c, sl=sl: e.matmul(
                                    ps[:], w[:, c, :], self.hT[:, c, sl], start=(c == 0), stop=(c == NCH - 1)), [wt, self.hT_t[c][tt]], [pst])
                            k.emit('act', lambda e, ps=ps, sl=sl, dst=dst, w1=w1: e.activation(out=dst[:, sl], in_=ps[:], func=AF.Copy, scale=w1),
                                   [pst, self.PT_t], [dst_t])
                        for tt in range(2):
                            ps, pst = pss[tt]
                            sl_ = min(seqlen, 512)
                            ns = 512 // sl_
                            pv = ps[:].rearrange("p (s n) -> p s n", s=ns)
                            av = dst[:, tt * 512:(tt + 1) * 512].rearrange("p (s n) -> p s n", s=ns)
                            k.emit('dve', lambda e, pv=pv, av=av, w0=w0, sl_=sl_: e.scalar_tensor_tensor(
                                out=av[:, :, 1:sl_], in0=pv[:, :, 0:sl_ - 1], scalar=w0, in1=av[:, :, 1:sl_], op0=ALU.mult, op1=ALU.add),
                                [pst, self.PT_t, dst_t], [dst_t])
                            k.emit('dve', lambda e, pv=pv, av=av, w2=w2, sl_=sl_: e.scalar_tensor_tensor(
                                out=av[:, :, 0:sl_ - 1], in0=pv[:, :, 1:sl_], scalar=w2, in1=av[:, :, 0:sl_ - 1], op0=ALU.mult, op1=ALU.add),
                                [pst, self.PT_t, dst_t], [dst_t])
                        if seqlen > 512:
                            p0, p0t = pss[0]
                            p1, p1t = pss[1]
                            k.emit('dve', lambda e, p0=p0, dst=dst, w0=w0: e.scalar_tensor_tensor(
                                out=dst[:, 512:513], in0=p0[:, 511:512], scalar=w0, in1=dst[:, 512:513], op0=ALU.mult, op1=ALU.add),
                                [p0t, self.PT_t, dst_t], [dst_t])
                            k.emit('dve', lambda e, p1=p1, dst=dst, w2=w2: e.scalar_tensor_tensor(
                                out=dst[:, 511:512], in0=p1[:, 0:1], scalar=w2, in1=dst[:, 511:512], op0=ALU.mult, op1=ALU.add),
                                [p1t, self.PT_t, dst_t], [dst_t])
                        k.emit('act', lambda e, dst=dst: e.activation(out=dst[:], in_=dst[:], func=AF.Silu), [dst_t], [dst_t])
                        if ti < 2:
                            sq, sq_t, rs, rs_t = ntm
                            for tt in range(2):
                                sl = slice(tt * 512, (tt + 1) * 512)
                                ps, pst = self.ps[5], self.ps_t[5]
                                k.emit('act', lambda e, dst=dst, sl=sl: e.activation(out=sq[:], in_=dst[:, sl], func=AF.Square), [dst_t], [sq_t])
                                k.emit('pe', lambda e, ps=ps: e.matmul(ps[:], self.onesR[:], sq[:], start=True, stop=True), [self.onesR_t, sq_t], [pst])
                                k.emit('act', lambda e, ps=ps: e.activation(out=rs[:], in_=ps[:], func=AF.Sqrt, bias=1e-6, scale=1.0), [pst], [rs_t])
                                k.emit('dve', lambda e: e.reciprocal(out=rs[:], in_=rs[:]), [rs_t], [rs_t])
                                sc_ = float(128.0 ** -0.5) if ti == 0 else 1.0
                                k.emit('dve', lambda e, dst=dst, sl=sl, sc_=sc_: e.scalar_tensor_tensor(
                                    out=dst[:, sl], in0=dst[:, sl], scalar=sc_, in1=rs[:], op0=ALU.mult, op1=ALU.mult), [dst_t, rs_t], [dst_t])
                    if stop <= 2:
                        continue
                    for di in range(2):
                        col = di * 8 + hh
                        for tt in range(2):
                            sl = slice(tt * 512, (tt + 1) * 512)
                            ps, pst = self.ps[6 + tt], self.ps_t[6 + tt]
                            k.emit('pe', lambda e, ps=ps, sl=sl, di=di: e.matmul(ps[:], SEL(di, 128), HR4[0:4, sl], start=True, stop=True),
                                   [HR4_t, self.CT_t], [pst])
                            k.emit('act', lambda e, ps=ps, sl=sl: e.activation(out=Qt[:, sl], in_=ps[:], func=AF.Exp), [pst], [Qt_t])
                        pos = 63 if di == 0 else 0
                        k.emit('pool', lambda e, pos=pos: e.tensor_copy(out=DKg[:], in_=Qt[:].rearrange("p (j n) -> p j n", n=64)[:, :, pos]), [Qt_t], [DKg_t])
                        k.emit('dve', lambda e: e.tensor_tensor(out=Qt[:], in0=Qt[:], in1=qn[:], op=ALU.mult), [Qt_t, qn_t], [Qt_t])
                        border = list(range(4)) if di == 0 else list(range(3, -1, -1))
                        if half == 1:
                            k.dma(Sb[0][:], d['st_gdn'][di, hh, :, :], writes=[Sb_t[0]])
                        si = 0
                        for bi in border:
                            chs = [bi * 4 + q for q in range(4)]
                            if di == 1:
                                chs = chs[::-1]
                            T0 = bi * 256
                            bsl = slice(T0, T0 + 256)
                            gcol = GC[:, bi * 4:bi * 4 + 4, col:col + 1].broadcast_to([64, 4, 64])
                            nbcol = NBE[:, bi * 4:bi * 4 + 4, col:col + 1].broadcast_to([64, 4, 64])
                            b0, b0t = self.ps[0], self.ps_t[0]
                            b1, b1t = self.ps[1], self.ps_t[1]
                            b2, b2t = self.ps[2], self.ps_t[2]
                            b3, b3t = self.ps[3], self.ps_t[3]
                            R = lambda ps: ps[0:64, 0:256]
                            R3 = lambda ps: ps[0:64, 0:256].rearrange("p (a n) -> p a n", a=4)
                            F2 = lambda t: t[:].rearrange("p a n -> p (a n)")
                            deps_c = [HR4_t, self.CT_t]
                            k.emit('pe', lambda e, di=di: e.matmul(R(b0), SEL(di, 64), HR4[0:4, bsl], start=True, stop=True), deps_c, [b0t])
                            k.emit('pe', lambda e, di=di: e.matmul(R(b2), SEL(2 + di, 64), HR4[0:4, bsl], start=True, stop=True), deps_c, [b2t])
                            DT, DT_t = bt[0], bt_t[0]
                            Dl, Dl_t = bt[1], bt_t[1]
                            DBT, DBT_t = bt[2], bt_t[2]
                            k.emit('dve', lambda e: e.tensor_tensor(out=DT[:], in0=R3(b0), in1=gcol, op=ALU.subtract), [b0t, TB_t], [DT_t])
                            k.emit('pool', lambda e, di=di: e.tensor_tensor(out=F2(Dl), in0=NEGL[di], in1=F2(DT), op=ALU.subtract), [DT_t, self.CT_t], [Dl_t])
                            k.emit('pool', lambda e, di=di: e.tensor_tensor(out=F2(DT), in0=F2(DT), in1=NEGU[di], op=ALU.add), [DT_t, self.CT_t], [DT_t])
                            k.emit('act', lambda e: e.activation(out=DT[:], in_=DT[:], func=AF.Exp), [DT_t], [DT_t])
                            k.emit('act', lambda e: e.activation(out=Dl[:], in_=Dl[:], func=AF.Exp), [Dl_t], [Dl_t])
                            k.emit('dve', lambda e, di=di: e.tensor_tensor(out=F2(DBT), in0=R(b2), in1=SNEG[di], op=ALU.mult), [b2t, self.CT_t], [DBT_t])
                            k.emit('pool', lambda e: e.tensor_tensor(out=DBT[:], in0=DBT[:], in1=DT[:], op=ALU.mult), [DBT_t, DT_t], [DBT_t])
                            k.emit('pool', lambda e: e.tensor_tensor(out=Dl[:], in0=Dl[:], in1=nbcol, op=ALU.mult), [Dl_t, TB_t], [Dl_t])
                            for q in range(4):
                                cs = slice(T0 + q * 64, T0 + (q + 1) * 64)
                                k.emit('pe', lambda e, q=q, cs=cs: e.matmul(b0[0:64, q * 64:(q + 1) * 64], kn[:, cs], kn[:, cs], start=True, stop=True), [kn_t], [b0t])
                                k.emit('pe', lambda e, q=q, cs=cs: e.matmul(b1[0:64, q * 64:(q + 1) * 64], kn[:, cs], qn[:, cs], start=True, stop=True), [kn_t, qn_t], [b1t])
                                k.emit('pe', lambda e, q=q, cs=cs: e.transpose(b2[0:64, q * 128:(q + 1) * 128], kn[:, cs], IDENT), [kn_t, self.CT_t], [b2t])
                                k.emit('pe', lambda e, q=q, cs=cs: e.transpose(b3[0:64, q * 128:(q + 1) * 128], vT[:, cs], IDENT), [vT_t, self.CT_t], [b3t])
                            NT, NT_t = bt[3], bt_t[3]
                            Nm, Nm_t = bt[4], bt_t[4]
                            QKT, QKT_t = bt[5], bt_t[5]
                            XT, XT_t = bt[6], bt_t[6]
                            k.emit('dve', lambda e: e.tensor_tensor(out=F2(NT), in0=R(b0), in1=F2(DBT), op=ALU.mult), [b0t, DBT_t], [NT_t])
                            k.emit('dve', lambda e: e.tensor_tensor(out=F2(Nm), in0=R(b0), in1=F2(Dl), op=ALU.mult), [b0t, Dl_t], [Nm_t])
                            k.emit('dve', lambda e: e.tensor_tensor(out=F2(QKT), in0=R(b1), in1=F2(DT), op=ALU.mult), [b1t, DT_t], [QKT_t])
                            k.emit('pool', lambda e: e.tensor_tensor(out=F2(XT), in0=F2(NT), in1=IDREP, op=ALU.add), [NT_t, self.CT_t], [XT_t])
                            k.emit('act', lambda e: e.activation(out=F2(ktok), in_=b2[0:64, :], func=AF.Copy), [b2t], [ktok_t])
                            k.emit('act', lambda e: e.activation(out=F2(vtok), in_=b3[0:64, :], func=AF.Copy), [b3t], [vtok_t])
                            P, P_t, PT_, PT_t = Nm, Nm_t, NT, NT_t
                            pp = [(bt[7], bt_t[7], bt[8], bt_t[8]), (bt[9], bt_t[9], bt[1], bt_t[1])]
                            XTs = [(bt[6], bt_t[6]), (bt[0], bt_t[0])]
                            xi = 0
                            for m in range(1, 6):
                                nP, nP_t, nPT, nPT_t = pp[(m - 1) % 2] if m > 1 else pp[0]
                                if m >= 3:
                                    nP, nP_t, nPT, nPT_t = pp[(m - 1) % 2]
                                if m == 2:
                                    nP, nP_t, nPT, nPT_t = pp[1]
                                for q in range(4):
                                    k.emit('pe', lambda e, q=q, P=P, PT_=PT_: e.matmul(b0[0:64, q * 64:(q + 1) * 64], PT_[:, q, :], P[:, q, :], start=True, stop=True),
                                           [P_t, PT_t], [b0t])
                                    if m < 5:
                                        k.emit('pe', lambda e, q=q, P=P, PT_=PT_: e.matmul(b1[0:64, q * 64:(q + 1) * 64], P[:, q, :], PT_[:, q, :], start=True, stop=True),
                                               [P_t, PT_t], [b1t])
                                k.emit('act', lambda e, nP=nP: e.activation(out=F2(nP), in_=R(b0), func=AF.Copy), [b0t], [nP_t])
                                if m < 5:
                                    k.emit('dve', lambda e, nPT=nPT: e.tensor_copy(out=F2(nPT), in_=R(b1)), [b1t], [nPT_t])
                                cX, cX_t = XTs[xi]
                                nX, nX_t = XTs[1 - xi]
                                for q in range(4):
                                    k.emit('pe', lambda e, q=q, nP=nP, cX=cX: e.matmul(b2[0:64, q * 64:(q + 1) * 64], nP[:, q, :], cX[:, q, :], start=True, stop=True),
                                           [nP_t, cX_t], [b2t])
                                k.emit('dve', lambda e, nX=nX, cX=cX: e.tensor_tensor(out=F2(nX), in0=R(b2), in1=F2(cX), op=ALU.add), [b2t, cX_t], [nX_t])
                                xi = 1 - xi
                                P, P_t, PT_, PT_t = nP, nP_t, nPT, nPT_t
                            XTf, XTf_t = XTs[xi]
                            if stop <= 3:
                                continue
                            psO, psO_t = self.ps[6], self.ps_t[6]
                            for ch in chs:
                                q = ch - bi * 4
                                cs = slice(ch * 64, (ch + 1) * 64)
                                loc = ch % cps
                                first = (loc == 0) if di == 0 else (loc == cps - 1)
                                last = (loc == cps - 1) if di == 0 else (loc == 0)
                                seq = ch // cps
                                zero_init = first and half == 0
                                S_prev, S_prev_t = Sb[si], Sb_t[si]
                                tmpv, tmpv_t = sm[0], sm_t[0]
                                r_, r_t = sm[1], sm_t[1]
                                vn, vn_t = sm[2], sm_t[2]
                                vs, vs_t = sm[3], sm_t[3]
                                k.emit('act', lambda e, q=q, ch=ch: e.activation(out=tmpv[:], in_=vtok[:, q, :], func=AF.Copy, scale=BE[:, ch, col:col + 1]),
                                       [vtok_t, TB_t], [tmpv_t])
                                if zero_init:
                                    rr, rr_t = tmpv, tmpv_t
                                else:
                                    p4, p4t = self.ps[4], self.ps_t[4]
                                    k.emit('pe', lambda e, cs=cs, S_prev=S_prev: e.matmul(p4[0:64, 0:128], kn[:, cs], S_prev[:], start=True, stop=True),
                                           [kn_t, S_prev_t], [p4t])
                                    k.emit('dve', lambda e, ch=ch: e.scalar_tensor_tensor(out=r_[:], in0=p4[0:64, 0:128], scalar=C1[:, ch, col:col + 1], in1=tmpv[:],
                                                                                         op0=ALU.mult, op1=ALU.add), [p4t, TB_t, tmpv_t], [r_t])
                                    rr, rr_t = r_, r_t
                                lvl = self.cfg.get('seq_lvl', 9)
                                if lvl <= 1:
                                    continue
                                p5, p5t = self.ps[5], self.ps_t[5]
                                k.emit('pe', lambda e, q=q, rr=rr: e.matmul(p5[0:64, 0:128], XTf[:, q, :], rr[:], start=True, stop=True), [XTf_t, rr_t], [p5t])
                                if lvl <= 1.5:
                                    continue
                                k.emit('act', lambda e: e.activation(out=vn[:], in_=p5[0:64, 0:128], func=AF.Copy), [p5t], [vn_t])
                                if lvl <= 1.7:
                                    continue
                                k.emit('dve', lambda e, ch=ch: e.tensor_scalar(out=vs[:], in0=p5[0:64, 0:128], scalar1=WW[:, ch, col:col + 1], scalar2=None, op0=ALU.mult),
                                       [p5t, TB_t], [vs_t])
                                if lvl <= 2:
                                    continue
                                k.emit('pe', lambda e, q=q, zero_init=zero_init: e.matmul(psO[:, q * 64:(q + 1) * 64], vn[:], QKT[:, q, :], start=True, stop=zero_init),
                                       [vn_t, QKT_t], [psO_t])
                                if not zero_init:
                                    k.emit('pe', lambda e, q=q, cs=cs, S_prev=S_prev: e.matmul(psO[:, q * 64:(q + 1) * 64], S_prev[:], Qt[:, cs], start=False, stop=True),
                                           [S_prev_t, Qt_t], [psO_t])
                                if lvl <= 3:
                                    continue
                                p7, p7t = self.ps[7], self.ps_t[7]
                                k.emit('pe', lambda e, q=q: e.matmul(p7[:, 0:128], ktok[:, q, :], vs[:], start=True, stop=True), [ktok_t, vs_t], [p7t])
                                if last and half == 0:
                                    dstS, dstS_t = sout[:, seq, di, :], sout_t
                                else:
                                    si = 1 - si
                                    dstS, dstS_t = Sb[si][:], Sb_t[si]
                                if zero_init:
                                    k.emit('dve', lambda e, dstS=dstS: e.tensor_copy(out=dstS, in_=p7[:, 0:128]), [p7t], [dstS_t])
                                else:
                                    k.emit('dve', lambda e, dstS=dstS, S_prev=S_prev, ch=ch: e.scalar_tensor_tensor(
                                        out=dstS, in0=S_prev[:], scalar=DKg[:, ch:ch + 1], in1=p7[:, 0:128], op0=ALU.mult, op1=ALU.add),
                                        [p7t, S_prev_t, DKg_t], [dstS_t])
                            if di == 0:
                                k.emit('act', lambda e, bsl=bsl: e.activation(out=oT[:, bsl], in_=psO[:, 0:256], func=AF.Copy), [psO_t], [oT_t])
                            else:
                                k.emit('dve', lambda e, bsl=bsl: e.tensor_tensor(out=oT[:, bsl], in0=oT[:, bsl], in1=psO[:, 0:256], op=ALU.add), [psO_t, oT_t], [oT_t])
                    if stop <= 4:
                        continue
                    if half == 0:
                        k.dma(d['gdout'][:, :, hh, :, :].rearrange("s d k e -> k s d e"), sout[:], reads=[sout_t])
                    (wg,), wt3 = self.wpiece([w_in[0, :, 3 * D + hh * 128:3 * D + (hh + 1) * 128]])
                    for tt in range(2):
                        sl = slice(tt * 512, (tt + 1) * 512)
                        ps, pst = self.ps[6 + tt], self.ps_t[6 + tt]
                        for c in range(NCH):
                            k.emit('pe', lambda e, ps=ps, c=c, sl=sl: e.matmul(ps[:], wg[:, c, :], self.hT[:, c, sl], start=(c == 0), stop=(c == NCH - 1)),
                                   [wt3, self.hT_t[c][tt]], [pst])
                        k.emit('act', lambda e, ps=ps, sl=sl: e.activation(out=vT[:, sl], in_=ps[:], func=AF.Silu), [pst], [vT_t])
                    self.HV_t = GV_t
                    self.head_norm(ntm, oT[:], oT_t, mix[:, 0, :], mix_t[0], GV[:, 1:2], 128.0 * 1e-6, 5, extra_mul=vT[:], extra_t=vT_t)
                    self.out_proj(d['gdn_w_out'][0, hh * 128:(hh + 1) * 128, :], mix, mix_t, 1, half, [6, 7])
                k.barrier()

    def ffn(self, layer, half):
        k, d, nc = self.k, self.d, self.nc
        seqlen = 256 if half == 0 else 1024
        oc, _ = PC['ffn_conv']
        ob, _ = PC['ffn_conv_b']

        def cw(tap, fchunk):
            col = oc + (layer * 3 + tap) * 44 + fchunk
            return self.PT[:, col:col + 1]

        def cb(fchunk):
            col = ob + layer * 44 + fchunk
            return self.PT[:, col:col + 1]

        groups = [(0, 8), (8, 16), (16, 22)]
        specs = []
        pidx = {}
        for (g0, g1) in groups:
            for j in range(g0, g1):
                pidx[('u', j)] = len(specs)
                specs.append([d['ffn_w_up'][layer, :, j * 128:(j + 1) * 128], d['ffn_w_up'][layer, :, D_FF + j * 128:D_FF + (j + 1) * 128]])
            for dp in range(4):
                pidx[('d', g0, dp)] = len(specs)
                specs.append([d['ffn_w_down'][layer, g0 * 128:g1 * 128, dp * 256:(dp + 1) * 256]])
        pf = PF(self, specs)
        PAIRS = [(0, 1), (2, 3), (4, 5)]
        u = 0
        v = 0
        with ExitStack() as es:
            aT = self.tmp(es, 'aT', [128, 8, TOK], F32R)
            aT_t = trks('aT', 8)
            acc = [self.tmp(es, f'facc{i}', [128, 2, TOK], F32) for i in range(2)]
            acc_t = trks('facc', 2, 2, 2)
            for (g0, g1) in groups:
                for j in range(g0, g1):
                    slot = j - g0
                    (wv, wg), wt = pf.get(pidx[('u', j)])
                    ai = j % 2
                    ac = acc[ai]
                    banks = {}
                    for tt in range(2):
                        pair = PAIRS[u % 3]
                        u += 1
                        sl = slice(tt * 512, (tt + 1) * 512)
                        for vi, (w, fch) in enumerate(((wv, j), (wg, NFF + j))):
                            ps, pst = self.ps[pair[vi]], self.ps_t[pair[vi]]
                            banks[(vi, tt)] = (ps, pst)
                            for c in range(NCH):
                                k.emit('pe', lambda e, ps=ps, w=w, c=c, sl=sl: e.matmul(
                                    ps[:], w[:, c, :], self.hT[:, c, sl],
                                    start=(c == 0), stop=(c == NCH - 1)), [wt, self.hT_t[c][tt]], [pst])
                        for vi, fch in ((0, j), (1, NFF + j)):
                            ps, pst = banks[(vi, tt)]
                            at_ = acc_t[ai][vi][tt]
                            k.emit('act', lambda e, ps=ps, sl=sl, fch=fch, vi=vi: e.activation(
                                out=ac[:, vi, sl], in_=ps[:], func=AF.Identity, bias=cb(fch), scale=cw(1, fch)), [pst, self.PT_t], [at_])
                            sl_ = min(seqlen, 512)
                            ns = 512 // sl_
                            pv = ps[:].rearrange("p (s n) -> p s n", s=ns)
                            av = ac[:, vi, sl].rearrange("p (s n) -> p s n", s=ns)
                            k.emit('dve', lambda e, pv=pv, av=av, fch=fch, sl_=sl_: e.scalar_tensor_tensor(
                                out=av[:, :, 1:sl_], in0=pv[:, :, 0:sl_ - 1], scalar=cw(0, fch), in1=av[:, :, 1:sl_],
                                op0=ALU.mult, op1=ALU.add), [pst, self.PT_t, at_], [at_])
                            k.emit('dve', lambda e, pv=pv, av=av, fch=fch, sl_=sl_: e.scalar_tensor_tensor(
                                out=av[:, :, 0:sl_ - 1], in0=pv[:, :, 1:sl_], scalar=cw(2, fch), in1=av[:, :, 0:sl_ - 1],
                                op0=ALU.mult, op1=ALU.add), [pst, self.PT_t, at_], [at_])
                    if seqlen > 512:
                        for vi, fch in ((0, j), (1, NFF + j)):
                            p0, p0t = banks[(vi, 0)]
                            p1, p1t = banks[(vi, 1)]
                            k.emit('dve', lambda e, p0=p0, fch=fch, vi=vi: e.scalar_tensor_tensor(
                                out=ac[:, vi, 512:513], in0=p0[:, 511:512], scalar=cw(0, fch), in1=ac[:, vi, 512:513],
                                op0=ALU.mult, op1=ALU.add), [p0t, self.PT_t, acc_t[ai][vi][1]], [acc_t[ai][vi][1]])
                            k.emit('dve', lambda e, p1=p1, fch=fch, vi=vi: e.scalar_tensor_tensor(
                                out=ac[:, vi, 511:512], in0=p1[:, 0:1], scalar=cw(2, fch), in1=ac[:, vi, 511:512],
                                op0=ALU.mult, op1=ALU.add), [p1t, self.PT_t, acc_t[ai][vi][0]], [acc_t[ai][vi][0]])
                    k.emit('act', lambda e, ac=ac: e.activation(out=ac[:, 1, :], in_=ac[:, 1, :], func=AF.Silu), acc_t[ai][1], acc_t[ai][1])
                    k.emit('pool', lambda e, slot=slot, ac=ac: e.tensor_tensor(out=aT[:, slot, :], in0=ac[:, 0, :], in1=ac[:, 1, :], op=ALU.mult),
                           acc_t[ai], [aT_t[slot]])
                    if self.modgen is not None:
                        next(self.modgen, None)
                ng = g1 - g0
                for dp in range(4):
                    (wd,), wdt = pf.get(pidx[('d', g0, dp)])
                    for dmi in range(2):
                        dm = dp * 2 + dmi
                        for tt in range(2):
                            b = v % 6
                            v += 1
                            ps, pst = self.ps[b], self.ps_t[b]
                            for jj in range(ng):
                                k.emit('pe', lambda e, ps=ps, jj=jj, dmi=dmi, tt=tt: e.matmul(
                                    ps[:], wd[:, jj, dmi * 128:(dmi + 1) * 128], aT[:, jj, tt * 512:(tt + 1) * 512],
                                    start=(jj == 0), stop=(jj == ng - 1)), [wdt, aT_t[jj]], [pst])
                            xs = self.xT[half][:, dm, tt * 512:(tt + 1) * 512]
                            k.emit('dve', lambda e, ps=ps, xs=xs, dm=dm: e.scalar_tensor_tensor(
                                out=xs, in0=ps[:], scalar=self.gate(1, dm, half), in1=xs, op0=ALU.mult, op1=ALU.add),
                                [pst, self.MOD_t, self.xT_t[half][dm][tt]], [self.xT_t[half][dm][tt]])
            k.barrier()

    def final(self, cfg):
        k, d, nc = self.k, self.d, self.nc
        og, _ = PC['final_g']
        with ExitStack() as es:
            sq = self.tmp(es, 'fsq', [128, NCH, 512], F32R)
            sq_t = Trk('fsq')
            rstd = self.tmp(es, 'frstd', [128, 512], F32)
            rstd_t = Trk('frstd')
            yo = self.tmp(es, 'fyo', [128, NCH, 512], F32)
            yo_t = self.ptrk('fyo', NCH)
            for half in range(2):
                for tt in range(2):
                    xs = self.xT[half][:, :, tt * 512:(tt + 1) * 512]
                    xs_t = [self.xT_t[half][c][tt] for c in range(NCH)]
                    k.emit('act', lambda e: e.activation(out=sq[:], in_=xs, func=AF.Square), xs_t, [sq_t])
                    ps, pst = self.ps[6], self.ps_t[6]
                    for c in range(NCH):
                        k.emit('pe', lambda e, c=c: e.matmul(ps[:], self.onesR[:], sq[:, c, :], start=(c == 0), stop=(c == NCH - 1)),
                               [self.onesR_t, sq_t], [pst])
                    k.emit('act', lambda e: e.activation(out=rstd[:], in_=ps[:], func=AF.Sqrt, bias=float(D * EPS), scale=1.0),
                           [pst], [rstd_t])
                    k.emit('dve', lambda e: e.reciprocal(out=rstd[:], in_=rstd[:]), [rstd_t], [rstd_t])
                    for c in range(NCH):
                        g = self.PT[:, og + c:og + c + 1]
                        k.emit('dve', lambda e, c=c, g=g: e.scalar_tensor_tensor(
                            out=yo[:, c, :], in0=self.xT[half][:, c, tt * 512:(tt + 1) * 512], scalar=g, in1=rstd[:],
                            op0=ALU.mult, op1=ALU.mult), [self.xT_t[half][c][tt], self.PT_t, rstd_t], [yo_t[c]])
                        k.emit('act', lambda e, c=c: e.activation(out=yo[:, c, :], in_=yo[:, c, :], func=AF.Copy, scale=32.0),
                               [yo_t[c]], [yo_t[c]])
                        k.dma(d['yout'][half, c, :, tt * 512:(tt + 1) * 512], yo[:, c, :], reads=[yo_t[c]])


_CACHE = {}


def _get_prog(cfg):
    key = repr(sorted(cfg.items()))
    if key not in _CACHE:
        p = Prog(dict(cfg))
        with p.es:
            p.declare()
            p.build()
        _CACHE[key] = p
    return _CACHE[key]


def _pack(plan, arrays, n):
    out = np.zeros((128, n), np.float32)
    for c0, key in plan:
        off = c0
        for (name, offset, rstride, rows, ncols) in key:
            flat = arrays[name].reshape(-1)
            a2 = np.lib.stride_tricks.as_strided(flat[offset:], shape=(rows, ncols), strides=(rstride * 4, 4))
            kc = rows // 128
            assert off + kc * ncols <= n
            out[:, off:off + kc * ncols] = a2.reshape(kc, 128, ncols).transpose(1, 0, 2).reshape(128, kc * ncols)
            off += kc * ncols
    return out


def _run(inp, cfg):
    p = _get_prog(cfg)
    consts, rope_tab = _build_consts()
    f32 = lambda a: np.ascontiguousarray(np.asarray(a, np.float32))
    warr = {n: f32(inp[n]) for n in ('w_mod', 'ffn_w_up', 'ffn_w_down', 'attn_w_in', 'attn_w_out',
                                     'hgrn_w_in', 'hgrn_w_out', 'gdn_w_in', 'gdn_w_out')}
    shared = {'wpk': _pack(p.wplan['wpk'], warr, WCOLS)}
    xp = f32(inp['x_prompt'])
    xs = f32(inp['x_sample'])
    ck = f32(inp['cache_attn_k'])
    cv = f32(inp['cache_attn_v'])
    in_maps = []
    for core in range(N_CORES):
        m = dict(shared)
        a = xp[4 * core:4 * core + 4].reshape(TOK, D).T.reshape(8, 128, TOK)
        b = xs[core].T.reshape(8, 128, TOK)
        m['xin'] = np.ascontiguousarray(np.stack([a, b], axis=0))
        m['params'] = _build_params(core, inp)
        m['consts'] = consts
        m['rope'] = rope_tab
        m['lamtab'] = np.ascontiguousarray(np.broadcast_to(np.asarray(inp['attn_lambda'], np.float32).reshape(1, 512), (128, 512)))
        carr = {'ck': np.ascontiguousarray(ck[core].transpose(0, 2, 3, 1)),
                'cv': np.ascontiguousarray(cv[core].reshape(2, 512, D))}
        m['cpk'] = _pack(p.wplan['cpk'], carr, CCOLS)
        m['st_hgrn'] = f32(inp['state_hgrn'][core, 0])
        m['st_gdn'] = f32(inp['state_gdn'][core, 0])
        in_maps.append(m)
    ncores = cfg.get('ncores', N_CORES)
    res = run_bass_kernel_spmd(p.nc, in_maps[:ncores], core_ids=list(range(ncores)))
    R = list(res.results) + [res.results[0]] * (N_CORES - ncores)
    y_prompt = np.empty((32, 256, D), np.float32)
    y_sample = np.empty((8, 1024, D), np.float32)
    new_k = np.empty((32, 2, 256, 8, 128), np.float32)
    new_v = np.empty((32, 2, 256, 8, 128), np.float32)
    new_h = np.empty((32, 1, 2, 8, 128, 128), np.float32)
    new_g = np.empty((32, 1, 2, 8, 128, 128), np.float32)
    for core in range(N_CORES):
        r = R[core]
        yo = r['yout']
        y_prompt[4 * core:4 * core + 4] = yo[0].reshape(D, TOK).T.reshape(4, 256, D)
        y_sample[core] = yo[1].reshape(D, TOK).T
        ko = r['kout']
        new_k[4 * core:4 * core + 4] = ko.reshape(2, 8, 128, 4, 256).transpose(3, 0, 4, 1, 2)
        vo = r['vout']
        new_v[4 * core:4 * core + 4] = vo.reshape(2, 4, 256, 8, 128).transpose(1, 0, 2, 3, 4)
        new_h[4 * core:4 * core + 4, 0] = r['hgout']
        new_g[4 * core:4 * core + 4, 0] = r['gdout']
    return (y_prompt, y_sample, new_k, new_v, new_h, new_g)


def kernel(**inputs):
    return _run(inputs, {})
```

```python
import math
from contextlib import ExitStack

import numpy as np
import concourse.bass as bass
import concourse.mybir as mybir
from concourse.bass_utils import run_bass_kernel_spmd

F32 = mybir.dt.float32
F32R = mybir.dt.float32r
AF = mybir.ActivationFunctionType
ALU = mybir.AluOpType
AX = mybir.AxisListType

D = 1024
NCH = 8
TOK = 1024
DEPTH = 4
D_FF = 2816
NFF = 22
EPS = 1e-6
N_CORES = 8
WCOLS = (4 * 1024 * 6144 + 4 * 1024 * 5632 + 4 * 2816 * 1024 + 2 * 1024 * 3072 + 2 * 1024 * 1024 + 1024 * 5120 + 1024 * 1024
         + 1024 * 4128 + 1024 * 1024) // 128
CCOLS = (2 * 8 * 128 * 512 + 2 * 512 * 1024) // 128


class Cols:
    def __init__(self):
        self.off = {}
        self.n = 0

    def add(self, name, w):
        self.off[name] = (self.n, w)
        self.n += w

    def __getitem__(self, name):
        return self.off[name]


def _param_cols():
    c = Cols()
    c.add('cond', 16)
    c.add('norm_g', 64)
    c.add('b_mod', 192)
    c.add('final_g', 8)
    c.add('subln', 2)
    c.add('hgrn_lb', 32)
    c.add('hgrn_norm', 1)
    c.add('gdn_norm', 1)
    c.add('gdn_conv', 72)
    c.add('ffn_conv', 528)
    c.add('ffn_conv_b', 176)
    c.add('gdn_alog_col', 1)
    c.add('gdn_dt_col', 1)
    c.add('gdn_alog_row', 16)
    c.add('gdn_dt_row', 16)
    return c


PC = _param_cols()


def _fm(v):
    v = np.asarray(v, np.float32)
    r = v.reshape(-1, 128)
    return np.ascontiguousarray(r.T)


def _build_params(core, inp):
    P = np.zeros((128, PC.n), np.float32)

    def put(name, arr):
        o, w = PC[name]
        assert arr.shape == (128, w), (name, arr.shape, w)
        P[:, o:o + w] = arr

    cond = np.stack([inp['c_ctx'], inp['c'][core]], axis=0)
    put('cond', np.ascontiguousarray(cond.reshape(2, 8, 128).transpose(2, 1, 0)).reshape(128, 16))
    put('norm_g', _fm(inp['norm_g']))
    put('b_mod', _fm(inp['b_mod']))
    put('final_g', _fm(inp['final_g']))
    put('subln', _fm(inp['attn_subln']))
    put('hgrn_lb', _fm(inp['hgrn_lb']))
    put('hgrn_norm', _fm(inp['hgrn_norm']))
    put('gdn_norm', _fm(inp['gdn_norm']))
    put('gdn_conv', _fm(inp['gdn_conv']))
    put('ffn_conv', _fm(inp['ffn_conv']))
    put('ffn_conv_b', _fm(inp['ffn_conv_b']))
    al = np.zeros((128, 1), np.float32)
    al[:16, 0] = np.asarray(inp['gdn_a_log'], np.float32).reshape(16)
    put('gdn_alog_col', al)
    dtb = np.zeros((128, 1), np.float32)
    dtb[:16, 0] = np.asarray(inp['gdn_dt_bias'], np.float32).reshape(16)
    put('gdn_dt_col', dtb)
    put('gdn_alog_row', np.broadcast_to(np.asarray(inp['gdn_a_log'], np.float32).reshape(1, 16), (128, 16)))
    put('gdn_dt_row', np.broadcast_to(np.asarray(inp['gdn_dt_bias'], np.float32).reshape(1, 16), (128, 16)))
    return P


def _const_cols():
    c = Cols()
    c.add('ident', 128)
    c.add('ones', 128)
    c.add('perm', 128)
    c.add('mask_f', 256)
    c.add('mask_b', 256)
    c.add('negu_f', 256)
    c.add('negl_f', 256)
    c.add('negu_b', 256)
    c.add('negl_b', 256)
    c.add('sneg_f', 256)
    c.add('sneg_b', 256)
    c.add('idrep', 256)
    c.add('sel', 512)
    c.add('nsel', 128)
    return c


CC = _const_cols()


def _build_consts():
    C = np.zeros((128, CC.n), np.float32)
    o, w = CC['ident']
    C[:, o:o + w] = np.eye(128, dtype=np.float32)
    o, w = CC['ones']
    C[:, o:o + w] = 1.0
    o, w = CC['perm']
    tok = np.arange(1024)
    row = (tok // 64).astype(np.float32)
    col = (tok % 64).astype(np.float32)
    inv = (np.float32(10000.0) ** (-np.arange(16, dtype=np.float32) / np.float32(16))).astype(np.float32)
    ROPE = np.zeros((128, 2048), np.float32)
    oc_, os_ = 0, 1024
    for p in range(128):
        dd = p % 64
        i = dd % 32
        partner = p + 16 if i < 16 else p - 16
        C[partner, o + p] = 1.0
        f = i % 16
        pos = row if dd < 32 else col
        ang = (pos * inv[f]).astype(np.float32)
        ROPE[p, oc_:oc_ + 1024] = np.cos(ang)
        ROPE[p, os_:os_ + 1024] = np.sin(ang) * (-1.0 if i < 16 else 1.0)
    sidx = np.arange(64)[:, None]
    tidx = np.arange(64)[None, :]
    om, _ = CC['mask_f']
    C[:64, om:om + 256] = np.tile((sidx <= tidx).astype(np.float32), (1, 4))
    om, _ = CC['mask_b']
    C[:64, om:om + 256] = np.tile((sidx >= tidx).astype(np.float32), (1, 4))
    BIG = 30000.0
    p_, j_ = sidx, tidx

    def putm(name, m):
        o_, _ = CC[name]
        C[:64, o_:o_ + 256] = np.tile(m.astype(np.float32), (1, 4))

    putm('negu_f', np.where(j_ >= p_, 0.0, -BIG))
    putm('negl_f', np.where(j_ < p_, 0.0, -BIG))
    putm('negu_b', np.where(j_ <= p_, 0.0, -BIG))
    putm('negl_b', np.where(j_ > p_, 0.0, -BIG))
    putm('sneg_f', np.where(j_ > p_, -1.0, 0.0))
    putm('sneg_b', np.where(j_ < p_, -1.0, 0.0))
    putm('idrep', (j_ == p_))
    o_, _ = CC['sel']
    for kk in range(4):
        C[kk, o_ + kk * 128:o_ + (kk + 1) * 128] = 1.0
    o_, _ = CC['nsel']
    for kk in range(2):
        C[kk, o_ + kk * 64:o_ + (kk + 1) * 64] = -1.0
    return C, ROPE


class _GT:
    def __init__(self, name):
        self.name = name


class Geo:
    def __init__(self, name, shape, offset=0, pat=None):
        self.tensor = _GT(name)
        if pat is None:
            pat = []
            st = 1
            for n in reversed(shape):
                pat.insert(0, (st, n))
                st *= n
        self.ap = tuple(pat)
        self.offset = offset

    @property
    def shape(self):
        return tuple(n for _, n in self.ap)

    def __getitem__(self, key):
        if not isinstance(key, tuple):
            key = (key,)
        key = key + (slice(None),) * (len(self.ap) - len(key))
        off = self.offset
        pat = []
        for (st, n), kk in zip(self.ap, key):
            if isinstance(kk, int):
                off += st * kk
            else:
                a, b, _ = kk.indices(n)
                off += st * a
                pat.append((st, b - a))
        return Geo(self.tensor.name, None, off, pat)


class Trk:
    __slots__ = ('name', 'w', 'rs', 'sem', 'cnt', 'psum')

    def __init__(self, name):
        self.name = name
        self.psum = False
        self.w = None
        self.rs = {}
        self.sem = None
        self.cnt = 0


def trks(name, *dims):
    if len(dims) == 1:
        return [Trk(f'{name}{i}') for i in range(dims[0])]
    return [trks(f'{name}{i}_', *dims[1:]) for i in range(dims[0])]


def flat(x):
    if isinstance(x, Trk):
        return [x]
    out = []
    for e in x:
        out.extend(flat(e))
    return out


class KB:
    def __init__(self, nc, es):
        self.nc = nc
        self.es = es
        self.E = {'pe': nc.tensor, 'act': nc.scalar, 'dve': nc.vector, 'pool': nc.gpsimd, 'sp': nc.sync}
        self.sem = {e: es.enter_context(nc.semaphore(f's_{e}')) for e in self.E}
        self.cnt = {e: 0 for e in self.E}
        self.waited = {e: {} for e in self.E}
        self.dma_sems = []
        self.n_ins = 0
        self.n_wait = 0

    def _wait(self, eng, deps):
        need = {}
        for key, val in deps:
            if key == 'pe' and eng == 'pe':
                continue
            if need.get(key, 0) < val:
                need[key] = val
        wt = self.waited[eng]
        for key, val in need.items():
            if wt.get(key, 0) >= val:
                continue
            sem = self.sem[key] if isinstance(key, str) else key
            self.E[eng].wait_ge(sem, val)
            self.n_wait += 1
            wt[key] = val

    def _deps(self, reads, writes):
        deps = []
        for t in reads:
            if t.w is not None:
                deps.append(t.w)
            if t.psum:
                deps.extend(t.rs.items())
        for t in writes:
            if t.w is not None:
                deps.append(t.w)
            deps.extend(t.rs.items())
        return deps

    def _mark(self, tok, reads, writes):
        k, v = tok
        for t in reads:
            if t.rs.get(k, 0) < v:
                t.rs[k] = v
        for t in writes:
            t.w = tok
            t.rs = {}

    def emit(self, eng, fn, reads=(), writes=()):
        reads = flat(reads)
        writes = flat(writes)
        self._wait(eng, self._deps(reads, writes))
        ins = fn(self.E[eng])
        self.cnt[eng] += 1
        ins.then_inc(self.sem[eng], 1)
        self.n_ins += 1
        tok = (eng, self.cnt[eng])
        self._mark(tok, reads, writes)
        return tok

    def dma(self, out, in_, reads=(), writes=(), q='sp'):
        reads = flat(reads)
        writes = flat(writes)
        self._wait(q, self._deps(reads, writes))
        owner = (writes + reads)[0]
        if owner.sem is None:
            owner.sem = self.es.enter_context(self.nc.semaphore(f'd_{owner.name}'))
            self.dma_sems.append(owner)
        self.E[q].dma_start(out=out, in_=in_).then_inc(owner.sem, 16)
        owner.cnt += 16
        self.n_ins += 1
        tok = (owner.sem, owner.cnt)
        self._mark(tok, reads, writes)
        return tok

    def dma_group(self, pairs, writes, q='sp'):
        writes = flat(writes)
        self._wait(q, self._deps([], writes))
        owner = writes[0]
        if owner.sem is None:
            owner.sem = self.es.enter_context(self.nc.semaphore(f'd_{owner.name}'))
            self.dma_sems.append(owner)
        for out, in_ in pairs:
            self.E[q].dma_start(out=out, in_=in_).then_inc(owner.sem, 16)
            owner.cnt += 16
            self.n_ins += 1
        tok = (owner.sem, owner.cnt)
        self._mark(tok, [], writes)
        return tok

    def barrier(self):
        for e in self.E:
            deps = [(o, self.cnt[o]) for o in self.E if o != e and self.cnt[o] > 0]
            deps += [(t.sem, t.cnt) for t in self.dma_sems]
            wt = self.waited[e]
            for key, val in deps:
                if wt.get(key, 0) >= val:
                    continue
                sem = self.sem[key] if isinstance(key, str) else key
                self.E[e].wait_ge(sem, val)
                self.n_wait += 1
                wt[key] = val

    def finish(self):
        deps = [(t.sem, t.cnt) for t in self.dma_sems]
        deps += [(o, self.cnt[o]) for o in self.E if o != 'sp' and self.cnt[o] > 0]
        self._wait('sp', deps)

    def sb(self, name, shape, dt=F32):
        return self.es.enter_context(self.nc.sbuf_tensor(name, list(shape), dt))


class PF:
    def __init__(self, prog, specs):
        self.p, self.specs, self.h = prog, specs, {}

    def get(self, i):
        for t in (i, i + 1):
            if t < len(self.specs) and t not in self.h:
                self.h[t] = self.p.wpiece(self.specs[t])
        return self.h.pop(i)


class Prog:
    def __init__(self, cfg):
        self.cfg = cfg
        nc = bass.Bass("TRN2", target_bir_lowering=False)
        self.nc = nc
        self.es = ExitStack()
        self.k = KB(nc, self.es)
        self.rr = 0
        self.wplan = {'wpk': [], 'cpk': []}
        self.wcols = {'wpk': 0, 'cpk': 0}
        self.wkeys = {}

    def declare(self):
        nc = self.nc

        def din(name, shape):
            return nc.dram_tensor(name, list(shape), F32, kind="ExternalInput").ap()

        def dout(name, shape):
            return nc.dram_tensor(name, list(shape), F32, kind="ExternalOutput").ap()

        d = {}
        d['xin'] = din('xin', [2, 8, 128, TOK])
        d['params'] = din('params', [128, PC.n])
        d['consts'] = din('consts', [128, CC.n])
        d['rope'] = din('rope', [128, 2048])
        d['wpk'] = din('wpk', [128, WCOLS])
        d['cpk'] = din('cpk', [128, CCOLS])
        d['lamtab'] = din('lamtab', [128, 512])
        d['gscr'] = nc.dram_tensor('gscr', [48, TOK], F32, kind="Internal").ap()
        d['w_mod'] = Geo('w_mod', [4, D, 6 * D])
        d['ffn_w_up'] = Geo('ffn_w_up', [4, D, 2 * D_FF])
        d['ffn_w_down'] = Geo('ffn_w_down', [4, D_FF, D])
        d['attn_w_in'] = Geo('attn_w_in', [2, D, 3 * D])
        d['attn_w_out'] = Geo('attn_w_out', [2, D, D])
        d['hgrn_w_in'] = Geo('hgrn_w_in', [1, D, 5 * D])
        d['hgrn_w_out'] = Geo('hgrn_w_out', [1, D, D])
        d['gdn_w_in'] = Geo('gdn_w_in', [1, D, 4 * D + 32])
        d['gdn_w_out'] = Geo('gdn_w_out', [1, D, D])
        d['ck'] = Geo('ck', [2, 8, 128, 512])
        d['cv'] = Geo('cv', [2, 512, D])
        d['st_hgrn'] = din('st_hgrn', [2, 8, 128, 128])
        d['st_gdn'] = din('st_gdn', [2, 8, 128, 128])
        d['yout'] = dout('yout', [2, 8, 128, TOK])
        d['kout'] = dout('kout', [2, 8, 128, TOK])
        d['vout'] = dout('vout', [2, TOK, D])
        d['hgout'] = dout('hgout', [4, 2, 8, 128, 128])
        d['gdout'] = dout('gdout', [4, 2, 8, 128, 128])
        self.d = d

    def ptrk(self, name, n=None):
        if not hasattr(self, '_pt'):
            self._pt = {}
        if name not in self._pt:
            self._pt[name] = Trk(name) if n is None else trks(name, n)
        return self._pt[name]

    def tmp(self, es, name, shape, dt=F32):
        self._uid = getattr(self, '_uid', 0) + 1
        return es.enter_context(self.nc.sbuf_tensor(f'{name}_{self._uid}', list(shape), dt))

    def evac_eng(self):
        self.rr += 1
        return 'act' if self.rr % 2 else 'dve'

    def copy(self, eng, out, in_, reads, writes):
        if eng == 'act':
            return self.k.emit('act', lambda e: e.activation(out=out, in_=in_, func=AF.Copy), reads, writes)
        return self.k.emit(eng, lambda e: e.tensor_copy(out=out, in_=in_), reads, writes)

    def init_wpool(self):
        k = self.k
        self.ws = [k.sb(f'ws{i}', [128, 2048], F32) for i in range(2)]
        self.ws_t = trks('ws', 2)
        self.wr = [k.sb(f'wr{i}', [128, 2048], F32R) for i in range(2)]
        self.wr_t = trks('wr', 2)
        self.ws_i = 0
        self.wr_i = 0

    def wpiece(self, segs, rounded=True, dest=None):
        k = self.k
        si = self.ws_i
        self.ws_i = (si + 1) % 2
        st, stt = self.ws[si], self.ws_t[si]
        off = 0
        views = []
        key = []
        percore = False
        for ap in segs:
            rows, ncols = ap.shape
            kc = rows // 128
            pat = tuple((int(a), int(b)) for a, b in ap.ap)
            assert len(pat) == 2 and pat[1][0] == 1, pat
            name = ap.tensor.name
            percore = percore or name in ('ck', 'cv')
            key.append((name, int(ap.offset), pat[0][0], rows, ncols))
            views.append((off, kc, ncols))
            off += kc * ncols
        key = tuple(key)
        pk = 'cpk' if percore else 'wpk'
        if key not in self.wkeys:
            self.wkeys[key] = (pk, self.wcols[pk])
            self.wplan[pk].append((self.wcols[pk], key))
            self.wcols[pk] += off
        pk, c0 = self.wkeys[key]
        k.dma(st[:, 0:off], self.d[pk][:, c0:c0 + off], writes=[stt])
        if not rounded:
            outs = [st[:, o:o + kc * n].rearrange("p (c n) -> p c n", c=kc) for (o, kc, n) in views]
            return outs, stt
        if dest is not None:
            rt, rtt = dest
        else:
            ri = self.wr_i
            self.wr_i = (ri + 1) % 2
            rt, rtt = self.wr[ri], self.wr_t[ri]
        k.emit('act', lambda e: e.activation(out=rt[:, 0:off], in_=st[:, 0:off], func=AF.Copy), [stt], [rtt])
        outs = [rt[:, o:o + kc * n].rearrange("p (c n) -> p c n", c=kc) for (o, kc, n) in views]
        return outs, rtt

    def build(self):
        nc, k, d, cfg = self.nc, self.k, self.d, self.cfg
        self.ps = [self.es.enter_context(nc.psum_tensor(f'ps{i}', [128, 512], F32)) for i in range(8)]
        self.ps_t = trks('ps', 8)
        for t in self.ps_t:
            t.psum = True
        self.PT = k.sb('PT', [128, PC.n])
        self.PT_t = Trk('PT')
        self.CT = k.sb('CT', [128, CC.n])
        self.CT_t = Trk('CT')
        self.onesR = k.sb('onesR', [128, 128], F32R)
        self.onesR_t = Trk('onesR')
        self.identR = k.sb('identR', [128, 128], F32R)
        self.identR_t = Trk('identR')
        self.xT = [k.sb(f'xT{h}', [128, NCH, TOK]) for h in range(2)]
        self.xT_t = trks('xT', 2, NCH, 2)
        self.hT = k.sb('hT', [128, NCH, TOK], F32R)
        self.hT_t = trks('hT', NCH, 2)
        self.permR = k.sb('permR', [128, 128], F32R)
        self.permR_t = Trk('permR')
        self.AV = k.sb('AV', [128, 8])
        self.AV_t = Trk('AV')
        self.MODs = [k.sb(f'MOD{i}', [128, 48, 2]) for i in range(2)]
        self.MODs_t = trks('MOD', 2)
        self.ABs = [k.sb(f'AB{i}', [128, 2, 2, 8, 2]) for i in range(2)]
        self.ABs_t = trks('AB', 2)
        self.modgen = None
        self.sc = k.sb('sc', [128, 16])
        self.sc_t = Trk('sc')
        self.init_wpool()

        k.dma(self.PT[:], d['params'][:, :], writes=[self.PT_t])
        k.dma(self.CT[:], d['consts'][:, :], writes=[self.CT_t])
        for h in range(2):
            k.dma_group([(self.xT[h][:, c, :], d['xin'][h, c, :, :]) for c in range(NCH)], self.xT_t[h])
        o, w = CC['ones']
        k.emit('dve', lambda e: e.tensor_copy(out=self.onesR[:], in_=self.CT[:, o:o + w]), [self.CT_t], [self.onesR_t])
        o2, w2 = CC['ident']
        k.emit('dve', lambda e: e.tensor_copy(out=self.identR[:], in_=self.CT[:, o2:o2 + w2]), [self.CT_t], [self.identR_t])
        o3, w3 = CC['perm']
        k.emit('dve', lambda e: e.tensor_copy(out=self.permR[:], in_=self.CT[:, o3:o3 + w3]), [self.CT_t], [self.permR_t])
        oc, wc = PC['cond']
        k.emit('act', lambda e: e.activation(out=self.sc[:], in_=self.PT[:, oc:oc + wc], func=AF.Silu), [self.PT_t], [self.sc_t])

        nl = cfg.get('layers', DEPTH)
        for _ in self.modulation(0):
            pass
        for layer in range(nl):
            p = layer % 2
            self.MOD, self.MOD_t, self.AB, self.AB_t = self.MODs[p], self.MODs_t[p], self.ABs[p], self.ABs_t[p]
            for half in range(2):
                if cfg.get('mixers', True):
                    self.rmsnorm_mod(layer, 0, half)
                    self.mixer(layer, half)
                if half == 1 and layer + 1 < nl:
                    self.modgen = self.modulation(layer + 1)
                if cfg.get('ffn', True):
                    self.rmsnorm_mod(layer, 1, half)
                    self.ffn(layer, half)
                if self.modgen is not None:
                    for _ in self.modgen:
                        pass
                    self.modgen = None
        self.final(cfg)
        k.finish()

    def modulation(self, layer):
        k, d = self.k, self.d
        p = layer % 2
        MOD, MOD_t, AB, AB_t = self.MODs[p], self.MODs_t[p], self.ABs[p], self.ABs_t[p]
        ps, pst = self.ps[7], self.ps_t[7]
        scv = self.sc[:].rearrange("p (c j) -> p c j", j=2)
        for piece in range(24):
            (w,), wt = self.wpiece([d['w_mod'][layer, :, piece * 256:(piece + 1) * 256]], rounded=False)
            for q2 in range(2):
                q = piece * 2 + q2
                for c in range(NCH):
                    k.emit('pe', lambda e, c=c, q2=q2, q=q: e.matmul(
                        ps[:, q * 2:q * 2 + 2], w[:, c, q2 * 128:(q2 + 1) * 128], scv[:, c, :],
                        start=(c == 0), stop=(c == NCH - 1)), [wt, self.sc_t], [pst])
            yield piece
        ob, wb = PC['b_mod']
        bm = self.PT[:, ob + layer * 48: ob + layer * 48 + 48]
        k.emit('dve', lambda e: e.tensor_tensor(
            out=MOD[:], in0=ps[:, 0:96].rearrange("p (q j) -> p q j", j=2),
            in1=bm.unsqueeze(2).broadcast_to([128, 48, 2]), op=ALU.add), [pst, self.PT_t], [MOD_t])
        og, wg = PC['norm_g']
        for s in range(2):
            g = self.PT[:, og + (layer * 2 + s) * 8: og + (layer * 2 + s) * 8 + 8]
            sh = MOD[:, s * 24 + 0: s * 24 + 8, :]
            scl = MOD[:, s * 24 + 8: s * 24 + 16, :]
            A = AB[:, s, 0, :, :]
            B = AB[:, s, 1, :, :]
            k.emit('dve', lambda e, scl=scl, A=A: e.tensor_scalar(
                out=A, in0=scl, scalar1=1.0, scalar2=32.0, op0=ALU.add, op1=ALU.mult), [MOD_t], [AB_t])
            k.emit('dve', lambda e, A=A, g=g: e.tensor_tensor(
                out=A, in0=A, in1=g.unsqueeze(2).broadcast_to([128, 8, 2]), op=ALU.mult), [AB_t, self.PT_t], [AB_t])
            k.emit('dve', lambda e, B=B, sh=sh: e.tensor_copy(out=B, in_=sh), [MOD_t], [AB_t])

    def gate(self, s, c, half):
        return self.MOD[:, s * 24 + 16 + c, half:half + 1]

    def rmsnorm_mod(self, layer, s, half):
        k = self.k
        with ExitStack() as es:
            sq = self.tmp(es, 'nsq', [128, NCH, 512], F32R)
            sq_t = Trk('nsq')
            tmp = self.tmp(es, 'ntmp', [128, NCH, 512], F32)
            tmp_t = Trk('ntmp')
            rstd = self.tmp(es, 'nrstd', [128, 512], F32)
            rstd_t = Trk('nrstd')
            for tt in range(2):
                xs = self.xT[half][:, :, tt * 512:(tt + 1) * 512]
                xs_t = [self.xT_t[half][c][tt] for c in range(NCH)]
                k.emit('act', lambda e: e.activation(out=sq[:], in_=xs, func=AF.Square), xs_t, [sq_t])
                ps, pst = self.ps[6], self.ps_t[6]
                for c in range(NCH):
                    k.emit('pe', lambda e, c=c: e.matmul(ps[:], self.onesR[:], sq[:, c, :], start=(c == 0), stop=(c == NCH - 1)),
                           [self.onesR_t, sq_t], [pst])
                k.emit('act', lambda e: e.activation(out=rstd[:], in_=ps[:], func=AF.Sqrt, bias=float(D * EPS), scale=1.0),
                       [pst], [rstd_t])
                k.emit('dve', lambda e: e.reciprocal(out=rstd[:], in_=rstd[:]), [rstd_t], [rstd_t])
                k.emit('dve', lambda e: e.tensor_tensor(out=tmp[:], in0=xs, in1=rstd[:].unsqueeze(1).broadcast_to([128, NCH, 512]),
                                                        op=ALU.mult), xs_t + [rstd_t], [tmp_t])
                for c in range(NCH):
                    A = self.AB[:, s, 0, c, half:half + 1]
                    B = self.AB[:, s, 1, c, half:half + 1]
                    out = self.hT[:, c, tt * 512:(tt + 1) * 512]
                    if c % 2 == 0:
                        k.emit('act', lambda e, c=c, A=A, B=B, out=out: e.activation(
                            out=out, in_=tmp[:, c, :], func=AF.Identity, bias=B, scale=A),
                            [tmp_t, self.AB_t], [self.hT_t[c][tt]])
                    else:
                        k.emit('dve', lambda e, c=c, A=A, B=B, out=out: e.tensor_scalar(
                            out=out, in0=tmp[:, c, :], scalar1=A, scalar2=B, op0=ALU.mult, op1=ALU.add),
                            [tmp_t, self.AB_t], [self.hT_t[c][tt]])
            k.barrier()

    def mixer(self, layer, half):
        kind = layer % 3
        ml = self.cfg.get('mixlist', (0, 1, 2))
        if kind not in ml:
            return
        if kind == 0:
            if half == 0:
                self.attn_prep(layer)
            self.attn(layer, half)
        elif kind == 1:
            self.hgrn(layer, half)
        else:
            self.gdn(layer, half)

    def out_proj(self, w_out_rows, src, src_t, nh, half, banks):
        k = self.k
        bi = 0
        pf = PF(self, [[w_out_rows[:, dp * 256:(dp + 1) * 256]] for dp in range(4)])
        for dp in range(4):
            (wo,), wot = pf.get(dp)
            for dmi in range(2):
                dm = dp * 2 + dmi
                for tt in range(2):
                    b = banks[bi % len(banks)]
                    bi += 1
                    ps, pst = self.ps[b], self.ps_t[b]
                    for hl in range(nh):
                        k.emit('pe', lambda e, ps=ps, hl=hl, dmi=dmi, tt=tt: e.matmul(
                            ps[:], wo[:, hl, dmi * 128:(dmi + 1) * 128], src[:, hl, tt * 512:(tt + 1) * 512],
                            start=(hl == 0), stop=(hl == nh - 1)), [wot, src_t[hl]], [pst])
                    xs = self.xT[half][:, dm, tt * 512:(tt + 1) * 512]
                    k.emit('dve', lambda e, ps=ps, xs=xs, dm=dm: e.scalar_tensor_tensor(
                        out=xs, in0=ps[:], scalar=self.gate(0, dm, half), in1=xs, op0=ALU.mult, op1=ALU.add),
                        [pst, self.MOD_t, self.xT_t[half][dm][tt]], [self.xT_t[half][dm][tt]])

    def head_norm(self, tmps, src, src_t, dst, dst_t, gcol, eps_total, bank, extra_mul=None, extra_t=None):
        k = self.k
        sq, sq_t, rs, rs_t = tmps
        ps, pst = self.ps[bank], self.ps_t[bank]
        for tt in range(2):
            sl = slice(tt * 512, (tt + 1) * 512)
            k.emit('act', lambda e, sl=sl: e.activation(out=sq[:], in_=src[:, sl], func=AF.Square), [src_t], [sq_t])
            k.emit('pe', lambda e: e.matmul(ps[:], self.onesR[:], sq[:], start=True, stop=True), [self.onesR_t, sq_t], [pst])
            k.emit('act', lambda e: e.activation(out=rs[:], in_=ps[:], func=AF.Sqrt, bias=float(eps_total), scale=1.0), [pst], [rs_t])
            k.emit('dve', lambda e: e.reciprocal(out=rs[:], in_=rs[:]), [rs_t], [rs_t])
            if extra_mul is not None:
                k.emit('dve', lambda e, sl=sl: e.tensor_tensor(out=rs[:], in0=rs[:], in1=extra_mul[:, sl], op=ALU.mult), [rs_t, extra_t], [rs_t])
            k.emit('dve', lambda e, sl=sl: e.scalar_tensor_tensor(out=dst[:, sl], in0=src[:, sl], scalar=gcol, in1=rs[:], op0=ALU.mult, op1=ALU.mult),
                   [src_t, rs_t, self.PT_t, self.AV_t] + ([self.HV_t] if hasattr(self, 'HV_t') else []), [dst_t])

    def norm_tmps(self, es):
        return (self.tmp(es, 'hsq', [128, 512], F32R), Trk('hsq'), self.tmp(es, 'hrs', [128, 512], F32), Trk('hrs'))

    def attn_prep(self, layer):
        k = self.k
        j = layer // 3
        lam_init = 0.8 - 0.6 * math.exp(-0.3 * layer)
        with ExitStack() as es:
            lt = self.tmp(es, 'ltab', [128, 512], F32)
            lt_t = self.ptrk('ltab')
            k.dma(lt[:], self.d['lamtab'][:, :], writes=[lt_t])
            pr = self.tmp(es, 'lpr', [128, 2, 64], F32)
            pr_t = Trk('lpr')
            sm = self.tmp(es, 'lsm', [128, 2], F32)
            sm_t = Trk('lsm')
            base = j * 256
            lq = lt[:, base:base + 256].rearrange("p (a r n) -> p a r n", a=2, r=2)
            k.emit('dve', lambda e: e.tensor_tensor(out=pr[:], in0=lq[:, :, 0, :], in1=lq[:, :, 1, :], op=ALU.mult), [lt_t], [pr_t])
            k.emit('dve', lambda e: e.reduce_sum(out=sm[:], in_=pr[:], axis=AX.X), [pr_t], [sm_t])
            k.emit('act', lambda e: e.activation(out=sm[:], in_=sm[:], func=AF.Exp), [sm_t], [sm_t])
            k.emit('dve', lambda e: e.tensor_tensor(out=self.AV[:, 0:1], in0=sm[:, 0:1], in1=sm[:, 1:2], op=ALU.subtract), [sm_t], [self.AV_t])
            k.emit('dve', lambda e: e.tensor_scalar(out=self.AV[:, 0:1], in0=self.AV[:, 0:1], scalar1=float(lam_init), scalar2=None, op0=ALU.add),
                   [self.AV_t], [self.AV_t])
            k.emit('dve', lambda e: e.tensor_scalar(out=self.AV[:, 1:2], in0=self.AV[:, 0:1], scalar1=-1.0, scalar2=None, op0=ALU.mult),
                   [self.AV_t], [self.AV_t])
            osl, _ = PC['subln']
            k.emit('dve', lambda e: e.tensor_scalar(out=self.AV[:, 2:3], in0=self.PT[:, osl + j:osl + j + 1],
                                                    scalar1=float((1.0 - lam_init) * math.sqrt(128.0)), scalar2=None, op0=ALU.mult),
                   [self.PT_t, self.AV_t], [self.AV_t])
            k.barrier()

    def attn(self, layer, half):
        k, d, nc = self.k, self.d, self.nc
        j = layer // 3
        w_in = d['attn_w_in']
        scale = 0.125
        nkc = 2 if half == 0 else 12
        with ExitStack() as es:
            GH = 2
            V = self.tmp(es, 'aV', [128, 8, GH * 128], F32R)
            V_t = self.ptrk('aV', 8)
            ntm = self.norm_tmps(es)
            agrp = self.tmp(es, 'agrp', [128, GH, TOK], F32R)
            agrp_t = trks('agrp', GH)
            QT = [self.tmp(es, f'aQ{i}', [128, TOK], F32R) for i in range(1)]
            QT_t = trks('aQ', 1)
            KT = [self.tmp(es, f'aK{i}', [128, TOK], F32R) for i in range(1)]
            KT_t = self.ptrk('aK', 1)
            Pt = [self.tmp(es, f'aP{i}', [128, 512], F32R) for i in range(2)]
            Pt_t = trks('aP', 2)
            att = self.tmp(es, 'att', [128, TOK], F32)
            att_t = Trk('att')
            Rr = self.tmp(es, 'aR', [128, 2, 512], F32)
            Rr_t = Trk('aR')
            Tt, Tt_t = Rr, Rr_t
            if half == 1:
                ropet = self.tmp(es, 'arope', [128, 2048], F32)
                ropet_t = self.ptrk('arope')
                k.dma(ropet[:], d['rope'][:, :], writes=[ropet_t])
                COS = ropet[:, 0:1024]
                SIN = ropet[:, 1024:2048]
                raw = [self.tmp(es, f'araw{i}', [128, 512], F32R) for i in range(1)]
                raw_t = trks('araw', 1)
                ri_ = 0
                t1 = self.tmp(es, 'at1', [128, 512], F32)
                t1_t = Trk('at1')
                kcr = self.tmp(es, 'akcr', [128, 512], F32R)
                kcr_t = Trk('akcr')
                vcr = self.tmp(es, 'avcr', [128, 4 * GH * 128], F32R)
                vcr_t = Trk('avcr')
            pi = 0
            pb = 0
            for grp in range(8 // GH):
                c0 = 2 * D + grp * 256
                gpf = PF(self, [[w_in[j, :, c0:c0 + 256]]] + [[w_in[j, :, (grp * GH + t) * 128:(grp * GH + t + 1) * 128],
                                                              w_in[j, :, D + (grp * GH + t) * 128:D + (grp * GH + t + 1) * 128]] for t in range(GH)])
                for piece in range(1):
                    (wv,), wvt = gpf.get(0)
                    for tile in range(8):
                        b = 6 + (pb % 2)
                        pb += 1
                        ps, pst = self.ps[b], self.ps_t[b]
                        for c in range(NCH):
                            k.emit('pe', lambda e, ps=ps, c=c, tile=tile: e.matmul(
                                ps[:, 0:256], self.hT[:, c, tile * 128:(tile + 1) * 128], wv[:, c, :],
                                start=(c == 0), stop=(c == NCH - 1)), [wvt, self.hT_t[c][tile // 4]], [pst])
                        self.copy(self.evac_eng(), V[:, tile, piece * 256:(piece + 1) * 256], ps[:, 0:256], [pst], [V_t[tile]])
                if half == 0:
                    for tile in range(8):
                        k.dma(d['vout'][j, tile * 128:(tile + 1) * 128, grp * 256:(grp + 1) * 256], V[:, tile, :].bitcast(F32), reads=[V_t[tile]])
                else:
                    (vcv,), _ = self.wpiece([d['cv'][j, :, grp * 256:(grp + 1) * 256]], dest=(vcr, vcr_t))
                for hl in range(GH):
                    hh = grp * GH + hl
                    qi = 0
                    Q, Q_t, Kk, K_t = QT[qi], QT_t[qi], KT[qi], KT_t[qi]
                    (wq, wk), wt = gpf.get(1 + hl)
                    for wi, (w, dst, dst_t) in enumerate(((wq, Q, Q_t), (wk, Kk, K_t))):
                        for tt in range(2):
                            sl = slice(tt * 512, (tt + 1) * 512)
                            b = 6 + (pb % 2)
                            pb += 1
                            ps, pst = self.ps[b], self.ps_t[b]
                            for c in range(NCH):
                                k.emit('pe', lambda e, ps=ps, w=w, c=c, sl=sl: e.matmul(
                                    ps[:], w[:, c, :], self.hT[:, c, sl],
                                    start=(c == 0), stop=(c == NCH - 1)), [wt, self.hT_t[c][tt]], [pst])
                            if half == 0:
                                self.copy(self.evac_eng(), dst[:, sl], ps[:], [pst], [dst_t])
                            else:
                                rw, rw_t = raw[0], raw_t[0]
                                ri_ += 1
                                self.copy('act', rw[:], ps[:], [pst], [rw_t])
                                b2 = 6 + (pb % 2)
                                pb += 1
                                ps2, ps2t = self.ps[b2], self.ps_t[b2]
                                k.emit('pe', lambda e, ps2=ps2, rw=rw: e.matmul(ps2[:], self.permR[:], rw[:], start=True, stop=True),
                                       [self.permR_t, rw_t], [ps2t])
                                k.emit('pool', lambda e, rw=rw, sl=sl: e.tensor_tensor(out=t1[:], in0=rw[:].bitcast(F32), in1=COS[:, sl], op=ALU.mult),
                                       [rw_t, ropet_t], [t1_t])
                                k.emit('dve', lambda e, ps2=ps2, sl=sl, dst=dst: e.tensor_tensor(out=dst[:, sl], in0=ps2[:], in1=SIN[:, sl], op=ALU.mult),
                                       [ps2t, ropet_t], [dst_t])
                                k.emit('dve', lambda e, dst=dst, sl=sl: e.tensor_tensor(out=dst[:, sl], in0=dst[:, sl].bitcast(F32), in1=t1[:], op=ALU.add),
                                       [t1_t, dst_t], [dst_t])
                    if half == 0:
                        k.dma(d['kout'][j, hh, :, :], Kk[:].bitcast(F32), reads=[K_t])
                    else:
                        self.wpiece([d['ck'][j, hh, :, :]], dest=(kcr, kcr_t))

                    def keyT(comp, kc, s=0):
                        r = slice(comp * 64, (comp + 1) * 64)
                        if half == 0:
                            return Kk[r, s * 256 + kc * 128: s * 256 + (kc + 1) * 128], K_t
                        if kc < 4:
                            return kcr[r, kc * 128:(kc + 1) * 128], kcr_t
                        return Kk[r, (kc - 4) * 128:(kc - 3) * 128], K_t

                    def valT(kc, s=0):
                        cs = slice(hl * 128, (hl + 1) * 128)
                        if half == 0:
                            return V[:, s * 2 + kc, cs], V_t[s * 2 + kc]
                        if kc < 4:
                            return vcv[:, kc, cs], vcr_t
                        return V[:, kc - 4, cs], V_t[kc - 4]

                    if half == 0:
                        qw = 256
                        units = [(s, 0) for s in range(4)]
                    else:
                        qw = 512
                        units = [(0, qt) for qt in range(2)]
                    for (s, qt) in units:
                        q0 = s * 256 if half == 0 else qt * 512
                        for comp in range(2):
                            r = slice(comp * 64, (comp + 1) * 64)
                            if half == 0:
                                ob, zb = 2, 3
                                osl = slice(comp * 256, (comp + 1) * 256)
                            else:
                                ob, zb = 2 + comp * 2, 3 + comp * 2
                                osl = slice(0, 512)
                            pO, pO_t = self.ps[ob], self.ps_t[ob]
                            pZ, pZ_t = self.ps[zb], self.ps_t[zb]
                            if half == 0:
                                sb_ = pi % 2
                                pS, pS_t = self.ps[sb_], self.ps_t[sb_]
                                P_, P_t = Pt[pi % 2], Pt_t[pi % 2]
                                pi += 1
                                for kc in range(2):
                                    kl, kl_t = keyT(comp, kc, s)
                                    k.emit('pe', lambda e, pS=pS, kl=kl, kc=kc, r=r, q0=q0: e.matmul(
                                        pS[:, kc * 256:(kc + 1) * 256], kl, Q[r, q0:q0 + 256], start=True, stop=True),
                                        [kl_t, Q_t], [pS_t])
                                k.emit('act', lambda e, pS=pS, P_=P_: e.activation(out=P_[:], in_=pS[:], func=AF.Exp, scale=scale), [pS_t], [P_t])
                                for kc in range(2):
                                    vl, vl_t = valT(kc, s)
                                    k.emit('pe', lambda e, pO=pO, vl=vl, P_=P_, kc=kc, osl=osl: e.matmul(
                                        pO[:, osl], vl, P_[:, kc * 256:(kc + 1) * 256], start=(kc == 0), stop=(kc == 1)),
                                        [vl_t, P_t], [pO_t])
                                for kc in range(2):
                                    k.emit('pe', lambda e, pZ=pZ, P_=P_, kc=kc, osl=osl: e.matmul(
                                        pZ[:, osl], self.onesR[:], P_[:, kc * 256:(kc + 1) * 256], start=(kc == 0), stop=(kc == 1)),
                                        [self.onesR_t, P_t], [pZ_t])
                            else:
                                for kc in range(nkc):
                                    sb_ = pi % 2
                                    pS, pS_t = self.ps[sb_], self.ps_t[sb_]
                                    P_, P_t = Pt[pi % 2], Pt_t[pi % 2]
                                    pi += 1
                                    kl, kl_t = keyT(comp, kc)
                                    k.emit('pe', lambda e, pS=pS, kl=kl, r=r, q0=q0: e.matmul(
                                        pS[:], kl, Q[r, q0:q0 + 512], start=True, stop=True), [kl_t, Q_t], [pS_t])
                                    k.emit('act', lambda e, pS=pS, P_=P_: e.activation(out=P_[:], in_=pS[:], func=AF.Exp, scale=scale), [pS_t], [P_t])
                                    vl, vl_t = valT(kc)
                                    k.emit('pe', lambda e, pO=pO, vl=vl, P_=P_, kc=kc: e.matmul(
                                        pO[:], vl, P_[:], start=(kc == 0), stop=(kc == nkc - 1)), [vl_t, P_t], [pO_t])
                                    k.emit('pe', lambda e, pZ=pZ, P_=P_, kc=kc: e.matmul(
                                        pZ[:], self.onesR[:], P_[:], start=(kc == 0), stop=(kc == nkc - 1)), [self.onesR_t, P_t], [pZ_t])
                        if half == 0:
                            pO, pO_t, pZ, pZ_t = self.ps[2], self.ps_t[2], self.ps[3], self.ps_t[3]
                            k.emit('dve', lambda e, pZ=pZ: e.reciprocal(out=Rr[:, 0, :], in_=pZ[:]), [pZ_t], [Rr_t])
                            k.emit('dve', lambda e, pO=pO: e.tensor_tensor(out=Tt[:, 0, :], in0=pO[:], in1=Rr[:, 0, :], op=ALU.mult), [pO_t, Rr_t], [Tt_t])
                            k.emit('dve', lambda e, q0=q0: e.scalar_tensor_tensor(
                                out=att[:, q0:q0 + 256], in0=Tt[:, 0, 256:512], scalar=self.AV[:, 1:2], in1=Tt[:, 0, 0:256],
                                op0=ALU.mult, op1=ALU.add), [Tt_t, self.AV_t], [att_t])
                        else:
                            for comp in range(2):
                                pO, pO_t = self.ps[2 + comp * 2], self.ps_t[2 + comp * 2]
                                pZ, pZ_t = self.ps[3 + comp * 2], self.ps_t[3 + comp * 2]
                                k.emit('dve', lambda e, pZ=pZ, comp=comp: e.reciprocal(out=Rr[:, comp, :], in_=pZ[:]), [pZ_t], [Rr_t])
                                k.emit('dve', lambda e, pO=pO, comp=comp: e.tensor_tensor(out=Tt[:, comp, :], in0=pO[:], in1=Rr[:, comp, :], op=ALU.mult),
                                       [pO_t, Rr_t], [Tt_t])
                            k.emit('dve', lambda e, q0=q0: e.scalar_tensor_tensor(
                                out=att[:, q0:q0 + 512], in0=Tt[:, 1, :], scalar=self.AV[:, 1:2], in1=Tt[:, 0, :],
                                op0=ALU.mult, op1=ALU.add), [Tt_t, self.AV_t], [att_t])
                    self.head_norm(ntm, att[:], att_t, agrp[:, hl, :], agrp_t[hl], self.AV[:, 2:3], 128.0 * 1e-5, 6 + (pb % 2))
                    pb += 1
                self.out_proj(d['attn_w_out'][j, grp * GH * 128:(grp + 1) * GH * 128, :], agrp, agrp_t, GH, half, [6, 7, 0, 1])
            k.barrier()

    def hgrn_prep(self, layer):
        k = self.k
        ol, _ = PC['hgrn_lb']
        self.HV = self.k.sb('HV', [128, 3, 8])
        self.HV_t = Trk('HV')
        with ExitStack() as es:
            ex = self.tmp(es, 'hex', [128, 4, 8], F32)
            ex_t = Trk('hex')
            tot = self.tmp(es, 'htot', [128, 8], F32)
            tot_t = Trk('htot')
            k.emit('act', lambda e: e.activation(out=ex[:], in_=self.PT[:, ol:ol + 32].rearrange("p (l c) -> p l c", l=4), func=AF.Exp),
                   [self.PT_t], [ex_t])
            k.emit('dve', lambda e: e.tensor_tensor(out=tot[:], in0=ex[:, 0, :], in1=ex[:, 1, :], op=ALU.add), [ex_t], [tot_t])
            for l in (2, 3):
                k.emit('dve', lambda e, l=l: e.tensor_tensor(out=tot[:], in0=tot[:], in1=ex[:, l, :], op=ALU.add), [ex_t, tot_t], [tot_t])
            k.emit('dve', lambda e: e.reciprocal(out=tot[:], in_=tot[:]), [tot_t], [tot_t])
            k.emit('dve', lambda e: e.tensor_copy(out=self.HV[:, 0, :], in_=ex[:, 1, :]), [ex_t], [self.HV_t])
            for l in range(2, layer + 1):
                k.emit('dve', lambda e, l=l: e.tensor_tensor(out=self.HV[:, 0, :], in0=self.HV[:, 0, :], in1=ex[:, l, :], op=ALU.add),
                       [ex_t, self.HV_t], [self.HV_t])
            k.emit('dve', lambda e: e.tensor_tensor(out=self.HV[:, 0, :], in0=self.HV[:, 0, :], in1=tot[:], op=ALU.mult), [tot_t, self.HV_t], [self.HV_t])
            k.emit('dve', lambda e: e.tensor_scalar(out=self.HV[:, 1, :], in0=self.HV[:, 0, :], scalar1=-1.0, scalar2=1.0, op0=ALU.mult, op1=ALU.add),
                   [self.HV_t], [self.HV_t])
            on, _ = PC['hgrn_norm']
            k.emit('dve', lambda e: e.tensor_scalar(out=self.HV[:, 2, 0:1], in0=self.PT[:, on:on + 1], scalar1=float(math.sqrt(128.0)), scalar2=None, op0=ALU.mult),
                   [self.PT_t, self.HV_t], [self.HV_t])
            k.barrier()

    def hgrn(self, layer, half):
        k, d, nc = self.k, self.d, self.nc
        j = layer // 3
        if half == 0:
            self.hgrn_prep(layer)
        w_in = d['hgrn_w_in']
        nseq = 4 if half == 0 else 1
        cps = 16 // nseq
        oo, _ = CC['ones']
        ONES = self.CT[:, oo:oo + 1].broadcast_to([128, TOK])
        oi, _ = CC['ident']
        IDENT = self.CT[:, oi:oi + 128]
        masks = []
        for nm in ('mask_f', 'mask_b'):
            om, _ = CC[nm]
            masks.append(self.CT[0:64, om:om + 256])
        with ExitStack() as es:
            V64 = self.tmp(es, 'hV', [64, 16, 128], F32)
            V64_t = trks('hV', 16)
            mix = self.tmp(es, 'hmix', [128, 1, TOK], F32R)
            mix_t = trks('hmix', 1)
            ntm = self.norm_tmps(es)
            qT = self.tmp(es, 'hq', [128, TOK], F32)
            qT_t = Trk('hq')
            gs, gs_t = qT, qT_t
            oT = self.tmp(es, 'ho', [128, TOK], F32)
            oT_t = Trk('ho')
            Fb = [self.tmp(es, f'hF{i}', [128, TOK], F32) for i in range(2)]
            Fb_t = trks('hF', 2)
            L = self.tmp(es, 'hL', [128, TOK], F32)
            L_t = Trk('hL')
            Gp = self.tmp(es, 'hGp', [128, 64 + TOK + 64], F32)
            Gp_t = Trk('hGp')
            E1 = self.tmp(es, 'hE1', [128, TOK], F32)
            E1_t = Trk('hE1')
            E2 = self.tmp(es, 'hE2', [128, TOK], F32)
            E2_t = Trk('hE2')
            Am = self.tmp(es, 'hAm', [64, 4, 64], F32)
            Am_t = Trk('hAm')
            Ktok = self.tmp(es, 'hKt', [64, 4, 128], F32)
            Ktok_t = Trk('hKt')
            Sb = [self.tmp(es, f'hS{i}', [128, 128], F32) for i in range(2)]
            Sb_t = trks('hS', 2)
            DK = self.tmp(es, 'hDK', [128, 3, 16], F32)
            DK_t = Trk('hDK')
            G3 = self.tmp(es, 'hG3', [128, 3, 16], F32)
            G3_t = Trk('hG3')
            Sp = [self.tmp(es, f'hSp{i}', [128, 128], F32) for i in range(2)]
            Sp_t = trks('hSp', 2)
            tS = self.tmp(es, 'htS', [128, 128], F32)
            tS_t = Trk('htS')
            spi = 0
            sout_t = self.ptrk('hso')
            if half == 0:
                sout = self.tmp(es, 'hso', [128, 4, 2, 128], F32)
            k.emit('dve', lambda e: e.memset(Gp[:], 0.0), [], [Gp_t])
            pb = 0
            for pair in range(4):
                for hl in range(2):
                    hh = pair * 2 + hl
                    hpf = PF(self, [[w_in[j, :, 3 * D + hh * 128:3 * D + (hh + 1) * 128]],
                                    [w_in[j, :, hh * 128:(hh + 1) * 128], w_in[j, :, D + hh * 128:D + (hh + 1) * 128]],
                                    [w_in[j, :, 2 * D + hh * 128:2 * D + (hh + 1) * 128]]])
                    (wi,), wit = hpf.get(0)
                    for tt in range(2):
                        sl = slice(tt * 512, (tt + 1) * 512)
                        b = 6 + (pb % 2)
                        pb += 1
                        ps, pst = self.ps[b], self.ps_t[b]
                        for c in range(NCH):
                            k.emit('pe', lambda e, ps=ps, c=c, sl=sl: e.matmul(ps[:], wi[:, c, :], self.hT[:, c, sl], start=(c == 0), stop=(c == NCH - 1)),
                                   [wit, self.hT_t[c][tt]], [pst])
                        self.copy(self.evac_eng(), L[:, sl], ps[:], [pst], [L_t])
                    for g4 in range(4):
                        b = 6 + (pb % 2)
                        pb += 1
                        ps, pst = self.ps[b], self.ps_t[b]
                        for q_ in range(4):
                            ch = g4 * 4 + q_
                            k.emit('pe', lambda e, ps=ps, q_=q_, ch=ch: e.transpose(ps[0:64, q_ * 128:(q_ + 1) * 128], L[:, ch * 64:(ch + 1) * 64], IDENT),
                                   [L_t, self.CT_t], [pst])
                        self.copy(self.evac_eng(), V64[:, g4 * 4:(g4 + 1) * 4, :].rearrange("p a n -> p (a n)"), ps[0:64, :], [pst], V64_t[g4 * 4:(g4 + 1) * 4])
                    lbc = self.HV[:, 0, hh:hh + 1]
                    omc = self.HV[:, 1, hh:hh + 1]
                    (wq, wzf), wt1 = hpf.get(1)
                    (wzb,), wt2 = hpf.get(2)
                    for (w, wt, kindp) in ((wq, wt1, 'q'), (wzf, wt1, 'zf'), (wzb, wt2, 'zb')):
                        for tt in range(2):
                            sl = slice(tt * 512, (tt + 1) * 512)
                            b = 6 + (pb % 2)
                            pb += 1
                            ps, pst = self.ps[b], self.ps_t[b]
                            for c in range(NCH):
                                k.emit('pe', lambda e, ps=ps, w=w, c=c, sl=sl: e.matmul(
                                    ps[:], w[:, c, :], self.hT[:, c, sl], start=(c == 0), stop=(c == NCH - 1)),
                                    [wt, self.hT_t[c][tt]], [pst])
                            if kindp == 'q':
                                k.emit('act', lambda e, ps=ps, sl=sl: e.activation(out=qT[:, sl], in_=ps[:], func=AF.Copy, scale=float(128.0 ** -0.5)),
                                       [pst], [qT_t])
                            elif kindp == 'g':
                                k.emit('act', lambda e, ps=ps, sl=sl: e.activation(out=gs[:, sl], in_=ps[:], func=AF.Silu), [pst], [gs_t])
                            else:
                                di = 0 if kindp == 'zf' else 1
                                k.emit('act', lambda e, ps=ps, sl=sl, di=di: e.activation(out=Fb[di][:, sl], in_=ps[:], func=AF.Sigmoid), [pst], [Fb_t[di]])
                    for di in range(2):
                        F_, F_t = Fb[di], Fb_t[di]
                        k.emit('dve', lambda e, F_=F_: e.tensor_scalar(out=F_[:], in0=F_[:], scalar1=omc, scalar2=lbc, op0=ALU.mult, op1=ALU.add),
                               [F_t, self.HV_t], [F_t])
                        k.emit('act', lambda e, F_=F_: e.activation(out=L[:], in_=F_[:], func=AF.Ln), [F_t], [L_t])
                        k.emit('pool', lambda e, F_=F_: e.tensor_scalar(out=F_[:], in0=F_[:], scalar1=-1.0, scalar2=1.0, op0=ALU.mult, op1=ALU.add),
                               [F_t], [F_t])
                        k.emit('dve', lambda e: e.tensor_tensor_scan(out=Gp[:, 64:64 + TOK], data0=ONES, data1=L[:], initial=0.0,
                                                                     op0=ALU.mult, op1=ALU.add), [L_t, self.CT_t], [Gp_t])
                        Lv = L[:].rearrange("p (j n) -> p j n", n=64)
                        if di == 0:
                            gprev = Gp[:, 63:63 + TOK].rearrange("p (j n) -> p j n", n=64)[:, :, 0:1].broadcast_to([128, 16, 64])
                            gcur = Gp[:, 64:64 + TOK].rearrange("p (j n) -> p j n", n=64)
                            k.emit('dve', lambda e: e.tensor_tensor(out=Lv, in0=gcur, in1=gprev, op=ALU.subtract), [Gp_t], [L_t])
                        else:
                            gend = Gp[:, 127:127 + TOK].rearrange("p (j n) -> p j n", n=64)[:, :, 0:1].broadcast_to([128, 16, 64])
                            gsh = Gp[:, 63:63 + TOK].rearrange("p (j n) -> p j n", n=64)
                            k.emit('dve', lambda e: e.tensor_tensor(out=Lv, in0=gend, in1=gsh, op=ALU.subtract), [Gp_t], [L_t])
                        pos = 63 if di == 0 else 0
                        mid = 31 if di == 0 else 32
                        k.emit('pool', lambda e, pos=pos: e.tensor_copy(out=G3[:, 0, :], in_=Lv[:, :, pos]), [L_t], [G3_t])
                        k.emit('pool', lambda e, mid=mid: e.tensor_copy(out=G3[:, 1, :], in_=Lv[:, :, mid]), [L_t], [G3_t])
                        k.emit('dve', lambda e: e.tensor_tensor(out=G3[:, 2, :], in0=G3[:, 0, :], in1=G3[:, 1, :], op=ALU.subtract), [G3_t], [G3_t])
                        k.emit('act', lambda e: e.activation(out=DK[:], in_=G3[:], func=AF.Exp), [G3_t], [DK_t])
                        k.emit('dve', lambda e: e.tensor_tensor(out=Lv, in0=Lv, in1=G3[:, 1, :].unsqueeze(2).broadcast_to([128, 16, 64]), op=ALU.subtract),
                               [L_t, G3_t], [L_t])
                        k.emit('act', lambda e: e.activation(out=E1[:], in_=L[:], func=AF.Exp), [L_t], [E1_t])
                        k.emit('act', lambda e: e.activation(out=E2[:], in_=L[:], func=AF.Exp, scale=-1.0), [L_t], [E2_t])
                        k.emit('dve', lambda e: e.tensor_tensor(out=E1[:], in0=E1[:], in1=qT[:], op=ALU.mult), [E1_t, qT_t], [E1_t])
                        k.emit('pool', lambda e, F_=F_: e.tensor_tensor(out=E2[:], in0=E2[:], in1=F_[:], op=ALU.mult), [E2_t, F_t], [E2_t])
                        order = list(range(16)) if di == 0 else list(range(15, -1, -1))
                        if half == 1:
                            k.dma(Sb[0][:], d['st_hgrn'][di, hh, :, :], writes=[Sb_t[0]])
                        si = 0
                        for gi in range(4):
                            chs = order[gi * 4:(gi + 1) * 4]
                            lo = min(chs)
                            psA, psA_t = self.ps[0], self.ps_t[0]
                            psT, psT_t = self.ps[1], self.ps_t[1]
                            for ch in chs:
                                cs = slice(ch * 64, (ch + 1) * 64)
                                q_ = ch - lo
                                k.emit('pe', lambda e, cs=cs, q_=q_: e.matmul(psA[0:64, q_ * 64:(q_ + 1) * 64], E2[:, cs], E1[:, cs], start=True, stop=True),
                                       [E1_t, E2_t], [psA_t])
                                k.emit('pe', lambda e, cs=cs, q_=q_: e.transpose(psT[0:64, q_ * 128:(q_ + 1) * 128], E2[:, cs], IDENT),
                                       [E2_t, self.CT_t], [psT_t])
                            k.emit('dve', lambda e, di=di: e.tensor_tensor(out=Am[:].rearrange("p a n -> p (a n)"), in0=psA[0:64, 0:256], in1=masks[di], op=ALU.mult),
                                   [psA_t, self.CT_t], [Am_t])
                            k.emit('act', lambda e: e.activation(out=Ktok[:].rearrange("p a n -> p (a n)"), in_=psT[0:64, :], func=AF.Copy), [psT_t], [Ktok_t])
                            psO, psO_t = self.ps[2], self.ps_t[2]
                            for ch in chs:
                                cs = slice(ch * 64, (ch + 1) * 64)
                                q_ = ch - lo
                                loc = ch % cps
                                first = (loc == 0) if di == 0 else (loc == cps - 1)
                                last = (loc == cps - 1) if di == 0 else (loc == 0)
                                seq = ch // cps
                                zero_init = first and half == 0
                                vv = V64[:, ch, :]
                                S_prev, S_prev_t = Sb[si], Sb_t[si]
                                k.emit('pe', lambda e, vv=vv, q_=q_, zero_init=zero_init: e.matmul(
                                    psO[:, q_ * 64:(q_ + 1) * 64], vv, Am[:, q_, :], start=True, stop=zero_init), [V64_t[ch], Am_t], [psO_t])
                                if not zero_init:
                                    sp_, sp_t = Sp[spi % 2], Sp_t[spi % 2]
                                    spi += 1
                                    k.emit('act', lambda e, sp_=sp_, S_prev=S_prev, ch=ch: e.activation(
                                        out=sp_[:], in_=S_prev[:], func=AF.Identity, scale=DK[:, 1, ch:ch + 1]), [S_prev_t, DK_t], [sp_t])
                                    k.emit('pe', lambda e, cs=cs, q_=q_, sp_=sp_: e.matmul(
                                        psO[:, q_ * 64:(q_ + 1) * 64], sp_[:], E1[:, cs], start=False, stop=True), [sp_t, E1_t], [psO_t])
                                bS = 3 + (pb % 2)
                                pb += 1
                                psS, psS_t = self.ps[bS], self.ps_t[bS]
                                k.emit('pe', lambda e, psS=psS, vv=vv, q_=q_: e.matmul(
                                    psS[:, 0:128], Ktok[:, q_, :], vv, start=True, stop=True), [Ktok_t, V64_t[ch]], [psS_t])
                                if last and half == 0:
                                    dstS, dstS_t = sout[:, seq, di, :], sout_t
                                else:
                                    si = 1 - si
                                    dstS, dstS_t = Sb[si][:], Sb_t[si]
                                if zero_init:
                                    k.emit('act', lambda e, psS=psS, ch=ch, dstS=dstS: e.activation(
                                        out=dstS, in_=psS[:, 0:128], func=AF.Identity, scale=DK[:, 2, ch:ch + 1]), [psS_t, DK_t], [dstS_t])
                                else:
                                    k.emit('act', lambda e, psS=psS, ch=ch: e.activation(
                                        out=tS[:], in_=psS[:, 0:128], func=AF.Identity, scale=DK[:, 2, ch:ch + 1]), [psS_t, DK_t], [tS_t])
                                    k.emit('dve', lambda e, dstS=dstS, S_prev=S_prev, ch=ch: e.scalar_tensor_tensor(
                                        out=dstS, in0=S_prev[:], scalar=DK[:, 0, ch:ch + 1], in1=tS[:], op0=ALU.mult, op1=ALU.add),
                                        [S_prev_t, DK_t, tS_t], [dstS_t])
                            osl = slice(lo * 64, lo * 64 + 256)
                            if di == 0:
                                k.emit('dve', lambda e, osl=osl: e.tensor_copy(out=oT[:, osl], in_=psO[:, 0:256]), [psO_t], [oT_t])
                            else:
                                k.emit('dve', lambda e, osl=osl: e.tensor_tensor(out=oT[:, osl], in0=oT[:, osl], in1=psO[:, 0:256], op=ALU.add), [psO_t, oT_t], [oT_t])
                    if half == 0:
                        k.dma(d['hgout'][:, :, hh, :, :].rearrange("s d k e -> k s d e"), sout[:], reads=[sout_t])
                    (wg,), wt3 = self.wpiece([w_in[j, :, 4 * D + hh * 128:4 * D + (hh + 1) * 128]])
                    for tt in range(2):
                        sl = slice(tt * 512, (tt + 1) * 512)
                        ps, pst = self.ps[6 + tt], self.ps_t[6 + tt]
                        for c in range(NCH):
                            k.emit('pe', lambda e, ps=ps, c=c, sl=sl: e.matmul(ps[:], wg[:, c, :], self.hT[:, c, sl], start=(c == 0), stop=(c == NCH - 1)),
                                   [wt3, self.hT_t[c][tt]], [pst])
                        k.emit('act', lambda e, ps=ps, sl=sl: e.activation(out=gs[:, sl], in_=ps[:], func=AF.Silu), [pst], [gs_t])
                    self.head_norm(ntm, oT[:], oT_t, mix[:, 0, :], mix_t[0], self.HV[:, 2, 0:1], 128.0 * 1e-6, 5, extra_mul=gs[:], extra_t=gs_t)
                    self.out_proj(d['hgrn_w_out'][j, hh * 128:(hh + 1) * 128, :], mix, mix_t, 1, half, [6, 7])
            k.barrier()

    def gdn(self, layer, half):
        k, d, nc = self.k, self.d, self.nc
        w_in = d['gdn_w_in']
        nseq = 4 if half == 0 else 1
        cps = 16 // nseq
        seqlen = TOK // nseq
        CT = self.CT

        def cc(name, rows=64, w=None):
            o_, w_ = CC[name]
            return CT[0:rows, o_:o_ + (w or w_)]

        ONES_ROW = cc('ones', 128, 1).broadcast_to([128, TOK])
        IDENT = cc('ident', 128)
        ID64 = cc('ident', 64, 64)
        TRIF = cc('mask_f', 64, 64)
        TRIB = cc('mask_b', 64, 64)
        ONES64 = cc('ones', 64, 64)
        NEGU = [cc('negu_f'), cc('negu_b')]
        NEGL = [cc('negl_f'), cc('negl_b')]
        SNEG = [cc('sneg_f'), cc('sneg_b')]
        IDREP = cc('idrep')
        osel, _ = CC['sel']
        onsel, _ = CC['nsel']

        def SEL(kk, m):
            return CT[0:4, osel + kk * 128: osel + kk * 128 + m]

        def NSEL(kk):
            return CT[0:4, onsel + kk * 64: onsel + (kk + 1) * 64]

        oa, _ = PC['gdn_alog_col']
        odt, _ = PC['gdn_dt_col']
        oar, _ = PC['gdn_alog_row']
        odr, _ = PC['gdn_dt_row']
        ocv, _ = PC['gdn_conv']
        ogn, _ = PC['gdn_norm']
        scr_t = self.ptrk('gscr')
        with ExitStack() as es0:
            GC = self.tmp(es0, 'gGC', [64, 16, 16]); BE = self.tmp(es0, 'gBE', [64, 16, 16])
            NBE = self.tmp(es0, 'gNBE', [64, 16, 16]); C1 = self.tmp(es0, 'gC1', [64, 16, 16])
            WW = self.tmp(es0, 'gWW', [64, 16, 16])
            TB_t = Trk('gTB')
            GV = self.tmp(es0, 'gGV', [128, 20])
            GV_t = Trk('gGV')
            k.emit('act', lambda e: e.activation(out=GV[:, 0:1], in_=self.PT[:, oa:oa + 1], func=AF.Exp), [self.PT_t], [GV_t])
            k.emit('dve', lambda e: e.tensor_scalar(out=GV[:, 0:1], in0=GV[:, 0:1], scalar1=-1.0, scalar2=None, op0=ALU.mult), [GV_t], [GV_t])
            k.emit('act', lambda e: e.activation(out=GV[:, 4:20], in_=self.PT[:, oar:oar + 16], func=AF.Exp), [self.PT_t], [GV_t])
            k.emit('dve', lambda e: e.tensor_scalar(out=GV[:, 4:20], in0=GV[:, 4:20], scalar1=-1.0, scalar2=None, op0=ALU.mult), [GV_t], [GV_t])
            k.emit('dve', lambda e: e.tensor_scalar(out=GV[:, 1:2], in0=self.PT[:, ogn:ogn + 1], scalar1=float(math.sqrt(128.0)), scalar2=None, op0=ALU.mult),
                   [self.PT_t, GV_t], [GV_t])
            (wab,), wab_t = self.wpiece([w_in[0, :, 4 * D:4 * D + 32]], rounded=False)
            with ExitStack() as es:
                LA = self.tmp(es, 'gLA', [16, TOK]); LA_t = Trk('gLA')
                Gp = self.tmp(es, 'gGp', [16, 64 + TOK + 64]); Gp_t = Trk('gGp')
                GF = self.tmp(es, 'gGF', [16, TOK]); GF_t = self.ptrk('gGF')
                GB = self.tmp(es, 'gGB', [16, TOK]); GB_t = self.ptrk('gGB')
                BT = self.tmp(es, 'gBT', [16, TOK]); BT_t = self.ptrk('gBT')
                LAt = self.tmp(es, 'gLAt', [64, 16, 16]); LAt_t = Trk('gLAt')
                k.emit('dve', lambda e: e.memset(Gp[:], 0.0), [], [Gp_t])
                hTf = self.hT[:].bitcast(F32)
                for part in range(2):
                    for tt in range(2):
                        sl = slice(tt * 512, (tt + 1) * 512)
                        ps, pst = self.ps[6 + tt], self.ps_t[6 + tt]
                        for c in range(NCH):
                            k.emit('pe', lambda e, ps=ps, c=c, sl=sl, part=part: e.matmul(
                                ps[0:16, :], wab[:, c, part * 16:(part + 1) * 16], hTf[:, c, sl], start=(c == 0), stop=(c == NCH - 1)),
                                [wab_t, self.hT_t[c][tt]], [pst])
                        if part == 0:
                            k.emit('act', lambda e, ps=ps, sl=sl: e.activation(out=LA[:, sl], in_=ps[0:16, :], func=AF.Exp, bias=self.PT[0:16, odt:odt + 1]),
                                   [pst, self.PT_t], [LA_t])
                        else:
                            k.emit('act', lambda e, ps=ps, sl=sl: e.activation(out=BT[:, sl], in_=ps[0:16, :], func=AF.Sigmoid), [pst], [BT_t])
                k.emit('act', lambda e: e.activation(out=LA[:], in_=LA[:], func=AF.Ln, bias=1.0), [LA_t], [LA_t])
                k.emit('dve', lambda e: e.tensor_scalar(out=LA[:], in0=LA[:], scalar1=GV[0:16, 0:1], scalar2=None, op0=ALU.mult), [LA_t, GV_t], [LA_t])
                k.emit('dve', lambda e: e.tensor_tensor_scan(out=Gp[:, 64:64 + TOK], data0=ONES_ROW[0:16, :], data1=LA[:], initial=0.0,
                                                             op0=ALU.mult, op1=ALU.add), [LA_t, self.CT_t], [Gp_t])
                gprev = Gp[:, 63:63 + TOK].rearrange("p (j n) -> p j n", n=64)[:, :, 0:1].broadcast_to([16, 16, 64])
                gcur = Gp[:, 64:64 + TOK].rearrange("p (j n) -> p j n", n=64)
                k.emit('dve', lambda e: e.tensor_tensor(out=GF[:].rearrange("p (j n) -> p j n", n=64), in0=gcur, in1=gprev, op=ALU.subtract), [Gp_t], [GF_t])
                gend = Gp[:, 127:127 + TOK].rearrange("p (j n) -> p j n", n=64)[:, :, 0:1].broadcast_to([16, 16, 64])
                gsh = Gp[:, 63:63 + TOK].rearrange("p (j n) -> p j n", n=64)
                k.emit('dve', lambda e: e.tensor_tensor(out=GB[:].rearrange("p (j n) -> p j n", n=64), in0=gend, in1=gsh, op=ALU.subtract), [Gp_t], [GB_t])
                k.dma(d['gscr'][0:16, :], GF[:], reads=[GF_t], writes=[scr_t])
                k.dma(d['gscr'][16:32, :], GB[:], reads=[GB_t], writes=[scr_t])
                k.dma(d['gscr'][32:48, :], BT[:], reads=[BT_t], writes=[scr_t])
                ps, pst = self.ps[5], self.ps_t[5]
                for ch in range(16):
                    for c in range(NCH):
                        k.emit('pe', lambda e, c=c, ch=ch: e.matmul(
                            ps[0:64, ch * 32:(ch + 1) * 32], hTf[:, c, ch * 64:(ch + 1) * 64], wab[:, c, :], start=(c == 0), stop=(c == NCH - 1)),
                            [wab_t, self.hT_t[c][ch // 8]], [pst])
                pv = ps[0:64, :].rearrange("p (c n) -> p c n", n=32)
                k.emit('dve', lambda e: e.tensor_tensor(out=LAt[:], in0=pv[:, :, 0:16],
                                                        in1=self.PT[0:64, odr:odr + 16].unsqueeze(1).broadcast_to([64, 16, 16]), op=ALU.add),
                       [pst, self.PT_t], [LAt_t])
                k.emit('act', lambda e: e.activation(out=BE[:], in_=pv[:, :, 16:32], func=AF.Sigmoid), [pst], [TB_t])
                k.emit('act', lambda e: e.activation(out=LAt[:], in_=LAt[:], func=AF.Exp), [LAt_t], [LAt_t])
                k.emit('act', lambda e: e.activation(out=LAt[:], in_=LAt[:], func=AF.Ln, bias=1.0), [LAt_t], [LAt_t])
                k.emit('dve', lambda e: e.tensor_tensor(out=LAt[:], in0=LAt[:], in1=GV[0:64, 4:20].unsqueeze(1).broadcast_to([64, 16, 16]), op=ALU.mult),
                       [LAt_t, GV_t], [LAt_t])
                LAf = LAt[:].rearrange("p c n -> p (c n)")
                pF, pF_t = self.ps[0], self.ps_t[0]
                pB, pB_t = self.ps[1], self.ps_t[1]
                pT, pT_t = self.ps[2], self.ps_t[2]
                k.emit('pe', lambda e: e.matmul(pF[0:64, 0:256], TRIF, LAf, start=True, stop=True), [LAt_t, self.CT_t], [pF_t])
                k.emit('pe', lambda e: e.matmul(pB[0:64, 0:256], TRIB, LAf, start=True, stop=True), [LAt_t, self.CT_t], [pB_t])
                k.emit('pe', lambda e: e.matmul(pT[0:64, 0:256], ONES64, LAf, start=True, stop=True), [LAt_t, self.CT_t], [pT_t])
                pFv = pF[0:64, 0:256].rearrange("p (c n) -> p c n", n=16)
                pBv = pB[0:64, 0:256].rearrange("p (c n) -> p c n", n=16)
                pTv = pT[0:64, 0:256].rearrange("p (c n) -> p c n", n=16)
                k.emit('dve', lambda e: e.tensor_copy(out=GC[:, :, 0:8], in_=pFv[:, :, 0:8]), [pF_t], [TB_t])
                k.emit('dve', lambda e: e.tensor_copy(out=GC[:, :, 8:16], in_=pBv[:, :, 8:16]), [pB_t], [TB_t])
                k.emit('dve', lambda e: e.tensor_tensor(out=WW[:], in0=pTv, in1=GC[:], op=ALU.subtract), [pT_t, TB_t], [TB_t])
                k.emit('act', lambda e: e.activation(out=WW[:], in_=WW[:], func=AF.Exp), [TB_t], [TB_t])
                k.emit('act', lambda e: e.activation(out=C1[:], in_=GC[:], func=AF.Exp), [TB_t], [TB_t])
                k.emit('dve', lambda e: e.scalar_tensor_tensor(out=C1[:], in0=C1[:], scalar=-1.0, in1=BE[:], op0=ALU.mult, op1=ALU.mult), [TB_t], [TB_t])
                k.emit('dve', lambda e: e.tensor_scalar(out=NBE[:], in0=BE[:], scalar1=-1.0, scalar2=None, op0=ALU.mult), [TB_t], [TB_t])
                k.barrier()
            stop = self.cfg.get('gdn_stop', 9)
            if stop <= 1:
                return
            with ExitStack() as es:
                qn = self.tmp(es, 'gq', [128, TOK]); qn_t = Trk('gq')
                kn = self.tmp(es, 'gk', [128, TOK]); kn_t = Trk('gk')
                vT = self.tmp(es, 'gv', [128, TOK]); vT_t = Trk('gv')
                oT = self.tmp(es, 'go', [128, TOK]); oT_t = Trk('go')
                Qt = self.tmp(es, 'gQt', [128, TOK]); Qt_t = Trk('gQt')
                mix = self.tmp(es, 'gmix', [128, 1, TOK], F32R); mix_t = trks('gmix', 1)
                ntm = self.norm_tmps(es)
                HR4 = self.tmp(es, 'gHR', [4, TOK]); HR4_t = self.ptrk('gHR')
                bt = [self.tmp(es, f'gb{i}', [64, 4, 64]) for i in range(10)]
                bt_t = trks('gb', 10)
                ktok = self.tmp(es, 'gkt', [64, 4, 128]); ktok_t = Trk('gkt')
                vtok = self.tmp(es, 'gvt', [64, 4, 128]); vtok_t = Trk('gvt')
                sm = [self.tmp(es, f'gs{i}', [64, 128]) for i in range(4)]
                sm_t = trks('gs', 4)
                Sb = [self.tmp(es, f'gS{i}', [128, 128]) for i in range(2)]
                Sb_t = trks('gS', 2)
                DKg = self.tmp(es, 'gDK', [128, 16]); DKg_t = Trk('gDK')
                sout_t = self.ptrk('gso')
                if half == 0:
                    sout = self.tmp(es, 'gso', [128, 4, 2, 128])
                pb = 0
                for hh in range(8):
                    (wq, wk), wt1 = self.wpiece([w_in[0, :, hh * 128:(hh + 1) * 128], w_in[0, :, D + hh * 128:D + (hh + 1) * 128]])
                    (wv,), wt2 = self.wpiece([w_in[0, :, 2 * D + hh * 128:2 * D + (hh + 1) * 128]])
                    k.dma_group([(HR4[0:1, :], d['gscr'][hh:hh + 1, :]), (HR4[1:2, :], d['gscr'][24 + hh:25 + hh, :]),
                                 (HR4[2:3, :], d['gscr'][32 + hh:33 + hh, :]), (HR4[3:4, :], d['gscr'][40 + hh:41 + hh, :])], [HR4_t])
                    HR4_t.rs[scr_t.w[0]] = 0
                    for ti, (w, wt, dst, dst_t) in enumerate(((wq, wt1, qn, qn_t), (wk, wt1, kn, kn_t), (wv, wt2, vT, vT_t))):
                        fch = ti * 8 + hh
                        w0 = self.PT[:, ocv + 0 * 24 + fch: ocv + 0 * 24 + fch + 1]
                        w1 = self.PT[:, ocv + 1 * 24 + fch: ocv + 1 * 24 + fch + 1]
                        w2 = self.PT[:, ocv + 2 * 24 + fch: ocv + 2 * 24 + fch + 1]
                        pss = []
                        for tt in range(2):
                            sl = slice(tt * 512, (tt + 1) * 512)
                            b = 6 + tt
                            ps, pst = self.ps[b], self.ps_t[b]
                            pss.append((ps, pst))
                            for c in range(NCH):
                                k.emit('pe', lambda e, ps=ps, w=w, c=c, sl=sl: e.matmul(
                                    ps[:], w[:, c, :], self.hT[:, c, sl], start=(c == 0), stop=(c == NCH - 1)), [wt, self.hT_t[c][tt]], [pst])
                            k.emit('act', lambda e, ps=ps, sl=sl, dst=dst, w1=w1: e.activation(out=dst[:, sl], in_=ps[:], func=AF.Copy, scale=w1),
                                   [pst, self.PT_t], [dst_t])
                        for tt in range(2):
                            ps, pst = pss[tt]
                            sl_ = min(seqlen, 512)
                            ns = 512 // sl_
                            pv = ps[:].rearrange("p (s n) -> p s n", s=ns)
                            av = dst[:, tt * 512:(tt + 1) * 512].rearrange("p (s n) -> p s n", s=ns)
                            k.emit('dve', lambda e, pv=pv, av=av, w0=w0, sl_=sl_: e.scalar_tensor_tensor(
                                out=av[:, :, 1:sl_], in0=pv[:, :, 0:sl_ - 1], scalar=w0, in1=av[:, :, 1:sl_], op0=ALU.mult, op1=ALU.add),
                                [pst, self.PT_t, dst_t], [dst_t])
                            k.emit('dve', lambda e, pv=pv, av=av, w2=w2, sl_=sl_: e.scalar_tensor_tensor(
                                out=av[:, :, 0:sl_ - 1], in0=pv[:, :, 1:sl_], scalar=w2, in1=av[:, :, 0:sl_ - 1], op0=ALU.mult, op1=ALU.add),
                                [pst, self.PT_t, dst_t], [dst_t])
                        if seqlen > 512:
                            p0, p0t = pss[0]
                            p1, p1t = pss[1]
                            k.emit('dve', lambda e, p0=p0, dst=dst, w0=w0: e.scalar_tensor_tensor(
                                out=dst[:, 512:513], in0=p0[:, 511:512], scalar=w0, in1=dst[:, 512:513], op0=ALU.mult, op1=ALU.add),
                                [p0t, self.PT_t, dst_t], [dst_t])
                            k.emit('dve', lambda e, p1=p1, dst=dst, w2=w2: e.scalar_tensor_tensor(
                                out=dst[:, 511:512], in0=p1[:, 0:1], scalar=w2, in1=dst[:, 511:512], op0=ALU.mult, op1=ALU.add),
                                [p1t, self.PT_t, dst_t], [dst_t])
                        k.emit('act', lambda e, dst=dst: e.activation(out=dst[:], in_=dst[:], func=AF.Silu), [dst_t], [dst_t])
                        if ti < 2:
                            sq, sq_t, rs, rs_t = ntm
                            for tt in range(2):
                                sl = slice(tt * 512, (tt + 1) * 512)
                                ps, pst = self.ps[5], self.ps_t[5]
                                k.emit('act', lambda e, dst=dst, sl=sl: e.activation(out=sq[:], in_=dst[:, sl], func=AF.Square), [dst_t], [sq_t])
                                k.emit('pe', lambda e, ps=ps: e.matmul(ps[:], self.onesR[:], sq[:], start=True, stop=True), [self.onesR_t, sq_t], [pst])
                                k.emit('act', lambda e, ps=ps: e.activation(out=rs[:], in_=ps[:], func=AF.Sqrt, bias=1e-6, scale=1.0), [pst], [rs_t])
                                k.emit('dve', lambda e: e.reciprocal(out=rs[:], in_=rs[:]), [rs_t], [rs_t])
                                sc_ = float(128.0 ** -0.5) if ti == 0 else 1.0
                                k.emit('dve', lambda e, dst=dst, sl=sl, sc_=sc_: e.scalar_tensor_tensor(
                                    out=dst[:, sl], in0=dst[:, sl], scalar=sc_, in1=rs[:], op0=ALU.mult, op1=ALU.mult), [dst_t, rs_t], [dst_t])
                    if stop <= 2:
                        continue
                    for di in range(2):
                        col = di * 8 + hh
                        for tt in range(2):
                            sl = slice(tt * 512, (tt + 1) * 512)
                            ps, pst = self.ps[6 + tt], self.ps_t[6 + tt]
                            k.emit('pe', lambda e, ps=ps, sl=sl, di=di: e.matmul(ps[:], SEL(di, 128), HR4[0:4, sl], start=True, stop=True),
                                   [HR4_t, self.CT_t], [pst])
                            k.emit('act', lambda e, ps=ps, sl=sl: e.activation(out=Qt[:, sl], in_=ps[:], func=AF.Exp), [pst], [Qt_t])
                        pos = 63 if di == 0 else 0
                        k.emit('pool', lambda e, pos=pos: e.tensor_copy(out=DKg[:], in_=Qt[:].rearrange("p (j n) -> p j n", n=64)[:, :, pos]), [Qt_t], [DKg_t])
                        k.emit('dve', lambda e: e.tensor_tensor(out=Qt[:], in0=Qt[:], in1=qn[:], op=ALU.mult), [Qt_t, qn_t], [Qt_t])
                        border = list(range(4)) if di == 0 else list(range(3, -1, -1))
                        if half == 1:
                            k.dma(Sb[0][:], d['st_gdn'][di, hh, :, :], writes=[Sb_t[0]])
                        si = 0
                        for bi in border:
                            chs = [bi * 4 + q for q in range(4)]
                            if di == 1:
                                chs = chs[::-1]
                            T0 = bi * 256
                            bsl = slice(T0, T0 + 256)
                            gcol = GC[:, bi * 4:bi * 4 + 4, col:col + 1].broadcast_to([64, 4, 64])
                            nbcol = NBE[:, bi * 4:bi * 4 + 4, col:col + 1].broadcast_to([64, 4, 64])
                            b0, b0t = self.ps[0], self.ps_t[0]
                            b1, b1t = self.ps[1], self.ps_t[1]
                            b2, b2t = self.ps[2], self.ps_t[2]
                            b3, b3t = self.ps[3], self.ps_t[3]
                            R = lambda ps: ps[0:64, 0:256]
                            R3 = lambda ps: ps[0:64, 0:256].rearrange("p (a n) -> p a n", a=4)
                            F2 = lambda t: t[:].rearrange("p a n -> p (a n)")
                            deps_c = [HR4_t, self.CT_t]
                            k.emit('pe', lambda e, di=di: e.matmul(R(b0), SEL(di, 64), HR4[0:4, bsl], start=True, stop=True), deps_c, [b0t])
                            k.emit('pe', lambda e, di=di: e.matmul(R(b2), SEL(2 + di, 64), HR4[0:4, bsl], start=True, stop=True), deps_c, [b2t])
                            DT, DT_t = bt[0], bt_t[0]
                            Dl, Dl_t = bt[1], bt_t[1]
                            DBT, DBT_t = bt[2], bt_t[2]
                            k.emit('dve', lambda e: e.tensor_tensor(out=DT[:], in0=R3(b0), in1=gcol, op=ALU.subtract), [b0t, TB_t], [DT_t])
                            k.emit('pool', lambda e, di=di: e.tensor_tensor(out=F2(Dl), in0=NEGL[di], in1=F2(DT), op=ALU.subtract), [DT_t, self.CT_t], [Dl_t])
                            k.emit('pool', lambda e, di=di: e.tensor_tensor(out=F2(DT), in0=F2(DT), in1=NEGU[di], op=ALU.add), [DT_t, self.CT_t], [DT_t])
                            k.emit('act', lambda e: e.activation(out=DT[:], in_=DT[:], func=AF.Exp), [DT_t], [DT_t])
                            k.emit('act', lambda e: e.activation(out=Dl[:], in_=Dl[:], func=AF.Exp), [Dl_t], [Dl_t])
                            k.emit('dve', lambda e, di=di: e.tensor_tensor(out=F2(DBT), in0=R(b2), in1=SNEG[di], op=ALU.mult), [b2t, self.CT_t], [DBT_t])
                            k.emit('pool', lambda e: e.tensor_tensor(out=DBT[:], in0=DBT[:], in1=DT[:], op=ALU.mult), [DBT_t, DT_t], [DBT_t])
                            k.emit('pool', lambda e: e.tensor_tensor(out=Dl[:], in0=Dl[:], in1=nbcol, op=ALU.mult), [Dl_t, TB_t], [Dl_t])
                            for q in range(4):
                                cs = slice(T0 + q * 64, T0 + (q + 1) * 64)
                                k.emit('pe', lambda e, q=q, cs=cs: e.matmul(b0[0:64, q * 64:(q + 1) * 64], kn[:, cs], kn[:, cs], start=True, stop=True), [kn_t], [b0t])
                                k.emit('pe', lambda e, q=q, cs=cs: e.matmul(b1[0:64, q * 64:(q + 1) * 64], kn[:, cs], qn[:, cs], start=True, stop=True), [kn_t, qn_t], [b1t])
                                k.emit('pe', lambda e, q=q, cs=cs: e.transpose(b2[0:64, q * 128:(q + 1) * 128], kn[:, cs], IDENT), [kn_t, self.CT_t], [b2t])
                                k.emit('pe', lambda e, q=q, cs=cs: e.transpose(b3[0:64, q * 128:(q + 1) * 128], vT[:, cs], IDENT), [vT_t, self.CT_t], [b3t])
                            NT, NT_t = bt[3], bt_t[3]
                            Nm, Nm_t = bt[4], bt_t[4]
                            QKT, QKT_t = bt[5], bt_t[5]
                            XT, XT_t = bt[6], bt_t[6]
                            k.emit('dve', lambda e: e.tensor_tensor(out=F2(NT), in0=R(b0), in1=F2(DBT), op=ALU.mult), [b0t, DBT_t], [NT_t])
                            k.emit('dve', lambda e: e.tensor_tensor(out=F2(Nm), in0=R(b0), in1=F2(Dl), op=ALU.mult), [b0t, Dl_t], [Nm_t])
                            k.emit('dve', lambda e: e.tensor_tensor(out=F2(QKT), in0=R(b1), in1=F2(DT), op=ALU.mult), [b1t, DT_t], [QKT_t])
                            k.emit('pool', lambda e: e.tensor_tensor(out=F2(XT), in0=F2(NT), in1=IDREP, op=ALU.add), [NT_t, self.CT_t], [XT_t])
                            k.emit('act', lambda e: e.activation(out=F2(ktok), in_=b2[0:64, :], func=AF.Copy), [b2t], [ktok_t])
                            k.emit('act', lambda e: e.activation(out=F2(vtok), in_=b3[0:64, :], func=AF.Copy), [b3t], [vtok_t])
                            P, P_t, PT_, PT_t = Nm, Nm_t, NT, NT_t
                            pp = [(bt[7], bt_t[7], bt[8], bt_t[8]), (bt[9], bt_t[9], bt[1], bt_t[1])]
                            XTs = [(bt[6], bt_t[6]), (bt[0], bt_t[0])]
                            xi = 0
                            for m in range(1, 6):
                                nP, nP_t, nPT, nPT_t = pp[(m - 1) % 2] if m > 1 else pp[0]
                                if m >= 3:
                                    nP, nP_t, nPT, nPT_t = pp[(m - 1) % 2]
                                if m == 2:
                                    nP, nP_t, nPT, nPT_t = pp[1]
                                for q in range(4):
                                    k.emit('pe', lambda e, q=q, P=P, PT_=PT_: e.matmul(b0[0:64, q * 64:(q + 1) * 64], PT_[:, q, :], P[:, q, :], start=True, stop=True),
                                           [P_t, PT_t], [b0t])
                                    if m < 5:
                                        k.emit('pe', lambda e, q=q, P=P, PT_=PT_: e.matmul(b1[0:64, q * 64:(q + 1) * 64], P[:, q, :], PT_[:, q, :], start=True, stop=True),
                                               [P_t, PT_t], [b1t])
                                k.emit('act', lambda e, nP=nP: e.activation(out=F2(nP), in_=R(b0), func=AF.Copy), [b0t], [nP_t])
                                if m < 5:
                                    k.emit('dve', lambda e, nPT=nPT: e.tensor_copy(out=F2(nPT), in_=R(b1)), [b1t], [nPT_t])
                                cX, cX_t = XTs[xi]
                                nX, nX_t = XTs[1 - xi]
                                for q in range(4):
                                    k.emit('pe', lambda e, q=q, nP=nP, cX=cX: e.matmul(b2[0:64, q * 64:(q + 1) * 64], nP[:, q, :], cX[:, q, :], start=True, stop=True),
                                           [nP_t, cX_t], [b2t])
                                k.emit('dve', lambda e, nX=nX, cX=cX: e.tensor_tensor(out=F2(nX), in0=R(b2), in1=F2(cX), op=ALU.add), [b2t, cX_t], [nX_t])
                                xi = 1 - xi
                                P, P_t, PT_, PT_t = nP, nP_t, nPT, nPT_t
                            XTf, XTf_t = XTs[xi]
                            if stop <= 3:
                                continue
                            psO, psO_t = self.ps[6], self.ps_t[6]
                            for ch in chs:
                                q = ch - bi * 4
                                cs = slice(ch * 64, (ch + 1) * 64)
                                loc = ch % cps
                                first = (loc == 0) if di == 0 else (loc == cps - 1)
                                last = (loc == cps - 1) if di == 0 else (loc == 0)
                                seq = ch // cps
                                zero_init = first and half == 0
                                S_prev, S_prev_t = Sb[si], Sb_t[si]
                                tmpv, tmpv_t = sm[0], sm_t[0]
                                r_, r_t = sm[1], sm_t[1]
                                vn, vn_t = sm[2], sm_t[2]
                                vs, vs_t = sm[3], sm_t[3]
                                k.emit('act', lambda e, q=q, ch=ch: e.activation(out=tmpv[:], in_=vtok[:, q, :], func=AF.Copy, scale=BE[:, ch, col:col + 1]),
                                       [vtok_t, TB_t], [tmpv_t])
                                if zero_init:
                                    rr, rr_t = tmpv, tmpv_t
                                else:
                                    p4, p4t = self.ps[4], self.ps_t[4]
                                    k.emit('pe', lambda e, cs=cs, S_prev=S_prev: e.matmul(p4[0:64, 0:128], kn[:, cs], S_prev[:], start=True, stop=True),
                                           [kn_t, S_prev_t], [p4t])
                                    k.emit('dve', lambda e, ch=ch: e.scalar_tensor_tensor(out=r_[:], in0=p4[0:64, 0:128], scalar=C1[:, ch, col:col + 1], in1=tmpv[:],
                                                                                         op0=ALU.mult, op1=ALU.add), [p4t, TB_t, tmpv_t], [r_t])
                                    rr, rr_t = r_, r_t
                                lvl = self.cfg.get('seq_lvl', 9)
                                if lvl <= 1:
                                    continue
                                p5, p5t = self.ps[5], self.ps_t[5]
                                k.emit('pe', lambda e, q=q, rr=rr: e.matmul(p5[0:64, 0:128], XTf[:, q, :], rr[:], start=True, stop=True), [XTf_t, rr_t], [p5t])
                                if lvl <= 1.5:
                                    continue
                                k.emit('act', lambda e: e.activation(out=vn[:], in_=p5[0:64, 0:128], func=AF.Copy), [p5t], [vn_t])
                                if lvl <= 1.7:
                                    continue
                                k.emit('dve', lambda e, ch=ch: e.tensor_scalar(out=vs[:], in0=p5[0:64, 0:128], scalar1=WW[:, ch, col:col + 1], scalar2=None, op0=ALU.mult),
                                       [p5t, TB_t], [vs_t])
                                if lvl <= 2:
                                    continue
                                k.emit('pe', lambda e, q=q, zero_init=zero_init: e.matmul(psO[:, q * 64:(q + 1) * 64], vn[:], QKT[:, q, :], start=True, stop=zero_init),
                                       [vn_t, QKT_t], [psO_t])
                                if not zero_init:
                                    k.emit('pe', lambda e, q=q, cs=cs, S_prev=S_prev: e.matmul(psO[:, q * 64:(q + 1) * 64], S_prev[:], Qt[:, cs], start=False, stop=True),
                                           [S_prev_t, Qt_t], [psO_t])
                                if lvl <= 3:
                                    continue
                                p7, p7t = self.ps[7], self.ps_t[7]
                                k.emit('pe', lambda e, q=q: e.matmul(p7[:, 0:128], ktok[:, q, :], vs[:], start=True, stop=True), [ktok_t, vs_t], [p7t])
                                if last and half == 0:
                                    dstS, dstS_t = sout[:, seq, di, :], sout_t
                                else:
                                    si = 1 - si
                                    dstS, dstS_t = Sb[si][:], Sb_t[si]
                                if zero_init:
                                    k.emit('dve', lambda e, dstS=dstS: e.tensor_copy(out=dstS, in_=p7[:, 0:128]), [p7t], [dstS_t])
                                else:
                                    k.emit('dve', lambda e, dstS=dstS, S_prev=S_prev, ch=ch: e.scalar_tensor_tensor(
                                        out=dstS, in0=S_prev[:], scalar=DKg[:, ch:ch + 1], in1=p7[:, 0:128], op0=ALU.mult, op1=ALU.add),
                                        [p7t, S_prev_t, DKg_t], [dstS_t])
                            if di == 0:
                                k.emit('act', lambda e, bsl=bsl: e.activation(out=oT[:, bsl], in_=psO[:, 0:256], func=AF.Copy), [psO_t], [oT_t])
                            else:
                                k.emit('dve', lambda e, bsl=bsl: e.tensor_tensor(out=oT[:, bsl], in0=oT[:, bsl], in1=psO[:, 0:256], op=ALU.add), [psO_t, oT_t], [oT_t])
                    if stop <= 4:
                        continue
                    if half == 0:
                        k.dma(d['gdout'][:, :, hh, :, :].rearrange("s d k e -> k s d e"), sout[:], reads=[sout_t])
                    (wg,), wt3 = self.wpiece([w_in[0, :, 3 * D + hh * 128:3 * D + (hh + 1) * 128]])
                    for tt in range(2):
                        sl = slice(tt * 512, (tt + 1) * 512)
                        ps, pst = self.ps[6 + tt], self.ps_t[6 + tt]
                        for c in range(NCH):
                            k.emit('pe', lambda e, ps=ps, c=c, sl=sl: e.matmul(ps[:], wg[:, c, :], self.hT[:, c, sl], start=(c == 0), stop=(c == NCH - 1)),
                                   [wt3, self.hT_t[c][tt]], [pst])
                        k.emit('act', lambda e, ps=ps, sl=sl: e.activation(out=vT[:, sl], in_=ps[:], func=AF.Silu), [pst], [vT_t])
                    self.HV_t = GV_t
                    self.head_norm(ntm, oT[:], oT_t, mix[:, 0, :], mix_t[0], GV[:, 1:2], 128.0 * 1e-6, 5, extra_mul=vT[:], extra_t=vT_t)
                    self.out_proj(d['gdn_w_out'][0, hh * 128:(hh + 1) * 128, :], mix, mix_t, 1, half, [6, 7])
                k.barrier()

    def ffn(self, layer, half):
        k, d, nc = self.k, self.d, self.nc
        seqlen = 256 if half == 0 else 1024
        oc, _ = PC['ffn_conv']
        ob, _ = PC['ffn_conv_b']

        def cw(tap, fchunk):
            col = oc + (layer * 3 + tap) * 44 + fchunk
            return self.PT[:, col:col + 1]

        def cb(fchunk):
            col = ob + layer * 44 + fchunk
            return self.PT[:, col:col + 1]

        groups = [(0, 8), (8, 16), (16, 22)]
        specs = []
        pidx = {}
        for (g0, g1) in groups:
            for j in range(g0, g1):
                pidx[('u', j)] = len(specs)
                specs.append([d['ffn_w_up'][layer, :, j * 128:(j + 1) * 128], d['ffn_w_up'][layer, :, D_FF + j * 128:D_FF + (j + 1) * 128]])
            for dp in range(4):
                pidx[('d', g0, dp)] = len(specs)
                specs.append([d['ffn_w_down'][layer, g0 * 128:g1 * 128, dp * 256:(dp + 1) * 256]])
        pf = PF(self, specs)
        PAIRS = [(0, 1), (2, 3), (4, 5)]
        u = 0
        v = 0
        with ExitStack() as es:
            aT = self.tmp(es, 'aT', [128, 8, TOK], F32R)
            aT_t = trks('aT', 8)
            acc = [self.tmp(es, f'facc{i}', [128, 2, TOK], F32) for i in range(2)]
            acc_t = trks('facc', 2, 2, 2)
            for (g0, g1) in groups:
                for j in range(g0, g1):
                    slot = j - g0
                    (wv, wg), wt = pf.get(pidx[('u', j)])
                    ai = j % 2
                    ac = acc[ai]
                    banks = {}
                    for tt in range(2):
                        pair = PAIRS[u % 3]
                        u += 1
                        sl = slice(tt * 512, (tt + 1) * 512)
                        for vi, (w, fch) in enumerate(((wv, j), (wg, NFF + j))):
                            ps, pst = self.ps[pair[vi]], self.ps_t[pair[vi]]
                            banks[(vi, tt)] = (ps, pst)
                            for c in range(NCH):
                                k.emit('pe', lambda e, ps=ps, w=w, c=c, sl=sl: e.matmul(
                                    ps[:], w[:, c, :], self.hT[:, c, sl],
                                    start=(c == 0), stop=(c == NCH - 1)), [wt, self.hT_t[c][tt]], [pst])
                        for vi, fch in ((0, j), (1, NFF + j)):
                            ps, pst = banks[(vi, tt)]
                            at_ = acc_t[ai][vi][tt]
                            k.emit('act', lambda e, ps=ps, sl=sl, fch=fch, vi=vi: e.activation(
                                out=ac[:, vi, sl], in_=ps[:], func=AF.Identity, bias=cb(fch), scale=cw(1, fch)), [pst, self.PT_t], [at_])
                            sl_ = min(seqlen, 512)
                            ns = 512 // sl_
                            pv = ps[:].rearrange("p (s n) -> p s n", s=ns)
                            av = ac[:, vi, sl].rearrange("p (s n) -> p s n", s=ns)
                            k.emit('dve', lambda e, pv=pv, av=av, fch=fch, sl_=sl_: e.scalar_tensor_tensor(
                                out=av[:, :, 1:sl_], in0=pv[:, :, 0:sl_ - 1], scalar=cw(0, fch), in1=av[:, :, 1:sl_],
                                op0=ALU.mult, op1=ALU.add), [pst, self.PT_t, at_], [at_])
                            k.emit('dve', lambda e, pv=pv, av=av, fch=fch, sl_=sl_: e.scalar_tensor_tensor(
                                out=av[:, :, 0:sl_ - 1], in0=pv[:, :, 1:sl_], scalar=cw(2, fch), in1=av[:, :, 0:sl_ - 1],
                                op0=ALU.mult, op1=ALU.add), [pst, self.PT_t, at_], [at_])
                    if seqlen > 512:
                        for vi, fch in ((0, j), (1, NFF + j)):
                            p0, p0t = banks[(vi, 0)]
                            p1, p1t = banks[(vi, 1)]
                            k.emit('dve', lambda e, p0=p0, fch=fch, vi=vi: e.scalar_tensor_tensor(
                                out=ac[:, vi, 512:513], in0=p0[:, 511:512], scalar=cw(0, fch), in1=ac[:, vi, 512:513],
                                op0=ALU.mult, op1=ALU.add), [p0t, self.PT_t, acc_t[ai][vi][1]], [acc_t[ai][vi][1]])
                            k.emit('dve', lambda e, p1=p1, fch=fch, vi=vi: e.scalar_tensor_tensor(
                                out=ac[:, vi, 511:512], in0=p1[:, 0:1], scalar=cw(2, fch), in1=ac[:, vi, 511:512],
                                op0=ALU.mult, op1=ALU.add), [p1t, self.PT_t, acc_t[ai][vi][0]], [acc_t[ai][vi][0]])
                    k.emit('act', lambda e, ac=ac: e.activation(out=ac[:, 1, :], in_=ac[:, 1, :], func=AF.Silu), acc_t[ai][1], acc_t[ai][1])
                    k.emit('pool', lambda e, slot=slot, ac=ac: e.tensor_tensor(out=aT[:, slot, :], in0=ac[:, 0, :], in1=ac[:, 1, :], op=ALU.mult),
                           acc_t[ai], [aT_t[slot]])
                    if self.modgen is not None:
                        next(self.modgen, None)
                ng = g1 - g0
                for dp in range(4):
                    (wd,), wdt = pf.get(pidx[('d', g0, dp)])
                    for dmi in range(2):
                        dm = dp * 2 + dmi
                        for tt in range(2):
                            b = v % 6
                            v += 1
                            ps, pst = self.ps[b], self.ps_t[b]
                            for jj in range(ng):
                                k.emit('pe', lambda e, ps=ps, jj=jj, dmi=dmi, tt=tt: e.matmul(
                                    ps[:], wd[:, jj, dmi * 128:(dmi + 1) * 128], aT[:, jj, tt * 512:(tt + 1) * 512],
                                    start=(jj == 0), stop=(jj == ng - 1)), [wdt, aT_t[jj]], [pst])
                            xs = self.xT[half][:, dm, tt * 512:(tt + 1) * 512]
                            k.emit('dve', lambda e, ps=ps, xs=xs, dm=dm: e.scalar_tensor_tensor(
                                out=xs, in0=ps[:], scalar=self.gate(1, dm, half), in1=xs, op0=ALU.mult, op1=ALU.add),
                                [pst, self.MOD_t, self.xT_t[half][dm][tt]], [self.xT_t[half][dm][tt]])
            k.barrier()

    def final(self, cfg):
        k, d, nc = self.k, self.d, self.nc
        og, _ = PC['final_g']
        with ExitStack() as es:
            sq = self.tmp(es, 'fsq', [128, NCH, 512], F32R)
            sq_t = Trk('fsq')
            rstd = self.tmp(es, 'frstd', [128, 512], F32)
            rstd_t = Trk('frstd')
            yo = self.tmp(es, 'fyo', [128, NCH, 512], F32)
            yo_t = self.ptrk('fyo', NCH)
            for half in range(2):
                for tt in range(2):
                    xs = self.xT[half][:, :, tt * 512:(tt + 1) * 512]
                    xs_t = [self.xT_t[half][c][tt] for c in range(NCH)]
                    k.emit('act', lambda e: e.activation(out=sq[:], in_=xs, func=AF.Square), xs_t, [sq_t])
                    ps, pst = self.ps[6], self.ps_t[6]
                    for c in range(NCH):
                        k.emit('pe', lambda e, c=c: e.matmul(ps[:], self.onesR[:], sq[:, c, :], start=(c == 0), stop=(c == NCH - 1)),
                               [self.onesR_t, sq_t], [pst])
                    k.emit('act', lambda e: e.activation(out=rstd[:], in_=ps[:], func=AF.Sqrt, bias=float(D * EPS), scale=1.0),
                           [pst], [rstd_t])
                    k.emit('dve', lambda e: e.reciprocal(out=rstd[:], in_=rstd[:]), [rstd_t], [rstd_t])
                    for c in range(NCH):
                        g = self.PT[:, og + c:og + c + 1]
                        k.emit('dve', lambda e, c=c, g=g: e.scalar_tensor_tensor(
                            out=yo[:, c, :], in0=self.xT[half][:, c, tt * 512:(tt + 1) * 512], scalar=g, in1=rstd[:],
                            op0=ALU.mult, op1=ALU.mult), [self.xT_t[half][c][tt], self.PT_t, rstd_t], [yo_t[c]])
                        k.emit('act', lambda e, c=c: e.activation(out=yo[:, c, :], in_=yo[:, c, :], func=AF.Copy, scale=32.0),
                               [yo_t[c]], [yo_t[c]])
                        k.dma(d['yout'][half, c, :, tt * 512:(tt + 1) * 512], yo[:, c, :], reads=[yo_t[c]])


_CACHE = {}


def _get_prog(cfg):
    key = repr(sorted(cfg.items()))
    if key not in _CACHE:
        p = Prog(dict(cfg))
        with p.es:
            p.declare()
            p.build()
        _CACHE[key] = p
    return _CACHE[key]


def _pack(plan, arrays, n):
    out = np.zeros((128, n), np.float32)
    for c0, key in plan:
        off = c0
        for (name, offset, rstride, rows, ncols) in key:
            flat = arrays[name].reshape(-1)
            a2 = np.lib.stride_tricks.as_strided(flat[offset:], shape=(rows, ncols), strides=(rstride * 4, 4))
            kc = rows // 128
            assert off + kc * ncols <= n
            out[:, off:off + kc * ncols] = a2.reshape(kc, 128, ncols).transpose(1, 0, 2).reshape(128, kc * ncols)
            off += kc * ncols
    return out


def _run(inp, cfg):
    p = _get_prog(cfg)
    consts, rope_tab = _build_consts()
    f32 = lambda a: np.ascontiguousarray(np.asarray(a, np.float32))
    warr = {n: f32(inp[n]) for n in ('w_mod', 'ffn_w_up', 'ffn_w_down', 'attn_w_in', 'attn_w_out',
                                     'hgrn_w_in', 'hgrn_w_out', 'gdn_w_in', 'gdn_w_out')}
    shared = {'wpk': _pack(p.wplan['wpk'], warr, WCOLS)}
    xp = f32(inp['x_prompt'])
    xs = f32(inp['x_sample'])
    ck = f32(inp['cache_attn_k'])
    cv = f32(inp['cache_attn_v'])
    in_maps = []
    for core in range(N_CORES):
        m = dict(shared)
        a = xp[4 * core:4 * core + 4].reshape(TOK, D).T.reshape(8, 128, TOK)
        b = xs[core].T.reshape(8, 128, TOK)
        m['xin'] = np.ascontiguousarray(np.stack([a, b], axis=0))
        m['params'] = _build_params(core, inp)
        m['consts'] = consts
        m['rope'] = rope_tab
        m['lamtab'] = np.ascontiguousarray(np.broadcast_to(np.asarray(inp['attn_lambda'], np.float32).reshape(1, 512), (128, 512)))
        carr = {'ck': np.ascontiguousarray(ck[core].transpose(0, 2, 3, 1)),
                'cv': np.ascontiguousarray(cv[core].reshape(2, 512, D))}
        m['cpk'] = _pack(p.wplan['cpk'], carr, CCOLS)
        m['st_hgrn'] = f32(inp['state_hgrn'][core, 0])
        m['st_gdn'] = f32(inp['state_gdn'][core, 0])
        in_maps.append(m)
    ncores = cfg.get('ncores', N_CORES)
    res = run_bass_kernel_spmd(p.nc, in_maps[:ncores], core_ids=list(range(ncores)))
    R = list(res.results) + [res.results[0]] * (N_CORES - ncores)
    y_prompt = np.empty((32, 256, D), np.float32)
    y_sample = np.empty((8, 1024, D), np.float32)
    new_k = np.empty((32, 2, 256, 8, 128), np.float32)
    new_v = np.empty((32, 2, 256, 8, 128), np.float32)
    new_h = np.empty((32, 1, 2, 8, 128, 128), np.float32)
    new_g = np.empty((32, 1, 2, 8, 128, 128), np.float32)
    for core in range(N_CORES):
        r = R[core]
        yo = r['yout']
        y_prompt[4 * core:4 * core + 4] = yo[0].reshape(D, TOK).T.reshape(4, 256, D)
        y_sample[core] = yo[1].reshape(D, TOK).T
        ko = r['kout']
        new_k[4 * core:4 * core + 4] = ko.reshape(2, 8, 128, 4, 256).transpose(3, 0, 4, 1, 2)
        vo = r['vout']
        new_v[4 * core:4 * core + 4] = vo.reshape(2, 4, 256, 8, 128).transpose(1, 0, 2, 3, 4)
        new_h[4 * core:4 * core + 4, 0] = r['hgout']
        new_g[4 * core:4 * core + 4, 0] = r['gdout']
    return (y_prompt, y_sample, new_k, new_v, new_h, new_g)


def kernel(**inputs):
    return _run(inputs, {})
```

```python
import math
from contextlib import ExitStack

import numpy as np
import concourse.bass as bass
import concourse.mybir as mybir
from concourse.bass_utils import run_bass_kernel_spmd

F32 = mybir.dt.float32
F32R = mybir.dt.float32r
AF = mybir.ActivationFunctionType
ALU = mybir.AluOpType
AX = mybir.AxisListType

D = 1024
NCH = 8
TOK = 1024
DEPTH = 4
D_FF = 2816
NFF = 22
EPS = 1e-6
N_CORES = 8
WCOLS = (4 * 1024 * 6144 + 4 * 1024 * 5632 + 4 * 2816 * 1024 + 2 * 1024 * 3072 + 2 * 1024 * 1024 + 1024 * 5120 + 1024 * 1024
         + 1024 * 4128 + 1024 * 1024) // 128
CCOLS = (2 * 8 * 128 * 512 + 2 * 512 * 1024) // 128


class Cols:
    def __init__(self):
        self.off = {}
        self.n = 0

    def add(self, name, w):
        self.off[name] = (self.n, w)
        self.n += w

    def __getitem__(self, name):
        return self.off[name]


def _param_cols():
    c = Cols()
    c.add('cond', 16)
    c.add('norm_g', 64)
    c.add('b_mod', 192)
    c.add('final_g', 8)
    c.add('subln', 2)
    c.add('hgrn_lb', 32)
    c.add('hgrn_norm', 1)
    c.add('gdn_norm', 1)
    c.add('gdn_conv', 72)
    c.add('ffn_conv', 528)
    c.add('ffn_conv_b', 176)
    c.add('gdn_alog_col', 1)
    c.add('gdn_dt_col', 1)
    c.add('gdn_alog_row', 16)
    c.add('gdn_dt_row', 16)
    return c


PC = _param_cols()


def _fm(v):
    v = np.asarray(v, np.float32)
    r = v.reshape(-1, 128)
    return np.ascontiguousarray(r.T)


def _build_params(core, inp):
    P = np.zeros((128, PC.n), np.float32)

    def put(name, arr):
        o, w = PC[name]
        assert arr.shape == (128, w), (name, arr.shape, w)
        P[:, o:o + w] = arr

    cond = np.stack([inp['c_ctx'], inp['c'][core]], axis=0)
    put('cond', np.ascontiguousarray(cond.reshape(2, 8, 128).transpose(2, 1, 0)).reshape(128, 16))
    put('norm_g', _fm(inp['norm_g']))
    put('b_mod', _fm(inp['b_mod']))
    put('final_g', _fm(inp['final_g']))
    put('subln', _fm(inp['attn_subln']))
    put('hgrn_lb', _fm(inp['hgrn_lb']))
    put('hgrn_norm', _fm(inp['hgrn_norm']))
    put('gdn_norm', _fm(inp['gdn_norm']))
    put('gdn_conv', _fm(inp['gdn_conv']))
    put('ffn_conv', _fm(inp['ffn_conv']))
    put('ffn_conv_b', _fm(inp['ffn_conv_b']))
    al = np.zeros((128, 1), np.float32)
    al[:16, 0] = np.asarray(inp['gdn_a_log'], np.float32).reshape(16)
    put('gdn_alog_col', al)
    dtb = np.zeros((128, 1), np.float32)
    dtb[:16, 0] = np.asarray(inp['gdn_dt_bias'], np.float32).reshape(16)
    put('gdn_dt_col', dtb)
    put('gdn_alog_row', np.broadcast_to(np.asarray(inp['gdn_a_log'], np.float32).reshape(1, 16), (128, 16)))
    put('gdn_dt_row', np.broadcast_to(np.asarray(inp['gdn_dt_bias'], np.float32).reshape(1, 16), (128, 16)))
    return P


def _const_cols():
    c = Cols()
    c.add('ident', 128)
    c.add('ones', 128)
    c.add('perm', 128)
    c.add('mask_f', 256)
    c.add('mask_b', 256)
    c.add('negu_f', 256)
    c.add('negl_f', 256)
    c.add('negu_b', 256)
    c.add('negl_b', 256)
    c.add('sneg_f', 256)
    c.add('sneg_b', 256)
    c.add('idrep', 256)
    c.add('sel', 512)
    c.add('nsel', 128)
    return c


CC = _const_cols()


def _build_consts():
    C = np.zeros((128, CC.n), np.float32)
    o, w = CC['ident']
    C[:, o:o + w] = np.eye(128, dtype=np.float32)
    o, w = CC['ones']
    C[:, o:o + w] = 1.0
    o, w = CC['perm']
    tok = np.arange(1024)
    row = (tok // 64).astype(np.float32)
    col = (tok % 64).astype(np.float32)
    inv = (np.float32(10000.0) ** (-np.arange(16, dtype=np.float32) / np.float32(16))).astype(np.float32)
    ROPE = np.zeros((128, 2048), np.float32)
    oc_, os_ = 0, 1024
    for p in range(128):
        dd = p % 64
        i = dd % 32
        partner = p + 16 if i < 16 else p - 16
        C[partner, o + p] = 1.0
        f = i % 16
        pos = row if dd < 32 else col
        ang = (pos * inv[f]).astype(np.float32)
        ROPE[p, oc_:oc_ + 1024] = np.cos(ang)
        ROPE[p, os_:os_ + 1024] = np.sin(ang) * (-1.0 if i < 16 else 1.0)
    sidx = np.arange(64)[:, None]
    tidx = np.arange(64)[None, :]
    om, _ = CC['mask_f']
    C[:64, om:om + 256] = np.tile((sidx <= tidx).astype(np.float32), (1, 4))
    om, _ = CC['mask_b']
    C[:64, om:om + 256] = np.tile((sidx >= tidx).astype(np.float32), (1, 4))
    BIG = 30000.0
    p_, j_ = sidx, tidx

    def putm(name, m):
        o_, _ = CC[name]
        C[:64, o_:o_ + 256] = np.tile(m.astype(np.float32), (1, 4))

    putm('negu_f', np.where(j_ >= p_, 0.0, -BIG))
    putm('negl_f', np.where(j_ < p_, 0.0, -BIG))
    putm('negu_b', np.where(j_ <= p_, 0.0, -BIG))
    putm('negl_b', np.where(j_ > p_, 0.0, -BIG))
    putm('sneg_f', np.where(j_ > p_, -1.0, 0.0))
    putm('sneg_b', np.where(j_ < p_, -1.0, 0.0))
    putm('idrep', (j_ == p_))
    o_, _ = CC['sel']
    for kk in range(4):
        C[kk, o_ + kk * 128:o_ + (kk + 1) * 128] = 1.0
    o_, _ = CC['nsel']
    for kk in range(2):
        C[kk, o_ + kk * 64:o_ + (kk + 1) * 64] = -1.0
    return C, ROPE


class _GT:
    def __init__(self, name):
        self.name = name


class Geo:
    def __init__(self, name, shape, offset=0, pat=None):
        self.tensor = _GT(name)
        if pat is None:
            pat = []
            st = 1
            for n in reversed(shape):
                pat.insert(0, (st, n))
                st *= n
        self.ap = tuple(pat)
        self.offset = offset

    @property
    def shape(self):
        return tuple(n for _, n in self.ap)

    def __getitem__(self, key):
        if not isinstance(key, tuple):
            key = (key,)
        key = key + (slice(None),) * (len(self.ap) - len(key))
        off = self.offset
        pat = []
        for (st, n), kk in zip(self.ap, key):
            if isinstance(kk, int):
                off += st * kk
            else:
                a, b, _ = kk.indices(n)
                off += st * a
                pat.append((st, b - a))
        return Geo(self.tensor.name, None, off, pat)


class Trk:
    __slots__ = ('name', 'w', 'rs', 'sem', 'cnt', 'psum')

    def __init__(self, name):
        self.name = name
        self.psum = False
        self.w = None
        self.rs = {}
        self.sem = None
        self.cnt = 0


def trks(name, *dims):
    if len(dims) == 1:
        return [Trk(f'{name}{i}') for i in range(dims[0])]
    return [trks(f'{name}{i}_', *dims[1:]) for i in range(dims[0])]


def flat(x):
    if isinstance(x, Trk):
        return [x]
    out = []
    for e in x:
        out.extend(flat(e))
    return out


class KB:
    def __init__(self, nc, es):
        self.nc = nc
        self.es = es
        self.E = {'pe': nc.tensor, 'act': nc.scalar, 'dve': nc.vector, 'pool': nc.gpsimd, 'sp': nc.sync}
        self.sem = {e: es.enter_context(nc.semaphore(f's_{e}')) for e in self.E}
        self.cnt = {e: 0 for e in self.E}
        self.waited = {e: {} for e in self.E}
        self.dma_sems = []
        self.n_ins = 0
        self.n_wait = 0

    def _wait(self, eng, deps):
        need = {}
        for key, val in deps:
            if key == 'pe' and eng == 'pe':
                continue
            if need.get(key, 0) < val:
                need[key] = val
        wt = self.waited[eng]
        for key, val in need.items():
            if wt.get(key, 0) >= val:
                continue
            sem = self.sem[key] if isinstance(key, str) else key
            self.E[eng].wait_ge(sem, val)
            self.n_wait += 1
            wt[key] = val

    def _deps(self, reads, writes):
        deps = []
        for t in reads:
            if t.w is not None:
                deps.append(t.w)
            if t.psum:
                deps.extend(t.rs.items())
        for t in writes:
            if t.w is not None:
                deps.append(t.w)
            deps.extend(t.rs.items())
        return deps

    def _mark(self, tok, reads, writes):
        k, v = tok
        for t in reads:
            if t.rs.get(k, 0) < v:
                t.rs[k] = v
        for t in writes:
            t.w = tok
            t.rs = {}

    def emit(self, eng, fn, reads=(), writes=()):
        reads = flat(reads)
        writes = flat(writes)
        self._wait(eng, self._deps(reads, writes))
        ins = fn(self.E[eng])
        self.cnt[eng] += 1
        ins.then_inc(self.sem[eng], 1)
        self.n_ins += 1
        tok = (eng, self.cnt[eng])
        self._mark(tok, reads, writes)
        return tok

    def dma(self, out, in_, reads=(), writes=(), q='sp'):
        reads = flat(reads)
        writes = flat(writes)
        self._wait(q, self._deps(reads, writes))
        owner = (writes + reads)[0]
        if owner.sem is None:
            owner.sem = self.es.enter_context(self.nc.semaphore(f'd_{owner.name}'))
            self.dma_sems.append(owner)
        self.E[q].dma_start(out=out, in_=in_).then_inc(owner.sem, 16)
        owner.cnt += 16
        self.n_ins += 1
        tok = (owner.sem, owner.cnt)
        self._mark(tok, reads, writes)
        return tok

    def dma_group(self, pairs, writes, q='sp'):
        writes = flat(writes)
        self._wait(q, self._deps([], writes))
        owner = writes[0]
        if owner.sem is None:
            owner.sem = self.es.enter_context(self.nc.semaphore(f'd_{owner.name}'))
            self.dma_sems.append(owner)
        for out, in_ in pairs:
            self.E[q].dma_start(out=out, in_=in_).then_inc(owner.sem, 16)
            owner.cnt += 16
            self.n_ins += 1
        tok = (owner.sem, owner.cnt)
        self._mark(tok, [], writes)
        return tok

    def barrier(self):
        for e in self.E:
            deps = [(o, self.cnt[o]) for o in self.E if o != e and self.cnt[o] > 0]
            deps += [(t.sem, t.cnt) for t in self.dma_sems]
            wt = self.waited[e]
            for key, val in deps:
                if wt.get(key, 0) >= val:
                    continue
                sem = self.sem[key] if isinstance(key, str) else key
                self.E[e].wait_ge(sem, val)
                self.n_wait += 1
                wt[key] = val

    def finish(self):
        deps = [(t.sem, t.cnt) for t in self.dma_sems]
        deps += [(o, self.cnt[o]) for o in self.E if o != 'sp' and self.cnt[o] > 0]
        self._wait('sp', deps)

    def sb(self, name, shape, dt=F32):
        return self.es.enter_context(self.nc.sbuf_tensor(name, list(shape), dt))


class PF:
    def __init__(self, prog, specs):
        self.p, self.specs, self.h = prog, specs, {}

    def get(self, i):
        for t in (i, i + 1):
            if t < len(self.specs) and t not in self.h:
                self.h[t] = self.p.wpiece(self.specs[t])
        return self.h.pop(i)


class Prog:
    def __init__(self, cfg):
        self.cfg = cfg
        nc = bass.Bass("TRN2", target_bir_lowering=False)
        self.nc = nc
        self.es = ExitStack()
        self.k = KB(nc, self.es)
        self.rr = 0
        self.wplan = {'wpk': [], 'cpk': []}
        self.wcols = {'wpk': 0, 'cpk': 0}
        self.wkeys = {}

    def declare(self):
        nc = self.nc

        def din(name, shape):
            return nc.dram_tensor(name, list(shape), F32, kind="ExternalInput").ap()

        def dout(name, shape):
            return nc.dram_tensor(name, list(shape), F32, kind="ExternalOutput").ap()

        d = {}
        d['xin'] = din('xin', [2, 8, 128, TOK])
        d['params'] = din('params', [128, PC.n])
        d['consts'] = din('consts', [128, CC.n])
        d['rope'] = din('rope', [128, 2048])
        d['wpk'] = din('wpk', [128, WCOLS])
        d['cpk'] = din('cpk', [128, CCOLS])
        d['lamtab'] = din('lamtab', [128, 512])
        d['gscr'] = nc.dram_tensor('gscr', [48, TOK], F32, kind="Internal").ap()
        d['w_mod'] = Geo('w_mod', [4, D, 6 * D])
        d['ffn_w_up'] = Geo('ffn_w_up', [4, D, 2 * D_FF])
        d['ffn_w_down'] = Geo('ffn_w_down', [4, D_FF, D])
        d['attn_w_in'] = Geo('attn_w_in', [2, D, 3 * D])
        d['attn_w_out'] = Geo('attn_w_out', [2, D, D])
        d['hgrn_w_in'] = Geo('hgrn_w_in', [1, D, 5 * D])
        d['hgrn_w_out'] = Geo('hgrn_w_out', [1, D, D])
        d['gdn_w_in'] = Geo('gdn_w_in', [1, D, 4 * D + 32])
        d['gdn_w_out'] = Geo('gdn_w_out', [1, D, D])
        d['ck'] = Geo('ck', [2, 8, 128, 512])
        d['cv'] = Geo('cv', [2, 512, D])
        d['st_hgrn'] = din('st_hgrn', [2, 8, 128, 128])
        d['st_gdn'] = din('st_gdn', [2, 8, 128, 128])
        d['yout'] = dout('yout', [2, 8, 128, TOK])
        d['kout'] = dout('kout', [2, 8, 128, TOK])
        d['vout'] = dout('vout', [2, TOK, D])
        d['hgout'] = dout('hgout', [4, 2, 8, 128, 128])
        d['gdout'] = dout('gdout', [4, 2, 8, 128, 128])
        self.d = d

    def ptrk(self, name, n=None):
        if not hasattr(self, '_pt'):
            self._pt = {}
        if name not in self._pt:
            self._pt[name] = Trk(name) if n is None else trks(name, n)
        return self._pt[name]

    def tmp(self, es, name, shape, dt=F32):
        self._uid = getattr(self, '_uid', 0) + 1
        return es.enter_context(self.nc.sbuf_tensor(f'{name}_{self._uid}', list(shape), dt))

    def evac_eng(self):
        self.rr += 1
        return 'act' if self.rr % 2 else 'dve'

    def copy(self, eng, out, in_, reads, writes):
        if eng == 'act':
            return self.k.emit('act', lambda e: e.activation(out=out, in_=in_, func=AF.Copy), reads, writes)
        return self.k.emit(eng, lambda e: e.tensor_copy(out=out, in_=in_), reads, writes)

    def init_wpool(self):
        k = self.k
        self.ws = [k.sb(f'ws{i}', [128, 2048], F32) for i in range(2)]
        self.ws_t = trks('ws', 2)
        self.wr = [k.sb(f'wr{i}', [128, 2048], F32R) for i in range(2)]
        self.wr_t = trks('wr', 2)
        self.ws_i = 0
        self.wr_i = 0

    def wpiece(self, segs, rounded=True, dest=None):
        k = self.k
        si = self.ws_i
        self.ws_i = (si + 1) % 2
        st, stt = self.ws[si], self.ws_t[si]
        off = 0
        views = []
        key = []
        percore = False
        for ap in segs:
            rows, ncols = ap.shape
            kc = rows // 128
            pat = tuple((int(a), int(b)) for a, b in ap.ap)
            assert len(pat) == 2 and pat[1][0] == 1, pat
            name = ap.tensor.name
            percore = percore or name in ('ck', 'cv')
            key.append((name, int(ap.offset), pat[0][0], rows, ncols))
            views.append((off, kc, ncols))
            off += kc * ncols
        key = tuple(key)
        pk = 'cpk' if percore else 'wpk'
        if key not in self.wkeys:
            self.wkeys[key] = (pk, self.wcols[pk])
            self.wplan[pk].append((self.wcols[pk], key))
            self.wcols[pk] += off
        pk, c0 = self.wkeys[key]
        k.dma(st[:, 0:off], self.d[pk][:, c0:c0 + off], writes=[stt])
        if not rounded:
            outs = [st[:, o:o + kc * n].rearrange("p (c n) -> p c n", c=kc) for (o, kc, n) in views]
            return outs, stt
        if dest is not None:
            rt, rtt = dest
        else:
            ri = self.wr_i
            self.wr_i = (ri + 1) % 2
            rt, rtt = self.wr[ri], self.wr_t[ri]
        k.emit('act', lambda e: e.activation(out=rt[:, 0:off], in_=st[:, 0:off], func=AF.Copy), [stt], [rtt])
        outs = [rt[:, o:o + kc * n].rearrange("p (c n) -> p c n", c=kc) for (o, kc, n) in views]
        return outs, rtt

    def build(self):
        nc, k, d, cfg = self.nc, self.k, self.d, self.cfg
        self.ps = [self.es.enter_context(nc.psum_tensor(f'ps{i}', [128, 512], F32)) for i in range(8)]
        self.ps_t = trks('ps', 8)
        for t in self.ps_t:
            t.psum = True
        self.PT = k.sb('PT', [128, PC.n])
        self.PT_t = Trk('PT')
        self.CT = k.sb('CT', [128, CC.n])
        self.CT_t = Trk('CT')
        self.onesR = k.sb('onesR', [128, 128], F32R)
        self.onesR_t = Trk('onesR')
        self.identR = k.sb('identR', [128, 128], F32R)
        self.identR_t = Trk('identR')
        self.xT = [k.sb(f'xT{h}', [128, NCH, TOK]) for h in range(2)]
        self.xT_t = trks('xT', 2, NCH, 2)
        self.hT = k.sb('hT', [128, NCH, TOK], F32R)
        self.hT_t = trks('hT', NCH, 2)
        self.permR = k.sb('permR', [128, 128], F32R)
        self.permR_t = Trk('permR')
        self.AV = k.sb('AV', [128, 8])
        self.AV_t = Trk('AV')
        self.MODs = [k.sb(f'MOD{i}', [128, 48, 2]) for i in range(2)]
        self.MODs_t = trks('MOD', 2)
        self.ABs = [k.sb(f'AB{i}', [128, 2, 2, 8, 2]) for i in range(2)]
        self.ABs_t = trks('AB', 2)
        self.modgen = None
        self.sc = k.sb('sc', [128, 16])
        self.sc_t = Trk('sc')
        self.init_wpool()

        k.dma(self.PT[:], d['params'][:, :], writes=[self.PT_t])
        k.dma(self.CT[:], d['consts'][:, :], writes=[self.CT_t])
        for h in range(2):
            k.dma_group([(self.xT[h][:, c, :], d['xin'][h, c, :, :]) for c in range(NCH)], self.xT_t[h])
        o, w = CC['ones']
        k.emit('dve', lambda e: e.tensor_copy(out=self.onesR[:], in_=self.CT[:, o:o + w]), [self.CT_t], [self.onesR_t])
        o2, w2 = CC['ident']
        k.emit('dve', lambda e: e.tensor_copy(out=self.identR[:], in_=self.CT[:, o2:o2 + w2]), [self.CT_t], [self.identR_t])
        o3, w3 = CC['perm']
        k.emit('dve', lambda e: e.tensor_copy(out=self.permR[:], in_=self.CT[:, o3:o3 + w3]), [self.CT_t], [self.permR_t])
        oc, wc = PC['cond']
        k.emit('act', lambda e: e.activation(out=self.sc[:], in_=self.PT[:, oc:oc + wc], func=AF.Silu), [self.PT_t], [self.sc_t])

        nl = cfg.get('layers', DEPTH)
        for _ in self.modulation(0):
            pass
        for layer in range(nl):
            p = layer % 2
            self.MOD, self.MOD_t, self.AB, self.AB_t = self.MODs[p], self.MODs_t[p], self.ABs[p], self.ABs_t[p]
            for half in range(2):
                if cfg.get('mixers', True):
                    self.rmsnorm_mod(layer, 0, half)
                    self.mixer(layer, half)
                if half == 1 and layer + 1 < nl:
                    self.modgen = self.modulation(layer + 1)
                if cfg.get('ffn', True):
                    self.rmsnorm_mod(layer, 1, half)
                    self.ffn(layer, half)
                if self.modgen is not None:
                    for _ in self.modgen:
                        pass
                    self.modgen = None
        self.final(cfg)
        k.finish()

    def modulation(self, layer):
        k, d = self.k, self.d
        p = layer % 2
        MOD, MOD_t, AB, AB_t = self.MODs[p], self.MODs_t[p], self.ABs[p], self.ABs_t[p]
        ps, pst = self.ps[7], self.ps_t[7]
        scv = self.sc[:].rearrange("p (c j) -> p c j", j=2)
        for piece in range(24):
            (w,), wt = self.wpiece([d['w_mod'][layer, :, piece * 256:(piece + 1) * 256]], rounded=False)
            for q2 in range(2):
                q = piece * 2 + q2
                for c in range(NCH):
                    k.emit('pe', lambda e, c=c, q2=q2, q=q: e.matmul(
                        ps[:, q * 2:q * 2 + 2], w[:, c, q2 * 128:(q2 + 1) * 128], scv[:, c, :],
                        start=(c == 0), stop=(c == NCH - 1)), [wt, self.sc_t], [pst])
            yield piece
        ob, wb = PC['b_mod']
        bm = self.PT[:, ob + layer * 48: ob + layer * 48 + 48]
        k.emit('dve', lambda e: e.tensor_tensor(
            out=MOD[:], in0=ps[:, 0:96].rearrange("p (q j) -> p q j", j=2),
            in1=bm.unsqueeze(2).broadcast_to([128, 48, 2]), op=ALU.add), [pst, self.PT_t], [MOD_t])
        og, wg = PC['norm_g']
        for s in range(2):
            g = self.PT[:, og + (layer * 2 + s) * 8: og + (layer * 2 + s) * 8 + 8]
            sh = MOD[:, s * 24 + 0: s * 24 + 8, :]
            scl = MOD[:, s * 24 + 8: s * 24 + 16, :]
            A = AB[:, s, 0, :, :]
            B = AB[:, s, 1, :, :]
            k.emit('dve', lambda e, scl=scl, A=A: e.tensor_scalar(
                out=A, in0=scl, scalar1=1.0, scalar2=32.0, op0=ALU.add, op1=ALU.mult), [MOD_t], [AB_t])
            k.emit('dve', lambda e, A=A, g=g: e.tensor_tensor(
                out=A, in0=A, in1=g.unsqueeze(2).broadcast_to([128, 8, 2]), op=ALU.mult), [AB_t, self.PT_t], [AB_t])
            k.emit('dve', lambda e, B=B, sh=sh: e.tensor_copy(out=B, in_=sh), [MOD_t], [AB_t])

    def gate(self, s, c, half):
        return self.MOD[:, s * 24 + 16 + c, half:half + 1]

    def rmsnorm_mod(self, layer, s, half):
        k = self.k
        with ExitStack() as es:
            sq = self.tmp(es, 'nsq', [128, NCH, 512], F32R)
            sq_t = Trk('nsq')
            tmp = self.tmp(es, 'ntmp', [128, NCH, 512], F32)
            tmp_t = Trk('ntmp')
            rstd = self.tmp(es, 'nrstd', [128, 512], F32)
            rstd_t = Trk('nrstd')
            for tt in range(2):
                xs = self.xT[half][:, :, tt * 512:(tt + 1) * 512]
                xs_t = [self.xT_t[half][c][tt] for c in range(NCH)]
                k.emit('act', lambda e: e.activation(out=sq[:], in_=xs, func=AF.Square), xs_t, [sq_t])
                ps, pst = self.ps[6], self.ps_t[6]
                for c in range(NCH):
                    k.emit('pe', lambda e, c=c: e.matmul(ps[:], self.onesR[:], sq[:, c, :], start=(c == 0), stop=(c == NCH - 1)),
                           [self.onesR_t, sq_t], [pst])
                k.emit('act', lambda e: e.activation(out=rstd[:], in_=ps[:], func=AF.Sqrt, bias=float(D * EPS), scale=1.0),
                       [pst], [rstd_t])
                k.emit('dve', lambda e: e.reciprocal(out=rstd[:], in_=rstd[:]), [rstd_t], [rstd_t])
                k.emit('dve', lambda e: e.tensor_tensor(out=tmp[:], in0=xs, in1=rstd[:].unsqueeze(1).broadcast_to([128, NCH, 512]),
                                                        op=ALU.mult), xs_t + [rstd_t], [tmp_t])
                for c in range(NCH):
                    A = self.AB[:, s, 0, c, half:half + 1]
                    B = self.AB[:, s, 1, c, half:half + 1]
                    out = self.hT[:, c, tt * 512:(tt + 1) * 512]
                    if c % 2 == 0:
                        k.emit('act', lambda e, c=c, A=A, B=B, out=out: e.activation(
                            out=out, in_=tmp[:, c, :], func=AF.Identity, bias=B, scale=A),
                            [tmp_t, self.AB_t], [self.hT_t[c][tt]])
                    else:
                        k.emit('dve', lambda e, c=c, A=A, B=B, out=out: e.tensor_scalar(
                            out=out, in0=tmp[:, c, :], scalar1=A, scalar2=B, op0=ALU.mult, op1=ALU.add),
                            [tmp_t, self.AB_t], [self.hT_t[c][tt]])
            k.barrier()

    def mixer(self, layer, half):
        kind = layer % 3
        ml = self.cfg.get('mixlist', (0, 1, 2))
        if kind not in ml:
            return
        if kind == 0:
            if half == 0:
                self.attn_prep(layer)
            self.attn(layer, half)
        elif kind == 1:
            self.hgrn(layer, half)
        else:
            self.gdn(layer, half)

    def out_pf(self, w_out_rows):
        return PF(self, [[w_out_rows[:, dp * 256:(dp + 1) * 256]] for dp in range(4)])

    def out_proj(self, w_out_rows, src, src_t, nh, half, banks, pf=None):
        k = self.k
        bi = 0
        pf = pf if pf is not None else self.out_pf(w_out_rows)
        for dp in range(4):
            (wo,), wot = pf.get(dp)
            for dmi in range(2):
                dm = dp * 2 + dmi
                for tt in range(2):
                    b = banks[bi % len(banks)]
                    bi += 1
                    ps, pst = self.ps[b], self.ps_t[b]
                    for hl in range(nh):
                        k.emit('pe', lambda e, ps=ps, hl=hl, dmi=dmi, tt=tt: e.matmul(
                            ps[:], wo[:, hl, dmi * 128:(dmi + 1) * 128], src[:, hl, tt * 512:(tt + 1) * 512],
                            start=(hl == 0), stop=(hl == nh - 1)), [wot, src_t[hl]], [pst])
                    xs = self.xT[half][:, dm, tt * 512:(tt + 1) * 512]
                    k.emit('dve', lambda e, ps=ps, xs=xs, dm=dm: e.scalar_tensor_tensor(
                        out=xs, in0=ps[:], scalar=self.gate(0, dm, half), in1=xs, op0=ALU.mult, op1=ALU.add),
                        [pst, self.MOD_t, self.xT_t[half][dm][tt]], [self.xT_t[half][dm][tt]])

    def head_norm(self, tmps, src, src_t, dst, dst_t, gcol, eps_total, bank, extra_mul=None, extra_t=None):
        k = self.k
        sq, sq_t, rs, rs_t = tmps
        ps, pst = self.ps[bank], self.ps_t[bank]
        for tt in range(2):
            sl = slice(tt * 512, (tt + 1) * 512)
            k.emit('act', lambda e, sl=sl: e.activation(out=sq[:], in_=src[:, sl], func=AF.Square), [src_t], [sq_t])
            k.emit('pe', lambda e: e.matmul(ps[:], self.onesR[:], sq[:], start=True, stop=True), [self.onesR_t, sq_t], [pst])
            k.emit('act', lambda e: e.activation(out=rs[:], in_=ps[:], func=AF.Sqrt, bias=float(eps_total), scale=1.0), [pst], [rs_t])
            k.emit('dve', lambda e: e.reciprocal(out=rs[:], in_=rs[:]), [rs_t], [rs_t])
            if extra_mul is not None:
                k.emit('dve', lambda e, sl=sl: e.tensor_tensor(out=rs[:], in0=rs[:], in1=extra_mul[:, sl], op=ALU.mult), [rs_t, extra_t], [rs_t])
            k.emit('dve', lambda e, sl=sl: e.scalar_tensor_tensor(out=dst[:, sl], in0=src[:, sl], scalar=gcol, in1=rs[:], op0=ALU.mult, op1=ALU.mult),
                   [src_t, rs_t, self.PT_t, self.AV_t] + ([self.HV_t] if hasattr(self, 'HV_t') else []), [dst_t])

    def norm_tmps(self, es):
        return (self.tmp(es, 'hsq', [128, 512], F32R), Trk('hsq'), self.tmp(es, 'hrs', [128, 512], F32), Trk('hrs'))

    def attn_prep(self, layer):
        k = self.k
        j = layer // 3
        lam_init = 0.8 - 0.6 * math.exp(-0.3 * layer)
        with ExitStack() as es:
            lt = self.tmp(es, 'ltab', [128, 512], F32)
            lt_t = self.ptrk('ltab')
            k.dma(lt[:], self.d['lamtab'][:, :], writes=[lt_t])
            pr = self.tmp(es, 'lpr', [128, 2, 64], F32)
            pr_t = Trk('lpr')
            sm = self.tmp(es, 'lsm', [128, 2], F32)
            sm_t = Trk('lsm')
            base = j * 256
            lq = lt[:, base:base + 256].rearrange("p (a r n) -> p a r n", a=2, r=2)
            k.emit('dve', lambda e: e.tensor_tensor(out=pr[:], in0=lq[:, :, 0, :], in1=lq[:, :, 1, :], op=ALU.mult), [lt_t], [pr_t])
            k.emit('dve', lambda e: e.reduce_sum(out=sm[:], in_=pr[:], axis=AX.X), [pr_t], [sm_t])
            k.emit('act', lambda e: e.activation(out=sm[:], in_=sm[:], func=AF.Exp), [sm_t], [sm_t])
            k.emit('dve', lambda e: e.tensor_tensor(out=self.AV[:, 0:1], in0=sm[:, 0:1], in1=sm[:, 1:2], op=ALU.subtract), [sm_t], [self.AV_t])
            k.emit('dve', lambda e: e.tensor_scalar(out=self.AV[:, 0:1], in0=self.AV[:, 0:1], scalar1=float(lam_init), scalar2=None, op0=ALU.add),
                   [self.AV_t], [self.AV_t])
            k.emit('dve', lambda e: e.tensor_scalar(out=self.AV[:, 1:2], in0=self.AV[:, 0:1], scalar1=-1.0, scalar2=None, op0=ALU.mult),
                   [self.AV_t], [self.AV_t])
            osl, _ = PC['subln']
            k.emit('dve', lambda e: e.tensor_scalar(out=self.AV[:, 2:3], in0=self.PT[:, osl + j:osl + j + 1],
                                                    scalar1=float((1.0 - lam_init) * math.sqrt(128.0)), scalar2=None, op0=ALU.mult),
                   [self.PT_t, self.AV_t], [self.AV_t])
            k.barrier()

    def attn(self, layer, half):
        k, d, nc = self.k, self.d, self.nc
        j = layer // 3
        w_in = d['attn_w_in']
        scale = 0.125
        nkc = 2 if half == 0 else 12
        with ExitStack() as es:
            GH = 2
            V = self.tmp(es, 'aV', [128, 8, GH * 128], F32R)
            V_t = self.ptrk('aV', 8)
            ntm = self.norm_tmps(es)
            agrp = self.tmp(es, 'agrp', [128, GH, TOK], F32R)
            agrp_t = trks('agrp', GH)
            QT = [self.tmp(es, f'aQ{i}', [128, TOK], F32R) for i in range(1)]
            QT_t = trks('aQ', 1)
            KT = [self.tmp(es, f'aK{i}', [128, TOK], F32R) for i in range(1)]
            KT_t = self.ptrk('aK', 1)
            Pt = [self.tmp(es, f'aP{i}', [128, 512], F32R) for i in range(2)]
            Pt_t = trks('aP', 2)
            att = self.tmp(es, 'att', [128, TOK], F32)
            att_t = Trk('att')
            Rr = self.tmp(es, 'aR', [128, 2, 512], F32)
            Rr_t = Trk('aR')
            Tt, Tt_t = Rr, Rr_t
            if half == 1:
                ropet = self.tmp(es, 'arope', [128, 2048], F32)
                ropet_t = self.ptrk('arope')
                k.dma(ropet[:], d['rope'][:, :], writes=[ropet_t])
                COS = ropet[:, 0:1024]
                SIN = ropet[:, 1024:2048]
                raw = [self.tmp(es, f'araw{i}', [128, 512], F32R) for i in range(1)]
                raw_t = trks('araw', 1)
                ri_ = 0
                t1 = self.tmp(es, 'at1', [128, 512], F32)
                t1_t = Trk('at1')
                kcr = self.tmp(es, 'akcr', [128, 512], F32R)
                kcr_t = Trk('akcr')
                vcr = self.tmp(es, 'avcr', [128, 4 * GH * 128], F32R)
                vcr_t = Trk('avcr')
            pi = 0
            pb = 0
            for grp in range(8 // GH):
                c0 = 2 * D + grp * 256
                gpf = PF(self, [[w_in[j, :, c0:c0 + 256]]] + [[w_in[j, :, (grp * GH + t) * 128:(grp * GH + t + 1) * 128],
                                                              w_in[j, :, D + (grp * GH + t) * 128:D + (grp * GH + t + 1) * 128]] for t in range(GH)])
                for piece in range(1):
                    (wv,), wvt = gpf.get(0)
                    for tile in range(8):
                        b = 6 + (pb % 2)
                        pb += 1
                        ps, pst = self.ps[b], self.ps_t[b]
                        for c in range(NCH):
                            k.emit('pe', lambda e, ps=ps, c=c, tile=tile: e.matmul(
                                ps[:, 0:256], self.hT[:, c, tile * 128:(tile + 1) * 128], wv[:, c, :],
                                start=(c == 0), stop=(c == NCH - 1)), [wvt, self.hT_t[c][tile // 4]], [pst])
                        self.copy(self.evac_eng(), V[:, tile, piece * 256:(piece + 1) * 256], ps[:, 0:256], [pst], [V_t[tile]])
                if half == 0:
                    for tile in range(8):
                        k.dma(d['vout'][j, tile * 128:(tile + 1) * 128, grp * 256:(grp + 1) * 256], V[:, tile, :].bitcast(F32), reads=[V_t[tile]])
                else:
                    (vcv,), _ = self.wpiece([d['cv'][j, :, grp * 256:(grp + 1) * 256]], dest=(vcr, vcr_t))
                for hl in range(GH):
                    hh = grp * GH + hl
                    qi = 0
                    Q, Q_t, Kk, K_t = QT[qi], QT_t[qi], KT[qi], KT_t[qi]
                    (wq, wk), wt = gpf.get(1 + hl)
                    for wi, (w, dst, dst_t) in enumerate(((wq, Q, Q_t), (wk, Kk, K_t))):
                        for tt in range(2):
                            sl = slice(tt * 512, (tt + 1) * 512)
                            b = 6 + (pb % 2)
                            pb += 1
                            ps, pst = self.ps[b], self.ps_t[b]
                            for c in range(NCH):
                                k.emit('pe', lambda e, ps=ps, w=w, c=c, sl=sl: e.matmul(
                                    ps[:], w[:, c, :], self.hT[:, c, sl],
                                    start=(c == 0), stop=(c == NCH - 1)), [wt, self.hT_t[c][tt]], [pst])
                            if half == 0:
                                self.copy(self.evac_eng(), dst[:, sl], ps[:], [pst], [dst_t])
                            else:
                                rw, rw_t = raw[0], raw_t[0]
                                ri_ += 1
                                self.copy('act', rw[:], ps[:], [pst], [rw_t])
                                b2 = 6 + (pb % 2)
                                pb += 1
                                ps2, ps2t = self.ps[b2], self.ps_t[b2]
                                k.emit('pe', lambda e, ps2=ps2, rw=rw: e.matmul(ps2[:], self.permR[:], rw[:], start=True, stop=True),
                                       [self.permR_t, rw_t], [ps2t])
                                k.emit('pool', lambda e, rw=rw, sl=sl: e.tensor_tensor(out=t1[:], in0=rw[:].bitcast(F32), in1=COS[:, sl], op=ALU.mult),
                                       [rw_t, ropet_t], [t1_t])
                                k.emit('dve', lambda e, ps2=ps2, sl=sl, dst=dst: e.tensor_tensor(out=dst[:, sl], in0=ps2[:], in1=SIN[:, sl], op=ALU.mult),
                                       [ps2t, ropet_t], [dst_t])
                                k.emit('dve', lambda e, dst=dst, sl=sl: e.tensor_tensor(out=dst[:, sl], in0=dst[:, sl].bitcast(F32), in1=t1[:], op=ALU.add),
                                       [t1_t, dst_t], [dst_t])
                    if half == 0:
                        k.dma(d['kout'][j, hh, :, :], Kk[:].bitcast(F32), reads=[K_t])
                    else:
                        self.wpiece([d['ck'][j, hh, :, :]], dest=(kcr, kcr_t))

                    def keyT(comp, kc, s=0):
                        r = slice(comp * 64, (comp + 1) * 64)
                        if half == 0:
                            return Kk[r, s * 256 + kc * 128: s * 256 + (kc + 1) * 128], K_t
                        if kc < 4:
                            return kcr[r, kc * 128:(kc + 1) * 128], kcr_t
                        return Kk[r, (kc - 4) * 128:(kc - 3) * 128], K_t

                    def valT(kc, s=0):
                        cs = slice(hl * 128, (hl + 1) * 128)
                        if half == 0:
                            return V[:, s * 2 + kc, cs], V_t[s * 2 + kc]
                        if kc < 4:
                            return vcv[:, kc, cs], vcr_t
                        return V[:, kc - 4, cs], V_t[kc - 4]

                    if half == 0:
                        qw = 256
                        units = [(s, 0) for s in range(4)]
                    else:
                        qw = 512
                        units = [(0, qt) for qt in range(2)]
                    for (s, qt) in units:
                        q0 = s * 256 if half == 0 else qt * 512
                        for comp in range(2):
                            r = slice(comp * 64, (comp + 1) * 64)
                            if half == 0:
                                ob, zb = 2, 3
                                osl = slice(comp * 256, (comp + 1) * 256)
                            else:
                                ob, zb = 2 + comp * 2, 3 + comp * 2
                                osl = slice(0, 512)
                            pO, pO_t = self.ps[ob], self.ps_t[ob]
                            pZ, pZ_t = self.ps[zb], self.ps_t[zb]
                            if half == 0:
                                sb_ = pi % 2
                                pS, pS_t = self.ps[sb_], self.ps_t[sb_]
                                P_, P_t = Pt[pi % 2], Pt_t[pi % 2]
                                pi += 1
                                for kc in range(2):
                                    kl, kl_t = keyT(comp, kc, s)
                                    k.emit('pe', lambda e, pS=pS, kl=kl, kc=kc, r=r, q0=q0: e.matmul(
                                        pS[:, kc * 256:(kc + 1) * 256], kl, Q[r, q0:q0 + 256], start=True, stop=True),
                                        [kl_t, Q_t], [pS_t])
                                k.emit('act', lambda e, pS=pS, P_=P_: e.activation(out=P_[:], in_=pS[:], func=AF.Exp, scale=scale), [pS_t], [P_t])
                                for kc in range(2):
                                    vl, vl_t = valT(kc, s)
                                    k.emit('pe', lambda e, pO=pO, vl=vl, P_=P_, kc=kc, osl=osl: e.matmul(
                                        pO[:, osl], vl, P_[:, kc * 256:(kc + 1) * 256], start=(kc == 0), stop=(kc == 1)),
                                        [vl_t, P_t], [pO_t])
                                for kc in range(2):
                                    k.emit('pe', lambda e, pZ=pZ, P_=P_, kc=kc, osl=osl: e.matmul(
                                        pZ[:, osl], self.onesR[:], P_[:, kc * 256:(kc + 1) * 256], start=(kc == 0), stop=(kc == 1)),
                                        [self.onesR_t, P_t], [pZ_t])
                            else:
                                for kc in range(nkc):
                                    sb_ = pi % 2
                                    pS, pS_t = self.ps[sb_], self.ps_t[sb_]
                                    P_, P_t = Pt[pi % 2], Pt_t[pi % 2]
                                    pi += 1
                                    kl, kl_t = keyT(comp, kc)
                                    k.emit('pe', lambda e, pS=pS, kl=kl, r=r, q0=q0: e.matmul(
                                        pS[:], kl, Q[r, q0:q0 + 512], start=True, stop=True), [kl_t, Q_t], [pS_t])
                                    k.emit('act', lambda e, pS=pS, P_=P_: e.activation(out=P_[:], in_=pS[:], func=AF.Exp, scale=scale), [pS_t], [P_t])
                                    vl, vl_t = valT(kc)
                                    k.emit('pe', lambda e, pO=pO, vl=vl, P_=P_, kc=kc: e.matmul(
                                        pO[:], vl, P_[:], start=(kc == 0), stop=(kc == nkc - 1)), [vl_t, P_t], [pO_t])
                                    k.emit('pe', lambda e, pZ=pZ, P_=P_, kc=kc: e.matmul(
                                        pZ[:], self.onesR[:], P_[:], start=(kc == 0), stop=(kc == nkc - 1)), [self.onesR_t, P_t], [pZ_t])
                        if half == 0:
                            pO, pO_t, pZ, pZ_t = self.ps[2], self.ps_t[2], self.ps[3], self.ps_t[3]
                            k.emit('dve', lambda e, pZ=pZ: e.reciprocal(out=Rr[:, 0, :], in_=pZ[:]), [pZ_t], [Rr_t])
                            k.emit('dve', lambda e, pO=pO: e.tensor_tensor(out=Tt[:, 0, :], in0=pO[:], in1=Rr[:, 0, :], op=ALU.mult), [pO_t, Rr_t], [Tt_t])
                            k.emit('dve', lambda e, q0=q0: e.scalar_tensor_tensor(
                                out=att[:, q0:q0 + 256], in0=Tt[:, 0, 256:512], scalar=self.AV[:, 1:2], in1=Tt[:, 0, 0:256],
                                op0=ALU.mult, op1=ALU.add), [Tt_t, self.AV_t], [att_t])
                        else:
                            for comp in range(2):
                                pO, pO_t = self.ps[2 + comp * 2], self.ps_t[2 + comp * 2]
                                pZ, pZ_t = self.ps[3 + comp * 2], self.ps_t[3 + comp * 2]
                                k.emit('dve', lambda e, pZ=pZ, comp=comp: e.reciprocal(out=Rr[:, comp, :], in_=pZ[:]), [pZ_t], [Rr_t])
                                k.emit('dve', lambda e, pO=pO, comp=comp: e.tensor_tensor(out=Tt[:, comp, :], in0=pO[:], in1=Rr[:, comp, :], op=ALU.mult),
                                       [pO_t, Rr_t], [Tt_t])
                            k.emit('dve', lambda e, q0=q0: e.scalar_tensor_tensor(
                                out=att[:, q0:q0 + 512], in0=Tt[:, 1, :], scalar=self.AV[:, 1:2], in1=Tt[:, 0, :],
                                op0=ALU.mult, op1=ALU.add), [Tt_t, self.AV_t], [att_t])
                    self.head_norm(ntm, att[:], att_t, agrp[:, hl, :], agrp_t[hl], self.AV[:, 2:3], 128.0 * 1e-5, 6 + (pb % 2))
                    pb += 1
                self.out_proj(d['attn_w_out'][j, grp * GH * 128:(grp + 1) * GH * 128, :], agrp, agrp_t, GH, half, [6, 7, 0, 1])
            k.barrier()

    def hgrn_prep(self, layer):
        k = self.k
        ol, _ = PC['hgrn_lb']
        self.HV = self.k.sb('HV', [128, 3, 8])
        self.HV_t = Trk('HV')
        with ExitStack() as es:
            ex = self.tmp(es, 'hex', [128, 4, 8], F32)
            ex_t = Trk('hex')
            tot = self.tmp(es, 'htot', [128, 8], F32)
            tot_t = Trk('htot')
            k.emit('act', lambda e: e.activation(out=ex[:], in_=self.PT[:, ol:ol + 32].rearrange("p (l c) -> p l c", l=4), func=AF.Exp),
                   [self.PT_t], [ex_t])
            k.emit('dve', lambda e: e.tensor_tensor(out=tot[:], in0=ex[:, 0, :], in1=ex[:, 1, :], op=ALU.add), [ex_t], [tot_t])
            for l in (2, 3):
                k.emit('dve', lambda e, l=l: e.tensor_tensor(out=tot[:], in0=tot[:], in1=ex[:, l, :], op=ALU.add), [ex_t, tot_t], [tot_t])
            k.emit('dve', lambda e: e.reciprocal(out=tot[:], in_=tot[:]), [tot_t], [tot_t])
            k.emit('dve', lambda e: e.tensor_copy(out=self.HV[:, 0, :], in_=ex[:, 1, :]), [ex_t], [self.HV_t])
            for l in range(2, layer + 1):
                k.emit('dve', lambda e, l=l: e.tensor_tensor(out=self.HV[:, 0, :], in0=self.HV[:, 0, :], in1=ex[:, l, :], op=ALU.add),
                       [ex_t, self.HV_t], [self.HV_t])
            k.emit('dve', lambda e: e.tensor_tensor(out=self.HV[:, 0, :], in0=self.HV[:, 0, :], in1=tot[:], op=ALU.mult), [tot_t, self.HV_t], [self.HV_t])
            k.emit('dve', lambda e: e.tensor_scalar(out=self.HV[:, 1, :], in0=self.HV[:, 0, :], scalar1=-1.0, scalar2=1.0, op0=ALU.mult, op1=ALU.add),
                   [self.HV_t], [self.HV_t])
            on, _ = PC['hgrn_norm']
            k.emit('dve', lambda e: e.tensor_scalar(out=self.HV[:, 2, 0:1], in0=self.PT[:, on:on + 1], scalar1=float(math.sqrt(128.0)), scalar2=None, op0=ALU.mult),
                   [self.PT_t, self.HV_t], [self.HV_t])
            k.barrier()

    def hgrn(self, layer, half):
        k, d, nc = self.k, self.d, self.nc
        j = layer // 3
        if half == 0:
            self.hgrn_prep(layer)
        w_in = d['hgrn_w_in']
        nseq = 4 if half == 0 else 1
        cps = 16 // nseq
        oo, _ = CC['ones']
        ONES = self.CT[:, oo:oo + 1].broadcast_to([128, TOK])
        oi, _ = CC['ident']
        IDENT = self.CT[:, oi:oi + 128]
        masks = []
        for nm in ('mask_f', 'mask_b'):
            om, _ = CC[nm]
            masks.append(self.CT[0:64, om:om + 256])
        with ExitStack() as es:
            V64 = self.tmp(es, 'hV', [64, 16, 128], F32)
            V64_t = trks('hV', 16)
            mix = self.tmp(es, 'hmix', [128, 1, TOK], F32R)
            mix_t = trks('hmix', 1)
            ntm = self.norm_tmps(es)
            qT = self.tmp(es, 'hq', [128, TOK], F32)
            qT_t = Trk('hq')
            gs, gs_t = qT, qT_t
            oT = self.tmp(es, 'ho', [128, TOK], F32)
            oT_t = Trk('ho')
            Fb = [self.tmp(es, f'hF{i}', [128, TOK], F32) for i in range(2)]
            Fb_t = trks('hF', 2)
            L = self.tmp(es, 'hL', [128, TOK], F32)
            L_t = Trk('hL')
            Gp = self.tmp(es, 'hGp', [128, 64 + TOK + 64], F32)
            Gp_t = Trk('hGp')
            E1 = self.tmp(es, 'hE1', [128, TOK], F32)
            E1_t = Trk('hE1')
            E2 = self.tmp(es, 'hE2', [128, TOK], F32)
            E2_t = Trk('hE2')
            Am = self.tmp(es, 'hAm', [64, 4, 64], F32)
            Am_t = Trk('hAm')
            Ktok = self.tmp(es, 'hKt', [64, 4, 128], F32)
            Ktok_t = Trk('hKt')
            Sb = [self.tmp(es, f'hS{i}', [128, 128], F32) for i in range(2)]
            Sb_t = trks('hS', 2)
            DK = self.tmp(es, 'hDK', [128, 3, 16], F32)
            DK_t = Trk('hDK')
            G3 = self.tmp(es, 'hG3', [128, 3, 16], F32)
            G3_t = Trk('hG3')
            Sp = [self.tmp(es, f'hSp{i}', [128, 128], F32) for i in range(2)]
            Sp_t = trks('hSp', 2)
            tS = self.tmp(es, 'htS', [128, 128], F32)
            tS_t = Trk('htS')
            spi = 0
            sout_t = self.ptrk('hso')
            if half == 0:
                sout = self.tmp(es, 'hso', [128, 4, 2, 128], F32)
            k.emit('dve', lambda e: e.memset(Gp[:], 0.0), [], [Gp_t])
            pb = 0
            for pair in range(4):
                for hl in range(2):
                    hh = pair * 2 + hl
                    hpf = PF(self, [[w_in[j, :, 3 * D + hh * 128:3 * D + (hh + 1) * 128]],
                                    [w_in[j, :, hh * 128:(hh + 1) * 128], w_in[j, :, D + hh * 128:D + (hh + 1) * 128]],
                                    [w_in[j, :, 2 * D + hh * 128:2 * D + (hh + 1) * 128]]])
                    (wi,), wit = hpf.get(0)
                    for tt in range(2):
                        sl = slice(tt * 512, (tt + 1) * 512)
                        b = 6 + (pb % 2)
                        pb += 1
                        ps, pst = self.ps[b], self.ps_t[b]
                        for c in range(NCH):
                            k.emit('pe', lambda e, ps=ps, c=c, sl=sl: e.matmul(ps[:], wi[:, c, :], self.hT[:, c, sl], start=(c == 0), stop=(c == NCH - 1)),
                                   [wit, self.hT_t[c][tt]], [pst])
                        self.copy(self.evac_eng(), L[:, sl], ps[:], [pst], [L_t])
                    for g4 in range(4):
                        b = 6 + (pb % 2)
                        pb += 1
                        ps, pst = self.ps[b], self.ps_t[b]
                        for q_ in range(4):
                            ch = g4 * 4 + q_
                            k.emit('pe', lambda e, ps=ps, q_=q_, ch=ch: e.transpose(ps[0:64, q_ * 128:(q_ + 1) * 128], L[:, ch * 64:(ch + 1) * 64], IDENT),
                                   [L_t, self.CT_t], [pst])
                        self.copy(self.evac_eng(), V64[:, g4 * 4:(g4 + 1) * 4, :].rearrange("p a n -> p (a n)"), ps[0:64, :], [pst], V64_t[g4 * 4:(g4 + 1) * 4])
                    lbc = self.HV[:, 0, hh:hh + 1]
                    omc = self.HV[:, 1, hh:hh + 1]
                    (wq, wzf), wt1 = hpf.get(1)
                    (wzb,), wt2 = hpf.get(2)
                    for (w, wt, kindp) in ((wq, wt1, 'q'), (wzf, wt1, 'zf'), (wzb, wt2, 'zb')):
                        for tt in range(2):
                            sl = slice(tt * 512, (tt + 1) * 512)
                            b = 6 + (pb % 2)
                            pb += 1
                            ps, pst = self.ps[b], self.ps_t[b]
                            for c in range(NCH):
                                k.emit('pe', lambda e, ps=ps, w=w, c=c, sl=sl: e.matmul(
                                    ps[:], w[:, c, :], self.hT[:, c, sl], start=(c == 0), stop=(c == NCH - 1)),
                                    [wt, self.hT_t[c][tt]], [pst])
                            if kindp == 'q':
                                k.emit('act', lambda e, ps=ps, sl=sl: e.activation(out=qT[:, sl], in_=ps[:], func=AF.Copy, scale=float(128.0 ** -0.5)),
                                       [pst], [qT_t])
                            elif kindp == 'g':
                                k.emit('act', lambda e, ps=ps, sl=sl: e.activation(out=gs[:, sl], in_=ps[:], func=AF.Silu), [pst], [gs_t])
                            else:
                                di = 0 if kindp == 'zf' else 1
                                k.emit('act', lambda e, ps=ps, sl=sl, di=di: e.activation(out=Fb[di][:, sl], in_=ps[:], func=AF.Sigmoid), [pst], [Fb_t[di]])
                    (wg,), wt3 = self.wpiece([w_in[j, :, 4 * D + hh * 128:4 * D + (hh + 1) * 128]])
                    opf = self.out_pf(d['hgrn_w_out'][j, hh * 128:(hh + 1) * 128, :])
                    opf.h[0] = self.wpiece(opf.specs[0])
                    for di in range(2):
                        F_, F_t = Fb[di], Fb_t[di]
                        k.emit('dve', lambda e, F_=F_: e.tensor_scalar(out=F_[:], in0=F_[:], scalar1=omc, scalar2=lbc, op0=ALU.mult, op1=ALU.add),
                               [F_t, self.HV_t], [F_t])
                        k.emit('act', lambda e, F_=F_: e.activation(out=L[:], in_=F_[:], func=AF.Ln), [F_t], [L_t])
                        k.emit('pool', lambda e, F_=F_: e.tensor_scalar(out=F_[:], in0=F_[:], scalar1=-1.0, scalar2=1.0, op0=ALU.mult, op1=ALU.add),
                               [F_t], [F_t])
                        k.emit('dve', lambda e: e.tensor_tensor_scan(out=Gp[:, 64:64 + TOK], data0=ONES, data1=L[:], initial=0.0,
                                                                     op0=ALU.mult, op1=ALU.add), [L_t, self.CT_t], [Gp_t])
                        Lv = L[:].rearrange("p (j n) -> p j n", n=64)
                        if di == 0:
                            gprev = Gp[:, 63:63 + TOK].rearrange("p (j n) -> p j n", n=64)[:, :, 0:1].broadcast_to([128, 16, 64])
                            gcur = Gp[:, 64:64 + TOK].rearrange("p (j n) -> p j n", n=64)
                            k.emit('dve', lambda e: e.tensor_tensor(out=Lv, in0=gcur, in1=gprev, op=ALU.subtract), [Gp_t], [L_t])
                        else:
                            gend = Gp[:, 127:127 + TOK].rearrange("p (j n) -> p j n", n=64)[:, :, 0:1].broadcast_to([128, 16, 64])
                            gsh = Gp[:, 63:63 + TOK].rearrange("p (j n) -> p j n", n=64)
                            k.emit('dve', lambda e: e.tensor_tensor(out=Lv, in0=gend, in1=gsh, op=ALU.subtract), [Gp_t], [L_t])
                        pos = 63 if di == 0 else 0
                        mid = 31 if di == 0 else 32
                        k.emit('pool', lambda e, pos=pos: e.tensor_copy(out=G3[:, 0, :], in_=Lv[:, :, pos]), [L_t], [G3_t])
                        k.emit('pool', lambda e, mid=mid: e.tensor_copy(out=G3[:, 1, :], in_=Lv[:, :, mid]), [L_t], [G3_t])
                        k.emit('dve', lambda e: e.tensor_tensor(out=G3[:, 2, :], in0=G3[:, 0, :], in1=G3[:, 1, :], op=ALU.subtract), [G3_t], [G3_t])
                        k.emit('act', lambda e: e.activation(out=DK[:], in_=G3[:], func=AF.Exp), [G3_t], [DK_t])
                        k.emit('dve', lambda e: e.tensor_tensor(out=Lv, in0=Lv, in1=G3[:, 1, :].unsqueeze(2).broadcast_to([128, 16, 64]), op=ALU.subtract),
                               [L_t, G3_t], [L_t])
                        k.emit('act', lambda e: e.activation(out=E1[:], in_=L[:], func=AF.Exp), [L_t], [E1_t])
                        k.emit('act', lambda e: e.activation(out=E2[:], in_=L[:], func=AF.Exp, scale=-1.0), [L_t], [E2_t])
                        k.emit('dve', lambda e: e.tensor_tensor(out=E1[:], in0=E1[:], in1=qT[:], op=ALU.mult), [E1_t, qT_t], [E1_t])
                        k.emit('pool', lambda e, F_=F_: e.tensor_tensor(out=E2[:], in0=E2[:], in1=F_[:], op=ALU.mult), [E2_t, F_t], [E2_t])
                        order = list(range(16)) if di == 0 else list(range(15, -1, -1))
                        if half == 1:
                            k.dma(Sb[0][:], d['st_hgrn'][di, hh, :, :], writes=[Sb_t[0]])
                        si = 0
                        for gi in range(4):
                            chs = order[gi * 4:(gi + 1) * 4]
                            lo = min(chs)
                            psA, psA_t = self.ps[0], self.ps_t[0]
                            psT, psT_t = self.ps[1], self.ps_t[1]
                            for ch in chs:
                                cs = slice(ch * 64, (ch + 1) * 64)
                                q_ = ch - lo
                                k.emit('pe', lambda e, cs=cs, q_=q_: e.matmul(psA[0:64, q_ * 64:(q_ + 1) * 64], E2[:, cs], E1[:, cs], start=True, stop=True),
                                       [E1_t, E2_t], [psA_t])
                                k.emit('pe', lambda e, cs=cs, q_=q_: e.transpose(psT[0:64, q_ * 128:(q_ + 1) * 128], E2[:, cs], IDENT),
                                       [E2_t, self.CT_t], [psT_t])
                            k.emit('dve', lambda e, di=di: e.tensor_tensor(out=Am[:].rearrange("p a n -> p (a n)"), in0=psA[0:64, 0:256], in1=masks[di], op=ALU.mult),
                                   [psA_t, self.CT_t], [Am_t])
                            k.emit('act', lambda e: e.activation(out=Ktok[:].rearrange("p a n -> p (a n)"), in_=psT[0:64, :], func=AF.Copy), [psT_t], [Ktok_t])
                            psO, psO_t = self.ps[2], self.ps_t[2]
                            for ch in chs:
                                cs = slice(ch * 64, (ch + 1) * 64)
                                q_ = ch - lo
                                loc = ch % cps
                                first = (loc == 0) if di == 0 else (loc == cps - 1)
                                last = (loc == cps - 1) if di == 0 else (loc == 0)
                                seq = ch // cps
                                zero_init = first and half == 0
                                vv = V64[:, ch, :]
                                S_prev, S_prev_t = Sb[si], Sb_t[si]
                                k.emit('pe', lambda e, vv=vv, q_=q_, zero_init=zero_init: e.matmul(
                                    psO[:, q_ * 64:(q_ + 1) * 64], vv, Am[:, q_, :], start=True, stop=zero_init), [V64_t[ch], Am_t], [psO_t])
                                if not zero_init:
                                    sp_, sp_t = Sp[spi % 2], Sp_t[spi % 2]
                                    spi += 1
                                    k.emit('act', lambda e, sp_=sp_, S_prev=S_prev, ch=ch: e.activation(
                                        out=sp_[:], in_=S_prev[:], func=AF.Identity, scale=DK[:, 1, ch:ch + 1]), [S_prev_t, DK_t], [sp_t])
                                    k.emit('pe', lambda e, cs=cs, q_=q_, sp_=sp_: e.matmul(
                                        psO[:, q_ * 64:(q_ + 1) * 64], sp_[:], E1[:, cs], start=False, stop=True), [sp_t, E1_t], [psO_t])
                                bS = 3 + (pb % 2)
                                pb += 1
                                psS, psS_t = self.ps[bS], self.ps_t[bS]
                                k.emit('pe', lambda e, psS=psS, vv=vv, q_=q_: e.matmul(
                                    psS[:, 0:128], Ktok[:, q_, :], vv, start=True, stop=True), [Ktok_t, V64_t[ch]], [psS_t])
                                if last and half == 0:
                                    dstS, dstS_t = sout[:, seq, di, :], sout_t
                                else:
                                    si = 1 - si
                                    dstS, dstS_t = Sb[si][:], Sb_t[si]
                                if zero_init:
                                    k.emit('act', lambda e, psS=psS, ch=ch, dstS=dstS: e.activation(
                                        out=dstS, in_=psS[:, 0:128], func=AF.Identity, scale=DK[:, 2, ch:ch + 1]), [psS_t, DK_t], [dstS_t])
                                else:
                                    k.emit('act', lambda e, psS=psS, ch=ch: e.activation(
                                        out=tS[:], in_=psS[:, 0:128], func=AF.Identity, scale=DK[:, 2, ch:ch + 1]), [psS_t, DK_t], [tS_t])
                                    k.emit('dve', lambda e, dstS=dstS, S_prev=S_prev, ch=ch: e.scalar_tensor_tensor(
                                        out=dstS, in0=S_prev[:], scalar=DK[:, 0, ch:ch + 1], in1=tS[:], op0=ALU.mult, op1=ALU.add),
                                        [S_prev_t, DK_t, tS_t], [dstS_t])
                            osl = slice(lo * 64, lo * 64 + 256)
                            if di == 0:
                                k.emit('dve', lambda e, osl=osl: e.tensor_copy(out=oT[:, osl], in_=psO[:, 0:256]), [psO_t], [oT_t])
                            else:
                                k.emit('dve', lambda e, osl=osl: e.tensor_tensor(out=oT[:, osl], in0=oT[:, osl], in1=psO[:, 0:256], op=ALU.add), [psO_t, oT_t], [oT_t])
                    if half == 0:
                        k.dma(d['hgout'][:, :, hh, :, :].rearrange("s d k e -> k s d e"), sout[:], reads=[sout_t])
                    for tt in range(2):
                        sl = slice(tt * 512, (tt + 1) * 512)
                        ps, pst = self.ps[6 + tt], self.ps_t[6 + tt]
                        for c in range(NCH):
                            k.emit('pe', lambda e, ps=ps, c=c, sl=sl: e.matmul(ps[:], wg[:, c, :], self.hT[:, c, sl], start=(c == 0), stop=(c == NCH - 1)),
                                   [wt3, self.hT_t[c][tt]], [pst])
                        k.emit('act', lambda e, ps=ps, sl=sl: e.activation(out=gs[:, sl], in_=ps[:], func=AF.Silu), [pst], [gs_t])
                    self.head_norm(ntm, oT[:], oT_t, mix[:, 0, :], mix_t[0], self.HV[:, 2, 0:1], 128.0 * 1e-6, 5, extra_mul=gs[:], extra_t=gs_t)
                    self.out_proj(d['hgrn_w_out'][j, hh * 128:(hh + 1) * 128, :], mix, mix_t, 1, half, [6, 7], pf=opf)
            k.barrier()

    def gdn(self, layer, half):
        k, d, nc = self.k, self.d, self.nc
        w_in = d['gdn_w_in']
        nseq = 4 if half == 0 else 1
        cps = 16 // nseq
        seqlen = TOK // nseq
        CT = self.CT

        def cc(name, rows=64, w=None):
            o_, w_ = CC[name]
            return CT[0:rows, o_:o_ + (w or w_)]

        ONES_ROW = cc('ones', 128, 1).broadcast_to([128, TOK])
        IDENT = cc('ident', 128)
        ID64 = cc('ident', 64, 64)
        TRIF = cc('mask_f', 64, 64)
        TRIB = cc('mask_b', 64, 64)
        ONES64 = cc('ones', 64, 64)
        NEGU = [cc('negu_f'), cc('negu_b')]
        NEGL = [cc('negl_f'), cc('negl_b')]
        SNEG = [cc('sneg_f'), cc('sneg_b')]
        IDREP = cc('idrep')
        osel, _ = CC['sel']
        onsel, _ = CC['nsel']

        def SEL(kk, m):
            return CT[0:4, osel + kk * 128: osel + kk * 128 + m]

        def NSEL(kk):
            return CT[0:4, onsel + kk * 64: onsel + (kk + 1) * 64]

        oa, _ = PC['gdn_alog_col']
        odt, _ = PC['gdn_dt_col']
        oar, _ = PC['gdn_alog_row']
        odr, _ = PC['gdn_dt_row']
        ocv, _ = PC['gdn_conv']
        ogn, _ = PC['gdn_norm']
        scr_t = self.ptrk('gscr')
        with ExitStack() as es0:
            GC = self.tmp(es0, 'gGC', [64, 16, 16]); BE = self.tmp(es0, 'gBE', [64, 16, 16])
            NBE = self.tmp(es0, 'gNBE', [64, 16, 16]); C1 = self.tmp(es0, 'gC1', [64, 16, 16])
            WW = self.tmp(es0, 'gWW', [64, 16, 16])
            TB_t = Trk('gTB')
            GV = self.tmp(es0, 'gGV', [128, 20])
            GV_t = Trk('gGV')
            k.emit('act', lambda e: e.activation(out=GV[:, 0:1], in_=self.PT[:, oa:oa + 1], func=AF.Exp), [self.PT_t], [GV_t])
            k.emit('dve', lambda e: e.tensor_scalar(out=GV[:, 0:1], in0=GV[:, 0:1], scalar1=-1.0, scalar2=None, op0=ALU.mult), [GV_t], [GV_t])
            k.emit('act', lambda e: e.activation(out=GV[:, 4:20], in_=self.PT[:, oar:oar + 16], func=AF.Exp), [self.PT_t], [GV_t])
            k.emit('dve', lambda e: e.tensor_scalar(out=GV[:, 4:20], in0=GV[:, 4:20], scalar1=-1.0, scalar2=None, op0=ALU.mult), [GV_t], [GV_t])
            k.emit('dve', lambda e: e.tensor_scalar(out=GV[:, 1:2], in0=self.PT[:, ogn:ogn + 1], scalar1=float(math.sqrt(128.0)), scalar2=None, op0=ALU.mult),
                   [self.PT_t, GV_t], [GV_t])
            (wab,), wab_t = self.wpiece([w_in[0, :, 4 * D:4 * D + 32]], rounded=False)
            with ExitStack() as es:
                LA = self.tmp(es, 'gLA', [16, TOK]); LA_t = Trk('gLA')
                Gp = self.tmp(es, 'gGp', [16, 64 + TOK + 64]); Gp_t = Trk('gGp')
                GF = self.tmp(es, 'gGF', [16, TOK]); GF_t = self.ptrk('gGF')
                GB = self.tmp(es, 'gGB', [16, TOK]); GB_t = self.ptrk('gGB')
                BT = self.tmp(es, 'gBT', [16, TOK]); BT_t = self.ptrk('gBT')
                LAt = self.tmp(es, 'gLAt', [64, 16, 16]); LAt_t = Trk('gLAt')
                k.emit('dve', lambda e: e.memset(Gp[:], 0.0), [], [Gp_t])
                hTf = self.hT[:].bitcast(F32)
                for part in range(2):
                    for tt in range(2):
                        sl = slice(tt * 512, (tt + 1) * 512)
                        ps, pst = self.ps[6 + tt], self.ps_t[6 + tt]
                        for c in range(NCH):
                            k.emit('pe', lambda e, ps=ps, c=c, sl=sl, part=part: e.matmul(
                                ps[0:16, :], wab[:, c, part * 16:(part + 1) * 16], hTf[:, c, sl], start=(c == 0), stop=(c == NCH - 1)),
                                [wab_t, self.hT_t[c][tt]], [pst])
                        if part == 0:
                            k.emit('act', lambda e, ps=ps, sl=sl: e.activation(out=LA[:, sl], in_=ps[0:16, :], func=AF.Exp, bias=self.PT[0:16, odt:odt + 1]),
                                   [pst, self.PT_t], [LA_t])
                        else:
                            k.emit('act', lambda e, ps=ps, sl=sl: e.activation(out=BT[:, sl], in_=ps[0:16, :], func=AF.Sigmoid), [pst], [BT_t])
                k.emit('act', lambda e: e.activation(out=LA[:], in_=LA[:], func=AF.Ln, bias=1.0), [LA_t], [LA_t])
                k.emit('dve', lambda e: e.tensor_scalar(out=LA[:], in0=LA[:], scalar1=GV[0:16, 0:1], scalar2=None, op0=ALU.mult), [LA_t, GV_t], [LA_t])
                k.emit('dve', lambda e: e.tensor_tensor_scan(out=Gp[:, 64:64 + TOK], data0=ONES_ROW[0:16, :], data1=LA[:], initial=0.0,
                                                             op0=ALU.mult, op1=ALU.add), [LA_t, self.CT_t], [Gp_t])
                gprev = Gp[:, 63:63 + TOK].rearrange("p (j n) -> p j n", n=64)[:, :, 0:1].broadcast_to([16, 16, 64])
                gcur = Gp[:, 64:64 + TOK].rearrange("p (j n) -> p j n", n=64)
                k.emit('dve', lambda e: e.tensor_tensor(out=GF[:].rearrange("p (j n) -> p j n", n=64), in0=gcur, in1=gprev, op=ALU.subtract), [Gp_t], [GF_t])
                gend = Gp[:, 127:127 + TOK].rearrange("p (j n) -> p j n", n=64)[:, :, 0:1].broadcast_to([16, 16, 64])
                gsh = Gp[:, 63:63 + TOK].rearrange("p (j n) -> p j n", n=64)
                k.emit('dve', lambda e: e.tensor_tensor(out=GB[:].rearrange("p (j n) -> p j n", n=64), in0=gend, in1=gsh, op=ALU.subtract), [Gp_t], [GB_t])
                k.dma(d['gscr'][0:16, :], GF[:], reads=[GF_t], writes=[scr_t])
                k.dma(d['gscr'][16:32, :], GB[:], reads=[GB_t], writes=[scr_t])
                k.dma(d['gscr'][32:48, :], BT[:], reads=[BT_t], writes=[scr_t])
                ps, pst = self.ps[5], self.ps_t[5]
                for ch in range(16):
                    for c in range(NCH):
                        k.emit('pe', lambda e, c=c, ch=ch: e.matmul(
                            ps[0:64, ch * 32:(ch + 1) * 32], hTf[:, c, ch * 64:(ch + 1) * 64], wab[:, c, :], start=(c == 0), stop=(c == NCH - 1)),
                            [wab_t, self.hT_t[c][ch // 8]], [pst])
                pv = ps[0:64, :].rearrange("p (c n) -> p c n", n=32)
                k.emit('dve', lambda e: e.tensor_tensor(out=LAt[:], in0=pv[:, :, 0:16],
                                                        in1=self.PT[0:64, odr:odr + 16].unsqueeze(1).broadcast_to([64, 16, 16]), op=ALU.add),
                       [pst, self.PT_t], [LAt_t])
                k.emit('act', lambda e: e.activation(out=BE[:], in_=pv[:, :, 16:32], func=AF.Sigmoid), [pst], [TB_t])
                k.emit('act', lambda e: e.activation(out=LAt[:], in_=LAt[:], func=AF.Exp), [LAt_t], [LAt_t])
                k.emit('act', lambda e: e.activation(out=LAt[:], in_=LAt[:], func=AF.Ln, bias=1.0), [LAt_t], [LAt_t])
                k.emit('dve', lambda e: e.tensor_tensor(out=LAt[:], in0=LAt[:], in1=GV[0:64, 4:20].unsqueeze(1).broadcast_to([64, 16, 16]), op=ALU.mult),
                       [LAt_t, GV_t], [LAt_t])
                LAf = LAt[:].rearrange("p c n -> p (c n)")
                pF, pF_t = self.ps[0], self.ps_t[0]
                pB, pB_t = self.ps[1], self.ps_t[1]
                pT, pT_t = self.ps[2], self.ps_t[2]
                k.emit('pe', lambda e: e.matmul(pF[0:64, 0:256], TRIF, LAf, start=True, stop=True), [LAt_t, self.CT_t], [pF_t])
                k.emit('pe', lambda e: e.matmul(pB[0:64, 0:256], TRIB, LAf, start=True, stop=True), [LAt_t, self.CT_t], [pB_t])
                k.emit('pe', lambda e: e.matmul(pT[0:64, 0:256], ONES64, LAf, start=True, stop=True), [LAt_t, self.CT_t], [pT_t])
                pFv = pF[0:64, 0:256].rearrange("p (c n) -> p c n", n=16)
                pBv = pB[0:64, 0:256].rearrange("p (c n) -> p c n", n=16)
                pTv = pT[0:64, 0:256].rearrange("p (c n) -> p c n", n=16)
                k.emit('dve', lambda e: e.tensor_copy(out=GC[:, :, 0:8], in_=pFv[:, :, 0:8]), [pF_t], [TB_t])
                k.emit('dve', lambda e: e.tensor_copy(out=GC[:, :, 8:16], in_=pBv[:, :, 8:16]), [pB_t], [TB_t])
                k.emit('dve', lambda e: e.tensor_tensor(out=WW[:], in0=pTv, in1=GC[:], op=ALU.subtract), [pT_t, TB_t], [TB_t])
                k.emit('act', lambda e: e.activation(out=WW[:], in_=WW[:], func=AF.Exp), [TB_t], [TB_t])
                k.emit('act', lambda e: e.activation(out=C1[:], in_=GC[:], func=AF.Exp), [TB_t], [TB_t])
                k.emit('dve', lambda e: e.scalar_tensor_tensor(out=C1[:], in0=C1[:], scalar=-1.0, in1=BE[:], op0=ALU.mult, op1=ALU.mult), [TB_t], [TB_t])
                k.emit('dve', lambda e: e.tensor_scalar(out=NBE[:], in0=BE[:], scalar1=-1.0, scalar2=None, op0=ALU.mult), [TB_t], [TB_t])
                k.barrier()
            stop = self.cfg.get('gdn_stop', 9)
            if stop <= 1:
                return
            with ExitStack() as es:
                qn = self.tmp(es, 'gq', [128, TOK]); qn_t = Trk('gq')
                kn = self.tmp(es, 'gk', [128, TOK]); kn_t = Trk('gk')
                vT = self.tmp(es, 'gv', [128, TOK]); vT_t = Trk('gv')
                oT = self.tmp(es, 'go', [128, TOK]); oT_t = Trk('go')
                Qt = self.tmp(es, 'gQt', [128, TOK]); Qt_t = Trk('gQt')
                mix = self.tmp(es, 'gmix', [128, 1, TOK], F32R); mix_t = trks('gmix', 1)
                ntm = self.norm_tmps(es)
                HR4 = self.tmp(es, 'gHR', [4, TOK]); HR4_t = self.ptrk('gHR')
                bt = [self.tmp(es, f'gb{i}', [64, 4, 64]) for i in range(10)]
                bt_t = trks('gb', 10)
                ktok = self.tmp(es, 'gkt', [64, 4, 128]); ktok_t = Trk('gkt')
                vtok = self.tmp(es, 'gvt', [64, 4, 128]); vtok_t = Trk('gvt')
                sm = [self.tmp(es, f'gs{i}', [64, 128]) for i in range(4)]
                sm_t = trks('gs', 4)
                Sb = [self.tmp(es, f'gS{i}', [128, 128]) for i in range(2)]
                Sb_t = trks('gS', 2)
                DKg = self.tmp(es, 'gDK', [128, 16]); DKg_t = Trk('gDK')
                sout_t = self.ptrk('gso')
                if half == 0:
                    sout = self.tmp(es, 'gso', [128, 4, 2, 128])
                pb = 0
                for hh in range(8):
                    (wq, wk), wt1 = self.wpiece([w_in[0, :, hh * 128:(hh + 1) * 128], w_in[0, :, D + hh * 128:D + (hh + 1) * 128]])
                    (wv,), wt2 = self.wpiece([w_in[0, :, 2 * D + hh * 128:2 * D + (hh + 1) * 128]])
                    k.dma_group([(HR4[0:1, :], d['gscr'][hh:hh + 1, :]), (HR4[1:2, :], d['gscr'][24 + hh:25 + hh, :]),
                                 (HR4[2:3, :], d['gscr'][32 + hh:33 + hh, :]), (HR4[3:4, :], d['gscr'][40 + hh:41 + hh, :])], [HR4_t])
                    HR4_t.rs[scr_t.w[0]] = 0
                    for ti, (w, wt, dst, dst_t) in enumerate(((wq, wt1, qn, qn_t), (wk, wt1, kn, kn_t), (wv, wt2, vT, vT_t))):
                        fch = ti * 8 + hh
                        w0 = self.PT[:, ocv + 0 * 24 + fch: ocv + 0 * 24 + fch + 1]
                        w1 = self.PT[:, ocv + 1 * 24 + fch: ocv + 1 * 24 + fch + 1]
                        w2 = self.PT[:, ocv + 2 * 24 + fch: ocv + 2 * 24 + fch + 1]
                        pss = []
                        for tt in range(2):
                            sl = slice(tt * 512, (tt + 1) * 512)
                            b = 6 + tt
                            ps, pst = self.ps[b], self.ps_t[b]
                            pss.append((ps, pst))
                            for c in range(NCH):
                                k.emit('pe', lambda e, ps=ps, w=w, c=c, sl=sl: e.matmul(
                                    ps[:], w[:, c, :], self.hT[:, c, sl], start=(c == 0), stop=(c == NCH - 1)), [wt, self.hT_t[c][tt]], [pst])
                            k.emit('act', lambda e, ps=ps, sl=sl, dst=dst, w1=w1: e.activation(out=dst[:, sl], in_=ps[:], func=AF.Copy, scale=w1),
                                   [pst, self.PT_t], [dst_t])
                        for tt in range(2):
                            ps, pst = pss[tt]
                            sl_ = min(seqlen, 512)
                            ns = 512 // sl_
                            pv = ps[:].rearrange("p (s n) -> p s n", s=ns)
                            av = dst[:, tt * 512:(tt + 1) * 512].rearrange("p (s n) -> p s n", s=ns)
                            k.emit('dve', lambda e, pv=pv, av=av, w0=w0, sl_=sl_: e.scalar_tensor_tensor(
                                out=av[:, :, 1:sl_], in0=pv[:, :, 0:sl_ - 1], scalar=w0, in1=av[:, :, 1:sl_], op0=ALU.mult, op1=ALU.add),
                                [pst, self.PT_t, dst_t], [dst_t])
                            k.emit('dve', lambda e, pv=pv, av=av, w2=w2, sl_=sl_: e.scalar_tensor_tensor(
                                out=av[:, :, 0:sl_ - 1], in0=pv[:, :, 1:sl_], scalar=w2, in1=av[:, :, 0:sl_ - 1], op0=ALU.mult, op1=ALU.add),
                                [pst, self.PT_t, dst_t], [dst_t])
                        if seqlen > 512:
                            p0, p0t = pss[0]
                            p1, p1t = pss[1]
                            k.emit('dve', lambda e, p0=p0, dst=dst, w0=w0: e.scalar_tensor_tensor(
                                out=dst[:, 512:513], in0=p0[:, 511:512], scalar=w0, in1=dst[:, 512:513], op0=ALU.mult, op1=ALU.add),
                                [p0t, self.PT_t, dst_t], [dst_t])
                            k.emit('dve', lambda e, p1=p1, dst=dst, w2=w2: e.scalar_tensor_tensor(
                                out=dst[:, 511:512], in0=p1[:, 0:1], scalar=w2, in1=dst[:, 511:512], op0=ALU.mult, op1=ALU.add),
                                [p1t, self.PT_t, dst_t], [dst_t])
                        k.emit('act', lambda e, dst=dst: e.activation(out=dst[:], in_=dst[:], func=AF.Silu), [dst_t], [dst_t])
                        if ti < 2:
                            sq, sq_t, rs, rs_t = ntm
                            for tt in range(2):
                                sl = slice(tt * 512, (tt + 1) * 512)
                                ps, pst = self.ps[5], self.ps_t[5]
                                k.emit('act', lambda e, dst=dst, sl=sl: e.activation(out=sq[:], in_=dst[:, sl], func=AF.Square), [dst_t], [sq_t])
                                k.emit('pe', lambda e, ps=ps: e.matmul(ps[:], self.onesR[:], sq[:], start=True, stop=True), [self.onesR_t, sq_t], [pst])
                                k.emit('act', lambda e, ps=ps: e.activation(out=rs[:], in_=ps[:], func=AF.Sqrt, bias=1e-6, scale=1.0), [pst], [rs_t])
                                k.emit('dve', lambda e: e.reciprocal(out=rs[:], in_=rs[:]), [rs_t], [rs_t])
                                sc_ = float(128.0 ** -0.5) if ti == 0 else 1.0
                                k.emit('dve', lambda e, dst=dst, sl=sl, sc_=sc_: e.scalar_tensor_tensor(
                                    out=dst[:, sl], in0=dst[:, sl], scalar=sc_, in1=rs[:], op0=ALU.mult, op1=ALU.mult), [dst_t, rs_t], [dst_t])
                    if stop <= 2:
                        continue
                    for di in range(2):
                        col = di * 8 + hh
                        for tt in range(2):
                            sl = slice(tt * 512, (tt + 1) * 512)
                            ps, pst = self.ps[6 + tt], self.ps_t[6 + tt]
                            k.emit('pe', lambda e, ps=ps, sl=sl, di=di: e.matmul(ps[:], SEL(di, 128), HR4[0:4, sl], start=True, stop=True),
                                   [HR4_t, self.CT_t], [pst])
                            k.emit('act', lambda e, ps=ps, sl=sl: e.activation(out=Qt[:, sl], in_=ps[:], func=AF.Exp), [pst], [Qt_t])
                        pos = 63 if di == 0 else 0
                        k.emit('pool', lambda e, pos=pos: e.tensor_copy(out=DKg[:], in_=Qt[:].rearrange("p (j n) -> p j n", n=64)[:, :, pos]), [Qt_t], [DKg_t])
                        k.emit('dve', lambda e: e.tensor_tensor(out=Qt[:], in0=Qt[:], in1=qn[:], op=ALU.mult), [Qt_t, qn_t], [Qt_t])
                        border = list(range(4)) if di == 0 else list(range(3, -1, -1))
                        if half == 1:
                            k.dma(Sb[0][:], d['st_gdn'][di, hh, :, :], writes=[Sb_t[0]])
                        si = 0
                        for bi in border:
                            chs = [bi * 4 + q for q in range(4)]
                            if di == 1:
                                chs = chs[::-1]
                            T0 = bi * 256
                            bsl = slice(T0, T0 + 256)
                            gcol = GC[:, bi * 4:bi * 4 + 4, col:col + 1].broadcast_to([64, 4, 64])
                            nbcol = NBE[:, bi * 4:bi * 4 + 4, col:col + 1].broadcast_to([64, 4, 64])
                            b0, b0t = self.ps[0], self.ps_t[0]
                            b1, b1t = self.ps[1], self.ps_t[1]
                            b2, b2t = self.ps[2], self.ps_t[2]
                            b3, b3t = self.ps[3], self.ps_t[3]
                            R = lambda ps: ps[0:64, 0:256]
                            R3 = lambda ps: ps[0:64, 0:256].rearrange("p (a n) -> p a n", a=4)
                            F2 = lambda t: t[:].rearrange("p a n -> p (a n)")
                            deps_c = [HR4_t, self.CT_t]
                            k.emit('pe', lambda e, di=di: e.matmul(R(b0), SEL(di, 64), HR4[0:4, bsl], start=True, stop=True), deps_c, [b0t])
                            k.emit('pe', lambda e, di=di: e.matmul(R(b2), SEL(2 + di, 64), HR4[0:4, bsl], start=True, stop=True), deps_c, [b2t])
                            DT, DT_t = bt[0], bt_t[0]
                            Dl, Dl_t = bt[1], bt_t[1]
                            DBT, DBT_t = bt[2], bt_t[2]
                            k.emit('dve', lambda e: e.tensor_tensor(out=DT[:], in0=R3(b0), in1=gcol, op=ALU.subtract), [b0t, TB_t], [DT_t])
                            k.emit('pool', lambda e, di=di: e.tensor_tensor(out=F2(Dl), in0=NEGL[di], in1=F2(DT), op=ALU.subtract), [DT_t, self.CT_t], [Dl_t])
                            k.emit('pool', lambda e, di=di: e.tensor_tensor(out=F2(DT), in0=F2(DT), in1=NEGU[di], op=ALU.add), [DT_t, self.CT_t], [DT_t])
                            k.emit('act', lambda e: e.activation(out=DT[:], in_=DT[:], func=AF.Exp), [DT_t], [DT_t])
                            k.emit('act', lambda e: e.activation(out=Dl[:], in_=Dl[:], func=AF.Exp), [Dl_t], [Dl_t])
                            k.emit('dve', lambda e, di=di: e.tensor_tensor(out=F2(DBT), in0=R(b2), in1=SNEG[di], op=ALU.mult), [b2t, self.CT_t], [DBT_t])
                            k.emit('pool', lambda e: e.tensor_tensor(out=DBT[:], in0=DBT[:], in1=DT[:], op=ALU.mult), [DBT_t, DT_t], [DBT_t])
                            k.emit('pool', lambda e: e.tensor_tensor(out=Dl[:], in0=Dl[:], in1=nbcol, op=ALU.mult), [Dl_t, TB_t], [Dl_t])
                            for q in range(4):
                                cs = slice(T0 + q * 64, T0 + (q + 1) * 64)
                                k.emit('pe', lambda e, q=q, cs=cs: e.matmul(b0[0:64, q * 64:(q + 1) * 64], kn[:, cs], kn[:, cs], start=True, stop=True), [kn_t], [b0t])
                                k.emit('pe', lambda e, q=q, cs=cs: e.matmul(b1[0:64, q * 64:(q + 1) * 64], kn[:, cs], qn[:, cs], start=True, stop=True), [kn_t, qn_t], [b1t])
                                k.emit('pe', lambda e, q=q, cs=cs: e.transpose(b2[0:64, q * 128:(q + 1) * 128], kn[:, cs], IDENT), [kn_t, self.CT_t], [b2t])
                                k.emit('pe', lambda e, q=q, cs=cs: e.transpose(b3[0:64, q * 128:(q + 1) * 128], vT[:, cs], IDENT), [vT_t, self.CT_t], [b3t])
                            NT, NT_t = bt[3], bt_t[3]
                            Nm, Nm_t = bt[4], bt_t[4]
                            QKT, QKT_t = bt[5], bt_t[5]
                            XT, XT_t = bt[6], bt_t[6]
                            k.emit('dve', lambda e: e.tensor_tensor(out=F2(NT), in0=R(b0), in1=F2(DBT), op=ALU.mult), [b0t, DBT_t], [NT_t])
                            k.emit('dve', lambda e: e.tensor_tensor(out=F2(Nm), in0=R(b0), in1=F2(Dl), op=ALU.mult), [b0t, Dl_t], [Nm_t])
                            k.emit('dve', lambda e: e.tensor_tensor(out=F2(QKT), in0=R(b1), in1=F2(DT), op=ALU.mult), [b1t, DT_t], [QKT_t])
                            k.emit('pool', lambda e: e.tensor_tensor(out=F2(XT), in0=F2(NT), in1=IDREP, op=ALU.add), [NT_t, self.CT_t], [XT_t])
                            k.emit('act', lambda e: e.activation(out=F2(ktok), in_=b2[0:64, :], func=AF.Copy), [b2t], [ktok_t])
                            k.emit('act', lambda e: e.activation(out=F2(vtok), in_=b3[0:64, :], func=AF.Copy), [b3t], [vtok_t])
                            P, P_t, PT_, PT_t = Nm, Nm_t, NT, NT_t
                            pp = [(bt[7], bt_t[7], bt[8], bt_t[8]), (bt[9], bt_t[9], bt[1], bt_t[1])]
                            XTs = [(bt[6], bt_t[6]), (bt[0], bt_t[0])]
                            xi = 0
                            for m in range(1, 6):
                                nP, nP_t, nPT, nPT_t = pp[(m - 1) % 2] if m > 1 else pp[0]
                                if m >= 3:
                                    nP, nP_t, nPT, nPT_t = pp[(m - 1) % 2]
                                if m == 2:
                                    nP, nP_t, nPT, nPT_t = pp[1]
                                for q in range(4):
                                    k.emit('pe', lambda e, q=q, P=P, PT_=PT_: e.matmul(b0[0:64, q * 64:(q + 1) * 64], PT_[:, q, :], P[:, q, :], start=True, stop=True),
                                           [P_t, PT_t], [b0t])
                                    if m < 5:
                                        k.emit('pe', lambda e, q=q, P=P, PT_=PT_: e.matmul(b1[0:64, q * 64:(q + 1) * 64], P[:, q, :], PT_[:, q, :], start=True, stop=True),
                                               [P_t, PT_t], [b1t])
                                k.emit('act', lambda e, nP=nP: e.activation(out=F2(nP), in_=R(b0), func=AF.Copy), [b0t], [nP_t])
                                if m < 5:
                                    k.emit('dve', lambda e, nPT=nPT: e.tensor_copy(out=F2(nPT), in_=R(b1)), [b1t], [nPT_t])
                                cX, cX_t = XTs[xi]
                                nX, nX_t = XTs[1 - xi]
                                for q in range(4):
                                    k.emit('pe', lambda e, q=q, nP=nP, cX=cX: e.matmul(b2[0:64, q * 64:(q + 1) * 64], nP[:, q, :], cX[:, q, :], start=True, stop=True),
                                           [nP_t, cX_t], [b2t])
                                k.emit('dve', lambda e, nX=nX, cX=cX: e.tensor_tensor(out=F2(nX), in0=R(b2), in1=F2(cX), op=ALU.add), [b2t, cX_t], [nX_t])
                                xi = 1 - xi
                                P, P_t, PT_, PT_t = nP, nP_t, nPT, nPT_t
                            XTf, XTf_t = XTs[xi]
                            if stop <= 3:
                                continue
                            psO, psO_t = self.ps[6], self.ps_t[6]
                            for ch in chs:
                                q = ch - bi * 4
                                cs = slice(ch * 64, (ch + 1) * 64)
                                loc = ch % cps
                                first = (loc == 0) if di == 0 else (loc == cps - 1)
                                last = (loc == cps - 1) if di == 0 else (loc == 0)
                                seq = ch // cps
                                zero_init = first and half == 0
                                S_prev, S_prev_t = Sb[si], Sb_t[si]
                                tmpv, tmpv_t = sm[0], sm_t[0]
                                r_, r_t = sm[1], sm_t[1]
                                vn, vn_t = sm[2], sm_t[2]
                                vs, vs_t = sm[3], sm_t[3]
                                k.emit('act', lambda e, q=q, ch=ch: e.activation(out=tmpv[:], in_=vtok[:, q, :], func=AF.Copy, scale=BE[:, ch, col:col + 1]),
                                       [vtok_t, TB_t], [tmpv_t])
                                if zero_init:
                                    rr, rr_t = tmpv, tmpv_t
                                else:
                                    p4, p4t = self.ps[4], self.ps_t[4]
                                    k.emit('pe', lambda e, cs=cs, S_prev=S_prev: e.matmul(p4[0:64, 0:128], kn[:, cs], S_prev[:], start=True, stop=True),
                                           [kn_t, S_prev_t], [p4t])
                                    k.emit('dve', lambda e, ch=ch: e.scalar_tensor_tensor(out=r_[:], in0=p4[0:64, 0:128], scalar=C1[:, ch, col:col + 1], in1=tmpv[:],
                                                                                         op0=ALU.mult, op1=ALU.add), [p4t, TB_t, tmpv_t], [r_t])
                                    rr, rr_t = r_, r_t
                                lvl = self.cfg.get('seq_lvl', 9)
                                if lvl <= 1:
                                    continue
                                p5, p5t = self.ps[5], self.ps_t[5]
                                k.emit('pe', lambda e, q=q, rr=rr: e.matmul(p5[0:64, 0:128], XTf[:, q, :], rr[:], start=True, stop=True), [XTf_t, rr_t], [p5t])
                                if lvl <= 1.5:
                                    continue
                                k.emit('act', lambda e: e.activation(out=vn[:], in_=p5[0:64, 0:128], func=AF.Copy), [p5t], [vn_t])
                                if lvl <= 1.7:
                                    continue
                                k.emit('dve', lambda e, ch=ch: e.tensor_scalar(out=vs[:], in0=p5[0:64, 0:128], scalar1=WW[:, ch, col:col + 1], scalar2=None, op0=ALU.mult),
                                       [p5t, TB_t], [vs_t])
                                if lvl <= 2:
                                    continue
                                k.emit('pe', lambda e, q=q, zero_init=zero_init: e.matmul(psO[:, q * 64:(q + 1) * 64], vn[:], QKT[:, q, :], start=True, stop=zero_init),
                                       [vn_t, QKT_t], [psO_t])
                                if not zero_init:
                                    k.emit('pe', lambda e, q=q, cs=cs, S_prev=S_prev: e.matmul(psO[:, q * 64:(q + 1) * 64], S_prev[:], Qt[:, cs], start=False, stop=True),
                                           [S_prev_t, Qt_t], [psO_t])
                                if lvl <= 3:
                                    continue
                                p7, p7t = self.ps[7], self.ps_t[7]
                                k.emit('pe', lambda e, q=q: e.matmul(p7[:, 0:128], ktok[:, q, :], vs[:], start=True, stop=True), [ktok_t, vs_t], [p7t])
                                if last and half == 0:
                                    dstS, dstS_t = sout[:, seq, di, :], sout_t
                                else:
                                    si = 1 - si
                                    dstS, dstS_t = Sb[si][:], Sb_t[si]
                                if zero_init:
                                    k.emit('dve', lambda e, dstS=dstS: e.tensor_copy(out=dstS, in_=p7[:, 0:128]), [p7t], [dstS_t])
                                else:
                                    k.emit('dve', lambda e, dstS=dstS, S_prev=S_prev, ch=ch: e.scalar_tensor_tensor(
                                        out=dstS, in0=S_prev[:], scalar=DKg[:, ch:ch + 1], in1=p7[:, 0:128], op0=ALU.mult, op1=ALU.add),
                                        [p7t, S_prev_t, DKg_t], [dstS_t])
                            if di == 0:
                                k.emit('act', lambda e, bsl=bsl: e.activation(out=oT[:, bsl], in_=psO[:, 0:256], func=AF.Copy), [psO_t], [oT_t])
                            else:
                                k.emit('dve', lambda e, bsl=bsl: e.tensor_tensor(out=oT[:, bsl], in0=oT[:, bsl], in1=psO[:, 0:256], op=ALU.add), [psO_t, oT_t], [oT_t])
                    if stop <= 4:
                        continue
                    if half == 0:
                        k.dma(d['gdout'][:, :, hh, :, :].rearrange("s d k e -> k s d e"), sout[:], reads=[sout_t])
                    (wg,), wt3 = self.wpiece([w_in[0, :, 3 * D + hh * 128:3 * D + (hh + 1) * 128]])
                    for tt in range(2):
                        sl = slice(tt * 512, (tt + 1) * 512)
                        ps, pst = self.ps[6 + tt], self.ps_t[6 + tt]
                        for c in range(NCH):
                            k.emit('pe', lambda e, ps=ps, c=c, sl=sl: e.matmul(ps[:], wg[:, c, :], self.hT[:, c, sl], start=(c == 0), stop=(c == NCH - 1)),
                                   [wt3, self.hT_t[c][tt]], [pst])
                        k.emit('act', lambda e, ps=ps, sl=sl: e.activation(out=vT[:, sl], in_=ps[:], func=AF.Silu), [pst], [vT_t])
                    self.HV_t = GV_t
                    self.head_norm(ntm, oT[:], oT_t, mix[:, 0, :], mix_t[0], GV[:, 1:2], 128.0 * 1e-6, 5, extra_mul=vT[:], extra_t=vT_t)
                    self.out_proj(d['gdn_w_out'][0, hh * 128:(hh + 1) * 128, :], mix, mix_t, 1, half, [6, 7])
                k.barrier()

    def ffn(self, layer, half):
        k, d, nc = self.k, self.d, self.nc
        seqlen = 256 if half == 0 else 1024
        oc, _ = PC['ffn_conv']
        ob, _ = PC['ffn_conv_b']

        def cw(tap, fchunk):
            col = oc + (layer * 3 + tap) * 44 + fchunk
            return self.PT[:, col:col + 1]

        def cb(fchunk):
            col = ob + layer * 44 + fchunk
            return self.PT[:, col:col + 1]

        groups = [(0, 8), (8, 16), (16, 22)]
        specs = []
        pidx = {}
        for (g0, g1) in groups:
            for j in range(g0, g1):
                pidx[('u', j)] = len(specs)
                specs.append([d['ffn_w_up'][layer, :, j * 128:(j + 1) * 128], d['ffn_w_up'][layer, :, D_FF + j * 128:D_FF + (j + 1) * 128]])
            for dp in range(4):
                pidx[('d', g0, dp)] = len(specs)
                specs.append([d['ffn_w_down'][layer, g0 * 128:g1 * 128, dp * 256:(dp + 1) * 256]])
        pf = PF(self, specs)
        PAIRS = [(0, 1), (2, 3), (4, 5)] if self.modgen is not None else [(0, 1), (2, 3), (4, 5), (6, 7)]
        u = 0
        v = 0
        with ExitStack() as es:
            aT = self.tmp(es, 'aT', [128, 8, TOK], F32R)
            aT_t = trks('aT', 8)
            acc = [self.tmp(es, f'facc{i}', [128, 2, TOK], F32) for i in range(2)]
            acc_t = trks('facc', 2, 2, 2)
            for (g0, g1) in groups:
                for j in range(g0, g1):
                    slot = j - g0
                    (wv, wg), wt = pf.get(pidx[('u', j)])
                    ai = j % 2
                    ac = acc[ai]
                    banks = {}
                    for tt in range(2):
                        pair = PAIRS[u % len(PAIRS)]
                        u += 1
                        sl = slice(tt * 512, (tt + 1) * 512)
                        for vi, (w, fch) in enumerate(((wv, j), (wg, NFF + j))):
                            ps, pst = self.ps[pair[vi]], self.ps_t[pair[vi]]
                            banks[(vi, tt)] = (ps, pst)
                            for c in range(NCH):
                                k.emit('pe', lambda e, ps=ps, w=w, c=c, sl=sl: e.matmul(
                                    ps[:], w[:, c, :], self.hT[:, c, sl],
                                    start=(c == 0), stop=(c == NCH - 1)), [wt, self.hT_t[c][tt]], [pst])
                        for vi, fch in ((0, j), (1, NFF + j)):
                            ps, pst = banks[(vi, tt)]
                            at_ = acc_t[ai][vi][tt]
                            k.emit('act', lambda e, ps=ps, sl=sl, fch=fch, vi=vi: e.activation(
                                out=ac[:, vi, sl], in_=ps[:], func=AF.Identity, bias=cb(fch), scale=cw(1, fch)), [pst, self.PT_t], [at_])
                            sl_ = min(seqlen, 512)
                            ns = 512 // sl_
                            pv = ps[:].rearrange("p (s n) -> p s n", s=ns)
                            av = ac[:, vi, sl].rearrange("p (s n) -> p s n", s=ns)
                            k.emit('dve', lambda e, pv=pv, av=av, fch=fch, sl_=sl_: e.scalar_tensor_tensor(
                                out=av[:, :, 1:sl_], in0=pv[:, :, 0:sl_ - 1], scalar=cw(0, fch), in1=av[:, :, 1:sl_],
                                op0=ALU.mult, op1=ALU.add), [pst, self.PT_t, at_], [at_])
                            k.emit('dve', lambda e, pv=pv, av=av, fch=fch, sl_=sl_: e.scalar_tensor_tensor(
                                out=av[:, :, 0:sl_ - 1], in0=pv[:, :, 1:sl_], scalar=cw(2, fch), in1=av[:, :, 0:sl_ - 1],
                                op0=ALU.mult, op1=ALU.add), [pst, self.PT_t, at_], [at_])
                    if seqlen > 512:
                        for vi, fch in ((0, j), (1, NFF + j)):
                            p0, p0t = banks[(vi, 0)]
                            p1, p1t = banks[(vi, 1)]
                            k.emit('dve', lambda e, p0=p0, fch=fch, vi=vi: e.scalar_tensor_tensor(
                                out=ac[:, vi, 512:513], in0=p0[:, 511:512], scalar=cw(0, fch), in1=ac[:, vi, 512:513],
                                op0=ALU.mult, op1=ALU.add), [p0t, self.PT_t, acc_t[ai][vi][1]], [acc_t[ai][vi][1]])
                            k.emit('dve', lambda e, p1=p1, fch=fch, vi=vi: e.scalar_tensor_tensor(
                                out=ac[:, vi, 511:512], in0=p1[:, 0:1], scalar=cw(2, fch), in1=ac[:, vi, 511:512],
                                op0=ALU.mult, op1=ALU.add), [p1t, self.PT_t, acc_t[ai][vi][0]], [acc_t[ai][vi][0]])
                    k.emit('act', lambda e, ac=ac: e.activation(out=ac[:, 1, :], in_=ac[:, 1, :], func=AF.Silu), acc_t[ai][1], acc_t[ai][1])
                    k.emit('pool', lambda e, slot=slot, ac=ac: e.tensor_tensor(out=aT[:, slot, :], in0=ac[:, 0, :], in1=ac[:, 1, :], op=ALU.mult),
                           acc_t[ai], [aT_t[slot]])
                    if self.modgen is not None:
                        next(self.modgen, None)
                ng = g1 - g0
                for dp in range(4):
                    (wd,), wdt = pf.get(pidx[('d', g0, dp)])
                    for dmi in range(2):
                        dm = dp * 2 + dmi
                        for tt in range(2):
                            b = v % 6
                            v += 1
                            ps, pst = self.ps[b], self.ps_t[b]
                            for jj in range(ng):
                                k.emit('pe', lambda e, ps=ps, jj=jj, dmi=dmi, tt=tt: e.matmul(
                                    ps[:], wd[:, jj, dmi * 128:(dmi + 1) * 128], aT[:, jj, tt * 512:(tt + 1) * 512],
                                    start=(jj == 0), stop=(jj == ng - 1)), [wdt, aT_t[jj]], [pst])
                            xs = self.xT[half][:, dm, tt * 512:(tt + 1) * 512]
                            k.emit('dve', lambda e, ps=ps, xs=xs, dm=dm: e.scalar_tensor_tensor(
                                out=xs, in0=ps[:], scalar=self.gate(1, dm, half), in1=xs, op0=ALU.mult, op1=ALU.add),
                                [pst, self.MOD_t, self.xT_t[half][dm][tt]], [self.xT_t[half][dm][tt]])
            k.barrier()

    def final(self, cfg):
        k, d, nc = self.k, self.d, self.nc
        og, _ = PC['final_g']
        with ExitStack() as es:
            sq = self.tmp(es, 'fsq', [128, NCH, 512], F32R)
            sq_t = Trk('fsq')
            rstd = self.tmp(es, 'frstd', [128, 512], F32)
            rstd_t = Trk('frstd')
            yo = self.tmp(es, 'fyo', [128, NCH, 512], F32)
            yo_t = self.ptrk('fyo', NCH)
            for half in range(2):
                for tt in range(2):
                    xs = self.xT[half][:, :, tt * 512:(tt + 1) * 512]
                    xs_t = [self.xT_t[half][c][tt] for c in range(NCH)]
                    k.emit('act', lambda e: e.activation(out=sq[:], in_=xs, func=AF.Square), xs_t, [sq_t])
                    ps, pst = self.ps[6], self.ps_t[6]
                    for c in range(NCH):
                        k.emit('pe', lambda e, c=c: e.matmul(ps[:], self.onesR[:], sq[:, c, :], start=(c == 0), stop=(c == NCH - 1)),
                               [self.onesR_t, sq_t], [pst])
                    k.emit('act', lambda e: e.activation(out=rstd[:], in_=ps[:], func=AF.Sqrt, bias=float(D * EPS), scale=1.0),
                           [pst], [rstd_t])
                    k.emit('dve', lambda e: e.reciprocal(out=rstd[:], in_=rstd[:]), [rstd_t], [rstd_t])
                    for c in range(NCH):
                        g = self.PT[:, og + c:og + c + 1]
                        k.emit('dve', lambda e, c=c, g=g: e.scalar_tensor_tensor(
                            out=yo[:, c, :], in0=self.xT[half][:, c, tt * 512:(tt + 1) * 512], scalar=g, in1=rstd[:],
                            op0=ALU.mult, op1=ALU.mult), [self.xT_t[half][c][tt], self.PT_t, rstd_t], [yo_t[c]])
                        k.emit('act', lambda e, c=c: e.activation(out=yo[:, c, :], in_=yo[:, c, :], func=AF.Copy, scale=32.0),
                               [yo_t[c]], [yo_t[c]])
                        k.dma(d['yout'][half, c, :, tt * 512:(tt + 1) * 512], yo[:, c, :], reads=[yo_t[c]])


_CACHE = {}


def _get_prog(cfg):
    key = repr(sorted(cfg.items()))
    if key not in _CACHE:
        p = Prog(dict(cfg))
        with p.es:
            p.declare()
            p.build()
        _CACHE[key] = p
    return _CACHE[key]


def _pack(plan, arrays, n):
    out = np.zeros((128, n), np.float32)
    for c0, key in plan:
        off = c0
        for (name, offset, rstride, rows, ncols) in key:
            flat = arrays[name].reshape(-1)
            a2 = np.lib.stride_tricks.as_strided(flat[offset:], shape=(rows, ncols), strides=(rstride * 4, 4))
            kc = rows // 128
            assert off + kc * ncols <= n
            out[:, off:off + kc * ncols] = a2.reshape(kc, 128, ncols).transpose(1, 0, 2).reshape(128, kc * ncols)
            off += kc * ncols
    return out


def _run(inp, cfg):
    p = _get_prog(cfg)
    consts, rope_tab = _build_consts()
    f32 = lambda a: np.ascontiguousarray(np.asarray(a, np.float32))
    warr = {n: f32(inp[n]) for n in ('w_mod', 'ffn_w_up', 'ffn_w_down', 'attn_w_in', 'attn_w_out',
                                     'hgrn_w_in', 'hgrn_w_out', 'gdn_w_in', 'gdn_w_out')}
    shared = {'wpk': _pack(p.wplan['wpk'], warr, WCOLS)}
    xp = f32(inp['x_prompt'])
    xs = f32(inp['x_sample'])
    ck = f32(inp['cache_attn_k'])
    cv = f32(inp['cache_attn_v'])
    in_maps = []
    for core in range(N_CORES):
        m = dict(shared)
        a = xp[4 * core:4 * core + 4].reshape(TOK, D).T.reshape(8, 128, TOK)
        b = xs[core].T.reshape(8, 128, TOK)
        m['xin'] = np.ascontiguousarray(np.stack([a, b], axis=0))
        m['params'] = _build_params(core, inp)
        m['consts'] = consts
        m['rope'] = rope_tab
        m['lamtab'] = np.ascontiguousarray(np.broadcast_to(np.asarray(inp['attn_lambda'], np.float32).reshape(1, 512), (128, 512)))
        carr = {'ck': np.ascontiguousarray(ck[core].transpose(0, 2, 3, 1)),
                'cv': np.ascontiguousarray(cv[core].reshape(2, 512, D))}
        m['cpk'] = _pack(p.wplan['cpk'], carr, CCOLS)
        m['st_hgrn'] = f32(inp['state_hgrn'][core, 0])
        m['st_gdn'] = f32(inp['state_gdn'][core, 0])
        in_maps.append(m)
    ncores = cfg.get('ncores', N_CORES)
    res = run_bass_kernel_spmd(p.nc, in_maps[:ncores], core_ids=list(range(ncores)))
    R = list(res.results) + [res.results[0]] * (N_CORES - ncores)
    y_prompt = np.empty((32, 256, D), np.float32)
    y_sample = np.empty((8, 1024, D), np.float32)
    new_k = np.empty((32, 2, 256, 8, 128), np.float32)
    new_v = np.empty((32, 2, 256, 8, 128), np.float32)
    new_h = np.empty((32, 1, 2, 8, 128, 128), np.float32)
    new_g = np.empty((32, 1, 2, 8, 128, 128), np.float32)
    for core in range(N_CORES):
        r = R[core]
        yo = r['yout']
        y_prompt[4 * core:4 * core + 4] = yo[0].reshape(D, TOK).T.reshape(4, 256, D)
        y_sample[core] = yo[1].reshape(D, TOK).T
        ko = r['kout']
        new_k[4 * core:4 * core + 4] = ko.reshape(2, 8, 128, 4, 256).transpose(3, 0, 4, 1, 2)
        vo = r['vout']
        new_v[4 * core:4 * core + 4] = vo.reshape(2, 4, 256, 8, 128).transpose(1, 0, 2, 3, 4)
        new_h[4 * core:4 * core + 4, 0] = r['hgout']
        new_g[4 * core:4 * core + 4, 0] = r['gdout']
    return (y_prompt, y_sample, new_k, new_v, new_h, new_g)


def kernel(**inputs):
    return _run(inputs, {})
```

```python
import math
from contextlib import ExitStack

import numpy as np
import concourse.bass as bass
import concourse.mybir as mybir
from concourse.bass_utils import run_bass_kernel_spmd

F32 = mybir.dt.float32
F32R = mybir.dt.float32r
AF = mybir.ActivationFunctionType
ALU = mybir.AluOpType
AX = mybir.AxisListType

D = 1024
NCH = 8
TOK = 1024
DEPTH = 4
D_FF = 2816
NFF = 22
EPS = 1e-6
N_CORES = 8
WCOLS = (4 * 1024 * 6144 + 4 * 1024 * 5632 + 4 * 2816 * 1024 + 2 * 1024 * 3072 + 2 * 1024 * 1024 + 1024 * 5120 + 1024 * 1024
         + 1024 * 4128 + 1024 * 1024) // 128
CCOLS = (2 * 8 * 128 * 512 + 2 * 512 * 1024) // 128


class Cols:
    def __init__(self):
        self.off = {}
        self.n = 0

    def add(self, name, w):
        self.off[name] = (self.n, w)
        self.n += w

    def __getitem__(self, name):
        return self.off[name]


def _param_cols():
    c = Cols()
    c.add('cond', 16)
    c.add('norm_g', 64)
    c.add('b_mod', 192)
    c.add('final_g', 8)
    c.add('subln', 2)
    c.add('hgrn_lb', 32)
    c.add('hgrn_norm', 1)
    c.add('gdn_norm', 1)
    c.add('gdn_conv', 72)
    c.add('ffn_conv', 528)
    c.add('ffn_conv_b', 176)
    c.add('gdn_alog_col', 1)
    c.add('gdn_dt_col', 1)
    c.add('gdn_alog_row', 16)
    c.add('gdn_dt_row', 16)
    return c


PC = _param_cols()


def _fm(v):
    v = np.asarray(v, np.float32)
    r = v.reshape(-1, 128)
    return np.ascontiguousarray(r.T)


def _build_params(core, inp):
    P = np.zeros((128, PC.n), np.float32)

    def put(name, arr):
        o, w = PC[name]
        assert arr.shape == (128, w), (name, arr.shape, w)
        P[:, o:o + w] = arr

    cond = np.stack([inp['c_ctx'], inp['c'][core]], axis=0)
    put('cond', np.ascontiguousarray(cond.reshape(2, 8, 128).transpose(2, 1, 0)).reshape(128, 16))
    put('norm_g', _fm(inp['norm_g']))
    put('b_mod', _fm(inp['b_mod']))
    put('final_g', _fm(inp['final_g']))
    put('subln', _fm(inp['attn_subln']))
    put('hgrn_lb', _fm(inp['hgrn_lb']))
    put('hgrn_norm', _fm(inp['hgrn_norm']))
    put('gdn_norm', _fm(inp['gdn_norm']))
    put('gdn_conv', _fm(inp['gdn_conv']))
    put('ffn_conv', _fm(inp['ffn_conv']))
    put('ffn_conv_b', _fm(inp['ffn_conv_b']))
    al = np.zeros((128, 1), np.float32)
    al[:16, 0] = np.asarray(inp['gdn_a_log'], np.float32).reshape(16)
    put('gdn_alog_col', al)
    dtb = np.zeros((128, 1), np.float32)
    dtb[:16, 0] = np.asarray(inp['gdn_dt_bias'], np.float32).reshape(16)
    put('gdn_dt_col', dtb)
    put('gdn_alog_row', np.broadcast_to(np.asarray(inp['gdn_a_log'], np.float32).reshape(1, 16), (128, 16)))
    put('gdn_dt_row', np.broadcast_to(np.asarray(inp['gdn_dt_bias'], np.float32).reshape(1, 16), (128, 16)))
    return P


def _const_cols():
    c = Cols()
    c.add('ident', 128)
    c.add('ones', 128)
    c.add('perm', 128)
    c.add('mask_f', 256)
    c.add('mask_b', 256)
    c.add('negu_f', 256)
    c.add('negl_f', 256)
    c.add('negu_b', 256)
    c.add('negl_b', 256)
    c.add('sneg_f', 256)
    c.add('sneg_b', 256)
    c.add('idrep', 256)
    c.add('sel', 512)
    c.add('nsel', 128)
    return c


CC = _const_cols()


def _build_consts():
    C = np.zeros((128, CC.n), np.float32)
    o, w = CC['ident']
    C[:, o:o + w] = np.eye(128, dtype=np.float32)
    o, w = CC['ones']
    C[:, o:o + w] = 1.0
    o, w = CC['perm']
    tok = np.arange(1024)
    row = (tok // 64).astype(np.float32)
    col = (tok % 64).astype(np.float32)
    inv = (np.float32(10000.0) ** (-np.arange(16, dtype=np.float32) / np.float32(16))).astype(np.float32)
    ROPE = np.zeros((128, 2048), np.float32)
    oc_, os_ = 0, 1024
    for p in range(128):
        dd = p % 64
        i = dd % 32
        partner = p + 16 if i < 16 else p - 16
        C[partner, o + p] = 1.0
        f = i % 16
        pos = row if dd < 32 else col
        ang = (pos * inv[f]).astype(np.float32)
        ROPE[p, oc_:oc_ + 1024] = np.cos(ang)
        ROPE[p, os_:os_ + 1024] = np.sin(ang) * (-1.0 if i < 16 else 1.0)
    sidx = np.arange(64)[:, None]
    tidx = np.arange(64)[None, :]
    om, _ = CC['mask_f']
    C[:64, om:om + 256] = np.tile((sidx <= tidx).astype(np.float32), (1, 4))
    om, _ = CC['mask_b']
    C[:64, om:om + 256] = np.tile((sidx >= tidx).astype(np.float32), (1, 4))
    BIG = 30000.0
    p_, j_ = sidx, tidx

    def putm(name, m):
        o_, _ = CC[name]
        C[:64, o_:o_ + 256] = np.tile(m.astype(np.float32), (1, 4))

    putm('negu_f', np.where(j_ >= p_, 0.0, -BIG))
    putm('negl_f', np.where(j_ < p_, 0.0, -BIG))
    putm('negu_b', np.where(j_ <= p_, 0.0, -BIG))
    putm('negl_b', np.where(j_ > p_, 0.0, -BIG))
    putm('sneg_f', np.where(j_ > p_, -1.0, 0.0))
    putm('sneg_b', np.where(j_ < p_, -1.0, 0.0))
    putm('idrep', (j_ == p_))
    o_, _ = CC['sel']
    for kk in range(4):
        C[kk, o_ + kk * 128:o_ + (kk + 1) * 128] = 1.0
    o_, _ = CC['nsel']
    for kk in range(2):
        C[kk, o_ + kk * 64:o_ + (kk + 1) * 64] = -1.0
    return C, ROPE


class _GT:
    def __init__(self, name):
        self.name = name


class Geo:
    def __init__(self, name, shape, offset=0, pat=None):
        self.tensor = _GT(name)
        if pat is None:
            pat = []
            st = 1
            for n in reversed(shape):
                pat.insert(0, (st, n))
                st *= n
        self.ap = tuple(pat)
        self.offset = offset

    @property
    def shape(self):
        return tuple(n for _, n in self.ap)

    def __getitem__(self, key):
        if not isinstance(key, tuple):
            key = (key,)
        key = key + (slice(None),) * (len(self.ap) - len(key))
        off = self.offset
        pat = []
        for (st, n), kk in zip(self.ap, key):
            if isinstance(kk, int):
                off += st * kk
            else:
                a, b, _ = kk.indices(n)
                off += st * a
                pat.append((st, b - a))
        return Geo(self.tensor.name, None, off, pat)


class Trk:
    __slots__ = ('name', 'w', 'rs', 'sem', 'cnt', 'psum')

    def __init__(self, name):
        self.name = name
        self.psum = False
        self.w = None
        self.rs = {}
        self.sem = None
        self.cnt = 0


def trks(name, *dims):
    if len(dims) == 1:
        return [Trk(f'{name}{i}') for i in range(dims[0])]
    return [trks(f'{name}{i}_', *dims[1:]) for i in range(dims[0])]


def flat(x):
    if isinstance(x, Trk):
        return [x]
    out = []
    for e in x:
        out.extend(flat(e))
    return out


class KB:
    def __init__(self, nc, es):
        self.nc = nc
        self.es = es
        self.E = {'pe': nc.tensor, 'act': nc.scalar, 'dve': nc.vector, 'pool': nc.gpsimd, 'sp': nc.sync}
        self.sem = {e: es.enter_context(nc.semaphore(f's_{e}')) for e in self.E}
        self.cnt = {e: 0 for e in self.E}
        self.waited = {e: {} for e in self.E}
        self.dma_sems = []
        self.n_ins = 0
        self.n_wait = 0

    def _wait(self, eng, deps):
        need = {}
        for key, val in deps:
            if key == 'pe' and eng == 'pe':
                continue
            if need.get(key, 0) < val:
                need[key] = val
        wt = self.waited[eng]
        for key, val in need.items():
            if wt.get(key, 0) >= val:
                continue
            sem = self.sem[key] if isinstance(key, str) else key
            self.E[eng].wait_ge(sem, val)
            self.n_wait += 1
            wt[key] = val

    def _deps(self, reads, writes):
        deps = []
        for t in reads:
            if t.w is not None:
                deps.append(t.w)
            if t.psum:
                deps.extend(t.rs.items())
        for t in writes:
            if t.w is not None:
                deps.append(t.w)
            deps.extend(t.rs.items())
        return deps

    def _mark(self, tok, reads, writes):
        k, v = tok
        for t in reads:
            if t.rs.get(k, 0) < v:
                t.rs[k] = v
        for t in writes:
            t.w = tok
            t.rs = {}

    def emit(self, eng, fn, reads=(), writes=()):
        reads = flat(reads)
        writes = flat(writes)
        self._wait(eng, self._deps(reads, writes))
        ins = fn(self.E[eng])
        self.cnt[eng] += 1
        ins.then_inc(self.sem[eng], 1)
        self.n_ins += 1
        tok = (eng, self.cnt[eng])
        self._mark(tok, reads, writes)
        return tok

    def dma(self, out, in_, reads=(), writes=(), q='sp'):
        reads = flat(reads)
        writes = flat(writes)
        self._wait(q, self._deps(reads, writes))
        owner = (writes + reads)[0]
        if owner.sem is None:
            owner.sem = self.es.enter_context(self.nc.semaphore(f'd_{owner.name}'))
            self.dma_sems.append(owner)
        self.E[q].dma_start(out=out, in_=in_).then_inc(owner.sem, 16)
        owner.cnt += 16
        self.n_ins += 1
        tok = (owner.sem, owner.cnt)
        self._mark(tok, reads, writes)
        return tok

    def dma_group(self, pairs, writes, q='sp'):
        writes = flat(writes)
        self._wait(q, self._deps([], writes))
        owner = writes[0]
        if owner.sem is None:
            owner.sem = self.es.enter_context(self.nc.semaphore(f'd_{owner.name}'))
            self.dma_sems.append(owner)
        for out, in_ in pairs:
            self.E[q].dma_start(out=out, in_=in_).then_inc(owner.sem, 16)
            owner.cnt += 16
            self.n_ins += 1
        tok = (owner.sem, owner.cnt)
        self._mark(tok, [], writes)
        return tok

    def barrier(self):
        for e in self.E:
            deps = [(o, self.cnt[o]) for o in self.E if o != e and self.cnt[o] > 0]
            deps += [(t.sem, t.cnt) for t in self.dma_sems]
            wt = self.waited[e]
            for key, val in deps:
                if wt.get(key, 0) >= val:
                    continue
                sem = self.sem[key] if isinstance(key, str) else key
                self.E[e].wait_ge(sem, val)
                self.n_wait += 1
                wt[key] = val

    def finish(self):
        deps = [(t.sem, t.cnt) for t in self.dma_sems]
        deps += [(o, self.cnt[o]) for o in self.E if o != 'sp' and self.cnt[o] > 0]
        self._wait('sp', deps)

    def sb(self, name, shape, dt=F32):
        return self.es.enter_context(self.nc.sbuf_tensor(name, list(shape), dt))


class PF:
    def __init__(self, prog, specs):
        self.p, self.specs, self.h = prog, specs, {}

    def get(self, i):
        for t in (i, i + 1):
            if t < len(self.specs) and t not in self.h:
                self.h[t] = self.p.wpiece(self.specs[t])
        return self.h.pop(i)


class Prog:
    def __init__(self, cfg):
        self.cfg = cfg
        nc = bass.Bass("TRN2", target_bir_lowering=False)
        self.nc = nc
        self.es = ExitStack()
        self.k = KB(nc, self.es)
        self.rr = 0
        self.wplan = {'wpk': [], 'cpk': []}
        self.wcols = {'wpk': 0, 'cpk': 0}
        self.wkeys = {}

    def declare(self):
        nc = self.nc

        def din(name, shape):
            return nc.dram_tensor(name, list(shape), F32, kind="ExternalInput").ap()

        def dout(name, shape):
            return nc.dram_tensor(name, list(shape), F32, kind="ExternalOutput").ap()

        d = {}
        d['xin'] = din('xin', [2, 8, 128, TOK])
        d['params'] = din('params', [128, PC.n])
        d['consts'] = din('consts', [128, CC.n])
        d['rope'] = din('rope', [128, 2048])
        d['wpk'] = din('wpk', [128, WCOLS])
        d['cpk'] = din('cpk', [128, CCOLS])
        d['lamtab'] = din('lamtab', [128, 512])
        d['gscr'] = nc.dram_tensor('gscr', [48, TOK], F32, kind="Internal").ap()
        d['w_mod'] = Geo('w_mod', [4, D, 6 * D])
        d['ffn_w_up'] = Geo('ffn_w_up', [4, D, 2 * D_FF])
        d['ffn_w_down'] = Geo('ffn_w_down', [4, D_FF, D])
        d['attn_w_in'] = Geo('attn_w_in', [2, D, 3 * D])
        d['attn_w_out'] = Geo('attn_w_out', [2, D, D])
        d['hgrn_w_in'] = Geo('hgrn_w_in', [1, D, 5 * D])
        d['hgrn_w_out'] = Geo('hgrn_w_out', [1, D, D])
        d['gdn_w_in'] = Geo('gdn_w_in', [1, D, 4 * D + 32])
        d['gdn_w_out'] = Geo('gdn_w_out', [1, D, D])
        d['ck'] = Geo('ck', [2, 8, 128, 512])
        d['cv'] = Geo('cv', [2, 512, D])
        d['st_hgrn'] = din('st_hgrn', [2, 8, 128, 128])
        d['st_gdn'] = din('st_gdn', [2, 8, 128, 128])
        d['yout'] = dout('yout', [2, 8, 128, TOK])
        d['kout'] = dout('kout', [2, 8, 128, TOK])
        d['vout'] = dout('vout', [2, TOK, D])
        d['hgout'] = dout('hgout', [4, 2, 8, 128, 128])
        d['gdout'] = dout('gdout', [4, 2, 8, 128, 128])
        self.d = d

    def ptrk(self, name, n=None):
        if not hasattr(self, '_pt'):
            self._pt = {}
        if name not in self._pt:
            self._pt[name] = Trk(name) if n is None else trks(name, n)
        return self._pt[name]

    def tmp(self, es, name, shape, dt=F32):
        self._uid = getattr(self, '_uid', 0) + 1
        return es.enter_context(self.nc.sbuf_tensor(f'{name}_{self._uid}', list(shape), dt))

    def evac_eng(self):
        self.rr += 1
        return 'act' if self.rr % 2 else 'dve'

    def copy(self, eng, out, in_, reads, writes):
        if eng == 'act':
            return self.k.emit('act', lambda e: e.activation(out=out, in_=in_, func=AF.Copy), reads, writes)
        return self.k.emit(eng, lambda e: e.tensor_copy(out=out, in_=in_), reads, writes)

    def init_wpool(self):
        k = self.k
        self.ws = [k.sb(f'ws{i}', [128, 2048], F32) for i in range(2)]
        self.ws_t = trks('ws', 2)
        self.wr = [k.sb(f'wr{i}', [128, 2048], F32R) for i in range(2)]
        self.wr_t = trks('wr', 2)
        self.ws_i = 0
        self.wr_i = 0

    def wpiece(self, segs, rounded=True, dest=None):
        k = self.k
        si = self.ws_i
        self.ws_i = (si + 1) % 2
        st, stt = self.ws[si], self.ws_t[si]
        off = 0
        views = []
        key = []
        percore = False
        for ap in segs:
            rows, ncols = ap.shape
            kc = rows // 128
            pat = tuple((int(a), int(b)) for a, b in ap.ap)
            assert len(pat) == 2 and pat[1][0] == 1, pat
            name = ap.tensor.name
            percore = percore or name in ('ck', 'cv')
            key.append((name, int(ap.offset), pat[0][0], rows, ncols))
            views.append((off, kc, ncols))
            off += kc * ncols
        key = tuple(key)
        pk = 'cpk' if percore else 'wpk'
        if key not in self.wkeys:
            self.wkeys[key] = (pk, self.wcols[pk])
            self.wplan[pk].append((self.wcols[pk], key))
            self.wcols[pk] += off
        pk, c0 = self.wkeys[key]
        k.dma(st[:, 0:off], self.d[pk][:, c0:c0 + off], writes=[stt])
        if not rounded:
            outs = [st[:, o:o + kc * n].rearrange("p (c n) -> p c n", c=kc) for (o, kc, n) in views]
            return outs, stt
        if dest is not None:
            rt, rtt = dest
        else:
            ri = self.wr_i
            self.wr_i = (ri + 1) % 2
            rt, rtt = self.wr[ri], self.wr_t[ri]
        k.emit('act', lambda e: e.activation(out=rt[:, 0:off], in_=st[:, 0:off], func=AF.Copy), [stt], [rtt])
        outs = [rt[:, o:o + kc * n].rearrange("p (c n) -> p c n", c=kc) for (o, kc, n) in views]
        return outs, rtt

    def build(self):
        nc, k, d, cfg = self.nc, self.k, self.d, self.cfg
        self.ps = [self.es.enter_context(nc.psum_tensor(f'ps{i}', [128, 512], F32)) for i in range(8)]
        self.ps_t = trks('ps', 8)
        for t in self.ps_t:
            t.psum = True
        self.PT = k.sb('PT', [128, PC.n])
        self.PT_t = Trk('PT')
        self.CT = k.sb('CT', [128, CC.n])
        self.CT_t = Trk('CT')
        self.onesR = k.sb('onesR', [128, 128], F32R)
        self.onesR_t = Trk('onesR')
        self.identR = k.sb('identR', [128, 128], F32R)
        self.identR_t = Trk('identR')
        self.xT = [k.sb(f'xT{h}', [128, NCH, TOK]) for h in range(2)]
        self.xT_t = trks('xT', 2, NCH, 2)
        self.hT = k.sb('hT', [128, NCH, TOK], F32R)
        self.hT_t = trks('hT', NCH, 2)
        self.permR = k.sb('permR', [128, 128], F32R)
        self.permR_t = Trk('permR')
        self.AV = k.sb('AV', [128, 8])
        self.AV_t = Trk('AV')
        self.MODs = [k.sb(f'MOD{i}', [128, 48, 2]) for i in range(2)]
        self.MODs_t = trks('MOD', 2)
        self.ABs = [k.sb(f'AB{i}', [128, 2, 2, 8, 2]) for i in range(2)]
        self.ABs_t = trks('AB', 2)
        self.modgen = None
        self.sc = k.sb('sc', [128, 16])
        self.sc_t = Trk('sc')
        self.init_wpool()

        k.dma(self.PT[:], d['params'][:, :], writes=[self.PT_t])
        k.dma(self.CT[:], d['consts'][:, :], writes=[self.CT_t])
        for h in range(2):
            k.dma_group([(self.xT[h][:, c, :], d['xin'][h, c, :, :]) for c in range(NCH)], self.xT_t[h])
        o, w = CC['ones']
        k.emit('dve', lambda e: e.tensor_copy(out=self.onesR[:], in_=self.CT[:, o:o + w]), [self.CT_t], [self.onesR_t])
        o2, w2 = CC['ident']
        k.emit('dve', lambda e: e.tensor_copy(out=self.identR[:], in_=self.CT[:, o2:o2 + w2]), [self.CT_t], [self.identR_t])
        o3, w3 = CC['perm']
        k.emit('dve', lambda e: e.tensor_copy(out=self.permR[:], in_=self.CT[:, o3:o3 + w3]), [self.CT_t], [self.permR_t])
        oc, wc = PC['cond']
        k.emit('act', lambda e: e.activation(out=self.sc[:], in_=self.PT[:, oc:oc + wc], func=AF.Silu), [self.PT_t], [self.sc_t])

        nl = cfg.get('layers', DEPTH)
        for _ in self.modulation(0):
            pass
        for layer in range(nl):
            p = layer % 2
            self.MOD, self.MOD_t, self.AB, self.AB_t = self.MODs[p], self.MODs_t[p], self.ABs[p], self.ABs_t[p]
            for half in range(2):
                if cfg.get('mixers', True):
                    self.rmsnorm_mod(layer, 0, half)
                    self.mixer(layer, half)
                if half == 1 and layer + 1 < nl:
                    self.modgen = self.modulation(layer + 1)
                if cfg.get('ffn', True):
                    self.rmsnorm_mod(layer, 1, half)
                    self.ffn(layer, half)
                if self.modgen is not None:
                    for _ in self.modgen:
                        pass
                    self.modgen = None
        self.final(cfg)
        k.finish()

    def modulation(self, layer):
        k, d = self.k, self.d
        p = layer % 2
        MOD, MOD_t, AB, AB_t = self.MODs[p], self.MODs_t[p], self.ABs[p], self.ABs_t[p]
        ps, pst = self.ps[7], self.ps_t[7]
        scv = self.sc[:].rearrange("p (c j) -> p c j", j=2)
        for piece in range(24):
            (w,), wt = self.wpiece([d['w_mod'][layer, :, piece * 256:(piece + 1) * 256]], rounded=False)
            for q2 in range(2):
                q = piece * 2 + q2
                for c in range(NCH):
                    k.emit('pe', lambda e, c=c, q2=q2, q=q: e.matmul(
                        ps[:, q * 2:q * 2 + 2], w[:, c, q2 * 128:(q2 + 1) * 128], scv[:, c, :],
                        start=(c == 0), stop=(c == NCH - 1)), [wt, self.sc_t], [pst])
            yield piece
        ob, wb = PC['b_mod']
        bm = self.PT[:, ob + layer * 48: ob + layer * 48 + 48]
        k.emit('dve', lambda e: e.tensor_tensor(
            out=MOD[:], in0=ps[:, 0:96].rearrange("p (q j) -> p q j", j=2),
            in1=bm.unsqueeze(2).broadcast_to([128, 48, 2]), op=ALU.add), [pst, self.PT_t], [MOD_t])
        og, wg = PC['norm_g']
        for s in range(2):
            g = self.PT[:, og + (layer * 2 + s) * 8: og + (layer * 2 + s) * 8 + 8]
            sh = MOD[:, s * 24 + 0: s * 24 + 8, :]
            scl = MOD[:, s * 24 + 8: s * 24 + 16, :]
            A = AB[:, s, 0, :, :]
            B = AB[:, s, 1, :, :]
            k.emit('dve', lambda e, scl=scl, A=A: e.tensor_scalar(
                out=A, in0=scl, scalar1=1.0, scalar2=32.0, op0=ALU.add, op1=ALU.mult), [MOD_t], [AB_t])
            k.emit('dve', lambda e, A=A, g=g: e.tensor_tensor(
                out=A, in0=A, in1=g.unsqueeze(2).broadcast_to([128, 8, 2]), op=ALU.mult), [AB_t, self.PT_t], [AB_t])
            k.emit('dve', lambda e, B=B, sh=sh: e.tensor_copy(out=B, in_=sh), [MOD_t], [AB_t])

    def gate(self, s, c, half):
        return self.MOD[:, s * 24 + 16 + c, half:half + 1]

    def rmsnorm_mod(self, layer, s, half):
        k = self.k
        with ExitStack() as es:
            sq = self.tmp(es, 'nsq', [128, NCH, 512], F32R)
            sq_t = Trk('nsq')
            tmp = self.tmp(es, 'ntmp', [128, NCH, 512], F32)
            tmp_t = Trk('ntmp')
            rstd = self.tmp(es, 'nrstd', [128, 512], F32)
            rstd_t = Trk('nrstd')
            for tt in range(2):
                xs = self.xT[half][:, :, tt * 512:(tt + 1) * 512]
                xs_t = [self.xT_t[half][c][tt] for c in range(NCH)]
                k.emit('act', lambda e: e.activation(out=sq[:], in_=xs, func=AF.Square), xs_t, [sq_t])
                ps, pst = self.ps[6], self.ps_t[6]
                for c in range(NCH):
                    k.emit('pe', lambda e, c=c: e.matmul(ps[:], self.onesR[:], sq[:, c, :], start=(c == 0), stop=(c == NCH - 1)),
                           [self.onesR_t, sq_t], [pst])
                k.emit('act', lambda e: e.activation(out=rstd[:], in_=ps[:], func=AF.Sqrt, bias=float(D * EPS), scale=1.0),
                       [pst], [rstd_t])
                k.emit('dve', lambda e: e.reciprocal(out=rstd[:], in_=rstd[:]), [rstd_t], [rstd_t])
                k.emit('dve', lambda e: e.tensor_tensor(out=tmp[:], in0=xs, in1=rstd[:].unsqueeze(1).broadcast_to([128, NCH, 512]),
                                                        op=ALU.mult), xs_t + [rstd_t], [tmp_t])
                for c in range(NCH):
                    A = self.AB[:, s, 0, c, half:half + 1]
                    B = self.AB[:, s, 1, c, half:half + 1]
                    out = self.hT[:, c, tt * 512:(tt + 1) * 512]
                    if c % 2 == 0:
                        k.emit('act', lambda e, c=c, A=A, B=B, out=out: e.activation(
                            out=out, in_=tmp[:, c, :], func=AF.Identity, bias=B, scale=A),
                            [tmp_t, self.AB_t], [self.hT_t[c][tt]])
                    else:
                        k.emit('dve', lambda e, c=c, A=A, B=B, out=out: e.tensor_scalar(
                            out=out, in0=tmp[:, c, :], scalar1=A, scalar2=B, op0=ALU.mult, op1=ALU.add),
                            [tmp_t, self.AB_t], [self.hT_t[c][tt]])
            k.barrier()

    def mixer(self, layer, half):
        kind = layer % 3
        ml = self.cfg.get('mixlist', (0, 1, 2))
        if kind not in ml:
            return
        if kind == 0:
            if half == 0:
                self.attn_prep(layer)
            self.attn(layer, half)
        elif kind == 1:
            self.hgrn(layer, half)
        else:
            self.gdn(layer, half)

    def out_pf(self, w_out_rows):
        return PF(self, [[w_out_rows[:, dp * 256:(dp + 1) * 256]] for dp in range(4)])

    def out_proj(self, w_out_rows, src, src_t, nh, half, banks, pf=None):
        k = self.k
        bi = 0
        pf = pf if pf is not None else self.out_pf(w_out_rows)
        for dp in range(4):
            (wo,), wot = pf.get(dp)
            for dmi in range(2):
                dm = dp * 2 + dmi
                for tt in range(2):
                    b = banks[bi % len(banks)]
                    bi += 1
                    ps, pst = self.ps[b], self.ps_t[b]
                    for hl in range(nh):
                        k.emit('pe', lambda e, ps=ps, hl=hl, dmi=dmi, tt=tt: e.matmul(
                            ps[:], wo[:, hl, dmi * 128:(dmi + 1) * 128], src[:, hl, tt * 512:(tt + 1) * 512],
                            start=(hl == 0), stop=(hl == nh - 1)), [wot, src_t[hl]], [pst])
                    xs = self.xT[half][:, dm, tt * 512:(tt + 1) * 512]
                    k.emit('dve', lambda e, ps=ps, xs=xs, dm=dm: e.scalar_tensor_tensor(
                        out=xs, in0=ps[:], scalar=self.gate(0, dm, half), in1=xs, op0=ALU.mult, op1=ALU.add),
                        [pst, self.MOD_t, self.xT_t[half][dm][tt]], [self.xT_t[half][dm][tt]])

    def head_norm(self, tmps, src, src_t, dst, dst_t, gcol, eps_total, bank, extra_mul=None, extra_t=None):
        k = self.k
        sq, sq_t, rs, rs_t = tmps
        ps, pst = self.ps[bank], self.ps_t[bank]
        for tt in range(2):
            sl = slice(tt * 512, (tt + 1) * 512)
            k.emit('act', lambda e, sl=sl: e.activation(out=sq[:], in_=src[:, sl], func=AF.Square), [src_t], [sq_t])
            k.emit('pe', lambda e: e.matmul(ps[:], self.onesR[:], sq[:], start=True, stop=True), [self.onesR_t, sq_t], [pst])
            k.emit('act', lambda e: e.activation(out=rs[:], in_=ps[:], func=AF.Sqrt, bias=float(eps_total), scale=1.0), [pst], [rs_t])
            k.emit('dve', lambda e: e.reciprocal(out=rs[:], in_=rs[:]), [rs_t], [rs_t])
            if extra_mul is not None:
                k.emit('dve', lambda e, sl=sl: e.tensor_tensor(out=rs[:], in0=rs[:], in1=extra_mul[:, sl], op=ALU.mult), [rs_t, extra_t], [rs_t])
            k.emit('dve', lambda e, sl=sl: e.scalar_tensor_tensor(out=dst[:, sl], in0=src[:, sl], scalar=gcol, in1=rs[:], op0=ALU.mult, op1=ALU.mult),
                   [src_t, rs_t, self.PT_t, self.AV_t] + ([self.HV_t] if hasattr(self, 'HV_t') else []), [dst_t])

    def norm_tmps(self, es):
        return (self.tmp(es, 'hsq', [128, 512], F32R), Trk('hsq'), self.tmp(es, 'hrs', [128, 512], F32), Trk('hrs'))

    def attn_prep(self, layer):
        k = self.k
        j = layer // 3
        lam_init = 0.8 - 0.6 * math.exp(-0.3 * layer)
        with ExitStack() as es:
            lt = self.tmp(es, 'ltab', [128, 512], F32)
            lt_t = self.ptrk('ltab')
            k.dma(lt[:], self.d['lamtab'][:, :], writes=[lt_t])
            pr = self.tmp(es, 'lpr', [128, 2, 64], F32)
            pr_t = Trk('lpr')
            sm = self.tmp(es, 'lsm', [128, 2], F32)
            sm_t = Trk('lsm')
            base = j * 256
            lq = lt[:, base:base + 256].rearrange("p (a r n) -> p a r n", a=2, r=2)
            k.emit('dve', lambda e: e.tensor_tensor(out=pr[:], in0=lq[:, :, 0, :], in1=lq[:, :, 1, :], op=ALU.mult), [lt_t], [pr_t])
            k.emit('dve', lambda e: e.reduce_sum(out=sm[:], in_=pr[:], axis=AX.X), [pr_t], [sm_t])
            k.emit('act', lambda e: e.activation(out=sm[:], in_=sm[:], func=AF.Exp), [sm_t], [sm_t])
            k.emit('dve', lambda e: e.tensor_tensor(out=self.AV[:, 0:1], in0=sm[:, 0:1], in1=sm[:, 1:2], op=ALU.subtract), [sm_t], [self.AV_t])
            k.emit('dve', lambda e: e.tensor_scalar(out=self.AV[:, 0:1], in0=self.AV[:, 0:1], scalar1=float(lam_init), scalar2=None, op0=ALU.add),
                   [self.AV_t], [self.AV_t])
            k.emit('dve', lambda e: e.tensor_scalar(out=self.AV[:, 1:2], in0=self.AV[:, 0:1], scalar1=-1.0, scalar2=None, op0=ALU.mult),
                   [self.AV_t], [self.AV_t])
            osl, _ = PC['subln']
            k.emit('dve', lambda e: e.tensor_scalar(out=self.AV[:, 2:3], in0=self.PT[:, osl + j:osl + j + 1],
                                                    scalar1=float((1.0 - lam_init) * math.sqrt(128.0)), scalar2=None, op0=ALU.mult),
                   [self.PT_t, self.AV_t], [self.AV_t])
            k.barrier()

    def attn(self, layer, half):
        k, d, nc = self.k, self.d, self.nc
        j = layer // 3
        w_in = d['attn_w_in']
        scale = 0.125
        nkc = 2 if half == 0 else 12
        with ExitStack() as es:
            GH = 2
            V = self.tmp(es, 'aV', [128, 8, GH * 128], F32R)
            V_t = self.ptrk('aV', 8)
            ntm = self.norm_tmps(es)
            agrp = self.tmp(es, 'agrp', [128, GH, TOK], F32R)
            agrp_t = trks('agrp', GH)
            QT = [self.tmp(es, f'aQ{i}', [128, TOK], F32R) for i in range(1)]
            QT_t = trks('aQ', 1)
            KT = [self.tmp(es, f'aK{i}', [128, TOK], F32R) for i in range(1)]
            KT_t = self.ptrk('aK', 1)
            Pt = [self.tmp(es, f'aP{i}', [128, 512], F32R) for i in range(2)]
            Pt_t = trks('aP', 2)
            att = self.tmp(es, 'att', [128, TOK], F32)
            att_t = Trk('att')
            Rr = self.tmp(es, 'aR', [128, 2, 512], F32)
            Rr_t = Trk('aR')
            Tt, Tt_t = Rr, Rr_t
            if half == 1:
                ropet = self.tmp(es, 'arope', [128, 2048], F32)
                ropet_t = self.ptrk('arope')
                k.dma(ropet[:], d['rope'][:, :], writes=[ropet_t])
                COS = ropet[:, 0:1024]
                SIN = ropet[:, 1024:2048]
                raw = [self.tmp(es, f'araw{i}', [128, 512], F32R) for i in range(1)]
                raw_t = trks('araw', 1)
                ri_ = 0
                t1 = self.tmp(es, 'at1', [128, 512], F32)
                t1_t = Trk('at1')
                kcr = self.tmp(es, 'akcr', [128, 512], F32R)
                kcr_t = Trk('akcr')
                vcr = self.tmp(es, 'avcr', [128, 4 * GH * 128], F32R)
                vcr_t = Trk('avcr')
            pi = 0
            pb = 0
            for grp in range(8 // GH):
                c0 = 2 * D + grp * 256
                gpf = PF(self, [[w_in[j, :, c0:c0 + 256]]] + [[w_in[j, :, (grp * GH + t) * 128:(grp * GH + t + 1) * 128],
                                                              w_in[j, :, D + (grp * GH + t) * 128:D + (grp * GH + t + 1) * 128]] for t in range(GH)])
                for piece in range(1):
                    (wv,), wvt = gpf.get(0)
                    for tile in range(8):
                        b = 6 + (pb % 2)
                        pb += 1
                        ps, pst = self.ps[b], self.ps_t[b]
                        for c in range(NCH):
                            k.emit('pe', lambda e, ps=ps, c=c, tile=tile: e.matmul(
                                ps[:, 0:256], self.hT[:, c, tile * 128:(tile + 1) * 128], wv[:, c, :],
                                start=(c == 0), stop=(c == NCH - 1)), [wvt, self.hT_t[c][tile // 4]], [pst])
                        self.copy(self.evac_eng(), V[:, tile, piece * 256:(piece + 1) * 256], ps[:, 0:256], [pst], [V_t[tile]])
                if half == 0:
                    for tile in range(8):
                        k.dma(d['vout'][j, tile * 128:(tile + 1) * 128, grp * 256:(grp + 1) * 256], V[:, tile, :].bitcast(F32), reads=[V_t[tile]])
                else:
                    (vcv,), _ = self.wpiece([d['cv'][j, :, grp * 256:(grp + 1) * 256]], dest=(vcr, vcr_t))
                for hl in range(GH):
                    hh = grp * GH + hl
                    qi = 0
                    Q, Q_t, Kk, K_t = QT[qi], QT_t[qi], KT[qi], KT_t[qi]
                    (wq, wk), wt = gpf.get(1 + hl)
                    for wi, (w, dst, dst_t) in enumerate(((wq, Q, Q_t), (wk, Kk, K_t))):
                        for tt in range(2):
                            sl = slice(tt * 512, (tt + 1) * 512)
                            b = 6 + (pb % 2)
                            pb += 1
                            ps, pst = self.ps[b], self.ps_t[b]
                            for c in range(NCH):
                                k.emit('pe', lambda e, ps=ps, w=w, c=c, sl=sl: e.matmul(
                                    ps[:], w[:, c, :], self.hT[:, c, sl],
                                    start=(c == 0), stop=(c == NCH - 1)), [wt, self.hT_t[c][tt]], [pst])
                            if half == 0:
                                self.copy(self.evac_eng(), dst[:, sl], ps[:], [pst], [dst_t])
                            else:
                                rw, rw_t = raw[0], raw_t[0]
                                ri_ += 1
                                self.copy('act', rw[:], ps[:], [pst], [rw_t])
                                b2 = 6 + (pb % 2)
                                pb += 1
                                ps2, ps2t = self.ps[b2], self.ps_t[b2]
                                k.emit('pe', lambda e, ps2=ps2, rw=rw: e.matmul(ps2[:], self.permR[:], rw[:], start=True, stop=True),
                                       [self.permR_t, rw_t], [ps2t])
                                k.emit('pool', lambda e, rw=rw, sl=sl: e.tensor_tensor(out=t1[:], in0=rw[:].bitcast(F32), in1=COS[:, sl], op=ALU.mult),
                                       [rw_t, ropet_t], [t1_t])
                                k.emit('dve', lambda e, ps2=ps2, sl=sl, dst=dst: e.tensor_tensor(out=dst[:, sl], in0=ps2[:], in1=SIN[:, sl], op=ALU.mult),
                                       [ps2t, ropet_t], [dst_t])
                                k.emit('dve', lambda e, dst=dst, sl=sl: e.tensor_tensor(out=dst[:, sl], in0=dst[:, sl].bitcast(F32), in1=t1[:], op=ALU.add),
                                       [t1_t, dst_t], [dst_t])
                    if half == 0:
                        k.dma(d['kout'][j, hh, :, :], Kk[:].bitcast(F32), reads=[K_t])
                    else:
                        self.wpiece([d['ck'][j, hh, :, :]], dest=(kcr, kcr_t))

                    def keyT(comp, kc, s=0):
                        r = slice(comp * 64, (comp + 1) * 64)
                        if half == 0:
                            return Kk[r, s * 256 + kc * 128: s * 256 + (kc + 1) * 128], K_t
                        if kc < 4:
                            return kcr[r, kc * 128:(kc + 1) * 128], kcr_t
                        return Kk[r, (kc - 4) * 128:(kc - 3) * 128], K_t

                    def valT(kc, s=0):
                        cs = slice(hl * 128, (hl + 1) * 128)
                        if half == 0:
                            return V[:, s * 2 + kc, cs], V_t[s * 2 + kc]
                        if kc < 4:
                            return vcv[:, kc, cs], vcr_t
                        return V[:, kc - 4, cs], V_t[kc - 4]

                    if half == 0:
                        qw = 256
                        units = [(s, 0) for s in range(4)]
                    else:
                        qw = 512
                        units = [(0, qt) for qt in range(2)]
                    for (s, qt) in units:
                        q0 = s * 256 if half == 0 else qt * 512
                        for comp in range(2):
                            r = slice(comp * 64, (comp + 1) * 64)
                            if half == 0:
                                ob, zb = 2, 3
                                osl = slice(comp * 256, (comp + 1) * 256)
                            else:
                                ob, zb = 2 + comp * 2, 3 + comp * 2
                                osl = slice(0, 512)
                            pO, pO_t = self.ps[ob], self.ps_t[ob]
                            pZ, pZ_t = self.ps[zb], self.ps_t[zb]
                            if half == 0:
                                sb_ = pi % 2
                                pS, pS_t = self.ps[sb_], self.ps_t[sb_]
                                P_, P_t = Pt[pi % 2], Pt_t[pi % 2]
                                pi += 1
                                for kc in range(2):
                                    kl, kl_t = keyT(comp, kc, s)
                                    k.emit('pe', lambda e, pS=pS, kl=kl, kc=kc, r=r, q0=q0: e.matmul(
                                        pS[:, kc * 256:(kc + 1) * 256], kl, Q[r, q0:q0 + 256], start=True, stop=True),
                                        [kl_t, Q_t], [pS_t])
                                k.emit('act', lambda e, pS=pS, P_=P_: e.activation(out=P_[:], in_=pS[:], func=AF.Exp, scale=scale), [pS_t], [P_t])
                                for kc in range(2):
                                    vl, vl_t = valT(kc, s)
                                    k.emit('pe', lambda e, pO=pO, vl=vl, P_=P_, kc=kc, osl=osl: e.matmul(
                                        pO[:, osl], vl, P_[:, kc * 256:(kc + 1) * 256], start=(kc == 0), stop=(kc == 1)),
                                        [vl_t, P_t], [pO_t])
                                for kc in range(2):
                                    k.emit('pe', lambda e, pZ=pZ, P_=P_, kc=kc, osl=osl: e.matmul(
                                        pZ[:, osl], self.onesR[:], P_[:, kc * 256:(kc + 1) * 256], start=(kc == 0), stop=(kc == 1)),
                                        [self.onesR_t, P_t], [pZ_t])
                            else:
                                for kc in range(nkc):
                                    sb_ = pi % 2
                                    pS, pS_t = self.ps[sb_], self.ps_t[sb_]
                                    P_, P_t = Pt[pi % 2], Pt_t[pi % 2]
                                    pi += 1
                                    kl, kl_t = keyT(comp, kc)
                                    k.emit('pe', lambda e, pS=pS, kl=kl, r=r, q0=q0: e.matmul(
                                        pS[:], kl, Q[r, q0:q0 + 512], start=True, stop=True), [kl_t, Q_t], [pS_t])
                                    k.emit('act', lambda e, pS=pS, P_=P_: e.activation(out=P_[:], in_=pS[:], func=AF.Exp, scale=scale), [pS_t], [P_t])
                                    vl, vl_t = valT(kc)
                                    k.emit('pe', lambda e, pO=pO, vl=vl, P_=P_, kc=kc: e.matmul(
                                        pO[:], vl, P_[:], start=(kc == 0), stop=(kc == nkc - 1)), [vl_t, P_t], [pO_t])
                                    k.emit('pe', lambda e, pZ=pZ, P_=P_, kc=kc: e.matmul(
                                        pZ[:], self.onesR[:], P_[:], start=(kc == 0), stop=(kc == nkc - 1)), [self.onesR_t, P_t], [pZ_t])
                        if half == 0:
                            pO, pO_t, pZ, pZ_t = self.ps[2], self.ps_t[2], self.ps[3], self.ps_t[3]
                            k.emit('dve', lambda e, pZ=pZ: e.reciprocal(out=Rr[:, 0, :], in_=pZ[:]), [pZ_t], [Rr_t])
                            k.emit('dve', lambda e, pO=pO: e.tensor_tensor(out=Tt[:, 0, :], in0=pO[:], in1=Rr[:, 0, :], op=ALU.mult), [pO_t, Rr_t], [Tt_t])
                            k.emit('dve', lambda e, q0=q0: e.scalar_tensor_tensor(
                                out=att[:, q0:q0 + 256], in0=Tt[:, 0, 256:512], scalar=self.AV[:, 1:2], in1=Tt[:, 0, 0:256],
                                op0=ALU.mult, op1=ALU.add), [Tt_t, self.AV_t], [att_t])
                        else:
                            for comp in range(2):
                                pO, pO_t = self.ps[2 + comp * 2], self.ps_t[2 + comp * 2]
                                pZ, pZ_t = self.ps[3 + comp * 2], self.ps_t[3 + comp * 2]
                                k.emit('dve', lambda e, pZ=pZ, comp=comp: e.reciprocal(out=Rr[:, comp, :], in_=pZ[:]), [pZ_t], [Rr_t])
                                k.emit('dve', lambda e, pO=pO, comp=comp: e.tensor_tensor(out=Tt[:, comp, :], in0=pO[:], in1=Rr[:, comp, :], op=ALU.mult),
                                       [pO_t, Rr_t], [Tt_t])
                            k.emit('dve', lambda e, q0=q0: e.scalar_tensor_tensor(
                                out=att[:, q0:q0 + 512], in0=Tt[:, 1, :], scalar=self.AV[:, 1:2], in1=Tt[:, 0, :],
                                op0=ALU.mult, op1=ALU.add), [Tt_t, self.AV_t], [att_t])
                    self.head_norm(ntm, att[:], att_t, agrp[:, hl, :], agrp_t[hl], self.AV[:, 2:3], 128.0 * 1e-5, 6 + (pb % 2))
                    pb += 1
                self.out_proj(d['attn_w_out'][j, grp * GH * 128:(grp + 1) * GH * 128, :], agrp, agrp_t, GH, half, [6, 7, 0, 1])
            k.barrier()

    def hgrn_prep(self, layer):
        k = self.k
        ol, _ = PC['hgrn_lb']
        self.HV = self.k.sb('HV', [128, 3, 8])
        self.HV_t = Trk('HV')
        with ExitStack() as es:
            ex = self.tmp(es, 'hex', [128, 4, 8], F32)
            ex_t = Trk('hex')
            tot = self.tmp(es, 'htot', [128, 8], F32)
            tot_t = Trk('htot')
            k.emit('act', lambda e: e.activation(out=ex[:], in_=self.PT[:, ol:ol + 32].rearrange("p (l c) -> p l c", l=4), func=AF.Exp),
                   [self.PT_t], [ex_t])
            k.emit('dve', lambda e: e.tensor_tensor(out=tot[:], in0=ex[:, 0, :], in1=ex[:, 1, :], op=ALU.add), [ex_t], [tot_t])
            for l in (2, 3):
                k.emit('dve', lambda e, l=l: e.tensor_tensor(out=tot[:], in0=tot[:], in1=ex[:, l, :], op=ALU.add), [ex_t, tot_t], [tot_t])
            k.emit('dve', lambda e: e.reciprocal(out=tot[:], in_=tot[:]), [tot_t], [tot_t])
            k.emit('dve', lambda e: e.tensor_copy(out=self.HV[:, 0, :], in_=ex[:, 1, :]), [ex_t], [self.HV_t])
            for l in range(2, layer + 1):
                k.emit('dve', lambda e, l=l: e.tensor_tensor(out=self.HV[:, 0, :], in0=self.HV[:, 0, :], in1=ex[:, l, :], op=ALU.add),
                       [ex_t, self.HV_t], [self.HV_t])
            k.emit('dve', lambda e: e.tensor_tensor(out=self.HV[:, 0, :], in0=self.HV[:, 0, :], in1=tot[:], op=ALU.mult), [tot_t, self.HV_t], [self.HV_t])
            k.emit('dve', lambda e: e.tensor_scalar(out=self.HV[:, 1, :], in0=self.HV[:, 0, :], scalar1=-1.0, scalar2=1.0, op0=ALU.mult, op1=ALU.add),
                   [self.HV_t], [self.HV_t])
            on, _ = PC['hgrn_norm']
            k.emit('dve', lambda e: e.tensor_scalar(out=self.HV[:, 2, 0:1], in0=self.PT[:, on:on + 1], scalar1=float(math.sqrt(128.0)), scalar2=None, op0=ALU.mult),
                   [self.PT_t, self.HV_t], [self.HV_t])
            k.barrier()

    def hgrn(self, layer, half):
        k, d, nc = self.k, self.d, self.nc
        j = layer // 3
        if half == 0:
            self.hgrn_prep(layer)
        w_in = d['hgrn_w_in']
        nseq = 4 if half == 0 else 1
        cps = 16 // nseq
        oo, _ = CC['ones']
        ONES = self.CT[:, oo:oo + 1].broadcast_to([128, TOK])
        oi, _ = CC['ident']
        IDENT = self.CT[:, oi:oi + 128]
        masks = []
        for nm in ('mask_f', 'mask_b'):
            om, _ = CC[nm]
            masks.append(self.CT[0:64, om:om + 256])
        with ExitStack() as es:
            V64 = self.tmp(es, 'hV', [64, 16, 128], F32)
            V64_t = trks('hV', 16)
            mix = self.tmp(es, 'hmix', [128, 1, TOK], F32R)
            mix_t = trks('hmix', 1)
            ntm = self.norm_tmps(es)
            qT = self.tmp(es, 'hq', [128, TOK], F32)
            qT_t = Trk('hq')
            gs, gs_t = qT, qT_t
            oT = self.tmp(es, 'ho', [128, TOK], F32)
            oT_t = Trk('ho')
            Fb = [self.tmp(es, f'hF{i}', [128, TOK], F32) for i in range(2)]
            Fb_t = trks('hF', 2)
            L = self.tmp(es, 'hL', [128, TOK], F32)
            L_t = Trk('hL')
            Gp = self.tmp(es, 'hGp', [128, 64 + TOK + 64], F32)
            Gp_t = Trk('hGp')
            E1 = self.tmp(es, 'hE1', [128, TOK], F32)
            E1_t = Trk('hE1')
            E2 = self.tmp(es, 'hE2', [128, TOK], F32)
            E2_t = Trk('hE2')
            Am = self.tmp(es, 'hAm', [64, 4, 64], F32)
            Am_t = Trk('hAm')
            Ktok = self.tmp(es, 'hKt', [64, 4, 128], F32)
            Ktok_t = Trk('hKt')
            Sb = [self.tmp(es, f'hS{i}', [128, 128], F32) for i in range(2)]
            Sb_t = trks('hS', 2)
            DK = self.tmp(es, 'hDK', [128, 3, 16], F32)
            DK_t = Trk('hDK')
            G3 = self.tmp(es, 'hG3', [128, 3, 16], F32)
            G3_t = Trk('hG3')
            Sp = [self.tmp(es, f'hSp{i}', [128, 128], F32) for i in range(4)]
            Sp_t = trks('hSp', 4)
            tS = self.tmp(es, 'htS', [128, 128], F32)
            tS_t = Trk('htS')
            spi = 0
            sout_t = self.ptrk('hso')
            if half == 0:
                sout = self.tmp(es, 'hso', [128, 4, 2, 128], F32)
            k.emit('dve', lambda e: e.memset(Gp[:], 0.0), [], [Gp_t])
            pb = 0
            for pair in range(4):
                for hl in range(2):
                    hh = pair * 2 + hl
                    hpf = PF(self, [[w_in[j, :, 3 * D + hh * 128:3 * D + (hh + 1) * 128]],
                                    [w_in[j, :, hh * 128:(hh + 1) * 128], w_in[j, :, D + hh * 128:D + (hh + 1) * 128]],
                                    [w_in[j, :, 2 * D + hh * 128:2 * D + (hh + 1) * 128]]])
                    (wi,), wit = hpf.get(0)
                    for tt in range(2):
                        sl = slice(tt * 512, (tt + 1) * 512)
                        b = 6 + (pb % 2)
                        pb += 1
                        ps, pst = self.ps[b], self.ps_t[b]
                        for c in range(NCH):
                            k.emit('pe', lambda e, ps=ps, c=c, sl=sl: e.matmul(ps[:], wi[:, c, :], self.hT[:, c, sl], start=(c == 0), stop=(c == NCH - 1)),
                                   [wit, self.hT_t[c][tt]], [pst])
                        self.copy(self.evac_eng(), L[:, sl], ps[:], [pst], [L_t])
                    for g4 in range(4):
                        b = 6 + (pb % 2)
                        pb += 1
                        ps, pst = self.ps[b], self.ps_t[b]
                        for q_ in range(4):
                            ch = g4 * 4 + q_
                            k.emit('pe', lambda e, ps=ps, q_=q_, ch=ch: e.transpose(ps[0:64, q_ * 128:(q_ + 1) * 128], L[:, ch * 64:(ch + 1) * 64], IDENT),
                                   [L_t, self.CT_t], [pst])
                        self.copy(self.evac_eng(), V64[:, g4 * 4:(g4 + 1) * 4, :].rearrange("p a n -> p (a n)"), ps[0:64, :], [pst], V64_t[g4 * 4:(g4 + 1) * 4])
                    lbc = self.HV[:, 0, hh:hh + 1]
                    omc = self.HV[:, 1, hh:hh + 1]
                    (wq, wzf), wt1 = hpf.get(1)
                    (wzb,), wt2 = hpf.get(2)
                    for (w, wt, kindp) in ((wq, wt1, 'q'), (wzf, wt1, 'zf'), (wzb, wt2, 'zb')):
                        for tt in range(2):
                            sl = slice(tt * 512, (tt + 1) * 512)
                            b = 6 + (pb % 2)
                            pb += 1
                            ps, pst = self.ps[b], self.ps_t[b]
                            for c in range(NCH):
                                k.emit('pe', lambda e, ps=ps, w=w, c=c, sl=sl: e.matmul(
                                    ps[:], w[:, c, :], self.hT[:, c, sl], start=(c == 0), stop=(c == NCH - 1)),
                                    [wt, self.hT_t[c][tt]], [pst])
                            if kindp == 'q':
                                k.emit('act', lambda e, ps=ps, sl=sl: e.activation(out=qT[:, sl], in_=ps[:], func=AF.Copy, scale=float(128.0 ** -0.5)),
                                       [pst], [qT_t])
                            elif kindp == 'g':
                                k.emit('act', lambda e, ps=ps, sl=sl: e.activation(out=gs[:, sl], in_=ps[:], func=AF.Silu), [pst], [gs_t])
                            else:
                                di = 0 if kindp == 'zf' else 1
                                k.emit('act', lambda e, ps=ps, sl=sl, di=di: e.activation(out=Fb[di][:, sl], in_=ps[:], func=AF.Sigmoid), [pst], [Fb_t[di]])
                    (wg,), wt3 = self.wpiece([w_in[j, :, 4 * D + hh * 128:4 * D + (hh + 1) * 128]])
                    opf = self.out_pf(d['hgrn_w_out'][j, hh * 128:(hh + 1) * 128, :])
                    opf.h[0] = self.wpiece(opf.specs[0])
                    for di in range(2):
                        F_, F_t = Fb[di], Fb_t[di]
                        k.emit('dve', lambda e, F_=F_: e.tensor_scalar(out=F_[:], in0=F_[:], scalar1=omc, scalar2=lbc, op0=ALU.mult, op1=ALU.add),
                               [F_t, self.HV_t], [F_t])
                        k.emit('act', lambda e, F_=F_: e.activation(out=L[:], in_=F_[:], func=AF.Ln), [F_t], [L_t])
                        k.emit('pool', lambda e, F_=F_: e.tensor_scalar(out=F_[:], in0=F_[:], scalar1=-1.0, scalar2=1.0, op0=ALU.mult, op1=ALU.add),
                               [F_t], [F_t])
                        k.emit('dve', lambda e: e.tensor_tensor_scan(out=Gp[:, 64:64 + TOK], data0=ONES, data1=L[:], initial=0.0,
                                                                     op0=ALU.mult, op1=ALU.add), [L_t, self.CT_t], [Gp_t])
                        Lv = L[:].rearrange("p (j n) -> p j n", n=64)
                        if di == 0:
                            gprev = Gp[:, 63:63 + TOK].rearrange("p (j n) -> p j n", n=64)[:, :, 0:1].broadcast_to([128, 16, 64])
                            gcur = Gp[:, 64:64 + TOK].rearrange("p (j n) -> p j n", n=64)
                            k.emit('dve', lambda e: e.tensor_tensor(out=Lv, in0=gcur, in1=gprev, op=ALU.subtract), [Gp_t], [L_t])
                        else:
                            gend = Gp[:, 127:127 + TOK].rearrange("p (j n) -> p j n", n=64)[:, :, 0:1].broadcast_to([128, 16, 64])
                            gsh = Gp[:, 63:63 + TOK].rearrange("p (j n) -> p j n", n=64)
                            k.emit('dve', lambda e: e.tensor_tensor(out=Lv, in0=gend, in1=gsh, op=ALU.subtract), [Gp_t], [L_t])
                        pos = 63 if di == 0 else 0
                        mid = 31 if di == 0 else 32
                        k.emit('pool', lambda e, pos=pos: e.tensor_copy(out=G3[:, 0, :], in_=Lv[:, :, pos]), [L_t], [G3_t])
                        k.emit('pool', lambda e, mid=mid: e.tensor_copy(out=G3[:, 1, :], in_=Lv[:, :, mid]), [L_t], [G3_t])
                        k.emit('dve', lambda e: e.tensor_tensor(out=G3[:, 2, :], in0=G3[:, 0, :], in1=G3[:, 1, :], op=ALU.subtract), [G3_t], [G3_t])
                        k.emit('act', lambda e: e.activation(out=DK[:], in_=G3[:], func=AF.Exp), [G3_t], [DK_t])
                        k.emit('dve', lambda e: e.tensor_tensor(out=Lv, in0=Lv, in1=G3[:, 1, :].unsqueeze(2).broadcast_to([128, 16, 64]), op=ALU.subtract),
                               [L_t, G3_t], [L_t])
                        k.emit('act', lambda e: e.activation(out=E1[:], in_=L[:], func=AF.Exp), [L_t], [E1_t])
                        k.emit('act', lambda e: e.activation(out=E2[:], in_=L[:], func=AF.Exp, scale=-1.0), [L_t], [E2_t])
                        k.emit('dve', lambda e: e.tensor_tensor(out=E1[:], in0=E1[:], in1=qT[:], op=ALU.mult), [E1_t, qT_t], [E1_t])
                        k.emit('pool', lambda e, F_=F_: e.tensor_tensor(out=E2[:], in0=E2[:], in1=F_[:], op=ALU.mult), [E2_t, F_t], [E2_t])
                        order = list(range(16)) if di == 0 else list(range(15, -1, -1))
                        if half == 1:
                            k.dma(Sb[0][:], d['st_hgrn'][di, hh, :, :], writes=[Sb_t[0]])
                        si = 0
                        psA, psA_t = self.ps[0], self.ps_t[0]
                        psT, psT_t = self.ps[1], self.ps_t[1]
                        psO, psO_t = self.ps[2], self.ps_t[2]

                        def step1(gi, di=di, order=order):
                            chs = order[gi * 4:(gi + 1) * 4]
                            lo = min(chs)
                            for ch in chs:
                                cs = slice(ch * 64, (ch + 1) * 64)
                                q_ = ch - lo
                                k.emit('pe', lambda e, cs=cs, q_=q_: e.matmul(psA[0:64, q_ * 64:(q_ + 1) * 64], E2[:, cs], E1[:, cs], start=True, stop=True),
                                       [E1_t, E2_t], [psA_t])
                                k.emit('pe', lambda e, cs=cs, q_=q_: e.transpose(psT[0:64, q_ * 128:(q_ + 1) * 128], E2[:, cs], IDENT),
                                       [E2_t, self.CT_t], [psT_t])
                            k.emit('dve', lambda e, di=di: e.tensor_tensor(out=Am[:].rearrange("p a n -> p (a n)"), in0=psA[0:64, 0:256], in1=masks[di], op=ALU.mult),
                                   [psA_t, self.CT_t], [Am_t])
                            k.emit('act', lambda e: e.activation(out=Ktok[:].rearrange("p a n -> p (a n)"), in_=psT[0:64, :], func=AF.Copy), [psT_t], [Ktok_t])

                        step1(0)
                        for gi in range(4):
                            chs = order[gi * 4:(gi + 1) * 4]
                            lo = min(chs)
                            bS = 3 + (gi % 2)
                            psS, psS_t = self.ps[bS], self.ps_t[bS]
                            info = []
                            for ch in chs:
                                loc = ch % cps
                                first = (loc == 0) if di == 0 else (loc == cps - 1)
                                last = (loc == cps - 1) if di == 0 else (loc == 0)
                                info.append((ch, ch - lo, first and half == 0, last, ch // cps))
                            for (ch, q_, zi, last, seq) in info:
                                k.emit('pe', lambda e, q_=q_, ch=ch: e.matmul(psS[:, q_ * 128:(q_ + 1) * 128], Ktok[:, q_, :], V64[:, ch, :], start=True, stop=True),
                                       [Ktok_t, V64_t[ch]], [psS_t])
                            for idx, (ch, q_, zi, last, seq) in enumerate(info):
                                k.emit('pe', lambda e, q_=q_, ch=ch, idx=idx, zi=zi: e.matmul(
                                    psO[:, q_ * 64:(q_ + 1) * 64], V64[:, ch, :], Am[:, q_, :], start=(idx == 0), stop=zi), [V64_t[ch], Am_t], [psO_t])
                            sps = {}
                            for (ch, q_, zi, last, seq) in info:
                                S_prev, S_prev_t = Sb[si], Sb_t[si]
                                if not zi:
                                    sp_, sp_t = Sp[q_], Sp_t[q_]
                                    k.emit('act', lambda e, sp_=sp_, S_prev=S_prev, ch=ch: e.activation(
                                        out=sp_[:], in_=S_prev[:], func=AF.Identity, scale=DK[:, 1, ch:ch + 1]), [S_prev_t, DK_t], [sp_t])
                                    sps[ch] = (sp_, sp_t)
                                if last and half == 0:
                                    dstS, dstS_t = sout[:, seq, di, :], sout_t
                                else:
                                    si = 1 - si
                                    dstS, dstS_t = Sb[si][:], Sb_t[si]
                                reg = psS[:, q_ * 128:(q_ + 1) * 128]
                                if zi:
                                    k.emit('act', lambda e, reg=reg, ch=ch, dstS=dstS: e.activation(
                                        out=dstS, in_=reg, func=AF.Identity, scale=DK[:, 2, ch:ch + 1]), [psS_t, DK_t], [dstS_t])
                                else:
                                    k.emit('act', lambda e, reg=reg, ch=ch: e.activation(
                                        out=tS[:], in_=reg, func=AF.Identity, scale=DK[:, 2, ch:ch + 1]), [psS_t, DK_t], [tS_t])
                                    k.emit('dve', lambda e, dstS=dstS, S_prev=S_prev, ch=ch: e.scalar_tensor_tensor(
                                        out=dstS, in0=S_prev[:], scalar=DK[:, 0, ch:ch + 1], in1=tS[:], op0=ALU.mult, op1=ALU.add),
                                        [S_prev_t, DK_t, tS_t], [dstS_t])
                            if gi + 1 < 4:
                                step1(gi + 1)
                            for (ch, q_, zi, last, seq) in info:
                                if zi:
                                    continue
                                sp_, sp_t = sps[ch]
                                cs = slice(ch * 64, (ch + 1) * 64)
                                k.emit('pe', lambda e, cs=cs, q_=q_, sp_=sp_: e.matmul(
                                    psO[:, q_ * 64:(q_ + 1) * 64], sp_[:], E1[:, cs], start=False, stop=True), [sp_t, E1_t], [psO_t])
                            osl = slice(lo * 64, lo * 64 + 256)
                            if di == 0:
                                k.emit('dve', lambda e, osl=osl: e.tensor_copy(out=oT[:, osl], in_=psO[:, 0:256]), [psO_t], [oT_t])
                            else:
                                k.emit('dve', lambda e, osl=osl: e.tensor_tensor(out=oT[:, osl], in0=oT[:, osl], in1=psO[:, 0:256], op=ALU.add), [psO_t, oT_t], [oT_t])
                    if half == 0:
                        k.dma(d['hgout'][:, :, hh, :, :].rearrange("s d k e -> k s d e"), sout[:], reads=[sout_t])
                    for tt in range(2):
                        sl = slice(tt * 512, (tt + 1) * 512)
                        ps, pst = self.ps[6 + tt], self.ps_t[6 + tt]
                        for c in range(NCH):
                            k.emit('pe', lambda e, ps=ps, c=c, sl=sl: e.matmul(ps[:], wg[:, c, :], self.hT[:, c, sl], start=(c == 0), stop=(c == NCH - 1)),
                                   [wt3, self.hT_t[c][tt]], [pst])
                        k.emit('act', lambda e, ps=ps, sl=sl: e.activation(out=gs[:, sl], in_=ps[:], func=AF.Silu), [pst], [gs_t])
                    self.head_norm(ntm, oT[:], oT_t, mix[:, 0, :], mix_t[0], self.HV[:, 2, 0:1], 128.0 * 1e-6, 5, extra_mul=gs[:], extra_t=gs_t)
                    self.out_proj(d['hgrn_w_out'][j, hh * 128:(hh + 1) * 128, :], mix, mix_t, 1, half, [6, 7], pf=opf)
            k.barrier()

    def gdn(self, layer, half):
        k, d, nc = self.k, self.d, self.nc
        w_in = d['gdn_w_in']
        nseq = 4 if half == 0 else 1
        cps = 16 // nseq
        seqlen = TOK // nseq
        CT = self.CT

        def cc(name, rows=64, w=None):
            o_, w_ = CC[name]
            return CT[0:rows, o_:o_ + (w or w_)]

        ONES_ROW = cc('ones', 128, 1).broadcast_to([128, TOK])
        IDENT = cc('ident', 128)
        ID64 = cc('ident', 64, 64)
        TRIF = cc('mask_f', 64, 64)
        TRIB = cc('mask_b', 64, 64)
        ONES64 = cc('ones', 64, 64)
        NEGU = [cc('negu_f'), cc('negu_b')]
        NEGL = [cc('negl_f'), cc('negl_b')]
        SNEG = [cc('sneg_f'), cc('sneg_b')]
        IDREP = cc('idrep')
        osel, _ = CC['sel']
        onsel, _ = CC['nsel']

        def SEL(kk, m):
            return CT[0:4, osel + kk * 128: osel + kk * 128 + m]

        def NSEL(kk):
            return CT[0:4, onsel + kk * 64: onsel + (kk + 1) * 64]

        oa, _ = PC['gdn_alog_col']
        odt, _ = PC['gdn_dt_col']
        oar, _ = PC['gdn_alog_row']
        odr, _ = PC['gdn_dt_row']
        ocv, _ = PC['gdn_conv']
        ogn, _ = PC['gdn_norm']
        scr_t = self.ptrk('gscr')
        with ExitStack() as es0:
            GC = self.tmp(es0, 'gGC', [64, 16, 16]); BE = self.tmp(es0, 'gBE', [64, 16, 16])
            NBE = self.tmp(es0, 'gNBE', [64, 16, 16]); C1 = self.tmp(es0, 'gC1', [64, 16, 16])
            WW = self.tmp(es0, 'gWW', [64, 16, 16])
            TB_t = Trk('gTB')
            GV = self.tmp(es0, 'gGV', [128, 20])
            GV_t = Trk('gGV')
            k.emit('act', lambda e: e.activation(out=GV[:, 0:1], in_=self.PT[:, oa:oa + 1], func=AF.Exp), [self.PT_t], [GV_t])
            k.emit('dve', lambda e: e.tensor_scalar(out=GV[:, 0:1], in0=GV[:, 0:1], scalar1=-1.0, scalar2=None, op0=ALU.mult), [GV_t], [GV_t])
            k.emit('act', lambda e: e.activation(out=GV[:, 4:20], in_=self.PT[:, oar:oar + 16], func=AF.Exp), [self.PT_t], [GV_t])
            k.emit('dve', lambda e: e.tensor_scalar(out=GV[:, 4:20], in0=GV[:, 4:20], scalar1=-1.0, scalar2=None, op0=ALU.mult), [GV_t], [GV_t])
            k.emit('dve', lambda e: e.tensor_scalar(out=GV[:, 1:2], in0=self.PT[:, ogn:ogn + 1], scalar1=float(math.sqrt(128.0)), scalar2=None, op0=ALU.mult),
                   [self.PT_t, GV_t], [GV_t])
            (wab,), wab_t = self.wpiece([w_in[0, :, 4 * D:4 * D + 32]], rounded=False)
            with ExitStack() as es:
                LA = self.tmp(es, 'gLA', [16, TOK]); LA_t = Trk('gLA')
                Gp = self.tmp(es, 'gGp', [16, 64 + TOK + 64]); Gp_t = Trk('gGp')
                GF = self.tmp(es, 'gGF', [16, TOK]); GF_t = self.ptrk('gGF')
                GB = self.tmp(es, 'gGB', [16, TOK]); GB_t = self.ptrk('gGB')
                BT = self.tmp(es, 'gBT', [16, TOK]); BT_t = self.ptrk('gBT')
                LAt = self.tmp(es, 'gLAt', [64, 16, 16]); LAt_t = Trk('gLAt')
                k.emit('dve', lambda e: e.memset(Gp[:], 0.0), [], [Gp_t])
                hTf = self.hT[:].bitcast(F32)
                for part in range(2):
                    for tt in range(2):
                        sl = slice(tt * 512, (tt + 1) * 512)
                        ps, pst = self.ps[6 + tt], self.ps_t[6 + tt]
                        for c in range(NCH):
                            k.emit('pe', lambda e, ps=ps, c=c, sl=sl, part=part: e.matmul(
                                ps[0:16, :], wab[:, c, part * 16:(part + 1) * 16], hTf[:, c, sl], start=(c == 0), stop=(c == NCH - 1)),
                                [wab_t, self.hT_t[c][tt]], [pst])
                        if part == 0:
                            k.emit('act', lambda e, ps=ps, sl=sl: e.activation(out=LA[:, sl], in_=ps[0:16, :], func=AF.Exp, bias=self.PT[0:16, odt:odt + 1]),
                                   [pst, self.PT_t], [LA_t])
                        else:
                            k.emit('act', lambda e, ps=ps, sl=sl: e.activation(out=BT[:, sl], in_=ps[0:16, :], func=AF.Sigmoid), [pst], [BT_t])
                k.emit('act', lambda e: e.activation(out=LA[:], in_=LA[:], func=AF.Ln, bias=1.0), [LA_t], [LA_t])
                k.emit('dve', lambda e: e.tensor_scalar(out=LA[:], in0=LA[:], scalar1=GV[0:16, 0:1], scalar2=None, op0=ALU.mult), [LA_t, GV_t], [LA_t])
                k.emit('dve', lambda e: e.tensor_tensor_scan(out=Gp[:, 64:64 + TOK], data0=ONES_ROW[0:16, :], data1=LA[:], initial=0.0,
                                                             op0=ALU.mult, op1=ALU.add), [LA_t, self.CT_t], [Gp_t])
                gprev = Gp[:, 63:63 + TOK].rearrange("p (j n) -> p j n", n=64)[:, :, 0:1].broadcast_to([16, 16, 64])
                gcur = Gp[:, 64:64 + TOK].rearrange("p (j n) -> p j n", n=64)
                k.emit('dve', lambda e: e.tensor_tensor(out=GF[:].rearrange("p (j n) -> p j n", n=64), in0=gcur, in1=gprev, op=ALU.subtract), [Gp_t], [GF_t])
                gend = Gp[:, 127:127 + TOK].rearrange("p (j n) -> p j n", n=64)[:, :, 0:1].broadcast_to([16, 16, 64])
                gsh = Gp[:, 63:63 + TOK].rearrange("p (j n) -> p j n", n=64)
                k.emit('dve', lambda e: e.tensor_tensor(out=GB[:].rearrange("p (j n) -> p j n", n=64), in0=gend, in1=gsh, op=ALU.subtract), [Gp_t], [GB_t])
                k.dma(d['gscr'][0:16, :], GF[:], reads=[GF_t], writes=[scr_t])
                k.dma(d['gscr'][16:32, :], GB[:], reads=[GB_t], writes=[scr_t])
                k.dma(d['gscr'][32:48, :], BT[:], reads=[BT_t], writes=[scr_t])
                ps, pst = self.ps[5], self.ps_t[5]
                for ch in range(16):
                    for c in range(NCH):
                        k.emit('pe', lambda e, c=c, ch=ch: e.matmul(
                            ps[0:64, ch * 32:(ch + 1) * 32], hTf[:, c, ch * 64:(ch + 1) * 64], wab[:, c, :], start=(c == 0), stop=(c == NCH - 1)),
                            [wab_t, self.hT_t[c][ch // 8]], [pst])
                pv = ps[0:64, :].rearrange("p (c n) -> p c n", n=32)
                k.emit('dve', lambda e: e.tensor_tensor(out=LAt[:], in0=pv[:, :, 0:16],
                                                        in1=self.PT[0:64, odr:odr + 16].unsqueeze(1).broadcast_to([64, 16, 16]), op=ALU.add),
                       [pst, self.PT_t], [LAt_t])
                k.emit('act', lambda e: e.activation(out=BE[:], in_=pv[:, :, 16:32], func=AF.Sigmoid), [pst], [TB_t])
                k.emit('act', lambda e: e.activation(out=LAt[:], in_=LAt[:], func=AF.Exp), [LAt_t], [LAt_t])
                k.emit('act', lambda e: e.activation(out=LAt[:], in_=LAt[:], func=AF.Ln, bias=1.0), [LAt_t], [LAt_t])
                k.emit('dve', lambda e: e.tensor_tensor(out=LAt[:], in0=LAt[:], in1=GV[0:64, 4:20].unsqueeze(1).broadcast_to([64, 16, 16]), op=ALU.mult),
                       [LAt_t, GV_t], [LAt_t])
                LAf = LAt[:].rearrange("p c n -> p (c n)")
                pF, pF_t = self.ps[0], self.ps_t[0]
                pB, pB_t = self.ps[1], self.ps_t[1]
                pT, pT_t = self.ps[2], self.ps_t[2]
                k.emit('pe', lambda e: e.matmul(pF[0:64, 0:256], TRIF, LAf, start=True, stop=True), [LAt_t, self.CT_t], [pF_t])
                k.emit('pe', lambda e: e.matmul(pB[0:64, 0:256], TRIB, LAf, start=True, stop=True), [LAt_t, self.CT_t], [pB_t])
                k.emit('pe', lambda e: e.matmul(pT[0:64, 0:256], ONES64, LAf, start=True, stop=True), [LAt_t, self.CT_t], [pT_t])
                pFv = pF[0:64, 0:256].rearrange("p (c n) -> p c n", n=16)
                pBv = pB[0:64, 0:256].rearrange("p (c n) -> p c n", n=16)
                pTv = pT[0:64, 0:256].rearrange("p (c n) -> p c n", n=16)
                k.emit('dve', lambda e: e.tensor_copy(out=GC[:, :, 0:8], in_=pFv[:, :, 0:8]), [pF_t], [TB_t])
                k.emit('dve', lambda e: e.tensor_copy(out=GC[:, :, 8:16], in_=pBv[:, :, 8:16]), [pB_t], [TB_t])
                k.emit('dve', lambda e: e.tensor_tensor(out=WW[:], in0=pTv, in1=GC[:], op=ALU.subtract), [pT_t, TB_t], [TB_t])
                k.emit('act', lambda e: e.activation(out=WW[:], in_=WW[:], func=AF.Exp), [TB_t], [TB_t])
                k.emit('act', lambda e: e.activation(out=C1[:], in_=GC[:], func=AF.Exp), [TB_t], [TB_t])
                k.emit('dve', lambda e: e.scalar_tensor_tensor(out=C1[:], in0=C1[:], scalar=-1.0, in1=BE[:], op0=ALU.mult, op1=ALU.mult), [TB_t], [TB_t])
                k.emit('dve', lambda e: e.tensor_scalar(out=NBE[:], in0=BE[:], scalar1=-1.0, scalar2=None, op0=ALU.mult), [TB_t], [TB_t])
                k.barrier()
            stop = self.cfg.get('gdn_stop', 9)
            if stop <= 1:
                return
            with ExitStack() as es:
                qn = self.tmp(es, 'gq', [128, TOK]); qn_t = Trk('gq')
                kn = self.tmp(es, 'gk', [128, TOK]); kn_t = Trk('gk')
                vT = self.tmp(es, 'gv', [128, TOK]); vT_t = Trk('gv')
                oT = self.tmp(es, 'go', [128, TOK]); oT_t = Trk('go')
                Qt = self.tmp(es, 'gQt', [128, TOK]); Qt_t = Trk('gQt')
                mix = self.tmp(es, 'gmix', [128, 1, TOK], F32R); mix_t = trks('gmix', 1)
                ntm = self.norm_tmps(es)
                HR4 = self.tmp(es, 'gHR', [4, TOK]); HR4_t = self.ptrk('gHR')
                bt = [self.tmp(es, f'gb{i}', [64, 4, 64]) for i in range(10)]
                bt_t = trks('gb', 10)
                ktok = self.tmp(es, 'gkt', [64, 4, 128]); ktok_t = Trk('gkt')
                vtok = self.tmp(es, 'gvt', [64, 4, 128]); vtok_t = Trk('gvt')
                sm = [self.tmp(es, f'gs{i}', [64, 128]) for i in range(4)]
                sm_t = trks('gs', 4)
                Sb = [self.tmp(es, f'gS{i}', [128, 128]) for i in range(2)]
                Sb_t = trks('gS', 2)
                DKg = self.tmp(es, 'gDK', [128, 16]); DKg_t = Trk('gDK')
                sout_t = self.ptrk('gso')
                if half == 0:
                    sout = self.tmp(es, 'gso', [128, 4, 2, 128])
                pb = 0
                for hh in range(8):
                    (wq, wk), wt1 = self.wpiece([w_in[0, :, hh * 128:(hh + 1) * 128], w_in[0, :, D + hh * 128:D + (hh + 1) * 128]])
                    (wv,), wt2 = self.wpiece([w_in[0, :, 2 * D + hh * 128:2 * D + (hh + 1) * 128]])
                    k.dma_group([(HR4[0:1, :], d['gscr'][hh:hh + 1, :]), (HR4[1:2, :], d['gscr'][24 + hh:25 + hh, :]),
                                 (HR4[2:3, :], d['gscr'][32 + hh:33 + hh, :]), (HR4[3:4, :], d['gscr'][40 + hh:41 + hh, :])], [HR4_t])
                    HR4_t.rs[scr_t.w[0]] = 0
                    for ti, (w, wt, dst, dst_t) in enumerate(((wq, wt1, qn, qn_t), (wk, wt1, kn, kn_t), (wv, wt2, vT, vT_t))):
                        fch = ti * 8 + hh
                        w0 = self.PT[:, ocv + 0 * 24 + fch: ocv + 0 * 24 + fch + 1]
                        w1 = self.PT[:, ocv + 1 * 24 + fch: ocv + 1 * 24 + fch + 1]
                        w2 = self.PT[:, ocv + 2 * 24 + fch: ocv + 2 * 24 + fch + 1]
                        pss = []
                        for tt in range(2):
                            sl = slice(tt * 512, (tt + 1) * 512)
                            b = 6 + tt
                            ps, pst = self.ps[b], self.ps_t[b]
                            pss.append((ps, pst))
                            for c in range(NCH):
                                k.emit('pe', lambda e, ps=ps, w=w, c=c, sl=sl: e.matmul(
                                    ps[:], w[:, c, :], self.hT[:, c, sl], start=(c == 0), stop=(c == NCH - 1)), [wt, self.hT_t[c][tt]], [pst])
                            k.emit('act', lambda e, ps=ps, sl=sl, dst=dst, w1=w1: e.activation(out=dst[:, sl], in_=ps[:], func=AF.Copy, scale=w1),
                                   [pst, self.PT_t], [dst_t])
                        for tt in range(2):
                            ps, pst = pss[tt]
                            sl_ = min(seqlen, 512)
                            ns = 512 // sl_
                            pv = ps[:].rearrange("p (s n) -> p s n", s=ns)
                            av = dst[:, tt * 512:(tt + 1) * 512].rearrange("p (s n) -> p s n", s=ns)
                            k.emit('dve', lambda e, pv=pv, av=av, w0=w0, sl_=sl_: e.scalar_tensor_tensor(
                                out=av[:, :, 1:sl_], in0=pv[:, :, 0:sl_ - 1], scalar=w0, in1=av[:, :, 1:sl_], op0=ALU.mult, op1=ALU.add),
                                [pst, self.PT_t, dst_t], [dst_t])
                            k.emit('dve', lambda e, pv=pv, av=av, w2=w2, sl_=sl_: e.scalar_tensor_tensor(
                                out=av[:, :, 0:sl_ - 1], in0=pv[:, :, 1:sl_], scalar=w2, in1=av[:, :, 0:sl_ - 1], op0=ALU.mult, op1=ALU.add),
                                [pst, self.PT_t, dst_t], [dst_t])
                        if seqlen > 512:
                            p0, p0t = pss[0]
                            p1, p1t = pss[1]
                            k.emit('dve', lambda e, p0=p0, dst=dst, w0=w0: e.scalar_tensor_tensor(
                                out=dst[:, 512:513], in0=p0[:, 511:512], scalar=w0, in1=dst[:, 512:513], op0=ALU.mult, op1=ALU.add),
                                [p0t, self.PT_t, dst_t], [dst_t])
                            k.emit('dve', lambda e, p1=p1, dst=dst, w2=w2: e.scalar_tensor_tensor(
                                out=dst[:, 511:512], in0=p1[:, 0:1], scalar=w2, in1=dst[:, 511:512], op0=ALU.mult, op1=ALU.add),
                                [p1t, self.PT_t, dst_t], [dst_t])
                        k.emit('act', lambda e, dst=dst: e.activation(out=dst[:], in_=dst[:], func=AF.Silu), [dst_t], [dst_t])
                        if ti < 2:
                            sq, sq_t, rs, rs_t = ntm
                            for tt in range(2):
                                sl = slice(tt * 512, (tt + 1) * 512)
                                ps, pst = self.ps[5], self.ps_t[5]
                                k.emit('act', lambda e, dst=dst, sl=sl: e.activation(out=sq[:], in_=dst[:, sl], func=AF.Square), [dst_t], [sq_t])
                                k.emit('pe', lambda e, ps=ps: e.matmul(ps[:], self.onesR[:], sq[:], start=True, stop=True), [self.onesR_t, sq_t], [pst])
                                k.emit('act', lambda e, ps=ps: e.activation(out=rs[:], in_=ps[:], func=AF.Sqrt, bias=1e-6, scale=1.0), [pst], [rs_t])
                                k.emit('dve', lambda e: e.reciprocal(out=rs[:], in_=rs[:]), [rs_t], [rs_t])
                                sc_ = float(128.0 ** -0.5) if ti == 0 else 1.0
                                k.emit('dve', lambda e, dst=dst, sl=sl, sc_=sc_: e.scalar_tensor_tensor(
                                    out=dst[:, sl], in0=dst[:, sl], scalar=sc_, in1=rs[:], op0=ALU.mult, op1=ALU.mult), [dst_t, rs_t], [dst_t])
                    if stop <= 2:
                        continue
                    for di in range(2):
                        col = di * 8 + hh
                        for tt in range(2):
                            sl = slice(tt * 512, (tt + 1) * 512)
                            ps, pst = self.ps[6 + tt], self.ps_t[6 + tt]
                            k.emit('pe', lambda e, ps=ps, sl=sl, di=di: e.matmul(ps[:], SEL(di, 128), HR4[0:4, sl], start=True, stop=True),
                                   [HR4_t, self.CT_t], [pst])
                            k.emit('act', lambda e, ps=ps, sl=sl: e.activation(out=Qt[:, sl], in_=ps[:], func=AF.Exp), [pst], [Qt_t])
                        pos = 63 if di == 0 else 0
                        k.emit('pool', lambda e, pos=pos: e.tensor_copy(out=DKg[:], in_=Qt[:].rearrange("p (j n) -> p j n", n=64)[:, :, pos]), [Qt_t], [DKg_t])
                        k.emit('dve', lambda e: e.tensor_tensor(out=Qt[:], in0=Qt[:], in1=qn[:], op=ALU.mult), [Qt_t, qn_t], [Qt_t])
                        border = list(range(4)) if di == 0 else list(range(3, -1, -1))
                        if half == 1:
                            k.dma(Sb[0][:], d['st_gdn'][di, hh, :, :], writes=[Sb_t[0]])
                        si = 0
                        for bi in border:
                            chs = [bi * 4 + q for q in range(4)]
                            if di == 1:
                                chs = chs[::-1]
                            T0 = bi * 256
                            bsl = slice(T0, T0 + 256)
                            gcol = GC[:, bi * 4:bi * 4 + 4, col:col + 1].broadcast_to([64, 4, 64])
                            nbcol = NBE[:, bi * 4:bi * 4 + 4, col:col + 1].broadcast_to([64, 4, 64])
                            b0, b0t = self.ps[0], self.ps_t[0]
                            b1, b1t = self.ps[1], self.ps_t[1]
                            b2, b2t = self.ps[2], self.ps_t[2]
                            b3, b3t = self.ps[3], self.ps_t[3]
                            R = lambda ps: ps[0:64, 0:256]
                            R3 = lambda ps: ps[0:64, 0:256].rearrange("p (a n) -> p a n", a=4)
                            F2 = lambda t: t[:].rearrange("p a n -> p (a n)")
                            deps_c = [HR4_t, self.CT_t]
                            k.emit('pe', lambda e, di=di: e.matmul(R(b0), SEL(di, 64), HR4[0:4, bsl], start=True, stop=True), deps_c, [b0t])
                            k.emit('pe', lambda e, di=di: e.matmul(R(b2), SEL(2 + di, 64), HR4[0:4, bsl], start=True, stop=True), deps_c, [b2t])
                            DT, DT_t = bt[0], bt_t[0]
                            Dl, Dl_t = bt[1], bt_t[1]
                            DBT, DBT_t = bt[2], bt_t[2]
                            k.emit('dve', lambda e: e.tensor_tensor(out=DT[:], in0=R3(b0), in1=gcol, op=ALU.subtract), [b0t, TB_t], [DT_t])
                            k.emit('pool', lambda e, di=di: e.tensor_tensor(out=F2(Dl), in0=NEGL[di], in1=F2(DT), op=ALU.subtract), [DT_t, self.CT_t], [Dl_t])
                            k.emit('pool', lambda e, di=di: e.tensor_tensor(out=F2(DT), in0=F2(DT), in1=NEGU[di], op=ALU.add), [DT_t, self.CT_t], [DT_t])
                            k.emit('act', lambda e: e.activation(out=DT[:], in_=DT[:], func=AF.Exp), [DT_t], [DT_t])
                            k.emit('act', lambda e: e.activation(out=Dl[:], in_=Dl[:], func=AF.Exp), [Dl_t], [Dl_t])
                            k.emit('dve', lambda e, di=di: e.tensor_tensor(out=F2(DBT), in0=R(b2), in1=SNEG[di], op=ALU.mult), [b2t, self.CT_t], [DBT_t])
                            k.emit('pool', lambda e: e.tensor_tensor(out=DBT[:], in0=DBT[:], in1=DT[:], op=ALU.mult), [DBT_t, DT_t], [DBT_t])
                            k.emit('pool', lambda e: e.tensor_tensor(out=Dl[:], in0=Dl[:], in1=nbcol, op=ALU.mult), [Dl_t, TB_t], [Dl_t])
                            for q in range(4):
                                cs = slice(T0 + q * 64, T0 + (q + 1) * 64)
                                k.emit('pe', lambda e, q=q, cs=cs: e.matmul(b0[0:64, q * 64:(q + 1) * 64], kn[:, cs], kn[:, cs], start=True, stop=True), [kn_t], [b0t])
                                k.emit('pe', lambda e, q=q, cs=cs: e.matmul(b1[0:64, q * 64:(q + 1) * 64], kn[:, cs], qn[:, cs], start=True, stop=True), [kn_t, qn_t], [b1t])
                                k.emit('pe', lambda e, q=q, cs=cs: e.transpose(b2[0:64, q * 128:(q + 1) * 128], kn[:, cs], IDENT), [kn_t, self.CT_t], [b2t])
                                k.emit('pe', lambda e, q=q, cs=cs: e.transpose(b3[0:64, q * 128:(q + 1) * 128], vT[:, cs], IDENT), [vT_t, self.CT_t], [b3t])
                            NT, NT_t = bt[3], bt_t[3]
                            Nm, Nm_t = bt[4], bt_t[4]
                            QKT, QKT_t = bt[5], bt_t[5]
                            XT, XT_t = bt[6], bt_t[6]
                            k.emit('dve', lambda e: e.tensor_tensor(out=F2(NT), in0=R(b0), in1=F2(DBT), op=ALU.mult), [b0t, DBT_t], [NT_t])
                            k.emit('dve', lambda e: e.tensor_tensor(out=F2(Nm), in0=R(b0), in1=F2(Dl), op=ALU.mult), [b0t, Dl_t], [Nm_t])
                            k.emit('dve', lambda e: e.tensor_tensor(out=F2(QKT), in0=R(b1), in1=F2(DT), op=ALU.mult), [b1t, DT_t], [QKT_t])
                            k.emit('pool', lambda e: e.tensor_tensor(out=F2(XT), in0=F2(NT), in1=IDREP, op=ALU.add), [NT_t, self.CT_t], [XT_t])
                            k.emit('act', lambda e: e.activation(out=F2(ktok), in_=b2[0:64, :], func=AF.Copy), [b2t], [ktok_t])
                            k.emit('act', lambda e: e.activation(out=F2(vtok), in_=b3[0:64, :], func=AF.Copy), [b3t], [vtok_t])
                            P, P_t, PT_, PT_t = Nm, Nm_t, NT, NT_t
                            pp = [(bt[7], bt_t[7], bt[8], bt_t[8]), (bt[9], bt_t[9], bt[1], bt_t[1])]
                            XTs = [(bt[6], bt_t[6]), (bt[0], bt_t[0])]
                            xi = 0
                            for m in range(1, 6):
                                nP, nP_t, nPT, nPT_t = pp[(m - 1) % 2] if m > 1 else pp[0]
                                if m >= 3:
                                    nP, nP_t, nPT, nPT_t = pp[(m - 1) % 2]
                                if m == 2:
                                    nP, nP_t, nPT, nPT_t = pp[1]
                                for q in range(4):
                                    k.emit('pe', lambda e, q=q, P=P, PT_=PT_: e.matmul(b0[0:64, q * 64:(q + 1) * 64], PT_[:, q, :], P[:, q, :], start=True, stop=True),
                                           [P_t, PT_t], [b0t])
                                    if m < 5:
                                        k.emit('pe', lambda e, q=q, P=P, PT_=PT_: e.matmul(b1[0:64, q * 64:(q + 1) * 64], P[:, q, :], PT_[:, q, :], start=True, stop=True),
                                               [P_t, PT_t], [b1t])
                                k.emit('act', lambda e, nP=nP: e.activation(out=F2(nP), in_=R(b0), func=AF.Copy), [b0t], [nP_t])
                                if m < 5:
                                    k.emit('dve', lambda e, nPT=nPT: e.tensor_copy(out=F2(nPT), in_=R(b1)), [b1t], [nPT_t])
                                cX, cX_t = XTs[xi]
                                nX, nX_t = XTs[1 - xi]
                                for q in range(4):
                                    k.emit('pe', lambda e, q=q, nP=nP, cX=cX: e.matmul(b2[0:64, q * 64:(q + 1) * 64], nP[:, q, :], cX[:, q, :], start=True, stop=True),
                                           [nP_t, cX_t], [b2t])
                                k.emit('dve', lambda e, nX=nX, cX=cX: e.tensor_tensor(out=F2(nX), in0=R(b2), in1=F2(cX), op=ALU.add), [b2t, cX_t], [nX_t])
                                xi = 1 - xi
                                P, P_t, PT_, PT_t = nP, nP_t, nPT, nPT_t
                            XTf, XTf_t = XTs[xi]
                            if stop <= 3:
                                continue
                            psO, psO_t = self.ps[6], self.ps_t[6]
                            for ch in chs:
                                q = ch - bi * 4
                                cs = slice(ch * 64, (ch + 1) * 64)
                                loc = ch % cps
                                first = (loc == 0) if di == 0 else (loc == cps - 1)
                                last = (loc == cps - 1) if di == 0 else (loc == 0)
                                seq = ch // cps
                                zero_init = first and half == 0
                                S_prev, S_prev_t = Sb[si], Sb_t[si]
                                tmpv, tmpv_t = sm[0], sm_t[0]
                                r_, r_t = sm[1], sm_t[1]
                                vn, vn_t = sm[2], sm_t[2]
                                vs, vs_t = sm[3], sm_t[3]
                                k.emit('act', lambda e, q=q, ch=ch: e.activation(out=tmpv[:], in_=vtok[:, q, :], func=AF.Copy, scale=BE[:, ch, col:col + 1]),
                                       [vtok_t, TB_t], [tmpv_t])
                                if zero_init:
                                    rr, rr_t = tmpv, tmpv_t
                                else:
                                    p4, p4t = self.ps[4], self.ps_t[4]
                                    k.emit('pe', lambda e, cs=cs, S_prev=S_prev: e.matmul(p4[0:64, 0:128], kn[:, cs], S_prev[:], start=True, stop=True),
                                           [kn_t, S_prev_t], [p4t])
                                    k.emit('dve', lambda e, ch=ch: e.scalar_tensor_tensor(out=r_[:], in0=p4[0:64, 0:128], scalar=C1[:, ch, col:col + 1], in1=tmpv[:],
                                                                                         op0=ALU.mult, op1=ALU.add), [p4t, TB_t, tmpv_t], [r_t])
                                    rr, rr_t = r_, r_t
                                lvl = self.cfg.get('seq_lvl', 9)
                                if lvl <= 1:
                                    continue
                                p5, p5t = self.ps[5], self.ps_t[5]
                                k.emit('pe', lambda e, q=q, rr=rr: e.matmul(p5[0:64, 0:128], XTf[:, q, :], rr[:], start=True, stop=True), [XTf_t, rr_t], [p5t])
                                if lvl <= 1.5:
                                    continue
                                k.emit('act', lambda e: e.activation(out=vn[:], in_=p5[0:64, 0:128], func=AF.Copy), [p5t], [vn_t])
                                if lvl <= 1.7:
                                    continue
                                k.emit('dve', lambda e, ch=ch: e.tensor_scalar(out=vs[:], in0=p5[0:64, 0:128], scalar1=WW[:, ch, col:col + 1], scalar2=None, op0=ALU.mult),
                                       [p5t, TB_t], [vs_t])
                                if lvl <= 2:
                                    continue
                                k.emit('pe', lambda e, q=q, zero_init=zero_init: e.matmul(psO[:, q * 64:(q + 1) * 64], vn[:], QKT[:, q, :], start=True, stop=zero_init),
                                       [vn_t, QKT_t], [psO_t])
                                if not zero_init:
                                    k.emit('pe', lambda e, q=q, cs=cs, S_prev=S_prev: e.matmul(psO[:, q * 64:(q + 1) * 64], S_prev[:], Qt[:, cs], start=False, stop=True),
                                           [S_prev_t, Qt_t], [psO_t])
                                if lvl <= 3:
                                    continue
                                p7, p7t = self.ps[7], self.ps_t[7]
                                k.emit('pe', lambda e, q=q: e.matmul(p7[:, 0:128], ktok[:, q, :], vs[:], start=True, stop=True), [ktok_t, vs_t], [p7t])
                                if last and half == 0:
                                    dstS, dstS_t = sout[:, seq, di, :], sout_t
                                else:
                                    si = 1 - si
                                    dstS, dstS_t = Sb[si][:], Sb_t[si]
                                if zero_init:
                                    k.emit('dve', lambda e, dstS=dstS: e.tensor_copy(out=dstS, in_=p7[:, 0:128]), [p7t], [dstS_t])
                                else:
                                    k.emit('dve', lambda e, dstS=dstS, S_prev=S_prev, ch=ch: e.scalar_tensor_tensor(
                                        out=dstS, in0=S_prev[:], scalar=DKg[:, ch:ch + 1], in1=p7[:, 0:128], op0=ALU.mult, op1=ALU.add),
                                        [p7t, S_prev_t, DKg_t], [dstS_t])
                            if di == 0:
                                k.emit('act', lambda e, bsl=bsl: e.activation(out=oT[:, bsl], in_=psO[:, 0:256], func=AF.Copy), [psO_t], [oT_t])
                            else:
                                k.emit('dve', lambda e, bsl=bsl: e.tensor_tensor(out=oT[:, bsl], in0=oT[:, bsl], in1=psO[:, 0:256], op=ALU.add), [psO_t, oT_t], [oT_t])
                    if stop <= 4:
                        continue
                    if half == 0:
                        k.dma(d['gdout'][:, :, hh, :, :].rearrange("s d k e -> k s d e"), sout[:], reads=[sout_t])
                    (wg,), wt3 = self.wpiece([w_in[0, :, 3 * D + hh * 128:3 * D + (hh + 1) * 128]])
                    for tt in range(2):
                        sl = slice(tt * 512, (tt + 1) * 512)
                        ps, pst = self.ps[6 + tt], self.ps_t[6 + tt]
                        for c in range(NCH):
                            k.emit('pe', lambda e, ps=ps, c=c, sl=sl: e.matmul(ps[:], wg[:, c, :], self.hT[:, c, sl], start=(c == 0), stop=(c == NCH - 1)),
                                   [wt3, self.hT_t[c][tt]], [pst])
                        k.emit('act', lambda e, ps=ps, sl=sl: e.activation(out=vT[:, sl], in_=ps[:], func=AF.Silu), [pst], [vT_t])
                    self.HV_t = GV_t
                    self.head_norm(ntm, oT[:], oT_t, mix[:, 0, :], mix_t[0], GV[:, 1:2], 128.0 * 1e-6, 5, extra_mul=vT[:], extra_t=vT_t)
                    self.out_proj(d['gdn_w_out'][0, hh * 128:(hh + 1) * 128, :], mix, mix_t, 1, half, [6, 7])
                k.barrier()

    def ffn(self, layer, half):
        k, d, nc = self.k, self.d, self.nc
        seqlen = 256 if half == 0 else 1024
        oc, _ = PC['ffn_conv']
        ob, _ = PC['ffn_conv_b']

        def cw(tap, fchunk):
            col = oc + (layer * 3 + tap) * 44 + fchunk
            return self.PT[:, col:col + 1]

        def cb(fchunk):
            col = ob + layer * 44 + fchunk
            return self.PT[:, col:col + 1]

        groups = [(0, 8), (8, 16), (16, 22)]
        specs = []
        pidx = {}
        for (g0, g1) in groups:
            for j in range(g0, g1):
                pidx[('u', j)] = len(specs)
                specs.append([d['ffn_w_up'][layer, :, j * 128:(j + 1) * 128], d['ffn_w_up'][layer, :, D_FF + j * 128:D_FF + (j + 1) * 128]])
            for dp in range(4):
                pidx[('d', g0, dp)] = len(specs)
                specs.append([d['ffn_w_down'][layer, g0 * 128:g1 * 128, dp * 256:(dp + 1) * 256]])
        pf = PF(self, specs)
        PAIRS = [(0, 1), (2, 3), (4, 5)] if self.modgen is not None else [(0, 1), (2, 3), (4, 5), (6, 7)]
        u = 0
        v = 0
        with ExitStack() as es:
            aT = self.tmp(es, 'aT', [128, 8, TOK], F32R)
            aT_t = trks('aT', 8)
            acc = [self.tmp(es, f'facc{i}', [128, 2, TOK], F32) for i in range(2)]
            acc_t = trks('facc', 2, 2, 2)
            for (g0, g1) in groups:
                for j in range(g0, g1):
                    slot = j - g0
                    (wv, wg), wt = pf.get(pidx[('u', j)])
                    ai = j % 2
                    ac = acc[ai]
                    banks = {}
                    for tt in range(2):
                        pair = PAIRS[u % len(PAIRS)]
                        u += 1
                        sl = slice(tt * 512, (tt + 1) * 512)
                        for vi, (w, fch) in enumerate(((wv, j), (wg, NFF + j))):
                            ps, pst = self.ps[pair[vi]], self.ps_t[pair[vi]]
                            banks[(vi, tt)] = (ps, pst)
                            for c in range(NCH):
                                k.emit('pe', lambda e, ps=ps, w=w, c=c, sl=sl: e.matmul(
                                    ps[:], w[:, c, :], self.hT[:, c, sl],
                                    start=(c == 0), stop=(c == NCH - 1)), [wt, self.hT_t[c][tt]], [pst])
                        for vi, fch in ((0, j), (1, NFF + j)):
                            ps, pst = banks[(vi, tt)]
                            at_ = acc_t[ai][vi][tt]
                            k.emit('act', lambda e, ps=ps, sl=sl, fch=fch, vi=vi: e.activation(
                                out=ac[:, vi, sl], in_=ps[:], func=AF.Identity, bias=cb(fch), scale=cw(1, fch)), [pst, self.PT_t], [at_])
                            sl_ = min(seqlen, 512)
                            ns = 512 // sl_
                            pv = ps[:].rearrange("p (s n) -> p s n", s=ns)
                            av = ac[:, vi, sl].rearrange("p (s n) -> p s n", s=ns)
                            k.emit('dve', lambda e, pv=pv, av=av, fch=fch, sl_=sl_: e.scalar_tensor_tensor(
                                out=av[:, :, 1:sl_], in0=pv[:, :, 0:sl_ - 1], scalar=cw(0, fch), in1=av[:, :, 1:sl_],
                                op0=ALU.mult, op1=ALU.add), [pst, self.PT_t, at_], [at_])
                            k.emit('dve', lambda e, pv=pv, av=av, fch=fch, sl_=sl_: e.scalar_tensor_tensor(
                                out=av[:, :, 0:sl_ - 1], in0=pv[:, :, 1:sl_], scalar=cw(2, fch), in1=av[:, :, 0:sl_ - 1],
                                op0=ALU.mult, op1=ALU.add), [pst, self.PT_t, at_], [at_])
                    if seqlen > 512:
                        for vi, fch in ((0, j), (1, NFF + j)):
                            p0, p0t = banks[(vi, 0)]
                            p1, p1t = banks[(vi, 1)]
                            k.emit('dve', lambda e, p0=p0, fch=fch, vi=vi: e.scalar_tensor_tensor(
                                out=ac[:, vi, 512:513], in0=p0[:, 511:512], scalar=cw(0, fch), in1=ac[:, vi, 512:513],
                                op0=ALU.mult, op1=ALU.add), [p0t, self.PT_t, acc_t[ai][vi][1]], [acc_t[ai][vi][1]])
                            k.emit('dve', lambda e, p1=p1, fch=fch, vi=vi: e.scalar_tensor_tensor(
                                out=ac[:, vi, 511:512], in0=p1[:, 0:1], scalar=cw(2, fch), in1=ac[:, vi, 511:512],
                                op0=ALU.mult, op1=ALU.add), [p1t, self.PT_t, acc_t[ai][vi][0]], [acc_t[ai][vi][0]])
                    k.emit('act', lambda e, ac=ac: e.activation(out=ac[:, 1, :], in_=ac[:, 1, :], func=AF.Silu), acc_t[ai][1], acc_t[ai][1])
                    k.emit('pool', lambda e, slot=slot, ac=ac: e.tensor_tensor(out=aT[:, slot, :], in0=ac[:, 0, :], in1=ac[:, 1, :], op=ALU.mult),
                           acc_t[ai], [aT_t[slot]])
                    if self.modgen is not None:
                        next(self.modgen, None)
                ng = g1 - g0
                for dp in range(4):
                    (wd,), wdt = pf.get(pidx[('d', g0, dp)])
                    for dmi in range(2):
                        dm = dp * 2 + dmi
                        for tt in range(2):
                            b = v % 6
                            v += 1
                            ps, pst = self.ps[b], self.ps_t[b]
                            for jj in range(ng):
                                k.emit('pe', lambda e, ps=ps, jj=jj, dmi=dmi, tt=tt: e.matmul(
                                    ps[:], wd[:, jj, dmi * 128:(dmi + 1) * 128], aT[:, jj, tt * 512:(tt + 1) * 512],
                                    start=(jj == 0), stop=(jj == ng - 1)), [wdt, aT_t[jj]], [pst])
                            xs = self.xT[half][:, dm, tt * 512:(tt + 1) * 512]
                            k.emit('dve', lambda e, ps=ps, xs=xs, dm=dm: e.scalar_tensor_tensor(
                                out=xs, in0=ps[:], scalar=self.gate(1, dm, half), in1=xs, op0=ALU.mult, op1=ALU.add),
                                [pst, self.MOD_t, self.xT_t[half][dm][tt]], [self.xT_t[half][dm][tt]])
            k.barrier()

    def final(self, cfg):
        k, d, nc = self.k, self.d, self.nc
        og, _ = PC['final_g']
        with ExitStack() as es:
            sq = self.tmp(es, 'fsq', [128, NCH, 512], F32R)
            sq_t = Trk('fsq')
            rstd = self.tmp(es, 'frstd', [128, 512], F32)
            rstd_t = Trk('frstd')
            yo = self.tmp(es, 'fyo', [128, NCH, 512], F32)
            yo_t = self.ptrk('fyo', NCH)
            for half in range(2):
                for tt in range(2):
                    xs = self.xT[half][:, :, tt * 512:(tt + 1) * 512]
                    xs_t = [self.xT_t[half][c][tt] for c in range(NCH)]
                    k.emit('act', lambda e: e.activation(out=sq[:], in_=xs, func=AF.Square), xs_t, [sq_t])
                    ps, pst = self.ps[6], self.ps_t[6]
                    for c in range(NCH):
                        k.emit('pe', lambda e, c=c: e.matmul(ps[:], self.onesR[:], sq[:, c, :], start=(c == 0), stop=(c == NCH - 1)),
                               [self.onesR_t, sq_t], [pst])
                    k.emit('act', lambda e: e.activation(out=rstd[:], in_=ps[:], func=AF.Sqrt, bias=float(D * EPS), scale=1.0),
                           [pst], [rstd_t])
                    k.emit('dve', lambda e: e.reciprocal(out=rstd[:], in_=rstd[:]), [rstd_t], [rstd_t])
                    for c in range(NCH):
                        g = self.PT[:, og + c:og + c + 1]
                        k.emit('dve', lambda e, c=c, g=g: e.scalar_tensor_tensor(
                            out=yo[:, c, :], in0=self.xT[half][:, c, tt * 512:(tt + 1) * 512], scalar=g, in1=rstd[:],
                            op0=ALU.mult, op1=ALU.mult), [self.xT_t[half][c][tt], self.PT_t, rstd_t], [yo_t[c]])
                        k.emit('act', lambda e, c=c: e.activation(out=yo[:, c, :], in_=yo[:, c, :], func=AF.Copy, scale=32.0),
                               [yo_t[c]], [yo_t[c]])
                        k.dma(d['yout'][half, c, :, tt * 512:(tt + 1) * 512], yo[:, c, :], reads=[yo_t[c]])


_CACHE = {}


def _get_prog(cfg):
    key = repr(sorted(cfg.items()))
    if key not in _CACHE:
        p = Prog(dict(cfg))
        with p.es:
            p.declare()
            p.build()
        _CACHE[key] = p
    return _CACHE[key]


def _pack(plan, arrays, n):
    out = np.zeros((128, n), np.float32)
    for c0, key in plan:
        off = c0
        for (name, offset, rstride, rows, ncols) in key:
            flat = arrays[name].reshape(-1)
            a2 = np.lib.stride_tricks.as_strided(flat[offset:], shape=(rows, ncols), strides=(rstride * 4, 4))
            kc = rows // 128
            assert off + kc * ncols <= n
            out[:, off:off + kc * ncols] = a2.reshape(kc, 128, ncols).transpose(1, 0, 2).reshape(128, kc * ncols)
            off += kc * ncols
    return out


def _run(inp, cfg):
    p = _get_prog(cfg)
    consts, rope_tab = _build_consts()
    f32 = lambda a: np.ascontiguousarray(np.asarray(a, np.float32))
    warr = {n: f32(inp[n]) for n in ('w_mod', 'ffn_w_up', 'ffn_w_down', 'attn_w_in', 'attn_w_out',
                                     'hgrn_w_in', 'hgrn_w_out', 'gdn_w_in', 'gdn_w_out')}
    shared = {'wpk': _pack(p.wplan['wpk'], warr, WCOLS)}
    xp = f32(inp['x_prompt'])
    xs = f32(inp['x_sample'])
    ck = f32(inp['cache_attn_k'])
    cv = f32(inp['cache_attn_v'])
    in_maps = []
    for core in range(N_CORES):
        m = dict(shared)
        a = xp[4 * core:4 * core + 4].reshape(TOK, D).T.reshape(8, 128, TOK)
        b = xs[core].T.reshape(8, 128, TOK)
        m['xin'] = np.ascontiguousarray(np.stack([a, b], axis=0))
        m['params'] = _build_params(core, inp)
        m['consts'] = consts
        m['rope'] = rope_tab
        m['lamtab'] = np.ascontiguousarray(np.broadcast_to(np.asarray(inp['attn_lambda'], np.float32).reshape(1, 512), (128, 512)))
        carr = {'ck': np.ascontiguousarray(ck[core].transpose(0, 2, 3, 1)),
                'cv': np.ascontiguousarray(cv[core].reshape(2, 512, D))}
        m['cpk'] = _pack(p.wplan['cpk'], carr, CCOLS)
        m['st_hgrn'] = f32(inp['state_hgrn'][core, 0])
        m['st_gdn'] = f32(inp['state_gdn'][core, 0])
        in_maps.append(m)
    ncores = cfg.get('ncores', N_CORES)
    res = run_bass_kernel_spmd(p.nc, in_maps[:ncores], core_ids=list(range(ncores)))
    R = list(res.results) + [res.results[0]] * (N_CORES - ncores)
    y_prompt = np.empty((32, 256, D), np.float32)
    y_sample = np.empty((8, 1024, D), np.float32)
    new_k = np.empty((32, 2, 256, 8, 128), np.float32)
    new_v = np.empty((32, 2, 256, 8, 128), np.float32)
    new_h = np.empty((32, 1, 2, 8, 128, 128), np.float32)
    new_g = np.empty((32, 1, 2, 8, 128, 128), np.float32)
    for core in range(N_CORES):
        r = R[core]
        yo = r['yout']
        y_prompt[4 * core:4 * core + 4] = yo[0].reshape(D, TOK).T.reshape(4, 256, D)
        y_sample[core] = yo[1].reshape(D, TOK).T
        ko = r['kout']
        new_k[4 * core:4 * core + 4] = ko.reshape(2, 8, 128, 4, 256).transpose(3, 0, 4, 1, 2)
        vo = r['vout']
        new_v[4 * core:4 * core + 4] = vo.reshape(2, 4, 256, 8, 128).transpose(1, 0, 2, 3, 4)
        new_h[4 * core:4 * core + 4, 0] = r['hgout']
        new_g[4 * core:4 * core + 4, 0] = r['gdout']
    return (y_prompt, y_sample, new_k, new_v, new_h, new_g)


def kernel(**inputs):
    return _run(inputs, {})
```

```python
import math
from contextlib import ExitStack

import numpy as np
import concourse.bass as bass
import concourse.mybir as mybir
from concourse.bass_utils import run_bass_kernel_spmd

F32 = mybir.dt.float32
F32R = mybir.dt.float32r
AF = mybir.ActivationFunctionType
ALU = mybir.AluOpType
AX = mybir.AxisListType

D = 1024
NCH = 8
TOK = 1024
DEPTH = 4
D_FF = 2816
NFF = 22
EPS = 1e-6
N_CORES = 8
WCOLS = (4 * 1024 * 6144 + 4 * 1024 * 5632 + 4 * 2816 * 1024 + 2 * 1024 * 3072 + 2 * 1024 * 1024 + 1024 * 5120 + 1024 * 1024
         + 1024 * 4128 + 1024 * 1024) // 128
CCOLS = (2 * 8 * 128 * 512 + 2 * 512 * 1024) // 128


class Cols:
    def __init__(self):
        self.off = {}
        self.n = 0

    def add(self, name, w):
        self.off[name] = (self.n, w)
        self.n += w

    def __getitem__(self, name):
        return self.off[name]


def _param_cols():
    c = Cols()
    c.add('cond', 16)
    c.add('norm_g', 64)
    c.add('b_mod', 192)
    c.add('final_g', 8)
    c.add('subln', 2)
    c.add('hgrn_lb', 32)
    c.add('hgrn_norm', 1)
    c.add('gdn_norm', 1)
    c.add('gdn_conv', 72)
    c.add('ffn_conv', 528)
    c.add('ffn_conv_b', 176)
    c.add('gdn_alog_col', 1)
    c.add('gdn_dt_col', 1)
    c.add('gdn_alog_row', 16)
    c.add('gdn_dt_row', 16)
    return c


PC = _param_cols()


def _fm(v):
    v = np.asarray(v, np.float32)
    r = v.reshape(-1, 128)
    return np.ascontiguousarray(r.T)


def _build_params(core, inp):
    P = np.zeros((128, PC.n), np.float32)

    def put(name, arr):
        o, w = PC[name]
        assert arr.shape == (128, w), (name, arr.shape, w)
        P[:, o:o + w] = arr

    cond = np.stack([inp['c_ctx'], inp['c'][core]], axis=0)
    put('cond', np.ascontiguousarray(cond.reshape(2, 8, 128).transpose(2, 1, 0)).reshape(128, 16))
    put('norm_g', _fm(inp['norm_g']))
    put('b_mod', _fm(inp['b_mod']))
    put('final_g', _fm(inp['final_g']))
    put('subln', _fm(inp['attn_subln']))
    put('hgrn_lb', _fm(inp['hgrn_lb']))
    put('hgrn_norm', _fm(inp['hgrn_norm']))
    put('gdn_norm', _fm(inp['gdn_norm']))
    put('gdn_conv', _fm(inp['gdn_conv']))
    put('ffn_conv', _fm(inp['ffn_conv']))
    put('ffn_conv_b', _fm(inp['ffn_conv_b']))
    al = np.zeros((128, 1), np.float32)
    al[:16, 0] = np.asarray(inp['gdn_a_log'], np.float32).reshape(16)
    put('gdn_alog_col', al)
    dtb = np.zeros((128, 1), np.float32)
    dtb[:16, 0] = np.asarray(inp['gdn_dt_bias'], np.float32).reshape(16)
    put('gdn_dt_col', dtb)
    put('gdn_alog_row', np.broadcast_to(np.asarray(inp['gdn_a_log'], np.float32).reshape(1, 16), (128, 16)))
    put('gdn_dt_row', np.broadcast_to(np.asarray(inp['gdn_dt_bias'], np.float32).reshape(1, 16), (128, 16)))
    return P


def _const_cols():
    c = Cols()
    c.add('ident', 128)
    c.add('ones', 128)
    c.add('perm', 128)
    c.add('mask_f', 256)
    c.add('mask_b', 256)
    c.add('negu_f', 256)
    c.add('negl_f', 256)
    c.add('negu_b', 256)
    c.add('negl_b', 256)
    c.add('sneg_f', 256)
    c.add('sneg_b', 256)
    c.add('idrep', 256)
    c.add('sel', 512)
    c.add('nsel', 128)
    return c


CC = _const_cols()


def _build_consts():
    C = np.zeros((128, CC.n), np.float32)
    o, w = CC['ident']
    C[:, o:o + w] = np.eye(128, dtype=np.float32)
    o, w = CC['ones']
    C[:, o:o + w] = 1.0
    o, w = CC['perm']
    tok = np.arange(1024)
    row = (tok // 64).astype(np.float32)
    col = (tok % 64).astype(np.float32)
    inv = (np.float32(10000.0) ** (-np.arange(16, dtype=np.float32) / np.float32(16))).astype(np.float32)
    ROPE = np.zeros((128, 2048), np.float32)
    oc_, os_ = 0, 1024
    for p in range(128):
        dd = p % 64
        i = dd % 32
        partner = p + 16 if i < 16 else p - 16
        C[partner, o + p] = 1.0
        f = i % 16
        pos = row if dd < 32 else col
        ang = (pos * inv[f]).astype(np.float32)
        ROPE[p, oc_:oc_ + 1024] = np.cos(ang)
        ROPE[p, os_:os_ + 1024] = np.sin(ang) * (-1.0 if i < 16 else 1.0)
    sidx = np.arange(64)[:, None]
    tidx = np.arange(64)[None, :]
    om, _ = CC['mask_f']
    C[:64, om:om + 256] = np.tile((sidx <= tidx).astype(np.float32), (1, 4))
    om, _ = CC['mask_b']
    C[:64, om:om + 256] = np.tile((sidx >= tidx).astype(np.float32), (1, 4))
    BIG = 30000.0
    p_, j_ = sidx, tidx

    def putm(name, m):
        o_, _ = CC[name]
        C[:64, o_:o_ + 256] = np.tile(m.astype(np.float32), (1, 4))

    putm('negu_f', np.where(j_ >= p_, 0.0, -BIG))
    putm('negl_f', np.where(j_ < p_, 0.0, -BIG))
    putm('negu_b', np.where(j_ <= p_, 0.0, -BIG))
    putm('negl_b', np.where(j_ > p_, 0.0, -BIG))
    putm('sneg_f', np.where(j_ > p_, -1.0, 0.0))
    putm('sneg_b', np.where(j_ < p_, -1.0, 0.0))
    putm('idrep', (j_ == p_))
    o_, _ = CC['sel']
    for kk in range(4):
        C[kk, o_ + kk * 128:o_ + (kk + 1) * 128] = 1.0
    o_, _ = CC['nsel']
    for kk in range(2):
        C[kk, o_ + kk * 64:o_ + (kk + 1) * 64] = -1.0
    return C, ROPE


class _GT:
    def __init__(self, name):
        self.name = name


class Geo:
    def __init__(self, name, shape, offset=0, pat=None):
        self.tensor = _GT(name)
        if pat is None:
            pat = []
            st = 1
            for n in reversed(shape):
                pat.insert(0, (st, n))
                st *= n
        self.ap = tuple(pat)
        self.offset = offset

    @property
    def shape(self):
        return tuple(n for _, n in self.ap)

    def __getitem__(self, key):
        if not isinstance(key, tuple):
            key = (key,)
        key = key + (slice(None),) * (len(self.ap) - len(key))
        off = self.offset
        pat = []
        for (st, n), kk in zip(self.ap, key):
            if isinstance(kk, int):
                off += st * kk
            else:
                a, b, _ = kk.indices(n)
                off += st * a
                pat.append((st, b - a))
        return Geo(self.tensor.name, None, off, pat)


class Trk:
    __slots__ = ('name', 'w', 'rs', 'sem', 'cnt', 'psum')

    def __init__(self, name):
        self.name = name
        self.psum = False
        self.w = None
        self.rs = {}
        self.sem = None
        self.cnt = 0


def trks(name, *dims):
    if len(dims) == 1:
        return [Trk(f'{name}{i}') for i in range(dims[0])]
    return [trks(f'{name}{i}_', *dims[1:]) for i in range(dims[0])]


def flat(x):
    if isinstance(x, Trk):
        return [x]
    out = []
    for e in x:
        out.extend(flat(e))
    return out


class KB:
    def __init__(self, nc, es):
        self.nc = nc
        self.es = es
        self.E = {'pe': nc.tensor, 'act': nc.scalar, 'dve': nc.vector, 'pool': nc.gpsimd, 'sp': nc.sync}
        self.sem = {e: es.enter_context(nc.semaphore(f's_{e}')) for e in self.E}
        self.cnt = {e: 0 for e in self.E}
        self.waited = {e: {} for e in self.E}
        self.dma_sems = []
        self.n_ins = 0
        self.n_wait = 0

    def _wait(self, eng, deps):
        need = {}
        for key, val in deps:
            if key == 'pe' and eng == 'pe':
                continue
            if need.get(key, 0) < val:
                need[key] = val
        wt = self.waited[eng]
        for key, val in need.items():
            if wt.get(key, 0) >= val:
                continue
            sem = self.sem[key] if isinstance(key, str) else key
            self.E[eng].wait_ge(sem, val)
            self.n_wait += 1
            wt[key] = val

    def _deps(self, reads, writes):
        deps = []
        for t in reads:
            if t.w is not None:
                deps.append(t.w)
            if t.psum:
                deps.extend(t.rs.items())
        for t in writes:
            if t.w is not None:
                deps.append(t.w)
            deps.extend(t.rs.items())
        return deps

    def _mark(self, tok, reads, writes):
        k, v = tok
        for t in reads:
            if t.rs.get(k, 0) < v:
                t.rs[k] = v
        for t in writes:
            t.w = tok
            t.rs = {}

    def emit(self, eng, fn, reads=(), writes=()):
        reads = flat(reads)
        writes = flat(writes)
        self._wait(eng, self._deps(reads, writes))
        ins = fn(self.E[eng])
        self.cnt[eng] += 1
        ins.then_inc(self.sem[eng], 1)
        self.n_ins += 1
        tok = (eng, self.cnt[eng])
        self._mark(tok, reads, writes)
        return tok

    def dma(self, out, in_, reads=(), writes=(), q='sp'):
        reads = flat(reads)
        writes = flat(writes)
        self._wait(q, self._deps(reads, writes))
        owner = (writes + reads)[0]
        if owner.sem is None:
            owner.sem = self.es.enter_context(self.nc.semaphore(f'd_{owner.name}'))
            self.dma_sems.append(owner)
        self.E[q].dma_start(out=out, in_=in_).then_inc(owner.sem, 16)
        owner.cnt += 16
        self.n_ins += 1
        tok = (owner.sem, owner.cnt)
        self._mark(tok, reads, writes)
        return tok

    def dma_group(self, pairs, writes, q='sp'):
        writes = flat(writes)
        self._wait(q, self._deps([], writes))
        owner = writes[0]
        if owner.sem is None:
            owner.sem = self.es.enter_context(self.nc.semaphore(f'd_{owner.name}'))
            self.dma_sems.append(owner)
        for out, in_ in pairs:
            self.E[q].dma_start(out=out, in_=in_).then_inc(owner.sem, 16)
            owner.cnt += 16
            self.n_ins += 1
        tok = (owner.sem, owner.cnt)
        self._mark(tok, [], writes)
        return tok

    def barrier(self):
        for e in self.E:
            deps = [(o, self.cnt[o]) for o in self.E if o != e and self.cnt[o] > 0]
            deps += [(t.sem, t.cnt) for t in self.dma_sems]
            wt = self.waited[e]
            for key, val in deps:
                if wt.get(key, 0) >= val:
                    continue
                sem = self.sem[key] if isinstance(key, str) else key
                self.E[e].wait_ge(sem, val)
                self.n_wait += 1
                wt[key] = val

    def finish(self):
        deps = [(t.sem, t.cnt) for t in self.dma_sems]
        deps += [(o, self.cnt[o]) for o in self.E if o != 'sp' and self.cnt[o] > 0]
        self._wait('sp', deps)

    def sb(self, name, shape, dt=F32):
        return self.es.enter_context(self.nc.sbuf_tensor(name, list(shape), dt))


class PF:
    def __init__(self, prog, specs):
        self.p, self.specs, self.h = prog, specs, {}

    def get(self, i):
        for t in (i, i + 1):
            if t < len(self.specs) and t not in self.h:
                self.h[t] = self.p.wpiece(self.specs[t])
        return self.h.pop(i)


class Prog:
    def __init__(self, cfg):
        self.cfg = cfg
        nc = bass.Bass("TRN2", target_bir_lowering=False)
        self.nc = nc
        self.es = ExitStack()
        self.k = KB(nc, self.es)
        self.rr = 0
        self.wplan = {'wpk': [], 'cpk': []}
        self.wcols = {'wpk': 0, 'cpk': 0}
        self.wkeys = {}

    def declare(self):
        nc = self.nc

        def din(name, shape):
            return nc.dram_tensor(name, list(shape), F32, kind="ExternalInput").ap()

        def dout(name, shape):
            return nc.dram_tensor(name, list(shape), F32, kind="ExternalOutput").ap()

        d = {}
        d['xin'] = din('xin', [2, 8, 128, TOK])
        d['params'] = din('params', [128, PC.n])
        d['consts'] = din('consts', [128, CC.n])
        d['rope'] = din('rope', [128, 2048])
        d['wpk'] = din('wpk', [128, WCOLS])
        d['cpk'] = din('cpk', [128, CCOLS])
        d['lamtab'] = din('lamtab', [128, 512])
        d['gscr'] = nc.dram_tensor('gscr', [48, TOK], F32, kind="Internal").ap()
        d['w_mod'] = Geo('w_mod', [4, D, 6 * D])
        d['ffn_w_up'] = Geo('ffn_w_up', [4, D, 2 * D_FF])
        d['ffn_w_down'] = Geo('ffn_w_down', [4, D_FF, D])
        d['attn_w_in'] = Geo('attn_w_in', [2, D, 3 * D])
        d['attn_w_out'] = Geo('attn_w_out', [2, D, D])
        d['hgrn_w_in'] = Geo('hgrn_w_in', [1, D, 5 * D])
        d['hgrn_w_out'] = Geo('hgrn_w_out', [1, D, D])
        d['gdn_w_in'] = Geo('gdn_w_in', [1, D, 4 * D + 32])
        d['gdn_w_out'] = Geo('gdn_w_out', [1, D, D])
        d['ck'] = Geo('ck', [2, 8, 128, 512])
        d['cv'] = Geo('cv', [2, 512, D])
        d['st_hgrn'] = din('st_hgrn', [2, 8, 128, 128])
        d['st_gdn'] = din('st_gdn', [2, 8, 128, 128])
        d['yout'] = dout('yout', [2, 8, 128, TOK])
        d['kout'] = dout('kout', [2, 8, 128, TOK])
        d['vout'] = dout('vout', [2, TOK, D])
        d['hgout'] = dout('hgout', [4, 2, 8, 128, 128])
        d['gdout'] = dout('gdout', [4, 2, 8, 128, 128])
        self.d = d

    def ptrk(self, name, n=None):
        if not hasattr(self, '_pt'):
            self._pt = {}
        if name not in self._pt:
            self._pt[name] = Trk(name) if n is None else trks(name, n)
        return self._pt[name]

    def tmp(self, es, name, shape, dt=F32):
        self._uid = getattr(self, '_uid', 0) + 1
        return es.enter_context(self.nc.sbuf_tensor(f'{name}_{self._uid}', list(shape), dt))

    def evac_eng(self):
        self.rr += 1
        return 'act' if self.rr % 2 else 'dve'

    def copy(self, eng, out, in_, reads, writes):
        if eng == 'act':
            return self.k.emit('act', lambda e: e.activation(out=out, in_=in_, func=AF.Copy), reads, writes)
        return self.k.emit(eng, lambda e: e.tensor_copy(out=out, in_=in_), reads, writes)

    def init_wpool(self):
        k = self.k
        self.ws = [k.sb(f'ws{i}', [128, 2048], F32) for i in range(2)]
        self.ws_t = trks('ws', 2)
        self.wr = [k.sb(f'wr{i}', [128, 2048], F32R) for i in range(2)]
        self.wr_t = trks('wr', 2)
        self.ws_i = 0
        self.wr_i = 0

    def wpiece(self, segs, rounded=True, dest=None):
        k = self.k
        si = self.ws_i
        self.ws_i = (si + 1) % 2
        st, stt = self.ws[si], self.ws_t[si]
        off = 0
        views = []
        key = []
        percore = False
        for ap in segs:
            rows, ncols = ap.shape
            kc = rows // 128
            pat = tuple((int(a), int(b)) for a, b in ap.ap)
            assert len(pat) == 2 and pat[1][0] == 1, pat
            name = ap.tensor.name
            percore = percore or name in ('ck', 'cv')
            key.append((name, int(ap.offset), pat[0][0], rows, ncols))
            views.append((off, kc, ncols))
            off += kc * ncols
        key = tuple(key)
        pk = 'cpk' if percore else 'wpk'
        if key not in self.wkeys:
            self.wkeys[key] = (pk, self.wcols[pk])
            self.wplan[pk].append((self.wcols[pk], key))
            self.wcols[pk] += off
        pk, c0 = self.wkeys[key]
        k.dma(st[:, 0:off], self.d[pk][:, c0:c0 + off], writes=[stt])
        if not rounded:
            outs = [st[:, o:o + kc * n].rearrange("p (c n) -> p c n", c=kc) for (o, kc, n) in views]
            return outs, stt
        if dest is not None:
            rt, rtt = dest
        else:
            ri = self.wr_i
            self.wr_i = (ri + 1) % 2
            rt, rtt = self.wr[ri], self.wr_t[ri]
        k.emit('act', lambda e: e.activation(out=rt[:, 0:off], in_=st[:, 0:off], func=AF.Copy), [stt], [rtt])
        outs = [rt[:, o:o + kc * n].rearrange("p (c n) -> p c n", c=kc) for (o, kc, n) in views]
        return outs, rtt

    def build(self):
        nc, k, d, cfg = self.nc, self.k, self.d, self.cfg
        self.ps = [self.es.enter_context(nc.psum_tensor(f'ps{i}', [128, 512], F32)) for i in range(8)]
        self.ps_t = trks('ps', 8)
        for t in self.ps_t:
            t.psum = True
        self.PT = k.sb('PT', [128, PC.n])
        self.PT_t = Trk('PT')
        self.CT = k.sb('CT', [128, CC.n])
        self.CT_t = Trk('CT')
        self.onesR = k.sb('onesR', [128, 128], F32R)
        self.onesR_t = Trk('onesR')
        self.identR = k.sb('identR', [128, 128], F32R)
        self.identR_t = Trk('identR')
        self.xT = [k.sb(f'xT{h}', [128, NCH, TOK]) for h in range(2)]
        self.xT_t = trks('xT', 2, NCH, 2)
        self.hT = k.sb('hT', [128, NCH, TOK], F32R)
        self.hT_t = trks('hT', NCH, 2)
        self.permR = k.sb('permR', [128, 128], F32R)
        self.permR_t = Trk('permR')
        self.AV = k.sb('AV', [128, 8])
        self.AV_t = Trk('AV')
        self.MODs = [k.sb(f'MOD{i}', [128, 48, 2]) for i in range(2)]
        self.MODs_t = trks('MOD', 2)
        self.ABs = [k.sb(f'AB{i}', [128, 2, 2, 8, 2]) for i in range(2)]
        self.ABs_t = trks('AB', 2)
        self.modgen = None
        self.sc = k.sb('sc', [128, 16])
        self.sc_t = Trk('sc')
        self.init_wpool()

        k.dma(self.PT[:], d['params'][:, :], writes=[self.PT_t])
        k.dma(self.CT[:], d['consts'][:, :], writes=[self.CT_t])
        for h in range(2):
            k.dma_group([(self.xT[h][:, c, :], d['xin'][h, c, :, :]) for c in range(NCH)], self.xT_t[h])
        o, w = CC['ones']
        k.emit('dve', lambda e: e.tensor_copy(out=self.onesR[:], in_=self.CT[:, o:o + w]), [self.CT_t], [self.onesR_t])
        o2, w2 = CC['ident']
        k.emit('dve', lambda e: e.tensor_copy(out=self.identR[:], in_=self.CT[:, o2:o2 + w2]), [self.CT_t], [self.identR_t])
        o3, w3 = CC['perm']
        k.emit('dve', lambda e: e.tensor_copy(out=self.permR[:], in_=self.CT[:, o3:o3 + w3]), [self.CT_t], [self.permR_t])
        oc, wc = PC['cond']
        k.emit('act', lambda e: e.activation(out=self.sc[:], in_=self.PT[:, oc:oc + wc], func=AF.Silu), [self.PT_t], [self.sc_t])

        nl = cfg.get('layers', DEPTH)
        for _ in self.modulation(0):
            pass
        for layer in range(nl):
            p = layer % 2
            self.MOD, self.MOD_t, self.AB, self.AB_t = self.MODs[p], self.MODs_t[p], self.ABs[p], self.ABs_t[p]
            for half in range(2):
                if cfg.get('mixers', True):
                    self.rmsnorm_mod(layer, 0, half)
                    self.mixer(layer, half)
                if half == 1 and layer + 1 < nl:
                    self.modgen = self.modulation(layer + 1)
                if cfg.get('ffn', True):
                    self.rmsnorm_mod(layer, 1, half)
                    self.ffn(layer, half)
                if self.modgen is not None:
                    for _ in self.modgen:
                        pass
                    self.modgen = None
        self.final(cfg)
        k.finish()

    def modulation(self, layer):
        k, d = self.k, self.d
        p = layer % 2
        MOD, MOD_t, AB, AB_t = self.MODs[p], self.MODs_t[p], self.ABs[p], self.ABs_t[p]
        ps, pst = self.ps[7], self.ps_t[7]
        scv = self.sc[:].rearrange("p (c j) -> p c j", j=2)
        for piece in range(24):
            (w,), wt = self.wpiece([d['w_mod'][layer, :, piece * 256:(piece + 1) * 256]], rounded=False)
            for q2 in range(2):
                q = piece * 2 + q2
                for c in range(NCH):
                    k.emit('pe', lambda e, c=c, q2=q2, q=q: e.matmul(
                        ps[:, q * 2:q * 2 + 2], w[:, c, q2 * 128:(q2 + 1) * 128], scv[:, c, :],
                        start=(c == 0), stop=(c == NCH - 1)), [wt, self.sc_t], [pst])
            yield piece
        ob, wb = PC['b_mod']
        bm = self.PT[:, ob + layer * 48: ob + layer * 48 + 48]
        k.emit('dve', lambda e: e.tensor_tensor(
            out=MOD[:], in0=ps[:, 0:96].rearrange("p (q j) -> p q j", j=2),
            in1=bm.unsqueeze(2).broadcast_to([128, 48, 2]), op=ALU.add), [pst, self.PT_t], [MOD_t])
        og, wg = PC['norm_g']
        for s in range(2):
            g = self.PT[:, og + (layer * 2 + s) * 8: og + (layer * 2 + s) * 8 + 8]
            sh = MOD[:, s * 24 + 0: s * 24 + 8, :]
            scl = MOD[:, s * 24 + 8: s * 24 + 16, :]
            A = AB[:, s, 0, :, :]
            B = AB[:, s, 1, :, :]
            k.emit('dve', lambda e, scl=scl, A=A: e.tensor_scalar(
                out=A, in0=scl, scalar1=1.0, scalar2=32.0, op0=ALU.add, op1=ALU.mult), [MOD_t], [AB_t])
            k.emit('dve', lambda e, A=A, g=g: e.tensor_tensor(
                out=A, in0=A, in1=g.unsqueeze(2).broadcast_to([128, 8, 2]), op=ALU.mult), [AB_t, self.PT_t], [AB_t])
            k.emit('dve', lambda e, B=B, sh=sh: e.tensor_copy(out=B, in_=sh), [MOD_t], [AB_t])

    def gate(self, s, c, half):
        return self.MOD[:, s * 24 + 16 + c, half:half + 1]

    def rmsnorm_mod(self, layer, s, half):
        k = self.k
        with ExitStack() as es:
            sq = self.tmp(es, 'nsq', [128, NCH, 512], F32R)
            sq_t = Trk('nsq')
            tmp = self.tmp(es, 'ntmp', [128, NCH, 512], F32)
            tmp_t = Trk('ntmp')
            rstd = self.tmp(es, 'nrstd', [128, 512], F32)
            rstd_t = Trk('nrstd')
            for tt in range(2):
                xs = self.xT[half][:, :, tt * 512:(tt + 1) * 512]
                xs_t = [self.xT_t[half][c][tt] for c in range(NCH)]
                k.emit('act', lambda e: e.activation(out=sq[:], in_=xs, func=AF.Square), xs_t, [sq_t])
                ps, pst = self.ps[6], self.ps_t[6]
                for c in range(NCH):
                    k.emit('pe', lambda e, c=c: e.matmul(ps[:], self.onesR[:], sq[:, c, :], start=(c == 0), stop=(c == NCH - 1)),
                           [self.onesR_t, sq_t], [pst])
                k.emit('act', lambda e: e.activation(out=rstd[:], in_=ps[:], func=AF.Sqrt, bias=float(D * EPS), scale=1.0),
                       [pst], [rstd_t])
                k.emit('dve', lambda e: e.reciprocal(out=rstd[:], in_=rstd[:]), [rstd_t], [rstd_t])
                k.emit('dve', lambda e: e.tensor_tensor(out=tmp[:], in0=xs, in1=rstd[:].unsqueeze(1).broadcast_to([128, NCH, 512]),
                                                        op=ALU.mult), xs_t + [rstd_t], [tmp_t])
                for c in range(NCH):
                    A = self.AB[:, s, 0, c, half:half + 1]
                    B = self.AB[:, s, 1, c, half:half + 1]
                    out = self.hT[:, c, tt * 512:(tt + 1) * 512]
                    if c % 2 == 0:
                        k.emit('act', lambda e, c=c, A=A, B=B, out=out: e.activation(
                            out=out, in_=tmp[:, c, :], func=AF.Identity, bias=B, scale=A),
                            [tmp_t, self.AB_t], [self.hT_t[c][tt]])
                    else:
                        k.emit('dve', lambda e, c=c, A=A, B=B, out=out: e.tensor_scalar(
                            out=out, in0=tmp[:, c, :], scalar1=A, scalar2=B, op0=ALU.mult, op1=ALU.add),
                            [tmp_t, self.AB_t], [self.hT_t[c][tt]])
            k.barrier()

    def mixer(self, layer, half):
        kind = layer % 3
        ml = self.cfg.get('mixlist', (0, 1, 2))
        if kind not in ml:
            return
        if kind == 0:
            if half == 0:
                self.attn_prep(layer)
            self.attn(layer, half)
        elif kind == 1:
            self.hgrn(layer, half)
        else:
            self.gdn(layer, half)

    def out_pf(self, w_out_rows):
        return PF(self, [[w_out_rows[:, dp * 256:(dp + 1) * 256]] for dp in range(4)])

    def out_proj(self, w_out_rows, src, src_t, nh, half, banks, pf=None):
        k = self.k
        bi = 0
        pf = pf if pf is not None else self.out_pf(w_out_rows)
        for dp in range(4):
            (wo,), wot = pf.get(dp)
            for dmi in range(2):
                dm = dp * 2 + dmi
                for tt in range(2):
                    b = banks[bi % len(banks)]
                    bi += 1
                    ps, pst = self.ps[b], self.ps_t[b]
                    for hl in range(nh):
                        k.emit('pe', lambda e, ps=ps, hl=hl, dmi=dmi, tt=tt: e.matmul(
                            ps[:], wo[:, hl, dmi * 128:(dmi + 1) * 128], src[:, hl, tt * 512:(tt + 1) * 512],
                            start=(hl == 0), stop=(hl == nh - 1)), [wot, src_t[hl]], [pst])
                    xs = self.xT[half][:, dm, tt * 512:(tt + 1) * 512]
                    k.emit('dve', lambda e, ps=ps, xs=xs, dm=dm: e.scalar_tensor_tensor(
                        out=xs, in0=ps[:], scalar=self.gate(0, dm, half), in1=xs, op0=ALU.mult, op1=ALU.add),
                        [pst, self.MOD_t, self.xT_t[half][dm][tt]], [self.xT_t[half][dm][tt]])

    def head_norm(self, tmps, src, src_t, dst, dst_t, gcol, eps_total, bank, extra_mul=None, extra_t=None):
        k = self.k
        sq, sq_t, rs, rs_t = tmps
        ps, pst = self.ps[bank], self.ps_t[bank]
        for tt in range(2):
            sl = slice(tt * 512, (tt + 1) * 512)
            k.emit('act', lambda e, sl=sl: e.activation(out=sq[:], in_=src[:, sl], func=AF.Square), [src_t], [sq_t])
            k.emit('pe', lambda e: e.matmul(ps[:], self.onesR[:], sq[:], start=True, stop=True), [self.onesR_t, sq_t], [pst])
            k.emit('act', lambda e: e.activation(out=rs[:], in_=ps[:], func=AF.Sqrt, bias=float(eps_total), scale=1.0), [pst], [rs_t])
            k.emit('dve', lambda e: e.reciprocal(out=rs[:], in_=rs[:]), [rs_t], [rs_t])
            if extra_mul is not None:
                k.emit('dve', lambda e, sl=sl: e.tensor_tensor(out=rs[:], in0=rs[:], in1=extra_mul[:, sl], op=ALU.mult), [rs_t, extra_t], [rs_t])
            k.emit('dve', lambda e, sl=sl: e.scalar_tensor_tensor(out=dst[:, sl], in0=src[:, sl], scalar=gcol, in1=rs[:], op0=ALU.mult, op1=ALU.mult),
                   [src_t, rs_t, self.PT_t, self.AV_t] + ([self.HV_t] if hasattr(self, 'HV_t') else []), [dst_t])

    def norm_tmps(self, es):
        return (self.tmp(es, 'hsq', [128, 512], F32R), Trk('hsq'), self.tmp(es, 'hrs', [128, 512], F32), Trk('hrs'))

    def attn_prep(self, layer):
        k = self.k
        j = layer // 3
        lam_init = 0.8 - 0.6 * math.exp(-0.3 * layer)
        with ExitStack() as es:
            lt = self.tmp(es, 'ltab', [128, 512], F32)
            lt_t = self.ptrk('ltab')
            k.dma(lt[:], self.d['lamtab'][:, :], writes=[lt_t])
            pr = self.tmp(es, 'lpr', [128, 2, 64], F32)
            pr_t = Trk('lpr')
            sm = self.tmp(es, 'lsm', [128, 2], F32)
            sm_t = Trk('lsm')
            base = j * 256
            lq = lt[:, base:base + 256].rearrange("p (a r n) -> p a r n", a=2, r=2)
            k.emit('dve', lambda e: e.tensor_tensor(out=pr[:], in0=lq[:, :, 0, :], in1=lq[:, :, 1, :], op=ALU.mult), [lt_t], [pr_t])
            k.emit('dve', lambda e: e.reduce_sum(out=sm[:], in_=pr[:], axis=AX.X), [pr_t], [sm_t])
            k.emit('act', lambda e: e.activation(out=sm[:], in_=sm[:], func=AF.Exp), [sm_t], [sm_t])
            k.emit('dve', lambda e: e.tensor_tensor(out=self.AV[:, 0:1], in0=sm[:, 0:1], in1=sm[:, 1:2], op=ALU.subtract), [sm_t], [self.AV_t])
            k.emit('dve', lambda e: e.tensor_scalar(out=self.AV[:, 0:1], in0=self.AV[:, 0:1], scalar1=float(lam_init), scalar2=None, op0=ALU.add),
                   [self.AV_t], [self.AV_t])
            k.emit('dve', lambda e: e.tensor_scalar(out=self.AV[:, 1:2], in0=self.AV[:, 0:1], scalar1=-1.0, scalar2=None, op0=ALU.mult),
                   [self.AV_t], [self.AV_t])
            osl, _ = PC['subln']
            k.emit('dve', lambda e: e.tensor_scalar(out=self.AV[:, 2:3], in0=self.PT[:, osl + j:osl + j + 1],
                                                    scalar1=float((1.0 - lam_init) * math.sqrt(128.0)), scalar2=None, op0=ALU.mult),
                   [self.PT_t, self.AV_t], [self.AV_t])
            k.barrier()

    def attn(self, layer, half):
        k, d, nc = self.k, self.d, self.nc
        j = layer // 3
        w_in = d['attn_w_in']
        scale = 0.125
        nkc = 2 if half == 0 else 12
        with ExitStack() as es:
            GH = 2
            V = self.tmp(es, 'aV', [128, 8, GH * 128], F32R)
            V_t = self.ptrk('aV', 8)
            ntm = self.norm_tmps(es)
            agrp = self.tmp(es, 'agrp', [128, GH, TOK], F32R)
            agrp_t = trks('agrp', GH)
            QT = [self.tmp(es, f'aQ{i}', [128, TOK], F32R) for i in range(1)]
            QT_t = trks('aQ', 1)
            KT = [self.tmp(es, f'aK{i}', [128, TOK], F32R) for i in range(1)]
            KT_t = self.ptrk('aK', 1)
            Pt = [self.tmp(es, f'aP{i}', [128, 512], F32R) for i in range(2)]
            Pt_t = trks('aP', 2)
            att = self.tmp(es, 'att', [128, TOK], F32)
            att_t = Trk('att')
            Rr = self.tmp(es, 'aR', [128, 2, 512], F32)
            Rr_t = Trk('aR')
            Tt, Tt_t = Rr, Rr_t
            if half == 1:
                ropet = self.tmp(es, 'arope', [128, 2048], F32)
                ropet_t = self.ptrk('arope')
                k.dma(ropet[:], d['rope'][:, :], writes=[ropet_t])
                COS = ropet[:, 0:1024]
                SIN = ropet[:, 1024:2048]
                raw = [self.tmp(es, f'araw{i}', [128, 512], F32R) for i in range(1)]
                raw_t = trks('araw', 1)
                ri_ = 0
                t1 = self.tmp(es, 'at1', [128, 512], F32)
                t1_t = Trk('at1')
                kcr = self.tmp(es, 'akcr', [128, 512], F32R)
                kcr_t = Trk('akcr')
                vcr = self.tmp(es, 'avcr', [128, 4 * GH * 128], F32R)
                vcr_t = Trk('avcr')
            pi = 0
            pb = 0
            for grp in range(8 // GH):
                c0 = 2 * D + grp * 256
                gpf = PF(self, [[w_in[j, :, c0:c0 + 256]]] + [[w_in[j, :, (grp * GH + t) * 128:(grp * GH + t + 1) * 128],
                                                              w_in[j, :, D + (grp * GH + t) * 128:D + (grp * GH + t + 1) * 128]] for t in range(GH)])
                for piece in range(1):
                    (wv,), wvt = gpf.get(0)
                    for tile in range(8):
                        b = 6 + (pb % 2)
                        pb += 1
                        ps, pst = self.ps[b], self.ps_t[b]
                        for c in range(NCH):
                            k.emit('pe', lambda e, ps=ps, c=c, tile=tile: e.matmul(
                                ps[:, 0:256], self.hT[:, c, tile * 128:(tile + 1) * 128], wv[:, c, :],
                                start=(c == 0), stop=(c == NCH - 1)), [wvt, self.hT_t[c][tile // 4]], [pst])
                        self.copy(self.evac_eng(), V[:, tile, piece * 256:(piece + 1) * 256], ps[:, 0:256], [pst], [V_t[tile]])
                if half == 0:
                    for tile in range(8):
                        k.dma(d['vout'][j, tile * 128:(tile + 1) * 128, grp * 256:(grp + 1) * 256], V[:, tile, :].bitcast(F32), reads=[V_t[tile]])
                else:
                    (vcv,), _ = self.wpiece([d['cv'][j, :, grp * 256:(grp + 1) * 256]], dest=(vcr, vcr_t))
                for hl in range(GH):
                    hh = grp * GH + hl
                    qi = 0
                    Q, Q_t, Kk, K_t = QT[qi], QT_t[qi], KT[qi], KT_t[qi]
                    (wq, wk), wt = gpf.get(1 + hl)
                    for wi, (w, dst, dst_t) in enumerate(((wq, Q, Q_t), (wk, Kk, K_t))):
                        for tt in range(2):
                            sl = slice(tt * 512, (tt + 1) * 512)
                            b = 6 + (pb % 2)
                            pb += 1
                            ps, pst = self.ps[b], self.ps_t[b]
                            for c in range(NCH):
                                k.emit('pe', lambda e, ps=ps, w=w, c=c, sl=sl: e.matmul(
                                    ps[:], w[:, c, :], self.hT[:, c, sl],
                                    start=(c == 0), stop=(c == NCH - 1)), [wt, self.hT_t[c][tt]], [pst])
                            if half == 0:
                                self.copy(self.evac_eng(), dst[:, sl], ps[:], [pst], [dst_t])
                            else:
                                rw, rw_t = raw[0], raw_t[0]
                                ri_ += 1
                                self.copy('act', rw[:], ps[:], [pst], [rw_t])
                                b2 = 6 + (pb % 2)
                                pb += 1
                                ps2, ps2t = self.ps[b2], self.ps_t[b2]
                                k.emit('pe', lambda e, ps2=ps2, rw=rw: e.matmul(ps2[:], self.permR[:], rw[:], start=True, stop=True),
                                       [self.permR_t, rw_t], [ps2t])
                                k.emit('pool', lambda e, rw=rw, sl=sl: e.tensor_tensor(out=t1[:], in0=rw[:].bitcast(F32), in1=COS[:, sl], op=ALU.mult),
                                       [rw_t, ropet_t], [t1_t])
                                k.emit('dve', lambda e, ps2=ps2, sl=sl, dst=dst: e.tensor_tensor(out=dst[:, sl], in0=ps2[:], in1=SIN[:, sl], op=ALU.mult),
                                       [ps2t, ropet_t], [dst_t])
                                k.emit('dve', lambda e, dst=dst, sl=sl: e.tensor_tensor(out=dst[:, sl], in0=dst[:, sl].bitcast(F32), in1=t1[:], op=ALU.add),
                                       [t1_t, dst_t], [dst_t])
                    if half == 0:
                        k.dma(d['kout'][j, hh, :, :], Kk[:].bitcast(F32), reads=[K_t])
                    else:
                        self.wpiece([d['ck'][j, hh, :, :]], dest=(kcr, kcr_t))

                    def keyT(comp, kc, s=0):
                        r = slice(comp * 64, (comp + 1) * 64)
                        if half == 0:
                            return Kk[r, s * 256 + kc * 128: s * 256 + (kc + 1) * 128], K_t
                        if kc < 4:
                            return kcr[r, kc * 128:(kc + 1) * 128], kcr_t
                        return Kk[r, (kc - 4) * 128:(kc - 3) * 128], K_t

                    def valT(kc, s=0):
                        cs = slice(hl * 128, (hl + 1) * 128)
                        if half == 0:
                            return V[:, s * 2 + kc, cs], V_t[s * 2 + kc]
                        if kc < 4:
                            return vcv[:, kc, cs], vcr_t
                        return V[:, kc - 4, cs], V_t[kc - 4]

                    if half == 0:
                        qw = 256
                        units = [(s, 0) for s in range(4)]
                    else:
                        qw = 512
                        units = [(0, qt) for qt in range(2)]
                    for (s, qt) in units:
                        q0 = s * 256 if half == 0 else qt * 512
                        for comp in range(2):
                            r = slice(comp * 64, (comp + 1) * 64)
                            if half == 0:
                                ob, zb = 2, 3
                                osl = slice(comp * 256, (comp + 1) * 256)
                            else:
                                ob, zb = 2 + comp * 2, 3 + comp * 2
                                osl = slice(0, 512)
                            pO, pO_t = self.ps[ob], self.ps_t[ob]
                            pZ, pZ_t = self.ps[zb], self.ps_t[zb]
                            if half == 0:
                                sb_ = pi % 2
                                pS, pS_t = self.ps[sb_], self.ps_t[sb_]
                                P_, P_t = Pt[pi % 2], Pt_t[pi % 2]
                                pi += 1
                                for kc in range(2):
                                    kl, kl_t = keyT(comp, kc, s)
                                    k.emit('pe', lambda e, pS=pS, kl=kl, kc=kc, r=r, q0=q0: e.matmul(
                                        pS[:, kc * 256:(kc + 1) * 256], kl, Q[r, q0:q0 + 256], start=True, stop=True),
                                        [kl_t, Q_t], [pS_t])
                                k.emit('act', lambda e, pS=pS, P_=P_: e.activation(out=P_[:], in_=pS[:], func=AF.Exp, scale=scale), [pS_t], [P_t])
                                for kc in range(2):
                                    vl, vl_t = valT(kc, s)
                                    k.emit('pe', lambda e, pO=pO, vl=vl, P_=P_, kc=kc, osl=osl: e.matmul(
                                        pO[:, osl], vl, P_[:, kc * 256:(kc + 1) * 256], start=(kc == 0), stop=(kc == 1)),
                                        [vl_t, P_t], [pO_t])
                                for kc in range(2):
                                    k.emit('pe', lambda e, pZ=pZ, P_=P_, kc=kc, osl=osl: e.matmul(
                                        pZ[:, osl], self.onesR[:], P_[:, kc * 256:(kc + 1) * 256], start=(kc == 0), stop=(kc == 1)),
                                        [self.onesR_t, P_t], [pZ_t])
                            else:
                                for kc in range(nkc):
                                    sb_ = pi % 2
                                    pS, pS_t = self.ps[sb_], self.ps_t[sb_]
                                    P_, P_t = Pt[pi % 2], Pt_t[pi % 2]
                                    pi += 1
                                    kl, kl_t = keyT(comp, kc)
                                    k.emit('pe', lambda e, pS=pS, kl=kl, r=r, q0=q0: e.matmul(
                                        pS[:], kl, Q[r, q0:q0 + 512], start=True, stop=True), [kl_t, Q_t], [pS_t])
                                    k.emit('act', lambda e, pS=pS, P_=P_: e.activation(out=P_[:], in_=pS[:], func=AF.Exp, scale=scale), [pS_t], [P_t])
                                    vl, vl_t = valT(kc)
                                    k.emit('pe', lambda e, pO=pO, vl=vl, P_=P_, kc=kc: e.matmul(
                                        pO[:], vl, P_[:], start=(kc == 0), stop=(kc == nkc - 1)), [vl_t, P_t], [pO_t])
                                    k.emit('pe', lambda e, pZ=pZ, P_=P_, kc=kc: e.matmul(
                                        pZ[:], self.onesR[:], P_[:], start=(kc == 0), stop=(kc == nkc - 1)), [self.onesR_t, P_t], [pZ_t])
                        if half == 0:
                            pO, pO_t, pZ, pZ_t = self.ps[2], self.ps_t[2], self.ps[3], self.ps_t[3]
                            k.emit('dve', lambda e, pZ=pZ: e.reciprocal(out=Rr[:, 0, :], in_=pZ[:]), [pZ_t], [Rr_t])
                            k.emit('dve', lambda e, pO=pO: e.tensor_tensor(out=Tt[:, 0, :], in0=pO[:], in1=Rr[:, 0, :], op=ALU.mult), [pO_t, Rr_t], [Tt_t])
                            k.emit('dve', lambda e, q0=q0: e.scalar_tensor_tensor(
                                out=att[:, q0:q0 + 256], in0=Tt[:, 0, 256:512], scalar=self.AV[:, 1:2], in1=Tt[:, 0, 0:256],
                                op0=ALU.mult, op1=ALU.add), [Tt_t, self.AV_t], [att_t])
                        else:
                            for comp in range(2):
                                pO, pO_t = self.ps[2 + comp * 2], self.ps_t[2 + comp * 2]
                                pZ, pZ_t = self.ps[3 + comp * 2], self.ps_t[3 + comp * 2]
                                k.emit('dve', lambda e, pZ=pZ, comp=comp: e.reciprocal(out=Rr[:, comp, :], in_=pZ[:]), [pZ_t], [Rr_t])
                                k.emit('dve', lambda e, pO=pO, comp=comp: e.tensor_tensor(out=Tt[:, comp, :], in0=pO[:], in1=Rr[:, comp, :], op=ALU.mult),
                                       [pO_t, Rr_t], [Tt_t])
                            k.emit('dve', lambda e, q0=q0: e.scalar_tensor_tensor(
                                out=att[:, q0:q0 + 512], in0=Tt[:, 1, :], scalar=self.AV[:, 1:2], in1=Tt[:, 0, :],
                                op0=ALU.mult, op1=ALU.add), [Tt_t, self.AV_t], [att_t])
                    self.head_norm(ntm, att[:], att_t, agrp[:, hl, :], agrp_t[hl], self.AV[:, 2:3], 128.0 * 1e-5, 6 + (pb % 2))
                    pb += 1
                self.out_proj(d['attn_w_out'][j, grp * GH * 128:(grp + 1) * GH * 128, :], agrp, agrp_t, GH, half, [6, 7, 0, 1])
            k.barrier()

    def hgrn_prep(self, layer):
        k = self.k
        ol, _ = PC['hgrn_lb']
        self.HV = self.k.sb('HV', [128, 3, 8])
        self.HV_t = Trk('HV')
        with ExitStack() as es:
            ex = self.tmp(es, 'hex', [128, 4, 8], F32)
            ex_t = Trk('hex')
            tot = self.tmp(es, 'htot', [128, 8], F32)
            tot_t = Trk('htot')
            k.emit('act', lambda e: e.activation(out=ex[:], in_=self.PT[:, ol:ol + 32].rearrange("p (l c) -> p l c", l=4), func=AF.Exp),
                   [self.PT_t], [ex_t])
            k.emit('dve', lambda e: e.tensor_tensor(out=tot[:], in0=ex[:, 0, :], in1=ex[:, 1, :], op=ALU.add), [ex_t], [tot_t])
            for l in (2, 3):
                k.emit('dve', lambda e, l=l: e.tensor_tensor(out=tot[:], in0=tot[:], in1=ex[:, l, :], op=ALU.add), [ex_t, tot_t], [tot_t])
            k.emit('dve', lambda e: e.reciprocal(out=tot[:], in_=tot[:]), [tot_t], [tot_t])
            k.emit('dve', lambda e: e.tensor_copy(out=self.HV[:, 0, :], in_=ex[:, 1, :]), [ex_t], [self.HV_t])
            for l in range(2, layer + 1):
                k.emit('dve', lambda e, l=l: e.tensor_tensor(out=self.HV[:, 0, :], in0=self.HV[:, 0, :], in1=ex[:, l, :], op=ALU.add),
                       [ex_t, self.HV_t], [self.HV_t])
            k.emit('dve', lambda e: e.tensor_tensor(out=self.HV[:, 0, :], in0=self.HV[:, 0, :], in1=tot[:], op=ALU.mult), [tot_t, self.HV_t], [self.HV_t])
            k.emit('dve', lambda e: e.tensor_scalar(out=self.HV[:, 1, :], in0=self.HV[:, 0, :], scalar1=-1.0, scalar2=1.0, op0=ALU.mult, op1=ALU.add),
                   [self.HV_t], [self.HV_t])
            on, _ = PC['hgrn_norm']
            k.emit('dve', lambda e: e.tensor_scalar(out=self.HV[:, 2, 0:1], in0=self.PT[:, on:on + 1], scalar1=float(math.sqrt(128.0)), scalar2=None, op0=ALU.mult),
                   [self.PT_t, self.HV_t], [self.HV_t])
            k.barrier()

    def hgrn(self, layer, half):
        k, d, nc = self.k, self.d, self.nc
        j = layer // 3
        if half == 0:
            self.hgrn_prep(layer)
        w_in = d['hgrn_w_in']
        nseq = 4 if half == 0 else 1
        cps = 16 // nseq
        oo, _ = CC['ones']
        ONES = self.CT[:, oo:oo + 1].broadcast_to([128, TOK])
        oi, _ = CC['ident']
        IDENT = self.CT[:, oi:oi + 128]
        masks = []
        for nm in ('mask_f', 'mask_b'):
            om, _ = CC[nm]
            masks.append(self.CT[0:64, om:om + 256])
        with ExitStack() as es:
            V64 = self.tmp(es, 'hV', [64, 16, 128], F32)
            V64_t = trks('hV', 16)
            mix = self.tmp(es, 'hmix', [128, 1, TOK], F32R)
            mix_t = trks('hmix', 1)
            ntm = self.norm_tmps(es)
            qT = self.tmp(es, 'hq', [128, TOK], F32)
            qT_t = Trk('hq')
            gs, gs_t = qT, qT_t
            oT = self.tmp(es, 'ho', [128, TOK], F32)
            oT_t = Trk('ho')
            Fb = [self.tmp(es, f'hF{i}', [128, TOK], F32) for i in range(2)]
            Fb_t = trks('hF', 2)
            L = self.tmp(es, 'hL', [128, TOK], F32)
            L_t = Trk('hL')
            Gp = self.tmp(es, 'hGp', [128, 64 + TOK + 64], F32)
            Gp_t = Trk('hGp')
            E1 = self.tmp(es, 'hE1', [128, TOK], F32)
            E1_t = Trk('hE1')
            E2 = self.tmp(es, 'hE2', [128, TOK], F32)
            E2_t = Trk('hE2')
            Am = self.tmp(es, 'hAm', [64, 4, 64], F32)
            Am_t = Trk('hAm')
            Ktok = self.tmp(es, 'hKt', [64, 4, 128], F32)
            Ktok_t = Trk('hKt')
            Sb = [self.tmp(es, f'hS{i}', [128, 128], F32) for i in range(2)]
            Sb_t = trks('hS', 2)
            DK = self.tmp(es, 'hDK', [128, 3, 16], F32)
            DK_t = Trk('hDK')
            G3 = self.tmp(es, 'hG3', [128, 3, 16], F32)
            G3_t = Trk('hG3')
            Sp = [self.tmp(es, f'hSp{i}', [128, 128], F32) for i in range(4)]
            Sp_t = trks('hSp', 4)
            tS = self.tmp(es, 'htS', [128, 128], F32)
            tS_t = Trk('htS')
            spi = 0
            sout_t = self.ptrk('hso')
            if half == 0:
                sout = self.tmp(es, 'hso', [128, 4, 2, 128], F32)
            k.emit('dve', lambda e: e.memset(Gp[:], 0.0), [], [Gp_t])
            pb = 0
            for pair in range(4):
                for hl in range(2):
                    hh = pair * 2 + hl
                    hpf = PF(self, [[w_in[j, :, 3 * D + hh * 128:3 * D + (hh + 1) * 128]],
                                    [w_in[j, :, hh * 128:(hh + 1) * 128], w_in[j, :, D + hh * 128:D + (hh + 1) * 128]],
                                    [w_in[j, :, 2 * D + hh * 128:2 * D + (hh + 1) * 128]]])
                    (wi,), wit = hpf.get(0)
                    for tt in range(2):
                        sl = slice(tt * 512, (tt + 1) * 512)
                        b = 6 + (pb % 2)
                        pb += 1
                        ps, pst = self.ps[b], self.ps_t[b]
                        for c in range(NCH):
                            k.emit('pe', lambda e, ps=ps, c=c, sl=sl: e.matmul(ps[:], wi[:, c, :], self.hT[:, c, sl], start=(c == 0), stop=(c == NCH - 1)),
                                   [wit, self.hT_t[c][tt]], [pst])
                        self.copy(self.evac_eng(), L[:, sl], ps[:], [pst], [L_t])
                    for g4 in range(4):
                        b = 6 + (pb % 2)
                        pb += 1
                        ps, pst = self.ps[b], self.ps_t[b]
                        for q_ in range(4):
                            ch = g4 * 4 + q_
                            k.emit('pe', lambda e, ps=ps, q_=q_, ch=ch: e.transpose(ps[0:64, q_ * 128:(q_ + 1) * 128], L[:, ch * 64:(ch + 1) * 64], IDENT),
                                   [L_t, self.CT_t], [pst])
                        self.copy(self.evac_eng(), V64[:, g4 * 4:(g4 + 1) * 4, :].rearrange("p a n -> p (a n)"), ps[0:64, :], [pst], V64_t[g4 * 4:(g4 + 1) * 4])
                    lbc = self.HV[:, 0, hh:hh + 1]
                    omc = self.HV[:, 1, hh:hh + 1]
                    (wq, wzf), wt1 = hpf.get(1)
                    (wzb,), wt2 = hpf.get(2)
                    for (w, wt, kindp) in ((wq, wt1, 'q'), (wzf, wt1, 'zf'), (wzb, wt2, 'zb')):
                        for tt in range(2):
                            sl = slice(tt * 512, (tt + 1) * 512)
                            b = 6 + (pb % 2)
                            pb += 1
                            ps, pst = self.ps[b], self.ps_t[b]
                            for c in range(NCH):
                                k.emit('pe', lambda e, ps=ps, w=w, c=c, sl=sl: e.matmul(
                                    ps[:], w[:, c, :], self.hT[:, c, sl], start=(c == 0), stop=(c == NCH - 1)),
                                    [wt, self.hT_t[c][tt]], [pst])
                            if kindp == 'q':
                                k.emit('act', lambda e, ps=ps, sl=sl: e.activation(out=qT[:, sl], in_=ps[:], func=AF.Copy, scale=float(128.0 ** -0.5)),
                                       [pst], [qT_t])
                            elif kindp == 'g':
                                k.emit('act', lambda e, ps=ps, sl=sl: e.activation(out=gs[:, sl], in_=ps[:], func=AF.Silu), [pst], [gs_t])
                            else:
                                di = 0 if kindp == 'zf' else 1
                                k.emit('act', lambda e, ps=ps, sl=sl, di=di: e.activation(out=Fb[di][:, sl], in_=ps[:], func=AF.Sigmoid), [pst], [Fb_t[di]])
                    (wg,), wt3 = self.wpiece([w_in[j, :, 4 * D + hh * 128:4 * D + (hh + 1) * 128]])
                    opf = self.out_pf(d['hgrn_w_out'][j, hh * 128:(hh + 1) * 128, :])
                    opf.h[0] = self.wpiece(opf.specs[0])
                    for di in range(2):
                        F_, F_t = Fb[di], Fb_t[di]
                        k.emit('dve', lambda e, F_=F_: e.tensor_scalar(out=F_[:], in0=F_[:], scalar1=omc, scalar2=lbc, op0=ALU.mult, op1=ALU.add),
                               [F_t, self.HV_t], [F_t])
                        k.emit('act', lambda e, F_=F_: e.activation(out=L[:], in_=F_[:], func=AF.Ln), [F_t], [L_t])
                        k.emit('pool', lambda e, F_=F_: e.tensor_scalar(out=F_[:], in0=F_[:], scalar1=-1.0, scalar2=1.0, op0=ALU.mult, op1=ALU.add),
                               [F_t], [F_t])
                        k.emit('dve', lambda e: e.tensor_tensor_scan(out=Gp[:, 64:64 + TOK], data0=ONES, data1=L[:], initial=0.0,
                                                                     op0=ALU.mult, op1=ALU.add), [L_t, self.CT_t], [Gp_t])
                        Lv = L[:].rearrange("p (j n) -> p j n", n=64)
                        if di == 0:
                            gprev = Gp[:, 63:63 + TOK].rearrange("p (j n) -> p j n", n=64)[:, :, 0:1].broadcast_to([128, 16, 64])
                            gcur = Gp[:, 64:64 + TOK].rearrange("p (j n) -> p j n", n=64)
                            k.emit('dve', lambda e: e.tensor_tensor(out=Lv, in0=gcur, in1=gprev, op=ALU.subtract), [Gp_t], [L_t])
                        else:
                            gend = Gp[:, 127:127 + TOK].rearrange("p (j n) -> p j n", n=64)[:, :, 0:1].broadcast_to([128, 16, 64])
                            gsh = Gp[:, 63:63 + TOK].rearrange("p (j n) -> p j n", n=64)
                            k.emit('dve', lambda e: e.tensor_tensor(out=Lv, in0=gend, in1=gsh, op=ALU.subtract), [Gp_t], [L_t])
                        pos = 63 if di == 0 else 0
                        mid = 31 if di == 0 else 32
                        k.emit('pool', lambda e, pos=pos: e.tensor_copy(out=G3[:, 0, :], in_=Lv[:, :, pos]), [L_t], [G3_t])
                        k.emit('pool', lambda e, mid=mid: e.tensor_copy(out=G3[:, 1, :], in_=Lv[:, :, mid]), [L_t], [G3_t])
                        k.emit('dve', lambda e: e.tensor_tensor(out=G3[:, 2, :], in0=G3[:, 0, :], in1=G3[:, 1, :], op=ALU.subtract), [G3_t], [G3_t])
                        k.emit('act', lambda e: e.activation(out=DK[:], in_=G3[:], func=AF.Exp), [G3_t], [DK_t])
                        k.emit('dve', lambda e: e.tensor_tensor(out=Lv, in0=Lv, in1=G3[:, 1, :].unsqueeze(2).broadcast_to([128, 16, 64]), op=ALU.subtract),
                               [L_t, G3_t], [L_t])
                        k.emit('act', lambda e: e.activation(out=E1[:], in_=L[:], func=AF.Exp), [L_t], [E1_t])
                        k.emit('act', lambda e: e.activation(out=E2[:], in_=L[:], func=AF.Exp, scale=-1.0), [L_t], [E2_t])
                        k.emit('dve', lambda e: e.tensor_tensor(out=E1[:], in0=E1[:], in1=qT[:], op=ALU.mult), [E1_t, qT_t], [E1_t])
                        k.emit('pool', lambda e, F_=F_: e.tensor_tensor(out=E2[:], in0=E2[:], in1=F_[:], op=ALU.mult), [E2_t, F_t], [E2_t])
                        order = list(range(16)) if di == 0 else list(range(15, -1, -1))
                        if half == 1:
                            k.dma(Sb[0][:], d['st_hgrn'][di, hh, :, :], writes=[Sb_t[0]])
                        si = 0
                        psA, psA_t = self.ps[0], self.ps_t[0]
                        psT, psT_t = self.ps[1], self.ps_t[1]
                        psO, psO_t = self.ps[2], self.ps_t[2]

                        def step1(gi, di=di, order=order):
                            chs = order[gi * 4:(gi + 1) * 4]
                            lo = min(chs)
                            for ch in chs:
                                cs = slice(ch * 64, (ch + 1) * 64)
                                q_ = ch - lo
                                k.emit('pe', lambda e, cs=cs, q_=q_: e.matmul(psA[0:64, q_ * 64:(q_ + 1) * 64], E2[:, cs], E1[:, cs], start=True, stop=True),
                                       [E1_t, E2_t], [psA_t])
                                k.emit('pe', lambda e, cs=cs, q_=q_: e.transpose(psT[0:64, q_ * 128:(q_ + 1) * 128], E2[:, cs], IDENT),
                                       [E2_t, self.CT_t], [psT_t])
                            k.emit('dve', lambda e, di=di: e.tensor_tensor(out=Am[:].rearrange("p a n -> p (a n)"), in0=psA[0:64, 0:256], in1=masks[di], op=ALU.mult),
                                   [psA_t, self.CT_t], [Am_t])
                            k.emit('act', lambda e: e.activation(out=Ktok[:].rearrange("p a n -> p (a n)"), in_=psT[0:64, :], func=AF.Copy), [psT_t], [Ktok_t])

                        step1(0)
                        for gi in range(4):
                            chs = order[gi * 4:(gi + 1) * 4]
                            lo = min(chs)
                            bS = 3 + (gi % 2)
                            psS, psS_t = self.ps[bS], self.ps_t[bS]
                            info = []
                            for ch in chs:
                                loc = ch % cps
                                first = (loc == 0) if di == 0 else (loc == cps - 1)
                                last = (loc == cps - 1) if di == 0 else (loc == 0)
                                info.append((ch, ch - lo, first and half == 0, last, ch // cps))
                            for (ch, q_, zi, last, seq) in info:
                                k.emit('pe', lambda e, q_=q_, ch=ch: e.matmul(psS[:, q_ * 128:(q_ + 1) * 128], Ktok[:, q_, :], V64[:, ch, :], start=True, stop=True),
                                       [Ktok_t, V64_t[ch]], [psS_t])
                            for idx, (ch, q_, zi, last, seq) in enumerate(info):
                                k.emit('pe', lambda e, q_=q_, ch=ch, idx=idx, zi=zi: e.matmul(
                                    psO[:, q_ * 64:(q_ + 1) * 64], V64[:, ch, :], Am[:, q_, :], start=(idx == 0), stop=zi), [V64_t[ch], Am_t], [psO_t])
                            sps = {}
                            for (ch, q_, zi, last, seq) in info:
                                S_prev, S_prev_t = Sb[si], Sb_t[si]
                                if not zi:
                                    sp_, sp_t = Sp[q_], Sp_t[q_]
                                    k.emit('act', lambda e, sp_=sp_, S_prev=S_prev, ch=ch: e.activation(
                                        out=sp_[:], in_=S_prev[:], func=AF.Identity, scale=DK[:, 1, ch:ch + 1]), [S_prev_t, DK_t], [sp_t])
                                    sps[ch] = (sp_, sp_t)
                                if last and half == 0:
                                    dstS, dstS_t = sout[:, seq, di, :], sout_t
                                else:
                                    si = 1 - si
                                    dstS, dstS_t = Sb[si][:], Sb_t[si]
                                reg = psS[:, q_ * 128:(q_ + 1) * 128]
                                if zi:
                                    k.emit('act', lambda e, reg=reg, ch=ch, dstS=dstS: e.activation(
                                        out=dstS, in_=reg, func=AF.Identity, scale=DK[:, 2, ch:ch + 1]), [psS_t, DK_t], [dstS_t])
                                else:
                                    k.emit('act', lambda e, reg=reg, ch=ch: e.activation(
                                        out=tS[:], in_=reg, func=AF.Identity, scale=DK[:, 2, ch:ch + 1]), [psS_t, DK_t], [tS_t])
                                    k.emit('dve', lambda e, dstS=dstS, S_prev=S_prev, ch=ch: e.scalar_tensor_tensor(
                                        out=dstS, in0=S_prev[:], scalar=DK[:, 0, ch:ch + 1], in1=tS[:], op0=ALU.mult, op1=ALU.add),
                                        [S_prev_t, DK_t, tS_t], [dstS_t])
                            if gi + 1 < 4:
                                step1(gi + 1)
                            for (ch, q_, zi, last, seq) in info:
                                if zi:
                                    continue
                                sp_, sp_t = sps[ch]
                                cs = slice(ch * 64, (ch + 1) * 64)
                                k.emit('pe', lambda e, cs=cs, q_=q_, sp_=sp_: e.matmul(
                                    psO[:, q_ * 64:(q_ + 1) * 64], sp_[:], E1[:, cs], start=False, stop=True), [sp_t, E1_t], [psO_t])
                            osl = slice(lo * 64, lo * 64 + 256)
                            if di == 0:
                                k.emit('dve', lambda e, osl=osl: e.tensor_copy(out=oT[:, osl], in_=psO[:, 0:256]), [psO_t], [oT_t])
                            else:
                                k.emit('dve', lambda e, osl=osl: e.tensor_tensor(out=oT[:, osl], in0=oT[:, osl], in1=psO[:, 0:256], op=ALU.add), [psO_t, oT_t], [oT_t])
                    if half == 0:
                        k.dma(d['hgout'][:, :, hh, :, :].rearrange("s d k e -> k s d e"), sout[:], reads=[sout_t])
                    for tt in range(2):
                        sl = slice(tt * 512, (tt + 1) * 512)
                        ps, pst = self.ps[6 + tt], self.ps_t[6 + tt]
                        for c in range(NCH):
                            k.emit('pe', lambda e, ps=ps, c=c, sl=sl: e.matmul(ps[:], wg[:, c, :], self.hT[:, c, sl], start=(c == 0), stop=(c == NCH - 1)),
                                   [wt3, self.hT_t[c][tt]], [pst])
                        k.emit('act', lambda e, ps=ps, sl=sl: e.activation(out=gs[:, sl], in_=ps[:], func=AF.Silu), [pst], [gs_t])
                    self.head_norm(ntm, oT[:], oT_t, mix[:, 0, :], mix_t[0], self.HV[:, 2, 0:1], 128.0 * 1e-6, 5, extra_mul=gs[:], extra_t=gs_t)
                    self.out_proj(d['hgrn_w_out'][j, hh * 128:(hh + 1) * 128, :], mix, mix_t, 1, half, [6, 7], pf=opf)
            k.barrier()

    def gdn(self, layer, half):
        k, d, nc = self.k, self.d, self.nc
        w_in = d['gdn_w_in']
        nseq = 4 if half == 0 else 1
        cps = 16 // nseq
        seqlen = TOK // nseq
        CT = self.CT

        def cc(name, rows=64, w=None):
            o_, w_ = CC[name]
            return CT[0:rows, o_:o_ + (w or w_)]

        ONES_ROW = cc('ones', 128, 1).broadcast_to([128, TOK])
        IDENT = cc('ident', 128)
        ID64 = cc('ident', 64, 64)
        TRIF = cc('mask_f', 64, 64)
        TRIB = cc('mask_b', 64, 64)
        ONES64 = cc('ones', 64, 64)
        NEGU = [cc('negu_f'), cc('negu_b')]
        NEGL = [cc('negl_f'), cc('negl_b')]
        SNEG = [cc('sneg_f'), cc('sneg_b')]
        IDREP = cc('idrep')
        osel, _ = CC['sel']
        onsel, _ = CC['nsel']

        def SEL(kk, m):
            return CT[0:4, osel + kk * 128: osel + kk * 128 + m]

        def NSEL(kk):
            return CT[0:4, onsel + kk * 64: onsel + (kk + 1) * 64]

        oa, _ = PC['gdn_alog_col']
        odt, _ = PC['gdn_dt_col']
        oar, _ = PC['gdn_alog_row']
        odr, _ = PC['gdn_dt_row']
        ocv, _ = PC['gdn_conv']
        ogn, _ = PC['gdn_norm']
        scr_t = self.ptrk('gscr')
        with ExitStack() as es0:
            GC = self.tmp(es0, 'gGC', [64, 16, 16]); BE = self.tmp(es0, 'gBE', [64, 16, 16])
            NBE = self.tmp(es0, 'gNBE', [64, 16, 16]); C1 = self.tmp(es0, 'gC1', [64, 16, 16])
            WW = self.tmp(es0, 'gWW', [64, 16, 16])
            TB_t = Trk('gTB')
            GV = self.tmp(es0, 'gGV', [128, 20])
            GV_t = Trk('gGV')
            k.emit('act', lambda e: e.activation(out=GV[:, 0:1], in_=self.PT[:, oa:oa + 1], func=AF.Exp), [self.PT_t], [GV_t])
            k.emit('dve', lambda e: e.tensor_scalar(out=GV[:, 0:1], in0=GV[:, 0:1], scalar1=-1.0, scalar2=None, op0=ALU.mult), [GV_t], [GV_t])
            k.emit('act', lambda e: e.activation(out=GV[:, 4:20], in_=self.PT[:, oar:oar + 16], func=AF.Exp), [self.PT_t], [GV_t])
            k.emit('dve', lambda e: e.tensor_scalar(out=GV[:, 4:20], in0=GV[:, 4:20], scalar1=-1.0, scalar2=None, op0=ALU.mult), [GV_t], [GV_t])
            k.emit('dve', lambda e: e.tensor_scalar(out=GV[:, 1:2], in0=self.PT[:, ogn:ogn + 1], scalar1=float(math.sqrt(128.0)), scalar2=None, op0=ALU.mult),
                   [self.PT_t, GV_t], [GV_t])
            (wab,), wab_t = self.wpiece([w_in[0, :, 4 * D:4 * D + 32]], rounded=False)
            with ExitStack() as es:
                LA = self.tmp(es, 'gLA', [16, TOK]); LA_t = Trk('gLA')
                Gp = self.tmp(es, 'gGp', [16, 64 + TOK + 64]); Gp_t = Trk('gGp')
                GF = self.tmp(es, 'gGF', [16, TOK]); GF_t = self.ptrk('gGF')
                GB = self.tmp(es, 'gGB', [16, TOK]); GB_t = self.ptrk('gGB')
                BT = self.tmp(es, 'gBT', [16, TOK]); BT_t = self.ptrk('gBT')
                LAt = self.tmp(es, 'gLAt', [64, 16, 16]); LAt_t = Trk('gLAt')
                k.emit('dve', lambda e: e.memset(Gp[:], 0.0), [], [Gp_t])
                hTf = self.hT[:].bitcast(F32)
                for part in range(2):
                    for tt in range(2):
                        sl = slice(tt * 512, (tt + 1) * 512)
                        ps, pst = self.ps[6 + tt], self.ps_t[6 + tt]
                        for c in range(NCH):
                            k.emit('pe', lambda e, ps=ps, c=c, sl=sl, part=part: e.matmul(
                                ps[0:16, :], wab[:, c, part * 16:(part + 1) * 16], hTf[:, c, sl], start=(c == 0), stop=(c == NCH - 1)),
                                [wab_t, self.hT_t[c][tt]], [pst])
                        if part == 0:
                            k.emit('act', lambda e, ps=ps, sl=sl: e.activation(out=LA[:, sl], in_=ps[0:16, :], func=AF.Exp, bias=self.PT[0:16, odt:odt + 1]),
                                   [pst, self.PT_t], [LA_t])
                        else:
                            k.emit('act', lambda e, ps=ps, sl=sl: e.activation(out=BT[:, sl], in_=ps[0:16, :], func=AF.Sigmoid), [pst], [BT_t])
                k.emit('act', lambda e: e.activation(out=LA[:], in_=LA[:], func=AF.Ln, bias=1.0), [LA_t], [LA_t])
                k.emit('dve', lambda e: e.tensor_scalar(out=LA[:], in0=LA[:], scalar1=GV[0:16, 0:1], scalar2=None, op0=ALU.mult), [LA_t, GV_t], [LA_t])
                k.emit('dve', lambda e: e.tensor_tensor_scan(out=Gp[:, 64:64 + TOK], data0=ONES_ROW[0:16, :], data1=LA[:], initial=0.0,
                                                             op0=ALU.mult, op1=ALU.add), [LA_t, self.CT_t], [Gp_t])
                gprev = Gp[:, 63:63 + TOK].rearrange("p (j n) -> p j n", n=64)[:, :, 0:1].broadcast_to([16, 16, 64])
                gcur = Gp[:, 64:64 + TOK].rearrange("p (j n) -> p j n", n=64)
                k.emit('dve', lambda e: e.tensor_tensor(out=GF[:].rearrange("p (j n) -> p j n", n=64), in0=gcur, in1=gprev, op=ALU.subtract), [Gp_t], [GF_t])
                gend = Gp[:, 127:127 + TOK].rearrange("p (j n) -> p j n", n=64)[:, :, 0:1].broadcast_to([16, 16, 64])
                gsh = Gp[:, 63:63 + TOK].rearrange("p (j n) -> p j n", n=64)
                k.emit('dve', lambda e: e.tensor_tensor(out=GB[:].rearrange("p (j n) -> p j n", n=64), in0=gend, in1=gsh, op=ALU.subtract), [Gp_t], [GB_t])
                k.dma(d['gscr'][0:16, :], GF[:], reads=[GF_t], writes=[scr_t])
                k.dma(d['gscr'][16:32, :], GB[:], reads=[GB_t], writes=[scr_t])
                k.dma(d['gscr'][32:48, :], BT[:], reads=[BT_t], writes=[scr_t])
                ps, pst = self.ps[5], self.ps_t[5]
                for ch in range(16):
                    for c in range(NCH):
                        k.emit('pe', lambda e, c=c, ch=ch: e.matmul(
                            ps[0:64, ch * 32:(ch + 1) * 32], hTf[:, c, ch * 64:(ch + 1) * 64], wab[:, c, :], start=(c == 0), stop=(c == NCH - 1)),
                            [wab_t, self.hT_t[c][ch // 8]], [pst])
                pv = ps[0:64, :].rearrange("p (c n) -> p c n", n=32)
                k.emit('dve', lambda e: e.tensor_tensor(out=LAt[:], in0=pv[:, :, 0:16],
                                                        in1=self.PT[0:64, odr:odr + 16].unsqueeze(1).broadcast_to([64, 16, 16]), op=ALU.add),
                       [pst, self.PT_t], [LAt_t])
                k.emit('act', lambda e: e.activation(out=BE[:], in_=pv[:, :, 16:32], func=AF.Sigmoid), [pst], [TB_t])
                k.emit('act', lambda e: e.activation(out=LAt[:], in_=LAt[:], func=AF.Exp), [LAt_t], [LAt_t])
                k.emit('act', lambda e: e.activation(out=LAt[:], in_=LAt[:], func=AF.Ln, bias=1.0), [LAt_t], [LAt_t])
                k.emit('dve', lambda e: e.tensor_tensor(out=LAt[:], in0=LAt[:], in1=GV[0:64, 4:20].unsqueeze(1).broadcast_to([64, 16, 16]), op=ALU.mult),
                       [LAt_t, GV_t], [LAt_t])
                LAf = LAt[:].rearrange("p c n -> p (c n)")
                pF, pF_t = self.ps[0], self.ps_t[0]
                pB, pB_t = self.ps[1], self.ps_t[1]
                pT, pT_t = self.ps[2], self.ps_t[2]
                k.emit('pe', lambda e: e.matmul(pF[0:64, 0:256], TRIF, LAf, start=True, stop=True), [LAt_t, self.CT_t], [pF_t])
                k.emit('pe', lambda e: e.matmul(pB[0:64, 0:256], TRIB, LAf, start=True, stop=True), [LAt_t, self.CT_t], [pB_t])
                k.emit('pe', lambda e: e.matmul(pT[0:64, 0:256], ONES64, LAf, start=True, stop=True), [LAt_t, self.CT_t], [pT_t])
                pFv = pF[0:64, 0:256].rearrange("p (c n) -> p c n", n=16)
                pBv = pB[0:64, 0:256].rearrange("p (c n) -> p c n", n=16)
                pTv = pT[0:64, 0:256].rearrange("p (c n) -> p c n", n=16)
                k.emit('dve', lambda e: e.tensor_copy(out=GC[:, :, 0:8], in_=pFv[:, :, 0:8]), [pF_t], [TB_t])
                k.emit('dve', lambda e: e.tensor_copy(out=GC[:, :, 8:16], in_=pBv[:, :, 8:16]), [pB_t], [TB_t])
                k.emit('dve', lambda e: e.tensor_tensor(out=WW[:], in0=pTv, in1=GC[:], op=ALU.subtract), [pT_t, TB_t], [TB_t])
                k.emit('act', lambda e: e.activation(out=WW[:], in_=WW[:], func=AF.Exp), [TB_t], [TB_t])
                k.emit('act', lambda e: e.activation(out=C1[:], in_=GC[:], func=AF.Exp), [TB_t], [TB_t])
                k.emit('dve', lambda e: e.scalar_tensor_tensor(out=C1[:], in0=C1[:], scalar=-1.0, in1=BE[:], op0=ALU.mult, op1=ALU.mult), [TB_t], [TB_t])
                k.emit('dve', lambda e: e.tensor_scalar(out=NBE[:], in0=BE[:], scalar1=-1.0, scalar2=None, op0=ALU.mult), [TB_t], [TB_t])
                k.barrier()
            stop = self.cfg.get('gdn_stop', 9)
            if stop <= 1:
                return
            with ExitStack() as es:
                qn = self.tmp(es, 'gq', [128, TOK]); qn_t = Trk('gq')
                kn = self.tmp(es, 'gk', [128, TOK]); kn_t = Trk('gk')
                vT = self.tmp(es, 'gv', [128, TOK]); vT_t = Trk('gv')
                oT = self.tmp(es, 'go', [128, TOK]); oT_t = Trk('go')
                Qt = self.tmp(es, 'gQt', [128, TOK]); Qt_t = Trk('gQt')
                mix = self.tmp(es, 'gmix', [128, 1, TOK], F32R); mix_t = trks('gmix', 1)
                ntm = self.norm_tmps(es)
                HR4 = self.tmp(es, 'gHR', [4, TOK]); HR4_t = self.ptrk('gHR')
                bt = [self.tmp(es, f'gb{i}', [64, 4, 64]) for i in range(10)]
                bt_t = trks('gb', 10)
                ktok = self.tmp(es, 'gkt', [64, 4, 128]); ktok_t = Trk('gkt')
                vtok = self.tmp(es, 'gvt', [64, 4, 128]); vtok_t = Trk('gvt')
                sm = [self.tmp(es, f'gs{i}', [64, 128]) for i in range(4)]
                sm_t = trks('gs', 4)
                Sb = [self.tmp(es, f'gS{i}', [128, 128]) for i in range(2)]
                Sb_t = trks('gS', 2)
                DKg = self.tmp(es, 'gDK', [128, 16]); DKg_t = Trk('gDK')
                sout_t = self.ptrk('gso')
                if half == 0:
                    sout = self.tmp(es, 'gso', [128, 4, 2, 128])
                pb = 0
                for hh in range(8):
                    (wq, wk), wt1 = self.wpiece([w_in[0, :, hh * 128:(hh + 1) * 128], w_in[0, :, D + hh * 128:D + (hh + 1) * 128]])
                    (wv,), wt2 = self.wpiece([w_in[0, :, 2 * D + hh * 128:2 * D + (hh + 1) * 128]])
                    k.dma_group([(HR4[0:1, :], d['gscr'][hh:hh + 1, :]), (HR4[1:2, :], d['gscr'][24 + hh:25 + hh, :]),
                                 (HR4[2:3, :], d['gscr'][32 + hh:33 + hh, :]), (HR4[3:4, :], d['gscr'][40 + hh:41 + hh, :])], [HR4_t])
                    HR4_t.rs[scr_t.w[0]] = 0
                    for ti, (w, wt, dst, dst_t) in enumerate(((wq, wt1, qn, qn_t), (wk, wt1, kn, kn_t), (wv, wt2, vT, vT_t))):
                        fch = ti * 8 + hh
                        w0 = self.PT[:, ocv + 0 * 24 + fch: ocv + 0 * 24 + fch + 1]
                        w1 = self.PT[:, ocv + 1 * 24 + fch: ocv + 1 * 24 + fch + 1]
                        w2 = self.PT[:, ocv + 2 * 24 + fch: ocv + 2 * 24 + fch + 1]
                        pss = []
                        for tt in range(2):
                            sl = slice(tt * 512, (tt + 1) * 512)
                            b = 6 + tt
                            ps, pst = self.ps[b], self.ps_t[b]
                            pss.append((ps, pst))
                            for c in range(NCH):
                                k.emit('pe', lambda e, ps=ps, w=w, c=c, sl=sl: e.matmul(
                                    ps[:], w[:, c, :], self.hT[:, c, sl], start=(c == 0), stop=(c == NCH - 1)), [wt, self.hT_t[c][tt]], [pst])
                            k.emit('act', lambda e, ps=ps, sl=sl, dst=dst, w1=w1: e.activation(out=dst[:, sl], in_=ps[:], func=AF.Copy, scale=w1),
                                   [pst, self.PT_t], [dst_t])
                        for tt in range(2):
                            ps, pst = pss[tt]
                            sl_ = min(seqlen, 512)
                            ns = 512 // sl_
                            pv = ps[:].rearrange("p (s n) -> p s n", s=ns)
                            av = dst[:, tt * 512:(tt + 1) * 512].rearrange("p (s n) -> p s n", s=ns)
                            k.emit('dve', lambda e, pv=pv, av=av, w0=w0, sl_=sl_: e.scalar_tensor_tensor(
                                out=av[:, :, 1:sl_], in0=pv[:, :, 0:sl_ - 1], scalar=w0, in1=av[:, :, 1:sl_], op0=ALU.mult, op1=ALU.add),
                                [pst, self.PT_t, dst_t], [dst_t])
                            k.emit('dve', lambda e, pv=pv, av=av, w2=w2, sl_=sl_: e.scalar_tensor_tensor(
                                out=av[:, :, 0:sl_ - 1], in0=pv[:, :, 1:sl_], scalar=w2, in1=av[:, :, 0:sl_ - 1], op0=ALU.mult, op1=ALU.add),
                                [pst, self.PT_t, dst_t], [dst_t])
                        if seqlen > 512:
                            p0, p0t = pss[0]
                            p1, p1t = pss[1]
                            k.emit('dve', lambda e, p0=p0, dst=dst, w0=w0: e.scalar_tensor_tensor(
                                out=dst[:, 512:513], in0=p0[:, 511:512], scalar=w0, in1=dst[:, 512:513], op0=ALU.mult, op1=ALU.add),
                                [p0t, self.PT_t, dst_t], [dst_t])
                            k.emit('dve', lambda e, p1=p1, dst=dst, w2=w2: e.scalar_tensor_tensor(
                                out=dst[:, 511:512], in0=p1[:, 0:1], scalar=w2, in1=dst[:, 511:512], op0=ALU.mult, op1=ALU.add),
                                [p1t, self.PT_t, dst_t], [dst_t])
                        k.emit('act', lambda e, dst=dst: e.activation(out=dst[:], in_=dst[:], func=AF.Silu), [dst_t], [dst_t])
                        if ti < 2:
                            sq, sq_t, rs, rs_t = ntm
                            for tt in range(2):
                                sl = slice(tt * 512, (tt + 1) * 512)
                                ps, pst = self.ps[5], self.ps_t[5]
                                k.emit('act', lambda e, dst=dst, sl=sl: e.activation(out=sq[:], in_=dst[:, sl], func=AF.Square), [dst_t], [sq_t])
                                k.emit('pe', lambda e, ps=ps: e.matmul(ps[:], self.onesR[:], sq[:], start=True, stop=True), [self.onesR_t, sq_t], [pst])
                                k.emit('act', lambda e, ps=ps: e.activation(out=rs[:], in_=ps[:], func=AF.Sqrt, bias=1e-6, scale=1.0), [pst], [rs_t])
                                k.emit('dve', lambda e: e.reciprocal(out=rs[:], in_=rs[:]), [rs_t], [rs_t])
                                sc_ = float(128.0 ** -0.5) if ti == 0 else 1.0
                                k.emit('dve', lambda e, dst=dst, sl=sl, sc_=sc_: e.scalar_tensor_tensor(
                                    out=dst[:, sl], in0=dst[:, sl], scalar=sc_, in1=rs[:], op0=ALU.mult, op1=ALU.mult), [dst_t, rs_t], [dst_t])
                    if stop <= 2:
                        continue
                    for di in range(2):
                        col = di * 8 + hh
                        for tt in range(2):
                            sl = slice(tt * 512, (tt + 1) * 512)
                            ps, pst = self.ps[6 + tt], self.ps_t[6 + tt]
                            k.emit('pe', lambda e, ps=ps, sl=sl, di=di: e.matmul(ps[:], SEL(di, 128), HR4[0:4, sl], start=True, stop=True),
                                   [HR4_t, self.CT_t], [pst])
                            k.emit('act', lambda e, ps=ps, sl=sl: e.activation(out=Qt[:, sl], in_=ps[:], func=AF.Exp), [pst], [Qt_t])
                        pos = 63 if di == 0 else 0
                        k.emit('pool', lambda e, pos=pos: e.tensor_copy(out=DKg[:], in_=Qt[:].rearrange("p (j n) -> p j n", n=64)[:, :, pos]), [Qt_t], [DKg_t])
                        k.emit('dve', lambda e: e.tensor_tensor(out=Qt[:], in0=Qt[:], in1=qn[:], op=ALU.mult), [Qt_t, qn_t], [Qt_t])
                        border = list(range(4)) if di == 0 else list(range(3, -1, -1))
                        if half == 1:
                            k.dma(Sb[0][:], d['st_gdn'][di, hh, :, :], writes=[Sb_t[0]])
                        si = 0
                        for bi in border:
                            chs = [bi * 4 + q for q in range(4)]
                            if di == 1:
                                chs = chs[::-1]
                            T0 = bi * 256
                            bsl = slice(T0, T0 + 256)
                            gcol = GC[:, bi * 4:bi * 4 + 4, col:col + 1].broadcast_to([64, 4, 64])
                            nbcol = NBE[:, bi * 4:bi * 4 + 4, col:col + 1].broadcast_to([64, 4, 64])
                            b0, b0t = self.ps[0], self.ps_t[0]
                            b1, b1t = self.ps[1], self.ps_t[1]
                            b2, b2t = self.ps[2], self.ps_t[2]
                            b3, b3t = self.ps[3], self.ps_t[3]
                            R = lambda ps: ps[0:64, 0:256]
                            R3 = lambda ps: ps[0:64, 0:256].rearrange("p (a n) -> p a n", a=4)
                            F2 = lambda t: t[:].rearrange("p a n -> p (a n)")
                            deps_c = [HR4_t, self.CT_t]
                            k.emit('pe', lambda e, di=di: e.matmul(R(b0), SEL(di, 64), HR4[0:4, bsl], start=True, stop=True), deps_c, [b0t])
                            k.emit('pe', lambda e, di=di: e.matmul(R(b2), SEL(2 + di, 64), HR4[0:4, bsl], start=True, stop=True), deps_c, [b2t])
                            DT, DT_t = bt[0], bt_t[0]
                            Dl, Dl_t = bt[1], bt_t[1]
                            DBT, DBT_t = bt[2], bt_t[2]
                            k.emit('dve', lambda e: e.tensor_tensor(out=DT[:], in0=R3(b0), in1=gcol, op=ALU.subtract), [b0t, TB_t], [DT_t])
                            k.emit('pool', lambda e, di=di: e.tensor_tensor(out=F2(Dl), in0=NEGL[di], in1=F2(DT), op=ALU.subtract), [DT_t, self.CT_t], [Dl_t])
                            k.emit('pool', lambda e, di=di: e.tensor_tensor(out=F2(DT), in0=F2(DT), in1=NEGU[di], op=ALU.add), [DT_t, self.CT_t], [DT_t])
                            k.emit('act', lambda e: e.activation(out=DT[:], in_=DT[:], func=AF.Exp), [DT_t], [DT_t])
                            k.emit('act', lambda e: e.activation(out=Dl[:], in_=Dl[:], func=AF.Exp), [Dl_t], [Dl_t])
                            k.emit('dve', lambda e, di=di: e.tensor_tensor(out=F2(DBT), in0=R(b2), in1=SNEG[di], op=ALU.mult), [b2t, self.CT_t], [DBT_t])
                            k.emit('pool', lambda e: e.tensor_tensor(out=DBT[:], in0=DBT[:], in1=DT[:], op=ALU.mult), [DBT_t, DT_t], [DBT_t])
                            k.emit('pool', lambda e: e.tensor_tensor(out=Dl[:], in0=Dl[:], in1=nbcol, op=ALU.mult), [Dl_t, TB_t], [Dl_t])
                            for q in range(4):
                                cs = slice(T0 + q * 64, T0 + (q + 1) * 64)
                                k.emit('pe', lambda e, q=q, cs=cs: e.matmul(b0[0:64, q * 64:(q + 1) * 64], kn[:, cs], kn[:, cs], start=True, stop=True), [kn_t], [b0t])
                                k.emit('pe', lambda e, q=q, cs=cs: e.matmul(b1[0:64, q * 64:(q + 1) * 64], kn[:, cs], qn[:, cs], start=True, stop=True), [kn_t, qn_t], [b1t])
                                k.emit('pe', lambda e, q=q, cs=cs: e.transpose(b2[0:64, q * 128:(q + 1) * 128], kn[:, cs], IDENT), [kn_t, self.CT_t], [b2t])
                                k.emit('pe', lambda e, q=q, cs=cs: e.transpose(b3[0:64, q * 128:(q + 1) * 128], vT[:, cs], IDENT), [vT_t, self.CT_t], [b3t])
                            NT, NT_t = bt[3], bt_t[3]
                            Nm, Nm_t = bt[4], bt_t[4]
                            QKT, QKT_t = bt[5], bt_t[5]
                            XT, XT_t = bt[6], bt_t[6]
                            k.emit('dve', lambda e: e.tensor_tensor(out=F2(NT), in0=R(b0), in1=F2(DBT), op=ALU.mult), [b0t, DBT_t], [NT_t])
                            k.emit('dve', lambda e: e.tensor_tensor(out=F2(Nm), in0=R(b0), in1=F2(Dl), op=ALU.mult), [b0t, Dl_t], [Nm_t])
                            k.emit('dve', lambda e: e.tensor_tensor(out=F2(QKT), in0=R(b1), in1=F2(DT), op=ALU.mult), [b1t, DT_t], [QKT_t])
                            k.emit('pool', lambda e: e.tensor_tensor(out=F2(XT), in0=F2(NT), in1=IDREP, op=ALU.add), [NT_t, self.CT_t], [XT_t])
                            k.emit('act', lambda e: e.activation(out=F2(ktok), in_=b2[0:64, :], func=AF.Copy), [b2t], [ktok_t])
                            k.emit('act', lambda e: e.activation(out=F2(vtok), in_=b3[0:64, :], func=AF.Copy), [b3t], [vtok_t])
                            P, P_t, PT_, PT_t = Nm, Nm_t, NT, NT_t
                            pp = [(bt[7], bt_t[7], bt[8], bt_t[8]), (bt[9], bt_t[9], bt[1], bt_t[1])]
                            XTs = [(bt[6], bt_t[6]), (bt[0], bt_t[0])]
                            xi = 0
                            for m in range(1, 6):
                                nP, nP_t, nPT, nPT_t = pp[(m - 1) % 2] if m > 1 else pp[0]
                                if m >= 3:
                                    nP, nP_t, nPT, nPT_t = pp[(m - 1) % 2]
                                if m == 2:
                                    nP, nP_t, nPT, nPT_t = pp[1]
                                for q in range(4):
                                    k.emit('pe', lambda e, q=q, P=P, PT_=PT_: e.matmul(b0[0:64, q * 64:(q + 1) * 64], PT_[:, q, :], P[:, q, :], start=True, stop=True),
                                           [P_t, PT_t], [b0t])
                                    if m < 5:
                                        k.emit('pe', lambda e, q=q, P=P, PT_=PT_: e.matmul(b1[0:64, q * 64:(q + 1) * 64], P[:, q, :], PT_[:, q, :], start=True, stop=True),
                                               [P_t, PT_t], [b1t])
                                k.emit('act', lambda e, nP=nP: e.activation(out=F2(nP), in_=R(b0), func=AF.Copy), [b0t], [nP_t])
                                if m < 5:
                                    k.emit('dve', lambda e, nPT=nPT: e.tensor_copy(out=F2(nPT), in_=R(b1)), [b1t], [nPT_t])
                                cX, cX_t = XTs[xi]
                                nX, nX_t = XTs[1 - xi]
                                for q in range(4):
                                    k.emit('pe', lambda e, q=q, nP=nP, cX=cX: e.matmul(b2[0:64, q * 64:(q + 1) * 64], nP[:, q, :], cX[:, q, :], start=True, stop=True),
                                           [nP_t, cX_t], [b2t])
                                k.emit('dve', lambda e, nX=nX, cX=cX: e.tensor_tensor(out=F2(nX), in0=R(b2), in1=F2(cX), op=ALU.add), [b2t, cX_t], [nX_t])
                                xi = 1 - xi
                                P, P_t, PT_, PT_t = nP, nP_t, nPT, nPT_t
                            XTf, XTf_t = XTs[xi]
                            if stop <= 3:
                                continue
                            psO, psO_t = self.ps[6], self.ps_t[6]
                            for ch in chs:
                                q = ch - bi * 4
                                cs = slice(ch * 64, (ch + 1) * 64)
                                loc = ch % cps
                                first = (loc == 0) if di == 0 else (loc == cps - 1)
                                last = (loc == cps - 1) if di == 0 else (loc == 0)
                                seq = ch // cps
                                zero_init = first and half == 0
                                S_prev, S_prev_t = Sb[si], Sb_t[si]
                                tmpv, tmpv_t = sm[0], sm_t[0]
                                r_, r_t = sm[1], sm_t[1]
                                vn, vn_t = sm[2], sm_t[2]
                                vs, vs_t = sm[3], sm_t[3]
                                k.emit('act', lambda e, q=q, ch=ch: e.activation(out=tmpv[:], in_=vtok[:, q, :], func=AF.Copy, scale=BE[:, ch, col:col + 1]),
                                       [vtok_t, TB_t], [tmpv_t])
                                if zero_init:
                                    rr, rr_t = tmpv, tmpv_t
                                else:
                                    p4, p4t = self.ps[4], self.ps_t[4]
                                    k.emit('pe', lambda e, cs=cs, S_prev=S_prev: e.matmul(p4[0:64, 0:128], kn[:, cs], S_prev[:], start=True, stop=True),
                                           [kn_t, S_prev_t], [p4t])
                                    k.emit('dve', lambda e, ch=ch: e.scalar_tensor_tensor(out=r_[:], in0=p4[0:64, 0:128], scalar=C1[:, ch, col:col + 1], in1=tmpv[:],
                                                                                         op0=ALU.mult, op1=ALU.add), [p4t, TB_t, tmpv_t], [r_t])
                                    rr, rr_t = r_, r_t
                                lvl = self.cfg.get('seq_lvl', 9)
                                if lvl <= 1:
                                    continue
                                p5, p5t = self.ps[5], self.ps_t[5]
                                k.emit('pe', lambda e, q=q, rr=rr: e.matmul(p5[0:64, 0:128], XTf[:, q, :], rr[:], start=True, stop=True), [XTf_t, rr_t], [p5t])
                                if lvl <= 1.5:
                                    continue
                                k.emit('act', lambda e: e.activation(out=vn[:], in_=p5[0:64, 0:128], func=AF.Copy), [p5t], [vn_t])
                                if lvl <= 1.7:
                                    continue
                                k.emit('dve', lambda e, ch=ch: e.tensor_scalar(out=vs[:], in0=p5[0:64, 0:128], scalar1=WW[:, ch, col:col + 1], scalar2=None, op0=ALU.mult),
                                       [p5t, TB_t], [vs_t])
                                if lvl <= 3:
                                    continue
                                p7, p7t = self.ps[7], self.ps_t[7]
                                k.emit('pe', lambda e, q=q: e.matmul(p7[:, 0:128], ktok[:, q, :], vs[:], start=True, stop=True), [ktok_t, vs_t], [p7t])
                                if last and half == 0:
                                    dstS, dstS_t = sout[:, seq, di, :], sout_t
                                else:
                                    si = 1 - si
                                    dstS, dstS_t = Sb[si][:], Sb_t[si]
                                if zero_init:
                                    k.emit('dve', lambda e, dstS=dstS: e.tensor_copy(out=dstS, in_=p7[:, 0:128]), [p7t], [dstS_t])
                                else:
                                    k.emit('dve', lambda e, dstS=dstS, S_prev=S_prev, ch=ch: e.scalar_tensor_tensor(
                                        out=dstS, in0=S_prev[:], scalar=DKg[:, ch:ch + 1], in1=p7[:, 0:128], op0=ALU.mult, op1=ALU.add),
                                        [p7t, S_prev_t, DKg_t], [dstS_t])
                                if lvl <= 2:
                                    continue
                                k.emit('pe', lambda e, q=q, zero_init=zero_init: e.matmul(psO[:, q * 64:(q + 1) * 64], vn[:], QKT[:, q, :], start=True, stop=zero_init),
                                       [vn_t, QKT_t], [psO_t])
                                if not zero_init:
                                    k.emit('pe', lambda e, q=q, cs=cs, S_prev=S_prev: e.matmul(psO[:, q * 64:(q + 1) * 64], S_prev[:], Qt[:, cs], start=False, stop=True),
                                           [S_prev_t, Qt_t], [psO_t])
                            if di == 0:
                                k.emit('act', lambda e, bsl=bsl: e.activation(out=oT[:, bsl], in_=psO[:, 0:256], func=AF.Copy), [psO_t], [oT_t])
                            else:
                                k.emit('dve', lambda e, bsl=bsl: e.tensor_tensor(out=oT[:, bsl], in0=oT[:, bsl], in1=psO[:, 0:256], op=ALU.add), [psO_t, oT_t], [oT_t])
                    if stop <= 4:
                        continue
                    if half == 0:
                        k.dma(d['gdout'][:, :, hh, :, :].rearrange("s d k e -> k s d e"), sout[:], reads=[sout_t])
                    (wg,), wt3 = self.wpiece([w_in[0, :, 3 * D + hh * 128:3 * D + (hh + 1) * 128]])
                    for tt in range(2):
                        sl = slice(tt * 512, (tt + 1) * 512)
                        ps, pst = self.ps[6 + tt], self.ps_t[6 + tt]
                        for c in range(NCH):
                            k.emit('pe', lambda e, ps=ps, c=c, sl=sl: e.matmul(ps[:], wg[:, c, :], self.hT[:, c, sl], start=(c == 0), stop=(c == NCH - 1)),
                                   [wt3, self.hT_t[c][tt]], [pst])
                        k.emit('act', lambda e, ps=ps, sl=sl: e.activation(out=vT[:, sl], in_=ps[:], func=AF.Silu), [pst], [vT_t])
                    self.HV_t = GV_t
                    self.head_norm(ntm, oT[:], oT_t, mix[:, 0, :], mix_t[0], GV[:, 1:2], 128.0 * 1e-6, 5, extra_mul=vT[:], extra_t=vT_t)
                    self.out_proj(d['gdn_w_out'][0, hh * 128:(hh + 1) * 128, :], mix, mix_t, 1, half, [6, 7])
                k.barrier()

    def ffn(self, layer, half):
        k, d, nc = self.k, self.d, self.nc
        seqlen = 256 if half == 0 else 1024
        oc, _ = PC['ffn_conv']
        ob, _ = PC['ffn_conv_b']

        def cw(tap, fchunk):
            col = oc + (layer * 3 + tap) * 44 + fchunk
            return self.PT[:, col:col + 1]

        def cb(fchunk):
            col = ob + layer * 44 + fchunk
            return self.PT[:, col:col + 1]

        groups = [(0, 8), (8, 16), (16, 22)]
        specs = []
        pidx = {}
        for (g0, g1) in groups:
            for j in range(g0, g1):
                pidx[('u', j)] = len(specs)
                specs.append([d['ffn_w_up'][layer, :, j * 128:(j + 1) * 128], d['ffn_w_up'][layer, :, D_FF + j * 128:D_FF + (j + 1) * 128]])
            for dp in range(4):
                pidx[('d', g0, dp)] = len(specs)
                specs.append([d['ffn_w_down'][layer, g0 * 128:g1 * 128, dp * 256:(dp + 1) * 256]])
        pf = PF(self, specs)
        PAIRS = [(0, 1), (2, 3), (4, 5)] if self.modgen is not None else [(0, 1), (2, 3), (4, 5), (6, 7)]
        u = 0
        v = 0
        with ExitStack() as es:
            aT = self.tmp(es, 'aT', [128, 8, TOK], F32R)
            aT_t = trks('aT', 8)
            acc = [self.tmp(es, f'facc{i}', [128, 2, TOK], F32) for i in range(2)]
            acc_t = trks('facc', 2, 2, 2)
            for (g0, g1) in groups:
                for j in range(g0, g1):
                    slot = j - g0
                    (wv, wg), wt = pf.get(pidx[('u', j)])
                    ai = j % 2
                    ac = acc[ai]
                    banks = {}
                    for tt in range(2):
                        pair = PAIRS[u % len(PAIRS)]
                        u += 1
                        sl = slice(tt * 512, (tt + 1) * 512)
                        for vi, (w, fch) in enumerate(((wv, j), (wg, NFF + j))):
                            ps, pst = self.ps[pair[vi]], self.ps_t[pair[vi]]
                            banks[(vi, tt)] = (ps, pst)
                            for c in range(NCH):
                                k.emit('pe', lambda e, ps=ps, w=w, c=c, sl=sl: e.matmul(
                                    ps[:], w[:, c, :], self.hT[:, c, sl],
                                    start=(c == 0), stop=(c == NCH - 1)), [wt, self.hT_t[c][tt]], [pst])
                        for vi, fch in ((0, j), (1, NFF + j)):
                            ps, pst = banks[(vi, tt)]
                            at_ = acc_t[ai][vi][tt]
                            k.emit('act', lambda e, ps=ps, sl=sl, fch=fch, vi=vi: e.activation(
                                out=ac[:, vi, sl], in_=ps[:], func=AF.Identity, bias=cb(fch), scale=cw(1, fch)), [pst, self.PT_t], [at_])
                            sl_ = min(seqlen, 512)
                            ns = 512 // sl_
                            pv = ps[:].rearrange("p (s n) -> p s n", s=ns)
                            av = ac[:, vi, sl].rearrange("p (s n) -> p s n", s=ns)
                            k.emit('dve', lambda e, pv=pv, av=av, fch=fch, sl_=sl_: e.scalar_tensor_tensor(
                                out=av[:, :, 1:sl_], in0=pv[:, :, 0:sl_ - 1], scalar=cw(0, fch), in1=av[:, :, 1:sl_],
                                op0=ALU.mult, op1=ALU.add), [pst, self.PT_t, at_], [at_])
                            k.emit('dve', lambda e, pv=pv, av=av, fch=fch, sl_=sl_: e.scalar_tensor_tensor(
                                out=av[:, :, 0:sl_ - 1], in0=pv[:, :, 1:sl_], scalar=cw(2, fch), in1=av[:, :, 0:sl_ - 1],
                                op0=ALU.mult, op1=ALU.add), [pst, self.PT_t, at_], [at_])
                    if seqlen > 512:
                        for vi, fch in ((0, j), (1, NFF + j)):
                            p0, p0t = banks[(vi, 0)]
                            p1, p1t = banks[(vi, 1)]
                            k.emit('dve', lambda e, p0=p0, fch=fch, vi=vi: e.scalar_tensor_tensor(
                                out=ac[:, vi, 512:513], in0=p0[:, 511:512], scalar=cw(0, fch), in1=ac[:, vi, 512:513],
                                op0=ALU.mult, op1=ALU.add), [p0t, self.PT_t, acc_t[ai][vi][1]], [acc_t[ai][vi][1]])
                            k.emit('dve', lambda e, p1=p1, fch=fch, vi=vi: e.scalar_tensor_tensor(
                                out=ac[:, vi, 511:512], in0=p1[:, 0:1], scalar=cw(2, fch), in1=ac[:, vi, 511:512],
                                op0=ALU.mult, op1=ALU.add), [p1t, self.PT_t, acc_t[ai][vi][0]], [acc_t[ai][vi][0]])
                    k.emit('act', lambda e, ac=ac: e.activation(out=ac[:, 1, :], in_=ac[:, 1, :], func=AF.Silu), acc_t[ai][1], acc_t[ai][1])
                    k.emit('pool', lambda e, slot=slot, ac=ac: e.tensor_tensor(out=aT[:, slot, :], in0=ac[:, 0, :], in1=ac[:, 1, :], op=ALU.mult),
                           acc_t[ai], [aT_t[slot]])
                    if self.modgen is not None:
                        next(self.modgen, None)
                ng = g1 - g0
                for dp in range(4):
                    (wd,), wdt = pf.get(pidx[('d', g0, dp)])
                    for dmi in range(2):
                        dm = dp * 2 + dmi
                        for tt in range(2):
                            b = v % 6
                            v += 1
                            ps, pst = self.ps[b], self.ps_t[b]
                            for jj in range(ng):
                                k.emit('pe', lambda e, ps=ps, jj=jj, dmi=dmi, tt=tt: e.matmul(
                                    ps[:], wd[:, jj, dmi * 128:(dmi + 1) * 128], aT[:, jj, tt * 512:(tt + 1) * 512],
                                    start=(jj == 0), stop=(jj == ng - 1)), [wdt, aT_t[jj]], [pst])
                            xs = self.xT[half][:, dm, tt * 512:(tt + 1) * 512]
                            k.emit('dve', lambda e, ps=ps, xs=xs, dm=dm: e.scalar_tensor_tensor(
                                out=xs, in0=ps[:], scalar=self.gate(1, dm, half), in1=xs, op0=ALU.mult, op1=ALU.add),
                                [pst, self.MOD_t, self.xT_t[half][dm][tt]], [self.xT_t[half][dm][tt]])
            k.barrier()

    def final(self, cfg):
        k, d, nc = self.k, self.d, self.nc
        og, _ = PC['final_g']
        with ExitStack() as es:
            sq = self.tmp(es, 'fsq', [128, NCH, 512], F32R)
            sq_t = Trk('fsq')
            rstd = self.tmp(es, 'frstd', [128, 512], F32)
            rstd_t = Trk('frstd')
            yo = self.tmp(es, 'fyo', [128, NCH, 512], F32)
            yo_t = self.ptrk('fyo', NCH)
            for half in range(2):
                for tt in range(2):
                    xs = self.xT[half][:, :, tt * 512:(tt + 1) * 512]
                    xs_t = [self.xT_t[half][c][tt] for c in range(NCH)]
                    k.emit('act', lambda e: e.activation(out=sq[:], in_=xs, func=AF.Square), xs_t, [sq_t])
                    ps, pst = self.ps[6], self.ps_t[6]
                    for c in range(NCH):
                        k.emit('pe', lambda e, c=c: e.matmul(ps[:], self.onesR[:], sq[:, c, :], start=(c == 0), stop=(c == NCH - 1)),
                               [self.onesR_t, sq_t], [pst])
                    k.emit('act', lambda e: e.activation(out=rstd[:], in_=ps[:], func=AF.Sqrt, bias=float(D * EPS), scale=1.0),
                           [pst], [rstd_t])
                    k.emit('dve', lambda e: e.reciprocal(out=rstd[:], in_=rstd[:]), [rstd_t], [rstd_t])
                    for c in range(NCH):
                        g = self.PT[:, og + c:og + c + 1]
                        k.emit('dve', lambda e, c=c, g=g: e.scalar_tensor_tensor(
                            out=yo[:, c, :], in0=self.xT[half][:, c, tt * 512:(tt + 1) * 512], scalar=g, in1=rstd[:],
                            op0=ALU.mult, op1=ALU.mult), [self.xT_t[half][c][tt], self.PT_t, rstd_t], [yo_t[c]])
                        k.emit('act', lambda e, c=c: e.activation(out=yo[:, c, :], in_=yo[:, c, :], func=AF.Copy, scale=32.0),
                               [yo_t[c]], [yo_t[c]])
                        k.dma(d['yout'][half, c, :, tt * 512:(tt + 1) * 512], yo[:, c, :], reads=[yo_t[c]])


_CACHE = {}


def _get_prog(cfg):
    key = repr(sorted(cfg.items()))
    if key not in _CACHE:
        p = Prog(dict(cfg))
        with p.es:
            p.declare()
            p.build()
        _CACHE[key] = p
    return _CACHE[key]


def _pack(plan, arrays, n):
    out = np.zeros((128, n), np.float32)
    for c0, key in plan:
        off = c0
        for (name, offset, rstride, rows, ncols) in key:
            flat = arrays[name].reshape(-1)
            a2 = np.lib.stride_tricks.as_strided(flat[offset:], shape=(rows, ncols), strides=(rstride * 4, 4))
            kc = rows // 128
            assert off + kc * ncols <= n
            out[:, off:off + kc * ncols] = a2.reshape(kc, 128, ncols).transpose(1, 0, 2).reshape(128, kc * ncols)
            off += kc * ncols
    return out


def _run(inp, cfg):
    p = _get_prog(cfg)
    consts, rope_tab = _build_consts()
    f32 = lambda a: np.ascontiguousarray(np.asarray(a, np.float32))
    warr = {n: f32(inp[n]) for n in ('w_mod', 'ffn_w_up', 'ffn_w_down', 'attn_w_in', 'attn_w_out',
                                     'hgrn_w_in', 'hgrn_w_out', 'gdn_w_in', 'gdn_w_out')}
    shared = {'wpk': _pack(p.wplan['wpk'], warr, WCOLS)}
    xp = f32(inp['x_prompt'])
    xs = f32(inp['x_sample'])
    ck = f32(inp['cache_attn_k'])
    cv = f32(inp['cache_attn_v'])
    in_maps = []
    for core in range(N_CORES):
        m = dict(shared)
        a = xp[4 * core:4 * core + 4].reshape(TOK, D).T.reshape(8, 128, TOK)
        b = xs[core].T.reshape(8, 128, TOK)
        m['xin'] = np.ascontiguousarray(np.stack([a, b], axis=0))
        m['params'] = _build_params(core, inp)
        m['consts'] = consts
        m['rope'] = rope_tab
        m['lamtab'] = np.ascontiguousarray(np.broadcast_to(np.asarray(inp['attn_lambda'], np.float32).reshape(1, 512), (128, 512)))
        carr = {'ck': np.ascontiguousarray(ck[core].transpose(0, 2, 3, 1)),
                'cv': np.ascontiguousarray(cv[core].reshape(2, 512, D))}
        m['cpk'] = _pack(p.wplan['cpk'], carr, CCOLS)
        m['st_hgrn'] = f32(inp['state_hgrn'][core, 0])
        m['st_gdn'] = f32(inp['state_gdn'][core, 0])
        in_maps.append(m)
    ncores = cfg.get('ncores', N_CORES)
    res = run_bass_kernel_spmd(p.nc, in_maps[:ncores], core_ids=list(range(ncores)))
    R = list(res.results) + [res.results[0]] * (N_CORES - ncores)
    y_prompt = np.empty((32, 256, D), np.float32)
    y_sample = np.empty((8, 1024, D), np.float32)
    new_k = np.empty((32, 2, 256, 8, 128), np.float32)
    new_v = np.empty((32, 2, 256, 8, 128), np.float32)
    new_h = np.empty((32, 1, 2, 8, 128, 128), np.float32)
    new_g = np.empty((32, 1, 2, 8, 128, 128), np.float32)
    for core in range(N_CORES):
        r = R[core]
        yo = r['yout']
        y_prompt[4 * core:4 * core + 4] = yo[0].reshape(D, TOK).T.reshape(4, 256, D)
        y_sample[core] = yo[1].reshape(D, TOK).T
        ko = r['kout']
        new_k[4 * core:4 * core + 4] = ko.reshape(2, 8, 128, 4, 256).transpose(3, 0, 4, 1, 2)
        vo = r['vout']
        new_v[4 * core:4 * core + 4] = vo.reshape(2, 4, 256, 8, 128).transpose(1, 0, 2, 3, 4)
        new_h[4 * core:4 * core + 4, 0] = r['hgout']
        new_g[4 * core:4 * core + 4, 0] = r['gdout']
    return (y_prompt, y_sample, new_k, new_v, new_h, new_g)


def kernel(**inputs):
    return _run(inputs, {})
```

```python
import math
from contextlib import ExitStack

import numpy as np
import concourse.bass as bass
import concourse.mybir as mybir
from concourse.bass_utils import run_bass_kernel_spmd

F32 = mybir.dt.float32
F32R = mybir.dt.float32r
AF = mybir.ActivationFunctionType
ALU = mybir.AluOpType
AX = mybir.AxisListType

D = 1024
NCH = 8
TOK = 1024
DEPTH = 4
D_FF = 2816
NFF = 22
EPS = 1e-6
N_CORES = 8
WCOLS = (4 * 1024 * 6144 + 4 * 1024 * 5632 + 4 * 2816 * 1024 + 2 * 1024 * 3072 + 2 * 1024 * 1024 + 1024 * 5120 + 1024 * 1024
         + 1024 * 4128 + 1024 * 1024) // 128
CCOLS = (2 * 8 * 128 * 512 + 2 * 512 * 1024) // 128


class Cols:
    def __init__(self):
        self.off = {}
        self.n = 0

    def add(self, name, w):
        self.off[name] = (self.n, w)
        self.n += w

    def __getitem__(self, name):
        return self.off[name]


def _param_cols():
    c = Cols()
    c.add('cond', 16)
    c.add('norm_g', 64)
    c.add('b_mod', 192)
    c.add('final_g', 8)
    c.add('subln', 2)
    c.add('hgrn_lb', 32)
    c.add('hgrn_norm', 1)
    c.add('gdn_norm', 1)
    c.add('gdn_conv', 72)
    c.add('ffn_conv', 528)
    c.add('ffn_conv_b', 176)
    c.add('gdn_alog_col', 1)
    c.add('gdn_dt_col', 1)
    c.add('gdn_alog_row', 16)
    c.add('gdn_dt_row', 16)
    return c


PC = _param_cols()


def _fm(v):
    v = np.asarray(v, np.float32)
    r = v.reshape(-1, 128)
    return np.ascontiguousarray(r.T)


def _build_params(core, inp):
    P = np.zeros((128, PC.n), np.float32)

    def put(name, arr):
        o, w = PC[name]
        assert arr.shape == (128, w), (name, arr.shape, w)
        P[:, o:o + w] = arr

    cond = np.stack([inp['c_ctx'], inp['c'][core]], axis=0)
    put('cond', np.ascontiguousarray(cond.reshape(2, 8, 128).transpose(2, 1, 0)).reshape(128, 16))
    put('norm_g', _fm(inp['norm_g']))
    put('b_mod', _fm(inp['b_mod']))
    put('final_g', _fm(inp['final_g']))
    put('subln', _fm(inp['attn_subln']))
    put('hgrn_lb', _fm(inp['hgrn_lb']))
    put('hgrn_norm', _fm(inp['hgrn_norm']))
    put('gdn_norm', _fm(inp['gdn_norm']))
    put('gdn_conv', _fm(inp['gdn_conv']))
    put('ffn_conv', _fm(inp['ffn_conv']))
    put('ffn_conv_b', _fm(inp['ffn_conv_b']))
    al = np.zeros((128, 1), np.float32)
    al[:16, 0] = np.asarray(inp['gdn_a_log'], np.float32).reshape(16)
    put('gdn_alog_col', al)
    dtb = np.zeros((128, 1), np.float32)
    dtb[:16, 0] = np.asarray(inp['gdn_dt_bias'], np.float32).reshape(16)
    put('gdn_dt_col', dtb)
    put('gdn_alog_row', np.broadcast_to(np.asarray(inp['gdn_a_log'], np.float32).reshape(1, 16), (128, 16)))
    put('gdn_dt_row', np.broadcast_to(np.asarray(inp['gdn_dt_bias'], np.float32).reshape(1, 16), (128, 16)))
    return P


def _const_cols():
    c = Cols()
    c.add('ident', 128)
    c.add('ones', 128)
    c.add('perm', 128)
    c.add('mask_f', 256)
    c.add('mask_b', 256)
    c.add('negu_f', 256)
    c.add('negl_f', 256)
    c.add('negu_b', 256)
    c.add('negl_b', 256)
    c.add('sneg_f', 256)
    c.add('sneg_b', 256)
    c.add('idrep', 256)
    c.add('sel', 512)
    c.add('nsel', 128)
    return c


CC = _const_cols()


def _build_consts():
    C = np.zeros((128, CC.n), np.float32)
    o, w = CC['ident']
    C[:, o:o + w] = np.eye(128, dtype=np.float32)
    o, w = CC['ones']
    C[:, o:o + w] = 1.0
    o, w = CC['perm']
    tok = np.arange(1024)
    row = (tok // 64).astype(np.float32)
    col = (tok % 64).astype(np.float32)
    inv = (np.float32(10000.0) ** (-np.arange(16, dtype=np.float32) / np.float32(16))).astype(np.float32)
    ROPE = np.zeros((128, 2048), np.float32)
    oc_, os_ = 0, 1024
    for p in range(128):
        dd = p % 64
        i = dd % 32
        partner = p + 16 if i < 16 else p - 16
        C[partner, o + p] = 1.0
        f = i % 16
        pos = row if dd < 32 else col
        ang = (pos * inv[f]).astype(np.float32)
        ROPE[p, oc_:oc_ + 1024] = np.cos(ang)
        ROPE[p, os_:os_ + 1024] = np.sin(ang) * (-1.0 if i < 16 else 1.0)
    sidx = np.arange(64)[:, None]
    tidx = np.arange(64)[None, :]
    om, _ = CC['mask_f']
    C[:64, om:om + 256] = np.tile((sidx <= tidx).astype(np.float32), (1, 4))
    om, _ = CC['mask_b']
    C[:64, om:om + 256] = np.tile((sidx >= tidx).astype(np.float32), (1, 4))
    BIG = 30000.0
    p_, j_ = sidx, tidx

    def putm(name, m):
        o_, _ = CC[name]
        C[:64, o_:o_ + 256] = np.tile(m.astype(np.float32), (1, 4))

    putm('negu_f', np.where(j_ >= p_, 0.0, -BIG))
    putm('negl_f', np.where(j_ < p_, 0.0, -BIG))
    putm('negu_b', np.where(j_ <= p_, 0.0, -BIG))
    putm('negl_b', np.where(j_ > p_, 0.0, -BIG))
    putm('sneg_f', np.where(j_ > p_, -1.0, 0.0))
    putm('sneg_b', np.where(j_ < p_, -1.0, 0.0))
    putm('idrep', (j_ == p_))
    o_, _ = CC['sel']
    for kk in range(4):
        C[kk, o_ + kk * 128:o_ + (kk + 1) * 128] = 1.0
    o_, _ = CC['nsel']
    for kk in range(2):
        C[kk, o_ + kk * 64:o_ + (kk + 1) * 64] = -1.0
    return C, ROPE


class _GT:
    def __init__(self, name):
        self.name = name


class Geo:
    def __init__(self, name, shape, offset=0, pat=None):
        self.tensor = _GT(name)
        if pat is None:
            pat = []
            st = 1
            for n in reversed(shape):
                pat.insert(0, (st, n))
                st *= n
        self.ap = tuple(pat)
        self.offset = offset

    @property
    def shape(self):
        return tuple(n for _, n in self.ap)

    def __getitem__(self, key):
        if not isinstance(key, tuple):
            key = (key,)
        key = key + (slice(None),) * (len(self.ap) - len(key))
        off = self.offset
        pat = []
        for (st, n), kk in zip(self.ap, key):
            if isinstance(kk, int):
                off += st * kk
            else:
                a, b, _ = kk.indices(n)
                off += st * a
                pat.append((st, b - a))
        return Geo(self.tensor.name, None, off, pat)


class Trk:
    __slots__ = ('name', 'w', 'rs', 'sem', 'cnt', 'psum')

    def __init__(self, name):
        self.name = name
        self.psum = False
        self.w = None
        self.rs = {}
        self.sem = None
        self.cnt = 0


def trks(name, *dims):
    if len(dims) == 1:
        return [Trk(f'{name}{i}') for i in range(dims[0])]
    return [trks(f'{name}{i}_', *dims[1:]) for i in range(dims[0])]


def flat(x):
    if isinstance(x, Trk):
        return [x]
    out = []
    for e in x:
        out.extend(flat(e))
    return out


class KB:
    def __init__(self, nc, es):
        self.nc = nc
        self.es = es
        self.E = {'pe': nc.tensor, 'act': nc.scalar, 'dve': nc.vector, 'pool': nc.gpsimd, 'sp': nc.sync}
        self.sem = {e: es.enter_context(nc.semaphore(f's_{e}')) for e in self.E}
        self.cnt = {e: 0 for e in self.E}
        self.waited = {e: {} for e in self.E}
        self.dma_sems = []
        self.n_ins = 0
        self.n_wait = 0

    def _wait(self, eng, deps):
        need = {}
        for key, val in deps:
            if key == 'pe' and eng == 'pe':
                continue
            if need.get(key, 0) < val:
                need[key] = val
        wt = self.waited[eng]
        for key, val in need.items():
            if wt.get(key, 0) >= val:
                continue
            sem = self.sem[key] if isinstance(key, str) else key
            self.E[eng].wait_ge(sem, val)
            self.n_wait += 1
            wt[key] = val

    def _deps(self, reads, writes):
        deps = []
        for t in reads:
            if t.w is not None:
                deps.append(t.w)
            if t.psum:
                deps.extend(t.rs.items())
        for t in writes:
            if t.w is not None:
                deps.append(t.w)
            deps.extend(t.rs.items())
        return deps

    def _mark(self, tok, reads, writes):
        k, v = tok
        for t in reads:
            if t.rs.get(k, 0) < v:
                t.rs[k] = v
        for t in writes:
            t.w = tok
            t.rs = {}

    def emit(self, eng, fn, reads=(), writes=()):
        reads = flat(reads)
        writes = flat(writes)
        self._wait(eng, self._deps(reads, writes))
        ins = fn(self.E[eng])
        self.cnt[eng] += 1
        ins.then_inc(self.sem[eng], 1)
        self.n_ins += 1
        tok = (eng, self.cnt[eng])
        self._mark(tok, reads, writes)
        return tok

    def dma(self, out, in_, reads=(), writes=(), q='sp'):
        reads = flat(reads)
        writes = flat(writes)
        self._wait(q, self._deps(reads, writes))
        owner = (writes + reads)[0]
        if owner.sem is None:
            owner.sem = self.es.enter_context(self.nc.semaphore(f'd_{owner.name}'))
            self.dma_sems.append(owner)
        self.E[q].dma_start(out=out, in_=in_).then_inc(owner.sem, 16)
        owner.cnt += 16
        self.n_ins += 1
        tok = (owner.sem, owner.cnt)
        self._mark(tok, reads, writes)
        return tok

    def dma_group(self, pairs, writes, q='sp'):
        writes = flat(writes)
        self._wait(q, self._deps([], writes))
        owner = writes[0]
        if owner.sem is None:
            owner.sem = self.es.enter_context(self.nc.semaphore(f'd_{owner.name}'))
            self.dma_sems.append(owner)
        for out, in_ in pairs:
            self.E[q].dma_start(out=out, in_=in_).then_inc(owner.sem, 16)
            owner.cnt += 16
            self.n_ins += 1
        tok = (owner.sem, owner.cnt)
        self._mark(tok, [], writes)
        return tok

    def barrier(self):
        for e in self.E:
            deps = [(o, self.cnt[o]) for o in self.E if o != e and self.cnt[o] > 0]
            deps += [(t.sem, t.cnt) for t in self.dma_sems]
            wt = self.waited[e]
            for key, val in deps:
                if wt.get(key, 0) >= val:
                    continue
                sem = self.sem[key] if isinstance(key, str) else key
                self.E[e].wait_ge(sem, val)
                self.n_wait += 1
                wt[key] = val

    def finish(self):
        deps = [(t.sem, t.cnt) for t in self.dma_sems]
        deps += [(o, self.cnt[o]) for o in self.E if o != 'sp' and self.cnt[o] > 0]
        self._wait('sp', deps)

    def sb(self, name, shape, dt=F32):
        return self.es.enter_context(self.nc.sbuf_tensor(name, list(shape), dt))


class PF:
    def __init__(self, prog, specs):
        self.p, self.specs, self.h = prog, specs, {}

    def get(self, i):
        for t in (i, i + 1):
            if t < len(self.specs) and t not in self.h:
                self.h[t] = self.p.wpiece(self.specs[t])
        return self.h.pop(i)


class Prog:
    def __init__(self, cfg):
        self.cfg = cfg
        nc = bass.Bass("TRN2", target_bir_lowering=False)
        self.nc = nc
        self.es = ExitStack()
        self.k = KB(nc, self.es)
        self.rr = 0
        self.wplan = {'wpk': [], 'cpk': []}
        self.wcols = {'wpk': 0, 'cpk': 0}
        self.wkeys = {}

    def declare(self):
        nc = self.nc

        def din(name, shape):
            return nc.dram_tensor(name, list(shape), F32, kind="ExternalInput").ap()

        def dout(name, shape):
            return nc.dram_tensor(name, list(shape), F32, kind="ExternalOutput").ap()

        d = {}
        d['xin'] = din('xin', [2, 8, 128, TOK])
        d['params'] = din('params', [128, PC.n])
        d['consts'] = din('consts', [128, CC.n])
        d['rope'] = din('rope', [128, 2048])
        d['wpk'] = din('wpk', [128, WCOLS])
        d['cpk'] = din('cpk', [128, CCOLS])
        d['lamtab'] = din('lamtab', [128, 512])
        d['gscr'] = nc.dram_tensor('gscr', [48, TOK], F32, kind="Internal").ap()
        d['w_mod'] = Geo('w_mod', [4, D, 6 * D])
        d['ffn_w_up'] = Geo('ffn_w_up', [4, D, 2 * D_FF])
        d['ffn_w_down'] = Geo('ffn_w_down', [4, D_FF, D])
        d['attn_w_in'] = Geo('attn_w_in', [2, D, 3 * D])
        d['attn_w_out'] = Geo('attn_w_out', [2, D, D])
        d['hgrn_w_in'] = Geo('hgrn_w_in', [1, D, 5 * D])
        d['hgrn_w_out'] = Geo('hgrn_w_out', [1, D, D])
        d['gdn_w_in'] = Geo('gdn_w_in', [1, D, 4 * D + 32])
        d['gdn_w_out'] = Geo('gdn_w_out', [1, D, D])
        d['ck'] = Geo('ck', [2, 8, 128, 512])
        d['cv'] = Geo('cv', [2, 512, D])
        d['st_hgrn'] = din('st_hgrn', [2, 8, 128, 128])
        d['st_gdn'] = din('st_gdn', [2, 8, 128, 128])
        d['yout'] = dout('yout', [2, 8, 128, TOK])
        d['kout'] = dout('kout', [2, 8, 128, TOK])
        d['vout'] = dout('vout', [2, TOK, D])
        d['hgout'] = dout('hgout', [4, 2, 8, 128, 128])
        d['gdout'] = dout('gdout', [4, 2, 8, 128, 128])
        self.d = d

    def ptrk(self, name, n=None):
        if not hasattr(self, '_pt'):
            self._pt = {}
        if name not in self._pt:
            self._pt[name] = Trk(name) if n is None else trks(name, n)
        return self._pt[name]

    def tmp(self, es, name, shape, dt=F32):
        self._uid = getattr(self, '_uid', 0) + 1
        return es.enter_context(self.nc.sbuf_tensor(f'{name}_{self._uid}', list(shape), dt))

    def evac_eng(self):
        self.rr += 1
        return 'act' if self.rr % 2 else 'dve'

    def copy(self, eng, out, in_, reads, writes):
        if eng == 'act':
            return self.k.emit('act', lambda e: e.activation(out=out, in_=in_, func=AF.Copy), reads, writes)
        return self.k.emit(eng, lambda e: e.tensor_copy(out=out, in_=in_), reads, writes)

    def init_wpool(self):
        k = self.k
        self.ws = [k.sb(f'ws{i}', [128, 2048], F32) for i in range(2)]
        self.ws_t = trks('ws', 2)
        self.wr = [k.sb(f'wr{i}', [128, 2048], F32R) for i in range(2)]
        self.wr_t = trks('wr', 2)
        self.ws_i = 0
        self.wr_i = 0

    def wpiece(self, segs, rounded=True, dest=None):
        k = self.k
        si = self.ws_i
        self.ws_i = (si + 1) % 2
        st, stt = self.ws[si], self.ws_t[si]
        off = 0
        views = []
        key = []
        percore = False
        for ap in segs:
            rows, ncols = ap.shape
            kc = rows // 128
            pat = tuple((int(a), int(b)) for a, b in ap.ap)
            assert len(pat) == 2 and pat[1][0] == 1, pat
            name = ap.tensor.name
            percore = percore or name in ('ck', 'cv')
            key.append((name, int(ap.offset), pat[0][0], rows, ncols))
            views.append((off, kc, ncols))
            off += kc * ncols
        key = tuple(key)
        pk = 'cpk' if percore else 'wpk'
        if key not in self.wkeys:
            self.wkeys[key] = (pk, self.wcols[pk])
            self.wplan[pk].append((self.wcols[pk], key))
            self.wcols[pk] += off
        pk, c0 = self.wkeys[key]
        k.dma(st[:, 0:off], self.d[pk][:, c0:c0 + off], writes=[stt])
        if not rounded:
            outs = [st[:, o:o + kc * n].rearrange("p (c n) -> p c n", c=kc) for (o, kc, n) in views]
            return outs, stt
        if dest is not None:
            rt, rtt = dest
        else:
            ri = self.wr_i
            self.wr_i = (ri + 1) % 2
            rt, rtt = self.wr[ri], self.wr_t[ri]
        k.emit('act', lambda e: e.activation(out=rt[:, 0:off], in_=st[:, 0:off], func=AF.Copy), [stt], [rtt])
        outs = [rt[:, o:o + kc * n].rearrange("p (c n) -> p c n", c=kc) for (o, kc, n) in views]
        return outs, rtt

    def build(self):
        nc, k, d, cfg = self.nc, self.k, self.d, self.cfg
        self.ps = [self.es.enter_context(nc.psum_tensor(f'ps{i}', [128, 512], F32)) for i in range(8)]
        self.ps_t = trks('ps', 8)
        for t in self.ps_t:
            t.psum = True
        self.PT = k.sb('PT', [128, PC.n])
        self.PT_t = Trk('PT')
        self.CT = k.sb('CT', [128, CC.n])
        self.CT_t = Trk('CT')
        self.onesR = k.sb('onesR', [128, 128], F32R)
        self.onesR_t = Trk('onesR')
        self.identR = k.sb('identR', [128, 128], F32R)
        self.identR_t = Trk('identR')
        self.xT = [k.sb(f'xT{h}', [128, NCH, TOK]) for h in range(2)]
        self.xT_t = trks('xT', 2, NCH, 2)
        self.hT = k.sb('hT', [128, NCH, TOK], F32R)
        self.hT_t = trks('hT', NCH, 2)
        self.permR = k.sb('permR', [128, 128], F32R)
        self.permR_t = Trk('permR')
        self.AV = k.sb('AV', [128, 8])
        self.AV_t = Trk('AV')
        self.MODs = [k.sb(f'MOD{i}', [128, 48, 2]) for i in range(2)]
        self.MODs_t = trks('MOD', 2)
        self.ABs = [k.sb(f'AB{i}', [128, 2, 2, 8, 2]) for i in range(2)]
        self.ABs_t = trks('AB', 2)
        self.modgen = None
        self.sc = k.sb('sc', [128, 16])
        self.sc_t = Trk('sc')
        self.init_wpool()

        k.dma(self.PT[:], d['params'][:, :], writes=[self.PT_t])
        k.dma(self.CT[:], d['consts'][:, :], writes=[self.CT_t])
        for h in range(2):
            k.dma_group([(self.xT[h][:, c, :], d['xin'][h, c, :, :]) for c in range(NCH)], self.xT_t[h])
        o, w = CC['ones']
        k.emit('dve', lambda e: e.tensor_copy(out=self.onesR[:], in_=self.CT[:, o:o + w]), [self.CT_t], [self.onesR_t])
        o2, w2 = CC['ident']
        k.emit('dve', lambda e: e.tensor_copy(out=self.identR[:], in_=self.CT[:, o2:o2 + w2]), [self.CT_t], [self.identR_t])
        o3, w3 = CC['perm']
        k.emit('dve', lambda e: e.tensor_copy(out=self.permR[:], in_=self.CT[:, o3:o3 + w3]), [self.CT_t], [self.permR_t])
        oc, wc = PC['cond']
        k.emit('act', lambda e: e.activation(out=self.sc[:], in_=self.PT[:, oc:oc + wc], func=AF.Silu), [self.PT_t], [self.sc_t])

        nl = cfg.get('layers', DEPTH)
        for _ in self.modulation(0):
            pass
        for layer in range(nl):
            p = layer % 2
            self.MOD, self.MOD_t, self.AB, self.AB_t = self.MODs[p], self.MODs_t[p], self.ABs[p], self.ABs_t[p]
            for half in range(2):
                if cfg.get('mixers', True):
                    self.rmsnorm_mod(layer, 0, half)
                    self.mixer(layer, half)
                if half == 1 and layer + 1 < nl:
                    self.modgen = self.modulation(layer + 1)
                if cfg.get('ffn', True):
                    self.rmsnorm_mod(layer, 1, half)
                    self.ffn(layer, half)
                if self.modgen is not None:
                    for _ in self.modgen:
                        pass
                    self.modgen = None
        self.final(cfg)
        k.finish()

    def modulation(self, layer):
        k, d = self.k, self.d
        p = layer % 2
        MOD, MOD_t, AB, AB_t = self.MODs[p], self.MODs_t[p], self.ABs[p], self.ABs_t[p]
        ps, pst = self.ps[7], self.ps_t[7]
        scv = self.sc[:].rearrange("p (c j) -> p c j", j=2)
        for piece in range(24):
            (w,), wt = self.wpiece([d['w_mod'][layer, :, piece * 256:(piece + 1) * 256]], rounded=False)
            for q2 in range(2):
                q = piece * 2 + q2
                for c in range(NCH):
                    k.emit('pe', lambda e, c=c, q2=q2, q=q: e.matmul(
                        ps[:, q * 2:q * 2 + 2], w[:, c, q2 * 128:(q2 + 1) * 128], scv[:, c, :],
                        start=(c == 0), stop=(c == NCH - 1)), [wt, self.sc_t], [pst])
            yield piece
        ob, wb = PC['b_mod']
        bm = self.PT[:, ob + layer * 48: ob + layer * 48 + 48]
        k.emit('dve', lambda e: e.tensor_tensor(
            out=MOD[:], in0=ps[:, 0:96].rearrange("p (q j) -> p q j", j=2),
            in1=bm.unsqueeze(2).broadcast_to([128, 48, 2]), op=ALU.add), [pst, self.PT_t], [MOD_t])
        og, wg = PC['norm_g']
        for s in range(2):
            g = self.PT[:, og + (layer * 2 + s) * 8: og + (layer * 2 + s) * 8 + 8]
            sh = MOD[:, s * 24 + 0: s * 24 + 8, :]
            scl = MOD[:, s * 24 + 8: s * 24 + 16, :]
            A = AB[:, s, 0, :, :]
            B = AB[:, s, 1, :, :]
            k.emit('dve', lambda e, scl=scl, A=A: e.tensor_scalar(
                out=A, in0=scl, scalar1=1.0, scalar2=32.0, op0=ALU.add, op1=ALU.mult), [MOD_t], [AB_t])
            k.emit('dve', lambda e, A=A, g=g: e.tensor_tensor(
                out=A, in0=A, in1=g.unsqueeze(2).broadcast_to([128, 8, 2]), op=ALU.mult), [AB_t, self.PT_t], [AB_t])
            k.emit('dve', lambda e, B=B, sh=sh: e.tensor_copy(out=B, in_=sh), [MOD_t], [AB_t])

    def gate(self, s, c, half):
        return self.MOD[:, s * 24 + 16 + c, half:half + 1]

    def rmsnorm_mod(self, layer, s, half):
        k = self.k
        with ExitStack() as es:
            sq = self.tmp(es, 'nsq', [128, NCH, 512], F32R)
            sq_t = Trk('nsq')
            tmp = self.tmp(es, 'ntmp', [128, NCH, 512], F32)
            tmp_t = Trk('ntmp')
            rstd = self.tmp(es, 'nrstd', [128, 512], F32)
            rstd_t = Trk('nrstd')
            for tt in range(2):
                xs = self.xT[half][:, :, tt * 512:(tt + 1) * 512]
                xs_t = [self.xT_t[half][c][tt] for c in range(NCH)]
                k.emit('act', lambda e: e.activation(out=sq[:], in_=xs, func=AF.Square), xs_t, [sq_t])
                ps, pst = self.ps[6], self.ps_t[6]
                for c in range(NCH):
                    k.emit('pe', lambda e, c=c: e.matmul(ps[:], self.onesR[:], sq[:, c, :], start=(c == 0), stop=(c == NCH - 1)),
                           [self.onesR_t, sq_t], [pst])
                k.emit('act', lambda e: e.activation(out=rstd[:], in_=ps[:], func=AF.Sqrt, bias=float(D * EPS), scale=1.0),
                       [pst], [rstd_t])
                k.emit('dve', lambda e: e.reciprocal(out=rstd[:], in_=rstd[:]), [rstd_t], [rstd_t])
                k.emit('dve', lambda e: e.tensor_tensor(out=tmp[:], in0=xs, in1=rstd[:].unsqueeze(1).broadcast_to([128, NCH, 512]),
                                                        op=ALU.mult), xs_t + [rstd_t], [tmp_t])
                for c in range(NCH):
                    A = self.AB[:, s, 0, c, half:half + 1]
                    B = self.AB[:, s, 1, c, half:half + 1]
                    out = self.hT[:, c, tt * 512:(tt + 1) * 512]
                    if c % 2 == 0:
                        k.emit('act', lambda e, c=c, A=A, B=B, out=out: e.activation(
                            out=out, in_=tmp[:, c, :], func=AF.Identity, bias=B, scale=A),
                            [tmp_t, self.AB_t], [self.hT_t[c][tt]])
                    else:
                        k.emit('dve', lambda e, c=c, A=A, B=B, out=out: e.tensor_scalar(
                            out=out, in0=tmp[:, c, :], scalar1=A, scalar2=B, op0=ALU.mult, op1=ALU.add),
                            [tmp_t, self.AB_t], [self.hT_t[c][tt]])
            k.barrier()

    def mixer(self, layer, half):
        kind = layer % 3
        ml = self.cfg.get('mixlist', (0, 1, 2))
        if kind not in ml:
            return
        if kind == 0:
            if half == 0:
                self.attn_prep(layer)
            self.attn(layer, half)
        elif kind == 1:
            self.hgrn(layer, half)
        else:
            self.gdn(layer, half)

    def out_pf(self, w_out_rows):
        return PF(self, [[w_out_rows[:, dp * 256:(dp + 1) * 256]] for dp in range(4)])

    def out_proj(self, w_out_rows, src, src_t, nh, half, banks, pf=None):
        k = self.k
        bi = 0
        pf = pf if pf is not None else self.out_pf(w_out_rows)
        for dp in range(4):
            (wo,), wot = pf.get(dp)
            for dmi in range(2):
                dm = dp * 2 + dmi
                for tt in range(2):
                    b = banks[bi % len(banks)]
                    bi += 1
                    ps, pst = self.ps[b], self.ps_t[b]
                    for hl in range(nh):
                        k.emit('pe', lambda e, ps=ps, hl=hl, dmi=dmi, tt=tt: e.matmul(
                            ps[:], wo[:, hl, dmi * 128:(dmi + 1) * 128], src[:, hl, tt * 512:(tt + 1) * 512],
                            start=(hl == 0), stop=(hl == nh - 1)), [wot, src_t[hl]], [pst])
                    xs = self.xT[half][:, dm, tt * 512:(tt + 1) * 512]
                    k.emit('dve', lambda e, ps=ps, xs=xs, dm=dm: e.scalar_tensor_tensor(
                        out=xs, in0=ps[:], scalar=self.gate(0, dm, half), in1=xs, op0=ALU.mult, op1=ALU.add),
                        [pst, self.MOD_t, self.xT_t[half][dm][tt]], [self.xT_t[half][dm][tt]])

    def head_norm(self, tmps, src, src_t, dst, dst_t, gcol, eps_total, bank, extra_mul=None, extra_t=None):
        k = self.k
        sq, sq_t, rs, rs_t = tmps
        ps, pst = self.ps[bank], self.ps_t[bank]
        for tt in range(2):
            sl = slice(tt * 512, (tt + 1) * 512)
            k.emit('act', lambda e, sl=sl: e.activation(out=sq[:], in_=src[:, sl], func=AF.Square), [src_t], [sq_t])
            k.emit('pe', lambda e: e.matmul(ps[:], self.onesR[:], sq[:], start=True, stop=True), [self.onesR_t, sq_t], [pst])
            k.emit('act', lambda e: e.activation(out=rs[:], in_=ps[:], func=AF.Sqrt, bias=float(eps_total), scale=1.0), [pst], [rs_t])
            k.emit('dve', lambda e: e.reciprocal(out=rs[:], in_=rs[:]), [rs_t], [rs_t])
            if extra_mul is not None:
                k.emit('dve', lambda e, sl=sl: e.tensor_tensor(out=rs[:], in0=rs[:], in1=extra_mul[:, sl], op=ALU.mult), [rs_t, extra_t], [rs_t])
            k.emit('dve', lambda e, sl=sl: e.scalar_tensor_tensor(out=dst[:, sl], in0=src[:, sl], scalar=gcol, in1=rs[:], op0=ALU.mult, op1=ALU.mult),
                   [src_t, rs_t, self.PT_t, self.AV_t] + ([self.HV_t] if hasattr(self, 'HV_t') else []), [dst_t])

    def norm_tmps(self, es):
        return (self.tmp(es, 'hsq', [128, 512], F32R), Trk('hsq'), self.tmp(es, 'hrs', [128, 512], F32), Trk('hrs'))

    def attn_prep(self, layer):
        k = self.k
        j = layer // 3
        lam_init = 0.8 - 0.6 * math.exp(-0.3 * layer)
        with ExitStack() as es:
            lt = self.tmp(es, 'ltab', [128, 512], F32)
            lt_t = self.ptrk('ltab')
            k.dma(lt[:], self.d['lamtab'][:, :], writes=[lt_t])
            pr = self.tmp(es, 'lpr', [128, 2, 64], F32)
            pr_t = Trk('lpr')
            sm = self.tmp(es, 'lsm', [128, 2], F32)
            sm_t = Trk('lsm')
            base = j * 256
            lq = lt[:, base:base + 256].rearrange("p (a r n) -> p a r n", a=2, r=2)
            k.emit('dve', lambda e: e.tensor_tensor(out=pr[:], in0=lq[:, :, 0, :], in1=lq[:, :, 1, :], op=ALU.mult), [lt_t], [pr_t])
            k.emit('dve', lambda e: e.reduce_sum(out=sm[:], in_=pr[:], axis=AX.X), [pr_t], [sm_t])
            k.emit('act', lambda e: e.activation(out=sm[:], in_=sm[:], func=AF.Exp), [sm_t], [sm_t])
            k.emit('dve', lambda e: e.tensor_tensor(out=self.AV[:, 0:1], in0=sm[:, 0:1], in1=sm[:, 1:2], op=ALU.subtract), [sm_t], [self.AV_t])
            k.emit('dve', lambda e: e.tensor_scalar(out=self.AV[:, 0:1], in0=self.AV[:, 0:1], scalar1=float(lam_init), scalar2=None, op0=ALU.add),
                   [self.AV_t], [self.AV_t])
            k.emit('dve', lambda e: e.tensor_scalar(out=self.AV[:, 1:2], in0=self.AV[:, 0:1], scalar1=-1.0, scalar2=None, op0=ALU.mult),
                   [self.AV_t], [self.AV_t])
            osl, _ = PC['subln']
            k.emit('dve', lambda e: e.tensor_scalar(out=self.AV[:, 2:3], in0=self.PT[:, osl + j:osl + j + 1],
                                                    scalar1=float((1.0 - lam_init) * math.sqrt(128.0)), scalar2=None, op0=ALU.mult),
                   [self.PT_t, self.AV_t], [self.AV_t])
            k.barrier()

    def attn(self, layer, half):
        k, d, nc = self.k, self.d, self.nc
        j = layer // 3
        w_in = d['attn_w_in']
        scale = 0.125
        nkc = 2 if half == 0 else 12
        with ExitStack() as es:
            GH = 2
            V = self.tmp(es, 'aV', [128, 8, GH * 128], F32R)
            V_t = self.ptrk('aV', 8)
            ntm = self.norm_tmps(es)
            agrp = self.tmp(es, 'agrp', [128, GH, TOK], F32R)
            agrp_t = trks('agrp', GH)
            QT = [self.tmp(es, f'aQ{i}', [128, TOK], F32R) for i in range(1)]
            QT_t = trks('aQ', 1)
            KT = [self.tmp(es, f'aK{i}', [128, TOK], F32R) for i in range(1)]
            KT_t = self.ptrk('aK', 1)
            Pt = [self.tmp(es, f'aP{i}', [128, 512], F32R) for i in range(2)]
            Pt_t = trks('aP', 2)
            att = self.tmp(es, 'att', [128, TOK], F32)
            att_t = Trk('att')
            Rr = self.tmp(es, 'aR', [128, 2, 512], F32)
            Rr_t = Trk('aR')
            Tt, Tt_t = Rr, Rr_t
            if half == 1:
                ropet = self.tmp(es, 'arope', [128, 2048], F32)
                ropet_t = self.ptrk('arope')
                k.dma(ropet[:], d['rope'][:, :], writes=[ropet_t])
                COS = ropet[:, 0:1024]
                SIN = ropet[:, 1024:2048]
                raw = [self.tmp(es, f'araw{i}', [128, 512], F32R) for i in range(1)]
                raw_t = trks('araw', 1)
                ri_ = 0
                t1 = self.tmp(es, 'at1', [128, 512], F32)
                t1_t = Trk('at1')
                kcr = self.tmp(es, 'akcr', [128, 512], F32R)
                kcr_t = Trk('akcr')
                vcr = self.tmp(es, 'avcr', [128, 4 * GH * 128], F32R)
                vcr_t = Trk('avcr')
            pi = 0
            pb = 0
            for grp in range(8 // GH):
                c0 = 2 * D + grp * 256
                gpf = PF(self, [[w_in[j, :, c0:c0 + 256]]] + [[w_in[j, :, (grp * GH + t) * 128:(grp * GH + t + 1) * 128],
                                                              w_in[j, :, D + (grp * GH + t) * 128:D + (grp * GH + t + 1) * 128]] for t in range(GH)])
                for piece in range(1):
                    (wv,), wvt = gpf.get(0)
                    for tile in range(8):
                        b = 6 + (pb % 2)
                        pb += 1
                        ps, pst = self.ps[b], self.ps_t[b]
                        for c in range(NCH):
                            k.emit('pe', lambda e, ps=ps, c=c, tile=tile: e.matmul(
                                ps[:, 0:256], self.hT[:, c, tile * 128:(tile + 1) * 128], wv[:, c, :],
                                start=(c == 0), stop=(c == NCH - 1)), [wvt, self.hT_t[c][tile // 4]], [pst])
                        self.copy(self.evac_eng(), V[:, tile, piece * 256:(piece + 1) * 256], ps[:, 0:256], [pst], [V_t[tile]])
                if half == 0:
                    for tile in range(8):
                        k.dma(d['vout'][j, tile * 128:(tile + 1) * 128, grp * 256:(grp + 1) * 256], V[:, tile, :].bitcast(F32), reads=[V_t[tile]])
                else:
                    (vcv,), _ = self.wpiece([d['cv'][j, :, grp * 256:(grp + 1) * 256]], dest=(vcr, vcr_t))
                for hl in range(GH):
                    hh = grp * GH + hl
                    qi = 0
                    Q, Q_t, Kk, K_t = QT[qi], QT_t[qi], KT[qi], KT_t[qi]
                    (wq, wk), wt = gpf.get(1 + hl)
                    for wi, (w, dst, dst_t) in enumerate(((wq, Q, Q_t), (wk, Kk, K_t))):
                        for tt in range(2):
                            sl = slice(tt * 512, (tt + 1) * 512)
                            b = 6 + (pb % 2)
                            pb += 1
                            ps, pst = self.ps[b], self.ps_t[b]
                            for c in range(NCH):
                                k.emit('pe', lambda e, ps=ps, w=w, c=c, sl=sl: e.matmul(
                                    ps[:], w[:, c, :], self.hT[:, c, sl],
                                    start=(c == 0), stop=(c == NCH - 1)), [wt, self.hT_t[c][tt]], [pst])
                            if half == 0:
                                self.copy(self.evac_eng(), dst[:, sl], ps[:], [pst], [dst_t])
                            else:
                                rw, rw_t = raw[0], raw_t[0]
                                ri_ += 1
                                self.copy('act', rw[:], ps[:], [pst], [rw_t])
                                b2 = 6 + (pb % 2)
                                pb += 1
                                ps2, ps2t = self.ps[b2], self.ps_t[b2]
                                k.emit('pe', lambda e, ps2=ps2, rw=rw: e.matmul(ps2[:], self.permR[:], rw[:], start=True, stop=True),
                                       [self.permR_t, rw_t], [ps2t])
                                k.emit('pool', lambda e, rw=rw, sl=sl: e.tensor_tensor(out=t1[:], in0=rw[:].bitcast(F32), in1=COS[:, sl], op=ALU.mult),
                                       [rw_t, ropet_t], [t1_t])
                                k.emit('dve', lambda e, ps2=ps2, sl=sl, dst=dst: e.tensor_tensor(out=dst[:, sl], in0=ps2[:], in1=SIN[:, sl], op=ALU.mult),
                                       [ps2t, ropet_t], [dst_t])
                                k.emit('dve', lambda e, dst=dst, sl=sl: e.tensor_tensor(out=dst[:, sl], in0=dst[:, sl].bitcast(F32), in1=t1[:], op=ALU.add),
                                       [t1_t, dst_t], [dst_t])
                    if half == 0:
                        k.dma(d['kout'][j, hh, :, :], Kk[:].bitcast(F32), reads=[K_t])
                    else:
                        self.wpiece([d['ck'][j, hh, :, :]], dest=(kcr, kcr_t))

                    def keyT(comp, kc, s=0):
                        r = slice(comp * 64, (comp + 1) * 64)
                        if half == 0:
                            return Kk[r, s * 256 + kc * 128: s * 256 + (kc + 1) * 128], K_t
                        if kc < 4:
                            return kcr[r, kc * 128:(kc + 1) * 128], kcr_t
                        return Kk[r, (kc - 4) * 128:(kc - 3) * 128], K_t

                    def valT(kc, s=0):
                        cs = slice(hl * 128, (hl + 1) * 128)
                        if half == 0:
                            return V[:, s * 2 + kc, cs], V_t[s * 2 + kc]
                        if kc < 4:
                            return vcv[:, kc, cs], vcr_t
                        return V[:, kc - 4, cs], V_t[kc - 4]

                    if half == 0:
                        qw = 256
                        units = [(s, 0) for s in range(4)]
                    else:
                        qw = 512
                        units = [(0, qt) for qt in range(2)]
                    for (s, qt) in units:
                        q0 = s * 256 if half == 0 else qt * 512
                        for comp in range(2):
                            r = slice(comp * 64, (comp + 1) * 64)
                            if half == 0:
                                ob, zb = 2, 3
                                osl = slice(comp * 256, (comp + 1) * 256)
                            else:
                                ob, zb = 2 + comp * 2, 3 + comp * 2
                                osl = slice(0, 512)
                            pO, pO_t = self.ps[ob], self.ps_t[ob]
                            pZ, pZ_t = self.ps[zb], self.ps_t[zb]
                            if half == 0:
                                sb_ = pi % 2
                                pS, pS_t = self.ps[sb_], self.ps_t[sb_]
                                P_, P_t = Pt[pi % 2], Pt_t[pi % 2]
                                pi += 1
                                for kc in range(2):
                                    kl, kl_t = keyT(comp, kc, s)
                                    k.emit('pe', lambda e, pS=pS, kl=kl, kc=kc, r=r, q0=q0: e.matmul(
                                        pS[:, kc * 256:(kc + 1) * 256], kl, Q[r, q0:q0 + 256], start=True, stop=True),
                                        [kl_t, Q_t], [pS_t])
                                k.emit('act', lambda e, pS=pS, P_=P_: e.activation(out=P_[:], in_=pS[:], func=AF.Exp, scale=scale), [pS_t], [P_t])
                                for kc in range(2):
                                    vl, vl_t = valT(kc, s)
                                    k.emit('pe', lambda e, pO=pO, vl=vl, P_=P_, kc=kc, osl=osl: e.matmul(
                                        pO[:, osl], vl, P_[:, kc * 256:(kc + 1) * 256], start=(kc == 0), stop=(kc == 1)),
                                        [vl_t, P_t], [pO_t])
                                for kc in range(2):
                                    k.emit('pe', lambda e, pZ=pZ, P_=P_, kc=kc, osl=osl: e.matmul(
                                        pZ[:, osl], self.onesR[:], P_[:, kc * 256:(kc + 1) * 256], start=(kc == 0), stop=(kc == 1)),
                                        [self.onesR_t, P_t], [pZ_t])
                            else:
                                for kc in range(nkc):
                                    sb_ = pi % 2
                                    pS, pS_t = self.ps[sb_], self.ps_t[sb_]
                                    P_, P_t = Pt[pi % 2], Pt_t[pi % 2]
                                    pi += 1
                                    kl, kl_t = keyT(comp, kc)
                                    k.emit('pe', lambda e, pS=pS, kl=kl, r=r, q0=q0: e.matmul(
                                        pS[:], kl, Q[r, q0:q0 + 512], start=True, stop=True), [kl_t, Q_t], [pS_t])
                                    k.emit('act', lambda e, pS=pS, P_=P_: e.activation(out=P_[:], in_=pS[:], func=AF.Exp, scale=scale), [pS_t], [P_t])
                                    vl, vl_t = valT(kc)
                                    k.emit('pe', lambda e, pO=pO, vl=vl, P_=P_, kc=kc: e.matmul(
                                        pO[:], vl, P_[:], start=(kc == 0), stop=(kc == nkc - 1)), [vl_t, P_t], [pO_t])
                                    k.emit('pe', lambda e, pZ=pZ, P_=P_, kc=kc: e.matmul(
                                        pZ[:], self.onesR[:], P_[:], start=(kc == 0), stop=(kc == nkc - 1)), [self.onesR_t, P_t], [pZ_t])
                        if half == 0:
                            pO, pO_t, pZ, pZ_t = self.ps[2], self.ps_t[2], self.ps[3], self.ps_t[3]
                            k.emit('dve', lambda e, pZ=pZ: e.reciprocal(out=Rr[:, 0, :], in_=pZ[:]), [pZ_t], [Rr_t])
                            k.emit('dve', lambda e, pO=pO: e.tensor_tensor(out=Tt[:, 0, :], in0=pO[:], in1=Rr[:, 0, :], op=ALU.mult), [pO_t, Rr_t], [Tt_t])
                            k.emit('dve', lambda e, q0=q0: e.scalar_tensor_tensor(
                                out=att[:, q0:q0 + 256], in0=Tt[:, 0, 256:512], scalar=self.AV[:, 1:2], in1=Tt[:, 0, 0:256],
                                op0=ALU.mult, op1=ALU.add), [Tt_t, self.AV_t], [att_t])
                        else:
                            for comp in range(2):
                                pO, pO_t = self.ps[2 + comp * 2], self.ps_t[2 + comp * 2]
                                pZ, pZ_t = self.ps[3 + comp * 2], self.ps_t[3 + comp * 2]
                                k.emit('dve', lambda e, pZ=pZ, comp=comp: e.reciprocal(out=Rr[:, comp, :], in_=pZ[:]), [pZ_t], [Rr_t])
                                k.emit('dve', lambda e, pO=pO, comp=comp: e.tensor_tensor(out=Tt[:, comp, :], in0=pO[:], in1=Rr[:, comp, :], op=ALU.mult),
                                       [pO_t, Rr_t], [Tt_t])
                            k.emit('dve', lambda e, q0=q0: e.scalar_tensor_tensor(
                                out=att[:, q0:q0 + 512], in0=Tt[:, 1, :], scalar=self.AV[:, 1:2], in1=Tt[:, 0, :],
                                op0=ALU.mult, op1=ALU.add), [Tt_t, self.AV_t], [att_t])
                    self.head_norm(ntm, att[:], att_t, agrp[:, hl, :], agrp_t[hl], self.AV[:, 2:3], 128.0 * 1e-5, 6 + (pb % 2))
                    pb += 1
                self.out_proj(d['attn_w_out'][j, grp * GH * 128:(grp + 1) * GH * 128, :], agrp, agrp_t, GH, half, [6, 7, 0, 1])
            k.barrier()

    def hgrn_prep(self, layer):
        k = self.k
        ol, _ = PC['hgrn_lb']
        self.HV = self.k.sb('HV', [128, 3, 8])
        self.HV_t = Trk('HV')
        with ExitStack() as es:
            ex = self.tmp(es, 'hex', [128, 4, 8], F32)
            ex_t = Trk('hex')
            tot = self.tmp(es, 'htot', [128, 8], F32)
            tot_t = Trk('htot')
            k.emit('act', lambda e: e.activation(out=ex[:], in_=self.PT[:, ol:ol + 32].rearrange("p (l c) -> p l c", l=4), func=AF.Exp),
                   [self.PT_t], [ex_t])
            k.emit('dve', lambda e: e.tensor_tensor(out=tot[:], in0=ex[:, 0, :], in1=ex[:, 1, :], op=ALU.add), [ex_t], [tot_t])
            for l in (2, 3):
                k.emit('dve', lambda e, l=l: e.tensor_tensor(out=tot[:], in0=tot[:], in1=ex[:, l, :], op=ALU.add), [ex_t, tot_t], [tot_t])
            k.emit('dve', lambda e: e.reciprocal(out=tot[:], in_=tot[:]), [tot_t], [tot_t])
            k.emit('dve', lambda e: e.tensor_copy(out=self.HV[:, 0, :], in_=ex[:, 1, :]), [ex_t], [self.HV_t])
            for l in range(2, layer + 1):
                k.emit('dve', lambda e, l=l: e.tensor_tensor(out=self.HV[:, 0, :], in0=self.HV[:, 0, :], in1=ex[:, l, :], op=ALU.add),
                       [ex_t, self.HV_t], [self.HV_t])
            k.emit('dve', lambda e: e.tensor_tensor(out=self.HV[:, 0, :], in0=self.HV[:, 0, :], in1=tot[:], op=ALU.mult), [tot_t, self.HV_t], [self.HV_t])
            k.emit('dve', lambda e: e.tensor_scalar(out=self.HV[:, 1, :], in0=self.HV[:, 0, :], scalar1=-1.0, scalar2=1.0, op0=ALU.mult, op1=ALU.add),
                   [self.HV_t], [self.HV_t])
            on, _ = PC['hgrn_norm']
            k.emit('dve', lambda e: e.tensor_scalar(out=self.HV[:, 2, 0:1], in0=self.PT[:, on:on + 1], scalar1=float(math.sqrt(128.0)), scalar2=None, op0=ALU.mult),
                   [self.PT_t, self.HV_t], [self.HV_t])
            k.barrier()

    def hgrn(self, layer, half):
        k, d, nc = self.k, self.d, self.nc
        j = layer // 3
        if half == 0:
            self.hgrn_prep(layer)
        w_in = d['hgrn_w_in']
        nseq = 4 if half == 0 else 1
        cps = 16 // nseq
        oo, _ = CC['ones']
        ONES = self.CT[:, oo:oo + 1].broadcast_to([128, TOK])
        oi, _ = CC['ident']
        IDENT = self.CT[:, oi:oi + 128]
        masks = []
        for nm in ('mask_f', 'mask_b'):
            om, _ = CC[nm]
            masks.append(self.CT[0:64, om:om + 256])
        with ExitStack() as es:
            V64 = self.tmp(es, 'hV', [64, 16, 128], F32)
            V64_t = trks('hV', 16)
            mix = self.tmp(es, 'hmix', [128, 1, TOK], F32R)
            mix_t = trks('hmix', 1)
            ntm = self.norm_tmps(es)
            qT = self.tmp(es, 'hq', [128, TOK], F32)
            qT_t = Trk('hq')
            gs, gs_t = qT, qT_t
            oT = self.tmp(es, 'ho', [128, TOK], F32)
            oT_t = Trk('ho')
            Fb = [self.tmp(es, f'hF{i}', [128, TOK], F32) for i in range(2)]
            Fb_t = trks('hF', 2)
            L = self.tmp(es, 'hL', [128, TOK], F32)
            L_t = Trk('hL')
            Gp = self.tmp(es, 'hGp', [128, 64 + TOK + 64], F32)
            Gp_t = Trk('hGp')
            E1 = self.tmp(es, 'hE1', [128, TOK], F32)
            E1_t = Trk('hE1')
            E2 = self.tmp(es, 'hE2', [128, TOK], F32)
            E2_t = Trk('hE2')
            Am = self.tmp(es, 'hAm', [64, 4, 64], F32)
            Am_t = Trk('hAm')
            Ktok = self.tmp(es, 'hKt', [64, 4, 128], F32)
            Ktok_t = Trk('hKt')
            Sb = [self.tmp(es, f'hS{i}', [128, 128], F32) for i in range(2)]
            Sb_t = trks('hS', 2)
            DK = self.tmp(es, 'hDK', [128, 3, 16], F32)
            DK_t = Trk('hDK')
            G3 = self.tmp(es, 'hG3', [128, 3, 16], F32)
            G3_t = Trk('hG3')
            Sp = [self.tmp(es, f'hSp{i}', [128, 128], F32) for i in range(4)]
            Sp_t = trks('hSp', 4)
            tS = self.tmp(es, 'htS', [128, 128], F32)
            tS_t = Trk('htS')
            spi = 0
            sout_t = self.ptrk('hso')
            if half == 0:
                sout = self.tmp(es, 'hso', [128, 4, 2, 128], F32)
            k.emit('dve', lambda e: e.memset(Gp[:], 0.0), [], [Gp_t])
            pb = 0
            for pair in range(4):
                for hl in range(2):
                    hh = pair * 2 + hl
                    hpf = PF(self, [[w_in[j, :, 3 * D + hh * 128:3 * D + (hh + 1) * 128]],
                                    [w_in[j, :, hh * 128:(hh + 1) * 128], w_in[j, :, D + hh * 128:D + (hh + 1) * 128]],
                                    [w_in[j, :, 2 * D + hh * 128:2 * D + (hh + 1) * 128]]])
                    (wi,), wit = hpf.get(0)
                    for tt in range(2):
                        sl = slice(tt * 512, (tt + 1) * 512)
                        b = 6 + (pb % 2)
                        pb += 1
                        ps, pst = self.ps[b], self.ps_t[b]
                        for c in range(NCH):
                            k.emit('pe', lambda e, ps=ps, c=c, sl=sl: e.matmul(ps[:], wi[:, c, :], self.hT[:, c, sl], start=(c == 0), stop=(c == NCH - 1)),
                                   [wit, self.hT_t[c][tt]], [pst])
                        self.copy(self.evac_eng(), L[:, sl], ps[:], [pst], [L_t])
                    for g4 in range(4):
                        b = 6 + (pb % 2)
                        pb += 1
                        ps, pst = self.ps[b], self.ps_t[b]
                        for q_ in range(4):
                            ch = g4 * 4 + q_
                            k.emit('pe', lambda e, ps=ps, q_=q_, ch=ch: e.transpose(ps[0:64, q_ * 128:(q_ + 1) * 128], L[:, ch * 64:(ch + 1) * 64], IDENT),
                                   [L_t, self.CT_t], [pst])
                        self.copy(self.evac_eng(), V64[:, g4 * 4:(g4 + 1) * 4, :].rearrange("p a n -> p (a n)"), ps[0:64, :], [pst], V64_t[g4 * 4:(g4 + 1) * 4])
                    lbc = self.HV[:, 0, hh:hh + 1]
                    omc = self.HV[:, 1, hh:hh + 1]
                    (wq, wzf), wt1 = hpf.get(1)
                    (wzb,), wt2 = hpf.get(2)
                    for (w, wt, kindp) in ((wq, wt1, 'q'), (wzf, wt1, 'zf'), (wzb, wt2, 'zb')):
                        for tt in range(2):
                            sl = slice(tt * 512, (tt + 1) * 512)
                            b = 6 + (pb % 2)
                            pb += 1
                            ps, pst = self.ps[b], self.ps_t[b]
                            for c in range(NCH):
                                k.emit('pe', lambda e, ps=ps, w=w, c=c, sl=sl: e.matmul(
                                    ps[:], w[:, c, :], self.hT[:, c, sl], start=(c == 0), stop=(c == NCH - 1)),
                                    [wt, self.hT_t[c][tt]], [pst])
                            if kindp == 'q':
                                k.emit('act', lambda e, ps=ps, sl=sl: e.activation(out=qT[:, sl], in_=ps[:], func=AF.Copy, scale=float(128.0 ** -0.5)),
                                       [pst], [qT_t])
                            elif kindp == 'g':
                                k.emit('act', lambda e, ps=ps, sl=sl: e.activation(out=gs[:, sl], in_=ps[:], func=AF.Silu), [pst], [gs_t])
                            else:
                                di = 0 if kindp == 'zf' else 1
                                k.emit('act', lambda e, ps=ps, sl=sl, di=di: e.activation(out=Fb[di][:, sl], in_=ps[:], func=AF.Sigmoid), [pst], [Fb_t[di]])
                    (wg,), wt3 = self.wpiece([w_in[j, :, 4 * D + hh * 128:4 * D + (hh + 1) * 128]])
                    opf = self.out_pf(d['hgrn_w_out'][j, hh * 128:(hh + 1) * 128, :])
                    opf.h[0] = self.wpiece(opf.specs[0])
                    for di in range(2):
                        F_, F_t = Fb[di], Fb_t[di]
                        k.emit('dve', lambda e, F_=F_: e.tensor_scalar(out=F_[:], in0=F_[:], scalar1=omc, scalar2=lbc, op0=ALU.mult, op1=ALU.add),
                               [F_t, self.HV_t], [F_t])
                        k.emit('act', lambda e, F_=F_: e.activation(out=L[:], in_=F_[:], func=AF.Ln), [F_t], [L_t])
                        k.emit('pool', lambda e, F_=F_: e.tensor_scalar(out=F_[:], in0=F_[:], scalar1=-1.0, scalar2=1.0, op0=ALU.mult, op1=ALU.add),
                               [F_t], [F_t])
                        k.emit('dve', lambda e: e.tensor_tensor_scan(out=Gp[:, 64:64 + TOK], data0=ONES, data1=L[:], initial=0.0,
                                                                     op0=ALU.mult, op1=ALU.add), [L_t, self.CT_t], [Gp_t])
                        Lv = L[:].rearrange("p (j n) -> p j n", n=64)
                        if di == 0:
                            gprev = Gp[:, 63:63 + TOK].rearrange("p (j n) -> p j n", n=64)[:, :, 0:1].broadcast_to([128, 16, 64])
                            gcur = Gp[:, 64:64 + TOK].rearrange("p (j n) -> p j n", n=64)
                            k.emit('dve', lambda e: e.tensor_tensor(out=Lv, in0=gcur, in1=gprev, op=ALU.subtract), [Gp_t], [L_t])
                        else:
                            gend = Gp[:, 127:127 + TOK].rearrange("p (j n) -> p j n", n=64)[:, :, 0:1].broadcast_to([128, 16, 64])
                            gsh = Gp[:, 63:63 + TOK].rearrange("p (j n) -> p j n", n=64)
                            k.emit('dve', lambda e: e.tensor_tensor(out=Lv, in0=gend, in1=gsh, op=ALU.subtract), [Gp_t], [L_t])
                        pos = 63 if di == 0 else 0
                        mid = 31 if di == 0 else 32
                        k.emit('pool', lambda e, pos=pos: e.tensor_copy(out=G3[:, 0, :], in_=Lv[:, :, pos]), [L_t], [G3_t])
                        k.emit('pool', lambda e, mid=mid: e.tensor_copy(out=G3[:, 1, :], in_=Lv[:, :, mid]), [L_t], [G3_t])
                        k.emit('dve', lambda e: e.tensor_tensor(out=G3[:, 2, :], in0=G3[:, 0, :], in1=G3[:, 1, :], op=ALU.subtract), [G3_t], [G3_t])
                        k.emit('act', lambda e: e.activation(out=DK[:], in_=G3[:], func=AF.Exp), [G3_t], [DK_t])
                        k.emit('dve', lambda e: e.tensor_tensor(out=Lv, in0=Lv, in1=G3[:, 1, :].unsqueeze(2).broadcast_to([128, 16, 64]), op=ALU.subtract),
                               [L_t, G3_t], [L_t])
                        k.emit('act', lambda e: e.activation(out=E1[:], in_=L[:], func=AF.Exp), [L_t], [E1_t])
                        k.emit('act', lambda e: e.activation(out=E2[:], in_=L[:], func=AF.Exp, scale=-1.0), [L_t], [E2_t])
                        k.emit('dve', lambda e: e.tensor_tensor(out=E1[:], in0=E1[:], in1=qT[:], op=ALU.mult), [E1_t, qT_t], [E1_t])
                        k.emit('pool', lambda e, F_=F_: e.tensor_tensor(out=E2[:], in0=E2[:], in1=F_[:], op=ALU.mult), [E2_t, F_t], [E2_t])
                        order = list(range(16)) if di == 0 else list(range(15, -1, -1))
                        if half == 1:
                            k.dma(Sb[0][:], d['st_hgrn'][di, hh, :, :], writes=[Sb_t[0]])
                        si = 0
                        psA, psA_t = self.ps[0], self.ps_t[0]
                        psT, psT_t = self.ps[1], self.ps_t[1]
                        psO, psO_t = self.ps[2], self.ps_t[2]

                        def step1(gi, di=di, order=order):
                            chs = order[gi * 4:(gi + 1) * 4]
                            lo = min(chs)
                            for ch in chs:
                                cs = slice(ch * 64, (ch + 1) * 64)
                                q_ = ch - lo
                                k.emit('pe', lambda e, cs=cs, q_=q_: e.matmul(psA[0:64, q_ * 64:(q_ + 1) * 64], E2[:, cs], E1[:, cs], start=True, stop=True),
                                       [E1_t, E2_t], [psA_t])
                                k.emit('pe', lambda e, cs=cs, q_=q_: e.transpose(psT[0:64, q_ * 128:(q_ + 1) * 128], E2[:, cs], IDENT),
                                       [E2_t, self.CT_t], [psT_t])
                            k.emit('dve', lambda e, di=di: e.tensor_tensor(out=Am[:].rearrange("p a n -> p (a n)"), in0=psA[0:64, 0:256], in1=masks[di], op=ALU.mult),
                                   [psA_t, self.CT_t], [Am_t])
                            k.emit('act', lambda e: e.activation(out=Ktok[:].rearrange("p a n -> p (a n)"), in_=psT[0:64, :], func=AF.Copy), [psT_t], [Ktok_t])

                        step1(0)
                        for gi in range(4):
                            chs = order[gi * 4:(gi + 1) * 4]
                            lo = min(chs)
                            bS = 3 + (gi % 2)
                            psS, psS_t = self.ps[bS], self.ps_t[bS]
                            info = []
                            for ch in chs:
                                loc = ch % cps
                                first = (loc == 0) if di == 0 else (loc == cps - 1)
                                last = (loc == cps - 1) if di == 0 else (loc == 0)
                                info.append((ch, ch - lo, first and half == 0, last, ch // cps))
                            for (ch, q_, zi, last, seq) in info:
                                k.emit('pe', lambda e, q_=q_, ch=ch: e.matmul(psS[:, q_ * 128:(q_ + 1) * 128], Ktok[:, q_, :], V64[:, ch, :], start=True, stop=True),
                                       [Ktok_t, V64_t[ch]], [psS_t])
                            for idx, (ch, q_, zi, last, seq) in enumerate(info):
                                k.emit('pe', lambda e, q_=q_, ch=ch, idx=idx, zi=zi: e.matmul(
                                    psO[:, q_ * 64:(q_ + 1) * 64], V64[:, ch, :], Am[:, q_, :], start=(idx == 0), stop=zi), [V64_t[ch], Am_t], [psO_t])
                            sps = {}
                            for (ch, q_, zi, last, seq) in info:
                                S_prev, S_prev_t = Sb[si], Sb_t[si]
                                if not zi:
                                    sp_, sp_t = Sp[q_], Sp_t[q_]
                                    k.emit('act', lambda e, sp_=sp_, S_prev=S_prev, ch=ch: e.activation(
                                        out=sp_[:], in_=S_prev[:], func=AF.Identity, scale=DK[:, 1, ch:ch + 1]), [S_prev_t, DK_t], [sp_t])
                                    sps[ch] = (sp_, sp_t)
                                if last and half == 0:
                                    dstS, dstS_t = sout[:, seq, di, :], sout_t
                                else:
                                    si = 1 - si
                                    dstS, dstS_t = Sb[si][:], Sb_t[si]
                                reg = psS[:, q_ * 128:(q_ + 1) * 128]
                                if zi:
                                    k.emit('act', lambda e, reg=reg, ch=ch, dstS=dstS: e.activation(
                                        out=dstS, in_=reg, func=AF.Identity, scale=DK[:, 2, ch:ch + 1]), [psS_t, DK_t], [dstS_t])
                                else:
                                    k.emit('act', lambda e, reg=reg, ch=ch: e.activation(
                                        out=tS[:], in_=reg, func=AF.Identity, scale=DK[:, 2, ch:ch + 1]), [psS_t, DK_t], [tS_t])
                                    k.emit('dve', lambda e, dstS=dstS, S_prev=S_prev, ch=ch: e.scalar_tensor_tensor(
                                        out=dstS, in0=S_prev[:], scalar=DK[:, 0, ch:ch + 1], in1=tS[:], op0=ALU.mult, op1=ALU.add),
                                        [S_prev_t, DK_t, tS_t], [dstS_t])
                            if gi + 1 < 4:
                                step1(gi + 1)
                            for (ch, q_, zi, last, seq) in info:
                                if zi:
                                    continue
                                sp_, sp_t = sps[ch]
                                cs = slice(ch * 64, (ch + 1) * 64)
                                k.emit('pe', lambda e, cs=cs, q_=q_, sp_=sp_: e.matmul(
                                    psO[:, q_ * 64:(q_ + 1) * 64], sp_[:], E1[:, cs], start=False, stop=True), [sp_t, E1_t], [psO_t])
                            osl = slice(lo * 64, lo * 64 + 256)
                            if di == 0:
                                k.emit('dve', lambda e, osl=osl: e.tensor_copy(out=oT[:, osl], in_=psO[:, 0:256]), [psO_t], [oT_t])
                            else:
                                k.emit('dve', lambda e, osl=osl: e.tensor_tensor(out=oT[:, osl], in0=oT[:, osl], in1=psO[:, 0:256], op=ALU.add), [psO_t, oT_t], [oT_t])
                    if half == 0:
                        k.dma(d['hgout'][:, :, hh, :, :].rearrange("s d k e -> k s d e"), sout[:], reads=[sout_t])
                    for tt in range(2):
                        sl = slice(tt * 512, (tt + 1) * 512)
                        ps, pst = self.ps[6 + tt], self.ps_t[6 + tt]
                        for c in range(NCH):
                            k.emit('pe', lambda e, ps=ps, c=c, sl=sl: e.matmul(ps[:], wg[:, c, :], self.hT[:, c, sl], start=(c == 0), stop=(c == NCH - 1)),
                                   [wt3, self.hT_t[c][tt]], [pst])
                        k.emit('act', lambda e, ps=ps, sl=sl: e.activation(out=gs[:, sl], in_=ps[:], func=AF.Silu), [pst], [gs_t])
                    self.head_norm(ntm, oT[:], oT_t, mix[:, 0, :], mix_t[0], self.HV[:, 2, 0:1], 128.0 * 1e-6, 5, extra_mul=gs[:], extra_t=gs_t)
                    self.out_proj(d['hgrn_w_out'][j, hh * 128:(hh + 1) * 128, :], mix, mix_t, 1, half, [6, 7], pf=opf)
            k.barrier()

    def gdn(self, layer, half):
        k, d, nc = self.k, self.d, self.nc
        w_in = d['gdn_w_in']
        nseq = 4 if half == 0 else 1
        cps = 16 // nseq
        seqlen = TOK // nseq
        CT = self.CT

        def cc(name, rows=64, w=None):
            o_, w_ = CC[name]
            return CT[0:rows, o_:o_ + (w or w_)]

        ONES_ROW = cc('ones', 128, 1).broadcast_to([128, TOK])
        IDENT = cc('ident', 128)
        ID64 = cc('ident', 64, 64)
        TRIF = cc('mask_f', 64, 64)
        TRIB = cc('mask_b', 64, 64)
        ONES64 = cc('ones', 64, 64)
        NEGU = [cc('negu_f'), cc('negu_b')]
        NEGL = [cc('negl_f'), cc('negl_b')]
        SNEG = [cc('sneg_f'), cc('sneg_b')]
        IDREP = cc('idrep')
        osel, _ = CC['sel']
        onsel, _ = CC['nsel']

        def SEL(kk, m):
            return CT[0:4, osel + kk * 128: osel + kk * 128 + m]

        def NSEL(kk):
            return CT[0:4, onsel + kk * 64: onsel + (kk + 1) * 64]

        oa, _ = PC['gdn_alog_col']
        odt, _ = PC['gdn_dt_col']
        oar, _ = PC['gdn_alog_row']
        odr, _ = PC['gdn_dt_row']
        ocv, _ = PC['gdn_conv']
        ogn, _ = PC['gdn_norm']
        scr_t = self.ptrk('gscr')
        with ExitStack() as es0:
            GC = self.tmp(es0, 'gGC', [64, 16, 16]); BE = self.tmp(es0, 'gBE', [64, 16, 16])
            NBE = self.tmp(es0, 'gNBE', [64, 16, 16]); C1 = self.tmp(es0, 'gC1', [64, 16, 16])
            WW = self.tmp(es0, 'gWW', [64, 16, 16])
            TB_t = Trk('gTB')
            GV = self.tmp(es0, 'gGV', [128, 20])
            GV_t = Trk('gGV')
            k.emit('act', lambda e: e.activation(out=GV[:, 0:1], in_=self.PT[:, oa:oa + 1], func=AF.Exp), [self.PT_t], [GV_t])
            k.emit('dve', lambda e: e.tensor_scalar(out=GV[:, 0:1], in0=GV[:, 0:1], scalar1=-1.0, scalar2=None, op0=ALU.mult), [GV_t], [GV_t])
            k.emit('act', lambda e: e.activation(out=GV[:, 4:20], in_=self.PT[:, oar:oar + 16], func=AF.Exp), [self.PT_t], [GV_t])
            k.emit('dve', lambda e: e.tensor_scalar(out=GV[:, 4:20], in0=GV[:, 4:20], scalar1=-1.0, scalar2=None, op0=ALU.mult), [GV_t], [GV_t])
            k.emit('dve', lambda e: e.tensor_scalar(out=GV[:, 1:2], in0=self.PT[:, ogn:ogn + 1], scalar1=float(math.sqrt(128.0)), scalar2=None, op0=ALU.mult),
                   [self.PT_t, GV_t], [GV_t])
            (wab,), wab_t = self.wpiece([w_in[0, :, 4 * D:4 * D + 32]], rounded=False)
            with ExitStack() as es:
                LA = self.tmp(es, 'gLA', [16, TOK]); LA_t = Trk('gLA')
                Gp = self.tmp(es, 'gGp', [16, 64 + TOK + 64]); Gp_t = Trk('gGp')
                GF = self.tmp(es, 'gGF', [16, TOK]); GF_t = self.ptrk('gGF')
                GB = self.tmp(es, 'gGB', [16, TOK]); GB_t = self.ptrk('gGB')
                BT = self.tmp(es, 'gBT', [16, TOK]); BT_t = self.ptrk('gBT')
                LAt = self.tmp(es, 'gLAt', [64, 16, 16]); LAt_t = Trk('gLAt')
                k.emit('dve', lambda e: e.memset(Gp[:], 0.0), [], [Gp_t])
                hTf = self.hT[:].bitcast(F32)
                for part in range(2):
                    for tt in range(2):
                        sl = slice(tt * 512, (tt + 1) * 512)
                        ps, pst = self.ps[6 + tt], self.ps_t[6 + tt]
                        for c in range(NCH):
                            k.emit('pe', lambda e, ps=ps, c=c, sl=sl, part=part: e.matmul(
                                ps[0:16, :], wab[:, c, part * 16:(part + 1) * 16], hTf[:, c, sl], start=(c == 0), stop=(c == NCH - 1)),
                                [wab_t, self.hT_t[c][tt]], [pst])
                        if part == 0:
                            k.emit('act', lambda e, ps=ps, sl=sl: e.activation(out=LA[:, sl], in_=ps[0:16, :], func=AF.Exp, bias=self.PT[0:16, odt:odt + 1]),
                                   [pst, self.PT_t], [LA_t])
                        else:
                            k.emit('act', lambda e, ps=ps, sl=sl: e.activation(out=BT[:, sl], in_=ps[0:16, :], func=AF.Sigmoid), [pst], [BT_t])
                k.emit('act', lambda e: e.activation(out=LA[:], in_=LA[:], func=AF.Ln, bias=1.0), [LA_t], [LA_t])
                k.emit('dve', lambda e: e.tensor_scalar(out=LA[:], in0=LA[:], scalar1=GV[0:16, 0:1], scalar2=None, op0=ALU.mult), [LA_t, GV_t], [LA_t])
                k.emit('dve', lambda e: e.tensor_tensor_scan(out=Gp[:, 64:64 + TOK], data0=ONES_ROW[0:16, :], data1=LA[:], initial=0.0,
                                                             op0=ALU.mult, op1=ALU.add), [LA_t, self.CT_t], [Gp_t])
                gprev = Gp[:, 63:63 + TOK].rearrange("p (j n) -> p j n", n=64)[:, :, 0:1].broadcast_to([16, 16, 64])
                gcur = Gp[:, 64:64 + TOK].rearrange("p (j n) -> p j n", n=64)
                k.emit('dve', lambda e: e.tensor_tensor(out=GF[:].rearrange("p (j n) -> p j n", n=64), in0=gcur, in1=gprev, op=ALU.subtract), [Gp_t], [GF_t])
                gend = Gp[:, 127:127 + TOK].rearrange("p (j n) -> p j n", n=64)[:, :, 0:1].broadcast_to([16, 16, 64])
                gsh = Gp[:, 63:63 + TOK].rearrange("p (j n) -> p j n", n=64)
                k.emit('dve', lambda e: e.tensor_tensor(out=GB[:].rearrange("p (j n) -> p j n", n=64), in0=gend, in1=gsh, op=ALU.subtract), [Gp_t], [GB_t])
                k.dma(d['gscr'][0:16, :], GF[:], reads=[GF_t], writes=[scr_t])
                k.dma(d['gscr'][16:32, :], GB[:], reads=[GB_t], writes=[scr_t])
                k.dma(d['gscr'][32:48, :], BT[:], reads=[BT_t], writes=[scr_t])
                ps, pst = self.ps[5], self.ps_t[5]
                for ch in range(16):
                    for c in range(NCH):
                        k.emit('pe', lambda e, c=c, ch=ch: e.matmul(
                            ps[0:64, ch * 32:(ch + 1) * 32], hTf[:, c, ch * 64:(ch + 1) * 64], wab[:, c, :], start=(c == 0), stop=(c == NCH - 1)),
                            [wab_t, self.hT_t[c][ch // 8]], [pst])
                pv = ps[0:64, :].rearrange("p (c n) -> p c n", n=32)
                k.emit('dve', lambda e: e.tensor_tensor(out=LAt[:], in0=pv[:, :, 0:16],
                                                        in1=self.PT[0:64, odr:odr + 16].unsqueeze(1).broadcast_to([64, 16, 16]), op=ALU.add),
                       [pst, self.PT_t], [LAt_t])
                k.emit('act', lambda e: e.activation(out=BE[:], in_=pv[:, :, 16:32], func=AF.Sigmoid), [pst], [TB_t])
                k.emit('act', lambda e: e.activation(out=LAt[:], in_=LAt[:], func=AF.Exp), [LAt_t], [LAt_t])
                k.emit('act', lambda e: e.activation(out=LAt[:], in_=LAt[:], func=AF.Ln, bias=1.0), [LAt_t], [LAt_t])
                k.emit('dve', lambda e: e.tensor_tensor(out=LAt[:], in0=LAt[:], in1=GV[0:64, 4:20].unsqueeze(1).broadcast_to([64, 16, 16]), op=ALU.mult),
                       [LAt_t, GV_t], [LAt_t])
                LAf = LAt[:].rearrange("p c n -> p (c n)")
                pF, pF_t = self.ps[0], self.ps_t[0]
                pB, pB_t = self.ps[1], self.ps_t[1]
                pT, pT_t = self.ps[2], self.ps_t[2]
                k.emit('pe', lambda e: e.matmul(pF[0:64, 0:256], TRIF, LAf, start=True, stop=True), [LAt_t, self.CT_t], [pF_t])
                k.emit('pe', lambda e: e.matmul(pB[0:64, 0:256], TRIB, LAf, start=True, stop=True), [LAt_t, self.CT_t], [pB_t])
                k.emit('pe', lambda e: e.matmul(pT[0:64, 0:256], ONES64, LAf, start=True, stop=True), [LAt_t, self.CT_t], [pT_t])
                pFv = pF[0:64, 0:256].rearrange("p (c n) -> p c n", n=16)
                pBv = pB[0:64, 0:256].rearrange("p (c n) -> p c n", n=16)
                pTv = pT[0:64, 0:256].rearrange("p (c n) -> p c n", n=16)
                k.emit('dve', lambda e: e.tensor_copy(out=GC[:, :, 0:8], in_=pFv[:, :, 0:8]), [pF_t], [TB_t])
                k.emit('dve', lambda e: e.tensor_copy(out=GC[:, :, 8:16], in_=pBv[:, :, 8:16]), [pB_t], [TB_t])
                k.emit('dve', lambda e: e.tensor_tensor(out=WW[:], in0=pTv, in1=GC[:], op=ALU.subtract), [pT_t, TB_t], [TB_t])
                k.emit('act', lambda e: e.activation(out=WW[:], in_=WW[:], func=AF.Exp), [TB_t], [TB_t])
                k.emit('act', lambda e: e.activation(out=C1[:], in_=GC[:], func=AF.Exp), [TB_t], [TB_t])
                k.emit('dve', lambda e: e.scalar_tensor_tensor(out=C1[:], in0=C1[:], scalar=-1.0, in1=BE[:], op0=ALU.mult, op1=ALU.mult), [TB_t], [TB_t])
                k.emit('dve', lambda e: e.tensor_scalar(out=NBE[:], in0=BE[:], scalar1=-1.0, scalar2=None, op0=ALU.mult), [TB_t], [TB_t])
                k.barrier()
            stop = self.cfg.get('gdn_stop', 9)
            if stop <= 1:
                return
            with ExitStack() as es:
                qn = self.tmp(es, 'gq', [128, TOK]); qn_t = Trk('gq')
                kn = self.tmp(es, 'gk', [128, TOK]); kn_t = Trk('gk')
                vT = self.tmp(es, 'gv', [128, TOK]); vT_t = Trk('gv')
                oT = self.tmp(es, 'go', [128, TOK]); oT_t = Trk('go')
                Qt = self.tmp(es, 'gQt', [128, TOK]); Qt_t = Trk('gQt')
                mix = self.tmp(es, 'gmix', [128, 1, TOK], F32R); mix_t = trks('gmix', 1)
                ntm = self.norm_tmps(es)
                HR4 = self.tmp(es, 'gHR', [4, TOK]); HR4_t = self.ptrk('gHR')
                bt = [self.tmp(es, f'gb{i}', [64, 4, 64]) for i in range(10)]
                bt_t = trks('gb', 10)
                ktok = self.tmp(es, 'gkt', [64, 4, 128]); ktok_t = Trk('gkt')
                vtok = self.tmp(es, 'gvt', [64, 4, 128]); vtok_t = Trk('gvt')
                sm = [self.tmp(es, f'gs{i}', [64, 128]) for i in range(4)]
                sm_t = trks('gs', 4)
                Sb = [self.tmp(es, f'gS{i}', [128, 128]) for i in range(2)]
                Sb_t = trks('gS', 2)
                DKg = self.tmp(es, 'gDK', [128, 16]); DKg_t = Trk('gDK')
                sout_t = self.ptrk('gso')
                if half == 0:
                    sout = self.tmp(es, 'gso', [128, 4, 2, 128])
                pb = 0
                for hh in range(8):
                    (wq, wk), wt1 = self.wpiece([w_in[0, :, hh * 128:(hh + 1) * 128], w_in[0, :, D + hh * 128:D + (hh + 1) * 128]])
                    (wv,), wt2 = self.wpiece([w_in[0, :, 2 * D + hh * 128:2 * D + (hh + 1) * 128]])
                    k.dma_group([(HR4[0:1, :], d['gscr'][hh:hh + 1, :]), (HR4[1:2, :], d['gscr'][24 + hh:25 + hh, :]),
                                 (HR4[2:3, :], d['gscr'][32 + hh:33 + hh, :]), (HR4[3:4, :], d['gscr'][40 + hh:41 + hh, :])], [HR4_t])
                    HR4_t.rs[scr_t.w[0]] = 0
                    for ti, (w, wt, dst, dst_t) in enumerate(((wq, wt1, qn, qn_t), (wk, wt1, kn, kn_t), (wv, wt2, vT, vT_t))):
                        fch = ti * 8 + hh
                        w0 = self.PT[:, ocv + 0 * 24 + fch: ocv + 0 * 24 + fch + 1]
                        w1 = self.PT[:, ocv + 1 * 24 + fch: ocv + 1 * 24 + fch + 1]
                        w2 = self.PT[:, ocv + 2 * 24 + fch: ocv + 2 * 24 + fch + 1]
                        pss = []
                        for tt in range(2):
                            sl = slice(tt * 512, (tt + 1) * 512)
                            b = 6 + tt
                            ps, pst = self.ps[b], self.ps_t[b]
                            pss.append((ps, pst))
                            for c in range(NCH):
                                k.emit('pe', lambda e, ps=ps, w=w, c=c, sl=sl: e.matmul(
                                    ps[:], w[:, c, :], self.hT[:, c, sl], start=(c == 0), stop=(c == NCH - 1)), [wt, self.hT_t[c][tt]], [pst])
                            k.emit('act', lambda e, ps=ps, sl=sl, dst=dst, w1=w1: e.activation(out=dst[:, sl], in_=ps[:], func=AF.Copy, scale=w1),
                                   [pst, self.PT_t], [dst_t])
                        for tt in range(2):
                            ps, pst = pss[tt]
                            sl_ = min(seqlen, 512)
                            ns = 512 // sl_
                            pv = ps[:].rearrange("p (s n) -> p s n", s=ns)
                            av = dst[:, tt * 512:(tt + 1) * 512].rearrange("p (s n) -> p s n", s=ns)
                            k.emit('dve', lambda e, pv=pv, av=av, w0=w0, sl_=sl_: e.scalar_tensor_tensor(
                                out=av[:, :, 1:sl_], in0=pv[:, :, 0:sl_ - 1], scalar=w0, in1=av[:, :, 1:sl_], op0=ALU.mult, op1=ALU.add),
                                [pst, self.PT_t, dst_t], [dst_t])
                            k.emit('dve', lambda e, pv=pv, av=av, w2=w2, sl_=sl_: e.scalar_tensor_tensor(
                                out=av[:, :, 0:sl_ - 1], in0=pv[:, :, 1:sl_], scalar=w2, in1=av[:, :, 0:sl_ - 1], op0=ALU.mult, op1=ALU.add),
                                [pst, self.PT_t, dst_t], [dst_t])
                        if seqlen > 512:
                            p0, p0t = pss[0]
                            p1, p1t = pss[1]
                            k.emit('dve', lambda e, p0=p0, dst=dst, w0=w0: e.scalar_tensor_tensor(
                                out=dst[:, 512:513], in0=p0[:, 511:512], scalar=w0, in1=dst[:, 512:513], op0=ALU.mult, op1=ALU.add),
                                [p0t, self.PT_t, dst_t], [dst_t])
                            k.emit('dve', lambda e, p1=p1, dst=dst, w2=w2: e.scalar_tensor_tensor(
                                out=dst[:, 511:512], in0=p1[:, 0:1], scalar=w2, in1=dst[:, 511:512], op0=ALU.mult, op1=ALU.add),
                                [p1t, self.PT_t, dst_t], [dst_t])
                        k.emit('act', lambda e, dst=dst: e.activation(out=dst[:], in_=dst[:], func=AF.Silu), [dst_t], [dst_t])
                        if ti < 2:
                            sq, sq_t, rs, rs_t = ntm
                            for tt in range(2):
                                sl = slice(tt * 512, (tt + 1) * 512)
                                ps, pst = self.ps[5], self.ps_t[5]
                                k.emit('act', lambda e, dst=dst, sl=sl: e.activation(out=sq[:], in_=dst[:, sl], func=AF.Square), [dst_t], [sq_t])
                                k.emit('pe', lambda e, ps=ps: e.matmul(ps[:], self.onesR[:], sq[:], start=True, stop=True), [self.onesR_t, sq_t], [pst])
                                k.emit('act', lambda e, ps=ps: e.activation(out=rs[:], in_=ps[:], func=AF.Sqrt, bias=1e-6, scale=1.0), [pst], [rs_t])
                                k.emit('dve', lambda e: e.reciprocal(out=rs[:], in_=rs[:]), [rs_t], [rs_t])
                                sc_ = float(128.0 ** -0.5) if ti == 0 else 1.0
                                k.emit('dve', lambda e, dst=dst, sl=sl, sc_=sc_: e.scalar_tensor_tensor(
                                    out=dst[:, sl], in0=dst[:, sl], scalar=sc_, in1=rs[:], op0=ALU.mult, op1=ALU.mult), [dst_t, rs_t], [dst_t])
                    if stop <= 2:
                        continue
                    for di in range(2):
                        col = di * 8 + hh
                        for tt in range(2):
                            sl = slice(tt * 512, (tt + 1) * 512)
                            ps, pst = self.ps[6 + tt], self.ps_t[6 + tt]
                            k.emit('pe', lambda e, ps=ps, sl=sl, di=di: e.matmul(ps[:], SEL(di, 128), HR4[0:4, sl], start=True, stop=True),
                                   [HR4_t, self.CT_t], [pst])
                            k.emit('act', lambda e, ps=ps, sl=sl: e.activation(out=Qt[:, sl], in_=ps[:], func=AF.Exp), [pst], [Qt_t])
                        pos = 63 if di == 0 else 0
                        k.emit('pool', lambda e, pos=pos: e.tensor_copy(out=DKg[:], in_=Qt[:].rearrange("p (j n) -> p j n", n=64)[:, :, pos]), [Qt_t], [DKg_t])
                        k.emit('dve', lambda e: e.tensor_tensor(out=Qt[:], in0=Qt[:], in1=qn[:], op=ALU.mult), [Qt_t, qn_t], [Qt_t])
                        border = list(range(4)) if di == 0 else list(range(3, -1, -1))
                        if half == 1:
                            k.dma(Sb[0][:], d['st_gdn'][di, hh, :, :], writes=[Sb_t[0]])
                        si = 0
                        for bi in border:
                            chs = [bi * 4 + q for q in range(4)]
                            if di == 1:
                                chs = chs[::-1]
                            T0 = bi * 256
                            bsl = slice(T0, T0 + 256)
                            gcol = GC[:, bi * 4:bi * 4 + 4, col:col + 1].broadcast_to([64, 4, 64])
                            nbcol = NBE[:, bi * 4:bi * 4 + 4, col:col + 1].broadcast_to([64, 4, 64])
                            b0, b0t = self.ps[0], self.ps_t[0]
                            b1, b1t = self.ps[1], self.ps_t[1]
                            b2, b2t = self.ps[2], self.ps_t[2]
                            b3, b3t = self.ps[3], self.ps_t[3]
                            R = lambda ps: ps[0:64, 0:256]
                            R3 = lambda ps: ps[0:64, 0:256].rearrange("p (a n) -> p a n", a=4)
                            F2 = lambda t: t[:].rearrange("p a n -> p (a n)")
                            deps_c = [HR4_t, self.CT_t]
                            k.emit('pe', lambda e, di=di: e.matmul(R(b0), SEL(di, 64), HR4[0:4, bsl], start=True, stop=True), deps_c, [b0t])
                            k.emit('pe', lambda e, di=di: e.matmul(R(b2), SEL(2 + di, 64), HR4[0:4, bsl], start=True, stop=True), deps_c, [b2t])
                            DT, DT_t = bt[0], bt_t[0]
                            Dl, Dl_t = bt[1], bt_t[1]
                            DBT, DBT_t = bt[2], bt_t[2]
                            k.emit('dve', lambda e: e.tensor_tensor(out=DT[:], in0=R3(b0), in1=gcol, op=ALU.subtract), [b0t, TB_t], [DT_t])
                            k.emit('pool', lambda e, di=di: e.tensor_tensor(out=F2(Dl), in0=NEGL[di], in1=F2(DT), op=ALU.subtract), [DT_t, self.CT_t], [Dl_t])
                            k.emit('pool', lambda e, di=di: e.tensor_tensor(out=F2(DT), in0=F2(DT), in1=NEGU[di], op=ALU.add), [DT_t, self.CT_t], [DT_t])
                            k.emit('act', lambda e: e.activation(out=DT[:], in_=DT[:], func=AF.Exp), [DT_t], [DT_t])
                            k.emit('act', lambda e: e.activation(out=Dl[:], in_=Dl[:], func=AF.Exp), [Dl_t], [Dl_t])
                            k.emit('dve', lambda e, di=di: e.tensor_tensor(out=F2(DBT), in0=R(b2), in1=SNEG[di], op=ALU.mult), [b2t, self.CT_t], [DBT_t])
                            k.emit('pool', lambda e: e.tensor_tensor(out=DBT[:], in0=DBT[:], in1=DT[:], op=ALU.mult), [DBT_t, DT_t], [DBT_t])
                            k.emit('pool', lambda e: e.tensor_tensor(out=Dl[:], in0=Dl[:], in1=nbcol, op=ALU.mult), [Dl_t, TB_t], [Dl_t])
                            for q in range(4):
                                cs = slice(T0 + q * 64, T0 + (q + 1) * 64)
                                k.emit('pe', lambda e, q=q, cs=cs: e.matmul(b0[0:64, q * 64:(q + 1) * 64], kn[:, cs], kn[:, cs], start=True, stop=True), [kn_t], [b0t])
                                k.emit('pe', lambda e, q=q, cs=cs: e.matmul(b1[0:64, q * 64:(q + 1) * 64], kn[:, cs], qn[:, cs], start=True, stop=True), [kn_t, qn_t], [b1t])
                                k.emit('pe', lambda e, q=q, cs=cs: e.transpose(b2[0:64, q * 128:(q + 1) * 128], kn[:, cs], IDENT), [kn_t, self.CT_t], [b2t])
                                k.emit('pe', lambda e, q=q, cs=cs: e.transpose(b3[0:64, q * 128:(q + 1) * 128], vT[:, cs], IDENT), [vT_t, self.CT_t], [b3t])
                            NT, NT_t = bt[3], bt_t[3]
                            Nm, Nm_t = bt[4], bt_t[4]
                            QKT, QKT_t = bt[5], bt_t[5]
                            XT, XT_t = bt[6], bt_t[6]
                            k.emit('dve', lambda e: e.tensor_tensor(out=F2(NT), in0=R(b0), in1=F2(DBT), op=ALU.mult), [b0t, DBT_t], [NT_t])
                            k.emit('dve', lambda e: e.tensor_tensor(out=F2(Nm), in0=R(b0), in1=F2(Dl), op=ALU.mult), [b0t, Dl_t], [Nm_t])
                            k.emit('dve', lambda e: e.tensor_tensor(out=F2(QKT), in0=R(b1), in1=F2(DT), op=ALU.mult), [b1t, DT_t], [QKT_t])
                            k.emit('pool', lambda e: e.tensor_tensor(out=F2(XT), in0=F2(NT), in1=IDREP, op=ALU.add), [NT_t, self.CT_t], [XT_t])
                            k.emit('act', lambda e: e.activation(out=F2(ktok), in_=b2[0:64, :], func=AF.Copy), [b2t], [ktok_t])
                            k.emit('act', lambda e: e.activation(out=F2(vtok), in_=b3[0:64, :], func=AF.Copy), [b3t], [vtok_t])
                            P, P_t, PT_, PT_t = Nm, Nm_t, NT, NT_t
                            pp = [(bt[7], bt_t[7], bt[8], bt_t[8]), (bt[9], bt_t[9], bt[1], bt_t[1])]
                            XTs = [(bt[6], bt_t[6]), (bt[0], bt_t[0])]
                            xi = 0
                            def x_update(nP, nP_t, xi):
                                cX, cX_t = XTs[xi]
                                nX, nX_t = XTs[1 - xi]
                                for q in range(4):
                                    k.emit('pe', lambda e, q=q, nP=nP, cX=cX: e.matmul(b2[0:64, q * 64:(q + 1) * 64], nP[:, q, :], cX[:, q, :], start=True, stop=True),
                                           [nP_t, cX_t], [b2t])
                                k.emit('dve', lambda e, nX=nX, cX=cX: e.tensor_tensor(out=F2(nX), in0=R(b2), in1=F2(cX), op=ALU.add), [b2t, cX_t], [nX_t])
                                return 1 - xi

                            pendX = None
                            for m in range(1, 6):
                                nP, nP_t, nPT, nPT_t = pp[(m - 1) % 2] if m > 1 else pp[0]
                                if m >= 3:
                                    nP, nP_t, nPT, nPT_t = pp[(m - 1) % 2]
                                if m == 2:
                                    nP, nP_t, nPT, nPT_t = pp[1]
                                for q in range(4):
                                    k.emit('pe', lambda e, q=q, P=P, PT_=PT_: e.matmul(b0[0:64, q * 64:(q + 1) * 64], PT_[:, q, :], P[:, q, :], start=True, stop=True),
                                           [P_t, PT_t], [b0t])
                                    if m < 5:
                                        k.emit('pe', lambda e, q=q, P=P, PT_=PT_: e.matmul(b1[0:64, q * 64:(q + 1) * 64], P[:, q, :], PT_[:, q, :], start=True, stop=True),
                                               [P_t, PT_t], [b1t])
                                k.emit('act', lambda e, nP=nP: e.activation(out=F2(nP), in_=R(b0), func=AF.Copy), [b0t], [nP_t])
                                if m < 5:
                                    k.emit('dve', lambda e, nPT=nPT: e.tensor_copy(out=F2(nPT), in_=R(b1)), [b1t], [nPT_t])
                                if pendX is not None:
                                    xi = x_update(pendX[0], pendX[1], xi)
                                pendX = (nP, nP_t)
                                P, P_t, PT_, PT_t = nP, nP_t, nPT, nPT_t
                            xi = x_update(pendX[0], pendX[1], xi)
                            XTf, XTf_t = XTs[xi]
                            if stop <= 3:
                                continue
                            psO, psO_t = self.ps[6], self.ps_t[6]
                            for ch in chs:
                                q = ch - bi * 4
                                cs = slice(ch * 64, (ch + 1) * 64)
                                loc = ch % cps
                                first = (loc == 0) if di == 0 else (loc == cps - 1)
                                last = (loc == cps - 1) if di == 0 else (loc == 0)
                                seq = ch // cps
                                zero_init = first and half == 0
                                S_prev, S_prev_t = Sb[si], Sb_t[si]
                                tmpv, tmpv_t = sm[0], sm_t[0]
                                r_, r_t = sm[1], sm_t[1]
                                vn, vn_t = sm[2], sm_t[2]
                                vs, vs_t = sm[3], sm_t[3]
                                k.emit('act', lambda e, q=q, ch=ch: e.activation(out=tmpv[:], in_=vtok[:, q, :], func=AF.Copy, scale=BE[:, ch, col:col + 1]),
                                       [vtok_t, TB_t], [tmpv_t])
                                if zero_init:
                                    rr, rr_t = tmpv, tmpv_t
                                else:
                                    p4, p4t = self.ps[4], self.ps_t[4]
                                    k.emit('pe', lambda e, cs=cs, S_prev=S_prev: e.matmul(p4[0:64, 0:128], kn[:, cs], S_prev[:], start=True, stop=True),
                                           [kn_t, S_prev_t], [p4t])
                                    k.emit('dve', lambda e, ch=ch: e.scalar_tensor_tensor(out=r_[:], in0=p4[0:64, 0:128], scalar=C1[:, ch, col:col + 1], in1=tmpv[:],
                                                                                         op0=ALU.mult, op1=ALU.add), [p4t, TB_t, tmpv_t], [r_t])
                                    rr, rr_t = r_, r_t
                                lvl = self.cfg.get('seq_lvl', 9)
                                if lvl <= 1:
                                    continue
                                p5, p5t = self.ps[5], self.ps_t[5]
                                k.emit('pe', lambda e, q=q, rr=rr: e.matmul(p5[0:64, 0:128], XTf[:, q, :], rr[:], start=True, stop=True), [XTf_t, rr_t], [p5t])
                                if lvl <= 1.5:
                                    continue
                                k.emit('act', lambda e: e.activation(out=vn[:], in_=p5[0:64, 0:128], func=AF.Copy), [p5t], [vn_t])
                                if lvl <= 1.7:
                                    continue
                                k.emit('dve', lambda e, ch=ch: e.tensor_scalar(out=vs[:], in0=p5[0:64, 0:128], scalar1=WW[:, ch, col:col + 1], scalar2=None, op0=ALU.mult),
                                       [p5t, TB_t], [vs_t])
                                if lvl <= 3:
                                    continue
                                p7, p7t = self.ps[7], self.ps_t[7]
                                k.emit('pe', lambda e, q=q: e.matmul(p7[:, 0:128], ktok[:, q, :], vs[:], start=True, stop=True), [ktok_t, vs_t], [p7t])
                                if last and half == 0:
                                    dstS, dstS_t = sout[:, seq, di, :], sout_t
                                else:
                                    si = 1 - si
                                    dstS, dstS_t = Sb[si][:], Sb_t[si]
                                if zero_init:
                                    k.emit('dve', lambda e, dstS=dstS: e.tensor_copy(out=dstS, in_=p7[:, 0:128]), [p7t], [dstS_t])
                                else:
                                    k.emit('dve', lambda e, dstS=dstS, S_prev=S_prev, ch=ch: e.scalar_tensor_tensor(
                                        out=dstS, in0=S_prev[:], scalar=DKg[:, ch:ch + 1], in1=p7[:, 0:128], op0=ALU.mult, op1=ALU.add),
                                        [p7t, S_prev_t, DKg_t], [dstS_t])
                                if lvl <= 2:
                                    continue
                                k.emit('pe', lambda e, q=q, zero_init=zero_init: e.matmul(psO[:, q * 64:(q + 1) * 64], vn[:], QKT[:, q, :], start=True, stop=zero_init),
                                       [vn_t, QKT_t], [psO_t])
                                if not zero_init:
                                    k.emit('pe', lambda e, q=q, cs=cs, S_prev=S_prev: e.matmul(psO[:, q * 64:(q + 1) * 64], S_prev[:], Qt[:, cs], start=False, stop=True),
                                           [S_prev_t, Qt_t], [psO_t])
                            if di == 0:
                                k.emit('act', lambda e, bsl=bsl: e.activation(out=oT[:, bsl], in_=psO[:, 0:256], func=AF.Copy), [psO_t], [oT_t])
                            else:
                                k.emit('dve', lambda e, bsl=bsl: e.tensor_tensor(out=oT[:, bsl], in0=oT[:, bsl], in1=psO[:, 0:256], op=ALU.add), [psO_t, oT_t], [oT_t])
                    if stop <= 4:
                        continue
                    if half == 0:
                        k.dma(d['gdout'][:, :, hh, :, :].rearrange("s d k e -> k s d e"), sout[:], reads=[sout_t])
                    (wg,), wt3 = self.wpiece([w_in[0, :, 3 * D + hh * 128:3 * D + (hh + 1) * 128]])
                    for tt in range(2):
                        sl = slice(tt * 512, (tt + 1) * 512)
                        ps, pst = self.ps[6 + tt], self.ps_t[6 + tt]
                        for c in range(NCH):
                            k.emit('pe', lambda e, ps=ps, c=c, sl=sl: e.matmul(ps[:], wg[:, c, :], self.hT[:, c, sl], start=(c == 0), stop=(c == NCH - 1)),
                                   [wt3, self.hT_t[c][tt]], [pst])
                        k.emit('act', lambda e, ps=ps, sl=sl: e.activation(out=vT[:, sl], in_=ps[:], func=AF.Silu), [pst], [vT_t])
                    self.HV_t = GV_t
                    self.head_norm(ntm, oT[:], oT_t, mix[:, 0, :], mix_t[0], GV[:, 1:2], 128.0 * 1e-6, 5, extra_mul=vT[:], extra_t=vT_t)
                    self.out_proj(d['gdn_w_out'][0, hh * 128:(hh + 1) * 128, :], mix, mix_t, 1, half, [6, 7])
                k.barrier()

    def ffn(self, layer, half):
        k, d, nc = self.k, self.d, self.nc
        seqlen = 256 if half == 0 else 1024
        oc, _ = PC['ffn_conv']
        ob, _ = PC['ffn_conv_b']

        def cw(tap, fchunk):
            col = oc + (layer * 3 + tap) * 44 + fchunk
            return self.PT[:, col:col + 1]

        def cb(fchunk):
            col = ob + layer * 44 + fchunk
            return self.PT[:, col:col + 1]

        groups = [(0, 8), (8, 16), (16, 22)]
        specs = []
        pidx = {}
        for (g0, g1) in groups:
            for j in range(g0, g1):
                pidx[('u', j)] = len(specs)
                specs.append([d['ffn_w_up'][layer, :, j * 128:(j + 1) * 128], d['ffn_w_up'][layer, :, D_FF + j * 128:D_FF + (j + 1) * 128]])
            for dp in range(4):
                pidx[('d', g0, dp)] = len(specs)
                specs.append([d['ffn_w_down'][layer, g0 * 128:g1 * 128, dp * 256:(dp + 1) * 256]])
        pf = PF(self, specs)
        PAIRS = [(0, 1), (2, 3), (4, 5)] if self.modgen is not None else [(0, 1), (2, 3), (4, 5), (6, 7)]
        u = 0
        v = 0
        with ExitStack() as es:
            aT = self.tmp(es, 'aT', [128, 8, TOK], F32R)
            aT_t = trks('aT', 8)
            acc = [self.tmp(es, f'facc{i}', [128, 2, TOK], F32) for i in range(2)]
            acc_t = trks('facc', 2, 2, 2)
            for (g0, g1) in groups:
                for j in range(g0, g1):
                    slot = j - g0
                    (wv, wg), wt = pf.get(pidx[('u', j)])
                    ai = j % 2
                    ac = acc[ai]
                    banks = {}
                    for tt in range(2):
                        pair = PAIRS[u % len(PAIRS)]
                        u += 1
                        sl = slice(tt * 512, (tt + 1) * 512)
                        for vi, (w, fch) in enumerate(((wv, j), (wg, NFF + j))):
                            ps, pst = self.ps[pair[vi]], self.ps_t[pair[vi]]
                            banks[(vi, tt)] = (ps, pst)
                            for c in range(NCH):
                                k.emit('pe', lambda e, ps=ps, w=w, c=c, sl=sl: e.matmul(
                                    ps[:], w[:, c, :], self.hT[:, c, sl],
                                    start=(c == 0), stop=(c == NCH - 1)), [wt, self.hT_t[c][tt]], [pst])
                        for vi, fch in ((0, j), (1, NFF + j)):
                            ps, pst = banks[(vi, tt)]
                            at_ = acc_t[ai][vi][tt]
                            k.emit('act', lambda e, ps=ps, sl=sl, fch=fch, vi=vi: e.activation(
                                out=ac[:, vi, sl], in_=ps[:], func=AF.Identity, bias=cb(fch), scale=cw(1, fch)), [pst, self.PT_t], [at_])
                            sl_ = min(seqlen, 512)
                            ns = 512 // sl_
                            pv = ps[:].rearrange("p (s n) -> p s n", s=ns)
                            av = ac[:, vi, sl].rearrange("p (s n) -> p s n", s=ns)
                            k.emit('dve', lambda e, pv=pv, av=av, fch=fch, sl_=sl_: e.scalar_tensor_tensor(
                                out=av[:, :, 1:sl_], in0=pv[:, :, 0:sl_ - 1], scalar=cw(0, fch), in1=av[:, :, 1:sl_],
                                op0=ALU.mult, op1=ALU.add), [pst, self.PT_t, at_], [at_])
                            k.emit('dve', lambda e, pv=pv, av=av, fch=fch, sl_=sl_: e.scalar_tensor_tensor(
                                out=av[:, :, 0:sl_ - 1], in0=pv[:, :, 1:sl_], scalar=cw(2, fch), in1=av[:, :, 0:sl_ - 1],
                                op0=ALU.mult, op1=ALU.add), [pst, self.PT_t, at_], [at_])
                    if seqlen > 512:
                        for vi, fch in ((0, j), (1, NFF + j)):
                            p0, p0t = banks[(vi, 0)]
                            p1, p1t = banks[(vi, 1)]
                            k.emit('dve', lambda e, p0=p0, fch=fch, vi=vi: e.scalar_tensor_tensor(
                                out=ac[:, vi, 512:513], in0=p0[:, 511:512], scalar=cw(0, fch), in1=ac[:, vi, 512:513],
                                op0=ALU.mult, op1=ALU.add), [p0t, self.PT_t, acc_t[ai][vi][1]], [acc_t[ai][vi][1]])
                            k.emit('dve', lambda e, p1=p1, fch=fch, vi=vi: e.scalar_tensor_tensor(
                                out=ac[:, vi, 511:512], in0=p1[:, 0:1], scalar=cw(2, fch), in1=ac[:, vi, 511:512],
                                op0=ALU.mult, op1=ALU.add), [p1t, self.PT_t, acc_t[ai][vi][0]], [acc_t[ai][vi][0]])
                    k.emit('act', lambda e, ac=ac: e.activation(out=ac[:, 1, :], in_=ac[:, 1, :], func=AF.Silu), acc_t[ai][1], acc_t[ai][1])
                    k.emit('pool', lambda e, slot=slot, ac=ac: e.tensor_tensor(out=aT[:, slot, :], in0=ac[:, 0, :], in1=ac[:, 1, :], op=ALU.mult),
                           acc_t[ai], [aT_t[slot]])
                    if self.modgen is not None:
                        next(self.modgen, None)
                ng = g1 - g0
                for dp in range(4):
                    (wd,), wdt = pf.get(pidx[('d', g0, dp)])
                    for dmi in range(2):
                        dm = dp * 2 + dmi
                        for tt in range(2):
                            b = v % 6
                            v += 1
                            ps, pst = self.ps[b], self.ps_t[b]
                            for jj in range(ng):
                                k.emit('pe', lambda e, ps=ps, jj=jj, dmi=dmi, tt=tt: e.matmul(
                                    ps[:], wd[:, jj, dmi * 128:(dmi + 1) * 128], aT[:, jj, tt * 512:(tt + 1) * 512],
                                    start=(jj == 0), stop=(jj == ng - 1)), [wdt, aT_t[jj]], [pst])
                            xs = self.xT[half][:, dm, tt * 512:(tt + 1) * 512]
                            k.emit('dve', lambda e, ps=ps, xs=xs, dm=dm: e.scalar_tensor_tensor(
                                out=xs, in0=ps[:], scalar=self.gate(1, dm, half), in1=xs, op0=ALU.mult, op1=ALU.add),
                                [pst, self.MOD_t, self.xT_t[half][dm][tt]], [self.xT_t[half][dm][tt]])
            k.barrier()

    def final(self, cfg):
        k, d, nc = self.k, self.d, self.nc
        og, _ = PC['final_g']
        with ExitStack() as es:
            sq = self.tmp(es, 'fsq', [128, NCH, 512], F32R)
            sq_t = Trk('fsq')
            rstd = self.tmp(es, 'frstd', [128, 512], F32)
            rstd_t = Trk('frstd')
            yo = self.tmp(es, 'fyo', [128, NCH, 512], F32)
            yo_t = self.ptrk('fyo', NCH)
            for half in range(2):
                for tt in range(2):
                    xs = self.xT[half][:, :, tt * 512:(tt + 1) * 512]
                    xs_t = [self.xT_t[half][c][tt] for c in range(NCH)]
                    k.emit('act', lambda e: e.activation(out=sq[:], in_=xs, func=AF.Square), xs_t, [sq_t])
                    ps, pst = self.ps[6], self.ps_t[6]
                    for c in range(NCH):
                        k.emit('pe', lambda e, c=c: e.matmul(ps[:], self.onesR[:], sq[:, c, :], start=(c == 0), stop=(c == NCH - 1)),
                               [self.onesR_t, sq_t], [pst])
                    k.emit('act', lambda e: e.activation(out=rstd[:], in_=ps[:], func=AF.Sqrt, bias=float(D * EPS), scale=1.0),
                           [pst], [rstd_t])
                    k.emit('dve', lambda e: e.reciprocal(out=rstd[:], in_=rstd[:]), [rstd_t], [rstd_t])
                    for c in range(NCH):
                        g = self.PT[:, og + c:og + c + 1]
                        k.emit('dve', lambda e, c=c, g=g: e.scalar_tensor_tensor(
                            out=yo[:, c, :], in0=self.xT[half][:, c, tt * 512:(tt + 1) * 512], scalar=g, in1=rstd[:],
                            op0=ALU.mult, op1=ALU.mult), [self.xT_t[half][c][tt], self.PT_t, rstd_t], [yo_t[c]])
                        k.emit('act', lambda e, c=c: e.activation(out=yo[:, c, :], in_=yo[:, c, :], func=AF.Copy, scale=32.0),
                               [yo_t[c]], [yo_t[c]])
                        k.dma(d['yout'][half, c, :, tt * 512:(tt + 1) * 512], yo[:, c, :], reads=[yo_t[c]])


_CACHE = {}


def _get_prog(cfg):
    key = repr(sorted(cfg.items()))
    if key not in _CACHE:
        p = Prog(dict(cfg))
        with p.es:
            p.declare()
            p.build()
        _CACHE[key] = p
    return _CACHE[key]


def _pack(plan, arrays, n):
    out = np.zeros((128, n), np.float32)
    for c0, key in plan:
        off = c0
        for (name, offset, rstride, rows, ncols) in key:
            flat = arrays[name].reshape(-1)
            a2 = np.lib.stride_tricks.as_strided(flat[offset:], shape=(rows, ncols), strides=(rstride * 4, 4))
            kc = rows // 128
            assert off + kc * ncols <= n
            out[:, off:off + kc * ncols] = a2.reshape(kc, 128, ncols).transpose(1, 0, 2).reshape(128, kc * ncols)
            off += kc * ncols
    return out


def _run(inp, cfg):
    p = _get_prog(cfg)
    consts, rope_tab = _build_consts()
    f32 = lambda a: np.ascontiguousarray(np.asarray(a, np.float32))
    warr = {n: f32(inp[n]) for n in ('w_mod', 'ffn_w_up', 'ffn_w_down', 'attn_w_in', 'attn_w_out',
                                     'hgrn_w_in', 'hgrn_w_out', 'gdn_w_in', 'gdn_w_out')}
    shared = {'wpk': _pack(p.wplan['wpk'], warr, WCOLS)}
    xp = f32(inp['x_prompt'])
    xs = f32(inp['x_sample'])
    ck = f32(inp['cache_attn_k'])
    cv = f32(inp['cache_attn_v'])
    in_maps = []
    for core in range(N_CORES):
        m = dict(shared)
        a = xp[4 * core:4 * core + 4].reshape(TOK, D).T.reshape(8, 128, TOK)
        b = xs[core].T.reshape(8, 128, TOK)
        m['xin'] = np.ascontiguousarray(np.stack([a, b], axis=0))
        m['params'] = _build_params(core, inp)
        m['consts'] = consts
        m['rope'] = rope_tab
        m['lamtab'] = np.ascontiguousarray(np.broadcast_to(np.asarray(inp['attn_lambda'], np.float32).reshape(1, 512), (128, 512)))
        carr = {'ck': np.ascontiguousarray(ck[core].transpose(0, 2, 3, 1)),
                'cv': np.ascontiguousarray(cv[core].reshape(2, 512, D))}
        m['cpk'] = _pack(p.wplan['cpk'], carr, CCOLS)
        m['st_hgrn'] = f32(inp['state_hgrn'][core, 0])
        m['st_gdn'] = f32(inp['state_gdn'][core, 0])
        in_maps.append(m)
    ncores = cfg.get('ncores', N_CORES)
    res = run_bass_kernel_spmd(p.nc, in_maps[:ncores], core_ids=list(range(ncores)))
    R = list(res.results) + [res.results[0]] * (N_CORES - ncores)
    y_prompt = np.empty((32, 256, D), np.float32)
    y_sample = np.empty((8, 1024, D), np.float32)
    new_k = np.empty((32, 2, 256, 8, 128), np.float32)
    new_v = np.empty((32, 2, 256, 8, 128), np.float32)
    new_h = np.empty((32, 1, 2, 8, 128, 128), np.float32)
    new_g = np.empty((32, 1, 2, 8, 128, 128), np.float32)
    for core in range(N_CORES):
        r = R[core]
        yo = r['yout']
        y_prompt[4 * core:4 * core + 4] = yo[0].reshape(D, TOK).T.reshape(4, 256, D)
        y_sample[core] = yo[1].reshape(D, TOK).T
        ko = r['kout']
        new_k[4 * core:4 * core + 4] = ko.reshape(2, 8, 128, 4, 256).transpose(3, 0, 4, 1, 2)
        vo = r['vout']
        new_v[4 * core:4 * core + 4] = vo.reshape(2, 4, 256, 8, 128).transpose(1, 0, 2, 3, 4)
        new_h[4 * core:4 * core + 4, 0] = r['hgout']
        new_g[4 * core:4 * core + 4, 0] = r['gdout']
    return (y_prompt, y_sample, new_k, new_v, new_h, new_g)


def kernel(**inputs):
    return _run(inputs, {})
```

```python
import math
from contextlib import ExitStack

import numpy as np
import concourse.bass as bass
import concourse.mybir as mybir
from concourse.bass_utils import run_bass_kernel_spmd

F32 = mybir.dt.float32
F32R = mybir.dt.float32r
AF = mybir.ActivationFunctionType
ALU = mybir.AluOpType
AX = mybir.AxisListType

D = 1024
NCH = 8
TOK = 1024
DEPTH = 4
D_FF = 2816
NFF = 22
EPS = 1e-6
N_CORES = 8
WCOLS = (4 * 1024 * 6144 + 4 * 1024 * 5632 + 4 * 2816 * 1024 + 2 * 1024 * 3072 + 2 * 1024 * 1024 + 1024 * 5120 + 1024 * 1024
         + 1024 * 4128 + 1024 * 1024) // 128
CCOLS = (2 * 8 * 128 * 512 + 2 * 512 * 1024) // 128


class Cols:
    def __init__(self):
        self.off = {}
        self.n = 0

    def add(self, name, w):
        self.off[name] = (self.n, w)
        self.n += w

    def __getitem__(self, name):
        return self.off[name]


def _param_cols():
    c = Cols()
    c.add('cond', 16)
    c.add('norm_g', 64)
    c.add('b_mod', 192)
    c.add('final_g', 8)
    c.add('subln', 2)
    c.add('hgrn_lb', 32)
    c.add('hgrn_norm', 1)
    c.add('gdn_norm', 1)
    c.add('gdn_conv', 72)
    c.add('ffn_conv', 528)
    c.add('ffn_conv_b', 176)
    c.add('gdn_alog_col', 1)
    c.add('gdn_dt_col', 1)
    c.add('gdn_alog_row', 16)
    c.add('gdn_dt_row', 16)
    return c


PC = _param_cols()


def _fm(v):
    v = np.asarray(v, np.float32)
    r = v.reshape(-1, 128)
    return np.ascontiguousarray(r.T)


def _build_params(core, inp):
    P = np.zeros((128, PC.n), np.float32)

    def put(name, arr):
        o, w = PC[name]
        assert arr.shape == (128, w), (name, arr.shape, w)
        P[:, o:o + w] = arr

    cond = np.stack([inp['c_ctx'], inp['c'][core]], axis=0)
    put('cond', np.ascontiguousarray(cond.reshape(2, 8, 128).transpose(2, 1, 0)).reshape(128, 16))
    put('norm_g', _fm(inp['norm_g']))
    put('b_mod', _fm(inp['b_mod']))
    put('final_g', _fm(inp['final_g']))
    put('subln', _fm(inp['attn_subln']))
    put('hgrn_lb', _fm(inp['hgrn_lb']))
    put('hgrn_norm', _fm(inp['hgrn_norm']))
    put('gdn_norm', _fm(inp['gdn_norm']))
    put('gdn_conv', _fm(inp['gdn_conv']))
    put('ffn_conv', _fm(inp['ffn_conv']))
    put('ffn_conv_b', _fm(inp['ffn_conv_b']))
    al = np.zeros((128, 1), np.float32)
    al[:16, 0] = np.asarray(inp['gdn_a_log'], np.float32).reshape(16)
    put('gdn_alog_col', al)
    dtb = np.zeros((128, 1), np.float32)
    dtb[:16, 0] = np.asarray(inp['gdn_dt_bias'], np.float32).reshape(16)
    put('gdn_dt_col', dtb)
    put('gdn_alog_row', np.broadcast_to(np.asarray(inp['gdn_a_log'], np.float32).reshape(1, 16), (128, 16)))
    put('gdn_dt_row', np.broadcast_to(np.asarray(inp['gdn_dt_bias'], np.float32).reshape(1, 16), (128, 16)))
    return P


def _const_cols():
    c = Cols()
    c.add('ident', 128)
    c.add('ones', 128)
    c.add('perm', 128)
    c.add('mask_f', 256)
    c.add('mask_b', 256)
    c.add('negu_f', 256)
    c.add('negl_f', 256)
    c.add('negu_b', 256)
    c.add('negl_b', 256)
    c.add('sneg_f', 256)
    c.add('sneg_b', 256)
    c.add('idrep', 256)
    c.add('sel', 512)
    c.add('nsel', 128)
    return c


CC = _const_cols()


def _build_consts():
    C = np.zeros((128, CC.n), np.float32)
    o, w = CC['ident']
    C[:, o:o + w] = np.eye(128, dtype=np.float32)
    o, w = CC['ones']
    C[:, o:o + w] = 1.0
    o, w = CC['perm']
    tok = np.arange(1024)
    row = (tok // 64).astype(np.float32)
    col = (tok % 64).astype(np.float32)
    inv = (np.float32(10000.0) ** (-np.arange(16, dtype=np.float32) / np.float32(16))).astype(np.float32)
    ROPE = np.zeros((128, 2048), np.float32)
    oc_, os_ = 0, 1024
    for p in range(128):
        dd = p % 64
        i = dd % 32
        partner = p + 16 if i < 16 else p - 16
        C[partner, o + p] = 1.0
        f = i % 16
        pos = row if dd < 32 else col
        ang = (pos * inv[f]).astype(np.float32)
        ROPE[p, oc_:oc_ + 1024] = np.cos(ang)
        ROPE[p, os_:os_ + 1024] = np.sin(ang) * (-1.0 if i < 16 else 1.0)
    sidx = np.arange(64)[:, None]
    tidx = np.arange(64)[None, :]
    om, _ = CC['mask_f']
    C[:64, om:om + 256] = np.tile((sidx <= tidx).astype(np.float32), (1, 4))
    om, _ = CC['mask_b']
    C[:64, om:om + 256] = np.tile((sidx >= tidx).astype(np.float32), (1, 4))
    BIG = 30000.0
    p_, j_ = sidx, tidx

    def putm(name, m):
        o_, _ = CC[name]
        C[:64, o_:o_ + 256] = np.tile(m.astype(np.float32), (1, 4))

    putm('negu_f', np.where(j_ >= p_, 0.0, -BIG))
    putm('negl_f', np.where(j_ < p_, 0.0, -BIG))
    putm('negu_b', np.where(j_ <= p_, 0.0, -BIG))
    putm('negl_b', np.where(j_ > p_, 0.0, -BIG))
    putm('sneg_f', np.where(j_ > p_, -1.0, 0.0))
    putm('sneg_b', np.where(j_ < p_, -1.0, 0.0))
    putm('idrep', (j_ == p_))
    o_, _ = CC['sel']
    for kk in range(4):
        C[kk, o_ + kk * 128:o_ + (kk + 1) * 128] = 1.0
    o_, _ = CC['nsel']
    for kk in range(2):
        C[kk, o_ + kk * 64:o_ + (kk + 1) * 64] = -1.0
    return C, ROPE


class _GT:
    def __init__(self, name):
        self.name = name


class Geo:
    def __init__(self, name, shape, offset=0, pat=None):
        self.tensor = _GT(name)
        if pat is None:
            pat = []
            st = 1
            for n in reversed(shape):
                pat.insert(0, (st, n))
                st *= n
        self.ap = tuple(pat)
        self.offset = offset

    @property
    def shape(self):
        return tuple(n for _, n in self.ap)

    def __getitem__(self, key):
        if not isinstance(key, tuple):
            key = (key,)
        key = key + (slice(None),) * (len(self.ap) - len(key))
        off = self.offset
        pat = []
        for (st, n), kk in zip(self.ap, key):
            if isinstance(kk, int):
                off += st * kk
            else:
                a, b, _ = kk.indices(n)
                off += st * a
                pat.append((st, b - a))
        return Geo(self.tensor.name, None, off, pat)


class Trk:
    __slots__ = ('name', 'w', 'rs', 'sem', 'cnt', 'psum')

    def __init__(self, name):
        self.name = name
        self.psum = False
        self.w = None
        self.rs = {}
        self.sem = None
        self.cnt = 0


def trks(name, *dims):
    if len(dims) == 1:
        return [Trk(f'{name}{i}') for i in range(dims[0])]
    return [trks(f'{name}{i}_', *dims[1:]) for i in range(dims[0])]


def flat(x):
    if isinstance(x, Trk):
        return [x]
    out = []
    for e in x:
        out.extend(flat(e))
    return out


class KB:
    def __init__(self, nc, es):
        self.nc = nc
        self.es = es
        self.E = {'pe': nc.tensor, 'act': nc.scalar, 'dve': nc.vector, 'pool': nc.gpsimd, 'sp': nc.sync}
        self.sem = {e: es.enter_context(nc.semaphore(f's_{e}')) for e in self.E}
        self.cnt = {e: 0 for e in self.E}
        self.waited = {e: {} for e in self.E}
        self.dma_sems = []
        self.n_ins = 0
        self.n_wait = 0

    def _wait(self, eng, deps):
        need = {}
        for key, val in deps:
            if key == 'pe' and eng == 'pe':
                continue
            if need.get(key, 0) < val:
                need[key] = val
        wt = self.waited[eng]
        for key, val in need.items():
            if wt.get(key, 0) >= val:
                continue
            sem = self.sem[key] if isinstance(key, str) else key
            self.E[eng].wait_ge(sem, val)
            self.n_wait += 1
            wt[key] = val

    def _deps(self, reads, writes):
        deps = []
        for t in reads:
            if t.w is not None:
                deps.append(t.w)
            if t.psum:
                deps.extend(t.rs.items())
        for t in writes:
            if t.w is not None:
                deps.append(t.w)
            deps.extend(t.rs.items())
        return deps

    def _mark(self, tok, reads, writes):
        k, v = tok
        for t in reads:
            if t.rs.get(k, 0) < v:
                t.rs[k] = v
        for t in writes:
            t.w = tok
            t.rs = {}

    def emit(self, eng, fn, reads=(), writes=()):
        reads = flat(reads)
        writes = flat(writes)
        self._wait(eng, self._deps(reads, writes))
        ins = fn(self.E[eng])
        self.cnt[eng] += 1
        ins.then_inc(self.sem[eng], 1)
        self.n_ins += 1
        tok = (eng, self.cnt[eng])
        self._mark(tok, reads, writes)
        return tok

    def dma(self, out, in_, reads=(), writes=(), q='sp'):
        reads = flat(reads)
        writes = flat(writes)
        self._wait(q, self._deps(reads, writes))
        owner = (writes + reads)[0]
        if owner.sem is None:
            owner.sem = self.es.enter_context(self.nc.semaphore(f'd_{owner.name}'))
            self.dma_sems.append(owner)
        self.E[q].dma_start(out=out, in_=in_).then_inc(owner.sem, 16)
        owner.cnt += 16
        self.n_ins += 1
        tok = (owner.sem, owner.cnt)
        self._mark(tok, reads, writes)
        return tok

    def dma_group(self, pairs, writes, q='sp'):
        writes = flat(writes)
        self._wait(q, self._deps([], writes))
        owner = writes[0]
        if owner.sem is None:
            owner.sem = self.es.enter_context(self.nc.semaphore(f'd_{owner.name}'))
            self.dma_sems.append(owner)
        for out, in_ in pairs:
            self.E[q].dma_start(out=out, in_=in_).then_inc(owner.sem, 16)
            owner.cnt += 16
            self.n_ins += 1
        tok = (owner.sem, owner.cnt)
        self._mark(tok, [], writes)
        return tok

    def barrier(self):
        for e in self.E:
            deps = [(o, self.cnt[o]) for o in self.E if o != e and self.cnt[o] > 0]
            deps += [(t.sem, t.cnt) for t in self.dma_sems]
            wt = self.waited[e]
            for key, val in deps:
                if wt.get(key, 0) >= val:
                    continue
                sem = self.sem[key] if isinstance(key, str) else key
                self.E[e].wait_ge(sem, val)
                self.n_wait += 1
                wt[key] = val

    def finish(self):
        deps = [(t.sem, t.cnt) for t in self.dma_sems]
        deps += [(o, self.cnt[o]) for o in self.E if o != 'sp' and self.cnt[o] > 0]
        self._wait('sp', deps)

    def sb(self, name, shape, dt=F32):
        return self.es.enter_context(self.nc.sbuf_tensor(name, list(shape), dt))


class PF:
    def __init__(self, prog, specs):
        self.p, self.specs, self.h = prog, specs, {}

    def get(self, i):
        for t in (i, i + 1):
            if t < len(self.specs) and t not in self.h:
                self.h[t] = self.p.wpiece(self.specs[t])
        return self.h.pop(i)


class Prog:
    def __init__(self, cfg):
        self.cfg = cfg
        nc = bass.Bass("TRN2", target_bir_lowering=False)
        self.nc = nc
        self.es = ExitStack()
        self.k = KB(nc, self.es)
        self.rr = 0
        self.wplan = {'wpk': [], 'cpk': []}
        self.wcols = {'wpk': 0, 'cpk': 0}
        self.wkeys = {}

    def declare(self):
        nc = self.nc

        def din(name, shape):
            return nc.dram_tensor(name, list(shape), F32, kind="ExternalInput").ap()

        def dout(name, shape):
            return nc.dram_tensor(name, list(shape), F32, kind="ExternalOutput").ap()

        d = {}
        d['xin'] = din('xin', [2, 8, 128, TOK])
        d['params'] = din('params', [128, PC.n])
        d['consts'] = din('consts', [128, CC.n])
        d['rope'] = din('rope', [128, 2048])
        d['wpk'] = din('wpk', [128, WCOLS])
        d['cpk'] = din('cpk', [128, CCOLS])
        d['lamtab'] = din('lamtab', [128, 512])
        d['gscr'] = nc.dram_tensor('gscr', [48, TOK], F32, kind="Internal").ap()
        d['w_mod'] = Geo('w_mod', [4, D, 6 * D])
        d['ffn_w_up'] = Geo('ffn_w_up', [4, D, 2 * D_FF])
        d['ffn_w_down'] = Geo('ffn_w_down', [4, D_FF, D])
        d['attn_w_in'] = Geo('attn_w_in', [2, D, 3 * D])
        d['attn_w_out'] = Geo('attn_w_out', [2, D, D])
        d['hgrn_w_in'] = Geo('hgrn_w_in', [1, D, 5 * D])
        d['hgrn_w_out'] = Geo('hgrn_w_out', [1, D, D])
        d['gdn_w_in'] = Geo('gdn_w_in', [1, D, 4 * D + 32])
        d['gdn_w_out'] = Geo('gdn_w_out', [1, D, D])
        d['ck'] = Geo('ck', [2, 8, 128, 512])
        d['cv'] = Geo('cv', [2, 512, D])
        d['st_hgrn'] = din('st_hgrn', [2, 8, 128, 128])
        d['st_gdn'] = din('st_gdn', [2, 8, 128, 128])
        d['yout'] = dout('yout', [2, 8, 128, TOK])
        d['kout'] = dout('kout', [2, 8, 128, TOK])
        d['vout'] = dout('vout', [2, TOK, D])
        d['hgout'] = dout('hgout', [4, 2, 8, 128, 128])
        d['gdout'] = dout('gdout', [4, 2, 8, 128, 128])
        self.d = d

    def ptrk(self, name, n=None):
        if not hasattr(self, '_pt'):
            self._pt = {}
        if name not in self._pt:
            self._pt[name] = Trk(name) if n is None else trks(name, n)
        return self._pt[name]

    def tmp(self, es, name, shape, dt=F32):
        self._uid = getattr(self, '_uid', 0) + 1
        return es.enter_context(self.nc.sbuf_tensor(f'{name}_{self._uid}', list(shape), dt))

    def evac_eng(self):
        self.rr += 1
        return 'act' if self.rr % 2 else 'dve'

    def copy(self, eng, out, in_, reads, writes):
        if eng == 'act':
            return self.k.emit('act', lambda e: e.activation(out=out, in_=in_, func=AF.Copy), reads, writes)
        return self.k.emit(eng, lambda e: e.tensor_copy(out=out, in_=in_), reads, writes)

    def init_wpool(self):
        k = self.k
        self.ws = [k.sb(f'ws{i}', [128, 2048], F32) for i in range(2)]
        self.ws_t = trks('ws', 2)
        self.wr = [k.sb(f'wr{i}', [128, 2048], F32R) for i in range(2)]
        self.wr_t = trks('wr', 2)
        self.ws_i = 0
        self.wr_i = 0

    def wpiece(self, segs, rounded=True, dest=None):
        k = self.k
        si = self.ws_i
        self.ws_i = (si + 1) % 2
        st, stt = self.ws[si], self.ws_t[si]
        off = 0
        views = []
        key = []
        percore = False
        for ap in segs:
            rows, ncols = ap.shape
            kc = rows // 128
            pat = tuple((int(a), int(b)) for a, b in ap.ap)
            assert len(pat) == 2 and pat[1][0] == 1, pat
            name = ap.tensor.name
            percore = percore or name in ('ck', 'cv')
            key.append((name, int(ap.offset), pat[0][0], rows, ncols))
            views.append((off, kc, ncols))
            off += kc * ncols
        key = tuple(key)
        pk = 'cpk' if percore else 'wpk'
        if key not in self.wkeys:
            self.wkeys[key] = (pk, self.wcols[pk])
            self.wplan[pk].append((self.wcols[pk], key))
            self.wcols[pk] += off
        pk, c0 = self.wkeys[key]
        k.dma(st[:, 0:off], self.d[pk][:, c0:c0 + off], writes=[stt])
        if not rounded:
            outs = [st[:, o:o + kc * n].rearrange("p (c n) -> p c n", c=kc) for (o, kc, n) in views]
            return outs, stt
        if dest is not None:
            rt, rtt = dest
        else:
            ri = self.wr_i
            self.wr_i = (ri + 1) % 2
            rt, rtt = self.wr[ri], self.wr_t[ri]
        k.emit('act', lambda e: e.activation(out=rt[:, 0:off], in_=st[:, 0:off], func=AF.Copy), [stt], [rtt])
        outs = [rt[:, o:o + kc * n].rearrange("p (c n) -> p c n", c=kc) for (o, kc, n) in views]
        return outs, rtt

    def build(self):
        nc, k, d, cfg = self.nc, self.k, self.d, self.cfg
        self.ps = [self.es.enter_context(nc.psum_tensor(f'ps{i}', [128, 512], F32)) for i in range(8)]
        self.ps_t = trks('ps', 8)
        for t in self.ps_t:
            t.psum = True
        self.PT = k.sb('PT', [128, PC.n])
        self.PT_t = Trk('PT')
        self.CT = k.sb('CT', [128, CC.n])
        self.CT_t = Trk('CT')
        self.onesR = k.sb('onesR', [128, 128], F32R)
        self.onesR_t = Trk('onesR')
        self.identR = k.sb('identR', [128, 128], F32R)
        self.identR_t = Trk('identR')
        self.xT = [k.sb(f'xT{h}', [128, NCH, TOK]) for h in range(2)]
        self.xT_t = trks('xT', 2, NCH, 2)
        self.hT = k.sb('hT', [128, NCH, TOK], F32R)
        self.hT_t = trks('hT', NCH, 2)
        self.permR = k.sb('permR', [128, 128], F32R)
        self.permR_t = Trk('permR')
        self.AV = k.sb('AV', [128, 8])
        self.AV_t = Trk('AV')
        self.MODs = [k.sb(f'MOD{i}', [128, 48, 2]) for i in range(2)]
        self.MODs_t = trks('MOD', 2)
        self.ABs = [k.sb(f'AB{i}', [128, 2, 2, 8, 2]) for i in range(2)]
        self.ABs_t = trks('AB', 2)
        self.modgen = None
        self.sc = k.sb('sc', [128, 16])
        self.sc_t = Trk('sc')
        self.init_wpool()

        k.dma(self.PT[:], d['params'][:, :], writes=[self.PT_t])
        k.dma(self.CT[:], d['consts'][:, :], writes=[self.CT_t])
        for h in range(2):
            k.dma_group([(self.xT[h][:, c, :], d['xin'][h, c, :, :]) for c in range(NCH)], self.xT_t[h])
        o, w = CC['ones']
        k.emit('dve', lambda e: e.tensor_copy(out=self.onesR[:], in_=self.CT[:, o:o + w]), [self.CT_t], [self.onesR_t])
        o2, w2 = CC['ident']
        k.emit('dve', lambda e: e.tensor_copy(out=self.identR[:], in_=self.CT[:, o2:o2 + w2]), [self.CT_t], [self.identR_t])
        o3, w3 = CC['perm']
        k.emit('dve', lambda e: e.tensor_copy(out=self.permR[:], in_=self.CT[:, o3:o3 + w3]), [self.CT_t], [self.permR_t])
        oc, wc = PC['cond']
        k.emit('act', lambda e: e.activation(out=self.sc[:], in_=self.PT[:, oc:oc + wc], func=AF.Silu), [self.PT_t], [self.sc_t])

        nl = cfg.get('layers', DEPTH)
        for _ in self.modulation(0):
            pass
        for layer in range(nl):
            p = layer % 2
            self.MOD, self.MOD_t, self.AB, self.AB_t = self.MODs[p], self.MODs_t[p], self.ABs[p], self.ABs_t[p]
            for half in range(2):
                if cfg.get('mixers', True):
                    self.rmsnorm_mod(layer, 0, half)
                    self.mixer(layer, half)
                if half == 1 and layer + 1 < nl:
                    self.modgen = self.modulation(layer + 1)
                if cfg.get('ffn', True):
                    self.rmsnorm_mod(layer, 1, half)
                    self.ffn(layer, half)
                if self.modgen is not None:
                    for _ in self.modgen:
                        pass
                    self.modgen = None
        self.final(cfg)
        k.finish()

    def modulation(self, layer):
        k, d = self.k, self.d
        p = layer % 2
        MOD, MOD_t, AB, AB_t = self.MODs[p], self.MODs_t[p], self.ABs[p], self.ABs_t[p]
        ps, pst = self.ps[7], self.ps_t[7]
        scv = self.sc[:].rearrange("p (c j) -> p c j", j=2)
        for piece in range(24):
            (w,), wt = self.wpiece([d['w_mod'][layer, :, piece * 256:(piece + 1) * 256]], rounded=False)
            for q2 in range(2):
                q = piece * 2 + q2
                for c in range(NCH):
                    k.emit('pe', lambda e, c=c, q2=q2, q=q: e.matmul(
                        ps[:, q * 2:q * 2 + 2], w[:, c, q2 * 128:(q2 + 1) * 128], scv[:, c, :],
                        start=(c == 0), stop=(c == NCH - 1)), [wt, self.sc_t], [pst])
            yield piece
        ob, wb = PC['b_mod']
        bm = self.PT[:, ob + layer * 48: ob + layer * 48 + 48]
        k.emit('dve', lambda e: e.tensor_tensor(
            out=MOD[:], in0=ps[:, 0:96].rearrange("p (q j) -> p q j", j=2),
            in1=bm.unsqueeze(2).broadcast_to([128, 48, 2]), op=ALU.add), [pst, self.PT_t], [MOD_t])
        og, wg = PC['norm_g']
        for s in range(2):
            g = self.PT[:, og + (layer * 2 + s) * 8: og + (layer * 2 + s) * 8 + 8]
            sh = MOD[:, s * 24 + 0: s * 24 + 8, :]
            scl = MOD[:, s * 24 + 8: s * 24 + 16, :]
            A = AB[:, s, 0, :, :]
            B = AB[:, s, 1, :, :]
            k.emit('dve', lambda e, scl=scl, A=A: e.tensor_scalar(
                out=A, in0=scl, scalar1=1.0, scalar2=32.0, op0=ALU.add, op1=ALU.mult), [MOD_t], [AB_t])
            k.emit('dve', lambda e, A=A, g=g: e.tensor_tensor(
                out=A, in0=A, in1=g.unsqueeze(2).broadcast_to([128, 8, 2]), op=ALU.mult), [AB_t, self.PT_t], [AB_t])
            k.emit('dve', lambda e, B=B, sh=sh: e.tensor_copy(out=B, in_=sh), [MOD_t], [AB_t])

    def gate(self, s, c, half):
        return self.MOD[:, s * 24 + 16 + c, half:half + 1]

    def rmsnorm_mod(self, layer, s, half):
        k = self.k
        with ExitStack() as es:
            sq = self.tmp(es, 'nsq', [128, NCH, 512], F32R)
            sq_t = Trk('nsq')
            tmp = self.tmp(es, 'ntmp', [128, NCH, 512], F32)
            tmp_t = Trk('ntmp')
            rstd = self.tmp(es, 'nrstd', [128, 512], F32)
            rstd_t = Trk('nrstd')
            for tt in range(2):
                xs = self.xT[half][:, :, tt * 512:(tt + 1) * 512]
                xs_t = [self.xT_t[half][c][tt] for c in range(NCH)]
                k.emit('act', lambda e: e.activation(out=sq[:], in_=xs, func=AF.Square), xs_t, [sq_t])
                ps, pst = self.ps[6], self.ps_t[6]
                for c in range(NCH):
                    k.emit('pe', lambda e, c=c: e.matmul(ps[:], self.onesR[:], sq[:, c, :], start=(c == 0), stop=(c == NCH - 1)),
                           [self.onesR_t, sq_t], [pst])
                k.emit('act', lambda e: e.activation(out=rstd[:], in_=ps[:], func=AF.Sqrt, bias=float(D * EPS), scale=1.0),
                       [pst], [rstd_t])
                k.emit('dve', lambda e: e.reciprocal(out=rstd[:], in_=rstd[:]), [rstd_t], [rstd_t])
                k.emit('dve', lambda e: e.tensor_tensor(out=tmp[:], in0=xs, in1=rstd[:].unsqueeze(1).broadcast_to([128, NCH, 512]),
                                                        op=ALU.mult), xs_t + [rstd_t], [tmp_t])
                for c in range(NCH):
                    A = self.AB[:, s, 0, c, half:half + 1]
                    B = self.AB[:, s, 1, c, half:half + 1]
                    out = self.hT[:, c, tt * 512:(tt + 1) * 512]
                    if c % 2 == 0:
                        k.emit('act', lambda e, c=c, A=A, B=B, out=out: e.activation(
                            out=out, in_=tmp[:, c, :], func=AF.Identity, bias=B, scale=A),
                            [tmp_t, self.AB_t], [self.hT_t[c][tt]])
                    else:
                        k.emit('dve', lambda e, c=c, A=A, B=B, out=out: e.tensor_scalar(
                            out=out, in0=tmp[:, c, :], scalar1=A, scalar2=B, op0=ALU.mult, op1=ALU.add),
                            [tmp_t, self.AB_t], [self.hT_t[c][tt]])
            k.barrier()

    def mixer(self, layer, half):
        kind = layer % 3
        ml = self.cfg.get('mixlist', (0, 1, 2))
        if kind not in ml:
            return
        if kind == 0:
            if half == 0:
                self.attn_prep(layer)
            self.attn(layer, half)
        elif kind == 1:
            self.hgrn(layer, half)
        else:
            self.gdn(layer, half)

    def out_pf(self, w_out_rows):
        return PF(self, [[w_out_rows[:, dp * 256:(dp + 1) * 256]] for dp in range(4)])

    def out_proj(self, w_out_rows, src, src_t, nh, half, banks, pf=None):
        k = self.k
        bi = 0
        pf = pf if pf is not None else self.out_pf(w_out_rows)
        for dp in range(4):
            (wo,), wot = pf.get(dp)
            for dmi in range(2):
                dm = dp * 2 + dmi
                for tt in range(2):
                    b = banks[bi % len(banks)]
                    bi += 1
                    ps, pst = self.ps[b], self.ps_t[b]
                    for hl in range(nh):
                        k.emit('pe', lambda e, ps=ps, hl=hl, dmi=dmi, tt=tt: e.matmul(
                            ps[:], wo[:, hl, dmi * 128:(dmi + 1) * 128], src[:, hl, tt * 512:(tt + 1) * 512],
                            start=(hl == 0), stop=(hl == nh - 1)), [wot, src_t[hl]], [pst])
                    xs = self.xT[half][:, dm, tt * 512:(tt + 1) * 512]
                    k.emit('dve', lambda e, ps=ps, xs=xs, dm=dm: e.scalar_tensor_tensor(
                        out=xs, in0=ps[:], scalar=self.gate(0, dm, half), in1=xs, op0=ALU.mult, op1=ALU.add),
                        [pst, self.MOD_t, self.xT_t[half][dm][tt]], [self.xT_t[half][dm][tt]])

    def head_norm(self, tmps, src, src_t, dst, dst_t, gcol, eps_total, bank, extra_mul=None, extra_t=None):
        k = self.k
        sq, sq_t, rs, rs_t = tmps
        ps, pst = self.ps[bank], self.ps_t[bank]
        for tt in range(2):
            sl = slice(tt * 512, (tt + 1) * 512)
            k.emit('act', lambda e, sl=sl: e.activation(out=sq[:], in_=src[:, sl], func=AF.Square), [src_t], [sq_t])
            k.emit('pe', lambda e: e.matmul(ps[:], self.onesR[:], sq[:], start=True, stop=True), [self.onesR_t, sq_t], [pst])
            k.emit('act', lambda e: e.activation(out=rs[:], in_=ps[:], func=AF.Sqrt, bias=float(eps_total), scale=1.0), [pst], [rs_t])
            k.emit('dve', lambda e: e.reciprocal(out=rs[:], in_=rs[:]), [rs_t], [rs_t])
            if extra_mul is not None:
                k.emit('dve', lambda e, sl=sl: e.tensor_tensor(out=rs[:], in0=rs[:], in1=extra_mul[:, sl], op=ALU.mult), [rs_t, extra_t], [rs_t])
            k.emit('dve', lambda e, sl=sl: e.scalar_tensor_tensor(out=dst[:, sl], in0=src[:, sl], scalar=gcol, in1=rs[:], op0=ALU.mult, op1=ALU.mult),
                   [src_t, rs_t, self.PT_t, self.AV_t] + ([self.HV_t] if hasattr(self, 'HV_t') else []), [dst_t])

    def norm_tmps(self, es):
        return (self.tmp(es, 'hsq', [128, 512], F32R), Trk('hsq'), self.tmp(es, 'hrs', [128, 512], F32), Trk('hrs'))

    def attn_prep(self, layer):
        k = self.k
        j = layer // 3
        lam_init = 0.8 - 0.6 * math.exp(-0.3 * layer)
        with ExitStack() as es:
            lt = self.tmp(es, 'ltab', [128, 512], F32)
            lt_t = self.ptrk('ltab')
            k.dma(lt[:], self.d['lamtab'][:, :], writes=[lt_t])
            pr = self.tmp(es, 'lpr', [128, 2, 64], F32)
            pr_t = Trk('lpr')
            sm = self.tmp(es, 'lsm', [128, 2], F32)
            sm_t = Trk('lsm')
            base = j * 256
            lq = lt[:, base:base + 256].rearrange("p (a r n) -> p a r n", a=2, r=2)
            k.emit('dve', lambda e: e.tensor_tensor(out=pr[:], in0=lq[:, :, 0, :], in1=lq[:, :, 1, :], op=ALU.mult), [lt_t], [pr_t])
            k.emit('dve', lambda e: e.reduce_sum(out=sm[:], in_=pr[:], axis=AX.X), [pr_t], [sm_t])
            k.emit('act', lambda e: e.activation(out=sm[:], in_=sm[:], func=AF.Exp), [sm_t], [sm_t])
            k.emit('dve', lambda e: e.tensor_tensor(out=self.AV[:, 0:1], in0=sm[:, 0:1], in1=sm[:, 1:2], op=ALU.subtract), [sm_t], [self.AV_t])
            k.emit('dve', lambda e: e.tensor_scalar(out=self.AV[:, 0:1], in0=self.AV[:, 0:1], scalar1=float(lam_init), scalar2=None, op0=ALU.add),
                   [self.AV_t], [self.AV_t])
            k.emit('dve', lambda e: e.tensor_scalar(out=self.AV[:, 1:2], in0=self.AV[:, 0:1], scalar1=-1.0, scalar2=None, op0=ALU.mult),
                   [self.AV_t], [self.AV_t])
            osl, _ = PC['subln']
            k.emit('dve', lambda e: e.tensor_scalar(out=self.AV[:, 2:3], in0=self.PT[:, osl + j:osl + j + 1],
                                                    scalar1=float((1.0 - lam_init) * math.sqrt(128.0)), scalar2=None, op0=ALU.mult),
                   [self.PT_t, self.AV_t], [self.AV_t])
            k.barrier()

    def attn(self, layer, half):
        k, d, nc = self.k, self.d, self.nc
        j = layer // 3
        w_in = d['attn_w_in']
        scale = 0.125
        nkc = 2 if half == 0 else 12
        with ExitStack() as es:
            GH = 2
            V = self.tmp(es, 'aV', [128, 8, GH * 128], F32R)
            V_t = self.ptrk('aV', 8)
            ntm = self.norm_tmps(es)
            agrp = self.tmp(es, 'agrp', [128, GH, TOK], F32R)
            agrp_t = trks('agrp', GH)
            QT = [self.tmp(es, f'aQ{i}', [128, TOK], F32R) for i in range(1)]
            QT_t = trks('aQ', 1)
            KT = [self.tmp(es, f'aK{i}', [128, TOK], F32R) for i in range(1)]
            KT_t = self.ptrk('aK', 1)
            Pt = [self.tmp(es, f'aP{i}', [128, 512], F32R) for i in range(2)]
            Pt_t = trks('aP', 2)
            att = self.tmp(es, 'att', [128, TOK], F32)
            att_t = Trk('att')
            Rr = self.tmp(es, 'aR', [128, 2, 512], F32)
            Rr_t = Trk('aR')
            Tt, Tt_t = Rr, Rr_t
            if half == 1:
                ropet = self.tmp(es, 'arope', [128, 2048], F32)
                ropet_t = self.ptrk('arope')
                k.dma(ropet[:], d['rope'][:, :], writes=[ropet_t])
                COS = ropet[:, 0:1024]
                SIN = ropet[:, 1024:2048]
                raw = [self.tmp(es, f'araw{i}', [128, 512], F32R) for i in range(1)]
                raw_t = trks('araw', 1)
                ri_ = 0
                t1 = self.tmp(es, 'at1', [128, 512], F32)
                t1_t = Trk('at1')
                kcr = self.tmp(es, 'akcr', [128, 512], F32R)
                kcr_t = Trk('akcr')
                vcr = self.tmp(es, 'avcr', [128, 4 * GH * 128], F32R)
                vcr_t = Trk('avcr')
            pi = 0
            pb = 0
            for grp in range(8 // GH):
                c0 = 2 * D + grp * 256
                gpf = PF(self, [[w_in[j, :, c0:c0 + 256]]] + [[w_in[j, :, (grp * GH + t) * 128:(grp * GH + t + 1) * 128],
                                                              w_in[j, :, D + (grp * GH + t) * 128:D + (grp * GH + t + 1) * 128]] for t in range(GH)])
                for piece in range(1):
                    (wv,), wvt = gpf.get(0)
                    for tile in range(8):
                        b = 6 + (pb % 2)
                        pb += 1
                        ps, pst = self.ps[b], self.ps_t[b]
                        for c in range(NCH):
                            k.emit('pe', lambda e, ps=ps, c=c, tile=tile: e.matmul(
                                ps[:, 0:256], self.hT[:, c, tile * 128:(tile + 1) * 128], wv[:, c, :],
                                start=(c == 0), stop=(c == NCH - 1)), [wvt, self.hT_t[c][tile // 4]], [pst])
                        self.copy(self.evac_eng(), V[:, tile, piece * 256:(piece + 1) * 256], ps[:, 0:256], [pst], [V_t[tile]])
                if half == 0:
                    for tile in range(8):
                        k.dma(d['vout'][j, tile * 128:(tile + 1) * 128, grp * 256:(grp + 1) * 256], V[:, tile, :].bitcast(F32), reads=[V_t[tile]])
                else:
                    (vcv,), _ = self.wpiece([d['cv'][j, :, grp * 256:(grp + 1) * 256]], dest=(vcr, vcr_t))
                for hl in range(GH):
                    hh = grp * GH + hl
                    qi = 0
                    Q, Q_t, Kk, K_t = QT[qi], QT_t[qi], KT[qi], KT_t[qi]
                    (wq, wk), wt = gpf.get(1 + hl)
                    for wi, (w, dst, dst_t) in enumerate(((wq, Q, Q_t), (wk, Kk, K_t))):
                        for tt in range(2):
                            sl = slice(tt * 512, (tt + 1) * 512)
                            b = 6 + (pb % 2)
                            pb += 1
                            ps, pst = self.ps[b], self.ps_t[b]
                            for c in range(NCH):
                                k.emit('pe', lambda e, ps=ps, w=w, c=c, sl=sl: e.matmul(
                                    ps[:], w[:, c, :], self.hT[:, c, sl],
                                    start=(c == 0), stop=(c == NCH - 1)), [wt, self.hT_t[c][tt]], [pst])
                            if half == 0:
                                self.copy(self.evac_eng(), dst[:, sl], ps[:], [pst], [dst_t])
                            else:
                                rw, rw_t = raw[0], raw_t[0]
                                ri_ += 1
                                self.copy('act', rw[:], ps[:], [pst], [rw_t])
                                b2 = 6 + (pb % 2)
                                pb += 1
                                ps2, ps2t = self.ps[b2], self.ps_t[b2]
                                k.emit('pe', lambda e, ps2=ps2, rw=rw: e.matmul(ps2[:], self.permR[:], rw[:], start=True, stop=True),
                                       [self.permR_t, rw_t], [ps2t])
                                k.emit('pool', lambda e, rw=rw, sl=sl: e.tensor_tensor(out=t1[:], in0=rw[:].bitcast(F32), in1=COS[:, sl], op=ALU.mult),
                                       [rw_t, ropet_t], [t1_t])
                                k.emit('dve', lambda e, ps2=ps2, sl=sl, dst=dst: e.tensor_tensor(out=dst[:, sl], in0=ps2[:], in1=SIN[:, sl], op=ALU.mult),
                                       [ps2t, ropet_t], [dst_t])
                                k.emit('dve', lambda e, dst=dst, sl=sl: e.tensor_tensor(out=dst[:, sl], in0=dst[:, sl].bitcast(F32), in1=t1[:], op=ALU.add),
                                       [t1_t, dst_t], [dst_t])
                    if half == 0:
                        k.dma(d['kout'][j, hh, :, :], Kk[:].bitcast(F32), reads=[K_t])
                    else:
                        self.wpiece([d['ck'][j, hh, :, :]], dest=(kcr, kcr_t))

                    def keyT(comp, kc, s=0):
                        r = slice(comp * 64, (comp + 1) * 64)
                        if half == 0:
                            return Kk[r, s * 256 + kc * 128: s * 256 + (kc + 1) * 128], K_t
                        if kc < 4:
                            return kcr[r, kc * 128:(kc + 1) * 128], kcr_t
                        return Kk[r, (kc - 4) * 128:(kc - 3) * 128], K_t

                    def valT(kc, s=0):
                        cs = slice(hl * 128, (hl + 1) * 128)
                        if half == 0:
                            return V[:, s * 2 + kc, cs], V_t[s * 2 + kc]
                        if kc < 4:
                            return vcv[:, kc, cs], vcr_t
                        return V[:, kc - 4, cs], V_t[kc - 4]

                    if half == 0:
                        qw = 256
                        units = [(s, 0) for s in range(4)]
                    else:
                        qw = 512
                        units = [(0, qt) for qt in range(2)]
                    for (s, qt) in units:
                        q0 = s * 256 if half == 0 else qt * 512
                        for comp in range(2):
                            r = slice(comp * 64, (comp + 1) * 64)
                            if half == 0:
                                ob, zb = 2, 3
                                osl = slice(comp * 256, (comp + 1) * 256)
                            else:
                                ob, zb = 2 + comp * 2, 3 + comp * 2
                                osl = slice(0, 512)
                            pO, pO_t = self.ps[ob], self.ps_t[ob]
                            pZ, pZ_t = self.ps[zb], self.ps_t[zb]
                            if half == 0:
                                sb_ = pi % 2
                                pS, pS_t = self.ps[sb_], self.ps_t[sb_]
                                P_, P_t = Pt[pi % 2], Pt_t[pi % 2]
                                pi += 1
                                for kc in range(2):
                                    kl, kl_t = keyT(comp, kc, s)
                                    k.emit('pe', lambda e, pS=pS, kl=kl, kc=kc, r=r, q0=q0: e.matmul(
                                        pS[:, kc * 256:(kc + 1) * 256], kl, Q[r, q0:q0 + 256], start=True, stop=True),
                                        [kl_t, Q_t], [pS_t])
                                k.emit('act', lambda e, pS=pS, P_=P_: e.activation(out=P_[:], in_=pS[:], func=AF.Exp, scale=scale), [pS_t], [P_t])
                                for kc in range(2):
                                    vl, vl_t = valT(kc, s)
                                    k.emit('pe', lambda e, pO=pO, vl=vl, P_=P_, kc=kc, osl=osl: e.matmul(
                                        pO[:, osl], vl, P_[:, kc * 256:(kc + 1) * 256], start=(kc == 0), stop=(kc == 1)),
                                        [vl_t, P_t], [pO_t])
                                for kc in range(2):
                                    k.emit('pe', lambda e, pZ=pZ, P_=P_, kc=kc, osl=osl: e.matmul(
                                        pZ[:, osl], self.onesR[:], P_[:, kc * 256:(kc + 1) * 256], start=(kc == 0), stop=(kc == 1)),
                                        [self.onesR_t, P_t], [pZ_t])
                            else:
                                for kc in range(nkc):
                                    sb_ = pi % 2
                                    pS, pS_t = self.ps[sb_], self.ps_t[sb_]
                                    P_, P_t = Pt[pi % 2], Pt_t[pi % 2]
                                    pi += 1
                                    kl, kl_t = keyT(comp, kc)
                                    k.emit('pe', lambda e, pS=pS, kl=kl, r=r, q0=q0: e.matmul(
                                        pS[:], kl, Q[r, q0:q0 + 512], start=True, stop=True), [kl_t, Q_t], [pS_t])
                                    k.emit('act', lambda e, pS=pS, P_=P_: e.activation(out=P_[:], in_=pS[:], func=AF.Exp, scale=scale), [pS_t], [P_t])
                                    vl, vl_t = valT(kc)
                                    k.emit('pe', lambda e, pO=pO, vl=vl, P_=P_, kc=kc: e.matmul(
                                        pO[:], vl, P_[:], start=(kc == 0), stop=(kc == nkc - 1)), [vl_t, P_t], [pO_t])
                                    k.emit('pe', lambda e, pZ=pZ, P_=P_, kc=kc: e.matmul(
                                        pZ[:], self.onesR[:], P_[:], start=(kc == 0), stop=(kc == nkc - 1)), [self.onesR_t, P_t], [pZ_t])
                        if half == 0:
                            pO, pO_t, pZ, pZ_t = self.ps[2], self.ps_t[2], self.ps[3], self.ps_t[3]
                            k.emit('dve', lambda e, pZ=pZ: e.reciprocal(out=Rr[:, 0, :], in_=pZ[:]), [pZ_t], [Rr_t])
                            k.emit('dve', lambda e, pO=pO: e.tensor_tensor(out=Tt[:, 0, :], in0=pO[:], in1=Rr[:, 0, :], op=ALU.mult), [pO_t, Rr_t], [Tt_t])
                            k.emit('dve', lambda e, q0=q0: e.scalar_tensor_tensor(
                                out=att[:, q0:q0 + 256], in0=Tt[:, 0, 256:512], scalar=self.AV[:, 1:2], in1=Tt[:, 0, 0:256],
                                op0=ALU.mult, op1=ALU.add), [Tt_t, self.AV_t], [att_t])
                        else:
                            for comp in range(2):
                                pO, pO_t = self.ps[2 + comp * 2], self.ps_t[2 + comp * 2]
                                pZ, pZ_t = self.ps[3 + comp * 2], self.ps_t[3 + comp * 2]
                                k.emit('dve', lambda e, pZ=pZ, comp=comp: e.reciprocal(out=Rr[:, comp, :], in_=pZ[:]), [pZ_t], [Rr_t])
                                k.emit('dve', lambda e, pO=pO, comp=comp: e.tensor_tensor(out=Tt[:, comp, :], in0=pO[:], in1=Rr[:, comp, :], op=ALU.mult),
                                       [pO_t, Rr_t], [Tt_t])
                            k.emit('dve', lambda e, q0=q0: e.scalar_tensor_tensor(
                                out=att[:, q0:q0 + 512], in0=Tt[:, 1, :], scalar=self.AV[:, 1:2], in1=Tt[:, 0, :],
                                op0=ALU.mult, op1=ALU.add), [Tt_t, self.AV_t], [att_t])
                    self.head_norm(ntm, att[:], att_t, agrp[:, hl, :], agrp_t[hl], self.AV[:, 2:3], 128.0 * 1e-5, 6 + (pb % 2))
                    pb += 1
                self.out_proj(d['attn_w_out'][j, grp * GH * 128:(grp + 1) * GH * 128, :], agrp, agrp_t, GH, half, [6, 7, 0, 1])
            k.barrier()

    def hgrn_prep(self, layer):
        k = self.k
        ol, _ = PC['hgrn_lb']
        self.HV = self.k.sb('HV', [128, 3, 8])
        self.HV_t = Trk('HV')
        with ExitStack() as es:
            ex = self.tmp(es, 'hex', [128, 4, 8], F32)
            ex_t = Trk('hex')
            tot = self.tmp(es, 'htot', [128, 8], F32)
            tot_t = Trk('htot')
            k.emit('act', lambda e: e.activation(out=ex[:], in_=self.PT[:, ol:ol + 32].rearrange("p (l c) -> p l c", l=4), func=AF.Exp),
                   [self.PT_t], [ex_t])
            k.emit('dve', lambda e: e.tensor_tensor(out=tot[:], in0=ex[:, 0, :], in1=ex[:, 1, :], op=ALU.add), [ex_t], [tot_t])
            for l in (2, 3):
                k.emit('dve', lambda e, l=l: e.tensor_tensor(out=tot[:], in0=tot[:], in1=ex[:, l, :], op=ALU.add), [ex_t, tot_t], [tot_t])
            k.emit('dve', lambda e: e.reciprocal(out=tot[:], in_=tot[:]), [tot_t], [tot_t])
            k.emit('dve', lambda e: e.tensor_copy(out=self.HV[:, 0, :], in_=ex[:, 1, :]), [ex_t], [self.HV_t])
            for l in range(2, layer + 1):
                k.emit('dve', lambda e, l=l: e.tensor_tensor(out=self.HV[:, 0, :], in0=self.HV[:, 0, :], in1=ex[:, l, :], op=ALU.add),
                       [ex_t, self.HV_t], [self.HV_t])
            k.emit('dve', lambda e: e.tensor_tensor(out=self.HV[:, 0, :], in0=self.HV[:, 0, :], in1=tot[:], op=ALU.mult), [tot_t, self.HV_t], [self.HV_t])
            k.emit('dve', lambda e: e.tensor_scalar(out=self.HV[:, 1, :], in0=self.HV[:, 0, :], scalar1=-1.0, scalar2=1.0, op0=ALU.mult, op1=ALU.add),
                   [self.HV_t], [self.HV_t])
            on, _ = PC['hgrn_norm']
            k.emit('dve', lambda e: e.tensor_scalar(out=self.HV[:, 2, 0:1], in0=self.PT[:, on:on + 1], scalar1=float(math.sqrt(128.0)), scalar2=None, op0=ALU.mult),
                   [self.PT_t, self.HV_t], [self.HV_t])
            k.barrier()

    def hgrn(self, layer, half):
        k, d, nc = self.k, self.d, self.nc
        j = layer // 3
        if half == 0:
            self.hgrn_prep(layer)
        w_in = d['hgrn_w_in']
        nseq = 4 if half == 0 else 1
        cps = 16 // nseq
        oo, _ = CC['ones']
        ONES = self.CT[:, oo:oo + 1].broadcast_to([128, TOK])
        oi, _ = CC['ident']
        IDENT = self.CT[:, oi:oi + 128]
        masks = []
        for nm in ('mask_f', 'mask_b'):
            om, _ = CC[nm]
            masks.append(self.CT[0:64, om:om + 256])
        with ExitStack() as es:
            V64 = self.tmp(es, 'hV', [64, 16, 128], F32)
            V64_t = trks('hV', 16)
            mix = self.tmp(es, 'hmix', [128, 1, TOK], F32R)
            mix_t = trks('hmix', 1)
            ntm = self.norm_tmps(es)
            qT = self.tmp(es, 'hq', [128, TOK], F32)
            qT_t = Trk('hq')
            gs, gs_t = qT, qT_t
            oT = self.tmp(es, 'ho', [128, TOK], F32)
            oT_t = Trk('ho')
            Fb = [self.tmp(es, f'hF{i}', [128, TOK], F32) for i in range(2)]
            Fb_t = trks('hF', 2)
            L = self.tmp(es, 'hL', [128, TOK], F32)
            L_t = Trk('hL')
            Gp = self.tmp(es, 'hGp', [128, 64 + TOK + 64], F32)
            Gp_t = Trk('hGp')
            E1 = self.tmp(es, 'hE1', [128, TOK], F32)
            E1_t = Trk('hE1')
            E2 = self.tmp(es, 'hE2', [128, TOK], F32)
            E2_t = Trk('hE2')
            Am = self.tmp(es, 'hAm', [64, 4, 64], F32)
            Am_t = Trk('hAm')
            Ktok = self.tmp(es, 'hKt', [64, 4, 128], F32)
            Ktok_t = Trk('hKt')
            Sb = [self.tmp(es, f'hS{i}', [128, 128], F32) for i in range(2)]
            Sb_t = trks('hS', 2)
            DK = self.tmp(es, 'hDK', [128, 3, 16], F32)
            DK_t = Trk('hDK')
            G3 = self.tmp(es, 'hG3', [128, 3, 16], F32)
            G3_t = Trk('hG3')
            Sp = [self.tmp(es, f'hSp{i}', [128, 128], F32) for i in range(4)]
            Sp_t = trks('hSp', 4)
            tS = self.tmp(es, 'htS', [128, 128], F32)
            tS_t = Trk('htS')
            spi = 0
            sout_t = self.ptrk('hso')
            if half == 0:
                sout = self.tmp(es, 'hso', [128, 4, 2, 128], F32)
            k.emit('dve', lambda e: e.memset(Gp[:], 0.0), [], [Gp_t])
            pb = 0
            for pair in range(4):
                for hl in range(2):
                    hh = pair * 2 + hl
                    hpf = PF(self, [[w_in[j, :, 3 * D + hh * 128:3 * D + (hh + 1) * 128]],
                                    [w_in[j, :, hh * 128:(hh + 1) * 128], w_in[j, :, D + hh * 128:D + (hh + 1) * 128]],
                                    [w_in[j, :, 2 * D + hh * 128:2 * D + (hh + 1) * 128]]])
                    (wi,), wit = hpf.get(0)
                    for tt in range(2):
                        sl = slice(tt * 512, (tt + 1) * 512)
                        b = 6 + (pb % 2)
                        pb += 1
                        ps, pst = self.ps[b], self.ps_t[b]
                        for c in range(NCH):
                            k.emit('pe', lambda e, ps=ps, c=c, sl=sl: e.matmul(ps[:], wi[:, c, :], self.hT[:, c, sl], start=(c == 0), stop=(c == NCH - 1)),
                                   [wit, self.hT_t[c][tt]], [pst])
                        self.copy(self.evac_eng(), L[:, sl], ps[:], [pst], [L_t])
                    for g4 in range(4):
                        b = 6 + (pb % 2)
                        pb += 1
                        ps, pst = self.ps[b], self.ps_t[b]
                        for q_ in range(4):
                            ch = g4 * 4 + q_
                            k.emit('pe', lambda e, ps=ps, q_=q_, ch=ch: e.transpose(ps[0:64, q_ * 128:(q_ + 1) * 128], L[:, ch * 64:(ch + 1) * 64], IDENT),
                                   [L_t, self.CT_t], [pst])
                        self.copy(self.evac_eng(), V64[:, g4 * 4:(g4 + 1) * 4, :].rearrange("p a n -> p (a n)"), ps[0:64, :], [pst], V64_t[g4 * 4:(g4 + 1) * 4])
                    lbc = self.HV[:, 0, hh:hh + 1]
                    omc = self.HV[:, 1, hh:hh + 1]
                    (wq, wzf), wt1 = hpf.get(1)
                    (wzb,), wt2 = hpf.get(2)
                    for (w, wt, kindp) in ((wq, wt1, 'q'), (wzf, wt1, 'zf'), (wzb, wt2, 'zb')):
                        for tt in range(2):
                            sl = slice(tt * 512, (tt + 1) * 512)
                            b = 6 + (pb % 2)
                            pb += 1
                            ps, pst = self.ps[b], self.ps_t[b]
                            for c in range(NCH):
                                k.emit('pe', lambda e, ps=ps, w=w, c=c, sl=sl: e.matmul(
                                    ps[:], w[:, c, :], self.hT[:, c, sl], start=(c == 0), stop=(c == NCH - 1)),
                                    [wt, self.hT_t[c][tt]], [pst])
                            if kindp == 'q':
                                k.emit('act', lambda e, ps=ps, sl=sl: e.activation(out=qT[:, sl], in_=ps[:], func=AF.Copy, scale=float(128.0 ** -0.5)),
                                       [pst], [qT_t])
                            elif kindp == 'g':
                                k.emit('act', lambda e, ps=ps, sl=sl: e.activation(out=gs[:, sl], in_=ps[:], func=AF.Silu), [pst], [gs_t])
                            else:
                                di = 0 if kindp == 'zf' else 1
                                k.emit('act', lambda e, ps=ps, sl=sl, di=di: e.activation(out=Fb[di][:, sl], in_=ps[:], func=AF.Sigmoid), [pst], [Fb_t[di]])
                    (wg,), wt3 = self.wpiece([w_in[j, :, 4 * D + hh * 128:4 * D + (hh + 1) * 128]])
                    opf = self.out_pf(d['hgrn_w_out'][j, hh * 128:(hh + 1) * 128, :])
                    opf.h[0] = self.wpiece(opf.specs[0])
                    for di in range(2):
                        F_, F_t = Fb[di], Fb_t[di]
                        k.emit('dve', lambda e, F_=F_: e.tensor_scalar(out=F_[:], in0=F_[:], scalar1=omc, scalar2=lbc, op0=ALU.mult, op1=ALU.add),
                               [F_t, self.HV_t], [F_t])
                        k.emit('act', lambda e, F_=F_: e.activation(out=L[:], in_=F_[:], func=AF.Ln), [F_t], [L_t])
                        k.emit('pool', lambda e, F_=F_: e.tensor_scalar(out=F_[:], in0=F_[:], scalar1=-1.0, scalar2=1.0, op0=ALU.mult, op1=ALU.add),
                               [F_t], [F_t])
                        k.emit('dve', lambda e: e.tensor_tensor_scan(out=Gp[:, 64:64 + TOK], data0=ONES, data1=L[:], initial=0.0,
                                                                     op0=ALU.mult, op1=ALU.add), [L_t, self.CT_t], [Gp_t])
                        Lv = L[:].rearrange("p (j n) -> p j n", n=64)
                        if di == 0:
                            gprev = Gp[:, 63:63 + TOK].rearrange("p (j n) -> p j n", n=64)[:, :, 0:1].broadcast_to([128, 16, 64])
                            gcur = Gp[:, 64:64 + TOK].rearrange("p (j n) -> p j n", n=64)
                            k.emit('dve', lambda e: e.tensor_tensor(out=Lv, in0=gcur, in1=gprev, op=ALU.subtract), [Gp_t], [L_t])
                        else:
                            gend = Gp[:, 127:127 + TOK].rearrange("p (j n) -> p j n", n=64)[:, :, 0:1].broadcast_to([128, 16, 64])
                            gsh = Gp[:, 63:63 + TOK].rearrange("p (j n) -> p j n", n=64)
                            k.emit('dve', lambda e: e.tensor_tensor(out=Lv, in0=gend, in1=gsh, op=ALU.subtract), [Gp_t], [L_t])
                        pos = 63 if di == 0 else 0
                        mid = 31 if di == 0 else 32
                        k.emit('pool', lambda e, pos=pos: e.tensor_copy(out=G3[:, 0, :], in_=Lv[:, :, pos]), [L_t], [G3_t])
                        k.emit('pool', lambda e, mid=mid: e.tensor_copy(out=G3[:, 1, :], in_=Lv[:, :, mid]), [L_t], [G3_t])
                        k.emit('dve', lambda e: e.tensor_tensor(out=G3[:, 2, :], in0=G3[:, 0, :], in1=G3[:, 1, :], op=ALU.subtract), [G3_t], [G3_t])
                        k.emit('act', lambda e: e.activation(out=DK[:], in_=G3[:], func=AF.Exp), [G3_t], [DK_t])
                        k.emit('dve', lambda e: e.tensor_tensor(out=Lv, in0=Lv, in1=G3[:, 1, :].unsqueeze(2).broadcast_to([128, 16, 64]), op=ALU.subtract),
                               [L_t, G3_t], [L_t])
                        k.emit('act', lambda e: e.activation(out=E1[:], in_=L[:], func=AF.Exp), [L_t], [E1_t])
                        k.emit('act', lambda e: e.activation(out=E2[:], in_=L[:], func=AF.Exp, scale=-1.0), [L_t], [E2_t])
                        k.emit('dve', lambda e: e.tensor_tensor(out=E1[:], in0=E1[:], in1=qT[:], op=ALU.mult), [E1_t, qT_t], [E1_t])
                        k.emit('pool', lambda e, F_=F_: e.tensor_tensor(out=E2[:], in0=E2[:], in1=F_[:], op=ALU.mult), [E2_t, F_t], [E2_t])
                        order = list(range(16)) if di == 0 else list(range(15, -1, -1))
                        if half == 1:
                            k.dma(Sb[0][:], d['st_hgrn'][di, hh, :, :], writes=[Sb_t[0]])
                        si = 0
                        psA, psA_t = self.ps[0], self.ps_t[0]
                        psT, psT_t = self.ps[1], self.ps_t[1]
                        psO, psO_t = self.ps[2], self.ps_t[2]

                        def step1(gi, di=di, order=order):
                            chs = order[gi * 4:(gi + 1) * 4]
                            lo = min(chs)
                            for ch in chs:
                                cs = slice(ch * 64, (ch + 1) * 64)
                                q_ = ch - lo
                                k.emit('pe', lambda e, cs=cs, q_=q_: e.matmul(psA[0:64, q_ * 64:(q_ + 1) * 64], E2[:, cs], E1[:, cs], start=True, stop=True),
                                       [E1_t, E2_t], [psA_t])
                                k.emit('pe', lambda e, cs=cs, q_=q_: e.transpose(psT[0:64, q_ * 128:(q_ + 1) * 128], E2[:, cs], IDENT),
                                       [E2_t, self.CT_t], [psT_t])
                            k.emit('dve', lambda e, di=di: e.tensor_tensor(out=Am[:].rearrange("p a n -> p (a n)"), in0=psA[0:64, 0:256], in1=masks[di], op=ALU.mult),
                                   [psA_t, self.CT_t], [Am_t])
                            k.emit('act', lambda e: e.activation(out=Ktok[:].rearrange("p a n -> p (a n)"), in_=psT[0:64, :], func=AF.Copy), [psT_t], [Ktok_t])

                        step1(0)
                        for gi in range(4):
                            chs = order[gi * 4:(gi + 1) * 4]
                            lo = min(chs)
                            bS = 3 + (gi % 2)
                            psS, psS_t = self.ps[bS], self.ps_t[bS]
                            info = []
                            for ch in chs:
                                loc = ch % cps
                                first = (loc == 0) if di == 0 else (loc == cps - 1)
                                last = (loc == cps - 1) if di == 0 else (loc == 0)
                                info.append((ch, ch - lo, first and half == 0, last, ch // cps))
                            for (ch, q_, zi, last, seq) in info:
                                k.emit('pe', lambda e, q_=q_, ch=ch: e.matmul(psS[:, q_ * 128:(q_ + 1) * 128], Ktok[:, q_, :], V64[:, ch, :], start=True, stop=True),
                                       [Ktok_t, V64_t[ch]], [psS_t])
                            for idx, (ch, q_, zi, last, seq) in enumerate(info):
                                k.emit('pe', lambda e, q_=q_, ch=ch, idx=idx, zi=zi: e.matmul(
                                    psO[:, q_ * 64:(q_ + 1) * 64], V64[:, ch, :], Am[:, q_, :], start=(idx == 0), stop=zi), [V64_t[ch], Am_t], [psO_t])
                            sps = {}
                            for (ch, q_, zi, last, seq) in info:
                                S_prev, S_prev_t = Sb[si], Sb_t[si]
                                if not zi:
                                    sp_, sp_t = Sp[q_], Sp_t[q_]
                                    k.emit('act', lambda e, sp_=sp_, S_prev=S_prev, ch=ch: e.activation(
                                        out=sp_[:], in_=S_prev[:], func=AF.Identity, scale=DK[:, 1, ch:ch + 1]), [S_prev_t, DK_t], [sp_t])
                                    sps[ch] = (sp_, sp_t)
                                if last and half == 0:
                                    dstS, dstS_t = sout[:, seq, di, :], sout_t
                                else:
                                    si = 1 - si
                                    dstS, dstS_t = Sb[si][:], Sb_t[si]
                                reg = psS[:, q_ * 128:(q_ + 1) * 128]
                                if zi:
                                    k.emit('act', lambda e, reg=reg, ch=ch, dstS=dstS: e.activation(
                                        out=dstS, in_=reg, func=AF.Identity, scale=DK[:, 2, ch:ch + 1]), [psS_t, DK_t], [dstS_t])
                                else:
                                    k.emit('act', lambda e, reg=reg, ch=ch: e.activation(
                                        out=tS[:], in_=reg, func=AF.Identity, scale=DK[:, 2, ch:ch + 1]), [psS_t, DK_t], [tS_t])
                                    k.emit('dve', lambda e, dstS=dstS, S_prev=S_prev, ch=ch: e.scalar_tensor_tensor(
                                        out=dstS, in0=S_prev[:], scalar=DK[:, 0, ch:ch + 1], in1=tS[:], op0=ALU.mult, op1=ALU.add),
                                        [S_prev_t, DK_t, tS_t], [dstS_t])
                            if gi + 1 < 4:
                                step1(gi + 1)
                            for (ch, q_, zi, last, seq) in info:
                                if zi:
                                    continue
                                sp_, sp_t = sps[ch]
                                cs = slice(ch * 64, (ch + 1) * 64)
                                k.emit('pe', lambda e, cs=cs, q_=q_, sp_=sp_: e.matmul(
                                    psO[:, q_ * 64:(q_ + 1) * 64], sp_[:], E1[:, cs], start=False, stop=True), [sp_t, E1_t], [psO_t])
                            osl = slice(lo * 64, lo * 64 + 256)
                            if di == 0:
                                k.emit('dve', lambda e, osl=osl: e.tensor_copy(out=oT[:, osl], in_=psO[:, 0:256]), [psO_t], [oT_t])
                            else:
                                k.emit('dve', lambda e, osl=osl: e.tensor_tensor(out=oT[:, osl], in0=oT[:, osl], in1=psO[:, 0:256], op=ALU.add), [psO_t, oT_t], [oT_t])
                    if half == 0:
                        k.dma(d['hgout'][:, :, hh, :, :].rearrange("s d k e -> k s d e"), sout[:], reads=[sout_t])
                    for tt in range(2):
                        sl = slice(tt * 512, (tt + 1) * 512)
                        ps, pst = self.ps[6 + tt], self.ps_t[6 + tt]
                        for c in range(NCH):
                            k.emit('pe', lambda e, ps=ps, c=c, sl=sl: e.matmul(ps[:], wg[:, c, :], self.hT[:, c, sl], start=(c == 0), stop=(c == NCH - 1)),
                                   [wt3, self.hT_t[c][tt]], [pst])
                        k.emit('act', lambda e, ps=ps, sl=sl: e.activation(out=gs[:, sl], in_=ps[:], func=AF.Silu), [pst], [gs_t])
                    self.head_norm(ntm, oT[:], oT_t, mix[:, 0, :], mix_t[0], self.HV[:, 2, 0:1], 128.0 * 1e-6, 5, extra_mul=gs[:], extra_t=gs_t)
                    self.out_proj(d['hgrn_w_out'][j, hh * 128:(hh + 1) * 128, :], mix, mix_t, 1, half, [6, 7], pf=opf)
            k.barrier()

    def gdn(self, layer, half):
        k, d, nc = self.k, self.d, self.nc
        w_in = d['gdn_w_in']
        nseq = 4 if half == 0 else 1
        cps = 16 // nseq
        seqlen = TOK // nseq
        CT = self.CT

        def cc(name, rows=64, w=None):
            o_, w_ = CC[name]
            return CT[0:rows, o_:o_ + (w or w_)]

        ONES_ROW = cc('ones', 128, 1).broadcast_to([128, TOK])
        IDENT = cc('ident', 128)
        ID64 = cc('ident', 64, 64)
        TRIF = cc('mask_f', 64, 64)
        TRIB = cc('mask_b', 64, 64)
        ONES64 = cc('ones', 64, 64)
        NEGU = [cc('negu_f'), cc('negu_b')]
        NEGL = [cc('negl_f'), cc('negl_b')]
        SNEG = [cc('sneg_f'), cc('sneg_b')]
        IDREP = cc('idrep')
        osel, _ = CC['sel']
        onsel, _ = CC['nsel']

        def SEL(kk, m):
            return CT[0:4, osel + kk * 128: osel + kk * 128 + m]

        def NSEL(kk):
            return CT[0:4, onsel + kk * 64: onsel + (kk + 1) * 64]

        oa, _ = PC['gdn_alog_col']
        odt, _ = PC['gdn_dt_col']
        oar, _ = PC['gdn_alog_row']
        odr, _ = PC['gdn_dt_row']
        ocv, _ = PC['gdn_conv']
        ogn, _ = PC['gdn_norm']
        scr_t = self.ptrk('gscr')
        with ExitStack() as es0:
            GC = self.tmp(es0, 'gGC', [64, 16, 16]); BE = self.tmp(es0, 'gBE', [64, 16, 16])
            NBE = self.tmp(es0, 'gNBE', [64, 16, 16]); C1 = self.tmp(es0, 'gC1', [64, 16, 16])
            WW = self.tmp(es0, 'gWW', [64, 16, 16])
            TB_t = Trk('gTB')
            GV = self.tmp(es0, 'gGV', [128, 20])
            GV_t = Trk('gGV')
            k.emit('act', lambda e: e.activation(out=GV[:, 0:1], in_=self.PT[:, oa:oa + 1], func=AF.Exp), [self.PT_t], [GV_t])
            k.emit('dve', lambda e: e.tensor_scalar(out=GV[:, 0:1], in0=GV[:, 0:1], scalar1=-1.0, scalar2=None, op0=ALU.mult), [GV_t], [GV_t])
            k.emit('act', lambda e: e.activation(out=GV[:, 4:20], in_=self.PT[:, oar:oar + 16], func=AF.Exp), [self.PT_t], [GV_t])
            k.emit('dve', lambda e: e.tensor_scalar(out=GV[:, 4:20], in0=GV[:, 4:20], scalar1=-1.0, scalar2=None, op0=ALU.mult), [GV_t], [GV_t])
            k.emit('dve', lambda e: e.tensor_scalar(out=GV[:, 1:2], in0=self.PT[:, ogn:ogn + 1], scalar1=float(math.sqrt(128.0)), scalar2=None, op0=ALU.mult),
                   [self.PT_t, GV_t], [GV_t])
            (wab,), wab_t = self.wpiece([w_in[0, :, 4 * D:4 * D + 32]], rounded=False)
            with ExitStack() as es:
                LA = self.tmp(es, 'gLA', [16, TOK]); LA_t = Trk('gLA')
                Gp = self.tmp(es, 'gGp', [16, 64 + TOK + 64]); Gp_t = Trk('gGp')
                GF = self.tmp(es, 'gGF', [16, TOK]); GF_t = self.ptrk('gGF')
                GB = self.tmp(es, 'gGB', [16, TOK]); GB_t = self.ptrk('gGB')
                BT = self.tmp(es, 'gBT', [16, TOK]); BT_t = self.ptrk('gBT')
                LAt = self.tmp(es, 'gLAt', [64, 16, 16]); LAt_t = Trk('gLAt')
                k.emit('dve', lambda e: e.memset(Gp[:], 0.0), [], [Gp_t])
                hTf = self.hT[:].bitcast(F32)
                for part in range(2):
                    for tt in range(2):
                        sl = slice(tt * 512, (tt + 1) * 512)
                        ps, pst = self.ps[6 + tt], self.ps_t[6 + tt]
                        for c in range(NCH):
                            k.emit('pe', lambda e, ps=ps, c=c, sl=sl, part=part: e.matmul(
                                ps[0:16, :], wab[:, c, part * 16:(part + 1) * 16], hTf[:, c, sl], start=(c == 0), stop=(c == NCH - 1)),
                                [wab_t, self.hT_t[c][tt]], [pst])
                        if part == 0:
                            k.emit('act', lambda e, ps=ps, sl=sl: e.activation(out=LA[:, sl], in_=ps[0:16, :], func=AF.Exp, bias=self.PT[0:16, odt:odt + 1]),
                                   [pst, self.PT_t], [LA_t])
                        else:
                            k.emit('act', lambda e, ps=ps, sl=sl: e.activation(out=BT[:, sl], in_=ps[0:16, :], func=AF.Sigmoid), [pst], [BT_t])
                k.emit('act', lambda e: e.activation(out=LA[:], in_=LA[:], func=AF.Ln, bias=1.0), [LA_t], [LA_t])
                k.emit('dve', lambda e: e.tensor_scalar(out=LA[:], in0=LA[:], scalar1=GV[0:16, 0:1], scalar2=None, op0=ALU.mult), [LA_t, GV_t], [LA_t])
                k.emit('dve', lambda e: e.tensor_tensor_scan(out=Gp[:, 64:64 + TOK], data0=ONES_ROW[0:16, :], data1=LA[:], initial=0.0,
                                                             op0=ALU.mult, op1=ALU.add), [LA_t, self.CT_t], [Gp_t])
                gprev = Gp[:, 63:63 + TOK].rearrange("p (j n) -> p j n", n=64)[:, :, 0:1].broadcast_to([16, 16, 64])
                gcur = Gp[:, 64:64 + TOK].rearrange("p (j n) -> p j n", n=64)
                k.emit('dve', lambda e: e.tensor_tensor(out=GF[:].rearrange("p (j n) -> p j n", n=64), in0=gcur, in1=gprev, op=ALU.subtract), [Gp_t], [GF_t])
                gend = Gp[:, 127:127 + TOK].rearrange("p (j n) -> p j n", n=64)[:, :, 0:1].broadcast_to([16, 16, 64])
                gsh = Gp[:, 63:63 + TOK].rearrange("p (j n) -> p j n", n=64)
                k.emit('dve', lambda e: e.tensor_tensor(out=GB[:].rearrange("p (j n) -> p j n", n=64), in0=gend, in1=gsh, op=ALU.subtract), [Gp_t], [GB_t])
                k.dma(d['gscr'][0:16, :], GF[:], reads=[GF_t], writes=[scr_t])
                k.dma(d['gscr'][16:32, :], GB[:], reads=[GB_t], writes=[scr_t])
                k.dma(d['gscr'][32:48, :], BT[:], reads=[BT_t], writes=[scr_t])
                ps, pst = self.ps[5], self.ps_t[5]
                for ch in range(16):
                    for c in range(NCH):
                        k.emit('pe', lambda e, c=c, ch=ch: e.matmul(
                            ps[0:64, ch * 32:(ch + 1) * 32], hTf[:, c, ch * 64:(ch + 1) * 64], wab[:, c, :], start=(c == 0), stop=(c == NCH - 1)),
                            [wab_t, self.hT_t[c][ch // 8]], [pst])
                pv = ps[0:64, :].rearrange("p (c n) -> p c n", n=32)
                k.emit('dve', lambda e: e.tensor_tensor(out=LAt[:], in0=pv[:, :, 0:16],
                                                        in1=self.PT[0:64, odr:odr + 16].unsqueeze(1).broadcast_to([64, 16, 16]), op=ALU.add),
                       [pst, self.PT_t], [LAt_t])
                k.emit('act', lambda e: e.activation(out=BE[:], in_=pv[:, :, 16:32], func=AF.Sigmoid), [pst], [TB_t])
                k.emit('act', lambda e: e.activation(out=LAt[:], in_=LAt[:], func=AF.Exp), [LAt_t], [LAt_t])
                k.emit('act', lambda e: e.activation(out=LAt[:], in_=LAt[:], func=AF.Ln, bias=1.0), [LAt_t], [LAt_t])
                k.emit('dve', lambda e: e.tensor_tensor(out=LAt[:], in0=LAt[:], in1=GV[0:64, 4:20].unsqueeze(1).broadcast_to([64, 16, 16]), op=ALU.mult),
                       [LAt_t, GV_t], [LAt_t])
                LAf = LAt[:].rearrange("p c n -> p (c n)")
                pF, pF_t = self.ps[0], self.ps_t[0]
                pB, pB_t = self.ps[1], self.ps_t[1]
                pT, pT_t = self.ps[2], self.ps_t[2]
                k.emit('pe', lambda e: e.matmul(pF[0:64, 0:256], TRIF, LAf, start=True, stop=True), [LAt_t, self.CT_t], [pF_t])
                k.emit('pe', lambda e: e.matmul(pB[0:64, 0:256], TRIB, LAf, start=True, stop=True), [LAt_t, self.CT_t], [pB_t])
                k.emit('pe', lambda e: e.matmul(pT[0:64, 0:256], ONES64, LAf, start=True, stop=True), [LAt_t, self.CT_t], [pT_t])
                pFv = pF[0:64, 0:256].rearrange("p (c n) -> p c n", n=16)
                pBv = pB[0:64, 0:256].rearrange("p (c n) -> p c n", n=16)
                pTv = pT[0:64, 0:256].rearrange("p (c n) -> p c n", n=16)
                k.emit('dve', lambda e: e.tensor_copy(out=GC[:, :, 0:8], in_=pFv[:, :, 0:8]), [pF_t], [TB_t])
                k.emit('dve', lambda e: e.tensor_copy(out=GC[:, :, 8:16], in_=pBv[:, :, 8:16]), [pB_t], [TB_t])
                k.emit('dve', lambda e: e.tensor_tensor(out=WW[:], in0=pTv, in1=GC[:], op=ALU.subtract), [pT_t, TB_t], [TB_t])
                k.emit('act', lambda e: e.activation(out=WW[:], in_=WW[:], func=AF.Exp), [TB_t], [TB_t])
                k.emit('act', lambda e: e.activation(out=C1[:], in_=GC[:], func=AF.Exp), [TB_t], [TB_t])
                k.emit('dve', lambda e: e.scalar_tensor_tensor(out=C1[:], in0=C1[:], scalar=-1.0, in1=BE[:], op0=ALU.mult, op1=ALU.mult), [TB_t], [TB_t])
                k.emit('dve', lambda e: e.tensor_scalar(out=NBE[:], in0=BE[:], scalar1=-1.0, scalar2=None, op0=ALU.mult), [TB_t], [TB_t])
                k.barrier()
            stop = self.cfg.get('gdn_stop', 9)
            if stop <= 1:
                return
            with ExitStack() as es:
                qn = self.tmp(es, 'gq', [128, TOK]); qn_t = Trk('gq')
                kn = self.tmp(es, 'gk', [128, TOK]); kn_t = Trk('gk')
                vT = self.tmp(es, 'gv', [128, TOK]); vT_t = Trk('gv')
                oT = self.tmp(es, 'go', [128, TOK]); oT_t = Trk('go')
                Qt = self.tmp(es, 'gQt', [128, TOK]); Qt_t = Trk('gQt')
                mix = self.tmp(es, 'gmix', [128, 1, TOK], F32R); mix_t = trks('gmix', 1)
                ntm = self.norm_tmps(es)
                HR4 = self.tmp(es, 'gHR', [4, TOK]); HR4_t = self.ptrk('gHR')
                bt = [self.tmp(es, f'gb{i}', [64, 4, 64]) for i in range(10)]
                bt_t = trks('gb', 10)
                ktok = self.tmp(es, 'gkt', [64, 4, 128]); ktok_t = Trk('gkt')
                vtok = self.tmp(es, 'gvt', [64, 4, 128]); vtok_t = Trk('gvt')
                sm = [self.tmp(es, f'gs{i}', [64, 128]) for i in range(4)]
                sm_t = trks('gs', 4)
                Sb = [self.tmp(es, f'gS{i}', [128, 128]) for i in range(2)]
                Sb_t = trks('gS', 2)
                DKg = self.tmp(es, 'gDK', [128, 16]); DKg_t = Trk('gDK')
                sout_t = self.ptrk('gso')
                if half == 0:
                    sout = self.tmp(es, 'gso', [128, 4, 2, 128])
                pb = 0
                for hh in range(8):
                    (wq, wk), wt1 = self.wpiece([w_in[0, :, hh * 128:(hh + 1) * 128], w_in[0, :, D + hh * 128:D + (hh + 1) * 128]])
                    (wv,), wt2 = self.wpiece([w_in[0, :, 2 * D + hh * 128:2 * D + (hh + 1) * 128]])
                    k.dma_group([(HR4[0:1, :], d['gscr'][hh:hh + 1, :]), (HR4[1:2, :], d['gscr'][24 + hh:25 + hh, :]),
                                 (HR4[2:3, :], d['gscr'][32 + hh:33 + hh, :]), (HR4[3:4, :], d['gscr'][40 + hh:41 + hh, :])], [HR4_t])
                    HR4_t.rs[scr_t.w[0]] = 0
                    for ti, (w, wt, dst, dst_t) in enumerate(((wq, wt1, qn, qn_t), (wk, wt1, kn, kn_t), (wv, wt2, vT, vT_t))):
                        fch = ti * 8 + hh
                        w0 = self.PT[:, ocv + 0 * 24 + fch: ocv + 0 * 24 + fch + 1]
                        w1 = self.PT[:, ocv + 1 * 24 + fch: ocv + 1 * 24 + fch + 1]
                        w2 = self.PT[:, ocv + 2 * 24 + fch: ocv + 2 * 24 + fch + 1]
                        pss = []
                        for tt in range(2):
                            sl = slice(tt * 512, (tt + 1) * 512)
                            b = 6 + tt
                            ps, pst = self.ps[b], self.ps_t[b]
                            pss.append((ps, pst))
                            for c in range(NCH):
                                k.emit('pe', lambda e, ps=ps, w=w, c=c, sl=sl: e.matmul(
                                    ps[:], w[:, c, :], self.hT[:, c, sl], start=(c == 0), stop=(c == NCH - 1)), [wt, self.hT_t[c][tt]], [pst])
                            k.emit('act', lambda e, ps=ps, sl=sl, dst=dst, w1=w1: e.activation(out=dst[:, sl], in_=ps[:], func=AF.Copy, scale=w1),
                                   [pst, self.PT_t], [dst_t])
                        for tt in range(2):
                            ps, pst = pss[tt]
                            sl_ = min(seqlen, 512)
                            ns = 512 // sl_
                            pv = ps[:].rearrange("p (s n) -> p s n", s=ns)
                            av = dst[:, tt * 512:(tt + 1) * 512].rearrange("p (s n) -> p s n", s=ns)
                            k.emit('dve', lambda e, pv=pv, av=av, w0=w0, sl_=sl_: e.scalar_tensor_tensor(
                                out=av[:, :, 1:sl_], in0=pv[:, :, 0:sl_ - 1], scalar=w0, in1=av[:, :, 1:sl_], op0=ALU.mult, op1=ALU.add),
                                [pst, self.PT_t, dst_t], [dst_t])
                            k.emit('dve', lambda e, pv=pv, av=av, w2=w2, sl_=sl_: e.scalar_tensor_tensor(
                                out=av[:, :, 0:sl_ - 1], in0=pv[:, :, 1:sl_], scalar=w2, in1=av[:, :, 0:sl_ - 1], op0=ALU.mult, op1=ALU.add),
                                [pst, self.PT_t, dst_t], [dst_t])
                        if seqlen > 512:
                            p0, p0t = pss[0]
                            p1, p1t = pss[1]
                            k.emit('dve', lambda e, p0=p0, dst=dst, w0=w0: e.scalar_tensor_tensor(
                                out=dst[:, 512:513], in0=p0[:, 511:512], scalar=w0, in1=dst[:, 512:513], op0=ALU.mult, op1=ALU.add),
                                [p0t, self.PT_t, dst_t], [dst_t])
                            k.emit('dve', lambda e, p1=p1, dst=dst, w2=w2: e.scalar_tensor_tensor(
                                out=dst[:, 511:512], in0=p1[:, 0:1], scalar=w2, in1=dst[:, 511:512], op0=ALU.mult, op1=ALU.add),
                                [p1t, self.PT_t, dst_t], [dst_t])
                        k.emit('act', lambda e, dst=dst: e.activation(out=dst[:], in_=dst[:], func=AF.Silu), [dst_t], [dst_t])
                        if ti < 2:
                            sq, sq_t, rs, rs_t = ntm
                            for tt in range(2):
                                sl = slice(tt * 512, (tt + 1) * 512)
                                ps, pst = self.ps[5], self.ps_t[5]
                                k.emit('act', lambda e, dst=dst, sl=sl: e.activation(out=sq[:], in_=dst[:, sl], func=AF.Square), [dst_t], [sq_t])
                                k.emit('pe', lambda e, ps=ps: e.matmul(ps[:], self.onesR[:], sq[:], start=True, stop=True), [self.onesR_t, sq_t], [pst])
                                k.emit('act', lambda e, ps=ps: e.activation(out=rs[:], in_=ps[:], func=AF.Sqrt, bias=1e-6, scale=1.0), [pst], [rs_t])
                                k.emit('dve', lambda e: e.reciprocal(out=rs[:], in_=rs[:]), [rs_t], [rs_t])
                                sc_ = float(128.0 ** -0.5) if ti == 0 else 1.0
                                k.emit('dve', lambda e, dst=dst, sl=sl, sc_=sc_: e.scalar_tensor_tensor(
                                    out=dst[:, sl], in0=dst[:, sl], scalar=sc_, in1=rs[:], op0=ALU.mult, op1=ALU.mult), [dst_t, rs_t], [dst_t])
                    if stop <= 2:
                        continue
                    for di in range(2):
                        col = di * 8 + hh
                        for tt in range(2):
                            sl = slice(tt * 512, (tt + 1) * 512)
                            ps, pst = self.ps[6 + tt], self.ps_t[6 + tt]
                            k.emit('pe', lambda e, ps=ps, sl=sl, di=di: e.matmul(ps[:], SEL(di, 128), HR4[0:4, sl], start=True, stop=True),
                                   [HR4_t, self.CT_t], [pst])
                            k.emit('act', lambda e, ps=ps, sl=sl: e.activation(out=Qt[:, sl], in_=ps[:], func=AF.Exp), [pst], [Qt_t])
                        pos = 63 if di == 0 else 0
                        k.emit('pool', lambda e, pos=pos: e.tensor_copy(out=DKg[:], in_=Qt[:].rearrange("p (j n) -> p j n", n=64)[:, :, pos]), [Qt_t], [DKg_t])
                        k.emit('dve', lambda e: e.tensor_tensor(out=Qt[:], in0=Qt[:], in1=qn[:], op=ALU.mult), [Qt_t, qn_t], [Qt_t])
                        border = list(range(4)) if di == 0 else list(range(3, -1, -1))
                        if half == 1:
                            k.dma(Sb[0][:], d['st_gdn'][di, hh, :, :], writes=[Sb_t[0]])
                        si = 0
                        for bi in border:
                            chs = [bi * 4 + q for q in range(4)]
                            if di == 1:
                                chs = chs[::-1]
                            T0 = bi * 256
                            bsl = slice(T0, T0 + 256)
                            gcol = GC[:, bi * 4:bi * 4 + 4, col:col + 1].broadcast_to([64, 4, 64])
                            nbcol = NBE[:, bi * 4:bi * 4 + 4, col:col + 1].broadcast_to([64, 4, 64])
                            b0, b0t = self.ps[0], self.ps_t[0]
                            b1, b1t = self.ps[1], self.ps_t[1]
                            b2, b2t = self.ps[2], self.ps_t[2]
                            b3, b3t = self.ps[3], self.ps_t[3]
                            R = lambda ps: ps[0:64, 0:256]
                            R3 = lambda ps: ps[0:64, 0:256].rearrange("p (a n) -> p a n", a=4)
                            F2 = lambda t: t[:].rearrange("p a n -> p (a n)")
                            deps_c = [HR4_t, self.CT_t]
                            k.emit('pe', lambda e, di=di: e.matmul(R(self.ps[4]), SEL(di, 64), HR4[0:4, bsl], start=True, stop=True), deps_c, [self.ps_t[4]])
                            k.emit('pe', lambda e, di=di: e.matmul(R(self.ps[5]), SEL(2 + di, 64), HR4[0:4, bsl], start=True, stop=True), deps_c, [self.ps_t[5]])
                            DT, DT_t = bt[0], bt_t[0]
                            Dl, Dl_t = bt[1], bt_t[1]
                            DBT, DBT_t = bt[2], bt_t[2]
                            k.emit('dve', lambda e: e.tensor_tensor(out=DT[:], in0=R3(self.ps[4]), in1=gcol, op=ALU.subtract), [self.ps_t[4], TB_t], [DT_t])
                            k.emit('pool', lambda e, di=di: e.tensor_tensor(out=F2(Dl), in0=NEGL[di], in1=F2(DT), op=ALU.subtract), [DT_t, self.CT_t], [Dl_t])
                            k.emit('pool', lambda e, di=di: e.tensor_tensor(out=F2(DT), in0=F2(DT), in1=NEGU[di], op=ALU.add), [DT_t, self.CT_t], [DT_t])
                            k.emit('act', lambda e: e.activation(out=DT[:], in_=DT[:], func=AF.Exp), [DT_t], [DT_t])
                            k.emit('act', lambda e: e.activation(out=Dl[:], in_=Dl[:], func=AF.Exp), [Dl_t], [Dl_t])
                            k.emit('dve', lambda e, di=di: e.tensor_tensor(out=F2(DBT), in0=R(self.ps[5]), in1=SNEG[di], op=ALU.mult), [self.ps_t[5], self.CT_t], [DBT_t])
                            k.emit('pool', lambda e: e.tensor_tensor(out=DBT[:], in0=DBT[:], in1=DT[:], op=ALU.mult), [DBT_t, DT_t], [DBT_t])
                            k.emit('pool', lambda e: e.tensor_tensor(out=Dl[:], in0=Dl[:], in1=nbcol, op=ALU.mult), [Dl_t, TB_t], [Dl_t])
                            for q in range(4):
                                cs = slice(T0 + q * 64, T0 + (q + 1) * 64)
                                k.emit('pe', lambda e, q=q, cs=cs: e.matmul(b0[0:64, q * 64:(q + 1) * 64], kn[:, cs], kn[:, cs], start=True, stop=True), [kn_t], [b0t])
                                k.emit('pe', lambda e, q=q, cs=cs: e.matmul(b1[0:64, q * 64:(q + 1) * 64], kn[:, cs], qn[:, cs], start=True, stop=True), [kn_t, qn_t], [b1t])
                                k.emit('pe', lambda e, q=q, cs=cs: e.transpose(b2[0:64, q * 128:(q + 1) * 128], kn[:, cs], IDENT), [kn_t, self.CT_t], [b2t])
                                k.emit('pe', lambda e, q=q, cs=cs: e.transpose(b3[0:64, q * 128:(q + 1) * 128], vT[:, cs], IDENT), [vT_t, self.CT_t], [b3t])
                            NT, NT_t = bt[3], bt_t[3]
                            Nm, Nm_t = bt[4], bt_t[4]
                            QKT, QKT_t = bt[5], bt_t[5]
                            XT, XT_t = bt[6], bt_t[6]
                            k.emit('dve', lambda e: e.tensor_tensor(out=F2(NT), in0=R(b0), in1=F2(DBT), op=ALU.mult), [b0t, DBT_t], [NT_t])
                            k.emit('dve', lambda e: e.tensor_tensor(out=F2(Nm), in0=R(b0), in1=F2(Dl), op=ALU.mult), [b0t, Dl_t], [Nm_t])
                            k.emit('dve', lambda e: e.tensor_tensor(out=F2(QKT), in0=R(b1), in1=F2(DT), op=ALU.mult), [b1t, DT_t], [QKT_t])
                            k.emit('pool', lambda e: e.tensor_tensor(out=F2(XT), in0=F2(NT), in1=IDREP, op=ALU.add), [NT_t, self.CT_t], [XT_t])
                            k.emit('act', lambda e: e.activation(out=F2(ktok), in_=b2[0:64, :], func=AF.Copy), [b2t], [ktok_t])
                            k.emit('act', lambda e: e.activation(out=F2(vtok), in_=b3[0:64, :], func=AF.Copy), [b3t], [vtok_t])
                            P, P_t, PT_, PT_t = Nm, Nm_t, NT, NT_t
                            pp = [(bt[7], bt_t[7], bt[8], bt_t[8]), (bt[9], bt_t[9], bt[1], bt_t[1])]
                            XTs = [(bt[6], bt_t[6]), (bt[0], bt_t[0])]
                            xi = 0
                            def x_update(nP, nP_t, xi):
                                cX, cX_t = XTs[xi]
                                nX, nX_t = XTs[1 - xi]
                                for q in range(4):
                                    k.emit('pe', lambda e, q=q, nP=nP, cX=cX: e.matmul(b2[0:64, q * 64:(q + 1) * 64], nP[:, q, :], cX[:, q, :], start=True, stop=True),
                                           [nP_t, cX_t], [b2t])
                                k.emit('dve', lambda e, nX=nX, cX=cX: e.tensor_tensor(out=F2(nX), in0=R(b2), in1=F2(cX), op=ALU.add), [b2t, cX_t], [nX_t])
                                return 1 - xi

                            pendX = None
                            for m in range(1, 6):
                                nP, nP_t, nPT, nPT_t = pp[(m - 1) % 2] if m > 1 else pp[0]
                                if m >= 3:
                                    nP, nP_t, nPT, nPT_t = pp[(m - 1) % 2]
                                if m == 2:
                                    nP, nP_t, nPT, nPT_t = pp[1]
                                for q in range(4):
                                    k.emit('pe', lambda e, q=q, P=P, PT_=PT_: e.matmul(b0[0:64, q * 64:(q + 1) * 64], PT_[:, q, :], P[:, q, :], start=True, stop=True),
                                           [P_t, PT_t], [b0t])
                                    if m < 5:
                                        k.emit('pe', lambda e, q=q, P=P, PT_=PT_: e.matmul(b1[0:64, q * 64:(q + 1) * 64], P[:, q, :], PT_[:, q, :], start=True, stop=True),
                                               [P_t, PT_t], [b1t])
                                k.emit('act', lambda e, nP=nP: e.activation(out=F2(nP), in_=R(b0), func=AF.Copy), [b0t], [nP_t])
                                if m < 5:
                                    k.emit('dve', lambda e, nPT=nPT: e.tensor_copy(out=F2(nPT), in_=R(b1)), [b1t], [nPT_t])
                                if pendX is not None:
                                    xi = x_update(pendX[0], pendX[1], xi)
                                pendX = (nP, nP_t)
                                P, P_t, PT_, PT_t = nP, nP_t, nPT, nPT_t
                            xi = x_update(pendX[0], pendX[1], xi)
                            XTf, XTf_t = XTs[xi]
                            if stop <= 3:
                                continue
                            psO, psO_t = self.ps[6], self.ps_t[6]
                            for ch in chs:
                                q = ch - bi * 4
                                cs = slice(ch * 64, (ch + 1) * 64)
                                loc = ch % cps
                                first = (loc == 0) if di == 0 else (loc == cps - 1)
                                last = (loc == cps - 1) if di == 0 else (loc == 0)
                                seq = ch // cps
                                zero_init = first and half == 0
                                S_prev, S_prev_t = Sb[si], Sb_t[si]
                                tmpv, tmpv_t = sm[0], sm_t[0]
                                r_, r_t = sm[1], sm_t[1]
                                vn, vn_t = sm[2], sm_t[2]
                                vs, vs_t = sm[3], sm_t[3]
                                k.emit('act', lambda e, q=q, ch=ch: e.activation(out=tmpv[:], in_=vtok[:, q, :], func=AF.Copy, scale=BE[:, ch, col:col + 1]),
                                       [vtok_t, TB_t], [tmpv_t])
                                if zero_init:
                                    rr, rr_t = tmpv, tmpv_t
                                else:
                                    p4, p4t = self.ps[4], self.ps_t[4]
                                    k.emit('pe', lambda e, cs=cs, S_prev=S_prev: e.matmul(p4[0:64, 0:128], kn[:, cs], S_prev[:], start=True, stop=True),
                                           [kn_t, S_prev_t], [p4t])
                                    k.emit('dve', lambda e, ch=ch: e.scalar_tensor_tensor(out=r_[:], in0=p4[0:64, 0:128], scalar=C1[:, ch, col:col + 1], in1=tmpv[:],
                                                                                         op0=ALU.mult, op1=ALU.add), [p4t, TB_t, tmpv_t], [r_t])
                                    rr, rr_t = r_, r_t
                                lvl = self.cfg.get('seq_lvl', 9)
                                if lvl <= 1:
                                    continue
                                p5, p5t = self.ps[5], self.ps_t[5]
                                k.emit('pe', lambda e, q=q, rr=rr: e.matmul(p5[0:64, 0:128], XTf[:, q, :], rr[:], start=True, stop=True), [XTf_t, rr_t], [p5t])
                                if lvl <= 1.5:
                                    continue
                                k.emit('act', lambda e: e.activation(out=vn[:], in_=p5[0:64, 0:128], func=AF.Copy), [p5t], [vn_t])
                                if lvl <= 1.7:
                                    continue
                                k.emit('dve', lambda e, ch=ch: e.tensor_scalar(out=vs[:], in0=p5[0:64, 0:128], scalar1=WW[:, ch, col:col + 1], scalar2=None, op0=ALU.mult),
                                       [p5t, TB_t], [vs_t])
                                if lvl <= 3:
                                    continue
                                p7, p7t = self.ps[7], self.ps_t[7]
                                k.emit('pe', lambda e, q=q: e.matmul(p7[:, 0:128], ktok[:, q, :], vs[:], start=True, stop=True), [ktok_t, vs_t], [p7t])
                                if last and half == 0:
                                    dstS, dstS_t = sout[:, seq, di, :], sout_t
                                else:
                                    si = 1 - si
                                    dstS, dstS_t = Sb[si][:], Sb_t[si]
                                if zero_init:
                                    k.emit('dve', lambda e, dstS=dstS: e.tensor_copy(out=dstS, in_=p7[:, 0:128]), [p7t], [dstS_t])
                                else:
                                    k.emit('dve', lambda e, dstS=dstS, S_prev=S_prev, ch=ch: e.scalar_tensor_tensor(
                                        out=dstS, in0=S_prev[:], scalar=DKg[:, ch:ch + 1], in1=p7[:, 0:128], op0=ALU.mult, op1=ALU.add),
                                        [p7t, S_prev_t, DKg_t], [dstS_t])
                                if lvl <= 2:
                                    continue
                                k.emit('pe', lambda e, q=q, zero_init=zero_init: e.matmul(psO[:, q * 64:(q + 1) * 64], vn[:], QKT[:, q, :], start=True, stop=zero_init),
                                       [vn_t, QKT_t], [psO_t])
                                if not zero_init:
                                    k.emit('pe', lambda e, q=q, cs=cs, S_prev=S_prev: e.matmul(psO[:, q * 64:(q + 1) * 64], S_prev[:], Qt[:, cs], start=False, stop=True),
                                           [S_prev_t, Qt_t], [psO_t])
                            if di == 0:
                                k.emit('act', lambda e, bsl=bsl: e.activation(out=oT[:, bsl], in_=psO[:, 0:256], func=AF.Copy), [psO_t], [oT_t])
                            else:
                                k.emit('dve', lambda e, bsl=bsl: e.tensor_tensor(out=oT[:, bsl], in0=oT[:, bsl], in1=psO[:, 0:256], op=ALU.add), [psO_t, oT_t], [oT_t])
                    if stop <= 4:
                        continue
                    if half == 0:
                        k.dma(d['gdout'][:, :, hh, :, :].rearrange("s d k e -> k s d e"), sout[:], reads=[sout_t])
                    (wg,), wt3 = self.wpiece([w_in[0, :, 3 * D + hh * 128:3 * D + (hh + 1) * 128]])
                    for tt in range(2):
                        sl = slice(tt * 512, (tt + 1) * 512)
                        ps, pst = self.ps[6 + tt], self.ps_t[6 + tt]
                        for c in range(NCH):
                            k.emit('pe', lambda e, ps=ps, c=c, sl=sl: e.matmul(ps[:], wg[:, c, :], self.hT[:, c, sl], start=(c == 0), stop=(c == NCH - 1)),
                                   [wt3, self.hT_t[c][tt]], [pst])
                        k.emit('act', lambda e, ps=ps, sl=sl: e.activation(out=vT[:, sl], in_=ps[:], func=AF.Silu), [pst], [vT_t])
                    self.HV_t = GV_t
                    self.head_norm(ntm, oT[:], oT_t, mix[:, 0, :], mix_t[0], GV[:, 1:2], 128.0 * 1e-6, 5, extra_mul=vT[:], extra_t=vT_t)
                    self.out_proj(d['gdn_w_out'][0, hh * 128:(hh + 1) * 128, :], mix, mix_t, 1, half, [6, 7])
                k.barrier()

    def ffn(self, layer, half):
        k, d, nc = self.k, self.d, self.nc
        seqlen = 256 if half == 0 else 1024
        oc, _ = PC['ffn_conv']
        ob, _ = PC['ffn_conv_b']

        def cw(tap, fchunk):
            col = oc + (layer * 3 + tap) * 44 + fchunk
            return self.PT[:, col:col + 1]

        def cb(fchunk):
            col = ob + layer * 44 + fchunk
            return self.PT[:, col:col + 1]

        groups = [(0, 8), (8, 16), (16, 22)]
        specs = []
        pidx = {}
        for (g0, g1) in groups:
            for j in range(g0, g1):
                pidx[('u', j)] = len(specs)
                specs.append([d['ffn_w_up'][layer, :, j * 128:(j + 1) * 128], d['ffn_w_up'][layer, :, D_FF + j * 128:D_FF + (j + 1) * 128]])
            for dp in range(4):
                pidx[('d', g0, dp)] = len(specs)
                specs.append([d['ffn_w_down'][layer, g0 * 128:g1 * 128, dp * 256:(dp + 1) * 256]])
        pf = PF(self, specs)
        PAIRS = [(0, 1), (2, 3), (4, 5)] if self.modgen is not None else [(0, 1), (2, 3), (4, 5), (6, 7)]
        u = 0
        v = 0
        with ExitStack() as es:
            aT = self.tmp(es, 'aT', [128, 8, TOK], F32R)
            aT_t = trks('aT', 8)
            acc = [self.tmp(es, f'facc{i}', [128, 2, TOK], F32) for i in range(2)]
            acc_t = trks('facc', 2, 2, 2)
            for (g0, g1) in groups:
                for j in range(g0, g1):
                    slot = j - g0
                    (wv, wg), wt = pf.get(pidx[('u', j)])
                    ai = j % 2
                    ac = acc[ai]
                    banks = {}
                    for tt in range(2):
                        pair = PAIRS[u % len(PAIRS)]
                        u += 1
                        sl = slice(tt * 512, (tt + 1) * 512)
                        for vi, (w, fch) in enumerate(((wv, j), (wg, NFF + j))):
                            ps, pst = self.ps[pair[vi]], self.ps_t[pair[vi]]
                            banks[(vi, tt)] = (ps, pst)
                            for c in range(NCH):
                                k.emit('pe', lambda e, ps=ps, w=w, c=c, sl=sl: e.matmul(
                                    ps[:], w[:, c, :], self.hT[:, c, sl],
                                    start=(c == 0), stop=(c == NCH - 1)), [wt, self.hT_t[c][tt]], [pst])
                        for vi, fch in ((0, j), (1, NFF + j)):
                            ps, pst = banks[(vi, tt)]
                            at_ = acc_t[ai][vi][tt]
                            k.emit('act', lambda e, ps=ps, sl=sl, fch=fch, vi=vi: e.activation(
                                out=ac[:, vi, sl], in_=ps[:], func=AF.Identity, bias=cb(fch), scale=cw(1, fch)), [pst, self.PT_t], [at_])
                            sl_ = min(seqlen, 512)
                            ns = 512 // sl_
                            pv = ps[:].rearrange("p (s n) -> p s n", s=ns)
                            av = ac[:, vi, sl].rearrange("p (s n) -> p s n", s=ns)
                            k.emit('dve', lambda e, pv=pv, av=av, fch=fch, sl_=sl_: e.scalar_tensor_tensor(
                                out=av[:, :, 1:sl_], in0=pv[:, :, 0:sl_ - 1], scalar=cw(0, fch), in1=av[:, :, 1:sl_],
                                op0=ALU.mult, op1=ALU.add), [pst, self.PT_t, at_], [at_])
                            k.emit('dve', lambda e, pv=pv, av=av, fch=fch, sl_=sl_: e.scalar_tensor_tensor(
                                out=av[:, :, 0:sl_ - 1], in0=pv[:, :, 1:sl_], scalar=cw(2, fch), in1=av[:, :, 0:sl_ - 1],
                                op0=ALU.mult, op1=ALU.add), [pst, self.PT_t, at_], [at_])
                    if seqlen > 512:
                        for vi, fch in ((0, j), (1, NFF + j)):
                            p0, p0t = banks[(vi, 0)]
                            p1, p1t = banks[(vi, 1)]
                            k.emit('dve', lambda e, p0=p0, fch=fch, vi=vi: e.scalar_tensor_tensor(
                                out=ac[:, vi, 512:513], in0=p0[:, 511:512], scalar=cw(0, fch), in1=ac[:, vi, 512:513],
                                op0=ALU.mult, op1=ALU.add), [p0t, self.PT_t, acc_t[ai][vi][1]], [acc_t[ai][vi][1]])
                            k.emit('dve', lambda e, p1=p1, fch=fch, vi=vi: e.scalar_tensor_tensor(
                                out=ac[:, vi, 511:512], in0=p1[:, 0:1], scalar=cw(2, fch), in1=ac[:, vi, 511:512],
                                op0=ALU.mult, op1=ALU.add), [p1t, self.PT_t, acc_t[ai][vi][0]], [acc_t[ai][vi][0]])
                    k.emit('act', lambda e, ac=ac: e.activation(out=ac[:, 1, :], in_=ac[:, 1, :], func=AF.Silu), acc_t[ai][1], acc_t[ai][1])
                    k.emit('pool', lambda e, slot=slot, ac=ac: e.tensor_tensor(out=aT[:, slot, :], in0=ac[:, 0, :], in1=ac[:, 1, :], op=ALU.mult),
                           acc_t[ai], [aT_t[slot]])
                    if self.modgen is not None:
                        next(self.modgen, None)
                ng = g1 - g0
                for dp in range(4):
                    (wd,), wdt = pf.get(pidx[('d', g0, dp)])
                    for dmi in range(2):
                        dm = dp * 2 + dmi
                        for tt in range(2):
                            b = v % 6
                            v += 1
                            ps, pst = self.ps[b], self.ps_t[b]
                            for jj in range(ng):
                                k.emit('pe', lambda e, ps=ps, jj=jj, dmi=dmi, tt=tt: e.matmul(
                                    ps[:], wd[:, jj, dmi * 128:(dmi + 1) * 128], aT[:, jj, tt * 512:(tt + 1) * 512],
                                    start=(jj == 0), stop=(jj == ng - 1)), [wdt, aT_t[jj]], [pst])
                            xs = self.xT[half][:, dm, tt * 512:(tt + 1) * 512]
                            k.emit('dve', lambda e, ps=ps, xs=xs, dm=dm: e.scalar_tensor_tensor(
                                out=xs, in0=ps[:], scalar=self.gate(1, dm, half), in1=xs, op0=ALU.mult, op1=ALU.add),
                                [pst, self.MOD_t, self.xT_t[half][dm][tt]], [self.xT_t[half][dm][tt]])
            k.barrier()

    def final(self, cfg):
        k, d, nc = self.k, self.d, self.nc
        og, _ = PC['final_g']
        with ExitStack() as es:
            sq = self.tmp(es, 'fsq', [128, NCH, 512], F32R)
            sq_t = Trk('fsq')
            rstd = self.tmp(es, 'frstd', [128, 512], F32)
            rstd_t = Trk('frstd')
            yo = self.tmp(es, 'fyo', [128, NCH, 512], F32)
            yo_t = self.ptrk('fyo', NCH)
            for half in range(2):
                for tt in range(2):
                    xs = self.xT[half][:, :, tt * 512:(tt + 1) * 512]
                    xs_t = [self.xT_t[half][c][tt] for c in range(NCH)]
                    k.emit('act', lambda e: e.activation(out=sq[:], in_=xs, func=AF.Square), xs_t, [sq_t])
                    ps, pst = self.ps[6], self.ps_t[6]
                    for c in range(NCH):
                        k.emit('pe', lambda e, c=c: e.matmul(ps[:], self.onesR[:], sq[:, c, :], start=(c == 0), stop=(c == NCH - 1)),
                               [self.onesR_t, sq_t], [pst])
                    k.emit('act', lambda e: e.activation(out=rstd[:], in_=ps[:], func=AF.Sqrt, bias=float(D * EPS), scale=1.0),
                           [pst], [rstd_t])
                    k.emit('dve', lambda e: e.reciprocal(out=rstd[:], in_=rstd[:]), [rstd_t], [rstd_t])
                    for c in range(NCH):
                        g = self.PT[:, og + c:og + c + 1]
                        k.emit('dve', lambda e, c=c, g=g: e.scalar_tensor_tensor(
                            out=yo[:, c, :], in0=self.xT[half][:, c, tt * 512:(tt + 1) * 512], scalar=g, in1=rstd[:],
                            op0=ALU.mult, op1=ALU.mult), [self.xT_t[half][c][tt], self.PT_t, rstd_t], [yo_t[c]])
                        k.emit('act', lambda e, c=c: e.activation(out=yo[:, c, :], in_=yo[:, c, :], func=AF.Copy, scale=32.0),
                               [yo_t[c]], [yo_t[c]])
                        k.dma(d['yout'][half, c, :, tt * 512:(tt + 1) * 512], yo[:, c, :], reads=[yo_t[c]])


_CACHE = {}


def _get_prog(cfg):
    key = repr(sorted(cfg.items()))
    if key not in _CACHE:
        p = Prog(dict(cfg))
        with p.es:
            p.declare()
            p.build()
        _CACHE[key] = p
    return _CACHE[key]


def _pack(plan, arrays, n):
    out = np.zeros((128, n), np.float32)
    for c0, key in plan:
        off = c0
        for (name, offset, rstride, rows, ncols) in key:
            flat = arrays[name].reshape(-1)
            a2 = np.lib.stride_tricks.as_strided(flat[offset:], shape=(rows, ncols), strides=(rstride * 4, 4))
            kc = rows // 128
            assert off + kc * ncols <= n
            out[:, off:off + kc * ncols] = a2.reshape(kc, 128, ncols).transpose(1, 0, 2).reshape(128, kc * ncols)
            off += kc * ncols
    return out


def _run(inp, cfg):
    p = _get_prog(cfg)
    consts, rope_tab = _build_consts()
    f32 = lambda a: np.ascontiguousarray(np.asarray(a, np.float32))
    warr = {n: f32(inp[n]) for n in ('w_mod', 'ffn_w_up', 'ffn_w_down', 'attn_w_in', 'attn_w_out',
                                     'hgrn_w_in', 'hgrn_w_out', 'gdn_w_in', 'gdn_w_out')}
    shared = {'wpk': _pack(p.wplan['wpk'], warr, WCOLS)}
    xp = f32(inp['x_prompt'])
    xs = f32(inp['x_sample'])
    ck = f32(inp['cache_attn_k'])
    cv = f32(inp['cache_attn_v'])
    in_maps = []
    for core in range(N_CORES):
        m = dict(shared)
        a = xp[4 * core:4 * core + 4].reshape(TOK, D).T.reshape(8, 128, TOK)
        b = xs[core].T.reshape(8, 128, TOK)
        m['xin'] = np.ascontiguousarray(np.stack([a, b], axis=0))
        m['params'] = _build_params(core, inp)
        m['consts'] = consts
        m['rope'] = rope_tab
        m['lamtab'] = np.ascontiguousarray(np.broadcast_to(np.asarray(inp['attn_lambda'], np.float32).reshape(1, 512), (128, 512)))
        carr = {'ck': np.ascontiguousarray(ck[core].transpose(0, 2, 3, 1)),
                'cv': np.ascontiguousarray(cv[core].reshape(2, 512, D))}
        m['cpk'] = _pack(p.wplan['cpk'], carr, CCOLS)
        m['st_hgrn'] = f32(inp['state_hgrn'][core, 0])
        m['st_gdn'] = f32(inp['state_gdn'][core, 0])
        in_maps.append(m)
    ncores = cfg.get('ncores', N_CORES)
    res = run_bass_kernel_spmd(p.nc, in_maps[:ncores], core_ids=list(range(ncores)))
    R = list(res.results) + [res.results[0]] * (N_CORES - ncores)
    y_prompt = np.empty((32, 256, D), np.float32)
    y_sample = np.empty((8, 1024, D), np.float32)
    new_k = np.empty((32, 2, 256, 8, 128), np.float32)
    new_v = np.empty((32, 2, 256, 8, 128), np.float32)
    new_h = np.empty((32, 1, 2, 8, 128, 128), np.float32)
    new_g = np.empty((32, 1, 2, 8, 128, 128), np.float32)
    for core in range(N_CORES):
        r = R[core]
        yo = r['yout']
        y_prompt[4 * core:4 * core + 4] = yo[0].reshape(D, TOK).T.reshape(4, 256, D)
        y_sample[core] = yo[1].reshape(D, TOK).T
        ko = r['kout']
        new_k[4 * core:4 * core + 4] = ko.reshape(2, 8, 128, 4, 256).transpose(3, 0, 4, 1, 2)
        vo = r['vout']
        new_v[4 * core:4 * core + 4] = vo.reshape(2, 4, 256, 8, 128).transpose(1, 0, 2, 3, 4)
        new_h[4 * core:4 * core + 4, 0] = r['hgout']
        new_g[4 * core:4 * core + 4, 0] = r['gdout']
    return (y_prompt, y_sample, new_k, new_v, new_h, new_g)


def kernel(**inputs):
    return _run(inputs, {})
```
